# Optimizing a Trainium2 kernel written in Bass

```python
import math
import jax, jax.numpy as jnp
from jax import lax
import numpy as np

D_MODEL = 1024
BATCH = 4
SEQ = 8192
DEPTH = 1

MLSTM_HEADS = 4
MLSTM_WIDTH = D_MODEL
MLSTM_HEAD_DIM = MLSTM_WIDTH // MLSTM_HEADS
MLSTM_CHUNK = 128
CONV_WIDTH = 4
S5_WIDTH = D_MODEL // 2
S5_GROUP = 16
S5_GROUPS = S5_WIDTH // S5_GROUP
S5_STATE = 64
DT_MIN = 0.001
DT_MAX = 0.1
N_BRANCHES = 2
NORM_EPS = 1e-6
IN_TOTAL = 3 * MLSTM_WIDTH + 2 * MLSTM_HEADS + 2 * S5_WIDTH + N_BRANCHES * D_MODEL

kernel_name = "hybrid_mlstm_s5_gated_block"


def _split_points():
    sizes = (MLSTM_WIDTH, MLSTM_WIDTH, MLSTM_WIDTH, MLSTM_HEADS, MLSTM_HEADS,
             S5_WIDTH, S5_WIDTH, N_BRANCHES * D_MODEL)
    return [int(s) for s in np.cumsum(sizes)[:-1]]


def rmsnorm(x, g):
    xf = x.astype(jnp.float32)
    y = xf * lax.rsqrt(jnp.mean(xf * xf, axis=-1, keepdims=True) + NORM_EPS)
    return (y * g.astype(jnp.float32)).astype(x.dtype)


def causal_dwconv(u, w, b):
    K, C = w.shape
    y = lax.conv_general_dilated(u, w[:, None, :].astype(u.dtype), window_strides=(1,),
                                 padding=[(K - 1, 0)], dimension_numbers=('NWC', 'WIO', 'NWC'),
                                 feature_group_count=C)
    return y + b.astype(u.dtype)


def mlstm_chunkwise(q, k, v, ig, lf):
    Bsz, H, L, dh = q.shape
    Lc = MLSTM_CHUNK
    NC = L // Lc
    q = q.reshape(Bsz, H, NC, Lc, dh)
    k = k.reshape(Bsz, H, NC, Lc, dh) * (dh ** -0.5)
    v = v.reshape(Bsz, H, NC, Lc, dh)
    ig = ig.reshape(Bsz, H, NC, Lc)
    b = jnp.cumsum(lf.reshape(Bsz, H, NC, Lc), axis=-1)
    b_last = b[..., -1]
    w_end = b_last[..., None] - b + ig
    m_loc = jnp.max(w_end, axis=-1)
    e_end = jnp.exp(w_end - m_loc[..., None])
    C_loc = jnp.einsum('bhcs,bhcsd,bhcse->bhcde', e_end, v, k)
    n_loc = jnp.einsum('bhcs,bhcse->bhce', e_end, k)

    def step(carry, inp):
        C, n, m = carry
        C_l, n_l, m_l, bl = inp
        m_new = jnp.maximum(bl + m, m_l)
        a = jnp.exp(bl + m - m_new)
        s = jnp.exp(m_l - m_new)
        C_new = a[..., None, None] * C + s[..., None, None] * C_l
        n_new = a[..., None] * n + s[..., None] * n_l
        return (C_new, n_new, m_new), (C, n, m)

    init = (jnp.zeros((Bsz, H, dh, dh), jnp.float32), jnp.zeros((Bsz, H, dh), jnp.float32),
            jnp.full((Bsz, H), -jnp.inf, jnp.float32))
    xs = (jnp.moveaxis(C_loc, 2, 0), jnp.moveaxis(n_loc, 2, 0),
          jnp.moveaxis(m_loc, 2, 0), jnp.moveaxis(b_last, 2, 0))
    _, (C_prev, n_prev, m_prev) = lax.scan(step, init, xs)
    C_prev = jnp.moveaxis(C_prev, 0, 2)
    n_prev = jnp.moveaxis(n_prev, 0, 2)
    m_prev = jnp.moveaxis(m_prev, 0, 2)

    causal = jnp.tril(jnp.ones((Lc, Lc), dtype=bool))
    log_d = jnp.where(causal, b[..., :, None] - b[..., None, :] + ig[..., None, :], -jnp.inf)
    log_inter = b + m_prev[..., None]
    m_t = jnp.maximum(log_inter, jnp.max(log_d, axis=-1))
    dmat = jnp.exp(log_d - m_t[..., None])
    inter = jnp.exp(log_inter - m_t)
    s = jnp.einsum('bhctd,bhcsd->bhcts', q, k) * dmat
    num = jnp.einsum('bhcts,bhcsd->bhctd', s, v) + inter[..., None] * jnp.einsum('bhcde,bhcte->bhctd', C_prev, q)
    den = jnp.sum(s, axis=-1) + inter * jnp.einsum('bhce,bhcte->bhct', n_prev, q)
    h = num / jnp.maximum(jnp.abs(den), jnp.exp(-m_t))[..., None]
    return h.reshape(Bsz, H, L, dh)


def mlstm_branch(u_a, z_a, o_pre, i_pre, f_pre, conv_w, conv_b, w_q, w_k, w_v, b_i, b_f, head_g, skip):
    Bsz, L, _ = u_a.shape
    H, dh = MLSTM_HEADS, MLSTM_HEAD_DIM
    c = jax.nn.silu(causal_dwconv(u_a, conv_w, conv_b))
    ch = c.reshape(Bsz, L, H, dh)
    uh = u_a.reshape(Bsz, L, H, dh)
    q = jnp.einsum('blhd,hde->bhle', ch, w_q).astype(jnp.float32)
    k = jnp.einsum('blhd,hde->bhle', ch, w_k).astype(jnp.float32)
    v = jnp.einsum('blhd,hde->bhle', uh, w_v).astype(jnp.float32)
    ig = jnp.transpose(i_pre.astype(jnp.float32) + b_i.astype(jnp.float32), (0, 2, 1))
    lf = jax.nn.log_sigmoid(jnp.transpose(f_pre.astype(jnp.float32) + b_f.astype(jnp.float32), (0, 2, 1)))
    h = mlstm_chunkwise(q, k, v, ig, lf)
    o = jnp.transpose(jax.nn.sigmoid(o_pre.astype(jnp.float32)).reshape(Bsz, L, H, dh), (0, 2, 1, 3))
    h = o * h
    h = h * lax.rsqrt(jnp.mean(h * h, axis=-1, keepdims=True) + NORM_EPS)
    h = jnp.transpose(h, (0, 2, 1, 3)).reshape(Bsz, L, MLSTM_WIDTH) * head_g.astype(jnp.float32)
    h = h.astype(u_a.dtype) + skip * c
    return h * jax.nn.silu(z_a)


def s5_branch(u_b, z_b, lam_re, lam_im, log_dt, B_re, B_im, C_re, C_im, D_skip, w_glu, b_glu):
    Bsz, L, _ = u_b.shape
    u = u_b.astype(jnp.float32).reshape(Bsz, L, S5_GROUPS, S5_GROUP)
    dt = jnp.exp(log_dt.astype(jnp.float32))[:, None]
    lr = lam_re.astype(jnp.float32)
    li = lam_im.astype(jnp.float32)
    mag = jnp.exp(lr * dt)
    a_re = mag * jnp.cos(li * dt)
    a_im = mag * jnp.sin(li * dt)
    den = lr * lr + li * li
    nr = a_re - 1.0
    q_re = (nr * lr + a_im * li) / den
    q_im = (a_im * lr - nr * li) / den
    Br = B_re.astype(jnp.float32)
    Bi = B_im.astype(jnp.float32)
    bb_re = q_re[..., None] * Br - q_im[..., None] * Bi
    bb_im = q_re[..., None] * Bi + q_im[..., None] * Br
    bu_re = jnp.einsum('blgn,gpn->blgp', u, bb_re)
    bu_im = jnp.einsum('blgn,gpn->blgp', u, bb_im)
    a_re_b = jnp.broadcast_to(a_re, bu_re.shape)
    a_im_b = jnp.broadcast_to(a_im, bu_re.shape)

    def combine(e1, e2):
        a1r, a1i, b1r, b1i = e1
        a2r, a2i, b2r, b2i = e2
        return (a2r * a1r - a2i * a1i, a2r * a1i + a2i * a1r,
                a2r * b1r - a2i * b1i + b2r, a2r * b1i + a2i * b1r + b2i)

    _, _, s_re, s_im = lax.associative_scan(combine, (a_re_b, a_im_b, bu_re, bu_im), axis=1)
    y = (jnp.einsum('blgp,gnp->blgn', s_re, C_re.astype(jnp.float32))
         - jnp.einsum('blgp,gnp->blgn', s_im, C_im.astype(jnp.float32))
         + D_skip.astype(jnp.float32) * u)
    y = jax.nn.gelu(y.reshape(Bsz, L, S5_WIDTH)).astype(u_b.dtype)
    y = y * jax.nn.sigmoid(y @ w_glu + b_glu)
    return y * jax.nn.silu(z_b)


def setup_inputs(seed: int = 0) -> dict:
    key = jax.random.key(seed)
    ks = jax.random.split(key, 32)
    f32 = jnp.float32
    nrm = lambda k, shape, scale: scale * jax.random.normal(k, shape, f32)
    H, dh, P, G, N = MLSTM_HEADS, MLSTM_HEAD_DIM, S5_STATE, S5_GROUPS, S5_GROUP
    x = jax.random.normal(ks[0], (BATCH, SEQ, D_MODEL), f32)
    norm_pre_g = 1.0 + nrm(ks[1], (DEPTH, D_MODEL), 0.02)
    w_in = nrm(ks[2], (DEPTH, D_MODEL, IN_TOTAL), D_MODEL ** -0.5)
    conv_w = nrm(ks[3], (DEPTH, CONV_WIDTH, MLSTM_WIDTH), CONV_WIDTH ** -0.5)
    conv_b = nrm(ks[4], (DEPTH, MLSTM_WIDTH), 0.01)
    w_q = nrm(ks[5], (DEPTH, H, dh, dh), dh ** -0.5)
    w_k = nrm(ks[6], (DEPTH, H, dh, dh), dh ** -0.5)
    w_v = nrm(ks[7], (DEPTH, H, dh, dh), dh ** -0.5)
    b_i = nrm(ks[8], (DEPTH, H), 0.1)
    b_f = 3.0 + 3.0 * jax.random.uniform(ks[9], (DEPTH, H), f32)
    head_g = 1.0 + nrm(ks[10], (DEPTH, MLSTM_WIDTH), 0.02)
    skip_a = 1.0 + nrm(ks[11], (DEPTH, MLSTM_WIDTH), 0.02)
    w_a_out = nrm(ks[12], (DEPTH, MLSTM_WIDTH, D_MODEL), MLSTM_WIDTH ** -0.5)
    lam_re = -0.5 + nrm(ks[13], (DEPTH, G, P), 0.01)
    lam_im = math.pi * jnp.arange(P, dtype=f32)[None, None, :] + nrm(ks[14], (DEPTH, G, P), 0.01)
    log_dt = math.log(DT_MIN) + (math.log(DT_MAX) - math.log(DT_MIN)) * jax.random.uniform(ks[15], (DEPTH, G), f32)
    B_re = nrm(ks[16], (DEPTH, G, P, N), (2.0 * N) ** -0.5)
    B_im = nrm(ks[17], (DEPTH, G, P, N), (2.0 * N) ** -0.5)
    C_re = nrm(ks[18], (DEPTH, G, N, P), P ** -0.5)
    C_im = nrm(ks[19], (DEPTH, G, N, P), P ** -0.5)
    D_skip = nrm(ks[20], (DEPTH, G, N), 1.0)
    w_glu = nrm(ks[21], (DEPTH, S5_WIDTH, S5_WIDTH), S5_WIDTH ** -0.5)
    b_glu = nrm(ks[22], (DEPTH, S5_WIDTH), 0.01)
    w_b_out = nrm(ks[23], (DEPTH, S5_WIDTH, D_MODEL), S5_WIDTH ** -0.5)
    w_o = nrm(ks[24], (DEPTH, D_MODEL, D_MODEL), D_MODEL ** -0.5)
    norm_post_g = 1.0 + nrm(ks[25], (DEPTH, D_MODEL), 0.02)
    return {"x": x, "norm_pre_g": norm_pre_g, "w_in": w_in, "conv_w": conv_w, "conv_b": conv_b,
            "w_q": w_q, "w_k": w_k, "w_v": w_v, "b_i": b_i, "b_f": b_f, "head_g": head_g,
            "skip_a": skip_a, "w_a_out": w_a_out, "lam_re": lam_re, "lam_im": lam_im,
            "log_dt": log_dt, "B_re": B_re, "B_im": B_im, "C_re": C_re, "C_im": C_im,
            "D_skip": D_skip, "w_glu": w_glu, "b_glu": b_glu, "w_b_out": w_b_out,
            "w_o": w_o, "norm_post_g": norm_post_g}


def reference(x, norm_pre_g, w_in, conv_w, conv_b, w_q, w_k, w_v, b_i, b_f, head_g, skip_a,
              w_a_out, lam_re, lam_im, log_dt, B_re, B_im, C_re, C_im, D_skip, w_glu, b_glu,
              w_b_out, w_o, norm_post_g):
    h = x
    split_points = _split_points()
    for l in range(DEPTH):
        xn = rmsnorm(h, norm_pre_g[l])
        proj = xn @ w_in[l]
        u_a, z_a, o_a, i_a, f_a, u_b, z_b, g_pre = jnp.split(proj, split_points, axis=-1)
        y_a = mlstm_branch(u_a, z_a, o_a, i_a, f_a, conv_w[l], conv_b[l], w_q[l], w_k[l], w_v[l],
                           b_i[l], b_f[l], head_g[l], skip_a[l]) @ w_a_out[l]
        y_b = s5_branch(u_b, z_b, lam_re[l], lam_im[l], log_dt[l], B_re[l], B_im[l], C_re[l],
                        C_im[l], D_skip[l], w_glu[l], b_glu[l]) @ w_b_out[l]
        g = jax.nn.sigmoid(g_pre.astype(jnp.float32)).astype(h.dtype)
        merged = g[..., :D_MODEL] * y_a + g[..., D_MODEL:] * y_b
        h = h + rmsnorm(merged @ w_o[l], norm_post_g[l])
    return h
```

```python
import contextlib
import numpy as np
import concourse.bass as bass
import concourse.mybir as mybir
from concourse.bass_utils import run_bass_kernel_spmd

F32 = mybir.dt.float32
BF16 = mybir.dt.bfloat16
AF = mybir.ActivationFunctionType
ALU = mybir.AluOpType
COMPUTE = ("pe", "act", "dve", "pool")
NDMA_SLOTS = 8
PI = float(np.pi)


class Prog:
    def __init__(self, nc):
        self.nc = nc
        self.stack = contextlib.ExitStack()
        self.engs = ("pe", "act", "dve", "pool", "sp")
        self.ops = {e: [] for e in self.engs}
        self.waited = {e: {} for e in self.engs}
        self.res = {}
        self.dma_use = {}
        self.dma_rr = {e: 0 for e in self.engs}
        self.sems = {}
        self.base = {e: 0 for e in COMPUTE}
        self.temp = None

    def sb(self, name, shape, dt):
        st = self.temp if self.temp is not None else self.stack
        return st.enter_context(self.nc.sbuf_tensor(name, list(shape), dt))

    def ps(self, name, shape, dt):
        return self.stack.enter_context(self.nc.psum_tensor(name, list(shape), dt))

    def _sem(self, key):
        if key not in self.sems:
            nm = "s_" + "_".join(str(k) for k in (key if isinstance(key, tuple) else (key,)))
            self.sems[key] = self.stack.enter_context(self.nc.semaphore(nm))
        return self.sems[key]

    def _deps(self, eng, reads, writes):
        deps = {}

        def add(tok):
            if tok is None:
                return
            k, v = tok
            if k == "pe" and eng == "pe":
                return
            if deps.get(k, -1) < v:
                deps[k] = v

        for r in reads:
            st = self.res.get(r)
            if st:
                for k, v in st[0].items():
                    add((k, v))
        for w in writes:
            st = self.res.get(w)
            if st:
                for k, v in st[0].items():
                    add((k, v))
                for k, v in st[1].items():
                    add((k, v))
        out = []
        wd = self.waited[eng]
        for k, v in deps.items():
            if wd.get(k, -1) >= v:
                continue
            wd[k] = v
            out.append((k, v))
        return out

    def _commit(self, tok, reads, writes):
        k, v = tok
        for r in reads:
            st = self.res.setdefault(r, [{}, {}])
            if st[1].get(k, -1) < v:
                st[1][k] = v
        for w in writes:
            old = self.res.get(w)
            wr = {}
            if old is not None and k not in COMPUTE:
                wr = {k2: v2 for k2, v2 in old[0].items() if k2 not in COMPUTE}
            wr[k] = v
            self.res[w] = [wr, {}]

    def op(self, eng, fn, reads=(), writes=(), sig=True):
        waits = self._deps(eng, reads, writes)
        idx = len(self.ops[eng])
        self.ops[eng].append(dict(fn=fn, waits=waits, sig=sig, dma=None))
        tok = (eng, idx)
        self._commit(tok, reads, writes)
        return tok

    def dma(self, fn, reads=(), writes=(), q="sp"):
        waits = self._deps(q, reads, writes)
        slot = self.dma_rr[q] % NDMA_SLOTS
        self.dma_rr[q] += 1
        key = ("d", q, slot)
        n = self.dma_use.get(key, 0)
        if n > 0:
            prev = n * 16
            if self.waited[q].get(key, -1) < prev:
                self.waited[q][key] = prev
                waits.append((key, prev))
        self.dma_use[key] = n + 1
        tok = (key, (n + 1) * 16)
        self.ops[q].append(dict(fn=fn, waits=waits, sig=False, dma=key))
        self._commit(tok, reads, writes)
        return tok

    def final_wait(self, eng, toks):
        self.ops[eng].append(dict(fn=None, waits=list(toks), sig=False, dma=None))

    def emit(self):
        nc = self.nc
        sigcount = {}
        totals = {}
        for e in COMPUTE:
            c = self.base[e]
            arr = []
            for o in self.ops[e]:
                if o["sig"]:
                    c += 1
                arr.append(c)
            need = [None] * len(arr)
            nxt = None
            for i in range(len(arr) - 1, -1, -1):
                if self.ops[e][i]["sig"]:
                    nxt = arr[i]
                need[i] = nxt
            sigcount[e] = need
            totals[e] = c
            self._sem(e)
        for k in self.dma_use:
            self._sem(k)

        def resolve(k, v):
            if k in COMPUTE:
                val = sigcount[k][v]
                assert val is not None, (k, v)
                return self.sems[k], val
            return self.sems[k], v

        with nc.Block() as block:

            def run(eng_name, eng):
                for o in self.ops[eng_name]:
                    for k, v in o["waits"]:
                        s, val = resolve(k, v)
                        eng.wait_ge(s, val)
                    if o["fn"] is None:
                        continue
                    ins = o["fn"](eng)
                    if o["dma"] is not None:
                        ins.then_inc(self.sems[o["dma"]], 16)
                    elif o["sig"]:
                        ins.then_inc(self.sems[eng_name], 1)
                for o2 in COMPUTE:
                    if o2 != eng_name and totals[o2] > 0:
                        eng.wait_ge(self.sems[o2], totals[o2])
                for k, n in self.dma_use.items():
                    eng.wait_ge(self.sems[k], n * 16)

            @block.tensor
            def _(e):
                run("pe", e)

            @block.scalar
            def _(e):
                run("act", e)

            @block.vector
            def _(e):
                run("dve", e)

            @block.gpsimd
            def _(e):
                run("pool", e)

            @block.sync
            def _(e):
                run("sp", e)

        self.base = totals
        self.ops = {e: [] for e in self.engs}
        self.waited = {e: {} for e in self.engs}
        self.res = {}

    def close(self):
        self.stack.close()


NBLK = 22
BLK_UA, BLK_UB, BLK_ZB, BLK_ZA, BLK_OA, BLK_G, BLK_AO, BLK_BO, BLK_WO, BLK_S5 = 0, 2, 3, 4, 6, 8, 12, 14, 15, 17
COL_UA, COL_ZA, COL_OA, COL_I, COL_UB, COL_ZB, COL_G = 0, 1024, 2048, 3072, 3080, 3592, 4104


def build_program(T, NPRE, NMAIN, dbg_stop=0):
    NCH = T // 128
    NC8 = T // 8
    nc = bass.Bass("TRN2", target_bir_lowering=False)
    dram = {}

    def din(name, shape):
        dram[name] = nc.dram_tensor(name, list(shape), F32, kind="ExternalInput").ap()
        return dram[name]

    x_pre = din("x_pre", [max(NPRE, 1) * T, 1024])
    x_main = din("x_main", [NMAIN * T, 1024])
    flag = din("flag", [128, 1])
    norm_pre_g = din("norm_pre_g", [1024]); w_in = din("w_in", [1024, 6152])
    conv_w = din("conv_w", [4, 1024]); conv_b = din("conv_b", [1024])
    w_q = din("w_q", [4, 256, 256]); w_k = din("w_k", [4, 256, 256]); w_v = din("w_v", [4, 256, 256])
    b_i = din("b_i", [1, 4]); b_f = din("b_f", [1, 4]); head_g = din("head_g", [1024]); skip_a = din("skip_a", [1024])
    w_a_out = din("w_a_out", [1024, 1024])
    lam_re = din("lam_re", [32, 64]); lam_im = din("lam_im", [32, 64]); log_dt = din("log_dt", [1, 32])
    B_re = din("B_re", [32, 64, 16]); B_im = din("B_im", [32, 64, 16])
    C_re = din("C_re", [32, 16, 64]); C_im = din("C_im", [32, 16, 64]); D_skip = din("D_skip", [32, 16])
    w_glu = din("w_glu", [512, 512]); b_glu = din("b_glu", [512]); w_b_out = din("w_b_out", [512, 1024])
    w_o = din("w_o", [1024, 1024]); norm_post_g = din("norm_post_g", [1, 1024])
    out = nc.dram_tensor("out", [NMAIN * T, 1024], F32, kind="ExternalOutput").ap()
    WS = nc.dram_tensor("wscratch", [NBLK, 128, 4096], BF16, kind="Internal").ap()

    P = Prog(nc)
    uid = [0]

    def sbt(shape, dt, name=None):
        uid[0] += 1
        return P.sb(name or ("t%d" % uid[0]), shape, dt)

    ident_f = sbt([128, 128], F32); ident_b = sbt([128, 128], BF16)
    maskT = sbt([128, 128], F32); mask4 = sbt([128, 4, 128], F32); ones_b = sbt([128, 128], BF16)
    P.op("pool", lambda e: e.memset(ident_f[:], 1.0), writes=["ident_f"])
    P.op("pool", lambda e: e.affine_select(out=ident_f[:], in_=ident_f[:], pattern=[[-1, 128]], compare_op=ALU.is_equal,
                                           fill=0.0, base=0, channel_multiplier=1), reads=["ident_f"], writes=["ident_f"])
    P.op("dve", lambda e: e.tensor_copy(out=ident_b[:], in_=ident_f[:]), reads=["ident_f"], writes=["ident_b"])
    P.op("pool", lambda e: e.memset(maskT[:], 1.0), writes=["maskT"])
    P.op("pool", lambda e: e.affine_select(out=maskT[:], in_=maskT[:], pattern=[[1, 128]], compare_op=ALU.is_ge,
                                           fill=0.0, base=0, channel_multiplier=-1), reads=["maskT"], writes=["maskT"])
    for h in range(4):
        P.op("pool", lambda e, h=h: e.tensor_copy(out=mask4[:, h, :], in_=maskT[:]), reads=["maskT"], writes=["mask4"])
    P.op("pool", lambda e: e.memset(ones_b[:], 1.0), writes=["ones_b"])

    gpre = sbt([128, 8], F32); cb = sbt([128, 8], F32); hg = sbt([128, 8], F32); skp = sbt([128, 8], F32)
    cw = sbt([128, 8, 4], F32); bglu = sbt([128, 4], F32); gpost = sbt([128, 1024], F32); bif = sbt([128, 8], F32)
    flg = sbt([128, 1], F32)
    nonc = dict(allow_slow_non_contiguous=True)
    P.dma(lambda e: e.dma_start(out=gpre[:], in_=norm_pre_g.rearrange("(k p) -> p k", p=128), **nonc), writes=["gpre"])
    P.dma(lambda e: e.dma_start(out=cb[:], in_=conv_b.rearrange("(k p) -> p k", p=128), **nonc), writes=["cb"])
    P.dma(lambda e: e.dma_start(out=hg[:], in_=head_g.rearrange("(k p) -> p k", p=128), **nonc), writes=["hg"])
    P.dma(lambda e: e.dma_start(out=skp[:], in_=skip_a.rearrange("(k p) -> p k", p=128), **nonc), writes=["skp"])
    for k in range(4):
        P.dma(lambda e, k=k: e.dma_start(out=cw[:, :, k], in_=conv_w[k].rearrange("(m p) -> p m", p=128), **nonc), writes=["cw"])
    P.dma(lambda e: e.dma_start(out=bglu[:], in_=b_glu.rearrange("(k p) -> p k", p=128), **nonc), writes=["bglu"])
    P.dma(lambda e: e.dma_start(out=gpost[:], in_=norm_post_g.partition_broadcast(128)), writes=["gpost"])
    P.dma(lambda e: e.dma_start(out=bif[:, 0:4], in_=b_i.partition_broadcast(128)), writes=["bif"])
    P.dma(lambda e: e.dma_start(out=bif[:, 4:8], in_=b_f.partition_broadcast(128)), writes=["bif"])
    P.dma(lambda e: e.dma_start(out=flg[:], in_=flag), writes=["flg"])

    Wqkv = [sbt([128, 4, 2, 256], BF16) for _ in range(3)]
    Wif = sbt([128, 8, 8], BF16); Wglu = sbt([128, 4, 512], BF16); cdiag = sbt([128, 8, 4, 128], BF16)
    AR32 = sbt([128, 2, 16], F32); ANI = sbt([128, 16], F32); API = sbt([128, 16], F32)
    pA = P.ps("pA", [128, 512], F32); pB = P.ps("pB", [128, 512], F32)
    pT = P.ps("pT", [128, 1024], BF16); pGD = P.ps("pGD", [128, 512], F32)
    pS = P.ps("pS", [128, 512], F32); pN0 = P.ps("pN0", [128, 512], F32); pN1 = P.ps("pN1", [128, 512], F32)
    pM = P.ps("pM", [128, 512], F32)
    P.temp = contextlib.ExitStack()
    stg = sbt([128, 4096], F32, "stg")
    stgb = sbt([128, 4096], BF16, "stgb")
    for wi, (wsrc, scl) in enumerate(((w_q, 1.0), (w_k, 1.0 / 16.0), (w_v, 1.0))):
        P.dma(lambda e, wsrc=wsrc: e.dma_start(out=stg[:, 0:2048].rearrange("p (h d n) -> p h d n", h=4, d=2),
                                               in_=wsrc.rearrange("h (d p) n -> p h d n", p=128)), writes=["stg"])
        P.op("dve", lambda e, wi=wi, scl=scl: e.tensor_scalar(out=Wqkv[wi][:].rearrange("p h d n -> p (h d n)"), in0=stg[:, 0:2048],
                                                              scalar1=scl, scalar2=None, op0=ALU.mult), reads=["stg"], writes=["Wqkv%d" % wi])
    Wq, Wk, Wv = Wqkv
    P.dma(lambda e: e.dma_start(out=stg[:, 0:64].rearrange("p (k n) -> p k n", k=8),
                                in_=w_in[:, COL_I:COL_I + 8].rearrange("(k p) n -> p k n", p=128), **nonc), writes=["stg"])
    for kt in range(8):
        P.op("dve", lambda e, kt=kt: e.tensor_scalar(out=Wif[:, kt, :], in0=stg[:, kt * 8:(kt + 1) * 8], scalar1=gpre[:, kt:kt + 1],
                                                     scalar2=None, op0=ALU.mult), reads=["stg", "gpre"], writes=["Wif"])
    P.dma(lambda e: e.dma_start(out=stg[:, 0:2048].rearrange("p (k n) -> p k n", k=4),
                                in_=w_glu.rearrange("(k p) n -> p k n", p=128)), writes=["stg"])
    P.op("dve", lambda e: e.tensor_copy(out=Wglu[:].rearrange("p k n -> p (k n)"), in_=stg[:, 0:2048]), reads=["stg"], writes=["Wglu"])
    for mt in range(8):
        for k in range(4):
            P.op("pool", lambda e, mt=mt, k=k: e.tensor_scalar(out=cdiag[:, mt, k, :], in0=ident_f[:], scalar1=cw[:, mt, k:k + 1],
                                                               scalar2=None, op0=ALU.mult), reads=["ident_f", "cw"], writes=["cdiag"])

    def stage_block(blk, src_ap_f, scale_gpre, nk):
        ncol = 4096 // nk
        P.dma(lambda e: e.dma_start(out=stg[:].rearrange("p (k n) -> p k n", k=nk), in_=src_ap_f), writes=["stg"])
        if scale_gpre:
            for kt in range(nk):
                P.op("dve" if kt % 2 == 0 else "pool",
                     lambda e, kt=kt: e.tensor_scalar(out=stgb[:, kt * ncol:(kt + 1) * ncol], in0=stg[:, kt * ncol:(kt + 1) * ncol],
                                                      scalar1=gpre[:, kt:kt + 1], scalar2=None, op0=ALU.mult),
                     reads=["stg", "gpre"], writes=["stgb"])
        else:
            P.op("dve", lambda e: e.tensor_copy(out=stgb[:, 0:2048], in_=stg[:, 0:2048]), reads=["stg"], writes=["stgb"])
            P.op("pool", lambda e: e.tensor_copy(out=stgb[:, 2048:4096], in_=stg[:, 2048:4096]), reads=["stg"], writes=["stgb"])
        P.dma(lambda e: e.dma_start(out=WS[blk], in_=stgb[:]), reads=["stgb"], writes=["WS%d" % blk])

    def win_cols(c0):
        return w_in[:, c0:c0 + 512].rearrange("(k p) n -> p k n", p=128)

    win_blocks = [(BLK_UA, COL_UA), (BLK_UA + 1, COL_UA + 512), (BLK_UB, COL_UB), (BLK_ZB, COL_ZB), (BLK_ZA, COL_ZA),
                  (BLK_ZA + 1, COL_ZA + 512), (BLK_OA, COL_OA), (BLK_OA + 1, COL_OA + 512)] + [(BLK_G + i, COL_G + 512 * i) for i in range(4)]
    for blk, c0 in win_blocks:
        stage_block(blk, win_cols(c0), True, 8)
    for i in range(2):
        stage_block(BLK_AO + i, w_a_out[:, 512 * i:512 * i + 512].rearrange("(k p) n -> p k n", p=128), False, 8)
        stage_block(BLK_WO + i, w_o[:, 512 * i:512 * i + 512].rearrange("(k p) n -> p k n", p=128), False, 8)
    stage_block(BLK_BO, w_b_out.rearrange("(k p) n -> p k n", p=128), False, 4)

    def small(n, name=None):
        return sbt([128, n], F32, name)

    cnt = [0]

    def el(eng, fn, r, w):
        P.op(eng, fn, reads=r, writes=w)

    def tt(outp, a, b, op, r, w, eng="dve"):
        el(eng, lambda e: e.tensor_tensor(out=outp, in0=a, in1=b, op=op), r, w)

    def ts(outp, a, s1, s2, op0, op1, r, w, eng="dve"):
        if op1 is None:
            el(eng, lambda e: e.tensor_scalar(out=outp, in0=a, scalar1=s1, scalar2=None, op0=op0), r, w)
        else:
            el(eng, lambda e: e.tensor_scalar(out=outp, in0=a, scalar1=s1, scalar2=s2, op0=op0, op1=op1), r, w)

    LR = small(32); LI = small(32); DT = small(32)
    for hf in range(2):
        sl = slice(64 * hf, 64 * hf + 64)
        P.dma(lambda e, sl=sl: e.dma_start(out=LR[sl, :], in_=lam_re.rearrange("g p -> p g"), **nonc), writes=["LR"])
        P.dma(lambda e, sl=sl: e.dma_start(out=LI[sl, :], in_=lam_im.rearrange("g p -> p g"), **nonc), writes=["LI"])
    P.dma(lambda e: e.dma_start(out=DT[:], in_=log_dt.partition_broadcast(128)), writes=["DT"])
    el("act", lambda e: e.activation(out=DT[:], in_=DT[:], func=AF.Exp), ["DT"], ["DT"])
    TH = small(32); MAG = small(32); t0 = small(32); t1 = small(32); t2 = small(32); kk = small(32)
    tt(TH[:], LI[:], DT[:], ALU.mult, ["LI", "DT"], ["TH"])
    tt(t0[:], LR[:], DT[:], ALU.mult, ["LR", "DT"], ["t0"])
    el("act", lambda e: e.activation(out=MAG[:], in_=t0[:], func=AF.Exp), ["t0"], ["MAG"])
    IMAG2 = small(32)
    el("act", lambda e: e.activation(out=IMAG2[:], in_=t0[:], func=AF.Exp, scale=-2.0), ["t0"], ["IMAG2"])

    def sin_of(dst, src, shift, nm):
        ts(t1[:], src, shift, None, ALU.add, None, [nm, "t1"], ["t1"])
        el("pool", lambda e: e.memset(kk[:], 0.0), [], ["kk"])
        for m in range(7):
            ts(t2[:], t1[:], (2 * m + 1) * PI, None, ALU.is_gt, None, ["t1"], ["t2"])
            tt(kk[:], kk[:], t2[:], ALU.add, ["kk", "t2"], ["kk"])
        ts(kk[:], kk[:], -2.0 * PI, None, ALU.mult, None, ["kk"], ["kk"])
        tt(t1[:], t1[:], kk[:], ALU.add, ["t1", "kk"], ["t1"])
        el("act", lambda e: e.activation(out=dst, in_=t1[:], func=AF.Sin), ["t1"], [nm + "_s"])

    SN = small(32); CS = small(32)
    sin_of(SN[:], TH[:], 0.0, "TH")
    sin_of(CS[:], TH[:], PI / 2.0, "TH")
    pwr = sbt([128, 9, 32], F32); pwi = sbt([128, 9, 32], F32); pnr = sbt([128, 8, 32], F32); pni = sbt([128, 8, 32], F32)
    el("pool", lambda e: e.memset(pwr[:, 0, :], 1.0), [], ["pw"]); el("pool", lambda e: e.memset(pwi[:, 0, :], 0.0), [], ["pw"])
    el("pool", lambda e: e.memset(pnr[:, 0, :], 1.0), [], ["pn"]); el("pool", lambda e: e.memset(pni[:, 0, :], 0.0), [], ["pn"])
    tt(pwr[:, 1, :], MAG[:], CS[:], ALU.mult, ["MAG", "TH_s"], ["pw"])
    tt(pwi[:, 1, :], MAG[:], SN[:], ALU.mult, ["MAG", "TH_s"], ["pw"])
    tt(pnr[:, 1, :], pwr[:, 1, :], IMAG2[:], ALU.mult, ["pw", "IMAG2"], ["pn"])
    tt(t0[:], pwi[:, 1, :], IMAG2[:], ALU.mult, ["pw", "IMAG2"], ["t0"])
    ts(pni[:, 1, :], t0[:], -1.0, None, ALU.mult, None, ["t0"], ["pn"])

    def cmul(or_, oi_, ar, ai, br, bi, r, w):
        raise NotImplementedError

    u0 = small(32); u1 = small(32)
    for k in range(1, 8):
        for (xr, xi, nm, lim) in ((pwr, pwi, "pw", 9), (pnr, pni, "pn", 8)):
            if k + 1 >= lim:
                continue
            tt(u0[:], xr[:, k, :], xr[:, 1, :], ALU.mult, [nm], ["u0"])
            tt(u1[:], xi[:, k, :], xi[:, 1, :], ALU.mult, [nm], ["u1"])
            tt(xr[:, k + 1, :], u0[:], u1[:], ALU.subtract, ["u0", "u1"], [nm])
            tt(u0[:], xr[:, k, :], xi[:, 1, :], ALU.mult, [nm], ["u0"])
            tt(u1[:], xi[:, k, :], xr[:, 1, :], ALU.mult, [nm], ["u1"])
            tt(xi[:, k + 1, :], u0[:], u1[:], ALU.add, ["u0", "u1"], [nm])
    den = small(32); qr = small(32); qi = small(32); nr = small(32)
    tt(u0[:], LR[:], LR[:], ALU.mult, ["LR"], ["u0"]); tt(u1[:], LI[:], LI[:], ALU.mult, ["LI"], ["u1"])
    tt(den[:], u0[:], u1[:], ALU.add, ["u0", "u1"], ["den"])
    el("dve", lambda e: e.reciprocal(out=den[:], in_=den[:]), ["den"], ["den"])
    ts(nr[:], pwr[:, 1, :], -1.0, None, ALU.add, None, ["pw"], ["nr"])
    tt(u0[:], nr[:], LR[:], ALU.mult, ["nr", "LR"], ["u0"]); tt(u1[:], pwi[:, 1, :], LI[:], ALU.mult, ["pw", "LI"], ["u1"])
    tt(qr[:], u0[:], u1[:], ALU.add, ["u0", "u1"], ["qr"]); tt(qr[:], qr[:], den[:], ALU.mult, ["qr", "den"], ["qr"])
    tt(u0[:], pwi[:, 1, :], LR[:], ALU.mult, ["pw", "LR"], ["u0"]); tt(u1[:], nr[:], LI[:], ALU.mult, ["nr", "LI"], ["u1"])
    tt(qi[:], u0[:], u1[:], ALU.subtract, ["u0", "u1"], ["qi"]); tt(qi[:], qi[:], den[:], ALU.mult, ["qi", "den"], ["qi"])
    Br = sbt([128, 32, 16], F32); Bi = sbt([128, 32, 16], F32); bbr = sbt([128, 32, 16], F32); bbi = sbt([128, 32, 16], F32)
    v0 = sbt([128, 32, 16], F32); v1 = sbt([128, 32, 16], F32)
    for hf in range(2):
        sl = slice(64 * hf, 64 * hf + 64)
        P.dma(lambda e, sl=sl: e.dma_start(out=Br[sl], in_=B_re.rearrange("g p n -> p g n"), **nonc), writes=["Br"])
        P.dma(lambda e, sl=sl: e.dma_start(out=Bi[sl], in_=B_im.rearrange("g p n -> p g n"), **nonc), writes=["Bi"])

    def bc(s):
        return s.unsqueeze(2).to_broadcast([128, 32, 16])

    def cmul3(orr, oii, sr, si, sn, xr, xi, xn, on):
        xn = [xn] if isinstance(xn, str) else list(xn)
        sn = [sn] if isinstance(sn, str) else list(sn)
        tt(v0[:], xr, bc(sr), ALU.mult, xn + sn, ["v0"]); tt(v1[:], xi, bc(si), ALU.mult, xn + sn, ["v1"])
        tt(orr, v0[:], v1[:], ALU.subtract, ["v0", "v1"], [on])
        tt(v0[:], xi, bc(sr), ALU.mult, xn + sn, ["v0"]); tt(v1[:], xr, bc(si), ALU.mult, xn + sn, ["v1"])
        tt(oii, v0[:], v1[:], ALU.add, ["v0", "v1"], [on])

    el("dve", lambda e: e.tensor_copy(out=u0[:], in_=qr[:]), ["qr"], ["qq"])
    cmul3(bbr[:], bbi[:], qr[:], qi[:], ["qr", "qi"], Br[:], Bi[:], ["Br", "Bi"], "bb")
    CTr = sbt([128, 32, 16], F32); CTi = sbt([128, 32, 16], F32)
    Cdup = sbt([128, 4, 2, 64], F32)
    for (Csrc, CTt, nm) in ((C_re, CTr, "CTr"), (C_im, CTi, "CTi")):
        for d in range(2):
            P.dma(lambda e, Csrc=Csrc, d=d: e.dma_start(out=Cdup[:, :, d, :], in_=Csrc.rearrange("(t g) n p -> (g n) t p", t=4)),
                  writes=["Cdup"])
        for t in range(4):
            P.op("pe", lambda e, t=t: e.transpose(out=pA[:, t * 128:(t + 1) * 128], in_=Cdup[:, t, :, :].rearrange("q d p -> q (d p)"),
                                                  identity=ident_f[:]), reads=["Cdup", "ident_f"], writes=["pA"])
        el("act", lambda e, CTt=CTt: e.copy(out=CTt[:].rearrange("p g n -> p (g n)"), in_=pA[:]), ["pA"], [nm])
    Er = sbt([128, 32, 8, 16], F32); Ei = sbt([128, 32, 8, 16], F32)
    Fr = sbt([128, 32, 8, 16], F32); Fi = sbt([128, 32, 8, 16], F32)
    for j in range(8):
        cmul3(Er[:, :, j, :], Ei[:, :, j, :], pwr[:, 7 - j, :], pwi[:, 7 - j, :], "pw", bbr[:], bbi[:], "bb", "E")
        cmul3(Fr[:, :, j, :], Fi[:, :, j, :], pwr[:, j + 1, :], pwi[:, j + 1, :], "pw", CTr[:], CTi[:], ["CTr", "CTi"], "F")
    halfm = sbt([128, 2], F32)
    el("pool", lambda e: e.memset(halfm[:], 0.0), [], ["halfm"])
    el("pool", lambda e: e.memset(halfm[0:64, 0:1], 1.0), ["halfm"], ["halfm"])
    el("pool", lambda e: e.memset(halfm[64:128, 1:2], 1.0), ["halfm"], ["halfm"])
    for ri, (Et, blk) in enumerate(((Er, BLK_S5), (Ei, BLK_S5 + 1))):
        el("pool", lambda e: e.memset(stgb[:], 0.0), ["stgb"], ["stgb"])
        for g in range(32):
            ps_ = pA if g % 2 == 0 else pB
            nm = "pA" if g % 2 == 0 else "pB"
            P.op("pe", lambda e, Et=Et, g=g, ps_=ps_: e.transpose(out=ps_[:, 0:64], in_=Et[0:64, g, :, :].rearrange("p j n -> p (j n)"),
                                                                  identity=ident_f[0:64, 0:64]), reads=["E", "ident_f"], writes=[nm])
            c0 = g * 128 + 64 * (g % 2)
            el("act" if g % 2 == 0 else "dve",
               (lambda e, ps_=ps_, c0=c0: e.copy(out=stgb[:, c0:c0 + 64], in_=ps_[:, 0:64])) if g % 2 == 0 else
               (lambda e, ps_=ps_, c0=c0: e.tensor_copy(out=stgb[:, c0:c0 + 64], in_=ps_[:, 0:64])), [nm], ["stgb"])
        P.dma(lambda e, blk=blk: e.dma_start(out=WS[blk], in_=stgb[:]), reads=["stgb"], writes=["WS%d" % blk])
    for ri, (Ft, blk, sg) in enumerate(((Fr, BLK_S5 + 3, 1.0), (Fi, BLK_S5 + 4, -1.0))):
        for g in range(32):
            P.op("dve" if g % 2 == 0 else "pool",
                 lambda e, Ft=Ft, g=g, sg=sg: e.tensor_scalar(out=stgb[:, g * 128:(g + 1) * 128], in0=Ft[:, g, :, :].rearrange("p j n -> p (j n)"),
                                                              scalar1=halfm[:, (g % 2):(g % 2) + 1], scalar2=sg, op0=ALU.mult, op1=ALU.mult),
                 reads=["F", "halfm"], writes=["stgb"])
        P.dma(lambda e, blk=blk: e.dma_start(out=WS[blk], in_=stgb[:]), reads=["stgb"], writes=["WS%d" % blk])
    Gr = sbt([128, 32, 8, 16], F32); Gni = sbt([128, 32, 8, 16], F32)
    i8r = small(32); i8i = small(32)
    tt(u0[:], pnr[:, 7, :], pnr[:, 1, :], ALU.mult, ["pn"], ["u0"]); tt(u1[:], pni[:, 7, :], pni[:, 1, :], ALU.mult, ["pn"], ["u1"])
    tt(i8r[:], u0[:], u1[:], ALU.subtract, ["u0", "u1"], ["i8"])
    tt(u0[:], pnr[:, 7, :], pni[:, 1, :], ALU.mult, ["pn"], ["u0"]); tt(u1[:], pni[:, 7, :], pnr[:, 1, :], ALU.mult, ["pn"], ["u1"])
    tt(i8i[:], u0[:], u1[:], ALU.add, ["u0", "u1"], ["i8"])
    for j in range(8):
        cmul3(Gr[:, :, j, :], Gni[:, :, j, :], i8r[:], i8i[:], "i8", Fr[:, :, j, :], Fi[:, :, j, :], "F", "G")
    ts(Gni[:], Gni[:], -1.0, None, ALU.mult, None, ["G"], ["G"])
    bmask = sbt([128, 8, 16], F32); Dcol = sbt([128, 32], F32)
    el("pool", lambda e: e.memset(bmask[:], 1.0), [], ["bmask"])
    el("pool", lambda e: e.affine_select(out=bmask[:], in_=bmask[:], pattern=[[16, 8], [0, 16]], compare_op=ALU.is_ge, fill=0.0,
                                         base=15, channel_multiplier=-1), ["bmask"], ["bmask"])
    for j in range(8):
        P.dma(lambda e, j=j: e.dma_start(out=Dcol[16 * j:16 * j + 16, :], in_=D_skip.rearrange("g n -> n g"), **nonc), writes=["Dcol"])
    wtmp = sbt([128, 128], F32)
    for g in range(32):
        ps_ = pA if g % 2 == 0 else pB
        nm = "pA" if g % 2 == 0 else "pB"
        P.op("pe", lambda e, g=g, ps_=ps_: e.matmul(ps_[:, 0:128], lhsT=Er[0:64, g, :, :].rearrange("p j n -> p (j n)"),
                                                    rhs=Gr[0:64, g, :, :].rearrange("p j n -> p (j n)"), start=True, stop=False),
             reads=["E", "G"], writes=[nm], sig=False)
        P.op("pe", lambda e, g=g, ps_=ps_: e.matmul(ps_[:, 0:128], lhsT=Ei[0:64, g, :, :].rearrange("p j n -> p (j n)"),
                                                    rhs=Gni[0:64, g, :, :].rearrange("p j n -> p (j n)"), start=False, stop=True),
             reads=["E", "G"], writes=[nm])
        tt(wtmp[:], ps_[:, 0:128], bmask[:].rearrange("p j n -> p (j n)"), ALU.mult, [nm, "bmask"], ["wtmp"])
        el("dve", lambda e, g=g: e.scalar_tensor_tensor(out=stgb[:, g * 128:(g + 1) * 128], in0=ident_f[:], scalar=Dcol[:, g:g + 1],
                                                        in1=wtmp[:], op0=ALU.mult, op1=ALU.add), ["ident_f", "Dcol", "wtmp", "stgb"], ["stgb"])
    P.dma(lambda e: e.dma_start(out=WS[BLK_S5 + 2], in_=stgb[:]), reads=["stgb"], writes=["WS%d" % (BLK_S5 + 2)])
    for g2 in range(2):
        sl = slice(64 * g2, 64 * g2 + 64)
        src_r = pwr[sl, 8, :].rearrange("p (q t) -> p q t", t=2)[:, :, g2]
        src_i = pwi[sl, 8, :].rearrange("p (q t) -> p q t", t=2)[:, :, g2]
        el("dve", lambda e, sl=sl, src_r=src_r: e.tensor_copy(out=AR32[sl, 0, :], in_=src_r), ["pw"], ["AR32"])
        el("dve", lambda e, sl=sl, src_r=src_r: e.tensor_copy(out=AR32[sl, 1, :], in_=src_r), ["pw"], ["AR32"])
        el("dve", lambda e, sl=sl, src_i=src_i: e.tensor_copy(out=API[sl, :], in_=src_i), ["pw"], ["API"])
        ts(ANI[sl, :], src_i, -1.0, None, ALU.mult, None, ["pw"], ["ANI"])

    if dbg_stop == 1:
        tk = P.dma(lambda e: e.dma_start(out=out[0:128, :], in_=gpost[:]), reads=["gpost"])
        P.final_wait("sp", [tk])
        P.emit(); P.temp.close(); P.close()
        return nc
    P.emit()
    P.temp.close()
    P.temp = None
    NSLOT = 3
    wslot = [sbt([128, 4096], BF16, "wslot%d" % i) for i in range(NSLOT)]
    slot_rr = [0]

    def load_block(blk):
        s = slot_rr[0] % NSLOT
        slot_rr[0] += 1
        P.dma(lambda e: e.dma_start(out=wslot[s][:], in_=WS[blk]), reads=["WS%d" % blk], writes=["wslot%d" % s])
        return wslot[s], "wslot%d" % s

    xs = [sbt([128, 1024], F32, "xs%d" % i) for i in range(2)]
    junk = sbt([128, 1024], BF16); xnb = sbt([128, 1024], BF16)
    ss = small(1); rstd = small(1)
    xnT = sbt([128, 8, T], BF16, "xnT"); uaT = sbt([128, 8, T + 4], BF16, "uaT"); cT = sbt([128, 8, T], BF16, "cT")
    zaT = sbt([128, 8, T], BF16, "zaT"); oaT = sbt([128, 8, T], BF16, "oaT"); zbT = sbt([128, 4, T], BF16, "zbT")
    qT = sbt([128, 4, 2, T], BF16, "qT"); kT = sbt([128, 4, 2, T], BF16, "kT")
    ktm = sbt([128, NCH, 4, 256], BF16, "ktm"); vw = sbt([128, NCH, 4, 256], BF16, "vw")
    gates = sbt([128, NCH, 8], F32, "gates")
    hfT = sbt([128, 8, T], BF16, "hfT"); mrgT = sbt([128, 8, T], BF16, "mrgT"); hbT = sbt([128, 4, T], BF16, "hbT")
    Cst = sbt([128, 4, 2, 256], F32, "Cst"); Cbf = sbt([128, 4, 2, 256], BF16, "Cbf")
    nst = sbt([128, 4, 2], F32, "nst"); nrep = sbt([128, 4, 2, 128], BF16, "nrep")
    Utm = sbt([128, 32, 8, 16], BF16, "Utm"); U2 = sbt([128, 32, NC8], BF16, "U2")
    Xall = sbt([128, NC8, 2, 16], F32, "Xall"); Sall = sbt([128, NC8 + 1, 2, 16], F32, "Sall"); Sbf = sbt([128, 2, 16, NC8], BF16, "Sbf")
    Yg = sbt([128, 32, NC8], BF16, "Yg"); Ytm = sbt([128, 8, 512], BF16, "Ytm"); yT = sbt([128, 4, T], BF16, "yT")
    for (t_, nm) in ((Cst, "Cst"), (Cbf, "Cbf"), (nst, "nst"), (nrep, "nrep"), (Sall, "Sall"), (uaT, "uaT")):
        flat = t_[:]
        el("pool", lambda e, flat=flat: e.memset(flat, 0.0), [], [nm])

    lfrep = sbt([128, 4, 128], F32); lf = small(4, "lf"); ig = small(4); bcol = small(4); wcol = small(4, "wcol"); wcolb = sbt([128, 4, 2], BF16)
    wrep = sbt([128, 4, 128], BF16); emb = sbt([128, 4, 128], F32, "emb"); dec = small(4, "dec"); SmT = sbt([128, 4, 128], BF16, "SmT")
    aden = sbt([128, 4, 128], F32); rden = sbt([128, 4, 128], F32, "rden"); hT = sbt([128, 8, 128], F32, "hT"); sq = sbt([128, 8, 128], BF16)
    rsh = sbt([128, 4, 128], F32); hn = sbt([128, 8, 128], F32, "hn"); e1 = small(4)
    gAll = sbt([128, 8, T], BF16); m1 = sbt([128, T], F32); m2 = sbt([128, T], F32)
    ot = [sbt([128, 1024], F32, "ot%d" % i) for i in range(2)]
    ss2 = small(1); rstd2 = small(1); sgl = sbt([128, T], BF16); xg = sbt([128, T], F32)

    def rsqrt_col(dst, src, scale, nm_src, nm_dst):
        ts(dst, src, scale, 1e-6, ALU.mult, ALU.add, [nm_src], [nm_dst])
        el("act", lambda e: e.activation(out=dst, in_=dst, func=AF.Ln), [nm_dst], [nm_dst])
        el("act", lambda e: e.activation(out=dst, in_=dst, func=AF.Exp, scale=-0.5), [nm_dst], [nm_dst])

    def mm(outp, lhsT, rhs, start, stop, r, w, sig=None):
        P.op("pe", lambda e: e.matmul(outp, lhsT=lhsT, rhs=rhs, start=start, stop=stop), reads=r, writes=w,
             sig=(stop if sig is None else sig))

    acc_rr = [0]

    def acc_bank():
        acc_rr[0] += 1
        return (pA, "pA") if acc_rr[0] % 2 else (pB, "pB")

    out_toks = []

    class StopBuild(Exception):
        pass

    def chk(level):
        if dbg_stop == level:
            raise StopBuild()

    def do_tile(xsrc, row0, main, ti):
        for sub in range(NCH):
            xb_, xnm = xs[sub % 2], "xs%d" % (sub % 2)
            P.dma(lambda e, xb_=xb_, sub=sub: e.dma_start(out=xb_[:], in_=xsrc[row0 + sub * 128: row0 + (sub + 1) * 128, :]), writes=[xnm])
            el("act", lambda e, xb_=xb_: e.activation(out=junk[:], in_=xb_[:], func=AF.Square, accum_out=ss[:]), [xnm], ["junk", "ss"])
            rsqrt_col(rstd[:], ss[:], 1.0 / 1024.0, "ss", "rstd")
            ts(xnb[:], xb_[:], rstd[:, 0:1], None, ALU.mult, None, [xnm, "rstd"], ["xnb"])
            for kt in range(8):
                P.op("pe", lambda e, kt=kt: e.transpose(out=pT[:, kt * 128:(kt + 1) * 128], in_=xnb[:, kt * 128:(kt + 1) * 128], identity=ident_b[:]),
                     reads=["xnb", "ident_b"], writes=["pT"], sig=(kt == 7))
            el("act", lambda e, sub=sub: e.copy(out=xnT[:, :, sub * 128:(sub + 1) * 128], in_=pT[:].rearrange("p (k t) -> p k t", k=8)),
               ["pT"], ["xnT"])

        chk(2)

        def proj_fm(blk, ncolt, evac):
            W, wn = load_block(blk)
            Wv_ = W[:].rearrange("p (k n) -> p k n", k=8)
            for ct in range(ncolt):
                ps_, pn = acc_bank()
                for kt in range(8):
                    mm(ps_[:, 0:T], Wv_[:, kt, ct * 128:(ct + 1) * 128], xnT[:, kt, :], kt == 0, kt == 7, [wn, "xnT"], [pn])
                evac(ct, ps_, pn)

        el("dve", lambda e: e.tensor_copy(out=uaT[:, :, 1:4], in_=uaT[:, :, T + 1:T + 4]), ["uaT"], ["uaT"])
        for half in range(2):
            proj_fm(BLK_UA + half, 4, lambda ct, ps_, pn, half=half: el(
                "act", lambda e: e.copy(out=uaT[:, half * 4 + ct, 4:T + 4], in_=ps_[:, 0:T]), [pn], ["uaT"]))
        chk(3)
        for ch in range(NCH):
            for kt in range(8):
                mm(pM[:, 0:8], xnT[:, kt, ch * 128:(ch + 1) * 128], Wif[:, kt, :], kt == 0, kt == 7, ["xnT", "Wif"], ["pM"])
            tt(gates[:, ch, :], pM[:, 0:8], bif[:], ALU.add, ["pM", "bif"], ["gates"])
        chk(4)
        W, wn = load_block(BLK_UB)
        Wv_ = W[:].rearrange("p (k n) -> p k n", k=8)
        for j in range(8):
            ps_, pn = acc_bank()
            for kt in range(8):
                lhs = xnT[:, kt, :].rearrange("p (c j) -> p j c", j=8)[:, j, :]
                mm(ps_[0:NC8, :], lhs, Wv_[:, kt, :], kt == 0, kt == 7, [wn, "xnT"], [pn])
            el("act" if j % 2 else "dve",
               (lambda e, ps_=ps_, j=j: e.copy(out=Utm[0:NC8, :, j, :], in_=ps_[0:NC8, :].rearrange("c (g n) -> c g n", g=32))) if j % 2 else
               (lambda e, ps_=ps_, j=j: e.tensor_copy(out=Utm[0:NC8, :, j, :], in_=ps_[0:NC8, :].rearrange("c (g n) -> c g n", g=32))), [pn], ["Utm"])
        chk(5)
        for g in range(32):
            P.op("pe", lambda e, g=g: e.transpose(out=pT[:, g * NC8:(g + 1) * NC8], in_=Utm[0:NC8, g, :, :].rearrange("c j n -> c (j n)"),
                                                  identity=ident_b[0:NC8, 0:NC8]), reads=["Utm", "ident_b"], writes=["pT"], sig=(g == 31))
        el("act", lambda e: e.copy(out=U2[:].rearrange("p g c -> p (g c)"), in_=pT[:, 0:32 * NC8]), ["pT"], ["U2"])
        W1r, w1rn = load_block(BLK_S5)
        W1i, w1in = load_block(BLK_S5 + 1)
        for ri, (Wt, wn_) in enumerate(((W1r, w1rn), (W1i, w1in))):
            Wg = Wt[:].rearrange("p (g n) -> p g n", g=32)
            for q in range(16):
                for g2 in range(2):
                    g = 2 * q + g2
                    mm(pM[:, (q * NC8):(q + 1) * NC8], Wg[:, g, :], U2[:, g, :], g2 == 0, g2 == 1, [wn_, "U2"], ["pM"], sig=(q == 15 and g2 == 1))
            el("act" if ri == 0 else "dve",
               (lambda e, ri=ri: e.copy(out=Xall[:, :, ri, :].rearrange("p c q -> p q c"), in_=pM[:, 0:16 * NC8].rearrange("p (q c) -> p q c", q=16))) if ri == 0 else
               (lambda e, ri=ri: e.tensor_copy(out=Xall[:, :, ri, :].rearrange("p c q -> p q c"), in_=pM[:, 0:16 * NC8].rearrange("p (q c) -> p q c", q=16))),
               ["pM"], ["Xall"])
        chk(6)
        T1 = sbt([128, 2, 16], F32, "scT1_%d" % ti) if False else None
        for c in range(NC8):
            sp_ = Sall[:, c, :, :]
            sn_ = Sall[:, c + 1, :, :]
            P.op("pool", lambda e, sp_=sp_: e.tensor_tensor(out=sc_t1[:], in0=sp_, in1=AR32[:], op=ALU.mult), reads=["Sall"], writes=["sc_t1"])
            P.op("pool", lambda e, c=c: e.tensor_tensor(out=sc_t2[:, 0, :], in0=Sall[:, c, 1, :], in1=ANI[:], op=ALU.mult), reads=["Sall"], writes=["sc_t2"])
            P.op("pool", lambda e, c=c: e.tensor_tensor(out=sc_t2[:, 1, :], in0=Sall[:, c, 0, :], in1=API[:], op=ALU.mult), reads=["Sall"], writes=["sc_t2"])
            P.op("pool", lambda e, c=c: e.tensor_tensor(out=sc_t1[:], in0=sc_t1[:], in1=Xall[:, c, :, :], op=ALU.add), reads=["sc_t1", "Xall"], writes=["sc_t1"])
            P.op("pool", lambda e, sn_=sn_: e.tensor_tensor(out=sn_, in0=sc_t1[:], in1=sc_t2[:], op=ALU.add), reads=["sc_t1", "sc_t2"], writes=["Sall"])
        if main:
            for ri in range(2):
                el("act" if ri else "dve",
                   (lambda e, ri=ri: e.copy(out=Sbf[:, ri, :, :], in_=Sall[:, 0:NC8, ri, :].rearrange("p c q -> p q c"))) if ri else
                   (lambda e, ri=ri: e.tensor_copy(out=Sbf[:, ri, :, :], in_=Sall[:, 0:NC8, ri, :].rearrange("p c q -> p q c"))),
                   ["Sall"], ["Sbf"])
        el("pool", lambda e: e.tensor_copy(out=Sall[:, 0, :, :], in_=Sall[:, NC8, :, :]), ["Sall"], ["Sall"])
        if main:
            proj_fm(BLK_ZB, 4, lambda ct, ps_, pn: el("act", lambda e: e.activation(out=zbT[:, ct, :], in_=ps_[:, 0:T], func=AF.Silu), [pn], ["zbT"]))
            Wi_, win_ = load_block(BLK_S5 + 2)
            Wr_, wrn_ = load_block(BLK_S5 + 3)
            Wm_, wmn_ = load_block(BLK_S5 + 4)
            Wi_g = Wi_[:].rearrange("p (g n) -> p g n", g=32); Wr_g = Wr_[:].rearrange("p (g n) -> p g n", g=32)
            Wm_g = Wm_[:].rearrange("p (g n) -> p g n", g=32)
            GP = 512 // NC8
            for g0 in range(0, 32, GP):
                ng = min(GP, 32 - g0)
                for gi in range(ng):
                    g = g0 + gi
                    o_ = pM[:, gi * NC8:(gi + 1) * NC8]
                    mm(o_, Wi_g[:, g, :], U2[:, g, :], True, False, [win_, "U2"], ["pM"], sig=False)
                    mm(o_, Wr_g[:, g, :], Sbf[:, 0, g // 2, :], False, False, [wrn_, "Sbf"], ["pM"], sig=False)
                    mm(o_, Wm_g[:, g, :], Sbf[:, 1, g // 2, :], False, True, [wmn_, "Sbf"], ["pM"], sig=(gi == ng - 1))
                el("act", lambda e, g0=g0, ng=ng: e.activation(out=Yg[:, g0:g0 + ng, :].rearrange("p g c -> p (g c)"), in_=pM[:, 0:ng * NC8],
                                                               func=AF.Gelu_apprx_tanh), ["pM"], ["Yg"])
            for g0 in range(0, 32, 8):
                for gi in range(8):
                    g = g0 + gi
                    P.op("pe", lambda e, g=g, gi=gi: e.transpose(out=pT[0:NC8, gi * 128:(gi + 1) * 128], in_=Yg[:, g, :], identity=ident_b[:]),
                         reads=["Yg", "ident_b"], writes=["pT"], sig=(gi == 7))
                el("act", lambda e, g0=g0: e.copy(out=Ytm[0:NC8, :, 16 * g0:16 * g0 + 128].rearrange("c j (g n) -> c g j n", g=8),
                                                  in_=pT[0:NC8, :].rearrange("c (g j n) -> c g j n", g=8, j=8)), ["pT"], ["Ytm"])
            for ct in range(4):
                for j in range(8):
                    P.op("pe", lambda e, ct=ct, j=j: e.transpose(out=pT[:, j * NC8:(j + 1) * NC8], in_=Ytm[0:NC8, j, ct * 128:(ct + 1) * 128],
                                                                 identity=ident_b[0:NC8, 0:NC8]), reads=["Ytm", "ident_b"], writes=["pT"], sig=(j == 7))
                el("act", lambda e, ct=ct: e.copy(out=yT[:, ct, :], in_=pT[:, 0:T]), ["pT"], ["yT"])
            for ot_ in range(4):
                ps_, pn = acc_bank()
                for ct in range(4):
                    mm(ps_[:, 0:T], Wglu[:, ct, ot_ * 128:(ot_ + 1) * 128], yT[:, ct, :], ct == 0, ct == 3, ["Wglu", "yT"], [pn])
                el("act", lambda e, ps_=ps_, ot_=ot_: e.activation(out=sgl[:], in_=ps_[:, 0:T], func=AF.Sigmoid, bias=bglu[:, ot_:ot_ + 1]),
                   [pn, "bglu"], ["sgl"])
                tt(xg[:], sgl[:], yT[:, ot_, :], ALU.mult, ["sgl", "yT"], ["xg"])
                tt(hbT[:, ot_, :].rearrange("p (j c) -> p j c", j=8), xg[:].rearrange("p (j c) -> p j c", j=8),
                   zbT[:, ot_, :].rearrange("p (c j) -> p j c", j=8), ALU.mult, ["xg", "zbT"], ["hbT"])
        chk(7)
        for mt in range(8):
            ps_, pn = acc_bank()
            for k in range(4):
                mm(ps_[:, 0:T], cdiag[:, mt, k, :], uaT[:, mt, k + 1:k + 1 + T], k == 0, k == 3, ["cdiag", "uaT"], [pn])
            el("act", lambda e, ps_=ps_, mt=mt: e.activation(out=cT[:, mt, :], in_=ps_[:, 0:T], func=AF.Silu, bias=cb[:, mt:mt + 1]),
               [pn, "cb"], ["cT"])
        if main:
            for half in range(2):
                proj_fm(BLK_ZA + half, 4, lambda ct, ps_, pn, half=half: el(
                    "act", lambda e: e.activation(out=zaT[:, half * 4 + ct, :], in_=ps_[:, 0:T], func=AF.Silu), [pn], ["zaT"]))
            for half in range(2):
                proj_fm(BLK_OA + half, 4, lambda ct, ps_, pn, half=half: el(
                    "act", lambda e: e.activation(out=oaT[:, half * 4 + ct, :], in_=ps_[:, 0:T], func=AF.Sigmoid), [pn], ["oaT"]))
        chk(8)
        for h in range(4):
            if main:
                for (Wx, wxn, dst, dn) in ((Wq, "Wqkv0", qT, "qT"), (Wk, "Wqkv1", kT, "kT")):
                    for et in range(2):
                        ps_, pn = acc_bank()
                        for d in range(2):
                            mm(ps_[:, 0:T], Wx[:, h, d, et * 128:(et + 1) * 128], cT[:, 2 * h + d, :], d == 0, d == 1, [wxn, "cT"], [pn])
                        el("act" if et else "dve",
                           (lambda e, ps_=ps_, dst=dst, h=h, et=et: e.copy(out=dst[:, h, et, :], in_=ps_[:, 0:T])) if et else
                           (lambda e, ps_=ps_, dst=dst, h=h, et=et: e.tensor_copy(out=dst[:, h, et, :], in_=ps_[:, 0:T])), [pn], [dn])
        for ch in range(NCH):
            tsl = slice(ch * 128, (ch + 1) * 128)
            ts(ig[:], gates[:, ch, 0:4], 1.0, None, ALU.mult, None, ["gates"], ["ig"])
            el("act", lambda e, ch=ch: e.activation(out=e1[:], in_=gates[:, ch, 4:8], func=AF.Exp, scale=-1.0), ["gates"], ["e1"])
            el("act", lambda e: e.activation(out=lf[:], in_=e1[:], func=AF.Ln, bias=1.0), ["e1"], ["lf"])
            ts(lf[:], lf[:], -1.0, None, ALU.mult, None, ["lf"], ["lf"])
            for h in range(4):
                el("dve", lambda e, h=h: e.tensor_copy(out=lfrep[:, h, :], in_=lf[:, h:h + 1].to_broadcast([128, 128])), ["lf"], ["lfrep"])
            for h in range(4):
                mm(pGD[:, h * 128:(h + 1) * 128], lfrep[:, h, :], maskT[:], True, True, ["lfrep", "maskT"], ["pGD"], sig=(h == 3))
            for h in range(4):
                mm(pM[:, h * 128:(h + 1) * 128], maskT[:], lfrep[:, h, :], True, True, ["maskT", "lfrep"], ["pM"], sig=(h == 3))
            el("act", lambda e: e.activation(out=emb[:].rearrange("p h t -> p (h t)"), in_=pGD[:], func=AF.Exp, scale=-1.0), ["pGD"], ["emb"])
            el("dve", lambda e: e.reciprocal(out=dec[:], in_=emb[:, :, 127]), ["emb"], ["dec"])
            tt(bcol[:], ig[:], pM[:].rearrange("p (h t) -> p h t", h=4)[:, :, 0], ALU.subtract, ["ig", "pM"], ["bcol"])
            el("act", lambda e: e.activation(out=wcol[:], in_=bcol[:], func=AF.Exp), ["bcol"], ["wcol"])
            el("dve", lambda e: e.tensor_copy(out=wcolb[:], in_=wcol[:].unsqueeze(2).to_broadcast([128, 4, 2])), ["wcol"], ["wcolb"])
            chk(81)
            for h in range(4):
                ps_, pn = acc_bank()
                no_v = dbg_stop in (821, 822, 823, 824)
                no_k = dbg_stop in (824,)
                for d in range(2):
                    if no_k:
                        continue
                    mm(ps_[:, 0:256], cT[:, 2 * h + d, tsl], Wk[:, h, d, :], d == 0, d == 1, ["cT", "Wqkv1"], [pn], sig=(no_v and d == 1))
                for d in range(2):
                    if no_v:
                        continue
                    mm(ps_[:, 256:512], uaT[:, 2 * h + d, 4 + ch * 128:4 + (ch + 1) * 128], Wv[:, h, d, :], d == 0, d == 1, ["uaT", "Wqkv2"], [pn])
                if dbg_stop != 822:
                    el("act", lambda e, ps_=ps_, ch=ch, h=h: e.copy(out=ktm[:, ch, h, :], in_=ps_[:, 0:256]), [pn], ["ktm"])
                if dbg_stop not in (822, 823):
                    el("act", lambda e, ps_=ps_, ch=ch, h=h: e.activation(out=vw[:, ch, h, :], in_=ps_[:, 256:512], func=AF.Copy, scale=wcol[:, h:h + 1]),
                       [pn, "wcol"], ["vw"])
            if main:
                for h in range(4):
                    el("dve", lambda e, h=h: e.tensor_copy(out=wrep[:, h, :], in_=wcol[:, h:h + 1].to_broadcast([128, 128])), ["wcol"], ["wrep"])
                for h in range(4):
                    for et in range(2):
                        mm(pS[:, h * 128:(h + 1) * 128], kT[:, h, et, tsl], qT[:, h, et, tsl], et == 0, et == 1, ["kT", "qT"], ["pS"], sig=(h == 3 and et == 1))
                tt(SmT[:].rearrange("p h t -> p (h t)"), pS[:], mask4[:].rearrange("p h t -> p (h t)"), ALU.mult, ["pS", "mask4"], ["SmT"])
                for h in range(4):
                    for d2 in range(2):
                        pn_, pnn = (pN0, "pN0") if h < 2 else (pN1, "pN1")
                        o_ = pn_[:, ((h % 2) * 2 + d2) * 128:((h % 2) * 2 + d2 + 1) * 128]
                        mm(o_, vw[:, ch, h, d2 * 128:(d2 + 1) * 128], SmT[:, h, :], True, False, ["vw", "SmT"], [pnn], sig=False)
                        mm(o_, Cbf[:, h, 0, d2 * 128:(d2 + 1) * 128], qT[:, h, 0, tsl], False, False, ["Cbf", "qT"], [pnn], sig=False)
                        mm(o_, Cbf[:, h, 1, d2 * 128:(d2 + 1) * 128], qT[:, h, 1, tsl], False, True, ["Cbf", "qT"], [pnn], sig=(h % 2 == 1 and d2 == 1))
                    o_ = pGD[:, h * 128:(h + 1) * 128]
                    mm(o_, wrep[:, h, :], SmT[:, h, :], True, False, ["wrep", "SmT", "emb", "dec"], ["pGD"], sig=False)
                    mm(o_, nrep[:, h, 0, :], qT[:, h, 0, tsl], False, False, ["nrep", "qT"], ["pGD"], sig=False)
                    mm(o_, nrep[:, h, 1, :], qT[:, h, 1, tsl], False, True, ["nrep", "qT"], ["pGD"], sig=(h == 3))
                el("act", lambda e: e.activation(out=aden[:].rearrange("p h t -> p (h t)"), in_=pGD[:], func=AF.Abs), ["pGD"], ["aden"])
                tt(aden[:], aden[:], emb[:], ALU.max, ["aden", "emb"], ["aden"])
                el("dve", lambda e: e.reciprocal(out=rden[:], in_=aden[:]), ["aden"], ["rden"])
                for h in range(4):
                    pn_, pnn = (pN0, "pN0") if h < 2 else (pN1, "pN1")
                    for d2 in range(2):
                        o_ = pn_[:, ((h % 2) * 2 + d2) * 128:((h % 2) * 2 + d2 + 1) * 128]
                        tt(hT[:, 2 * h + d2, :], o_, rden[:, h, :], ALU.mult, [pnn, "rden"], ["hT"])
                tt(hT[:], hT[:], oaT[:, :, tsl], ALU.mult, ["hT", "oaT"], ["hT"])
                el("act", lambda e: e.activation(out=sq[:], in_=hT[:], func=AF.Square), ["hT"], ["sq"])
                for h in range(4):
                    for d2 in range(2):
                        mm(pS[:, h * 128:(h + 1) * 128], ones_b[:], sq[:, 2 * h + d2, :], d2 == 0, d2 == 1, ["ones_b", "sq", "SmT"], ["pS"], sig=(h == 3 and d2 == 1))
                ts(rsh[:].rearrange("p h t -> p (h t)"), pS[:], 1.0 / 256.0, 1e-6, ALU.mult, ALU.add, ["pS"], ["rsh"])
                el("act", lambda e: e.activation(out=rsh[:], in_=rsh[:], func=AF.Ln), ["rsh"], ["rsh"])
                el("act", lambda e: e.activation(out=rsh[:], in_=rsh[:], func=AF.Exp, scale=-0.5), ["rsh"], ["rsh"])
                for mt in range(8):
                    el("dve", lambda e, mt=mt: e.scalar_tensor_tensor(out=hn[:, mt, :], in0=hT[:, mt, :], scalar=hg[:, mt:mt + 1],
                                                                      in1=rsh[:, mt // 2, :], op0=ALU.mult, op1=ALU.mult), ["hT", "hg", "rsh"], ["hn"])
                    el("dve", lambda e, mt=mt: e.scalar_tensor_tensor(out=hn[:, mt, :], in0=cT[:, mt, tsl], scalar=skp[:, mt:mt + 1],
                                                                      in1=hn[:, mt, :], op0=ALU.mult, op1=ALU.add), ["cT", "skp", "hn"], ["hn"])
                tt(hfT[:, :, tsl], hn[:], zaT[:, :, tsl], ALU.mult, ["hn", "zaT"], ["hfT"])
            chk(82)
            chk(821)
            chk(822)
            chk(823)
            chk(824)
            for h in range(4):
                for et in range(2):
                    ps_, pn = acc_bank()
                    mm(ps_[:, 0:256], ktm[:, ch, h, et * 128:(et + 1) * 128], vw[:, ch, h, :], True, True, ["ktm", "vw"], [pn], sig=False)
                    mm(ps_[:, 256:258], ktm[:, ch, h, et * 128:(et + 1) * 128], wcolb[:, h, :], True, True, ["ktm", "wcolb"], [pn])
                    tt(Cst[:, h, et, :], Cst[:, h, et, :], ps_[:, 0:256], ALU.add, ["Cst", pn, "Cbf"], ["Cst"])
                    tt(nst[:, h, et:et + 1], nst[:, h, et:et + 1], ps_[:, 256:257], ALU.add, ["nst", pn, "nrep"], ["nst"])
                ts(Cst[:, h, :, :], Cst[:, h, :, :], dec[:, h:h + 1], None, ALU.mult, None, ["Cst", "dec"], ["Cst"], eng="pool")
                ts(nst[:, h, :], nst[:, h, :], dec[:, h:h + 1], None, ALU.mult, None, ["nst", "dec"], ["nst"])
                el("act", lambda e, h=h: e.copy(out=Cbf[:, h, :, :], in_=Cst[:, h, :, :]), ["Cst"], ["Cbf"])
                for et in range(2):
                    el("dve", lambda e, h=h, et=et: e.tensor_copy(out=nrep[:, h, et, :], in_=nst[:, h, et:et + 1].to_broadcast([128, 128])),
                       ["nst"], ["nrep"])
        chk(9)
        if not main:
            return
        def gate_blocks(br):
            for bi_ in range(2):
                Wg_, wgn = load_block(BLK_G + 2 * br + bi_)
                Wgv = Wg_[:].rearrange("p (k n) -> p k n", k=8)
                for c4 in range(4):
                    ft = bi_ * 4 + c4
                    psg, png = acc_bank()
                    for kt in range(8):
                        mm(psg[:, 0:T], Wgv[:, kt, c4 * 128:(c4 + 1) * 128], xnT[:, kt, :], kt == 0, kt == 7, [wgn, "xnT"], [png])
                    el("act", lambda e, psg=psg, ft=ft: e.activation(out=gAll[:, ft, :], in_=psg[:, 0:T], func=AF.Sigmoid), [png], ["gAll"])

        gate_blocks(0)
        for hf_ in range(2):
            Wa, wan = load_block(BLK_AO + hf_)
            Wav = Wa[:].rearrange("p (k n) -> p k n", k=8)
            for c4 in range(4):
                ft = hf_ * 4 + c4
                psa, pna = acc_bank()
                for kt in range(8):
                    mm(psa[:, 0:T], Wav[:, kt, c4 * 128:(c4 + 1) * 128], hfT[:, kt, :], kt == 0, kt == 7, [wan, "hfT"], [pna])
                tt(mrgT[:, ft, :], psa[:, 0:T], gAll[:, ft, :], ALU.mult, [pna, "gAll"], ["mrgT"])
        gate_blocks(1)
        Wbo, wbon = load_block(BLK_BO)
        Wbv = Wbo[:].rearrange("p (k n) -> p k n", k=4)
        for ft in range(8):
            psb, pnb = acc_bank()
            for kt in range(4):
                mm(psb[:, 0:T], Wbv[:, kt, ft * 128:(ft + 1) * 128], hbT[:, kt, :], kt == 0, kt == 3, [wbon, "hbT"], [pnb])
            el("act", lambda e, psb=psb: e.copy(out=m2[:].rearrange("p (c j) -> p j c", j=8), in_=psb[:, 0:T].rearrange("p (j c) -> p j c", j=8)),
               [pnb], ["m2"])
            tt(m2[:], m2[:], gAll[:, ft, :], ALU.mult, ["m2", "gAll"], ["m2"])
            tt(mrgT[:, ft, :], m2[:], mrgT[:, ft, :], ALU.add, ["m2", "mrgT"], ["mrgT"], eng="pool")
        Wo0, wo0n = load_block(BLK_WO); Wo1, wo1n = load_block(BLK_WO + 1)
        for ch in range(NCH):
            tsl = slice(ch * 128, (ch + 1) * 128)
            for hf_, (Wo_, won, pn_, pnn) in enumerate(((Wo0, wo0n, pN0, "pN0"), (Wo1, wo1n, pN1, "pN1"))):
                Wov = Wo_[:].rearrange("p (k n) -> p k n", k=8)
                for kt in range(8):
                    mm(pn_[:, :], mrgT[:, kt, tsl], Wov[:, kt, :], kt == 0, kt == 7, ["mrgT", won], [pnn])
            el("act", lambda e: e.activation(out=junk[:, 0:512], in_=pN0[:], func=AF.Square, accum_out=ss2[:]), ["pN0"], ["junk", "ss2"])
            el("act", lambda e: e.activation(out=junk[:, 512:1024], in_=pN1[:], func=AF.Square, accum_out=rstd2[:]), ["pN1"], ["junk", "rstd2"])
            tt(ss2[:], ss2[:], rstd2[:], ALU.add, ["ss2", "rstd2"], ["ss2"])
            rsqrt_col(rstd2[:], ss2[:], 1.0 / 1024.0, "ss2", "rstd2")
            ob, obn = ot[ch % 2], "ot%d" % (ch % 2)
            xb_, xnm = xs[ch % 2], "xs%d" % (ch % 2)
            P.dma(lambda e, xb_=xb_, ch=ch: e.dma_start(out=xb_[:], in_=xsrc[row0 + ch * 128: row0 + (ch + 1) * 128, :]), writes=[xnm])
            for hf_, (pn_, pnn) in enumerate(((pN0, "pN0"), (pN1, "pN1"))):
                cs = slice(hf_ * 512, hf_ * 512 + 512)
                el("dve", lambda e, ob=ob, pn_=pn_, cs=cs: e.scalar_tensor_tensor(out=ob[:, cs], in0=pn_[:], scalar=rstd2[:, 0:1], in1=gpost[:, cs],
                                                                                 op0=ALU.mult, op1=ALU.mult), [pnn, "rstd2", "gpost"], [obn])
            tt(ob[:], ob[:], xb_[:], ALU.add, [obn, xnm], [obn], eng="pool")
            tok = P.dma(lambda e, ob=ob, ch=ch: e.dma_start(out=out[row0 + ch * 128: row0 + (ch + 1) * 128, :], in_=ob[:]), reads=[obn])
            out_toks.append(tok)

    sc_t1 = sbt([128, 2, 16], F32, "sc_t1"); sc_t2 = sbt([128, 2, 16], F32, "sc_t2")
    try:
        for ti in range(NPRE):
            do_tile(x_pre, ti * T, False, ti)
    except StopBuild:
        tk = P.dma(lambda e: e.dma_start(out=out[0:128, :], in_=gpost[:]), reads=["gpost"])
        P.final_wait("sp", [tk])
        P.emit(); P.close()
        return nc
    if NPRE > 0:
        ts(Cst[:].rearrange("p h e d -> p (h e d)"), Cst[:].rearrange("p h e d -> p (h e d)"), flg[:, 0:1], None, ALU.mult, None, ["Cst", "flg"], ["Cst"])
        ts(nst[:].rearrange("p h e -> p (h e)"), nst[:].rearrange("p h e -> p (h e)"), flg[:, 0:1], None, ALU.mult, None, ["nst", "flg"], ["nst"])
        el("act", lambda e: e.copy(out=Cbf[:].rearrange("p h e d -> p (h e d)"), in_=Cst[:].rearrange("p h e d -> p (h e d)")), ["Cst"], ["Cbf"])
        for h in range(4):
            for et in range(2):
                el("dve", lambda e, h=h, et=et: e.tensor_copy(out=nrep[:, h, et, :], in_=nst[:, h, et:et + 1].to_broadcast([128, 128])),
                   ["nst"], ["nrep"])
    for ti in range(NMAIN):
        do_tile(x_main, ti * T, True, NPRE + ti)
    P.final_wait("sp", out_toks)
    P.emit()
    P.close()
    return nc


T_TILE = 128
_cache = {}


def kernel(**inputs):
    x = np.ascontiguousarray(inputs["x"], dtype=np.float32)
    Bsz, L, Dm = x.shape
    half = L // 2
    npre = half // T_TILE
    nmain = half // T_TILE
    key = (T_TILE, npre, nmain)
    if key not in _cache:
        _cache[key] = build_program(T_TILE, npre, nmain)
    nc = _cache[key]
    shared = {}
    for k, v in inputs.items():
        if k == "x":
            continue
        a = np.ascontiguousarray(np.asarray(v, dtype=np.float32)[0])
        if k in ("b_i", "b_f", "log_dt", "norm_post_g"):
            a = a.reshape(1, -1)
        shared[k] = a
    in_maps = []
    zeros = np.zeros((half, Dm), np.float32)
    for core in range(8):
        b, hf = core // 2, core % 2
        m = dict(shared)
        m["x_main"] = np.ascontiguousarray(x[b, hf * half:(hf + 1) * half])
        m["x_pre"] = zeros if hf == 0 else np.ascontiguousarray(x[b, 0:half])
        m["flag"] = np.full((128, 1), float(hf), np.float32)
        in_maps.append(m)
    res = run_bass_kernel_spmd(nc, in_maps, core_ids=list(range(8)))
    outp = np.empty((Bsz, L, Dm), np.float32)
    for core in range(8):
        b, hf = core // 2, core % 2
        outp[b, hf * half:(hf + 1) * half] = res.results[core]["out"]
    return outp
```

```python
import contextlib
import numpy as np
import concourse.bass as bass
import concourse.mybir as mybir
from concourse.bass_utils import run_bass_kernel_spmd

F32 = mybir.dt.float32
BF16 = mybir.dt.bfloat16
AF = mybir.ActivationFunctionType
ALU = mybir.AluOpType
COMPUTE = ("pe", "act", "dve", "pool")
NDMA_SLOTS = 8
PI = float(np.pi)


class Prog:
    def __init__(self, nc):
        self.nc = nc
        self.stack = contextlib.ExitStack()
        self.engs = ("pe", "act", "dve", "pool", "sp")
        self.ops = {e: [] for e in self.engs}
        self.waited = {e: {} for e in self.engs}
        self.res = {}
        self.dma_use = {}
        self.dma_rr = {e: 0 for e in self.engs}
        self.sems = {}
        self.base = {e: 0 for e in COMPUTE}
        self.temp = None

    def sb(self, name, shape, dt):
        st = self.temp if self.temp is not None else self.stack
        return st.enter_context(self.nc.sbuf_tensor(name, list(shape), dt))

    def ps(self, name, shape, dt):
        return self.stack.enter_context(self.nc.psum_tensor(name, list(shape), dt))

    def _sem(self, key):
        if key not in self.sems:
            nm = "s_" + "_".join(str(k) for k in (key if isinstance(key, tuple) else (key,)))
            self.sems[key] = self.stack.enter_context(self.nc.semaphore(nm))
        return self.sems[key]

    def _deps(self, eng, reads, writes):
        deps = {}

        def add(tok):
            if tok is None:
                return
            k, v = tok
            if k == "pe" and eng == "pe":
                return
            if deps.get(k, -1) < v:
                deps[k] = v

        for r in reads:
            st = self.res.get(r)
            if st:
                for k, v in st[0].items():
                    add((k, v))
        for w in writes:
            st = self.res.get(w)
            if st:
                for k, v in st[0].items():
                    add((k, v))
                for k, v in st[1].items():
                    add((k, v))
        out = []
        wd = self.waited[eng]
        for k, v in deps.items():
            if wd.get(k, -1) >= v:
                continue
            wd[k] = v
            out.append((k, v))
        return out

    def _commit(self, tok, reads, writes):
        k, v = tok
        for r in reads:
            st = self.res.setdefault(r, [{}, {}])
            if st[1].get(k, -1) < v:
                st[1][k] = v
        for w in writes:
            old = self.res.get(w)
            wr = {}
            if old is not None and k not in COMPUTE:
                wr = {k2: v2 for k2, v2 in old[0].items() if k2 not in COMPUTE}
            wr[k] = v
            self.res[w] = [wr, {}]

    def op(self, eng, fn, reads=(), writes=(), sig=True):
        waits = self._deps(eng, reads, writes)
        idx = len(self.ops[eng])
        self.ops[eng].append(dict(fn=fn, waits=waits, sig=sig, dma=None))
        tok = (eng, idx)
        self._commit(tok, reads, writes)
        return tok

    def dma(self, fn, reads=(), writes=(), q="sp"):
        waits = self._deps(q, reads, writes)
        slot = self.dma_rr[q] % NDMA_SLOTS
        self.dma_rr[q] += 1
        key = ("d", q, slot)
        n = self.dma_use.get(key, 0)
        if n > 0:
            prev = n * 16
            if self.waited[q].get(key, -1) < prev:
                self.waited[q][key] = prev
                waits.append((key, prev))
        self.dma_use[key] = n + 1
        tok = (key, (n + 1) * 16)
        self.ops[q].append(dict(fn=fn, waits=waits, sig=False, dma=key))
        self._commit(tok, reads, writes)
        return tok

    def final_wait(self, eng, toks):
        self.ops[eng].append(dict(fn=None, waits=list(toks), sig=False, dma=None))

    def emit(self):
        nc = self.nc
        sigcount = {}
        totals = {}
        for e in COMPUTE:
            c = self.base[e]
            arr = []
            for o in self.ops[e]:
                if o["sig"]:
                    c += 1
                arr.append(c)
            need = [None] * len(arr)
            nxt = None
            for i in range(len(arr) - 1, -1, -1):
                if self.ops[e][i]["sig"]:
                    nxt = arr[i]
                need[i] = nxt
            sigcount[e] = need
            totals[e] = c
            self._sem(e)
        for k in self.dma_use:
            self._sem(k)

        def resolve(k, v):
            if k in COMPUTE:
                val = sigcount[k][v]
                assert val is not None, (k, v)
                return self.sems[k], val
            return self.sems[k], v

        with nc.Block() as block:

            def run(eng_name, eng):
                for o in self.ops[eng_name]:
                    for k, v in o["waits"]:
                        s, val = resolve(k, v)
                        eng.wait_ge(s, val)
                    if o["fn"] is None:
                        continue
                    ins = o["fn"](eng)
                    if o["dma"] is not None:
                        ins.then_inc(self.sems[o["dma"]], 16)
                    elif o["sig"]:
                        ins.then_inc(self.sems[eng_name], 1)
                for o2 in COMPUTE:
                    if o2 != eng_name and totals[o2] > 0:
                        eng.wait_ge(self.sems[o2], totals[o2])
                for k, n in self.dma_use.items():
                    eng.wait_ge(self.sems[k], n * 16)

            @block.tensor
            def _(e):
                run("pe", e)

            @block.scalar
            def _(e):
                run("act", e)

            @block.vector
            def _(e):
                run("dve", e)

            @block.gpsimd
            def _(e):
                run("pool", e)

            @block.sync
            def _(e):
                run("sp", e)

        self.base = totals
        self.ops = {e: [] for e in self.engs}
        self.waited = {e: {} for e in self.engs}
        self.res = {}

    def close(self):
        self.stack.close()


NBLK = 22
BLK_UA, BLK_UB, BLK_ZB, BLK_ZA, BLK_OA, BLK_G, BLK_AO, BLK_BO, BLK_WO, BLK_S5 = 0, 2, 3, 4, 6, 8, 12, 14, 15, 17
COL_UA, COL_ZA, COL_OA, COL_I, COL_UB, COL_ZB, COL_G = 0, 1024, 2048, 3072, 3080, 3592, 4104


def build_program(T, NPRE, NMAIN, dbg_stop=0):
    NCH = T // 128
    NC8 = T // 8
    nc = bass.Bass("TRN2", target_bir_lowering=False)
    dram = {}

    def din(name, shape):
        dram[name] = nc.dram_tensor(name, list(shape), F32, kind="ExternalInput").ap()
        return dram[name]

    x_pre = din("x_pre", [max(NPRE, 1) * T, 1024])
    x_main = din("x_main", [NMAIN * T, 1024])
    flag = din("flag", [128, 1])
    norm_pre_g = din("norm_pre_g", [1024]); w_in = din("w_in", [1024, 6152])
    conv_w = din("conv_w", [4, 1024]); conv_b = din("conv_b", [1024])
    w_q = din("w_q", [4, 256, 256]); w_k = din("w_k", [4, 256, 256]); w_v = din("w_v", [4, 256, 256])
    b_i = din("b_i", [1, 4]); b_f = din("b_f", [1, 4]); head_g = din("head_g", [1024]); skip_a = din("skip_a", [1024])
    w_a_out = din("w_a_out", [1024, 1024])
    lam_re = din("lam_re", [32, 64]); lam_im = din("lam_im", [32, 64]); log_dt = din("log_dt", [1, 32])
    B_re = din("B_re", [32, 64, 16]); B_im = din("B_im", [32, 64, 16])
    C_re = din("C_re", [32, 16, 64]); C_im = din("C_im", [32, 16, 64]); D_skip = din("D_skip", [32, 16])
    w_glu = din("w_glu", [512, 512]); b_glu = din("b_glu", [512]); w_b_out = din("w_b_out", [512, 1024])
    w_o = din("w_o", [1024, 1024]); norm_post_g = din("norm_post_g", [1, 1024])
    out = nc.dram_tensor("out", [NMAIN * T, 1024], F32, kind="ExternalOutput").ap()
    WS = nc.dram_tensor("wscratch", [NBLK, 128, 4096], BF16, kind="Internal").ap()

    P = Prog(nc)
    uid = [0]

    def sbt(shape, dt, name=None):
        uid[0] += 1
        return P.sb(name or ("t%d" % uid[0]), shape, dt)

    ident_f = sbt([128, 128], F32); ident_b = sbt([128, 128], BF16)
    maskT = sbt([128, 128], F32); mask4 = sbt([128, 4, 128], F32); ones_b = sbt([128, 128], BF16)
    P.op("pool", lambda e: e.memset(ident_f[:], 1.0), writes=["ident_f"])
    P.op("pool", lambda e: e.affine_select(out=ident_f[:], in_=ident_f[:], pattern=[[-1, 128]], compare_op=ALU.is_equal,
                                           fill=0.0, base=0, channel_multiplier=1), reads=["ident_f"], writes=["ident_f"])
    P.op("dve", lambda e: e.tensor_copy(out=ident_b[:], in_=ident_f[:]), reads=["ident_f"], writes=["ident_b"])
    P.op("pool", lambda e: e.memset(maskT[:], 1.0), writes=["maskT"])
    P.op("pool", lambda e: e.affine_select(out=maskT[:], in_=maskT[:], pattern=[[1, 128]], compare_op=ALU.is_ge,
                                           fill=0.0, base=0, channel_multiplier=-1), reads=["maskT"], writes=["maskT"])
    for h in range(4):
        P.op("pool", lambda e, h=h: e.tensor_copy(out=mask4[:, h, :], in_=maskT[:]), reads=["maskT"], writes=["mask4"])
    P.op("pool", lambda e: e.memset(ones_b[:], 1.0), writes=["ones_b"])

    gpre = sbt([128, 8], F32); cb = sbt([128, 8], F32); hg = sbt([128, 8], F32); skp = sbt([128, 8], F32)
    cw = sbt([128, 8, 4], F32); bglu = sbt([128, 4], F32); gpost = sbt([128, 1024], F32); bif = sbt([128, 8], F32)
    flg = sbt([128, 1], F32)
    nonc = dict(allow_slow_non_contiguous=True)
    P.dma(lambda e: e.dma_start(out=gpre[:], in_=norm_pre_g.rearrange("(k p) -> p k", p=128), **nonc), writes=["gpre"])
    P.dma(lambda e: e.dma_start(out=cb[:], in_=conv_b.rearrange("(k p) -> p k", p=128), **nonc), writes=["cb"])
    P.dma(lambda e: e.dma_start(out=hg[:], in_=head_g.rearrange("(k p) -> p k", p=128), **nonc), writes=["hg"])
    P.dma(lambda e: e.dma_start(out=skp[:], in_=skip_a.rearrange("(k p) -> p k", p=128), **nonc), writes=["skp"])
    for k in range(4):
        P.dma(lambda e, k=k: e.dma_start(out=cw[:, :, k], in_=conv_w[k].rearrange("(m p) -> p m", p=128), **nonc), writes=["cw"])
    P.dma(lambda e: e.dma_start(out=bglu[:], in_=b_glu.rearrange("(k p) -> p k", p=128), **nonc), writes=["bglu"])
    P.dma(lambda e: e.dma_start(out=gpost[:], in_=norm_post_g.partition_broadcast(128)), writes=["gpost"])
    P.dma(lambda e: e.dma_start(out=bif[:, 0:4], in_=b_i.partition_broadcast(128)), writes=["bif"])
    P.dma(lambda e: e.dma_start(out=bif[:, 4:8], in_=b_f.partition_broadcast(128)), writes=["bif"])
    P.dma(lambda e: e.dma_start(out=flg[:], in_=flag), writes=["flg"])

    Wqkv = [sbt([128, 4, 2, 256], BF16) for _ in range(3)]
    Wif = sbt([128, 8, 8], BF16); Wglu = sbt([128, 4, 512], BF16); cdiag = sbt([128, 8, 4, 128], BF16)
    AR32 = sbt([128, 2, 16], F32); ANI = sbt([128, 16], F32); API = sbt([128, 16], F32)
    pA = P.ps("pA", [128, 512], F32); pB = P.ps("pB", [128, 512], F32)
    pT = P.ps("pT", [128, 1024], BF16); pGD = P.ps("pGD", [128, 512], F32)
    pS = P.ps("pS", [128, 512], F32); pN0 = P.ps("pN0", [128, 512], F32); pN1 = P.ps("pN1", [128, 512], F32)
    pM = P.ps("pM", [128, 512], F32)
    P.temp = contextlib.ExitStack()
    stg = sbt([128, 4096], F32, "stg")
    stgb = sbt([128, 4096], BF16, "stgb")
    for wi, (wsrc, scl) in enumerate(((w_q, 1.0), (w_k, 1.0 / 16.0), (w_v, 1.0))):
        P.dma(lambda e, wsrc=wsrc: e.dma_start(out=stg[:, 0:2048].rearrange("p (h d n) -> p h d n", h=4, d=2),
                                               in_=wsrc.rearrange("h (d p) n -> p h d n", p=128)), writes=["stg"])
        P.op("dve", lambda e, wi=wi, scl=scl: e.tensor_scalar(out=Wqkv[wi][:].rearrange("p h d n -> p (h d n)"), in0=stg[:, 0:2048],
                                                              scalar1=scl, scalar2=None, op0=ALU.mult), reads=["stg"], writes=["Wqkv%d" % wi])
    Wq, Wk, Wv = Wqkv
    P.dma(lambda e: e.dma_start(out=stg[:, 0:64].rearrange("p (k n) -> p k n", k=8),
                                in_=w_in[:, COL_I:COL_I + 8].rearrange("(k p) n -> p k n", p=128), **nonc), writes=["stg"])
    for kt in range(8):
        P.op("dve", lambda e, kt=kt: e.tensor_scalar(out=Wif[:, kt, :], in0=stg[:, kt * 8:(kt + 1) * 8], scalar1=gpre[:, kt:kt + 1],
                                                     scalar2=None, op0=ALU.mult), reads=["stg", "gpre"], writes=["Wif"])
    P.dma(lambda e: e.dma_start(out=stg[:, 0:2048].rearrange("p (k n) -> p k n", k=4),
                                in_=w_glu.rearrange("(k p) n -> p k n", p=128)), writes=["stg"])
    P.op("dve", lambda e: e.tensor_copy(out=Wglu[:].rearrange("p k n -> p (k n)"), in_=stg[:, 0:2048]), reads=["stg"], writes=["Wglu"])
    for mt in range(8):
        for k in range(4):
            P.op("pool", lambda e, mt=mt, k=k: e.tensor_scalar(out=cdiag[:, mt, k, :], in0=ident_f[:], scalar1=cw[:, mt, k:k + 1],
                                                               scalar2=None, op0=ALU.mult), reads=["ident_f", "cw"], writes=["cdiag"])

    def stage_block(blk, src_ap_f, scale_gpre, nk):
        ncol = 4096 // nk
        P.dma(lambda e: e.dma_start(out=stg[:].rearrange("p (k n) -> p k n", k=nk), in_=src_ap_f), writes=["stg"])
        if scale_gpre:
            for kt in range(nk):
                P.op("dve" if kt % 2 == 0 else "pool",
                     lambda e, kt=kt: e.tensor_scalar(out=stgb[:, kt * ncol:(kt + 1) * ncol], in0=stg[:, kt * ncol:(kt + 1) * ncol],
                                                      scalar1=gpre[:, kt:kt + 1], scalar2=None, op0=ALU.mult),
                     reads=["stg", "gpre"], writes=["stgb"])
        else:
            P.op("dve", lambda e: e.tensor_copy(out=stgb[:, 0:2048], in_=stg[:, 0:2048]), reads=["stg"], writes=["stgb"])
            P.op("pool", lambda e: e.tensor_copy(out=stgb[:, 2048:4096], in_=stg[:, 2048:4096]), reads=["stg"], writes=["stgb"])
        P.dma(lambda e: e.dma_start(out=WS[blk], in_=stgb[:]), reads=["stgb"], writes=["WS%d" % blk])

    def win_cols(c0):
        return w_in[:, c0:c0 + 512].rearrange("(k p) n -> p k n", p=128)

    win_blocks = [(BLK_UA, COL_UA), (BLK_UA + 1, COL_UA + 512), (BLK_UB, COL_UB), (BLK_ZB, COL_ZB), (BLK_ZA, COL_ZA),
                  (BLK_ZA + 1, COL_ZA + 512), (BLK_OA, COL_OA), (BLK_OA + 1, COL_OA + 512)] + [(BLK_G + i, COL_G + 512 * i) for i in range(4)]
    for blk, c0 in win_blocks:
        stage_block(blk, win_cols(c0), True, 8)
    for i in range(2):
        stage_block(BLK_AO + i, w_a_out[:, 512 * i:512 * i + 512].rearrange("(k p) n -> p k n", p=128), False, 8)
        stage_block(BLK_WO + i, w_o[:, 512 * i:512 * i + 512].rearrange("(k p) n -> p k n", p=128), False, 8)
    stage_block(BLK_BO, w_b_out.rearrange("(k p) n -> p k n", p=128), False, 4)

    def small(n, name=None):
        return sbt([128, n], F32, name)

    cnt = [0]

    def el(eng, fn, r, w):
        P.op(eng, fn, reads=r, writes=w)

    def tt(outp, a, b, op, r, w, eng="dve"):
        el(eng, lambda e: e.tensor_tensor(out=outp, in0=a, in1=b, op=op), r, w)

    def ts(outp, a, s1, s2, op0, op1, r, w, eng="dve"):
        if op1 is None:
            el(eng, lambda e: e.tensor_scalar(out=outp, in0=a, scalar1=s1, scalar2=None, op0=op0), r, w)
        else:
            el(eng, lambda e: e.tensor_scalar(out=outp, in0=a, scalar1=s1, scalar2=s2, op0=op0, op1=op1), r, w)

    LR = small(32); LI = small(32); DT = small(32)
    for hf in range(2):
        sl = slice(64 * hf, 64 * hf + 64)
        P.dma(lambda e, sl=sl: e.dma_start(out=LR[sl, :], in_=lam_re.rearrange("g p -> p g"), **nonc), writes=["LR"])
        P.dma(lambda e, sl=sl: e.dma_start(out=LI[sl, :], in_=lam_im.rearrange("g p -> p g"), **nonc), writes=["LI"])
    P.dma(lambda e: e.dma_start(out=DT[:], in_=log_dt.partition_broadcast(128)), writes=["DT"])
    el("act", lambda e: e.activation(out=DT[:], in_=DT[:], func=AF.Exp), ["DT"], ["DT"])
    TH = small(32); MAG = small(32); t0 = small(32); t1 = small(32); t2 = small(32); kk = small(32)
    tt(TH[:], LI[:], DT[:], ALU.mult, ["LI", "DT"], ["TH"])
    tt(t0[:], LR[:], DT[:], ALU.mult, ["LR", "DT"], ["t0"])
    el("act", lambda e: e.activation(out=MAG[:], in_=t0[:], func=AF.Exp), ["t0"], ["MAG"])
    IMAG2 = small(32)
    el("act", lambda e: e.activation(out=IMAG2[:], in_=t0[:], func=AF.Exp, scale=-2.0), ["t0"], ["IMAG2"])

    def sin_of(dst, src, shift, nm):
        ts(t1[:], src, shift, None, ALU.add, None, [nm, "t1"], ["t1"])
        el("pool", lambda e: e.memset(kk[:], 0.0), [], ["kk"])
        for m in range(7):
            ts(t2[:], t1[:], (2 * m + 1) * PI, None, ALU.is_gt, None, ["t1"], ["t2"])
            tt(kk[:], kk[:], t2[:], ALU.add, ["kk", "t2"], ["kk"])
        ts(kk[:], kk[:], -2.0 * PI, None, ALU.mult, None, ["kk"], ["kk"])
        tt(t1[:], t1[:], kk[:], ALU.add, ["t1", "kk"], ["t1"])
        el("act", lambda e: e.activation(out=dst, in_=t1[:], func=AF.Sin), ["t1"], [nm + "_s"])

    SN = small(32); CS = small(32)
    sin_of(SN[:], TH[:], 0.0, "TH")
    sin_of(CS[:], TH[:], PI / 2.0, "TH")
    pwr = sbt([128, 9, 32], F32); pwi = sbt([128, 9, 32], F32); pnr = sbt([128, 8, 32], F32); pni = sbt([128, 8, 32], F32)
    el("pool", lambda e: e.memset(pwr[:, 0, :], 1.0), [], ["pw"]); el("pool", lambda e: e.memset(pwi[:, 0, :], 0.0), [], ["pw"])
    el("pool", lambda e: e.memset(pnr[:, 0, :], 1.0), [], ["pn"]); el("pool", lambda e: e.memset(pni[:, 0, :], 0.0), [], ["pn"])
    tt(pwr[:, 1, :], MAG[:], CS[:], ALU.mult, ["MAG", "TH_s"], ["pw"])
    tt(pwi[:, 1, :], MAG[:], SN[:], ALU.mult, ["MAG", "TH_s"], ["pw"])
    tt(pnr[:, 1, :], pwr[:, 1, :], IMAG2[:], ALU.mult, ["pw", "IMAG2"], ["pn"])
    tt(t0[:], pwi[:, 1, :], IMAG2[:], ALU.mult, ["pw", "IMAG2"], ["t0"])
    ts(pni[:, 1, :], t0[:], -1.0, None, ALU.mult, None, ["t0"], ["pn"])

    def cmul(or_, oi_, ar, ai, br, bi, r, w):
        raise NotImplementedError

    u0 = small(32); u1 = small(32)
    for k in range(1, 8):
        for (xr, xi, nm, lim) in ((pwr, pwi, "pw", 9), (pnr, pni, "pn", 8)):
            if k + 1 >= lim:
                continue
            tt(u0[:], xr[:, k, :], xr[:, 1, :], ALU.mult, [nm], ["u0"])
            tt(u1[:], xi[:, k, :], xi[:, 1, :], ALU.mult, [nm], ["u1"])
            tt(xr[:, k + 1, :], u0[:], u1[:], ALU.subtract, ["u0", "u1"], [nm])
            tt(u0[:], xr[:, k, :], xi[:, 1, :], ALU.mult, [nm], ["u0"])
            tt(u1[:], xi[:, k, :], xr[:, 1, :], ALU.mult, [nm], ["u1"])
            tt(xi[:, k + 1, :], u0[:], u1[:], ALU.add, ["u0", "u1"], [nm])
    den = small(32); qr = small(32); qi = small(32); nr = small(32)
    tt(u0[:], LR[:], LR[:], ALU.mult, ["LR"], ["u0"]); tt(u1[:], LI[:], LI[:], ALU.mult, ["LI"], ["u1"])
    tt(den[:], u0[:], u1[:], ALU.add, ["u0", "u1"], ["den"])
    el("dve", lambda e: e.reciprocal(out=den[:], in_=den[:]), ["den"], ["den"])
    ts(nr[:], pwr[:, 1, :], -1.0, None, ALU.add, None, ["pw"], ["nr"])
    tt(u0[:], nr[:], LR[:], ALU.mult, ["nr", "LR"], ["u0"]); tt(u1[:], pwi[:, 1, :], LI[:], ALU.mult, ["pw", "LI"], ["u1"])
    tt(qr[:], u0[:], u1[:], ALU.add, ["u0", "u1"], ["qr"]); tt(qr[:], qr[:], den[:], ALU.mult, ["qr", "den"], ["qr"])
    tt(u0[:], pwi[:, 1, :], LR[:], ALU.mult, ["pw", "LR"], ["u0"]); tt(u1[:], nr[:], LI[:], ALU.mult, ["nr", "LI"], ["u1"])
    tt(qi[:], u0[:], u1[:], ALU.subtract, ["u0", "u1"], ["qi"]); tt(qi[:], qi[:], den[:], ALU.mult, ["qi", "den"], ["qi"])
    Br = sbt([128, 32, 16], F32); Bi = sbt([128, 32, 16], F32); bbr = sbt([128, 32, 16], F32); bbi = sbt([128, 32, 16], F32)
    v0 = sbt([128, 32, 16], F32); v1 = sbt([128, 32, 16], F32)
    for hf in range(2):
        sl = slice(64 * hf, 64 * hf + 64)
        P.dma(lambda e, sl=sl: e.dma_start(out=Br[sl], in_=B_re.rearrange("g p n -> p g n"), **nonc), writes=["Br"])
        P.dma(lambda e, sl=sl: e.dma_start(out=Bi[sl], in_=B_im.rearrange("g p n -> p g n"), **nonc), writes=["Bi"])

    def bc(s):
        return s.unsqueeze(2).to_broadcast([128, 32, 16])

    def cmul3(orr, oii, sr, si, sn, xr, xi, xn, on):
        xn = [xn] if isinstance(xn, str) else list(xn)
        sn = [sn] if isinstance(sn, str) else list(sn)
        tt(v0[:], xr, bc(sr), ALU.mult, xn + sn, ["v0"]); tt(v1[:], xi, bc(si), ALU.mult, xn + sn, ["v1"])
        tt(orr, v0[:], v1[:], ALU.subtract, ["v0", "v1"], [on])
        tt(v0[:], xi, bc(sr), ALU.mult, xn + sn, ["v0"]); tt(v1[:], xr, bc(si), ALU.mult, xn + sn, ["v1"])
        tt(oii, v0[:], v1[:], ALU.add, ["v0", "v1"], [on])

    el("dve", lambda e: e.tensor_copy(out=u0[:], in_=qr[:]), ["qr"], ["qq"])
    cmul3(bbr[:], bbi[:], qr[:], qi[:], ["qr", "qi"], Br[:], Bi[:], ["Br", "Bi"], "bb")
    CTr = sbt([128, 32, 16], F32); CTi = sbt([128, 32, 16], F32)
    Cdup = sbt([128, 4, 2, 64], F32)
    for (Csrc, CTt, nm) in ((C_re, CTr, "CTr"), (C_im, CTi, "CTi")):
        for d in range(2):
            P.dma(lambda e, Csrc=Csrc, d=d: e.dma_start(out=Cdup[:, :, d, :], in_=Csrc.rearrange("(t g) n p -> (g n) t p", t=4)),
                  writes=["Cdup"])
        for t in range(4):
            P.op("pe", lambda e, t=t: e.transpose(out=pA[:, t * 128:(t + 1) * 128], in_=Cdup[:, t, :, :].rearrange("q d p -> q (d p)"),
                                                  identity=ident_f[:]), reads=["Cdup", "ident_f"], writes=["pA"])
        el("act", lambda e, CTt=CTt: e.copy(out=CTt[:].rearrange("p g n -> p (g n)"), in_=pA[:]), ["pA"], [nm])
    Er = sbt([128, 32, 8, 16], F32); Ei = sbt([128, 32, 8, 16], F32)
    Fr = sbt([128, 32, 8, 16], F32); Fi = sbt([128, 32, 8, 16], F32)
    for j in range(8):
        cmul3(Er[:, :, j, :], Ei[:, :, j, :], pwr[:, 7 - j, :], pwi[:, 7 - j, :], "pw", bbr[:], bbi[:], "bb", "E")
        cmul3(Fr[:, :, j, :], Fi[:, :, j, :], pwr[:, j + 1, :], pwi[:, j + 1, :], "pw", CTr[:], CTi[:], ["CTr", "CTi"], "F")
    halfm = sbt([128, 2], F32)
    el("pool", lambda e: e.memset(halfm[:], 0.0), [], ["halfm"])
    el("pool", lambda e: e.memset(halfm[0:64, 0:1], 1.0), ["halfm"], ["halfm"])
    el("pool", lambda e: e.memset(halfm[64:128, 1:2], 1.0), ["halfm"], ["halfm"])
    for ri, (Et, blk) in enumerate(((Er, BLK_S5), (Ei, BLK_S5 + 1))):
        el("pool", lambda e: e.memset(stgb[:], 0.0), ["stgb"], ["stgb"])
        for g in range(32):
            ps_ = pA if g % 2 == 0 else pB
            nm = "pA" if g % 2 == 0 else "pB"
            P.op("pe", lambda e, Et=Et, g=g, ps_=ps_: e.transpose(out=ps_[:, 0:64], in_=Et[0:64, g, :, :].rearrange("p j n -> p (j n)"),
                                                                  identity=ident_f[0:64, 0:64]), reads=["E", "ident_f"], writes=[nm])
            c0 = g * 128 + 64 * (g % 2)
            el("act" if g % 2 == 0 else "dve",
               (lambda e, ps_=ps_, c0=c0: e.copy(out=stgb[:, c0:c0 + 64], in_=ps_[:, 0:64])) if g % 2 == 0 else
               (lambda e, ps_=ps_, c0=c0: e.tensor_copy(out=stgb[:, c0:c0 + 64], in_=ps_[:, 0:64])), [nm], ["stgb"])
        P.dma(lambda e, blk=blk: e.dma_start(out=WS[blk], in_=stgb[:]), reads=["stgb"], writes=["WS%d" % blk])
    for ri, (Ft, blk, sg) in enumerate(((Fr, BLK_S5 + 3, 1.0), (Fi, BLK_S5 + 4, -1.0))):
        for g in range(32):
            P.op("dve" if g % 2 == 0 else "pool",
                 lambda e, Ft=Ft, g=g, sg=sg: e.tensor_scalar(out=stgb[:, g * 128:(g + 1) * 128], in0=Ft[:, g, :, :].rearrange("p j n -> p (j n)"),
                                                              scalar1=halfm[:, (g % 2):(g % 2) + 1], scalar2=sg, op0=ALU.mult, op1=ALU.mult),
                 reads=["F", "halfm"], writes=["stgb"])
        P.dma(lambda e, blk=blk: e.dma_start(out=WS[blk], in_=stgb[:]), reads=["stgb"], writes=["WS%d" % blk])
    Gr = sbt([128, 32, 8, 16], F32); Gni = sbt([128, 32, 8, 16], F32)
    i8r = small(32); i8i = small(32)
    tt(u0[:], pnr[:, 7, :], pnr[:, 1, :], ALU.mult, ["pn"], ["u0"]); tt(u1[:], pni[:, 7, :], pni[:, 1, :], ALU.mult, ["pn"], ["u1"])
    tt(i8r[:], u0[:], u1[:], ALU.subtract, ["u0", "u1"], ["i8"])
    tt(u0[:], pnr[:, 7, :], pni[:, 1, :], ALU.mult, ["pn"], ["u0"]); tt(u1[:], pni[:, 7, :], pnr[:, 1, :], ALU.mult, ["pn"], ["u1"])
    tt(i8i[:], u0[:], u1[:], ALU.add, ["u0", "u1"], ["i8"])
    for j in range(8):
        cmul3(Gr[:, :, j, :], Gni[:, :, j, :], i8r[:], i8i[:], "i8", Fr[:, :, j, :], Fi[:, :, j, :], "F", "G")
    ts(Gni[:], Gni[:], -1.0, None, ALU.mult, None, ["G"], ["G"])
    bmask = sbt([128, 8, 16], F32); Dcol = sbt([128, 32], F32)
    el("pool", lambda e: e.memset(bmask[:], 1.0), [], ["bmask"])
    el("pool", lambda e: e.affine_select(out=bmask[:], in_=bmask[:], pattern=[[16, 8], [0, 16]], compare_op=ALU.is_ge, fill=0.0,
                                         base=15, channel_multiplier=-1), ["bmask"], ["bmask"])
    for j in range(8):
        P.dma(lambda e, j=j: e.dma_start(out=Dcol[16 * j:16 * j + 16, :], in_=D_skip.rearrange("g n -> n g"), **nonc), writes=["Dcol"])
    wtmp = sbt([128, 128], F32)
    for g in range(32):
        ps_ = pA if g % 2 == 0 else pB
        nm = "pA" if g % 2 == 0 else "pB"
        P.op("pe", lambda e, g=g, ps_=ps_: e.matmul(ps_[:, 0:128], lhsT=Er[0:64, g, :, :].rearrange("p j n -> p (j n)"),
                                                    rhs=Gr[0:64, g, :, :].rearrange("p j n -> p (j n)"), start=True, stop=False),
             reads=["E", "G"], writes=[nm], sig=False)
        P.op("pe", lambda e, g=g, ps_=ps_: e.matmul(ps_[:, 0:128], lhsT=Ei[0:64, g, :, :].rearrange("p j n -> p (j n)"),
                                                    rhs=Gni[0:64, g, :, :].rearrange("p j n -> p (j n)"), start=False, stop=True),
             reads=["E", "G"], writes=[nm])
        tt(wtmp[:], ps_[:, 0:128], bmask[:].rearrange("p j n -> p (j n)"), ALU.mult, [nm, "bmask"], ["wtmp"])
        el("dve", lambda e, g=g: e.scalar_tensor_tensor(out=stgb[:, g * 128:(g + 1) * 128], in0=ident_f[:], scalar=Dcol[:, g:g + 1],
                                                        in1=wtmp[:], op0=ALU.mult, op1=ALU.add), ["ident_f", "Dcol", "wtmp", "stgb"], ["stgb"])
    P.dma(lambda e: e.dma_start(out=WS[BLK_S5 + 2], in_=stgb[:]), reads=["stgb"], writes=["WS%d" % (BLK_S5 + 2)])
    for g2 in range(2):
        sl = slice(64 * g2, 64 * g2 + 64)
        src_r = pwr[sl, 8, :].rearrange("p (q t) -> p q t", t=2)[:, :, g2]
        src_i = pwi[sl, 8, :].rearrange("p (q t) -> p q t", t=2)[:, :, g2]
        el("dve", lambda e, sl=sl, src_r=src_r: e.tensor_copy(out=AR32[sl, 0, :], in_=src_r), ["pw"], ["AR32"])
        el("dve", lambda e, sl=sl, src_r=src_r: e.tensor_copy(out=AR32[sl, 1, :], in_=src_r), ["pw"], ["AR32"])
        el("dve", lambda e, sl=sl, src_i=src_i: e.tensor_copy(out=API[sl, :], in_=src_i), ["pw"], ["API"])
        ts(ANI[sl, :], src_i, -1.0, None, ALU.mult, None, ["pw"], ["ANI"])

    if dbg_stop == 1:
        tk = P.dma(lambda e: e.dma_start(out=out[0:128, :], in_=gpost[:]), reads=["gpost"])
        P.final_wait("sp", [tk])
        P.emit(); P.temp.close(); P.close()
        return nc
    P.emit()
    P.temp.close()
    P.temp = None
    NSLOT = 3
    wslot = [sbt([128, 4096], BF16, "wslot%d" % i) for i in range(NSLOT)]
    slot_rr = [0]

    def load_block(blk):
        s = slot_rr[0] % NSLOT
        slot_rr[0] += 1
        P.dma(lambda e: e.dma_start(out=wslot[s][:], in_=WS[blk]), reads=["WS%d" % blk], writes=["wslot%d" % s])
        return wslot[s], "wslot%d" % s

    xs = [sbt([128, 1024], F32, "xs%d" % i) for i in range(2)]
    junk = sbt([128, 1024], BF16); xnb = sbt([128, 1024], BF16)
    ss = small(1); rstd = small(1)
    xnT = sbt([128, 8, T], BF16, "xnT"); uaT = sbt([128, 8, T + 4], BF16, "uaT"); cT = sbt([128, 8, T], BF16, "cT")
    zaT = sbt([128, 8, T], BF16, "zaT"); oaT = sbt([128, 8, T], BF16, "oaT"); zbT = sbt([128, 4, T], BF16, "zbT")
    qT = sbt([128, 4, 2, T], BF16, "qT"); kT = sbt([128, 4, 2, T], BF16, "kT")
    ktm = sbt([128, NCH, 4, 256], BF16, "ktm"); vw = sbt([128, NCH, 4, 256], BF16, "vw")
    gates = sbt([128, NCH, 8], F32, "gates")
    hfT = sbt([128, 8, T], BF16, "hfT"); mrgT = zaT; hbT = sbt([128, 4, T], BF16, "hbT")
    Cst = sbt([128, 4, 2, 256], F32, "Cst"); Cbf = sbt([128, 4, 2, 256], BF16, "Cbf")
    nst = sbt([128, 4, 2], F32, "nst"); nrep = sbt([128, 4, 2, 128], BF16, "nrep")
    Utm = sbt([128, 32, 8, 16], BF16, "Utm"); U2 = sbt([128, 32, NC8], BF16, "U2")
    Xall = sbt([128, NC8, 2, 16], F32, "Xall"); Sall = sbt([128, NC8 + 1, 2, 16], F32, "Sall"); Sbf = sbt([128, 2, 16, NC8], BF16, "Sbf")
    Yg = sbt([128, 32, NC8], BF16, "Yg"); Ytm = Utm[:].rearrange("p g j n -> p (g j n)").rearrange("p (j c) -> p j c", j=8); yT = sbt([128, 4, T], BF16, "yT")
    for (t_, nm) in ((Cst, "Cst"), (Cbf, "Cbf"), (nst, "nst"), (nrep, "nrep"), (Sall, "Sall"), (uaT, "uaT")):
        flat = t_[:]
        el("pool", lambda e, flat=flat: e.memset(flat, 0.0), [], [nm])

    lfrep = sbt([128, 4, 128], F32); lf = small(4, "lf"); ig = small(4); bcol = small(4); wcol = small(4, "wcol"); wcolb = sbt([128, 4, 2], BF16)
    wrep = sbt([128, 4, 128], BF16); emb = sbt([128, 4, 128], F32, "emb"); dec = small(4, "dec"); SmT = sbt([128, 4, 128], BF16, "SmT")
    aden = sbt([128, 4, 128], F32); rden = sbt([128, 4, 128], F32, "rden"); hT = sbt([128, 8, 128], F32, "hT"); sq = sbt([128, 8, 128], BF16)
    rsh = sbt([128, 4, 128], F32); hn = sbt([128, 8, 128], F32, "hn"); e1 = small(4)
    gAll = oaT; m1 = sbt([128, T], F32); m2 = sbt([128, T], F32)
    ot = [sbt([128, 1024], F32, "ot%d" % i) for i in range(2)]
    ss2 = small(1); rstd2 = small(1); sgl = sbt([128, T], BF16); xg = sbt([128, T], F32)

    def rsqrt_col(dst, src, scale, nm_src, nm_dst):
        ts(dst, src, scale, 1e-6, ALU.mult, ALU.add, [nm_src], [nm_dst])
        el("act", lambda e: e.activation(out=dst, in_=dst, func=AF.Ln), [nm_dst], [nm_dst])
        el("act", lambda e: e.activation(out=dst, in_=dst, func=AF.Exp, scale=-0.5), [nm_dst], [nm_dst])

    def mm(outp, lhsT, rhs, start, stop, r, w, sig=None):
        P.op("pe", lambda e: e.matmul(outp, lhsT=lhsT, rhs=rhs, start=start, stop=stop), reads=r, writes=w,
             sig=(stop if sig is None else sig))

    acc_rr = [0]

    def acc_bank():
        acc_rr[0] += 1
        return (pA, "pA") if acc_rr[0] % 2 else (pB, "pB")

    out_toks = []

    class StopBuild(Exception):
        pass

    def chk(level):
        if dbg_stop == level:
            raise StopBuild()

    def do_tile(xsrc, row0, main, ti):
        for sub in range(NCH):
            xb_, xnm = xs[sub % 2], "xs%d" % (sub % 2)
            P.dma(lambda e, xb_=xb_, sub=sub: e.dma_start(out=xb_[:], in_=xsrc[row0 + sub * 128: row0 + (sub + 1) * 128, :]), writes=[xnm])
            el("act", lambda e, xb_=xb_: e.activation(out=junk[:], in_=xb_[:], func=AF.Square, accum_out=ss[:]), [xnm], ["junk", "ss"])
            rsqrt_col(rstd[:], ss[:], 1.0 / 1024.0, "ss", "rstd")
            ts(xnb[:], xb_[:], rstd[:, 0:1], None, ALU.mult, None, [xnm, "rstd"], ["xnb"])
            for kt in range(8):
                P.op("pe", lambda e, kt=kt: e.transpose(out=pT[:, kt * 128:(kt + 1) * 128], in_=xnb[:, kt * 128:(kt + 1) * 128], identity=ident_b[:]),
                     reads=["xnb", "ident_b"], writes=["pT"], sig=(kt == 7))
            el("act", lambda e, sub=sub: e.copy(out=xnT[:, :, sub * 128:(sub + 1) * 128], in_=pT[:].rearrange("p (k t) -> p k t", k=8)),
               ["pT"], ["xnT"])

        chk(2)

        def proj_fm(blk, ncolt, evac):
            W, wn = load_block(blk)
            Wv_ = W[:].rearrange("p (k n) -> p k n", k=8)
            for ct in range(ncolt):
                ps_, pn = acc_bank()
                for kt in range(8):
                    mm(ps_[:, 0:T], Wv_[:, kt, ct * 128:(ct + 1) * 128], xnT[:, kt, :], kt == 0, kt == 7, [wn, "xnT"], [pn])
                evac(ct, ps_, pn)

        el("dve", lambda e: e.tensor_copy(out=uaT[:, :, 1:4], in_=uaT[:, :, T + 1:T + 4]), ["uaT"], ["uaT"])
        for half in range(2):
            proj_fm(BLK_UA + half, 4, lambda ct, ps_, pn, half=half: el(
                "act", lambda e: e.copy(out=uaT[:, half * 4 + ct, 4:T + 4], in_=ps_[:, 0:T]), [pn], ["uaT"]))
        chk(3)
        for ch in range(NCH):
            for kt in range(8):
                mm(pM[:, 0:8], xnT[:, kt, ch * 128:(ch + 1) * 128], Wif[:, kt, :], kt == 0, kt == 7, ["xnT", "Wif"], ["pM"])
            tt(gates[:, ch, :], pM[:, 0:8], bif[:], ALU.add, ["pM", "bif"], ["gates"])
        chk(4)
        W, wn = load_block(BLK_UB)
        Wv_ = W[:].rearrange("p (k n) -> p k n", k=8)
        for j in range(8):
            ps_, pn = acc_bank()
            for kt in range(8):
                lhs = xnT[:, kt, :].rearrange("p (c j) -> p j c", j=8)[:, j, :]
                mm(ps_[0:NC8, :], lhs, Wv_[:, kt, :], kt == 0, kt == 7, [wn, "xnT"], [pn])
            el("act" if j % 2 else "dve",
               (lambda e, ps_=ps_, j=j: e.copy(out=Utm[0:NC8, :, j, :], in_=ps_[0:NC8, :].rearrange("c (g n) -> c g n", g=32))) if j % 2 else
               (lambda e, ps_=ps_, j=j: e.tensor_copy(out=Utm[0:NC8, :, j, :], in_=ps_[0:NC8, :].rearrange("c (g n) -> c g n", g=32))), [pn], ["Utm"])
        chk(5)
        for g in range(32):
            P.op("pe", lambda e, g=g: e.transpose(out=pT[:, g * NC8:(g + 1) * NC8], in_=Utm[0:NC8, g, :, :].rearrange("c j n -> c (j n)"),
                                                  identity=ident_b[0:NC8, 0:NC8]), reads=["Utm", "ident_b"], writes=["pT"], sig=(g == 31))
        el("act", lambda e: e.copy(out=U2[:].rearrange("p g c -> p (g c)"), in_=pT[:, 0:32 * NC8]), ["pT"], ["U2"])
        W1r, w1rn = load_block(BLK_S5)
        W1i, w1in = load_block(BLK_S5 + 1)
        for ri, (Wt, wn_) in enumerate(((W1r, w1rn), (W1i, w1in))):
            Wg = Wt[:].rearrange("p (g n) -> p g n", g=32)
            for q in range(16):
                for g2 in range(2):
                    g = 2 * q + g2
                    mm(pM[:, (q * NC8):(q + 1) * NC8], Wg[:, g, :], U2[:, g, :], g2 == 0, g2 == 1, [wn_, "U2"], ["pM"], sig=(q == 15 and g2 == 1))
            el("act" if ri == 0 else "dve",
               (lambda e, ri=ri: e.copy(out=Xall[:, :, ri, :].rearrange("p c q -> p q c"), in_=pM[:, 0:16 * NC8].rearrange("p (q c) -> p q c", q=16))) if ri == 0 else
               (lambda e, ri=ri: e.tensor_copy(out=Xall[:, :, ri, :].rearrange("p c q -> p q c"), in_=pM[:, 0:16 * NC8].rearrange("p (q c) -> p q c", q=16))),
               ["pM"], ["Xall"])
        chk(6)
        T1 = sbt([128, 2, 16], F32, "scT1_%d" % ti) if False else None
        for c in range(NC8):
            sp_ = Sall[:, c, :, :]
            sn_ = Sall[:, c + 1, :, :]
            P.op("pool", lambda e, sp_=sp_: e.tensor_tensor(out=sc_t1[:], in0=sp_, in1=AR32[:], op=ALU.mult), reads=["Sall"], writes=["sc_t1"])
            P.op("pool", lambda e, c=c: e.tensor_tensor(out=sc_t2[:, 0, :], in0=Sall[:, c, 1, :], in1=ANI[:], op=ALU.mult), reads=["Sall"], writes=["sc_t2"])
            P.op("pool", lambda e, c=c: e.tensor_tensor(out=sc_t2[:, 1, :], in0=Sall[:, c, 0, :], in1=API[:], op=ALU.mult), reads=["Sall"], writes=["sc_t2"])
            P.op("pool", lambda e, c=c: e.tensor_tensor(out=sc_t1[:], in0=sc_t1[:], in1=Xall[:, c, :, :], op=ALU.add), reads=["sc_t1", "Xall"], writes=["sc_t1"])
            P.op("pool", lambda e, sn_=sn_: e.tensor_tensor(out=sn_, in0=sc_t1[:], in1=sc_t2[:], op=ALU.add), reads=["sc_t1", "sc_t2"], writes=["Sall"])
        if main:
            for ri in range(2):
                el("act" if ri else "dve",
                   (lambda e, ri=ri: e.copy(out=Sbf[:, ri, :, :], in_=Sall[:, 0:NC8, ri, :].rearrange("p c q -> p q c"))) if ri else
                   (lambda e, ri=ri: e.tensor_copy(out=Sbf[:, ri, :, :], in_=Sall[:, 0:NC8, ri, :].rearrange("p c q -> p q c"))),
                   ["Sall"], ["Sbf"])
        el("pool", lambda e: e.tensor_copy(out=Sall[:, 0, :, :], in_=Sall[:, NC8, :, :]), ["Sall"], ["Sall"])
        if main:
            proj_fm(BLK_ZB, 4, lambda ct, ps_, pn: el("act", lambda e: e.activation(out=zbT[:, ct, :], in_=ps_[:, 0:T], func=AF.Silu), [pn], ["zbT"]))
            Wi_, win_ = load_block(BLK_S5 + 2)
            Wr_, wrn_ = load_block(BLK_S5 + 3)
            Wm_, wmn_ = load_block(BLK_S5 + 4)
            Wi_g = Wi_[:].rearrange("p (g n) -> p g n", g=32); Wr_g = Wr_[:].rearrange("p (g n) -> p g n", g=32)
            Wm_g = Wm_[:].rearrange("p (g n) -> p g n", g=32)
            GP = 512 // NC8
            for g0 in range(0, 32, GP):
                ng = min(GP, 32 - g0)
                for gi in range(ng):
                    g = g0 + gi
                    o_ = pM[:, gi * NC8:(gi + 1) * NC8]
                    mm(o_, Wi_g[:, g, :], U2[:, g, :], True, False, [win_, "U2"], ["pM"], sig=False)
                    mm(o_, Wr_g[:, g, :], Sbf[:, 0, g // 2, :], False, False, [wrn_, "Sbf"], ["pM"], sig=False)
                    mm(o_, Wm_g[:, g, :], Sbf[:, 1, g // 2, :], False, True, [wmn_, "Sbf"], ["pM"], sig=(gi == ng - 1))
                el("act", lambda e, g0=g0, ng=ng: e.activation(out=Yg[:, g0:g0 + ng, :].rearrange("p g c -> p (g c)"), in_=pM[:, 0:ng * NC8],
                                                               func=AF.Gelu_apprx_tanh), ["pM"], ["Yg"])
            for g0 in range(0, 32, 8):
                for gi in range(8):
                    g = g0 + gi
                    P.op("pe", lambda e, g=g, gi=gi: e.transpose(out=pT[0:NC8, gi * 128:(gi + 1) * 128], in_=Yg[:, g, :], identity=ident_b[:]),
                         reads=["Yg", "ident_b"], writes=["pT"], sig=(gi == 7))
                el("act", lambda e, g0=g0: e.copy(out=Ytm[0:NC8, :, 16 * g0:16 * g0 + 128].rearrange("c j (g n) -> c g j n", g=8),
                                                  in_=pT[0:NC8, :].rearrange("c (g j n) -> c g j n", g=8, j=8)), ["pT"], ["Utm"])
            for ct in range(4):
                for j in range(8):
                    P.op("pe", lambda e, ct=ct, j=j: e.transpose(out=pT[:, j * NC8:(j + 1) * NC8], in_=Ytm[0:NC8, j, ct * 128:(ct + 1) * 128],
                                                                 identity=ident_b[0:NC8, 0:NC8]), reads=["Utm", "ident_b"], writes=["pT"], sig=(j == 7))
                el("act", lambda e, ct=ct: e.copy(out=yT[:, ct, :], in_=pT[:, 0:T]), ["pT"], ["yT"])
            for ot_ in range(4):
                ps_, pn = acc_bank()
                for ct in range(4):
                    mm(ps_[:, 0:T], Wglu[:, ct, ot_ * 128:(ot_ + 1) * 128], yT[:, ct, :], ct == 0, ct == 3, ["Wglu", "yT"], [pn])
                el("act", lambda e, ps_=ps_, ot_=ot_: e.activation(out=sgl[:], in_=ps_[:, 0:T], func=AF.Sigmoid, bias=bglu[:, ot_:ot_ + 1]),
                   [pn, "bglu"], ["sgl"])
                tt(xg[:], sgl[:], yT[:, ot_, :], ALU.mult, ["sgl", "yT"], ["xg"])
                tt(hbT[:, ot_, :].rearrange("p (j c) -> p j c", j=8), xg[:].rearrange("p (j c) -> p j c", j=8),
                   zbT[:, ot_, :].rearrange("p (c j) -> p j c", j=8), ALU.mult, ["xg", "zbT"], ["hbT"])
        chk(7)
        for mt in range(8):
            ps_, pn = acc_bank()
            for k in range(4):
                mm(ps_[:, 0:T], cdiag[:, mt, k, :], uaT[:, mt, k + 1:k + 1 + T], k == 0, k == 3, ["cdiag", "uaT"], [pn])
            el("act", lambda e, ps_=ps_, mt=mt: e.activation(out=cT[:, mt, :], in_=ps_[:, 0:T], func=AF.Silu, bias=cb[:, mt:mt + 1]),
               [pn, "cb"], ["cT"])
        if main:
            for half in range(2):
                proj_fm(BLK_ZA + half, 4, lambda ct, ps_, pn, half=half: el(
                    "act", lambda e: e.activation(out=zaT[:, half * 4 + ct, :], in_=ps_[:, 0:T], func=AF.Silu), [pn], ["zaT"]))
            for half in range(2):
                proj_fm(BLK_OA + half, 4, lambda ct, ps_, pn, half=half: el(
                    "act", lambda e: e.activation(out=oaT[:, half * 4 + ct, :], in_=ps_[:, 0:T], func=AF.Sigmoid), [pn], ["oaT"]))
        chk(8)
        for h in range(4):
            if main:
                for (Wx, wxn, dst, dn) in ((Wq, "Wqkv0", qT, "qT"), (Wk, "Wqkv1", kT, "kT")):
                    for et in range(2):
                        ps_, pn = acc_bank()
                        for d in range(2):
                            mm(ps_[:, 0:T], Wx[:, h, d, et * 128:(et + 1) * 128], cT[:, 2 * h + d, :], d == 0, d == 1, [wxn, "cT"], [pn])
                        el("act" if et else "dve",
                           (lambda e, ps_=ps_, dst=dst, h=h, et=et: e.copy(out=dst[:, h, et, :], in_=ps_[:, 0:T])) if et else
                           (lambda e, ps_=ps_, dst=dst, h=h, et=et: e.tensor_copy(out=dst[:, h, et, :], in_=ps_[:, 0:T])), [pn], [dn])
        for ch in range(NCH):
            tsl = slice(ch * 128, (ch + 1) * 128)
            ts(ig[:], gates[:, ch, 0:4], 1.0, None, ALU.mult, None, ["gates"], ["ig"])
            el("act", lambda e, ch=ch: e.activation(out=e1[:], in_=gates[:, ch, 4:8], func=AF.Exp, scale=-1.0), ["gates"], ["e1"])
            el("act", lambda e: e.activation(out=lf[:], in_=e1[:], func=AF.Ln, bias=1.0), ["e1"], ["lf"])
            ts(lf[:], lf[:], -1.0, None, ALU.mult, None, ["lf"], ["lf"])
            for h in range(4):
                el("dve", lambda e, h=h: e.tensor_copy(out=lfrep[:, h, :], in_=lf[:, h:h + 1].to_broadcast([128, 128])), ["lf"], ["lfrep"])
            for h in range(4):
                mm(pGD[:, h * 128:(h + 1) * 128], lfrep[:, h, :], maskT[:], True, True, ["lfrep", "maskT"], ["pGD"], sig=(h == 3))
            for h in range(4):
                mm(pM[:, h * 128:(h + 1) * 128], maskT[:], lfrep[:, h, :], True, True, ["maskT", "lfrep"], ["pM"], sig=(h == 3))
            el("act", lambda e: e.activation(out=emb[:].rearrange("p h t -> p (h t)"), in_=pGD[:], func=AF.Exp, scale=-1.0), ["pGD"], ["emb"])
            el("dve", lambda e: e.reciprocal(out=dec[:], in_=emb[:, :, 127]), ["emb"], ["dec"])
            tt(bcol[:], ig[:], pM[:].rearrange("p (h t) -> p h t", h=4)[:, :, 0], ALU.subtract, ["ig", "pM"], ["bcol"])
            el("act", lambda e: e.activation(out=wcol[:], in_=bcol[:], func=AF.Exp), ["bcol"], ["wcol"])
            el("dve", lambda e: e.tensor_copy(out=wcolb[:], in_=wcol[:].unsqueeze(2).to_broadcast([128, 4, 2])), ["wcol"], ["wcolb"])
            chk(81)
            for h in range(4):
                ps_, pn = acc_bank()
                no_v = dbg_stop in (821, 822, 823, 824)
                no_k = dbg_stop in (824,)
                for d in range(2):
                    if no_k:
                        continue
                    mm(ps_[:, 0:256], cT[:, 2 * h + d, tsl], Wk[:, h, d, :], d == 0, d == 1, ["cT", "Wqkv1"], [pn], sig=(no_v and d == 1))
                for d in range(2):
                    if no_v:
                        continue
                    mm(ps_[:, 256:512], uaT[:, 2 * h + d, 4 + ch * 128:4 + (ch + 1) * 128], Wv[:, h, d, :], d == 0, d == 1, ["uaT", "Wqkv2"], [pn])
                if dbg_stop != 822:
                    el("act", lambda e, ps_=ps_, ch=ch, h=h: e.copy(out=ktm[:, ch, h, :], in_=ps_[:, 0:256]), [pn], ["ktm"])
                if dbg_stop not in (822, 823):
                    el("act", lambda e, ps_=ps_, ch=ch, h=h: e.activation(out=vw[:, ch, h, :], in_=ps_[:, 256:512], func=AF.Copy, scale=wcol[:, h:h + 1]),
                       [pn, "wcol"], ["vw"])
            if main:
                for h in range(4):
                    el("dve", lambda e, h=h: e.tensor_copy(out=wrep[:, h, :], in_=wcol[:, h:h + 1].to_broadcast([128, 128])), ["wcol"], ["wrep"])
                for h in range(4):
                    for et in range(2):
                        mm(pS[:, h * 128:(h + 1) * 128], kT[:, h, et, tsl], qT[:, h, et, tsl], et == 0, et == 1, ["kT", "qT"], ["pS"], sig=(h == 3 and et == 1))
                tt(SmT[:].rearrange("p h t -> p (h t)"), pS[:], mask4[:].rearrange("p h t -> p (h t)"), ALU.mult, ["pS", "mask4"], ["SmT"])
                for h in range(4):
                    for d2 in range(2):
                        pn_, pnn = (pN0, "pN0") if h < 2 else (pN1, "pN1")
                        o_ = pn_[:, ((h % 2) * 2 + d2) * 128:((h % 2) * 2 + d2 + 1) * 128]
                        mm(o_, vw[:, ch, h, d2 * 128:(d2 + 1) * 128], SmT[:, h, :], True, False, ["vw", "SmT"], [pnn], sig=False)
                        mm(o_, Cbf[:, h, 0, d2 * 128:(d2 + 1) * 128], qT[:, h, 0, tsl], False, False, ["Cbf", "qT"], [pnn], sig=False)
                        mm(o_, Cbf[:, h, 1, d2 * 128:(d2 + 1) * 128], qT[:, h, 1, tsl], False, True, ["Cbf", "qT"], [pnn], sig=(h % 2 == 1 and d2 == 1))
                    o_ = pGD[:, h * 128:(h + 1) * 128]
                    mm(o_, wrep[:, h, :], SmT[:, h, :], True, False, ["wrep", "SmT", "emb", "dec"], ["pGD"], sig=False)
                    mm(o_, nrep[:, h, 0, :], qT[:, h, 0, tsl], False, False, ["nrep", "qT"], ["pGD"], sig=False)
                    mm(o_, nrep[:, h, 1, :], qT[:, h, 1, tsl], False, True, ["nrep", "qT"], ["pGD"], sig=(h == 3))
                el("act", lambda e: e.activation(out=aden[:].rearrange("p h t -> p (h t)"), in_=pGD[:], func=AF.Abs), ["pGD"], ["aden"])
                tt(aden[:], aden[:], emb[:], ALU.max, ["aden", "emb"], ["aden"])
                el("dve", lambda e: e.reciprocal(out=rden[:], in_=aden[:]), ["aden"], ["rden"])
                for h in range(4):
                    pn_, pnn = (pN0, "pN0") if h < 2 else (pN1, "pN1")
                    for d2 in range(2):
                        o_ = pn_[:, ((h % 2) * 2 + d2) * 128:((h % 2) * 2 + d2 + 1) * 128]
                        tt(hT[:, 2 * h + d2, :], o_, rden[:, h, :], ALU.mult, [pnn, "rden"], ["hT"])
                tt(hT[:], hT[:], oaT[:, :, tsl], ALU.mult, ["hT", "oaT"], ["hT"])
                el("act", lambda e: e.activation(out=sq[:], in_=hT[:], func=AF.Square), ["hT"], ["sq"])
                for h in range(4):
                    for d2 in range(2):
                        mm(pS[:, h * 128:(h + 1) * 128], ones_b[:], sq[:, 2 * h + d2, :], d2 == 0, d2 == 1, ["ones_b", "sq", "SmT"], ["pS"], sig=(h == 3 and d2 == 1))
                ts(rsh[:].rearrange("p h t -> p (h t)"), pS[:], 1.0 / 256.0, 1e-6, ALU.mult, ALU.add, ["pS"], ["rsh"])
                el("act", lambda e: e.activation(out=rsh[:], in_=rsh[:], func=AF.Ln), ["rsh"], ["rsh"])
                el("act", lambda e: e.activation(out=rsh[:], in_=rsh[:], func=AF.Exp, scale=-0.5), ["rsh"], ["rsh"])
                for mt in range(8):
                    el("dve", lambda e, mt=mt: e.scalar_tensor_tensor(out=hn[:, mt, :], in0=hT[:, mt, :], scalar=hg[:, mt:mt + 1],
                                                                      in1=rsh[:, mt // 2, :], op0=ALU.mult, op1=ALU.mult), ["hT", "hg", "rsh"], ["hn"])
                    el("dve", lambda e, mt=mt, tsl=tsl: e.scalar_tensor_tensor(out=hn[:, mt, :], in0=cT[:, mt, tsl], scalar=skp[:, mt:mt + 1],
                                                                      in1=hn[:, mt, :], op0=ALU.mult, op1=ALU.add), ["cT", "skp", "hn"], ["hn"])
                tt(hfT[:, :, tsl], hn[:], zaT[:, :, tsl], ALU.mult, ["hn", "zaT"], ["hfT"])
            chk(82)
            chk(821)
            chk(822)
            chk(823)
            chk(824)
            for h in range(4):
                for et in range(2):
                    ps_, pn = acc_bank()
                    mm(ps_[:, 0:256], ktm[:, ch, h, et * 128:(et + 1) * 128], vw[:, ch, h, :], True, True, ["ktm", "vw"], [pn], sig=False)
                    mm(ps_[:, 256:258], ktm[:, ch, h, et * 128:(et + 1) * 128], wcolb[:, h, :], True, True, ["ktm", "wcolb"], [pn])
                    tt(Cst[:, h, et, :], Cst[:, h, et, :], ps_[:, 0:256], ALU.add, ["Cst", pn, "Cbf"], ["Cst"])
                    tt(nst[:, h, et:et + 1], nst[:, h, et:et + 1], ps_[:, 256:257], ALU.add, ["nst", pn, "nrep"], ["nst"])
                ts(Cst[:, h, :, :], Cst[:, h, :, :], dec[:, h:h + 1], None, ALU.mult, None, ["Cst", "dec"], ["Cst"], eng="pool")
                ts(nst[:, h, :], nst[:, h, :], dec[:, h:h + 1], None, ALU.mult, None, ["nst", "dec"], ["nst"])
                el("act", lambda e, h=h: e.copy(out=Cbf[:, h, :, :], in_=Cst[:, h, :, :]), ["Cst"], ["Cbf"])
                for et in range(2):
                    el("dve", lambda e, h=h, et=et: e.tensor_copy(out=nrep[:, h, et, :], in_=nst[:, h, et:et + 1].to_broadcast([128, 128])),
                       ["nst"], ["nrep"])
        chk(9)
        if not main:
            return
        def gate_blocks(br):
            for bi_ in range(2):
                Wg_, wgn = load_block(BLK_G + 2 * br + bi_)
                Wgv = Wg_[:].rearrange("p (k n) -> p k n", k=8)
                for c4 in range(4):
                    ft = bi_ * 4 + c4
                    psg, png = acc_bank()
                    for kt in range(8):
                        mm(psg[:, 0:T], Wgv[:, kt, c4 * 128:(c4 + 1) * 128], xnT[:, kt, :], kt == 0, kt == 7, [wgn, "xnT"], [png])
                    el("act", lambda e, psg=psg, ft=ft: e.activation(out=gAll[:, ft, :], in_=psg[:, 0:T], func=AF.Sigmoid), [png], ["oaT"])

        gate_blocks(0)
        for hf_ in range(2):
            Wa, wan = load_block(BLK_AO + hf_)
            Wav = Wa[:].rearrange("p (k n) -> p k n", k=8)
            for c4 in range(4):
                ft = hf_ * 4 + c4
                psa, pna = acc_bank()
                for kt in range(8):
                    mm(psa[:, 0:T], Wav[:, kt, c4 * 128:(c4 + 1) * 128], hfT[:, kt, :], kt == 0, kt == 7, [wan, "hfT"], [pna])
                tt(mrgT[:, ft, :], psa[:, 0:T], gAll[:, ft, :], ALU.mult, [pna, "oaT"], ["zaT"])
        gate_blocks(1)
        Wbo, wbon = load_block(BLK_BO)
        Wbv = Wbo[:].rearrange("p (k n) -> p k n", k=4)
        for ft in range(8):
            psb, pnb = acc_bank()
            for kt in range(4):
                mm(psb[:, 0:T], Wbv[:, kt, ft * 128:(ft + 1) * 128], hbT[:, kt, :], kt == 0, kt == 3, [wbon, "hbT"], [pnb])
            el("act", lambda e, psb=psb: e.copy(out=m2[:].rearrange("p (c j) -> p j c", j=8), in_=psb[:, 0:T].rearrange("p (j c) -> p j c", j=8)),
               [pnb], ["m2"])
            tt(m2[:], m2[:], gAll[:, ft, :], ALU.mult, ["m2", "oaT"], ["m2"])
            tt(mrgT[:, ft, :], m2[:], mrgT[:, ft, :], ALU.add, ["m2", "zaT"], ["zaT"], eng="pool")
        Wo0, wo0n = load_block(BLK_WO); Wo1, wo1n = load_block(BLK_WO + 1)
        for ch in range(NCH):
            tsl = slice(ch * 128, (ch + 1) * 128)
            for hf_, (Wo_, won, pn_, pnn) in enumerate(((Wo0, wo0n, pN0, "pN0"), (Wo1, wo1n, pN1, "pN1"))):
                Wov = Wo_[:].rearrange("p (k n) -> p k n", k=8)
                for kt in range(8):
                    mm(pn_[:, :], mrgT[:, kt, tsl], Wov[:, kt, :], kt == 0, kt == 7, ["zaT", won], [pnn])
            el("act", lambda e: e.activation(out=junk[:, 0:512], in_=pN0[:], func=AF.Square, accum_out=ss2[:]), ["pN0"], ["junk", "ss2"])
            el("act", lambda e: e.activation(out=junk[:, 512:1024], in_=pN1[:], func=AF.Square, accum_out=rstd2[:]), ["pN1"], ["junk", "rstd2"])
            tt(ss2[:], ss2[:], rstd2[:], ALU.add, ["ss2", "rstd2"], ["ss2"])
            rsqrt_col(rstd2[:], ss2[:], 1.0 / 1024.0, "ss2", "rstd2")
            ob, obn = ot[ch % 2], "ot%d" % (ch % 2)
            xb_, xnm = xs[ch % 2], "xs%d" % (ch % 2)
            P.dma(lambda e, xb_=xb_, ch=ch: e.dma_start(out=xb_[:], in_=xsrc[row0 + ch * 128: row0 + (ch + 1) * 128, :]), writes=[xnm])
            for hf_, (pn_, pnn) in enumerate(((pN0, "pN0"), (pN1, "pN1"))):
                cs = slice(hf_ * 512, hf_ * 512 + 512)
                el("dve", lambda e, ob=ob, pn_=pn_, cs=cs: e.scalar_tensor_tensor(out=ob[:, cs], in0=pn_[:], scalar=rstd2[:, 0:1], in1=gpost[:, cs],
                                                                                 op0=ALU.mult, op1=ALU.mult), [pnn, "rstd2", "gpost"], [obn])
            tt(ob[:], ob[:], xb_[:], ALU.add, [obn, xnm], [obn], eng="pool")
            tok = P.dma(lambda e, ob=ob, ch=ch: e.dma_start(out=out[row0 + ch * 128: row0 + (ch + 1) * 128, :], in_=ob[:]), reads=[obn])
            out_toks.append(tok)

    sc_t1 = sbt([128, 2, 16], F32, "sc_t1"); sc_t2 = sbt([128, 2, 16], F32, "sc_t2")
    try:
        for ti in range(NPRE):
            do_tile(x_pre, ti * T, False, ti)
    except StopBuild:
        tk = P.dma(lambda e: e.dma_start(out=out[0:128, :], in_=gpost[:]), reads=["gpost"])
        P.final_wait("sp", [tk])
        P.emit(); P.close()
        return nc
    if NPRE > 0:
        ts(Cst[:].rearrange("p h e d -> p (h e d)"), Cst[:].rearrange("p h e d -> p (h e d)"), flg[:, 0:1], None, ALU.mult, None, ["Cst", "flg"], ["Cst"])
        ts(nst[:].rearrange("p h e -> p (h e)"), nst[:].rearrange("p h e -> p (h e)"), flg[:, 0:1], None, ALU.mult, None, ["nst", "flg"], ["nst"])
        el("act", lambda e: e.copy(out=Cbf[:].rearrange("p h e d -> p (h e d)"), in_=Cst[:].rearrange("p h e d -> p (h e d)")), ["Cst"], ["Cbf"])
        for h in range(4):
            for et in range(2):
                el("dve", lambda e, h=h, et=et: e.tensor_copy(out=nrep[:, h, et, :], in_=nst[:, h, et:et + 1].to_broadcast([128, 128])),
                   ["nst"], ["nrep"])
    for ti in range(NMAIN):
        do_tile(x_main, ti * T, True, NPRE + ti)
    P.final_wait("sp", out_toks)
    P.emit()
    P.close()
    return nc


T_TILE = 256
_cache = {}


def kernel(**inputs):
    x = np.ascontiguousarray(inputs["x"], dtype=np.float32)
    Bsz, L, Dm = x.shape
    half = L // 2
    npre = half // T_TILE
    nmain = half // T_TILE
    key = (T_TILE, npre, nmain)
    if key not in _cache:
        _cache[key] = build_program(T_TILE, npre, nmain)
    nc = _cache[key]
    shared = {}
    for k, v in inputs.items():
        if k == "x":
            continue
        a = np.ascontiguousarray(np.asarray(v, dtype=np.float32)[0])
        if k in ("b_i", "b_f", "log_dt", "norm_post_g"):
            a = a.reshape(1, -1)
        shared[k] = a
    in_maps = []
    zeros = np.zeros((half, Dm), np.float32)
    for core in range(8):
        b, hf = core // 2, core % 2
        m = dict(shared)
        m["x_main"] = np.ascontiguousarray(x[b, hf * half:(hf + 1) * half])
        m["x_pre"] = zeros if hf == 0 else np.ascontiguousarray(x[b, 0:half])
        m["flag"] = np.full((128, 1), float(hf), np.float32)
        in_maps.append(m)
    res = run_bass_kernel_spmd(nc, in_maps, core_ids=list(range(8)))
    outp = np.empty((Bsz, L, Dm), np.float32)
    for core in range(8):
        b, hf = core // 2, core % 2
        outp[b, hf * half:(hf + 1) * half] = res.results[core]["out"]
    return outp
```

```python
import contextlib
import numpy as np
import concourse.bass as bass
import concourse.mybir as mybir
from concourse.bass_utils import run_bass_kernel_spmd

F32 = mybir.dt.float32
BF16 = mybir.dt.bfloat16
AF = mybir.ActivationFunctionType
ALU = mybir.AluOpType
COMPUTE = ("pe", "act", "dve", "pool")
NDMA_SLOTS = 8
PI = float(np.pi)


class Prog:
    def __init__(self, nc):
        self.nc = nc
        self.stack = contextlib.ExitStack()
        self.engs = ("pe", "act", "dve", "pool", "sp")
        self.ops = {e: [] for e in self.engs}
        self.waited = {e: {} for e in self.engs}
        self.res = {}
        self.dma_use = {}
        self.dma_rr = {e: 0 for e in self.engs}
        self.sems = {}
        self.base = {e: 0 for e in COMPUTE}
        self.temp = None

    def sb(self, name, shape, dt):
        st = self.temp if self.temp is not None else self.stack
        return st.enter_context(self.nc.sbuf_tensor(name, list(shape), dt))

    def ps(self, name, shape, dt):
        return self.stack.enter_context(self.nc.psum_tensor(name, list(shape), dt))

    def _sem(self, key):
        if key not in self.sems:
            nm = "s_" + "_".join(str(k) for k in (key if isinstance(key, tuple) else (key,)))
            self.sems[key] = self.stack.enter_context(self.nc.semaphore(nm))
        return self.sems[key]

    def _deps(self, eng, reads, writes):
        deps = {}

        def add(tok):
            if tok is None:
                return
            k, v = tok
            if k == "pe" and eng == "pe":
                return
            if deps.get(k, -1) < v:
                deps[k] = v

        for r in reads:
            st = self.res.get(r)
            if st:
                for k, v in st[0].items():
                    add((k, v))
        for w in writes:
            st = self.res.get(w)
            if st:
                for k, v in st[0].items():
                    add((k, v))
                for k, v in st[1].items():
                    add((k, v))
        out = []
        wd = self.waited[eng]
        for k, v in deps.items():
            if wd.get(k, -1) >= v:
                continue
            wd[k] = v
            out.append((k, v))
        return out

    def _commit(self, tok, reads, writes):
        k, v = tok
        for r in reads:
            st = self.res.setdefault(r, [{}, {}])
            if st[1].get(k, -1) < v:
                st[1][k] = v
        for w in writes:
            old = self.res.get(w)
            wr = {}
            if old is not None and k not in COMPUTE:
                wr = {k2: v2 for k2, v2 in old[0].items() if k2 not in COMPUTE}
            wr[k] = v
            self.res[w] = [wr, {}]

    def op(self, eng, fn, reads=(), writes=(), sig=True):
        waits = self._deps(eng, reads, writes)
        idx = len(self.ops[eng])
        self.ops[eng].append(dict(fn=fn, waits=waits, sig=sig, dma=None))
        tok = (eng, idx)
        self._commit(tok, reads, writes)
        return tok

    def dma(self, fn, reads=(), writes=(), q="sp"):
        waits = self._deps(q, reads, writes)
        slot = self.dma_rr[q] % NDMA_SLOTS
        self.dma_rr[q] += 1
        key = ("d", q, slot)
        n = self.dma_use.get(key, 0)
        if n > 0:
            prev = n * 16
            if self.waited[q].get(key, -1) < prev:
                self.waited[q][key] = prev
                waits.append((key, prev))
        self.dma_use[key] = n + 1
        tok = (key, (n + 1) * 16)
        self.ops[q].append(dict(fn=fn, waits=waits, sig=False, dma=key))
        self._commit(tok, reads, writes)
        return tok

    def final_wait(self, eng, toks):
        self.ops[eng].append(dict(fn=None, waits=list(toks), sig=False, dma=None))

    def emit(self):
        nc = self.nc
        sigcount = {}
        totals = {}
        for e in COMPUTE:
            c = self.base[e]
            arr = []
            for o in self.ops[e]:
                if o["sig"]:
                    c += 1
                arr.append(c)
            need = [None] * len(arr)
            nxt = None
            for i in range(len(arr) - 1, -1, -1):
                if self.ops[e][i]["sig"]:
                    nxt = arr[i]
                need[i] = nxt
            sigcount[e] = need
            totals[e] = c
            self._sem(e)
        for k in self.dma_use:
            self._sem(k)

        def resolve(k, v):
            if k in COMPUTE:
                val = sigcount[k][v]
                assert val is not None, (k, v)
                return self.sems[k], val
            return self.sems[k], v

        with nc.Block() as block:

            def run(eng_name, eng):
                for o in self.ops[eng_name]:
                    for k, v in o["waits"]:
                        s, val = resolve(k, v)
                        eng.wait_ge(s, val)
                    if o["fn"] is None:
                        continue
                    ins = o["fn"](eng)
                    if o["dma"] is not None:
                        ins.then_inc(self.sems[o["dma"]], 16)
                    elif o["sig"]:
                        ins.then_inc(self.sems[eng_name], 1)
                for o2 in COMPUTE:
                    if o2 != eng_name and totals[o2] > 0:
                        eng.wait_ge(self.sems[o2], totals[o2])
                for k, n in self.dma_use.items():
                    eng.wait_ge(self.sems[k], n * 16)

            @block.tensor
            def _(e):
                run("pe", e)

            @block.scalar
            def _(e):
                run("act", e)

            @block.vector
            def _(e):
                run("dve", e)

            @block.gpsimd
            def _(e):
                run("pool", e)

            @block.sync
            def _(e):
                run("sp", e)

        self.base = totals
        self.ops = {e: [] for e in self.engs}
        self.waited = {e: {} for e in self.engs}
        self.res = {}

    def close(self):
        self.stack.close()


NBLK = 22
BLK_UA, BLK_UB, BLK_ZB, BLK_ZA, BLK_OA, BLK_G, BLK_AO, BLK_BO, BLK_WO, BLK_S5 = 0, 2, 3, 4, 6, 8, 12, 14, 15, 17
COL_UA, COL_ZA, COL_OA, COL_I, COL_UB, COL_ZB, COL_G = 0, 1024, 2048, 3072, 3080, 3592, 4104


def build_program(T, NPRE, NMAIN, dbg_stop=0):
    NCH = T // 128
    NC8 = T // 8
    nc = bass.Bass("TRN2", target_bir_lowering=False)
    dram = {}

    def din(name, shape):
        dram[name] = nc.dram_tensor(name, list(shape), F32, kind="ExternalInput").ap()
        return dram[name]

    x_pre = din("x_pre", [max(NPRE, 1) * T, 1024])
    x_main = din("x_main", [NMAIN * T, 1024])
    flag = din("flag", [128, 1])
    norm_pre_g = din("norm_pre_g", [1024]); w_in = din("w_in", [1024, 6152])
    conv_w = din("conv_w", [4, 1024]); conv_b = din("conv_b", [1024])
    w_q = din("w_q", [4, 256, 256]); w_k = din("w_k", [4, 256, 256]); w_v = din("w_v", [4, 256, 256])
    b_i = din("b_i", [1, 4]); b_f = din("b_f", [1, 4]); head_g = din("head_g", [1024]); skip_a = din("skip_a", [1024])
    w_a_out = din("w_a_out", [1024, 1024])
    lam_re = din("lam_re", [32, 64]); lam_im = din("lam_im", [32, 64]); log_dt = din("log_dt", [1, 32])
    B_re = din("B_re", [32, 64, 16]); B_im = din("B_im", [32, 64, 16])
    C_re = din("C_re", [32, 16, 64]); C_im = din("C_im", [32, 16, 64]); D_skip = din("D_skip", [32, 16])
    w_glu = din("w_glu", [512, 512]); b_glu = din("b_glu", [512]); w_b_out = din("w_b_out", [512, 1024])
    w_o = din("w_o", [1024, 1024]); norm_post_g = din("norm_post_g", [1, 1024])
    out = nc.dram_tensor("out", [NMAIN * T, 1024], F32, kind="ExternalOutput").ap()
    WS = nc.dram_tensor("wscratch", [NBLK, 128, 4096], BF16, kind="Internal").ap()

    P = Prog(nc)
    uid = [0]

    def sbt(shape, dt, name=None):
        uid[0] += 1
        return P.sb(name or ("t%d" % uid[0]), shape, dt)

    ident_f = sbt([128, 128], F32); ident_b = sbt([128, 128], BF16)
    maskT = sbt([128, 128], F32); mask4 = sbt([128, 4, 128], F32); ones_b = sbt([128, 128], BF16)
    P.op("pool", lambda e: e.memset(ident_f[:], 1.0), writes=["ident_f"])
    P.op("pool", lambda e: e.affine_select(out=ident_f[:], in_=ident_f[:], pattern=[[-1, 128]], compare_op=ALU.is_equal,
                                           fill=0.0, base=0, channel_multiplier=1), reads=["ident_f"], writes=["ident_f"])
    P.op("dve", lambda e: e.tensor_copy(out=ident_b[:], in_=ident_f[:]), reads=["ident_f"], writes=["ident_b"])
    P.op("pool", lambda e: e.memset(maskT[:], 1.0), writes=["maskT"])
    P.op("pool", lambda e: e.affine_select(out=maskT[:], in_=maskT[:], pattern=[[1, 128]], compare_op=ALU.is_ge,
                                           fill=0.0, base=0, channel_multiplier=-1), reads=["maskT"], writes=["maskT"])
    for h in range(4):
        P.op("pool", lambda e, h=h: e.tensor_copy(out=mask4[:, h, :], in_=maskT[:]), reads=["maskT"], writes=["mask4"])
    P.op("pool", lambda e: e.memset(ones_b[:], 1.0), writes=["ones_b"])

    gpre = sbt([128, 8], F32); cb = sbt([128, 8], F32); hg = sbt([128, 8], F32); skp = sbt([128, 8], F32)
    cw = sbt([128, 8, 4], F32); bglu = sbt([128, 4], F32); gpost = sbt([128, 1024], F32); bif = sbt([128, 8], F32)
    flg = sbt([128, 1], F32)
    nonc = dict(allow_slow_non_contiguous=True)
    P.dma(lambda e: e.dma_start(out=gpre[:], in_=norm_pre_g.rearrange("(k p) -> p k", p=128), **nonc), writes=["gpre"])
    P.dma(lambda e: e.dma_start(out=cb[:], in_=conv_b.rearrange("(k p) -> p k", p=128), **nonc), writes=["cb"])
    P.dma(lambda e: e.dma_start(out=hg[:], in_=head_g.rearrange("(k p) -> p k", p=128), **nonc), writes=["hg"])
    P.dma(lambda e: e.dma_start(out=skp[:], in_=skip_a.rearrange("(k p) -> p k", p=128), **nonc), writes=["skp"])
    for k in range(4):
        P.dma(lambda e, k=k: e.dma_start(out=cw[:, :, k], in_=conv_w[k].rearrange("(m p) -> p m", p=128), **nonc), writes=["cw"])
    P.dma(lambda e: e.dma_start(out=bglu[:], in_=b_glu.rearrange("(k p) -> p k", p=128), **nonc), writes=["bglu"])
    P.dma(lambda e: e.dma_start(out=gpost[:], in_=norm_post_g.partition_broadcast(128)), writes=["gpost"])
    P.dma(lambda e: e.dma_start(out=bif[:, 0:4], in_=b_i.partition_broadcast(128)), writes=["bif"])
    P.dma(lambda e: e.dma_start(out=bif[:, 4:8], in_=b_f.partition_broadcast(128)), writes=["bif"])
    P.dma(lambda e: e.dma_start(out=flg[:], in_=flag), writes=["flg"])

    Wqkv = [sbt([128, 4, 2, 256], BF16) for _ in range(3)]
    Wif = sbt([128, 8, 8], BF16); Wglu = sbt([128, 4, 512], BF16); cdiag = sbt([128, 8, 4, 128], BF16)
    AR32 = sbt([128, 2, 16], F32); ANI = sbt([128, 16], F32); API = sbt([128, 16], F32)
    pA = P.ps("pA", [128, 512], F32); pB = P.ps("pB", [128, 512], F32)
    pT = P.ps("pT", [128, 1024], BF16); pGD = P.ps("pGD", [128, 512], F32)
    pS = P.ps("pS", [128, 512], F32); pN0 = P.ps("pN0", [128, 512], F32); pN1 = P.ps("pN1", [128, 512], F32)
    pM = P.ps("pM", [128, 512], F32)
    P.temp = contextlib.ExitStack()
    stg = sbt([128, 4096], F32, "stg")
    stgb = sbt([128, 4096], BF16, "stgb")
    for wi, (wsrc, scl) in enumerate(((w_q, 1.0), (w_k, 1.0 / 16.0), (w_v, 1.0))):
        P.dma(lambda e, wsrc=wsrc: e.dma_start(out=stg[:, 0:2048].rearrange("p (h d n) -> p h d n", h=4, d=2),
                                               in_=wsrc.rearrange("h (d p) n -> p h d n", p=128)), writes=["stg"])
        P.op("dve", lambda e, wi=wi, scl=scl: e.tensor_scalar(out=Wqkv[wi][:].rearrange("p h d n -> p (h d n)"), in0=stg[:, 0:2048],
                                                              scalar1=scl, scalar2=None, op0=ALU.mult), reads=["stg"], writes=["Wqkv%d" % wi])
    Wq, Wk, Wv = Wqkv
    P.dma(lambda e: e.dma_start(out=stg[:, 0:64].rearrange("p (k n) -> p k n", k=8),
                                in_=w_in[:, COL_I:COL_I + 8].rearrange("(k p) n -> p k n", p=128), **nonc), writes=["stg"])
    for kt in range(8):
        P.op("dve", lambda e, kt=kt: e.tensor_scalar(out=Wif[:, kt, :], in0=stg[:, kt * 8:(kt + 1) * 8], scalar1=gpre[:, kt:kt + 1],
                                                     scalar2=None, op0=ALU.mult), reads=["stg", "gpre"], writes=["Wif"])
    P.dma(lambda e: e.dma_start(out=stg[:, 0:2048].rearrange("p (k n) -> p k n", k=4),
                                in_=w_glu.rearrange("(k p) n -> p k n", p=128)), writes=["stg"])
    P.op("dve", lambda e: e.tensor_copy(out=Wglu[:].rearrange("p k n -> p (k n)"), in_=stg[:, 0:2048]), reads=["stg"], writes=["Wglu"])
    for mt in range(8):
        for k in range(4):
            P.op("dve", lambda e, mt=mt, k=k: e.tensor_scalar(out=cdiag[:, mt, k, :], in0=ident_f[:], scalar1=cw[:, mt, k:k + 1],
                                                               scalar2=None, op0=ALU.mult), reads=["ident_f", "cw"], writes=["cdiag"])

    def stage_block(blk, src_ap_f, scale_gpre, nk):
        ncol = 4096 // nk
        P.dma(lambda e: e.dma_start(out=stg[:].rearrange("p (k n) -> p k n", k=nk), in_=src_ap_f), writes=["stg"])
        if scale_gpre:
            for kt in range(nk):
                P.op("dve",
                     lambda e, kt=kt: e.tensor_scalar(out=stgb[:, kt * ncol:(kt + 1) * ncol], in0=stg[:, kt * ncol:(kt + 1) * ncol],
                                                      scalar1=gpre[:, kt:kt + 1], scalar2=None, op0=ALU.mult),
                     reads=["stg", "gpre"], writes=["stgb"])
        else:
            P.op("dve", lambda e: e.tensor_copy(out=stgb[:, 0:2048], in_=stg[:, 0:2048]), reads=["stg"], writes=["stgb"])
            P.op("act", lambda e: e.copy(out=stgb[:, 2048:4096], in_=stg[:, 2048:4096]), reads=["stg"], writes=["stgb"])
        P.dma(lambda e: e.dma_start(out=WS[blk], in_=stgb[:]), reads=["stgb"], writes=["WS%d" % blk])

    def win_cols(c0):
        return w_in[:, c0:c0 + 512].rearrange("(k p) n -> p k n", p=128)

    win_blocks = [(BLK_UA, COL_UA), (BLK_UA + 1, COL_UA + 512), (BLK_UB, COL_UB), (BLK_ZB, COL_ZB), (BLK_ZA, COL_ZA),
                  (BLK_ZA + 1, COL_ZA + 512), (BLK_OA, COL_OA), (BLK_OA + 1, COL_OA + 512)] + [(BLK_G + i, COL_G + 512 * i) for i in range(4)]
    for blk, c0 in win_blocks:
        stage_block(blk, win_cols(c0), True, 8)
    for i in range(2):
        stage_block(BLK_AO + i, w_a_out[:, 512 * i:512 * i + 512].rearrange("(k p) n -> p k n", p=128), False, 8)
        stage_block(BLK_WO + i, w_o[:, 512 * i:512 * i + 512].rearrange("(k p) n -> p k n", p=128), False, 8)
    stage_block(BLK_BO, w_b_out.rearrange("(k p) n -> p k n", p=128), False, 4)

    def small(n, name=None):
        return sbt([128, n], F32, name)

    cnt = [0]

    def el(eng, fn, r, w):
        P.op(eng, fn, reads=r, writes=w)

    def tt(outp, a, b, op, r, w, eng="dve"):
        el(eng, lambda e: e.tensor_tensor(out=outp, in0=a, in1=b, op=op), r, w)

    def ts(outp, a, s1, s2, op0, op1, r, w, eng="dve"):
        if op1 is None:
            el(eng, lambda e: e.tensor_scalar(out=outp, in0=a, scalar1=s1, scalar2=None, op0=op0), r, w)
        else:
            el(eng, lambda e: e.tensor_scalar(out=outp, in0=a, scalar1=s1, scalar2=s2, op0=op0, op1=op1), r, w)

    LR = small(32); LI = small(32); DT = small(32)
    for hf in range(2):
        sl = slice(64 * hf, 64 * hf + 64)
        P.dma(lambda e, sl=sl: e.dma_start(out=LR[sl, :], in_=lam_re.rearrange("g p -> p g"), **nonc), writes=["LR"])
        P.dma(lambda e, sl=sl: e.dma_start(out=LI[sl, :], in_=lam_im.rearrange("g p -> p g"), **nonc), writes=["LI"])
    P.dma(lambda e: e.dma_start(out=DT[:], in_=log_dt.partition_broadcast(128)), writes=["DT"])
    el("act", lambda e: e.activation(out=DT[:], in_=DT[:], func=AF.Exp), ["DT"], ["DT"])
    TH = small(32); MAG = small(32); t0 = small(32); t1 = small(32); t2 = small(32); kk = small(32)
    tt(TH[:], LI[:], DT[:], ALU.mult, ["LI", "DT"], ["TH"])
    tt(t0[:], LR[:], DT[:], ALU.mult, ["LR", "DT"], ["t0"])
    el("act", lambda e: e.activation(out=MAG[:], in_=t0[:], func=AF.Exp), ["t0"], ["MAG"])
    IMAG2 = small(32)
    el("act", lambda e: e.activation(out=IMAG2[:], in_=t0[:], func=AF.Exp, scale=-2.0), ["t0"], ["IMAG2"])

    def sin_of(dst, src, shift, nm):
        ts(t1[:], src, shift, None, ALU.add, None, [nm, "t1"], ["t1"])
        el("pool", lambda e: e.memset(kk[:], 0.0), [], ["kk"])
        for m in range(7):
            ts(t2[:], t1[:], (2 * m + 1) * PI, None, ALU.is_gt, None, ["t1"], ["t2"])
            tt(kk[:], kk[:], t2[:], ALU.add, ["kk", "t2"], ["kk"])
        ts(kk[:], kk[:], -2.0 * PI, None, ALU.mult, None, ["kk"], ["kk"])
        tt(t1[:], t1[:], kk[:], ALU.add, ["t1", "kk"], ["t1"])
        el("act", lambda e: e.activation(out=dst, in_=t1[:], func=AF.Sin), ["t1"], [nm + "_s"])

    SN = small(32); CS = small(32)
    sin_of(SN[:], TH[:], 0.0, "TH")
    sin_of(CS[:], TH[:], PI / 2.0, "TH")
    pwr = sbt([128, 9, 32], F32); pwi = sbt([128, 9, 32], F32); pnr = sbt([128, 8, 32], F32); pni = sbt([128, 8, 32], F32)
    el("pool", lambda e: e.memset(pwr[:, 0, :], 1.0), [], ["pw"]); el("pool", lambda e: e.memset(pwi[:, 0, :], 0.0), [], ["pw"])
    el("pool", lambda e: e.memset(pnr[:, 0, :], 1.0), [], ["pn"]); el("pool", lambda e: e.memset(pni[:, 0, :], 0.0), [], ["pn"])
    tt(pwr[:, 1, :], MAG[:], CS[:], ALU.mult, ["MAG", "TH_s"], ["pw"])
    tt(pwi[:, 1, :], MAG[:], SN[:], ALU.mult, ["MAG", "TH_s"], ["pw"])
    tt(pnr[:, 1, :], pwr[:, 1, :], IMAG2[:], ALU.mult, ["pw", "IMAG2"], ["pn"])
    tt(t0[:], pwi[:, 1, :], IMAG2[:], ALU.mult, ["pw", "IMAG2"], ["t0"])
    ts(pni[:, 1, :], t0[:], -1.0, None, ALU.mult, None, ["t0"], ["pn"])

    def cmul(or_, oi_, ar, ai, br, bi, r, w):
        raise NotImplementedError

    u0 = small(32); u1 = small(32)
    for k in range(1, 8):
        for (xr, xi, nm, lim) in ((pwr, pwi, "pw", 9), (pnr, pni, "pn", 8)):
            if k + 1 >= lim:
                continue
            tt(u0[:], xr[:, k, :], xr[:, 1, :], ALU.mult, [nm], ["u0"])
            tt(u1[:], xi[:, k, :], xi[:, 1, :], ALU.mult, [nm], ["u1"])
            tt(xr[:, k + 1, :], u0[:], u1[:], ALU.subtract, ["u0", "u1"], [nm])
            tt(u0[:], xr[:, k, :], xi[:, 1, :], ALU.mult, [nm], ["u0"])
            tt(u1[:], xi[:, k, :], xr[:, 1, :], ALU.mult, [nm], ["u1"])
            tt(xi[:, k + 1, :], u0[:], u1[:], ALU.add, ["u0", "u1"], [nm])
    den = small(32); qr = small(32); qi = small(32); nr = small(32)
    tt(u0[:], LR[:], LR[:], ALU.mult, ["LR"], ["u0"]); tt(u1[:], LI[:], LI[:], ALU.mult, ["LI"], ["u1"])
    tt(den[:], u0[:], u1[:], ALU.add, ["u0", "u1"], ["den"])
    el("dve", lambda e: e.reciprocal(out=den[:], in_=den[:]), ["den"], ["den"])
    ts(nr[:], pwr[:, 1, :], -1.0, None, ALU.add, None, ["pw"], ["nr"])
    tt(u0[:], nr[:], LR[:], ALU.mult, ["nr", "LR"], ["u0"]); tt(u1[:], pwi[:, 1, :], LI[:], ALU.mult, ["pw", "LI"], ["u1"])
    tt(qr[:], u0[:], u1[:], ALU.add, ["u0", "u1"], ["qr"]); tt(qr[:], qr[:], den[:], ALU.mult, ["qr", "den"], ["qr"])
    tt(u0[:], pwi[:, 1, :], LR[:], ALU.mult, ["pw", "LR"], ["u0"]); tt(u1[:], nr[:], LI[:], ALU.mult, ["nr", "LI"], ["u1"])
    tt(qi[:], u0[:], u1[:], ALU.subtract, ["u0", "u1"], ["qi"]); tt(qi[:], qi[:], den[:], ALU.mult, ["qi", "den"], ["qi"])
    Br = sbt([128, 32, 16], F32); Bi = sbt([128, 32, 16], F32); bbr = sbt([128, 32, 16], F32); bbi = sbt([128, 32, 16], F32)
    v0 = sbt([128, 32, 16], F32); v1 = sbt([128, 32, 16], F32)
    for hf in range(2):
        sl = slice(64 * hf, 64 * hf + 64)
        P.dma(lambda e, sl=sl: e.dma_start(out=Br[sl], in_=B_re.rearrange("g p n -> p g n"), **nonc), writes=["Br"])
        P.dma(lambda e, sl=sl: e.dma_start(out=Bi[sl], in_=B_im.rearrange("g p n -> p g n"), **nonc), writes=["Bi"])

    def bc(s):
        return s.unsqueeze(2).to_broadcast([128, 32, 16])

    def cmul3(orr, oii, sr, si, sn, xr, xi, xn, on):
        xn = [xn] if isinstance(xn, str) else list(xn)
        sn = [sn] if isinstance(sn, str) else list(sn)
        tt(v0[:], xr, bc(sr), ALU.mult, xn + sn, ["v0"]); tt(v1[:], xi, bc(si), ALU.mult, xn + sn, ["v1"])
        tt(orr, v0[:], v1[:], ALU.subtract, ["v0", "v1"], [on])
        tt(v0[:], xi, bc(sr), ALU.mult, xn + sn, ["v0"]); tt(v1[:], xr, bc(si), ALU.mult, xn + sn, ["v1"])
        tt(oii, v0[:], v1[:], ALU.add, ["v0", "v1"], [on])

    el("dve", lambda e: e.tensor_copy(out=u0[:], in_=qr[:]), ["qr"], ["qq"])
    cmul3(bbr[:], bbi[:], qr[:], qi[:], ["qr", "qi"], Br[:], Bi[:], ["Br", "Bi"], "bb")
    CTr = sbt([128, 32, 16], F32); CTi = sbt([128, 32, 16], F32)
    Cdup = sbt([128, 4, 2, 64], F32)
    for (Csrc, CTt, nm) in ((C_re, CTr, "CTr"), (C_im, CTi, "CTi")):
        for d in range(2):
            P.dma(lambda e, Csrc=Csrc, d=d: e.dma_start(out=Cdup[:, :, d, :], in_=Csrc.rearrange("(t g) n p -> (g n) t p", t=4)),
                  writes=["Cdup"])
        for t in range(4):
            P.op("pe", lambda e, t=t: e.transpose(out=pA[:, t * 128:(t + 1) * 128], in_=Cdup[:, t, :, :].rearrange("q d p -> q (d p)"),
                                                  identity=ident_f[:]), reads=["Cdup", "ident_f"], writes=["pA"])
        el("act", lambda e, CTt=CTt: e.copy(out=CTt[:].rearrange("p g n -> p (g n)"), in_=pA[:]), ["pA"], [nm])
    Er = sbt([128, 32, 8, 16], F32); Ei = sbt([128, 32, 8, 16], F32)
    Fr = sbt([128, 32, 8, 16], F32); Fi = sbt([128, 32, 8, 16], F32)
    for j in range(8):
        cmul3(Er[:, :, j, :], Ei[:, :, j, :], pwr[:, 7 - j, :], pwi[:, 7 - j, :], "pw", bbr[:], bbi[:], "bb", "E")
        cmul3(Fr[:, :, j, :], Fi[:, :, j, :], pwr[:, j + 1, :], pwi[:, j + 1, :], "pw", CTr[:], CTi[:], ["CTr", "CTi"], "F")
    halfm = sbt([128, 2], F32)
    el("pool", lambda e: e.memset(halfm[:], 0.0), [], ["halfm"])
    el("pool", lambda e: e.memset(halfm[0:64, 0:1], 1.0), ["halfm"], ["halfm"])
    el("pool", lambda e: e.memset(halfm[64:128, 1:2], 1.0), ["halfm"], ["halfm"])
    for ri, (Et, blk) in enumerate(((Er, BLK_S5), (Ei, BLK_S5 + 1))):
        el("pool", lambda e: e.memset(stgb[:], 0.0), ["stgb"], ["stgb"])
        for g in range(32):
            ps_ = pA if g % 2 == 0 else pB
            nm = "pA" if g % 2 == 0 else "pB"
            P.op("pe", lambda e, Et=Et, g=g, ps_=ps_: e.transpose(out=ps_[:, 0:64], in_=Et[0:64, g, :, :].rearrange("p j n -> p (j n)"),
                                                                  identity=ident_f[0:64, 0:64]), reads=["E", "ident_f"], writes=[nm])
            c0 = g * 128 + 64 * (g % 2)
            el("act" if g % 2 == 0 else "dve",
               (lambda e, ps_=ps_, c0=c0: e.copy(out=stgb[:, c0:c0 + 64], in_=ps_[:, 0:64])) if g % 2 == 0 else
               (lambda e, ps_=ps_, c0=c0: e.tensor_copy(out=stgb[:, c0:c0 + 64], in_=ps_[:, 0:64])), [nm], ["stgb"])
        P.dma(lambda e, blk=blk: e.dma_start(out=WS[blk], in_=stgb[:]), reads=["stgb"], writes=["WS%d" % blk])
    for ri, (Ft, blk, sg) in enumerate(((Fr, BLK_S5 + 3, 1.0), (Fi, BLK_S5 + 4, -1.0))):
        for g in range(32):
            P.op("dve",
                 lambda e, Ft=Ft, g=g, sg=sg: e.tensor_scalar(out=stgb[:, g * 128:(g + 1) * 128], in0=Ft[:, g, :, :].rearrange("p j n -> p (j n)"),
                                                              scalar1=halfm[:, (g % 2):(g % 2) + 1], scalar2=sg, op0=ALU.mult, op1=ALU.mult),
                 reads=["F", "halfm"], writes=["stgb"])
        P.dma(lambda e, blk=blk: e.dma_start(out=WS[blk], in_=stgb[:]), reads=["stgb"], writes=["WS%d" % blk])
    Gr = sbt([128, 32, 8, 16], F32); Gni = sbt([128, 32, 8, 16], F32)
    i8r = small(32); i8i = small(32)
    tt(u0[:], pnr[:, 7, :], pnr[:, 1, :], ALU.mult, ["pn"], ["u0"]); tt(u1[:], pni[:, 7, :], pni[:, 1, :], ALU.mult, ["pn"], ["u1"])
    tt(i8r[:], u0[:], u1[:], ALU.subtract, ["u0", "u1"], ["i8"])
    tt(u0[:], pnr[:, 7, :], pni[:, 1, :], ALU.mult, ["pn"], ["u0"]); tt(u1[:], pni[:, 7, :], pnr[:, 1, :], ALU.mult, ["pn"], ["u1"])
    tt(i8i[:], u0[:], u1[:], ALU.add, ["u0", "u1"], ["i8"])
    for j in range(8):
        cmul3(Gr[:, :, j, :], Gni[:, :, j, :], i8r[:], i8i[:], "i8", Fr[:, :, j, :], Fi[:, :, j, :], "F", "G")
    ts(Gni[:], Gni[:], -1.0, None, ALU.mult, None, ["G"], ["G"])
    bmask = sbt([128, 8, 16], F32); Dcol = sbt([128, 32], F32)
    el("pool", lambda e: e.memset(bmask[:], 1.0), [], ["bmask"])
    el("pool", lambda e: e.affine_select(out=bmask[:], in_=bmask[:], pattern=[[16, 8], [0, 16]], compare_op=ALU.is_ge, fill=0.0,
                                         base=15, channel_multiplier=-1), ["bmask"], ["bmask"])
    for j in range(8):
        P.dma(lambda e, j=j: e.dma_start(out=Dcol[16 * j:16 * j + 16, :], in_=D_skip.rearrange("g n -> n g"), **nonc), writes=["Dcol"])
    wtmp = sbt([128, 128], F32)
    for g in range(32):
        ps_ = pA if g % 2 == 0 else pB
        nm = "pA" if g % 2 == 0 else "pB"
        P.op("pe", lambda e, g=g, ps_=ps_: e.matmul(ps_[:, 0:128], lhsT=Er[0:64, g, :, :].rearrange("p j n -> p (j n)"),
                                                    rhs=Gr[0:64, g, :, :].rearrange("p j n -> p (j n)"), start=True, stop=False),
             reads=["E", "G"], writes=[nm], sig=False)
        P.op("pe", lambda e, g=g, ps_=ps_: e.matmul(ps_[:, 0:128], lhsT=Ei[0:64, g, :, :].rearrange("p j n -> p (j n)"),
                                                    rhs=Gni[0:64, g, :, :].rearrange("p j n -> p (j n)"), start=False, stop=True),
             reads=["E", "G"], writes=[nm])
        tt(wtmp[:], ps_[:, 0:128], bmask[:].rearrange("p j n -> p (j n)"), ALU.mult, [nm, "bmask"], ["wtmp"])
        el("dve", lambda e, g=g: e.scalar_tensor_tensor(out=stgb[:, g * 128:(g + 1) * 128], in0=ident_f[:], scalar=Dcol[:, g:g + 1],
                                                        in1=wtmp[:], op0=ALU.mult, op1=ALU.add), ["ident_f", "Dcol", "wtmp", "stgb"], ["stgb"])
    P.dma(lambda e: e.dma_start(out=WS[BLK_S5 + 2], in_=stgb[:]), reads=["stgb"], writes=["WS%d" % (BLK_S5 + 2)])
    for g2 in range(2):
        sl = slice(64 * g2, 64 * g2 + 64)
        src_r = pwr[sl, 8, :].rearrange("p (q t) -> p q t", t=2)[:, :, g2]
        src_i = pwi[sl, 8, :].rearrange("p (q t) -> p q t", t=2)[:, :, g2]
        el("dve", lambda e, sl=sl, src_r=src_r: e.tensor_copy(out=AR32[sl, 0, :], in_=src_r), ["pw"], ["AR32"])
        el("dve", lambda e, sl=sl, src_r=src_r: e.tensor_copy(out=AR32[sl, 1, :], in_=src_r), ["pw"], ["AR32"])
        el("dve", lambda e, sl=sl, src_i=src_i: e.tensor_copy(out=API[sl, :], in_=src_i), ["pw"], ["API"])
        ts(ANI[sl, :], src_i, -1.0, None, ALU.mult, None, ["pw"], ["ANI"])

    if dbg_stop == 1:
        tk = P.dma(lambda e: e.dma_start(out=out[0:128, :], in_=gpost[:]), reads=["gpost"])
        P.final_wait("sp", [tk])
        P.emit(); P.temp.close(); P.close()
        return nc
    P.emit()
    P.temp.close()
    P.temp = None
    NSLOT = 3
    wslot = [sbt([128, 4096], BF16, "wslot%d" % i) for i in range(NSLOT)]
    slot_rr = [0]

    def load_block(blk):
        s = slot_rr[0] % NSLOT
        slot_rr[0] += 1
        P.dma(lambda e: e.dma_start(out=wslot[s][:], in_=WS[blk]), reads=["WS%d" % blk], writes=["wslot%d" % s])
        return wslot[s], "wslot%d" % s

    xs = [sbt([128, 1024], F32, "xs%d" % i) for i in range(2)]
    junk = sbt([128, 1024], BF16); xnb = sbt([128, 1024], BF16)
    ss = small(1); rstd = small(1)
    xnT = sbt([128, 8, T], BF16, "xnT"); uaT = sbt([128, 8, T + 4], BF16, "uaT"); cT = sbt([128, 8, T], BF16, "cT")
    zaT = sbt([128, 8, T], BF16, "zaT"); oaT = sbt([128, 8, T], BF16, "oaT"); zbT = sbt([128, 4, T], BF16, "zbT")
    qT = sbt([128, 4, 2, T], BF16, "qT"); kT = sbt([128, 4, 2, T], BF16, "kT")
    ktm = sbt([128, NCH, 4, 256], BF16, "ktm"); vw = sbt([128, NCH, 4, 256], BF16, "vw")
    gates = sbt([128, NCH, 8], F32, "gates")
    hfT = sbt([128, 8, T], BF16, "hfT"); mrgT = zaT; hbT = sbt([128, 4, T], BF16, "hbT")
    Cst = sbt([128, 4, 2, 256], F32, "Cst"); Cbf = sbt([128, 4, 2, 256], BF16, "Cbf")
    nst = sbt([128, 4, 2], F32, "nst"); nrep = sbt([128, 4, 2, 128], BF16, "nrep")
    Utm = sbt([128, 32, 8, 16], BF16, "Utm"); U2 = sbt([128, 32, NC8], BF16, "U2")
    Xall = sbt([128, NC8, 2, 16], F32, "Xall"); Sall = sbt([128, NC8 + 1, 2, 16], F32, "Sall"); Sbf = sbt([128, 2, 16, NC8], BF16, "Sbf")
    Yg = sbt([128, 32, NC8], BF16, "Yg"); Ytm = Utm[:].rearrange("p g j n -> p (g j n)").rearrange("p (j c) -> p j c", j=8); yT = sbt([128, 4, T], BF16, "yT")
    for (t_, nm) in ((Cst, "Cst"), (Cbf, "Cbf"), (nst, "nst"), (nrep, "nrep"), (Sall, "Sall"), (uaT, "uaT")):
        flat = t_[:]
        el("pool", lambda e, flat=flat: e.memset(flat, 0.0), [], [nm])

    lfrep = sbt([128, 4, 128], F32); lf = small(4, "lf"); ig = small(4); bcol = small(4); wcol = small(4, "wcol"); wcolb = sbt([128, 4, 2], BF16)
    wrep = sbt([128, 4, 128], BF16); emb = sbt([128, 4, 128], F32, "emb"); dec = small(4, "dec"); SmT = sbt([128, 4, 128], BF16, "SmT")
    aden = sbt([128, 4, 128], F32); rden = sbt([128, 4, 128], F32, "rden"); hT = sbt([128, 8, 128], F32, "hT"); sq = sbt([128, 8, 128], BF16)
    rsh = sbt([128, 4, 128], F32); hn = sbt([128, 8, 128], F32, "hn"); e1 = small(4)
    gAll = oaT; m1 = sbt([128, T], F32); m2 = sbt([128, T], F32)
    ot = [sbt([128, 1024], F32, "ot%d" % i) for i in range(2)]
    ss2 = small(1); rstd2 = small(1); sgl = sbt([128, T], BF16); xg = sbt([128, T], F32)

    def rsqrt_col(dst, src, scale, nm_src, nm_dst):
        ts(dst, src, scale, 1e-6, ALU.mult, ALU.add, [nm_src], [nm_dst])
        el("act", lambda e: e.activation(out=dst, in_=dst, func=AF.Ln), [nm_dst], [nm_dst])
        el("act", lambda e: e.activation(out=dst, in_=dst, func=AF.Exp, scale=-0.5), [nm_dst], [nm_dst])

    def mm(outp, lhsT, rhs, start, stop, r, w, sig=None):
        P.op("pe", lambda e: e.matmul(outp, lhsT=lhsT, rhs=rhs, start=start, stop=stop), reads=r, writes=w,
             sig=(stop if sig is None else sig))

    acc_rr = [0]

    def acc_bank():
        acc_rr[0] += 1
        return (pA, "pA") if acc_rr[0] % 2 else (pB, "pB")

    out_toks = []

    class StopBuild(Exception):
        pass

    def chk(level):
        if dbg_stop == level:
            raise StopBuild()

    def do_tile(xsrc, row0, main, ti):
        for sub in range(NCH):
            xb_, xnm = xs[sub % 2], "xs%d" % (sub % 2)
            P.dma(lambda e, xb_=xb_, sub=sub: e.dma_start(out=xb_[:], in_=xsrc[row0 + sub * 128: row0 + (sub + 1) * 128, :]), writes=[xnm])
            el("act", lambda e, xb_=xb_: e.activation(out=junk[:], in_=xb_[:], func=AF.Square, accum_out=ss[:]), [xnm], ["junk", "ss"])
            rsqrt_col(rstd[:], ss[:], 1.0 / 1024.0, "ss", "rstd")
            ts(xnb[:], xb_[:], rstd[:, 0:1], None, ALU.mult, None, [xnm, "rstd"], ["xnb"])
            for kt in range(8):
                P.op("pe", lambda e, kt=kt: e.transpose(out=pT[:, kt * 128:(kt + 1) * 128], in_=xnb[:, kt * 128:(kt + 1) * 128], identity=ident_b[:]),
                     reads=["xnb", "ident_b"], writes=["pT"], sig=(kt == 7))
            el("act", lambda e, sub=sub: e.copy(out=xnT[:, :, sub * 128:(sub + 1) * 128], in_=pT[:].rearrange("p (k t) -> p k t", k=8)),
               ["pT"], ["xnT"])

        chk(2)

        def proj_fm(blk, ncolt, evac):
            W, wn = load_block(blk)
            Wv_ = W[:].rearrange("p (k n) -> p k n", k=8)
            for ct in range(ncolt):
                ps_, pn = acc_bank()
                for kt in range(8):
                    mm(ps_[:, 0:T], Wv_[:, kt, ct * 128:(ct + 1) * 128], xnT[:, kt, :], kt == 0, kt == 7, [wn, "xnT"], [pn])
                evac(ct, ps_, pn)

        el("dve", lambda e: e.tensor_copy(out=uaT[:, :, 1:4], in_=uaT[:, :, T + 1:T + 4]), ["uaT"], ["uaT"])
        for half in range(2):
            proj_fm(BLK_UA + half, 4, lambda ct, ps_, pn, half=half: el(
                "act", lambda e: e.copy(out=uaT[:, half * 4 + ct, 4:T + 4], in_=ps_[:, 0:T]), [pn], ["uaT"]))
        chk(3)
        for ch in range(NCH):
            for kt in range(8):
                mm(pM[:, 0:8], xnT[:, kt, ch * 128:(ch + 1) * 128], Wif[:, kt, :], kt == 0, kt == 7, ["xnT", "Wif"], ["pM"])
            tt(gates[:, ch, :], pM[:, 0:8], bif[:], ALU.add, ["pM", "bif"], ["gates"])
        chk(4)
        W, wn = load_block(BLK_UB)
        Wv_ = W[:].rearrange("p (k n) -> p k n", k=8)
        for j in range(8):
            ps_, pn = acc_bank()
            for kt in range(8):
                lhs = xnT[:, kt, :].rearrange("p (c j) -> p j c", j=8)[:, j, :]
                mm(ps_[0:NC8, :], lhs, Wv_[:, kt, :], kt == 0, kt == 7, [wn, "xnT"], [pn])
            el("act" if j % 2 else "dve",
               (lambda e, ps_=ps_, j=j: e.copy(out=Utm[0:NC8, :, j, :], in_=ps_[0:NC8, :].rearrange("c (g n) -> c g n", g=32))) if j % 2 else
               (lambda e, ps_=ps_, j=j: e.tensor_copy(out=Utm[0:NC8, :, j, :], in_=ps_[0:NC8, :].rearrange("c (g n) -> c g n", g=32))), [pn], ["Utm"])
        chk(5)
        for g in range(32):
            P.op("pe", lambda e, g=g: e.transpose(out=pT[:, g * NC8:(g + 1) * NC8], in_=Utm[0:NC8, g, :, :].rearrange("c j n -> c (j n)"),
                                                  identity=ident_b[0:NC8, 0:NC8]), reads=["Utm", "ident_b"], writes=["pT"], sig=(g == 31))
        el("act", lambda e: e.copy(out=U2[:].rearrange("p g c -> p (g c)"), in_=pT[:, 0:32 * NC8]), ["pT"], ["U2"])
        W1r, w1rn = load_block(BLK_S5)
        W1i, w1in = load_block(BLK_S5 + 1)
        for ri, (Wt, wn_) in enumerate(((W1r, w1rn), (W1i, w1in))):
            Wg = Wt[:].rearrange("p (g n) -> p g n", g=32)
            for q in range(16):
                for g2 in range(2):
                    g = 2 * q + g2
                    mm(pM[:, (q * NC8):(q + 1) * NC8], Wg[:, g, :], U2[:, g, :], g2 == 0, g2 == 1, [wn_, "U2"], ["pM"], sig=(q == 15 and g2 == 1))
            el("act" if ri == 0 else "dve",
               (lambda e, ri=ri: e.copy(out=Xall[:, :, ri, :].rearrange("p c q -> p q c"), in_=pM[:, 0:16 * NC8].rearrange("p (q c) -> p q c", q=16))) if ri == 0 else
               (lambda e, ri=ri: e.tensor_copy(out=Xall[:, :, ri, :].rearrange("p c q -> p q c"), in_=pM[:, 0:16 * NC8].rearrange("p (q c) -> p q c", q=16))),
               ["pM"], ["Xall"])
        chk(6)
        T1 = sbt([128, 2, 16], F32, "scT1_%d" % ti) if False else None
        for c in range(NC8):
            sp_ = Sall[:, c, :, :]
            sn_ = Sall[:, c + 1, :, :]
            P.op("pool", lambda e, sp_=sp_: e.tensor_tensor(out=sc_t1[:], in0=sp_, in1=AR32[:], op=ALU.mult), reads=["Sall"], writes=["sc_t1"])
            P.op("pool", lambda e, c=c: e.tensor_tensor(out=sc_t2[:, 0, :], in0=Sall[:, c, 1, :], in1=ANI[:], op=ALU.mult), reads=["Sall"], writes=["sc_t2"])
            P.op("pool", lambda e, c=c: e.tensor_tensor(out=sc_t2[:, 1, :], in0=Sall[:, c, 0, :], in1=API[:], op=ALU.mult), reads=["Sall"], writes=["sc_t2"])
            P.op("pool", lambda e, c=c: e.tensor_tensor(out=sc_t1[:], in0=sc_t1[:], in1=Xall[:, c, :, :], op=ALU.add), reads=["sc_t1", "Xall"], writes=["sc_t1"])
            P.op("pool", lambda e, sn_=sn_: e.tensor_tensor(out=sn_, in0=sc_t1[:], in1=sc_t2[:], op=ALU.add), reads=["sc_t1", "sc_t2"], writes=["Sall"])
        if main:
            for ri in range(2):
                el("act" if ri else "dve",
                   (lambda e, ri=ri: e.copy(out=Sbf[:, ri, :, :], in_=Sall[:, 0:NC8, ri, :].rearrange("p c q -> p q c"))) if ri else
                   (lambda e, ri=ri: e.tensor_copy(out=Sbf[:, ri, :, :], in_=Sall[:, 0:NC8, ri, :].rearrange("p c q -> p q c"))),
                   ["Sall"], ["Sbf"])
        el("pool", lambda e: e.tensor_copy(out=Sall[:, 0, :, :], in_=Sall[:, NC8, :, :]), ["Sall"], ["Sall"])
        def s5_out():
            proj_fm(BLK_ZB, 4, lambda ct, ps_, pn: el("act", lambda e: e.activation(out=zbT[:, ct, :], in_=ps_[:, 0:T], func=AF.Silu), [pn], ["zbT"]))
            Wi_, win_ = load_block(BLK_S5 + 2)
            Wr_, wrn_ = load_block(BLK_S5 + 3)
            Wm_, wmn_ = load_block(BLK_S5 + 4)
            Wi_g = Wi_[:].rearrange("p (g n) -> p g n", g=32); Wr_g = Wr_[:].rearrange("p (g n) -> p g n", g=32)
            Wm_g = Wm_[:].rearrange("p (g n) -> p g n", g=32)
            GP = 512 // NC8
            for g0 in range(0, 32, GP):
                ng = min(GP, 32 - g0)
                for gi in range(ng):
                    g = g0 + gi
                    o_ = pM[:, gi * NC8:(gi + 1) * NC8]
                    mm(o_, Wi_g[:, g, :], U2[:, g, :], True, False, [win_, "U2"], ["pM"], sig=False)
                    mm(o_, Wr_g[:, g, :], Sbf[:, 0, g // 2, :], False, False, [wrn_, "Sbf"], ["pM"], sig=False)
                    mm(o_, Wm_g[:, g, :], Sbf[:, 1, g // 2, :], False, True, [wmn_, "Sbf"], ["pM"], sig=(gi == ng - 1))
                el("act", lambda e, g0=g0, ng=ng: e.activation(out=Yg[:, g0:g0 + ng, :].rearrange("p g c -> p (g c)"), in_=pM[:, 0:ng * NC8],
                                                               func=AF.Gelu_apprx_tanh), ["pM"], ["Yg"])
            for g0 in range(0, 32, 8):
                for gi in range(8):
                    g = g0 + gi
                    P.op("pe", lambda e, g=g, gi=gi: e.transpose(out=pT[0:NC8, gi * 128:(gi + 1) * 128], in_=Yg[:, g, :], identity=ident_b[:]),
                         reads=["Yg", "ident_b"], writes=["pT"], sig=(gi == 7))
                el("act", lambda e, g0=g0: e.copy(out=Ytm[0:NC8, :, 16 * g0:16 * g0 + 128].rearrange("c j (g n) -> c g j n", g=8),
                                                  in_=pT[0:NC8, :].rearrange("c (g j n) -> c g j n", g=8, j=8)), ["pT"], ["Utm"])
            for ct in range(4):
                for j in range(8):
                    P.op("pe", lambda e, ct=ct, j=j: e.transpose(out=pT[:, j * NC8:(j + 1) * NC8], in_=Ytm[0:NC8, j, ct * 128:(ct + 1) * 128],
                                                                 identity=ident_b[0:NC8, 0:NC8]), reads=["Utm", "ident_b"], writes=["pT"], sig=(j == 7))
                el("act", lambda e, ct=ct: e.copy(out=yT[:, ct, :], in_=pT[:, 0:T]), ["pT"], ["yT"])
            for ot_ in range(4):
                ps_, pn = acc_bank()
                for ct in range(4):
                    mm(ps_[:, 0:T], Wglu[:, ct, ot_ * 128:(ot_ + 1) * 128], yT[:, ct, :], ct == 0, ct == 3, ["Wglu", "yT"], [pn])
                el("act", lambda e, ps_=ps_, ot_=ot_: e.activation(out=sgl[:], in_=ps_[:, 0:T], func=AF.Sigmoid, bias=bglu[:, ot_:ot_ + 1]),
                   [pn, "bglu"], ["sgl"])
                tt(xg[:], sgl[:], yT[:, ot_, :], ALU.mult, ["sgl", "yT"], ["xg"])
                tt(hbT[:, ot_, :].rearrange("p (j c) -> p j c", j=8), xg[:].rearrange("p (j c) -> p j c", j=8),
                   zbT[:, ot_, :].rearrange("p (c j) -> p j c", j=8), ALU.mult, ["xg", "zbT"], ["hbT"])
        chk(7)
        for mt in range(8):
            ps_, pn = acc_bank()
            for k in range(4):
                mm(ps_[:, 0:T], cdiag[:, mt, k, :], uaT[:, mt, k + 1:k + 1 + T], k == 0, k == 3, ["cdiag", "uaT"], [pn])
            el("act", lambda e, ps_=ps_, mt=mt: e.activation(out=cT[:, mt, :], in_=ps_[:, 0:T], func=AF.Silu, bias=cb[:, mt:mt + 1]),
               [pn, "cb"], ["cT"])
        if main:
            for half in range(2):
                proj_fm(BLK_ZA + half, 4, lambda ct, ps_, pn, half=half: el(
                    "act", lambda e: e.activation(out=zaT[:, half * 4 + ct, :], in_=ps_[:, 0:T], func=AF.Silu), [pn], ["zaT"]))
            for half in range(2):
                proj_fm(BLK_OA + half, 4, lambda ct, ps_, pn, half=half: el(
                    "act", lambda e: e.activation(out=oaT[:, half * 4 + ct, :], in_=ps_[:, 0:T], func=AF.Sigmoid), [pn], ["oaT"]))
        chk(8)
        for h in range(4):
            if main:
                for (Wx, wxn, dst, dn) in ((Wq, "Wqkv0", qT, "qT"), (Wk, "Wqkv1", kT, "kT")):
                    for et in range(2):
                        ps_, pn = acc_bank()
                        for d in range(2):
                            mm(ps_[:, 0:T], Wx[:, h, d, et * 128:(et + 1) * 128], cT[:, 2 * h + d, :], d == 0, d == 1, [wxn, "cT"], [pn])
                        el("act" if et else "dve",
                           (lambda e, ps_=ps_, dst=dst, h=h, et=et: e.copy(out=dst[:, h, et, :], in_=ps_[:, 0:T])) if et else
                           (lambda e, ps_=ps_, dst=dst, h=h, et=et: e.tensor_copy(out=dst[:, h, et, :], in_=ps_[:, 0:T])), [pn], [dn])
        for ch in range(NCH):
            tsl = slice(ch * 128, (ch + 1) * 128)
            ts(ig[:], gates[:, ch, 0:4], 1.0, None, ALU.mult, None, ["gates"], ["ig"])
            el("act", lambda e, ch=ch: e.activation(out=e1[:], in_=gates[:, ch, 4:8], func=AF.Exp, scale=-1.0), ["gates"], ["e1"])
            el("act", lambda e: e.activation(out=lf[:], in_=e1[:], func=AF.Ln, bias=1.0), ["e1"], ["lf"])
            ts(lf[:], lf[:], -1.0, None, ALU.mult, None, ["lf"], ["lf"])
            for h in range(4):
                el("dve", lambda e, h=h: e.tensor_copy(out=lfrep[:, h, :], in_=lf[:, h:h + 1].to_broadcast([128, 128])), ["lf"], ["lfrep"])
            for h in range(4):
                mm(pGD[:, h * 128:(h + 1) * 128], lfrep[:, h, :], maskT[:], True, True, ["lfrep", "maskT"], ["pGD"], sig=(h == 3))
            for h in range(4):
                mm(pM[:, h * 128:(h + 1) * 128], maskT[:], lfrep[:, h, :], True, True, ["maskT", "lfrep"], ["pM"], sig=(h == 3))
            el("act", lambda e: e.activation(out=emb[:].rearrange("p h t -> p (h t)"), in_=pGD[:], func=AF.Exp, scale=-1.0), ["pGD"], ["emb"])
            el("dve", lambda e: e.reciprocal(out=dec[:], in_=emb[:, :, 127]), ["emb"], ["dec"])
            tt(bcol[:], ig[:], pM[:].rearrange("p (h t) -> p h t", h=4)[:, :, 0], ALU.subtract, ["ig", "pM"], ["bcol"])
            el("act", lambda e: e.activation(out=wcol[:], in_=bcol[:], func=AF.Exp), ["bcol"], ["wcol"])
            el("dve", lambda e: e.tensor_copy(out=wcolb[:], in_=wcol[:].unsqueeze(2).to_broadcast([128, 4, 2])), ["wcol"], ["wcolb"])
            chk(81)
            for h in range(4):
                ps_, pn = acc_bank()
                no_v = dbg_stop in (821, 822, 823, 824)
                no_k = dbg_stop in (824,)
                for d in range(2):
                    if no_k:
                        continue
                    mm(ps_[:, 0:256], cT[:, 2 * h + d, tsl], Wk[:, h, d, :], d == 0, d == 1, ["cT", "Wqkv1"], [pn], sig=(no_v and d == 1))
                for d in range(2):
                    if no_v:
                        continue
                    mm(ps_[:, 256:512], uaT[:, 2 * h + d, 4 + ch * 128:4 + (ch + 1) * 128], Wv[:, h, d, :], d == 0, d == 1, ["uaT", "Wqkv2"], [pn])
                if dbg_stop != 822:
                    el("act", lambda e, ps_=ps_, ch=ch, h=h: e.copy(out=ktm[:, ch, h, :], in_=ps_[:, 0:256]), [pn], ["ktm"])
                if dbg_stop not in (822, 823):
                    el("act", lambda e, ps_=ps_, ch=ch, h=h: e.activation(out=vw[:, ch, h, :], in_=ps_[:, 256:512], func=AF.Copy, scale=wcol[:, h:h + 1]),
                       [pn, "wcol"], ["vw"])
            if main:
                for h in range(4):
                    el("dve", lambda e, h=h: e.tensor_copy(out=wrep[:, h, :], in_=wcol[:, h:h + 1].to_broadcast([128, 128])), ["wcol"], ["wrep"])
                for h in range(4):
                    for et in range(2):
                        mm(pS[:, h * 128:(h + 1) * 128], kT[:, h, et, tsl], qT[:, h, et, tsl], et == 0, et == 1, ["kT", "qT"], ["pS"], sig=(h == 3 and et == 1))
                tt(SmT[:].rearrange("p h t -> p (h t)"), pS[:], mask4[:].rearrange("p h t -> p (h t)"), ALU.mult, ["pS", "mask4"], ["SmT"])
                for h in range(4):
                    for d2 in range(2):
                        pn_, pnn = (pN0, "pN0") if h < 2 else (pN1, "pN1")
                        o_ = pn_[:, ((h % 2) * 2 + d2) * 128:((h % 2) * 2 + d2 + 1) * 128]
                        mm(o_, vw[:, ch, h, d2 * 128:(d2 + 1) * 128], SmT[:, h, :], True, False, ["vw", "SmT"], [pnn], sig=False)
                        mm(o_, Cbf[:, h, 0, d2 * 128:(d2 + 1) * 128], qT[:, h, 0, tsl], False, False, ["Cbf", "qT"], [pnn], sig=False)
                        mm(o_, Cbf[:, h, 1, d2 * 128:(d2 + 1) * 128], qT[:, h, 1, tsl], False, True, ["Cbf", "qT"], [pnn], sig=(h % 2 == 1 and d2 == 1))
                    o_ = pGD[:, h * 128:(h + 1) * 128]
                    mm(o_, wrep[:, h, :], SmT[:, h, :], True, False, ["wrep", "SmT", "emb", "dec"], ["pGD"], sig=False)
                    mm(o_, nrep[:, h, 0, :], qT[:, h, 0, tsl], False, False, ["nrep", "qT"], ["pGD"], sig=False)
                    mm(o_, nrep[:, h, 1, :], qT[:, h, 1, tsl], False, True, ["nrep", "qT"], ["pGD"], sig=(h == 3))
                el("act", lambda e: e.activation(out=aden[:].rearrange("p h t -> p (h t)"), in_=pGD[:], func=AF.Abs), ["pGD"], ["aden"])
                tt(aden[:], aden[:], emb[:], ALU.max, ["aden", "emb"], ["aden"])
                el("dve", lambda e: e.reciprocal(out=rden[:], in_=aden[:]), ["aden"], ["rden"])
                for h in range(4):
                    pn_, pnn = (pN0, "pN0") if h < 2 else (pN1, "pN1")
                    for d2 in range(2):
                        o_ = pn_[:, ((h % 2) * 2 + d2) * 128:((h % 2) * 2 + d2 + 1) * 128]
                        tt(hT[:, 2 * h + d2, :], o_, rden[:, h, :], ALU.mult, [pnn, "rden"], ["hT"])
                tt(hT[:], hT[:], oaT[:, :, tsl], ALU.mult, ["hT", "oaT"], ["hT"])
                el("act", lambda e: e.activation(out=sq[:], in_=hT[:], func=AF.Square), ["hT"], ["sq"])
                for h in range(4):
                    for d2 in range(2):
                        mm(pS[:, h * 128:(h + 1) * 128], ones_b[:], sq[:, 2 * h + d2, :], d2 == 0, d2 == 1, ["ones_b", "sq", "SmT"], ["pS"], sig=(h == 3 and d2 == 1))
                ts(rsh[:].rearrange("p h t -> p (h t)"), pS[:], 1.0 / 256.0, 1e-6, ALU.mult, ALU.add, ["pS"], ["rsh"])
                el("act", lambda e: e.activation(out=rsh[:], in_=rsh[:], func=AF.Ln), ["rsh"], ["rsh"])
                el("act", lambda e: e.activation(out=rsh[:], in_=rsh[:], func=AF.Exp, scale=-0.5), ["rsh"], ["rsh"])
                for mt in range(8):
                    el("dve", lambda e, mt=mt: e.scalar_tensor_tensor(out=hn[:, mt, :], in0=hT[:, mt, :], scalar=hg[:, mt:mt + 1],
                                                                      in1=rsh[:, mt // 2, :], op0=ALU.mult, op1=ALU.mult), ["hT", "hg", "rsh"], ["hn"])
                    el("dve", lambda e, mt=mt, tsl=tsl: e.scalar_tensor_tensor(out=hn[:, mt, :], in0=cT[:, mt, tsl], scalar=skp[:, mt:mt + 1],
                                                                      in1=hn[:, mt, :], op0=ALU.mult, op1=ALU.add), ["cT", "skp", "hn"], ["hn"])
                tt(hfT[:, :, tsl], hn[:], zaT[:, :, tsl], ALU.mult, ["hn", "zaT"], ["hfT"])
            chk(82)
            chk(821)
            chk(822)
            chk(823)
            chk(824)
            for h in range(4):
                for et in range(2):
                    ps_, pn = acc_bank()
                    mm(ps_[:, 0:256], ktm[:, ch, h, et * 128:(et + 1) * 128], vw[:, ch, h, :], True, True, ["ktm", "vw"], [pn], sig=False)
                    mm(ps_[:, 256:258], ktm[:, ch, h, et * 128:(et + 1) * 128], wcolb[:, h, :], True, True, ["ktm", "wcolb"], [pn])
                    tt(Cst[:, h, et, :], Cst[:, h, et, :], ps_[:, 0:256], ALU.add, ["Cst", pn, "Cbf"], ["Cst"])
                    tt(nst[:, h, et:et + 1], nst[:, h, et:et + 1], ps_[:, 256:257], ALU.add, ["nst", pn, "nrep"], ["nst"])
                el("act", lambda e, h=h: e.activation(out=Cst[:, h, :, :], in_=Cst[:, h, :, :], func=AF.Copy, scale=dec[:, h:h + 1]), ["Cst", "dec"], ["Cst"])
                ts(nst[:, h, :], nst[:, h, :], dec[:, h:h + 1], None, ALU.mult, None, ["nst", "dec"], ["nst"])
                el("act", lambda e, h=h: e.copy(out=Cbf[:, h, :, :], in_=Cst[:, h, :, :]), ["Cst"], ["Cbf"])
                for et in range(2):
                    el("dve", lambda e, h=h, et=et: e.tensor_copy(out=nrep[:, h, et, :], in_=nst[:, h, et:et + 1].to_broadcast([128, 128])),
                       ["nst"], ["nrep"])
        chk(9)
        if not main:
            return
        s5_out()
        def gate_blocks(br):
            for bi_ in range(2):
                Wg_, wgn = load_block(BLK_G + 2 * br + bi_)
                Wgv = Wg_[:].rearrange("p (k n) -> p k n", k=8)
                for c4 in range(4):
                    ft = bi_ * 4 + c4
                    psg, png = acc_bank()
                    for kt in range(8):
                        mm(psg[:, 0:T], Wgv[:, kt, c4 * 128:(c4 + 1) * 128], xnT[:, kt, :], kt == 0, kt == 7, [wgn, "xnT"], [png])
                    el("act", lambda e, psg=psg, ft=ft: e.activation(out=gAll[:, ft, :], in_=psg[:, 0:T], func=AF.Sigmoid), [png], ["oaT"])

        gate_blocks(0)
        for hf_ in range(2):
            Wa, wan = load_block(BLK_AO + hf_)
            Wav = Wa[:].rearrange("p (k n) -> p k n", k=8)
            for c4 in range(4):
                ft = hf_ * 4 + c4
                psa, pna = acc_bank()
                for kt in range(8):
                    mm(psa[:, 0:T], Wav[:, kt, c4 * 128:(c4 + 1) * 128], hfT[:, kt, :], kt == 0, kt == 7, [wan, "hfT"], [pna])
                tt(mrgT[:, ft, :], psa[:, 0:T], gAll[:, ft, :], ALU.mult, [pna, "oaT"], ["zaT"])
        gate_blocks(1)
        Wbo, wbon = load_block(BLK_BO)
        Wbv = Wbo[:].rearrange("p (k n) -> p k n", k=4)
        for ft in range(8):
            psb, pnb = acc_bank()
            for kt in range(4):
                mm(psb[:, 0:T], Wbv[:, kt, ft * 128:(ft + 1) * 128], hbT[:, kt, :], kt == 0, kt == 3, [wbon, "hbT"], [pnb])
            el("act", lambda e, psb=psb: e.copy(out=m2[:].rearrange("p (c j) -> p j c", j=8), in_=psb[:, 0:T].rearrange("p (j c) -> p j c", j=8)),
               [pnb], ["m2"])
            tt(m2[:], m2[:], gAll[:, ft, :], ALU.mult, ["m2", "oaT"], ["m2"])
            tt(mrgT[:, ft, :], m2[:], mrgT[:, ft, :], ALU.add, ["m2", "zaT"], ["zaT"], eng="pool")
        Wo0, wo0n = load_block(BLK_WO); Wo1, wo1n = load_block(BLK_WO + 1)
        for ch in range(NCH):
            tsl = slice(ch * 128, (ch + 1) * 128)
            for hf_, (Wo_, won, pn_, pnn) in enumerate(((Wo0, wo0n, pN0, "pN0"), (Wo1, wo1n, pN1, "pN1"))):
                Wov = Wo_[:].rearrange("p (k n) -> p k n", k=8)
                for kt in range(8):
                    mm(pn_[:, :], mrgT[:, kt, tsl], Wov[:, kt, :], kt == 0, kt == 7, ["zaT", won], [pnn])
            el("act", lambda e: e.activation(out=junk[:, 0:512], in_=pN0[:], func=AF.Square, accum_out=ss2[:]), ["pN0"], ["junk", "ss2"])
            el("act", lambda e: e.activation(out=junk[:, 512:1024], in_=pN1[:], func=AF.Square, accum_out=rstd2[:]), ["pN1"], ["junk", "rstd2"])
            tt(ss2[:], ss2[:], rstd2[:], ALU.add, ["ss2", "rstd2"], ["ss2"])
            rsqrt_col(rstd2[:], ss2[:], 1.0 / 1024.0, "ss2", "rstd2")
            ob, obn = ot[ch % 2], "ot%d" % (ch % 2)
            xb_, xnm = xs[ch % 2], "xs%d" % (ch % 2)
            P.dma(lambda e, xb_=xb_, ch=ch: e.dma_start(out=xb_[:], in_=xsrc[row0 + ch * 128: row0 + (ch + 1) * 128, :]), writes=[xnm])
            for hf_, (pn_, pnn) in enumerate(((pN0, "pN0"), (pN1, "pN1"))):
                cs = slice(hf_ * 512, hf_ * 512 + 512)
                el("dve", lambda e, ob=ob, pn_=pn_, cs=cs: e.scalar_tensor_tensor(out=ob[:, cs], in0=pn_[:], scalar=rstd2[:, 0:1], in1=gpost[:, cs],
                                                                                 op0=ALU.mult, op1=ALU.mult), [pnn, "rstd2", "gpost"], [obn])
            tt(ob[:], ob[:], xb_[:], ALU.add, [obn, xnm], [obn], eng="pool")
            tok = P.dma(lambda e, ob=ob, ch=ch: e.dma_start(out=out[row0 + ch * 128: row0 + (ch + 1) * 128, :], in_=ob[:]), reads=[obn])
            out_toks.append(tok)

    sc_t1 = sbt([128, 2, 16], F32, "sc_t1"); sc_t2 = sbt([128, 2, 16], F32, "sc_t2")
    try:
        for ti in range(NPRE):
            do_tile(x_pre, ti * T, False, ti)
    except StopBuild:
        tk = P.dma(lambda e: e.dma_start(out=out[0:128, :], in_=gpost[:]), reads=["gpost"])
        P.final_wait("sp", [tk])
        P.emit(); P.close()
        return nc
    if NPRE > 0:
        ts(Cst[:].rearrange("p h e d -> p (h e d)"), Cst[:].rearrange("p h e d -> p (h e d)"), flg[:, 0:1], None, ALU.mult, None, ["Cst", "flg"], ["Cst"])
        ts(nst[:].rearrange("p h e -> p (h e)"), nst[:].rearrange("p h e -> p (h e)"), flg[:, 0:1], None, ALU.mult, None, ["nst", "flg"], ["nst"])
        el("act", lambda e: e.copy(out=Cbf[:].rearrange("p h e d -> p (h e d)"), in_=Cst[:].rearrange("p h e d -> p (h e d)")), ["Cst"], ["Cbf"])
        for h in range(4):
            for et in range(2):
                el("dve", lambda e, h=h, et=et: e.tensor_copy(out=nrep[:, h, et, :], in_=nst[:, h, et:et + 1].to_broadcast([128, 128])),
                   ["nst"], ["nrep"])
    for ti in range(NMAIN):
        do_tile(x_main, ti * T, True, NPRE + ti)
    P.final_wait("sp", out_toks)
    P.emit()
    P.close()
    return nc


T_TILE = 256
_cache = {}


def kernel(**inputs):
    x = np.ascontiguousarray(inputs["x"], dtype=np.float32)
    Bsz, L, Dm = x.shape
    half = L // 2
    npre = half // T_TILE
    nmain = half // T_TILE
    key = (T_TILE, npre, nmain)
    if key not in _cache:
        _cache[key] = build_program(T_TILE, npre, nmain)
    nc = _cache[key]
    shared = {}
    for k, v in inputs.items():
        if k == "x":
            continue
        a = np.ascontiguousarray(np.asarray(v, dtype=np.float32)[0])
        if k in ("b_i", "b_f", "log_dt", "norm_post_g"):
            a = a.reshape(1, -1)
        shared[k] = a
    in_maps = []
    zeros = np.zeros((half, Dm), np.float32)
    for core in range(8):
        b, hf = core // 2, core % 2
        m = dict(shared)
        m["x_main"] = np.ascontiguousarray(x[b, hf * half:(hf + 1) * half])
        m["x_pre"] = zeros if hf == 0 else np.ascontiguousarray(x[b, 0:half])
        m["flag"] = np.full((128, 1), float(hf), np.float32)
        in_maps.append(m)
    res = run_bass_kernel_spmd(nc, in_maps, core_ids=list(range(8)))
    outp = np.empty((Bsz, L, Dm), np.float32)
    for core in range(8):
        b, hf = core // 2, core % 2
        outp[b, hf * half:(hf + 1) * half] = res.results[core]["out"]
    return outp
```

```python
import contextlib
import numpy as np
import concourse.bass as bass
import concourse.mybir as mybir
from concourse.bass_utils import run_bass_kernel_spmd

F32 = mybir.dt.float32
BF16 = mybir.dt.bfloat16
AF = mybir.ActivationFunctionType
ALU = mybir.AluOpType
COMPUTE = ("pe", "act", "dve", "pool")
NDMA_SLOTS = 8
DEBUG_TAGS = False
PI = float(np.pi)


class Prog:
    def __init__(self, nc):
        self.nc = nc
        self.stack = contextlib.ExitStack()
        self.engs = ("pe", "act", "dve", "pool", "sp")
        self.ops = {e: [] for e in self.engs}
        self.waited = {e: {} for e in self.engs}
        self.res = {}
        self.dma_use = {}
        self.dma_rr = {e: 0 for e in self.engs}
        self.sems = {}
        self.base = {e: 0 for e in COMPUTE}
        self.temp = None

    def sb(self, name, shape, dt):
        st = self.temp if self.temp is not None else self.stack
        return st.enter_context(self.nc.sbuf_tensor(name, list(shape), dt))

    def ps(self, name, shape, dt):
        return self.stack.enter_context(self.nc.psum_tensor(name, list(shape), dt))

    def _sem(self, key):
        if key not in self.sems:
            nm = "s_" + "_".join(str(k) for k in (key if isinstance(key, tuple) else (key,)))
            self.sems[key] = self.stack.enter_context(self.nc.semaphore(nm))
        return self.sems[key]

    def _deps(self, eng, reads, writes):
        deps = {}

        def add(tok):
            if tok is None:
                return
            k, v = tok
            if k == "pe" and eng == "pe":
                return
            if deps.get(k, -1) < v:
                deps[k] = v

        for r in reads:
            st = self.res.get(r)
            if st:
                for k, v in st[0].items():
                    add((k, v))
        for w in writes:
            st = self.res.get(w)
            if st:
                for k, v in st[0].items():
                    add((k, v))
                for k, v in st[1].items():
                    add((k, v))
        out = []
        wd = self.waited[eng]
        for k, v in deps.items():
            if wd.get(k, -1) >= v:
                continue
            wd[k] = v
            out.append((k, v))
        return out

    def _commit(self, tok, reads, writes):
        k, v = tok
        for r in reads:
            st = self.res.setdefault(r, [{}, {}])
            if st[1].get(k, -1) < v:
                st[1][k] = v
        for w in writes:
            old = self.res.get(w)
            wr = {}
            if old is not None and k not in COMPUTE:
                wr = {k2: v2 for k2, v2 in old[0].items() if k2 not in COMPUTE}
            wr[k] = v
            self.res[w] = [wr, {}]

    def _tag(self):
        if not DEBUG_TAGS:
            return None
        import sys as _sys
        f = _sys._getframe(2)
        while f is not None and f.f_code.co_name not in ("do_tile", "build_program", "s5_out", "gate_blocks", "proj_fm"):
            f = f.f_back
        return str(f.f_lineno) if f is not None else None

    def op(self, eng, fn, reads=(), writes=(), sig=True):
        waits = self._deps(eng, reads, writes)
        idx = len(self.ops[eng])
        self.ops[eng].append(dict(fn=fn, waits=waits, sig=sig, dma=None, tag=self._tag()))
        tok = (eng, idx)
        self._commit(tok, reads, writes)
        return tok

    def dma(self, fn, reads=(), writes=(), q="sp"):
        waits = self._deps(q, reads, writes)
        slot = self.dma_rr[q] % NDMA_SLOTS
        self.dma_rr[q] += 1
        key = ("d", q, slot)
        n = self.dma_use.get(key, 0)
        if n > 0:
            prev = n * 16
            if self.waited[q].get(key, -1) < prev:
                self.waited[q][key] = prev
                waits.append((key, prev))
        self.dma_use[key] = n + 1
        tok = (key, (n + 1) * 16)
        self.ops[q].append(dict(fn=fn, waits=waits, sig=False, dma=key))
        self._commit(tok, reads, writes)
        return tok

    def final_wait(self, eng, toks):
        self.ops[eng].append(dict(fn=None, waits=list(toks), sig=False, dma=None))

    def emit(self):
        nc = self.nc
        sigcount = {}
        totals = {}
        for e in COMPUTE:
            c = self.base[e]
            arr = []
            for o in self.ops[e]:
                if o["sig"]:
                    c += 1
                arr.append(c)
            need = [None] * len(arr)
            nxt = None
            for i in range(len(arr) - 1, -1, -1):
                if self.ops[e][i]["sig"]:
                    nxt = arr[i]
                need[i] = nxt
            sigcount[e] = need
            totals[e] = c
            self._sem(e)
        for k in self.dma_use:
            self._sem(k)

        def resolve(k, v):
            if k in COMPUTE:
                val = sigcount[k][v]
                assert val is not None, (k, v)
                return self.sems[k], val
            return self.sems[k], v

        with nc.Block() as block:

            def run(eng_name, eng):
                for o in self.ops[eng_name]:
                    for k, v in o["waits"]:
                        s, val = resolve(k, v)
                        eng.wait_ge(s, val)
                    if o["fn"] is None:
                        continue
                    ins = o["fn"](eng)
                    if o.get("tag"):
                        ins.annotate(o["tag"])
                    if o["dma"] is not None:
                        ins.then_inc(self.sems[o["dma"]], 16)
                    elif o["sig"]:
                        ins.then_inc(self.sems[eng_name], 1)
                for o2 in COMPUTE:
                    if o2 != eng_name and totals[o2] > 0:
                        eng.wait_ge(self.sems[o2], totals[o2])
                for k, n in self.dma_use.items():
                    eng.wait_ge(self.sems[k], n * 16)

            @block.tensor
            def _(e):
                run("pe", e)

            @block.scalar
            def _(e):
                run("act", e)

            @block.vector
            def _(e):
                run("dve", e)

            @block.gpsimd
            def _(e):
                run("pool", e)

            @block.sync
            def _(e):
                run("sp", e)

        self.base = totals
        self.ops = {e: [] for e in self.engs}
        self.waited = {e: {} for e in self.engs}
        self.res = {}

    def close(self):
        self.stack.close()


NBLK = 22
BLK_UA, BLK_UB, BLK_ZB, BLK_ZA, BLK_OA, BLK_G, BLK_AO, BLK_BO, BLK_WO, BLK_S5 = 0, 2, 3, 4, 6, 8, 12, 14, 15, 17
COL_UA, COL_ZA, COL_OA, COL_I, COL_UB, COL_ZB, COL_G = 0, 1024, 2048, 3072, 3080, 3592, 4104


def build_program(T, NPRE, NMAIN, dbg_stop=0):
    NCH = T // 128
    NC8 = T // 8
    nc = bass.Bass("TRN2", target_bir_lowering=False)
    dram = {}

    def din(name, shape):
        dram[name] = nc.dram_tensor(name, list(shape), F32, kind="ExternalInput").ap()
        return dram[name]

    x_pre = din("x_pre", [max(NPRE, 1) * T, 1024])
    x_main = din("x_main", [NMAIN * T, 1024])
    flag = din("flag", [128, 1])
    norm_pre_g = din("norm_pre_g", [1024]); w_in = din("w_in", [1024, 6152])
    conv_w = din("conv_w", [4, 1024]); conv_b = din("conv_b", [1024])
    w_q = din("w_q", [4, 256, 256]); w_k = din("w_k", [4, 256, 256]); w_v = din("w_v", [4, 256, 256])
    b_i = din("b_i", [1, 4]); b_f = din("b_f", [1, 4]); head_g = din("head_g", [1024]); skip_a = din("skip_a", [1024])
    w_a_out = din("w_a_out", [1024, 1024])
    lam_re = din("lam_re", [32, 64]); lam_im = din("lam_im", [32, 64]); log_dt = din("log_dt", [1, 32])
    B_re = din("B_re", [32, 64, 16]); B_im = din("B_im", [32, 64, 16])
    C_re = din("C_re", [32, 16, 64]); C_im = din("C_im", [32, 16, 64]); D_skip = din("D_skip", [32, 16])
    w_glu = din("w_glu", [512, 512]); b_glu = din("b_glu", [512]); w_b_out = din("w_b_out", [512, 1024])
    w_o = din("w_o", [1024, 1024]); norm_post_g = din("norm_post_g", [1, 1024])
    out = nc.dram_tensor("out", [NMAIN * T, 1024], F32, kind="ExternalOutput").ap()
    WS = nc.dram_tensor("wscratch", [NBLK, 128, 4096], BF16, kind="Internal").ap()

    P = Prog(nc)
    uid = [0]

    def sbt(shape, dt, name=None):
        uid[0] += 1
        return P.sb(name or ("t%d" % uid[0]), shape, dt)

    ident_f = sbt([128, 128], F32); ident_b = sbt([128, 128], BF16)
    maskT = sbt([128, 128], F32); mask4 = sbt([128, 4, 128], F32); ones_b = sbt([128, 128], BF16)
    P.op("pool", lambda e: e.memset(ident_f[:], 1.0), writes=["ident_f"])
    P.op("pool", lambda e: e.affine_select(out=ident_f[:], in_=ident_f[:], pattern=[[-1, 128]], compare_op=ALU.is_equal,
                                           fill=0.0, base=0, channel_multiplier=1), reads=["ident_f"], writes=["ident_f"])
    P.op("dve", lambda e: e.tensor_copy(out=ident_b[:], in_=ident_f[:]), reads=["ident_f"], writes=["ident_b"])
    P.op("pool", lambda e: e.memset(maskT[:], 1.0), writes=["maskT"])
    P.op("pool", lambda e: e.affine_select(out=maskT[:], in_=maskT[:], pattern=[[1, 128]], compare_op=ALU.is_ge,
                                           fill=0.0, base=0, channel_multiplier=-1), reads=["maskT"], writes=["maskT"])
    for h in range(4):
        P.op("pool", lambda e, h=h: e.tensor_copy(out=mask4[:, h, :], in_=maskT[:]), reads=["maskT"], writes=["mask4"])
    P.op("pool", lambda e: e.memset(ones_b[:], 1.0), writes=["ones_b"])

    gpre = sbt([128, 8], F32); cb = sbt([128, 8], F32); hg = sbt([128, 8], F32); skp = sbt([128, 8], F32)
    cw = sbt([128, 8, 4], F32); bglu = sbt([128, 4], F32); gpost = sbt([128, 1024], F32); bif = sbt([128, 8], F32)
    flg = sbt([128, 1], F32)
    nonc = dict(allow_slow_non_contiguous=True)
    P.dma(lambda e: e.dma_start(out=gpre[:], in_=norm_pre_g.rearrange("(k p) -> p k", p=128), **nonc), writes=["gpre"])
    P.dma(lambda e: e.dma_start(out=cb[:], in_=conv_b.rearrange("(k p) -> p k", p=128), **nonc), writes=["cb"])
    P.dma(lambda e: e.dma_start(out=hg[:], in_=head_g.rearrange("(k p) -> p k", p=128), **nonc), writes=["hg"])
    P.dma(lambda e: e.dma_start(out=skp[:], in_=skip_a.rearrange("(k p) -> p k", p=128), **nonc), writes=["skp"])
    for k in range(4):
        P.dma(lambda e, k=k: e.dma_start(out=cw[:, :, k], in_=conv_w[k].rearrange("(m p) -> p m", p=128), **nonc), writes=["cw"])
    P.dma(lambda e: e.dma_start(out=bglu[:], in_=b_glu.rearrange("(k p) -> p k", p=128), **nonc), writes=["bglu"])
    P.dma(lambda e: e.dma_start(out=gpost[:], in_=norm_post_g.partition_broadcast(128)), writes=["gpost"])
    P.dma(lambda e: e.dma_start(out=bif[:, 0:4], in_=b_i.partition_broadcast(128)), writes=["bif"])
    P.dma(lambda e: e.dma_start(out=bif[:, 4:8], in_=b_f.partition_broadcast(128)), writes=["bif"])
    P.dma(lambda e: e.dma_start(out=flg[:], in_=flag), writes=["flg"])

    Wqkv = [sbt([128, 4, 2, 256], BF16) for _ in range(3)]
    Wif = sbt([128, 8, 8], BF16); Wglu = sbt([128, 4, 512], BF16); cdiag = sbt([128, 8, 4, 128], BF16)
    AR32 = sbt([128, 2, 16], F32); ANI = sbt([128, 16], F32); API = sbt([128, 16], F32)
    pA = P.ps("pA", [128, 512], F32); pB = P.ps("pB", [128, 512], F32)
    pT = P.ps("pT", [128, 1024], BF16); pGD = P.ps("pGD", [128, 512], F32)
    pS = P.ps("pS", [128, 512], F32); pN0 = P.ps("pN0", [128, 512], F32); pN1 = P.ps("pN1", [128, 512], F32)
    pM = P.ps("pM", [128, 512], F32)
    P.temp = contextlib.ExitStack()
    stg = sbt([128, 4096], F32, "stg")
    stgb = sbt([128, 4096], BF16, "stgb")
    for wi, (wsrc, scl) in enumerate(((w_q, 1.0), (w_k, 1.0 / 16.0), (w_v, 1.0))):
        P.dma(lambda e, wsrc=wsrc: e.dma_start(out=stg[:, 0:2048].rearrange("p (h d n) -> p h d n", h=4, d=2),
                                               in_=wsrc.rearrange("h (d p) n -> p h d n", p=128)), writes=["stg"])
        P.op("dve", lambda e, wi=wi, scl=scl: e.tensor_scalar(out=Wqkv[wi][:].rearrange("p h d n -> p (h d n)"), in0=stg[:, 0:2048],
                                                              scalar1=scl, scalar2=None, op0=ALU.mult), reads=["stg"], writes=["Wqkv%d" % wi])
    Wq, Wk, Wv = Wqkv
    P.dma(lambda e: e.dma_start(out=stg[:, 0:64].rearrange("p (k n) -> p k n", k=8),
                                in_=w_in[:, COL_I:COL_I + 8].rearrange("(k p) n -> p k n", p=128), **nonc), writes=["stg"])
    for kt in range(8):
        P.op("dve", lambda e, kt=kt: e.tensor_scalar(out=Wif[:, kt, :], in0=stg[:, kt * 8:(kt + 1) * 8], scalar1=gpre[:, kt:kt + 1],
                                                     scalar2=None, op0=ALU.mult), reads=["stg", "gpre"], writes=["Wif"])
    P.dma(lambda e: e.dma_start(out=stg[:, 0:2048].rearrange("p (k n) -> p k n", k=4),
                                in_=w_glu.rearrange("(k p) n -> p k n", p=128)), writes=["stg"])
    P.op("dve", lambda e: e.tensor_copy(out=Wglu[:].rearrange("p k n -> p (k n)"), in_=stg[:, 0:2048]), reads=["stg"], writes=["Wglu"])
    for mt in range(8):
        for k in range(4):
            P.op("dve", lambda e, mt=mt, k=k: e.tensor_scalar(out=cdiag[:, mt, k, :], in0=ident_f[:], scalar1=cw[:, mt, k:k + 1],
                                                               scalar2=None, op0=ALU.mult), reads=["ident_f", "cw"], writes=["cdiag"])

    def stage_block(blk, src_ap_f, scale_gpre, nk):
        ncol = 4096 // nk
        P.dma(lambda e: e.dma_start(out=stg[:].rearrange("p (k n) -> p k n", k=nk), in_=src_ap_f), writes=["stg"])
        if scale_gpre:
            for kt in range(nk):
                P.op("dve",
                     lambda e, kt=kt: e.tensor_scalar(out=stgb[:, kt * ncol:(kt + 1) * ncol], in0=stg[:, kt * ncol:(kt + 1) * ncol],
                                                      scalar1=gpre[:, kt:kt + 1], scalar2=None, op0=ALU.mult),
                     reads=["stg", "gpre"], writes=["stgb"])
        else:
            P.op("dve", lambda e: e.tensor_copy(out=stgb[:, 0:2048], in_=stg[:, 0:2048]), reads=["stg"], writes=["stgb"])
            P.op("act", lambda e: e.copy(out=stgb[:, 2048:4096], in_=stg[:, 2048:4096]), reads=["stg"], writes=["stgb"])
        P.dma(lambda e: e.dma_start(out=WS[blk], in_=stgb[:]), reads=["stgb"], writes=["WS%d" % blk])

    def win_cols(c0):
        return w_in[:, c0:c0 + 512].rearrange("(k p) n -> p k n", p=128)

    win_blocks = [(BLK_UA, COL_UA), (BLK_UA + 1, COL_UA + 512), (BLK_UB, COL_UB), (BLK_ZB, COL_ZB), (BLK_ZA, COL_ZA),
                  (BLK_ZA + 1, COL_ZA + 512), (BLK_OA, COL_OA), (BLK_OA + 1, COL_OA + 512)] + [(BLK_G + i, COL_G + 512 * i) for i in range(4)]
    pending = []
    for blk, c0 in win_blocks:
        pending.append((blk, win_cols(c0), True, 8))
    for i in range(2):
        pending.append((BLK_AO + i, w_a_out[:, 512 * i:512 * i + 512].rearrange("(k p) n -> p k n", p=128), False, 8))
        pending.append((BLK_WO + i, w_o[:, 512 * i:512 * i + 512].rearrange("(k p) n -> p k n", p=128), False, 8))
    pending.append((BLK_BO, w_b_out.rearrange("(k p) n -> p k n", p=128), False, 4))

    def stage_some(n=1):
        for _ in range(n):
            if pending:
                stage_block(*pending.pop(0))

    stage_some(2)

    def small(n, name=None):
        return sbt([128, n], F32, name)

    cnt = [0]

    def el(eng, fn, r, w):
        P.op(eng, fn, reads=r, writes=w)

    def tt(outp, a, b, op, r, w, eng="dve"):
        el(eng, lambda e: e.tensor_tensor(out=outp, in0=a, in1=b, op=op), r, w)

    def ts(outp, a, s1, s2, op0, op1, r, w, eng="dve"):
        if op1 is None:
            el(eng, lambda e: e.tensor_scalar(out=outp, in0=a, scalar1=s1, scalar2=None, op0=op0), r, w)
        else:
            el(eng, lambda e: e.tensor_scalar(out=outp, in0=a, scalar1=s1, scalar2=s2, op0=op0, op1=op1), r, w)

    LR = small(32); LI = small(32); DT = small(32)
    for hf in range(2):
        sl = slice(64 * hf, 64 * hf + 64)
        P.dma(lambda e, sl=sl: e.dma_start(out=LR[sl, :], in_=lam_re.rearrange("g p -> p g"), **nonc), writes=["LR"])
        P.dma(lambda e, sl=sl: e.dma_start(out=LI[sl, :], in_=lam_im.rearrange("g p -> p g"), **nonc), writes=["LI"])
    P.dma(lambda e: e.dma_start(out=DT[:], in_=log_dt.partition_broadcast(128)), writes=["DT"])
    el("act", lambda e: e.activation(out=DT[:], in_=DT[:], func=AF.Exp), ["DT"], ["DT"])
    TH = small(32); MAG = small(32); t0 = small(32); t1 = small(32); t2 = small(32); kk = small(32)
    tt(TH[:], LI[:], DT[:], ALU.mult, ["LI", "DT"], ["TH"])
    tt(t0[:], LR[:], DT[:], ALU.mult, ["LR", "DT"], ["t0"])
    el("act", lambda e: e.activation(out=MAG[:], in_=t0[:], func=AF.Exp), ["t0"], ["MAG"])
    IMAG2 = small(32)
    el("act", lambda e: e.activation(out=IMAG2[:], in_=t0[:], func=AF.Exp, scale=-2.0), ["t0"], ["IMAG2"])

    def sin_of(dst, src, shift, nm):
        ts(t1[:], src, shift, None, ALU.add, None, [nm, "t1"], ["t1"])
        el("pool", lambda e: e.memset(kk[:], 0.0), [], ["kk"])
        for m in range(7):
            ts(t2[:], t1[:], (2 * m + 1) * PI, None, ALU.is_gt, None, ["t1"], ["t2"])
            tt(kk[:], kk[:], t2[:], ALU.add, ["kk", "t2"], ["kk"])
        ts(kk[:], kk[:], -2.0 * PI, None, ALU.mult, None, ["kk"], ["kk"])
        tt(t1[:], t1[:], kk[:], ALU.add, ["t1", "kk"], ["t1"])
        el("act", lambda e: e.activation(out=dst, in_=t1[:], func=AF.Sin), ["t1"], [nm + "_s"])

    SN = small(32); CS = small(32)
    sin_of(SN[:], TH[:], 0.0, "TH")
    sin_of(CS[:], TH[:], PI / 2.0, "TH")
    pwr = sbt([128, 9, 32], F32); pwi = sbt([128, 9, 32], F32); pnr = sbt([128, 8, 32], F32); pni = sbt([128, 8, 32], F32)
    el("pool", lambda e: e.memset(pwr[:, 0, :], 1.0), [], ["pw"]); el("pool", lambda e: e.memset(pwi[:, 0, :], 0.0), [], ["pw"])
    el("pool", lambda e: e.memset(pnr[:, 0, :], 1.0), [], ["pn"]); el("pool", lambda e: e.memset(pni[:, 0, :], 0.0), [], ["pn"])
    tt(pwr[:, 1, :], MAG[:], CS[:], ALU.mult, ["MAG", "TH_s"], ["pw"])
    tt(pwi[:, 1, :], MAG[:], SN[:], ALU.mult, ["MAG", "TH_s"], ["pw"])
    tt(pnr[:, 1, :], pwr[:, 1, :], IMAG2[:], ALU.mult, ["pw", "IMAG2"], ["pn"])
    tt(t0[:], pwi[:, 1, :], IMAG2[:], ALU.mult, ["pw", "IMAG2"], ["t0"])
    ts(pni[:, 1, :], t0[:], -1.0, None, ALU.mult, None, ["t0"], ["pn"])

    def cmul(or_, oi_, ar, ai, br, bi, r, w):
        raise NotImplementedError

    u0 = small(32); u1 = small(32)
    for k in range(1, 8):
        for (xr, xi, nm, lim) in ((pwr, pwi, "pw", 9), (pnr, pni, "pn", 8)):
            if k + 1 >= lim:
                continue
            tt(u0[:], xr[:, k, :], xr[:, 1, :], ALU.mult, [nm], ["u0"])
            tt(u1[:], xi[:, k, :], xi[:, 1, :], ALU.mult, [nm], ["u1"])
            tt(xr[:, k + 1, :], u0[:], u1[:], ALU.subtract, ["u0", "u1"], [nm])
            tt(u0[:], xr[:, k, :], xi[:, 1, :], ALU.mult, [nm], ["u0"])
            tt(u1[:], xi[:, k, :], xr[:, 1, :], ALU.mult, [nm], ["u1"])
            tt(xi[:, k + 1, :], u0[:], u1[:], ALU.add, ["u0", "u1"], [nm])
    den = small(32); qr = small(32); qi = small(32); nr = small(32)
    tt(u0[:], LR[:], LR[:], ALU.mult, ["LR"], ["u0"]); tt(u1[:], LI[:], LI[:], ALU.mult, ["LI"], ["u1"])
    tt(den[:], u0[:], u1[:], ALU.add, ["u0", "u1"], ["den"])
    el("dve", lambda e: e.reciprocal(out=den[:], in_=den[:]), ["den"], ["den"])
    ts(nr[:], pwr[:, 1, :], -1.0, None, ALU.add, None, ["pw"], ["nr"])
    tt(u0[:], nr[:], LR[:], ALU.mult, ["nr", "LR"], ["u0"]); tt(u1[:], pwi[:, 1, :], LI[:], ALU.mult, ["pw", "LI"], ["u1"])
    tt(qr[:], u0[:], u1[:], ALU.add, ["u0", "u1"], ["qr"]); tt(qr[:], qr[:], den[:], ALU.mult, ["qr", "den"], ["qr"])
    tt(u0[:], pwi[:, 1, :], LR[:], ALU.mult, ["pw", "LR"], ["u0"]); tt(u1[:], nr[:], LI[:], ALU.mult, ["nr", "LI"], ["u1"])
    tt(qi[:], u0[:], u1[:], ALU.subtract, ["u0", "u1"], ["qi"]); tt(qi[:], qi[:], den[:], ALU.mult, ["qi", "den"], ["qi"])
    Br = sbt([128, 32, 16], F32); Bi = sbt([128, 32, 16], F32); bbr = sbt([128, 32, 16], F32); bbi = sbt([128, 32, 16], F32)
    v0 = sbt([128, 32, 16], F32); v1 = sbt([128, 32, 16], F32)
    for hf in range(2):
        sl = slice(64 * hf, 64 * hf + 64)
        P.dma(lambda e, sl=sl: e.dma_start(out=Br[sl], in_=B_re.rearrange("g p n -> p g n"), **nonc), writes=["Br"])
        P.dma(lambda e, sl=sl: e.dma_start(out=Bi[sl], in_=B_im.rearrange("g p n -> p g n"), **nonc), writes=["Bi"])

    def bc(s):
        return s.unsqueeze(2).to_broadcast([128, 32, 16])

    def cmul3(orr, oii, sr, si, sn, xr, xi, xn, on):
        xn = [xn] if isinstance(xn, str) else list(xn)
        sn = [sn] if isinstance(sn, str) else list(sn)
        stage_some(1)
        tt(v0[:], xr, bc(sr), ALU.mult, xn + sn, ["v0"]); tt(v1[:], xi, bc(si), ALU.mult, xn + sn, ["v1"])
        tt(orr, v0[:], v1[:], ALU.subtract, ["v0", "v1"], [on])
        tt(v0[:], xi, bc(sr), ALU.mult, xn + sn, ["v0"]); tt(v1[:], xr, bc(si), ALU.mult, xn + sn, ["v1"])
        tt(oii, v0[:], v1[:], ALU.add, ["v0", "v1"], [on])

    el("dve", lambda e: e.tensor_copy(out=u0[:], in_=qr[:]), ["qr"], ["qq"])
    cmul3(bbr[:], bbi[:], qr[:], qi[:], ["qr", "qi"], Br[:], Bi[:], ["Br", "Bi"], "bb")
    CTr = sbt([128, 32, 16], F32); CTi = sbt([128, 32, 16], F32)
    Cdup = sbt([128, 4, 2, 64], F32)
    for (Csrc, CTt, nm) in ((C_re, CTr, "CTr"), (C_im, CTi, "CTi")):
        for d in range(2):
            P.dma(lambda e, Csrc=Csrc, d=d: e.dma_start(out=Cdup[:, :, d, :], in_=Csrc.rearrange("(t g) n p -> (g n) t p", t=4)),
                  writes=["Cdup"])
        for t in range(4):
            P.op("pe", lambda e, t=t: e.transpose(out=pA[:, t * 128:(t + 1) * 128], in_=Cdup[:, t, :, :].rearrange("q d p -> q (d p)"),
                                                  identity=ident_f[:]), reads=["Cdup", "ident_f"], writes=["pA"])
        el("act", lambda e, CTt=CTt: e.copy(out=CTt[:].rearrange("p g n -> p (g n)"), in_=pA[:]), ["pA"], [nm])
    Er = sbt([128, 32, 8, 16], F32); Ei = sbt([128, 32, 8, 16], F32)
    Fr = sbt([128, 32, 8, 16], F32); Fi = sbt([128, 32, 8, 16], F32)
    for j in range(8):
        cmul3(Er[:, :, j, :], Ei[:, :, j, :], pwr[:, 7 - j, :], pwi[:, 7 - j, :], "pw", bbr[:], bbi[:], "bb", "E")
        cmul3(Fr[:, :, j, :], Fi[:, :, j, :], pwr[:, j + 1, :], pwi[:, j + 1, :], "pw", CTr[:], CTi[:], ["CTr", "CTi"], "F")
    halfm = sbt([128, 2], F32)
    el("pool", lambda e: e.memset(halfm[:], 0.0), [], ["halfm"])
    el("pool", lambda e: e.memset(halfm[0:64, 0:1], 1.0), ["halfm"], ["halfm"])
    el("pool", lambda e: e.memset(halfm[64:128, 1:2], 1.0), ["halfm"], ["halfm"])
    for ri, (Et, blk) in enumerate(((Er, BLK_S5), (Ei, BLK_S5 + 1))):
        el("pool", lambda e: e.memset(stgb[:], 0.0), ["stgb"], ["stgb"])
        for g in range(32):
            ps_ = pA if g % 2 == 0 else pB
            nm = "pA" if g % 2 == 0 else "pB"
            P.op("pe", lambda e, Et=Et, g=g, ps_=ps_: e.transpose(out=ps_[:, 0:64], in_=Et[0:64, g, :, :].rearrange("p j n -> p (j n)"),
                                                                  identity=ident_f[0:64, 0:64]), reads=["E", "ident_f"], writes=[nm])
            c0 = g * 128 + 64 * (g % 2)
            el("act" if g % 2 == 0 else "dve",
               (lambda e, ps_=ps_, c0=c0: e.copy(out=stgb[:, c0:c0 + 64], in_=ps_[:, 0:64])) if g % 2 == 0 else
               (lambda e, ps_=ps_, c0=c0: e.tensor_copy(out=stgb[:, c0:c0 + 64], in_=ps_[:, 0:64])), [nm], ["stgb"])
        P.dma(lambda e, blk=blk: e.dma_start(out=WS[blk], in_=stgb[:]), reads=["stgb"], writes=["WS%d" % blk])
    for ri, (Ft, blk, sg) in enumerate(((Fr, BLK_S5 + 3, 1.0), (Fi, BLK_S5 + 4, -1.0))):
        for g in range(32):
            P.op("dve",
                 lambda e, Ft=Ft, g=g, sg=sg: e.tensor_scalar(out=stgb[:, g * 128:(g + 1) * 128], in0=Ft[:, g, :, :].rearrange("p j n -> p (j n)"),
                                                              scalar1=halfm[:, (g % 2):(g % 2) + 1], scalar2=sg, op0=ALU.mult, op1=ALU.mult),
                 reads=["F", "halfm"], writes=["stgb"])
        P.dma(lambda e, blk=blk: e.dma_start(out=WS[blk], in_=stgb[:]), reads=["stgb"], writes=["WS%d" % blk])
    Gr = sbt([128, 32, 8, 16], F32); Gni = sbt([128, 32, 8, 16], F32)
    i8r = small(32); i8i = small(32)
    tt(u0[:], pnr[:, 7, :], pnr[:, 1, :], ALU.mult, ["pn"], ["u0"]); tt(u1[:], pni[:, 7, :], pni[:, 1, :], ALU.mult, ["pn"], ["u1"])
    tt(i8r[:], u0[:], u1[:], ALU.subtract, ["u0", "u1"], ["i8"])
    tt(u0[:], pnr[:, 7, :], pni[:, 1, :], ALU.mult, ["pn"], ["u0"]); tt(u1[:], pni[:, 7, :], pnr[:, 1, :], ALU.mult, ["pn"], ["u1"])
    tt(i8i[:], u0[:], u1[:], ALU.add, ["u0", "u1"], ["i8"])
    for j in range(8):
        cmul3(Gr[:, :, j, :], Gni[:, :, j, :], i8r[:], i8i[:], "i8", Fr[:, :, j, :], Fi[:, :, j, :], "F", "G")
    ts(Gni[:], Gni[:], -1.0, None, ALU.mult, None, ["G"], ["G"])
    bmask = sbt([128, 8, 16], F32); Dcol = sbt([128, 32], F32)
    el("pool", lambda e: e.memset(bmask[:], 1.0), [], ["bmask"])
    el("pool", lambda e: e.affine_select(out=bmask[:], in_=bmask[:], pattern=[[16, 8], [0, 16]], compare_op=ALU.is_ge, fill=0.0,
                                         base=15, channel_multiplier=-1), ["bmask"], ["bmask"])
    for j in range(8):
        P.dma(lambda e, j=j: e.dma_start(out=Dcol[16 * j:16 * j + 16, :], in_=D_skip.rearrange("g n -> n g"), **nonc), writes=["Dcol"])
    wtmp = sbt([128, 128], F32)
    for g in range(32):
        ps_ = pA if g % 2 == 0 else pB
        nm = "pA" if g % 2 == 0 else "pB"
        P.op("pe", lambda e, g=g, ps_=ps_: e.matmul(ps_[:, 0:128], lhsT=Er[0:64, g, :, :].rearrange("p j n -> p (j n)"),
                                                    rhs=Gr[0:64, g, :, :].rearrange("p j n -> p (j n)"), start=True, stop=False),
             reads=["E", "G"], writes=[nm], sig=False)
        P.op("pe", lambda e, g=g, ps_=ps_: e.matmul(ps_[:, 0:128], lhsT=Ei[0:64, g, :, :].rearrange("p j n -> p (j n)"),
                                                    rhs=Gni[0:64, g, :, :].rearrange("p j n -> p (j n)"), start=False, stop=True),
             reads=["E", "G"], writes=[nm])
        tt(wtmp[:], ps_[:, 0:128], bmask[:].rearrange("p j n -> p (j n)"), ALU.mult, [nm, "bmask"], ["wtmp"])
        el("dve", lambda e, g=g: e.scalar_tensor_tensor(out=stgb[:, g * 128:(g + 1) * 128], in0=ident_f[:], scalar=Dcol[:, g:g + 1],
                                                        in1=wtmp[:], op0=ALU.mult, op1=ALU.add), ["ident_f", "Dcol", "wtmp", "stgb"], ["stgb"])
    P.dma(lambda e: e.dma_start(out=WS[BLK_S5 + 2], in_=stgb[:]), reads=["stgb"], writes=["WS%d" % (BLK_S5 + 2)])
    for g2 in range(2):
        sl = slice(64 * g2, 64 * g2 + 64)
        src_r = pwr[sl, 8, :].rearrange("p (q t) -> p q t", t=2)[:, :, g2]
        src_i = pwi[sl, 8, :].rearrange("p (q t) -> p q t", t=2)[:, :, g2]
        el("dve", lambda e, sl=sl, src_r=src_r: e.tensor_copy(out=AR32[sl, 0, :], in_=src_r), ["pw"], ["AR32"])
        el("dve", lambda e, sl=sl, src_r=src_r: e.tensor_copy(out=AR32[sl, 1, :], in_=src_r), ["pw"], ["AR32"])
        el("dve", lambda e, sl=sl, src_i=src_i: e.tensor_copy(out=API[sl, :], in_=src_i), ["pw"], ["API"])
        ts(ANI[sl, :], src_i, -1.0, None, ALU.mult, None, ["pw"], ["ANI"])

    stage_some(100)
    if dbg_stop == 1:
        tk = P.dma(lambda e: e.dma_start(out=out[0:128, :], in_=gpost[:]), reads=["gpost"])
        P.final_wait("sp", [tk])
        P.emit(); P.temp.close(); P.close()
        return nc
    P.emit()
    P.temp.close()
    P.temp = None
    NSLOT = 3
    wslot = [sbt([128, 4096], BF16, "wslot%d" % i) for i in range(NSLOT)]
    slot_rr = [0]

    def load_block(blk):
        s = slot_rr[0] % NSLOT
        slot_rr[0] += 1
        P.dma(lambda e: e.dma_start(out=wslot[s][:], in_=WS[blk]), reads=["WS%d" % blk], writes=["wslot%d" % s])
        return wslot[s], "wslot%d" % s

    xs = [sbt([128, 1024], F32, "xs%d" % i) for i in range(2)]
    junk = sbt([128, 1024], BF16); xnb = sbt([128, 1024], BF16)
    ss = small(1); rstd = small(1)
    xnT = sbt([128, 8, T], BF16, "xnT"); uaT = sbt([128, 8, T + 4], BF16, "uaT"); cT = sbt([128, 8, T], BF16, "cT")
    zaT = sbt([128, 8, T], BF16, "zaT"); oaT = sbt([128, 8, T], BF16, "oaT"); zbT = sbt([128, 4, T], BF16, "zbT")
    qT = sbt([128, 4, 2, T], BF16, "qT"); kT = sbt([128, 4, 2, T], BF16, "kT")
    ktm = sbt([128, NCH, 4, 256], BF16, "ktm"); vw = sbt([128, NCH, 4, 256], BF16, "vw")
    gates = sbt([128, NCH, 8], F32, "gates")
    hfT = sbt([128, 8, T], BF16, "hfT"); mrgT = zaT; hbT = sbt([128, 4, T], BF16, "hbT")
    Cst = sbt([128, 4, 2, 256], F32, "Cst"); Cbf = sbt([128, 4, 2, 256], BF16, "Cbf")
    nst = sbt([128, 4, 2], F32, "nst"); nrep = sbt([128, 4, 2, 128], BF16, "nrep")
    Utm = sbt([128, 32, 8, 16], BF16, "Utm"); U2 = sbt([128, 32, NC8], BF16, "U2")
    Xall = sbt([128, NC8, 2, 16], F32, "Xall"); Sall = sbt([128, NC8 + 1, 2, 16], F32, "Sall"); Sbf = sbt([128, 2, 16, NC8], BF16, "Sbf")
    Yg = sbt([128, 32, NC8], BF16, "Yg"); Ytm = Utm[:].rearrange("p g j n -> p (g j n)").rearrange("p (j c) -> p j c", j=8); yT = sbt([128, 4, T], BF16, "yT")
    for (t_, nm) in ((Cst, "Cst"), (Cbf, "Cbf"), (nst, "nst"), (nrep, "nrep"), (Sall, "Sall"), (uaT, "uaT")):
        flat = t_[:]
        el("pool", lambda e, flat=flat: e.memset(flat, 0.0), [], [nm])

    e1a = sbt([128, NCH, 4], F32); lfa = sbt([128, NCH, 4], F32); emb4 = sbt([128, NCH, 4, 128], F32); dec4 = sbt([128, NCH, 4], F32)
    wcol4 = sbt([128, NCH, 4], F32); wcolb4 = sbt([128, NCH, 4, 2], BF16); wrep4 = sbt([128, NCH, 4, 128], BF16)
    lfrep = sbt([128, 4, 128], F32); lf = small(4, "lf"); ig = small(4); bcol = small(4); wcol = small(4, "wcol"); wcolb = sbt([128, 4, 2], BF16)
    wrep = sbt([128, 4, 128], BF16); emb = sbt([128, 4, 128], F32, "emb"); dec = small(4, "dec"); SmT = sbt([128, 4, 128], BF16, "SmT")
    aden = sbt([128, 4, 128], F32); rden = sbt([128, 4, 128], F32, "rden"); hT = sbt([128, 8, 128], F32, "hT"); sq = sbt([128, 8, 128], BF16)
    rsh = sbt([128, 4, 128], F32); hn = sbt([128, 8, 128], F32, "hn"); e1 = small(4)
    gAll = oaT; m1 = sbt([128, T], F32); m2 = sbt([128, T], F32)
    ot = [sbt([128, 1024], F32, "ot%d" % i) for i in range(2)]
    ss2 = small(1); rstd2 = small(1); sgl = sbt([128, T], BF16); xg = sbt([128, T], F32)

    def rsqrt_col(dst, src, scale, nm_src, nm_dst):
        ts(dst, src, scale, 1e-6, ALU.mult, ALU.add, [nm_src], [nm_dst])
        el("act", lambda e: e.activation(out=dst, in_=dst, func=AF.Ln), [nm_dst], [nm_dst])
        el("act", lambda e: e.activation(out=dst, in_=dst, func=AF.Exp, scale=-0.5), [nm_dst], [nm_dst])

    def mm(outp, lhsT, rhs, start, stop, r, w, sig=None):
        P.op("pe", lambda e: e.matmul(outp, lhsT=lhsT, rhs=rhs, start=start, stop=stop), reads=r, writes=w,
             sig=(stop if sig is None else sig))

    acc_rr = [0]

    def acc_bank():
        acc_rr[0] += 1
        return (pA, "pA") if acc_rr[0] % 2 else (pB, "pB")

    out_toks = []

    class StopBuild(Exception):
        pass

    def chk(level):
        if dbg_stop == level:
            raise StopBuild()

    def do_tile(xsrc, row0, main, ti):
        for sub in range(NCH):
            xb_, xnm = xs[sub % 2], "xs%d" % (sub % 2)
            P.dma(lambda e, xb_=xb_, sub=sub: e.dma_start(out=xb_[:], in_=xsrc[row0 + sub * 128: row0 + (sub + 1) * 128, :]), writes=[xnm])
            el("act", lambda e, xb_=xb_: e.activation(out=junk[:], in_=xb_[:], func=AF.Square, accum_out=ss[:]), [xnm], ["junk", "ss"])
            rsqrt_col(rstd[:], ss[:], 1.0 / 1024.0, "ss", "rstd")
            ts(xnb[:], xb_[:], rstd[:, 0:1], None, ALU.mult, None, [xnm, "rstd"], ["xnb"])
            for kt in range(8):
                P.op("pe", lambda e, kt=kt: e.transpose(out=pT[:, kt * 128:(kt + 1) * 128], in_=xnb[:, kt * 128:(kt + 1) * 128], identity=ident_b[:]),
                     reads=["xnb", "ident_b"], writes=["pT"], sig=(kt == 7))
            el("act", lambda e, sub=sub: e.copy(out=xnT[:, :, sub * 128:(sub + 1) * 128], in_=pT[:].rearrange("p (k t) -> p k t", k=8)),
               ["pT"], ["xnT"])

        chk(2)

        def proj_fm(blk, ncolt, evac):
            W, wn = load_block(blk)
            Wv_ = W[:].rearrange("p (k n) -> p k n", k=8)
            for ct in range(ncolt):
                ps_, pn = acc_bank()
                for kt in range(8):
                    mm(ps_[:, 0:T], Wv_[:, kt, ct * 128:(ct + 1) * 128], xnT[:, kt, :], kt == 0, kt == 7, [wn, "xnT"], [pn])
                evac(ct, ps_, pn)

        el("dve", lambda e: e.tensor_copy(out=uaT[:, :, 1:4], in_=uaT[:, :, T + 1:T + 4]), ["uaT"], ["uaT"])
        for half in range(2):
            proj_fm(BLK_UA + half, 4, lambda ct, ps_, pn, half=half: el(
                "act", lambda e: e.copy(out=uaT[:, half * 4 + ct, 4:T + 4], in_=ps_[:, 0:T]), [pn], ["uaT"]))
        chk(3)
        for ch in range(NCH):
            for kt in range(8):
                mm(pM[:, 0:8], xnT[:, kt, ch * 128:(ch + 1) * 128], Wif[:, kt, :], kt == 0, kt == 7, ["xnT", "Wif"], ["pM"])
            tt(gates[:, ch, :], pM[:, 0:8], bif[:], ALU.add, ["pM", "bif"], ["gates"])
        chk(4)
        W, wn = load_block(BLK_UB)
        Wv_ = W[:].rearrange("p (k n) -> p k n", k=8)
        for j in range(8):
            ps_, pn = acc_bank()
            for kt in range(8):
                lhs = xnT[:, kt, :].rearrange("p (c j) -> p j c", j=8)[:, j, :]
                mm(ps_[0:NC8, :], lhs, Wv_[:, kt, :], kt == 0, kt == 7, [wn, "xnT"], [pn])
            el("act" if j % 2 else "dve",
               (lambda e, ps_=ps_, j=j: e.copy(out=Utm[0:NC8, :, j, :], in_=ps_[0:NC8, :].rearrange("c (g n) -> c g n", g=32))) if j % 2 else
               (lambda e, ps_=ps_, j=j: e.tensor_copy(out=Utm[0:NC8, :, j, :], in_=ps_[0:NC8, :].rearrange("c (g n) -> c g n", g=32))), [pn], ["Utm"])
        chk(5)
        for g in range(32):
            P.op("pe", lambda e, g=g: e.transpose(out=pT[:, g * NC8:(g + 1) * NC8], in_=Utm[0:NC8, g, :, :].rearrange("c j n -> c (j n)"),
                                                  identity=ident_b[0:NC8, 0:NC8]), reads=["Utm", "ident_b"], writes=["pT"], sig=(g == 31))
        el("act", lambda e: e.copy(out=U2[:].rearrange("p g c -> p (g c)"), in_=pT[:, 0:32 * NC8]), ["pT"], ["U2"])
        W1r, w1rn = load_block(BLK_S5)
        W1i, w1in = load_block(BLK_S5 + 1)
        for ri, (Wt, wn_) in enumerate(((W1r, w1rn), (W1i, w1in))):
            Wg = Wt[:].rearrange("p (g n) -> p g n", g=32)
            for q in range(16):
                for g2 in range(2):
                    g = 2 * q + g2
                    mm(pM[:, (q * NC8):(q + 1) * NC8], Wg[:, g, :], U2[:, g, :], g2 == 0, g2 == 1, [wn_, "U2"], ["pM"], sig=(q == 15 and g2 == 1))
            el("act" if ri == 0 else "dve",
               (lambda e, ri=ri: e.copy(out=Xall[:, :, ri, :].rearrange("p c q -> p q c"), in_=pM[:, 0:16 * NC8].rearrange("p (q c) -> p q c", q=16))) if ri == 0 else
               (lambda e, ri=ri: e.tensor_copy(out=Xall[:, :, ri, :].rearrange("p c q -> p q c"), in_=pM[:, 0:16 * NC8].rearrange("p (q c) -> p q c", q=16))),
               ["pM"], ["Xall"])
        chk(6)
        T1 = sbt([128, 2, 16], F32, "scT1_%d" % ti) if False else None
        for c in range(NC8):
            sp_ = Sall[:, c, :, :]
            sn_ = Sall[:, c + 1, :, :]
            P.op("pool", lambda e, sp_=sp_: e.tensor_tensor(out=sc_t1[:], in0=sp_, in1=AR32[:], op=ALU.mult), reads=["Sall"], writes=["sc_t1"])
            P.op("pool", lambda e, c=c: e.tensor_tensor(out=sc_t2[:, 0, :], in0=Sall[:, c, 1, :], in1=ANI[:], op=ALU.mult), reads=["Sall"], writes=["sc_t2"])
            P.op("pool", lambda e, c=c: e.tensor_tensor(out=sc_t2[:, 1, :], in0=Sall[:, c, 0, :], in1=API[:], op=ALU.mult), reads=["Sall"], writes=["sc_t2"])
            P.op("pool", lambda e, c=c: e.tensor_tensor(out=sc_t1[:], in0=sc_t1[:], in1=Xall[:, c, :, :], op=ALU.add), reads=["sc_t1", "Xall"], writes=["sc_t1"])
            P.op("pool", lambda e, sn_=sn_: e.tensor_tensor(out=sn_, in0=sc_t1[:], in1=sc_t2[:], op=ALU.add), reads=["sc_t1", "sc_t2"], writes=["Sall"])
        if main:
            for ri in range(2):
                el("pool", lambda e, ri=ri: e.tensor_copy(out=Sbf[:, ri, :, :], in_=Sall[:, 0:NC8, ri, :].rearrange("p c q -> p q c")),
                   ["Sall"], ["Sbf"])
        el("pool", lambda e: e.tensor_copy(out=Sall[:, 0, :, :], in_=Sall[:, NC8, :, :]), ["Sall"], ["Sall"])
        def s5_out():
            proj_fm(BLK_ZB, 4, lambda ct, ps_, pn: el("act", lambda e: e.activation(out=zbT[:, ct, :], in_=ps_[:, 0:T], func=AF.Silu), [pn], ["zbT"]))
            Wi_, win_ = load_block(BLK_S5 + 2)
            Wr_, wrn_ = load_block(BLK_S5 + 3)
            Wm_, wmn_ = load_block(BLK_S5 + 4)
            Wi_g = Wi_[:].rearrange("p (g n) -> p g n", g=32); Wr_g = Wr_[:].rearrange("p (g n) -> p g n", g=32)
            Wm_g = Wm_[:].rearrange("p (g n) -> p g n", g=32)
            GP = 512 // NC8
            for g0 in range(0, 32, GP):
                ng = min(GP, 32 - g0)
                for gi in range(ng):
                    g = g0 + gi
                    o_ = pM[:, gi * NC8:(gi + 1) * NC8]
                    mm(o_, Wi_g[:, g, :], U2[:, g, :], True, False, [win_, "U2"], ["pM"], sig=False)
                    mm(o_, Wr_g[:, g, :], Sbf[:, 0, g // 2, :], False, False, [wrn_, "Sbf"], ["pM"], sig=False)
                    mm(o_, Wm_g[:, g, :], Sbf[:, 1, g // 2, :], False, True, [wmn_, "Sbf"], ["pM"], sig=(gi == ng - 1))
                el("act", lambda e, g0=g0, ng=ng: e.activation(out=Yg[:, g0:g0 + ng, :].rearrange("p g c -> p (g c)"), in_=pM[:, 0:ng * NC8],
                                                               func=AF.Gelu_apprx_tanh), ["pM"], ["Yg"])
            for g0 in range(0, 32, 8):
                for gi in range(8):
                    g = g0 + gi
                    P.op("pe", lambda e, g=g, gi=gi: e.transpose(out=pT[0:NC8, gi * 128:(gi + 1) * 128], in_=Yg[:, g, :], identity=ident_b[:]),
                         reads=["Yg", "ident_b"], writes=["pT"], sig=(gi == 7))
                el("act", lambda e, g0=g0: e.copy(out=Ytm[0:NC8, :, 16 * g0:16 * g0 + 128].rearrange("c j (g n) -> c g j n", g=8),
                                                  in_=pT[0:NC8, :].rearrange("c (g j n) -> c g j n", g=8, j=8)), ["pT"], ["Utm"])
            for ct in range(4):
                for j in range(8):
                    P.op("pe", lambda e, ct=ct, j=j: e.transpose(out=pT[:, j * NC8:(j + 1) * NC8], in_=Ytm[0:NC8, j, ct * 128:(ct + 1) * 128],
                                                                 identity=ident_b[0:NC8, 0:NC8]), reads=["Utm", "ident_b"], writes=["pT"], sig=(j == 7))
                el("act", lambda e, ct=ct: e.copy(out=yT[:, ct, :], in_=pT[:, 0:T]), ["pT"], ["yT"])
            for ot_ in range(4):
                ps_, pn = acc_bank()
                for ct in range(4):
                    mm(ps_[:, 0:T], Wglu[:, ct, ot_ * 128:(ot_ + 1) * 128], yT[:, ct, :], ct == 0, ct == 3, ["Wglu", "yT"], [pn])
                el("act", lambda e, ps_=ps_, ot_=ot_: e.activation(out=sgl[:], in_=ps_[:, 0:T], func=AF.Sigmoid, bias=bglu[:, ot_:ot_ + 1]),
                   [pn, "bglu"], ["sgl"])
                tt(xg[:], sgl[:], yT[:, ot_, :], ALU.mult, ["sgl", "yT"], ["xg"])
                tt(hbT[:, ot_, :].rearrange("p (j c) -> p j c", j=8), xg[:].rearrange("p (j c) -> p j c", j=8),
                   zbT[:, ot_, :].rearrange("p (c j) -> p j c", j=8), ALU.mult, ["xg", "zbT"], ["hbT"])
        chk(7)
        for mt in range(8):
            ps_, pn = acc_bank()
            for k in range(4):
                mm(ps_[:, 0:T], cdiag[:, mt, k, :], uaT[:, mt, k + 1:k + 1 + T], k == 0, k == 3, ["cdiag", "uaT"], [pn])
            el("act", lambda e, ps_=ps_, mt=mt: e.activation(out=cT[:, mt, :], in_=ps_[:, 0:T], func=AF.Silu, bias=cb[:, mt:mt + 1]),
               [pn, "cb"], ["cT"])
        if main:
            for half in range(2):
                proj_fm(BLK_ZA + half, 4, lambda ct, ps_, pn, half=half: el(
                    "act", lambda e: e.activation(out=zaT[:, half * 4 + ct, :], in_=ps_[:, 0:T], func=AF.Silu), [pn], ["zaT"]))
            for half in range(2):
                proj_fm(BLK_OA + half, 4, lambda ct, ps_, pn, half=half: el(
                    "act", lambda e: e.activation(out=oaT[:, half * 4 + ct, :], in_=ps_[:, 0:T], func=AF.Sigmoid), [pn], ["oaT"]))
        chk(8)
        for h in range(4):
            if main:
                for (Wx, wxn, dst, dn) in ((Wq, "Wqkv0", qT, "qT"), (Wk, "Wqkv1", kT, "kT")):
                    for et in range(2):
                        ps_, pn = acc_bank()
                        for d in range(2):
                            mm(ps_[:, 0:T], Wx[:, h, d, et * 128:(et + 1) * 128], cT[:, 2 * h + d, :], d == 0, d == 1, [wxn, "cT"], [pn])
                        el("act" if et else "dve",
                           (lambda e, ps_=ps_, dst=dst, h=h, et=et: e.copy(out=dst[:, h, et, :], in_=ps_[:, 0:T])) if et else
                           (lambda e, ps_=ps_, dst=dst, h=h, et=et: e.tensor_copy(out=dst[:, h, et, :], in_=ps_[:, 0:T])), [pn], [dn])
        el("act", lambda e: e.activation(out=e1a[:], in_=gates[:, :, 4:8], func=AF.Exp, scale=-1.0), ["gates"], ["e1a"])
        el("act", lambda e: e.activation(out=lfa[:], in_=e1a[:], func=AF.Ln, bias=1.0), ["e1a"], ["lfa"])
        ts(lfa[:], lfa[:], -1.0, None, ALU.mult, None, ["lfa"], ["lfa"])
        for ch in range(NCH):
            tsl = slice(ch * 128, (ch + 1) * 128)
            for h in range(4):
                el("dve", lambda e, h=h, ch=ch: e.tensor_copy(out=lfrep[:, h, :], in_=lfa[:, ch, h:h + 1].to_broadcast([128, 128])), ["lfa"], ["lfrep"])
            for h in range(4):
                mm(pGD[:, h * 128:(h + 1) * 128], lfrep[:, h, :], maskT[:], True, True, ["lfrep", "maskT"], ["pGD"], sig=(h == 3))
            for h in range(4):
                mm(pM[:, h * 128:(h + 1) * 128], maskT[:], lfrep[:, h, :], True, True, ["maskT", "lfrep"], ["pM"], sig=(h == 3))
            en, dn, wn_, wbn, wrn = "emb%d" % ch, "dec%d" % ch, "wcol%d" % ch, "wcolb%d" % ch, "wrep%d" % ch
            el("act", lambda e, ch=ch: e.activation(out=emb4[:, ch, :, :].rearrange("p h t -> p (h t)"), in_=pGD[:], func=AF.Exp, scale=-1.0), ["pGD"], [en])
            el("dve", lambda e, ch=ch: e.reciprocal(out=dec4[:, ch, :], in_=emb4[:, ch, :, 127]), [en], [dn])
            tt(bcol[:], gates[:, ch, 0:4], pM[:].rearrange("p (h t) -> p h t", h=4)[:, :, 0], ALU.subtract, ["gates", "pM"], ["bcol"])
            el("act", lambda e, ch=ch: e.activation(out=wcol4[:, ch, :], in_=bcol[:], func=AF.Exp), ["bcol"], [wn_])
            el("dve", lambda e, ch=ch: e.tensor_copy(out=wcolb4[:, ch, :, :], in_=wcol4[:, ch, :].unsqueeze(2).to_broadcast([128, 4, 2])), [wn_], [wbn])
            if main:
                for h in range(4):
                    el("dve", lambda e, h=h, ch=ch: e.tensor_copy(out=wrep4[:, ch, h, :], in_=wcol4[:, ch, h:h + 1].to_broadcast([128, 128])), [wn_], [wrn])
            for h in range(4):
                ps_, pn = acc_bank()
                for d in range(2):
                    mm(ps_[:, 0:256], cT[:, 2 * h + d, tsl], Wk[:, h, d, :], d == 0, d == 1, ["cT", "Wqkv1"], [pn], sig=False)
                for d in range(2):
                    mm(ps_[:, 256:512], uaT[:, 2 * h + d, 4 + ch * 128:4 + (ch + 1) * 128], Wv[:, h, d, :], d == 0, d == 1, ["uaT", "Wqkv2"], [pn])
                el("act", lambda e, ps_=ps_, ch=ch, h=h: e.copy(out=ktm[:, ch, h, :], in_=ps_[:, 0:256]), [pn], ["ktm"])
                el("act", lambda e, ps_=ps_, ch=ch, h=h: e.activation(out=vw[:, ch, h, :], in_=ps_[:, 256:512], func=AF.Copy, scale=wcol4[:, ch, h:h + 1]),
                   [pn, wn_], ["vw"])
        for ch in range(NCH):
            tsl = slice(ch * 128, (ch + 1) * 128)
            en, dn, wn_, wbn, wrn = "emb%d" % ch, "dec%d" % ch, "wcol%d" % ch, "wcolb%d" % ch, "wrep%d" % ch
            if main:
                for h in range(4):
                    for et in range(2):
                        mm(pS[:, h * 128:(h + 1) * 128], kT[:, h, et, tsl], qT[:, h, et, tsl], et == 0, et == 1, ["kT", "qT"], ["pS"], sig=(h == 3 and et == 1))
                tt(SmT[:].rearrange("p h t -> p (h t)"), pS[:], mask4[:].rearrange("p h t -> p (h t)"), ALU.mult, ["pS", "mask4"], ["SmT"])
                for h in range(4):
                    for d2 in range(2):
                        pn_, pnn = (pN0, "pN0") if h < 2 else (pN1, "pN1")
                        o_ = pn_[:, ((h % 2) * 2 + d2) * 128:((h % 2) * 2 + d2 + 1) * 128]
                        mm(o_, vw[:, ch, h, d2 * 128:(d2 + 1) * 128], SmT[:, h, :], True, False, ["vw", "SmT"], [pnn], sig=False)
                        mm(o_, Cbf[:, h, 0, d2 * 128:(d2 + 1) * 128], qT[:, h, 0, tsl], False, False, ["Cbf", "qT"], [pnn], sig=False)
                        mm(o_, Cbf[:, h, 1, d2 * 128:(d2 + 1) * 128], qT[:, h, 1, tsl], False, True, ["Cbf", "qT"], [pnn], sig=(h % 2 == 1 and d2 == 1))
                    o_ = pGD[:, h * 128:(h + 1) * 128]
                    mm(o_, wrep4[:, ch, h, :], SmT[:, h, :], True, False, [wrn, "SmT"], ["pGD"], sig=False)
                    mm(o_, nrep[:, h, 0, :], qT[:, h, 0, tsl], False, False, ["nrep", "qT"], ["pGD"], sig=False)
                    mm(o_, nrep[:, h, 1, :], qT[:, h, 1, tsl], False, True, ["nrep", "qT"], ["pGD"], sig=(h == 3))
            for h in range(4):
                for et in range(2):
                    ps_, pn = acc_bank()
                    mm(ps_[:, 0:256], ktm[:, ch, h, et * 128:(et + 1) * 128], vw[:, ch, h, :], True, True, ["ktm", "vw"], [pn], sig=False)
                    mm(ps_[:, 256:258], ktm[:, ch, h, et * 128:(et + 1) * 128], wcolb4[:, ch, h, :], True, True, ["ktm", wbn], [pn])
                    tt(Cst[:, h, et, :], Cst[:, h, et, :], ps_[:, 0:256], ALU.add, ["Cst", pn, "Cbf"], ["Cst"])
                    tt(nst[:, h, et:et + 1], nst[:, h, et:et + 1], ps_[:, 256:257], ALU.add, ["nst", pn, "nrep"], ["nst"])
                el("act", lambda e, h=h, ch=ch: e.activation(out=Cst[:, h, :, :], in_=Cst[:, h, :, :], func=AF.Copy, scale=dec4[:, ch, h:h + 1]), ["Cst", dn], ["Cst"])
                ts(nst[:, h, :], nst[:, h, :], dec4[:, ch, h:h + 1], None, ALU.mult, None, ["nst", dn], ["nst"])
                el("act", lambda e, h=h: e.copy(out=Cbf[:, h, :, :], in_=Cst[:, h, :, :]), ["Cst"], ["Cbf"])
                for et in range(2):
                    el("dve", lambda e, h=h, et=et: e.tensor_copy(out=nrep[:, h, et, :], in_=nst[:, h, et:et + 1].to_broadcast([128, 128])),
                       ["nst"], ["nrep"])
            if main:
                el("act", lambda e: e.activation(out=aden[:].rearrange("p h t -> p (h t)"), in_=pGD[:], func=AF.Abs), ["pGD"], ["aden"])
                tt(aden[:], aden[:], emb4[:, ch, :, :], ALU.max, ["aden", en], ["aden"])
                el("dve", lambda e: e.reciprocal(out=rden[:], in_=aden[:]), ["aden"], ["rden"])
                for h in range(4):
                    pn_, pnn = (pN0, "pN0") if h < 2 else (pN1, "pN1")
                    for d2 in range(2):
                        o_ = pn_[:, ((h % 2) * 2 + d2) * 128:((h % 2) * 2 + d2 + 1) * 128]
                        tt(hT[:, 2 * h + d2, :], o_, rden[:, h, :], ALU.mult, [pnn, "rden"], ["hT"])
                tt(hT[:], hT[:], oaT[:, :, tsl], ALU.mult, ["hT", "oaT"], ["hT"])
                el("act", lambda e: e.activation(out=sq[:], in_=hT[:], func=AF.Square), ["hT"], ["sq"])
                for h in range(4):
                    for d2 in range(2):
                        mm(pS[:, h * 128:(h + 1) * 128], ones_b[:], sq[:, 2 * h + d2, :], d2 == 0, d2 == 1, ["ones_b", "sq", "SmT"], ["pS"], sig=(h == 3 and d2 == 1))
                ts(rsh[:].rearrange("p h t -> p (h t)"), pS[:], 1.0 / 256.0, 1e-6, ALU.mult, ALU.add, ["pS"], ["rsh"])
                el("act", lambda e: e.activation(out=rsh[:], in_=rsh[:], func=AF.Ln), ["rsh"], ["rsh"])
                el("act", lambda e: e.activation(out=rsh[:], in_=rsh[:], func=AF.Exp, scale=-0.5), ["rsh"], ["rsh"])
                for mt in range(8):
                    el("dve", lambda e, mt=mt: e.scalar_tensor_tensor(out=hn[:, mt, :], in0=hT[:, mt, :], scalar=hg[:, mt:mt + 1],
                                                                      in1=rsh[:, mt // 2, :], op0=ALU.mult, op1=ALU.mult), ["hT", "hg", "rsh"], ["hn"])
                    el("dve", lambda e, mt=mt, tsl=tsl: e.scalar_tensor_tensor(out=hn[:, mt, :], in0=cT[:, mt, tsl], scalar=skp[:, mt:mt + 1],
                                                                               in1=hn[:, mt, :], op0=ALU.mult, op1=ALU.add), ["cT", "skp", "hn"], ["hn"])
                tt(hfT[:, :, tsl], hn[:], zaT[:, :, tsl], ALU.mult, ["hn", "zaT"], ["hfT"])
        chk(9)
        if not main:
            return
        s5_out()
        def gate_blocks(br):
            for bi_ in range(2):
                Wg_, wgn = load_block(BLK_G + 2 * br + bi_)
                Wgv = Wg_[:].rearrange("p (k n) -> p k n", k=8)
                for c4 in range(4):
                    ft = bi_ * 4 + c4
                    psg, png = acc_bank()
                    for kt in range(8):
                        mm(psg[:, 0:T], Wgv[:, kt, c4 * 128:(c4 + 1) * 128], xnT[:, kt, :], kt == 0, kt == 7, [wgn, "xnT"], [png])
                    el("act", lambda e, psg=psg, ft=ft: e.activation(out=gAll[:, ft, :], in_=psg[:, 0:T], func=AF.Sigmoid), [png], ["oaT"])

        gate_blocks(0)
        for hf_ in range(2):
            Wa, wan = load_block(BLK_AO + hf_)
            Wav = Wa[:].rearrange("p (k n) -> p k n", k=8)
            for c4 in range(4):
                ft = hf_ * 4 + c4
                psa, pna = acc_bank()
                for kt in range(8):
                    mm(psa[:, 0:T], Wav[:, kt, c4 * 128:(c4 + 1) * 128], hfT[:, kt, :], kt == 0, kt == 7, [wan, "hfT"], [pna])
                tt(mrgT[:, ft, :], psa[:, 0:T], gAll[:, ft, :], ALU.mult, [pna, "oaT"], ["zaT"])
        gate_blocks(1)
        Wbo, wbon = load_block(BLK_BO)
        Wbv = Wbo[:].rearrange("p (k n) -> p k n", k=4)
        for ft in range(8):
            psb, pnb = acc_bank()
            for kt in range(4):
                mm(psb[:, 0:T], Wbv[:, kt, ft * 128:(ft + 1) * 128], hbT[:, kt, :], kt == 0, kt == 3, [wbon, "hbT"], [pnb])
            el("act", lambda e, psb=psb: e.copy(out=m2[:].rearrange("p (c j) -> p j c", j=8), in_=psb[:, 0:T].rearrange("p (j c) -> p j c", j=8)),
               [pnb], ["m2"])
            tt(m2[:], m2[:], gAll[:, ft, :], ALU.mult, ["m2", "oaT"], ["m2"])
            tt(mrgT[:, ft, :], m2[:], mrgT[:, ft, :], ALU.add, ["m2", "zaT"], ["zaT"], eng="pool")
        Wo0, wo0n = load_block(BLK_WO); Wo1, wo1n = load_block(BLK_WO + 1)
        for ch in range(NCH):
            tsl = slice(ch * 128, (ch + 1) * 128)
            for hf_, (Wo_, won, pn_, pnn) in enumerate(((Wo0, wo0n, pN0, "pN0"), (Wo1, wo1n, pN1, "pN1"))):
                Wov = Wo_[:].rearrange("p (k n) -> p k n", k=8)
                for kt in range(8):
                    mm(pn_[:, :], mrgT[:, kt, tsl], Wov[:, kt, :], kt == 0, kt == 7, ["zaT", won], [pnn])
            el("act", lambda e: e.activation(out=junk[:, 0:512], in_=pN0[:], func=AF.Square, accum_out=ss2[:]), ["pN0"], ["junk", "ss2"])
            el("act", lambda e: e.activation(out=junk[:, 512:1024], in_=pN1[:], func=AF.Square, accum_out=rstd2[:]), ["pN1"], ["junk", "rstd2"])
            tt(ss2[:], ss2[:], rstd2[:], ALU.add, ["ss2", "rstd2"], ["ss2"])
            rsqrt_col(rstd2[:], ss2[:], 1.0 / 1024.0, "ss2", "rstd2")
            ob, obn = ot[ch % 2], "ot%d" % (ch % 2)
            xb_, xnm = xs[ch % 2], "xs%d" % (ch % 2)
            P.dma(lambda e, xb_=xb_, ch=ch: e.dma_start(out=xb_[:], in_=xsrc[row0 + ch * 128: row0 + (ch + 1) * 128, :]), writes=[xnm])
            for hf_, (pn_, pnn) in enumerate(((pN0, "pN0"), (pN1, "pN1"))):
                cs = slice(hf_ * 512, hf_ * 512 + 512)
                el("dve", lambda e, ob=ob, pn_=pn_, cs=cs: e.scalar_tensor_tensor(out=ob[:, cs], in0=pn_[:], scalar=rstd2[:, 0:1], in1=gpost[:, cs],
                                                                                 op0=ALU.mult, op1=ALU.mult), [pnn, "rstd2", "gpost"], [obn])
            tt(ob[:], ob[:], xb_[:], ALU.add, [obn, xnm], [obn], eng="pool")
            tok = P.dma(lambda e, ob=ob, ch=ch: e.dma_start(out=out[row0 + ch * 128: row0 + (ch + 1) * 128, :], in_=ob[:]), reads=[obn])
            out_toks.append(tok)

    sc_t1 = sbt([128, 2, 16], F32, "sc_t1"); sc_t2 = sbt([128, 2, 16], F32, "sc_t2")
    try:
        for ti in range(NPRE):
            do_tile(x_pre, ti * T, False, ti)
    except StopBuild:
        tk = P.dma(lambda e: e.dma_start(out=out[0:128, :], in_=gpost[:]), reads=["gpost"])
        P.final_wait("sp", [tk])
        P.emit(); P.close()
        return nc
    if NPRE > 0:
        ts(Cst[:].rearrange("p h e d -> p (h e d)"), Cst[:].rearrange("p h e d -> p (h e d)"), flg[:, 0:1], None, ALU.mult, None, ["Cst", "flg"], ["Cst"])
        ts(nst[:].rearrange("p h e -> p (h e)"), nst[:].rearrange("p h e -> p (h e)"), flg[:, 0:1], None, ALU.mult, None, ["nst", "flg"], ["nst"])
        el("act", lambda e: e.copy(out=Cbf[:].rearrange("p h e d -> p (h e d)"), in_=Cst[:].rearrange("p h e d -> p (h e d)")), ["Cst"], ["Cbf"])
        for h in range(4):
            for et in range(2):
                el("dve", lambda e, h=h, et=et: e.tensor_copy(out=nrep[:, h, et, :], in_=nst[:, h, et:et + 1].to_broadcast([128, 128])),
                   ["nst"], ["nrep"])
    for ti in range(NMAIN):
        do_tile(x_main, ti * T, True, NPRE + ti)
    P.final_wait("sp", out_toks)
    P.emit()
    P.close()
    return nc


T_TILE = 256
_cache = {}


def kernel(**inputs):
    x = np.ascontiguousarray(inputs["x"], dtype=np.float32)
    Bsz, L, Dm = x.shape
    half = L // 2
    npre = half // T_TILE
    nmain = half // T_TILE
    key = (T_TILE, npre, nmain)
    if key not in _cache:
        _cache[key] = build_program(T_TILE, npre, nmain)
    nc = _cache[key]
    shared = {}
    for k, v in inputs.items():
        if k == "x":
            continue
        a = np.ascontiguousarray(np.asarray(v, dtype=np.float32)[0])
        if k in ("b_i", "b_f", "log_dt", "norm_post_g"):
            a = a.reshape(1, -1)
        shared[k] = a
    in_maps = []
    zeros = np.zeros((half, Dm), np.float32)
    for core in range(8):
        b, hf = core // 2, core % 2
        m = dict(shared)
        m["x_main"] = np.ascontiguousarray(x[b, hf * half:(hf + 1) * half])
        m["x_pre"] = zeros if hf == 0 else np.ascontiguousarray(x[b, 0:half])
        m["flag"] = np.full((128, 1), float(hf), np.float32)
        in_maps.append(m)
    res = run_bass_kernel_spmd(nc, in_maps, core_ids=list(range(8)))
    outp = np.empty((Bsz, L, Dm), np.float32)
    for core in range(8):
        b, hf = core // 2, core % 2
        outp[b, hf * half:(hf + 1) * half] = res.results[core]["out"]
    return outp
```

```python
import contextlib
import numpy as np
import concourse.bass as bass
import concourse.mybir as mybir
from concourse.bass_utils import run_bass_kernel_spmd

F32 = mybir.dt.float32
BF16 = mybir.dt.bfloat16
AF = mybir.ActivationFunctionType
ALU = mybir.AluOpType
COMPUTE = ("pe", "act", "dve", "pool")
NDMA_SLOTS = 8
DEBUG_TAGS = False
PI = float(np.pi)


class Prog:
    def __init__(self, nc):
        self.nc = nc
        self.stack = contextlib.ExitStack()
        self.engs = ("pe", "act", "dve", "pool", "sp")
        self.ops = {e: [] for e in self.engs}
        self.waited = {e: {} for e in self.engs}
        self.res = {}
        self.dma_use = {}
        self.dma_rr = {e: 0 for e in self.engs}
        self.sems = {}
        self.base = {e: 0 for e in COMPUTE}
        self.temp = None

    def sb(self, name, shape, dt):
        st = self.temp if self.temp is not None else self.stack
        return st.enter_context(self.nc.sbuf_tensor(name, list(shape), dt))

    def ps(self, name, shape, dt):
        return self.stack.enter_context(self.nc.psum_tensor(name, list(shape), dt))

    def _sem(self, key):
        if key not in self.sems:
            nm = "s_" + "_".join(str(k) for k in (key if isinstance(key, tuple) else (key,)))
            self.sems[key] = self.stack.enter_context(self.nc.semaphore(nm))
        return self.sems[key]

    def _deps(self, eng, reads, writes):
        deps = {}

        def add(tok):
            if tok is None:
                return
            k, v = tok
            if k == "pe" and eng == "pe":
                return
            if deps.get(k, -1) < v:
                deps[k] = v

        for r in reads:
            st = self.res.get(r)
            if st:
                for k, v in st[0].items():
                    add((k, v))
        for w in writes:
            st = self.res.get(w)
            if st:
                for k, v in st[0].items():
                    add((k, v))
                for k, v in st[1].items():
                    add((k, v))
        out = []
        wd = self.waited[eng]
        for k, v in deps.items():
            if wd.get(k, -1) >= v:
                continue
            wd[k] = v
            out.append((k, v))
        return out

    def _commit(self, tok, reads, writes):
        k, v = tok
        for r in reads:
            st = self.res.setdefault(r, [{}, {}])
            if st[1].get(k, -1) < v:
                st[1][k] = v
        for w in writes:
            old = self.res.get(w)
            wr = {}
            if old is not None and k not in COMPUTE:
                wr = {k2: v2 for k2, v2 in old[0].items() if k2 not in COMPUTE}
            wr[k] = v
            self.res[w] = [wr, {}]

    def _tag(self):
        if not DEBUG_TAGS:
            return None
        import sys as _sys
        f = _sys._getframe(2)
        while f is not None and f.f_code.co_name not in ("do_tile", "build_program", "s5_out", "gate_blocks", "proj_fm"):
            f = f.f_back
        return str(f.f_lineno) if f is not None else None

    def op(self, eng, fn, reads=(), writes=(), sig=True):
        waits = self._deps(eng, reads, writes)
        idx = len(self.ops[eng])
        self.ops[eng].append(dict(fn=fn, waits=waits, sig=sig, dma=None, tag=self._tag()))
        tok = (eng, idx)
        self._commit(tok, reads, writes)
        return tok

    def dma(self, fn, reads=(), writes=(), q="sp"):
        waits = self._deps(q, reads, writes)
        slot = self.dma_rr[q] % NDMA_SLOTS
        self.dma_rr[q] += 1
        key = ("d", q, slot)
        n = self.dma_use.get(key, 0)
        if n > 0:
            prev = n * 16
            if self.waited[q].get(key, -1) < prev:
                self.waited[q][key] = prev
                waits.append((key, prev))
        self.dma_use[key] = n + 1
        tok = (key, (n + 1) * 16)
        self.ops[q].append(dict(fn=fn, waits=waits, sig=False, dma=key))
        self._commit(tok, reads, writes)
        return tok

    def final_wait(self, eng, toks):
        self.ops[eng].append(dict(fn=None, waits=list(toks), sig=False, dma=None))

    def emit(self):
        nc = self.nc
        sigcount = {}
        totals = {}
        for e in COMPUTE:
            c = self.base[e]
            arr = []
            for o in self.ops[e]:
                if o["sig"]:
                    c += 1
                arr.append(c)
            need = [None] * len(arr)
            nxt = None
            for i in range(len(arr) - 1, -1, -1):
                if self.ops[e][i]["sig"]:
                    nxt = arr[i]
                need[i] = nxt
            sigcount[e] = need
            totals[e] = c
            self._sem(e)
        for k in self.dma_use:
            self._sem(k)

        def resolve(k, v):
            if k in COMPUTE:
                val = sigcount[k][v]
                assert val is not None, (k, v)
                return self.sems[k], val
            return self.sems[k], v

        with nc.Block() as block:

            def run(eng_name, eng):
                for o in self.ops[eng_name]:
                    for k, v in o["waits"]:
                        s, val = resolve(k, v)
                        eng.wait_ge(s, val)
                    if o["fn"] is None:
                        continue
                    ins = o["fn"](eng)
                    if o.get("tag"):
                        ins.annotate(o["tag"])
                    if o["dma"] is not None:
                        ins.then_inc(self.sems[o["dma"]], 16)
                    elif o["sig"]:
                        ins.then_inc(self.sems[eng_name], 1)
                for o2 in COMPUTE:
                    if o2 != eng_name and totals[o2] > 0:
                        eng.wait_ge(self.sems[o2], totals[o2])
                for k, n in self.dma_use.items():
                    eng.wait_ge(self.sems[k], n * 16)

            @block.tensor
            def _(e):
                run("pe", e)

            @block.scalar
            def _(e):
                run("act", e)

            @block.vector
            def _(e):
                run("dve", e)

            @block.gpsimd
            def _(e):
                run("pool", e)

            @block.sync
            def _(e):
                run("sp", e)

        self.base = totals
        self.ops = {e: [] for e in self.engs}
        self.waited = {e: {} for e in self.engs}
        self.res = {}

    def close(self):
        self.stack.close()


NBLK = 22
BLK_UA, BLK_UB, BLK_ZB, BLK_ZA, BLK_OA, BLK_G, BLK_AO, BLK_BO, BLK_WO, BLK_S5 = 0, 2, 3, 4, 6, 8, 12, 14, 15, 17
COL_UA, COL_ZA, COL_OA, COL_I, COL_UB, COL_ZB, COL_G = 0, 1024, 2048, 3072, 3080, 3592, 4104


def build_program(T, NPRE, NMAIN, dbg_stop=0):
    NCH = T // 128
    NC8 = T // 8
    nc = bass.Bass("TRN2", target_bir_lowering=False)
    dram = {}

    def din(name, shape):
        dram[name] = nc.dram_tensor(name, list(shape), F32, kind="ExternalInput").ap()
        return dram[name]

    x_pre = din("x_pre", [max(NPRE, 1) * T, 1024])
    x_main = din("x_main", [NMAIN * T, 1024])
    flag = din("flag", [128, 1])
    norm_pre_g = din("norm_pre_g", [1024]); w_in = din("w_in", [1024, 6152])
    conv_w = din("conv_w", [4, 1024]); conv_b = din("conv_b", [1024])
    w_q = din("w_q", [4, 256, 256]); w_k = din("w_k", [4, 256, 256]); w_v = din("w_v", [4, 256, 256])
    b_i = din("b_i", [1, 4]); b_f = din("b_f", [1, 4]); head_g = din("head_g", [1024]); skip_a = din("skip_a", [1024])
    w_a_out = din("w_a_out", [1024, 1024])
    lam_re = din("lam_re", [32, 64]); lam_im = din("lam_im", [32, 64]); log_dt = din("log_dt", [1, 32])
    B_re = din("B_re", [32, 64, 16]); B_im = din("B_im", [32, 64, 16])
    C_re = din("C_re", [32, 16, 64]); C_im = din("C_im", [32, 16, 64]); D_skip = din("D_skip", [32, 16])
    w_glu = din("w_glu", [512, 512]); b_glu = din("b_glu", [512]); w_b_out = din("w_b_out", [512, 1024])
    w_o = din("w_o", [1024, 1024]); norm_post_g = din("norm_post_g", [1, 1024])
    out = nc.dram_tensor("out", [NMAIN * T, 1024], F32, kind="ExternalOutput").ap()
    WS = nc.dram_tensor("wscratch", [NBLK, 128, 4096], BF16, kind="Internal").ap()

    P = Prog(nc)
    uid = [0]

    def sbt(shape, dt, name=None):
        uid[0] += 1
        return P.sb(name or ("t%d" % uid[0]), shape, dt)

    ident_f = sbt([128, 128], F32); ident_b = sbt([128, 128], BF16)
    maskT = sbt([128, 128], F32); mask4 = sbt([128, 4, 128], F32); ones_b = sbt([128, 128], BF16)
    P.op("pool", lambda e: e.memset(ident_f[:], 1.0), writes=["ident_f"])
    P.op("pool", lambda e: e.affine_select(out=ident_f[:], in_=ident_f[:], pattern=[[-1, 128]], compare_op=ALU.is_equal,
                                           fill=0.0, base=0, channel_multiplier=1), reads=["ident_f"], writes=["ident_f"])
    P.op("dve", lambda e: e.tensor_copy(out=ident_b[:], in_=ident_f[:]), reads=["ident_f"], writes=["ident_b"])
    P.op("pool", lambda e: e.memset(maskT[:], 1.0), writes=["maskT"])
    P.op("pool", lambda e: e.affine_select(out=maskT[:], in_=maskT[:], pattern=[[1, 128]], compare_op=ALU.is_ge,
                                           fill=0.0, base=0, channel_multiplier=-1), reads=["maskT"], writes=["maskT"])
    for h in range(4):
        P.op("pool", lambda e, h=h: e.tensor_copy(out=mask4[:, h, :], in_=maskT[:]), reads=["maskT"], writes=["mask4"])
    P.op("pool", lambda e: e.memset(ones_b[:], 1.0), writes=["ones_b"])

    gpre = sbt([128, 8], F32); cb = sbt([128, 8], F32); hg = sbt([128, 8], F32); skp = sbt([128, 8], F32)
    cw = sbt([128, 8, 4], F32); bglu = sbt([128, 4], F32); gpost = sbt([128, 1024], F32); bif = sbt([128, 8], F32)
    flg = sbt([128, 1], F32)
    nonc = dict(allow_slow_non_contiguous=True)
    P.dma(lambda e: e.dma_start(out=gpre[:], in_=norm_pre_g.rearrange("(k p) -> p k", p=128), **nonc), writes=["gpre"])
    P.dma(lambda e: e.dma_start(out=cb[:], in_=conv_b.rearrange("(k p) -> p k", p=128), **nonc), writes=["cb"])
    P.dma(lambda e: e.dma_start(out=hg[:], in_=head_g.rearrange("(k p) -> p k", p=128), **nonc), writes=["hg"])
    P.dma(lambda e: e.dma_start(out=skp[:], in_=skip_a.rearrange("(k p) -> p k", p=128), **nonc), writes=["skp"])
    for k in range(4):
        P.dma(lambda e, k=k: e.dma_start(out=cw[:, :, k], in_=conv_w[k].rearrange("(m p) -> p m", p=128), **nonc), writes=["cw"])
    P.dma(lambda e: e.dma_start(out=bglu[:], in_=b_glu.rearrange("(k p) -> p k", p=128), **nonc), writes=["bglu"])
    P.dma(lambda e: e.dma_start(out=gpost[:], in_=norm_post_g.partition_broadcast(128)), writes=["gpost"])
    P.dma(lambda e: e.dma_start(out=bif[:, 0:4], in_=b_i.partition_broadcast(128)), writes=["bif"])
    P.dma(lambda e: e.dma_start(out=bif[:, 4:8], in_=b_f.partition_broadcast(128)), writes=["bif"])
    P.dma(lambda e: e.dma_start(out=flg[:], in_=flag), writes=["flg"])

    Wqkv = [sbt([128, 4, 2, 256], BF16) for _ in range(3)]
    Wif = sbt([128, 8, 8], BF16); Wglu = sbt([128, 4, 512], BF16); cdiag = sbt([128, 8, 4, 128], BF16)
    AR32 = sbt([128, 2, 16], F32); ANI = sbt([128, 16], F32); API = sbt([128, 16], F32)
    pA = P.ps("pA", [128, 512], F32); pB = P.ps("pB", [128, 512], F32)
    pT = P.ps("pT", [128, 1024], BF16); pGD = P.ps("pGD", [128, 512], F32)
    pS = P.ps("pS", [128, 512], F32); pN0 = P.ps("pN0", [128, 512], F32); pN1 = P.ps("pN1", [128, 512], F32)
    pM = P.ps("pM", [128, 512], F32)
    P.temp = contextlib.ExitStack()
    stg = sbt([128, 4096], F32, "stg")
    stgb = sbt([128, 4096], BF16, "stgb")
    for wi, (wsrc, scl) in enumerate(((w_q, 1.0), (w_k, 1.0 / 16.0), (w_v, 1.0))):
        P.dma(lambda e, wsrc=wsrc: e.dma_start(out=stg[:, 0:2048].rearrange("p (h d n) -> p h d n", h=4, d=2),
                                               in_=wsrc.rearrange("h (d p) n -> p h d n", p=128)), writes=["stg"])
        P.op("dve", lambda e, wi=wi, scl=scl: e.tensor_scalar(out=Wqkv[wi][:].rearrange("p h d n -> p (h d n)"), in0=stg[:, 0:2048],
                                                              scalar1=scl, scalar2=None, op0=ALU.mult), reads=["stg"], writes=["Wqkv%d" % wi])
    Wq, Wk, Wv = Wqkv
    P.dma(lambda e: e.dma_start(out=stg[:, 0:64].rearrange("p (k n) -> p k n", k=8),
                                in_=w_in[:, COL_I:COL_I + 8].rearrange("(k p) n -> p k n", p=128), **nonc), writes=["stg"])
    for kt in range(8):
        P.op("dve", lambda e, kt=kt: e.tensor_scalar(out=Wif[:, kt, :], in0=stg[:, kt * 8:(kt + 1) * 8], scalar1=gpre[:, kt:kt + 1],
                                                     scalar2=None, op0=ALU.mult), reads=["stg", "gpre"], writes=["Wif"])
    P.dma(lambda e: e.dma_start(out=stg[:, 0:2048].rearrange("p (k n) -> p k n", k=4),
                                in_=w_glu.rearrange("(k p) n -> p k n", p=128)), writes=["stg"])
    P.op("dve", lambda e: e.tensor_copy(out=Wglu[:].rearrange("p k n -> p (k n)"), in_=stg[:, 0:2048]), reads=["stg"], writes=["Wglu"])
    for mt in range(8):
        for k in range(4):
            P.op("dve", lambda e, mt=mt, k=k: e.tensor_scalar(out=cdiag[:, mt, k, :], in0=ident_f[:], scalar1=cw[:, mt, k:k + 1],
                                                               scalar2=None, op0=ALU.mult), reads=["ident_f", "cw"], writes=["cdiag"])

    def stage_block(blk, src_ap_f, scale_gpre, nk):
        ncol = 4096 // nk
        P.dma(lambda e: e.dma_start(out=stg[:].rearrange("p (k n) -> p k n", k=nk), in_=src_ap_f), writes=["stg"])
        if scale_gpre:
            for kt in range(nk):
                P.op("dve",
                     lambda e, kt=kt: e.tensor_scalar(out=stgb[:, kt * ncol:(kt + 1) * ncol], in0=stg[:, kt * ncol:(kt + 1) * ncol],
                                                      scalar1=gpre[:, kt:kt + 1], scalar2=None, op0=ALU.mult),
                     reads=["stg", "gpre"], writes=["stgb"])
        else:
            P.op("dve", lambda e: e.tensor_copy(out=stgb[:, 0:2048], in_=stg[:, 0:2048]), reads=["stg"], writes=["stgb"])
            P.op("act", lambda e: e.copy(out=stgb[:, 2048:4096], in_=stg[:, 2048:4096]), reads=["stg"], writes=["stgb"])
        P.dma(lambda e: e.dma_start(out=WS[blk], in_=stgb[:]), reads=["stgb"], writes=["WS%d" % blk])

    def win_cols(c0):
        return w_in[:, c0:c0 + 512].rearrange("(k p) n -> p k n", p=128)

    win_blocks = [(BLK_UA, COL_UA), (BLK_UA + 1, COL_UA + 512), (BLK_UB, COL_UB), (BLK_ZB, COL_ZB), (BLK_ZA, COL_ZA),
                  (BLK_ZA + 1, COL_ZA + 512), (BLK_OA, COL_OA), (BLK_OA + 1, COL_OA + 512)] + [(BLK_G + i, COL_G + 512 * i) for i in range(4)]
    pending = []
    for blk, c0 in win_blocks:
        pending.append((blk, win_cols(c0), True, 8))
    for i in range(2):
        pending.append((BLK_AO + i, w_a_out[:, 512 * i:512 * i + 512].rearrange("(k p) n -> p k n", p=128), False, 8))
        pending.append((BLK_WO + i, w_o[:, 512 * i:512 * i + 512].rearrange("(k p) n -> p k n", p=128), False, 8))
    pending.append((BLK_BO, w_b_out.rearrange("(k p) n -> p k n", p=128), False, 4))

    def stage_some(n=1):
        for _ in range(n):
            if pending:
                stage_block(*pending.pop(0))

    stage_some(2)

    def small(n, name=None):
        return sbt([128, n], F32, name)

    cnt = [0]

    def el(eng, fn, r, w):
        P.op(eng, fn, reads=r, writes=w)

    def tt(outp, a, b, op, r, w, eng="dve"):
        el(eng, lambda e: e.tensor_tensor(out=outp, in0=a, in1=b, op=op), r, w)

    def ts(outp, a, s1, s2, op0, op1, r, w, eng="dve"):
        if op1 is None:
            el(eng, lambda e: e.tensor_scalar(out=outp, in0=a, scalar1=s1, scalar2=None, op0=op0), r, w)
        else:
            el(eng, lambda e: e.tensor_scalar(out=outp, in0=a, scalar1=s1, scalar2=s2, op0=op0, op1=op1), r, w)

    LR = small(32); LI = small(32); DT = small(32)
    for hf in range(2):
        sl = slice(64 * hf, 64 * hf + 64)
        P.dma(lambda e, sl=sl: e.dma_start(out=LR[sl, :], in_=lam_re.rearrange("g p -> p g"), **nonc), writes=["LR"])
        P.dma(lambda e, sl=sl: e.dma_start(out=LI[sl, :], in_=lam_im.rearrange("g p -> p g"), **nonc), writes=["LI"])
    P.dma(lambda e: e.dma_start(out=DT[:], in_=log_dt.partition_broadcast(128)), writes=["DT"])
    el("act", lambda e: e.activation(out=DT[:], in_=DT[:], func=AF.Exp), ["DT"], ["DT"])
    TH = small(32); MAG = small(32); t0 = small(32); t1 = small(32); t2 = small(32); kk = small(32)
    tt(TH[:], LI[:], DT[:], ALU.mult, ["LI", "DT"], ["TH"])
    tt(t0[:], LR[:], DT[:], ALU.mult, ["LR", "DT"], ["t0"])
    el("act", lambda e: e.activation(out=MAG[:], in_=t0[:], func=AF.Exp), ["t0"], ["MAG"])
    IMAG2 = small(32)
    el("act", lambda e: e.activation(out=IMAG2[:], in_=t0[:], func=AF.Exp, scale=-2.0), ["t0"], ["IMAG2"])

    def sin_of(dst, src, shift, nm):
        ts(t1[:], src, shift, None, ALU.add, None, [nm, "t1"], ["t1"])
        el("pool", lambda e: e.memset(kk[:], 0.0), [], ["kk"])
        for m in range(7):
            ts(t2[:], t1[:], (2 * m + 1) * PI, None, ALU.is_gt, None, ["t1"], ["t2"])
            tt(kk[:], kk[:], t2[:], ALU.add, ["kk", "t2"], ["kk"])
        ts(kk[:], kk[:], -2.0 * PI, None, ALU.mult, None, ["kk"], ["kk"])
        tt(t1[:], t1[:], kk[:], ALU.add, ["t1", "kk"], ["t1"])
        el("act", lambda e: e.activation(out=dst, in_=t1[:], func=AF.Sin), ["t1"], [nm + "_s"])

    SN = small(32); CS = small(32)
    sin_of(SN[:], TH[:], 0.0, "TH")
    sin_of(CS[:], TH[:], PI / 2.0, "TH")
    pwr = sbt([128, 9, 32], F32); pwi = sbt([128, 9, 32], F32); pnr = sbt([128, 8, 32], F32); pni = sbt([128, 8, 32], F32)
    el("pool", lambda e: e.memset(pwr[:, 0, :], 1.0), [], ["pw"]); el("pool", lambda e: e.memset(pwi[:, 0, :], 0.0), [], ["pw"])
    el("pool", lambda e: e.memset(pnr[:, 0, :], 1.0), [], ["pn"]); el("pool", lambda e: e.memset(pni[:, 0, :], 0.0), [], ["pn"])
    tt(pwr[:, 1, :], MAG[:], CS[:], ALU.mult, ["MAG", "TH_s"], ["pw"])
    tt(pwi[:, 1, :], MAG[:], SN[:], ALU.mult, ["MAG", "TH_s"], ["pw"])
    tt(pnr[:, 1, :], pwr[:, 1, :], IMAG2[:], ALU.mult, ["pw", "IMAG2"], ["pn"])
    tt(t0[:], pwi[:, 1, :], IMAG2[:], ALU.mult, ["pw", "IMAG2"], ["t0"])
    ts(pni[:, 1, :], t0[:], -1.0, None, ALU.mult, None, ["t0"], ["pn"])

    def cmul(or_, oi_, ar, ai, br, bi, r, w):
        raise NotImplementedError

    u0 = small(32); u1 = small(32)
    for k in range(1, 8):
        for (xr, xi, nm, lim) in ((pwr, pwi, "pw", 9), (pnr, pni, "pn", 8)):
            if k + 1 >= lim:
                continue
            tt(u0[:], xr[:, k, :], xr[:, 1, :], ALU.mult, [nm], ["u0"])
            tt(u1[:], xi[:, k, :], xi[:, 1, :], ALU.mult, [nm], ["u1"])
            tt(xr[:, k + 1, :], u0[:], u1[:], ALU.subtract, ["u0", "u1"], [nm])
            tt(u0[:], xr[:, k, :], xi[:, 1, :], ALU.mult, [nm], ["u0"])
            tt(u1[:], xi[:, k, :], xr[:, 1, :], ALU.mult, [nm], ["u1"])
            tt(xi[:, k + 1, :], u0[:], u1[:], ALU.add, ["u0", "u1"], [nm])
    den = small(32); qr = small(32); qi = small(32); nr = small(32)
    tt(u0[:], LR[:], LR[:], ALU.mult, ["LR"], ["u0"]); tt(u1[:], LI[:], LI[:], ALU.mult, ["LI"], ["u1"])
    tt(den[:], u0[:], u1[:], ALU.add, ["u0", "u1"], ["den"])
    el("dve", lambda e: e.reciprocal(out=den[:], in_=den[:]), ["den"], ["den"])
    ts(nr[:], pwr[:, 1, :], -1.0, None, ALU.add, None, ["pw"], ["nr"])
    tt(u0[:], nr[:], LR[:], ALU.mult, ["nr", "LR"], ["u0"]); tt(u1[:], pwi[:, 1, :], LI[:], ALU.mult, ["pw", "LI"], ["u1"])
    tt(qr[:], u0[:], u1[:], ALU.add, ["u0", "u1"], ["qr"]); tt(qr[:], qr[:], den[:], ALU.mult, ["qr", "den"], ["qr"])
    tt(u0[:], pwi[:, 1, :], LR[:], ALU.mult, ["pw", "LR"], ["u0"]); tt(u1[:], nr[:], LI[:], ALU.mult, ["nr", "LI"], ["u1"])
    tt(qi[:], u0[:], u1[:], ALU.subtract, ["u0", "u1"], ["qi"]); tt(qi[:], qi[:], den[:], ALU.mult, ["qi", "den"], ["qi"])
    Br = sbt([128, 32, 16], F32); Bi = sbt([128, 32, 16], F32); bbr = sbt([128, 32, 16], F32); bbi = sbt([128, 32, 16], F32)
    v0 = sbt([128, 32, 16], F32); v1 = sbt([128, 32, 16], F32)
    for hf in range(2):
        sl = slice(64 * hf, 64 * hf + 64)
        P.dma(lambda e, sl=sl: e.dma_start(out=Br[sl], in_=B_re.rearrange("g p n -> p g n"), **nonc), writes=["Br"])
        P.dma(lambda e, sl=sl: e.dma_start(out=Bi[sl], in_=B_im.rearrange("g p n -> p g n"), **nonc), writes=["Bi"])

    def bc(s):
        return s.unsqueeze(2).to_broadcast([128, 32, 16])

    def cmul3(orr, oii, sr, si, sn, xr, xi, xn, on):
        xn = [xn] if isinstance(xn, str) else list(xn)
        sn = [sn] if isinstance(sn, str) else list(sn)
        stage_some(1)
        tt(v0[:], xr, bc(sr), ALU.mult, xn + sn, ["v0"]); tt(v1[:], xi, bc(si), ALU.mult, xn + sn, ["v1"])
        tt(orr, v0[:], v1[:], ALU.subtract, ["v0", "v1"], [on])
        tt(v0[:], xi, bc(sr), ALU.mult, xn + sn, ["v0"]); tt(v1[:], xr, bc(si), ALU.mult, xn + sn, ["v1"])
        tt(oii, v0[:], v1[:], ALU.add, ["v0", "v1"], [on])

    el("dve", lambda e: e.tensor_copy(out=u0[:], in_=qr[:]), ["qr"], ["qq"])
    cmul3(bbr[:], bbi[:], qr[:], qi[:], ["qr", "qi"], Br[:], Bi[:], ["Br", "Bi"], "bb")
    CTr = sbt([128, 32, 16], F32); CTi = sbt([128, 32, 16], F32)
    Cdup = sbt([128, 4, 2, 64], F32)
    for (Csrc, CTt, nm) in ((C_re, CTr, "CTr"), (C_im, CTi, "CTi")):
        for d in range(2):
            P.dma(lambda e, Csrc=Csrc, d=d: e.dma_start(out=Cdup[:, :, d, :], in_=Csrc.rearrange("(t g) n p -> (g n) t p", t=4)),
                  writes=["Cdup"])
        for t in range(4):
            P.op("pe", lambda e, t=t: e.transpose(out=pA[:, t * 128:(t + 1) * 128], in_=Cdup[:, t, :, :].rearrange("q d p -> q (d p)"),
                                                  identity=ident_f[:]), reads=["Cdup", "ident_f"], writes=["pA"])
        el("act", lambda e, CTt=CTt: e.copy(out=CTt[:].rearrange("p g n -> p (g n)"), in_=pA[:]), ["pA"], [nm])
    Er = sbt([128, 32, 8, 16], F32); Ei = sbt([128, 32, 8, 16], F32)
    Fr = sbt([128, 32, 8, 16], F32); Fi = sbt([128, 32, 8, 16], F32)
    for j in range(8):
        cmul3(Er[:, :, j, :], Ei[:, :, j, :], pwr[:, 7 - j, :], pwi[:, 7 - j, :], "pw", bbr[:], bbi[:], "bb", "E")
        cmul3(Fr[:, :, j, :], Fi[:, :, j, :], pwr[:, j + 1, :], pwi[:, j + 1, :], "pw", CTr[:], CTi[:], ["CTr", "CTi"], "F")
    halfm = sbt([128, 2], F32)
    el("pool", lambda e: e.memset(halfm[:], 0.0), [], ["halfm"])
    el("pool", lambda e: e.memset(halfm[0:64, 0:1], 1.0), ["halfm"], ["halfm"])
    el("pool", lambda e: e.memset(halfm[64:128, 1:2], 1.0), ["halfm"], ["halfm"])
    for ri, (Et, blk) in enumerate(((Er, BLK_S5), (Ei, BLK_S5 + 1))):
        el("pool", lambda e: e.memset(stgb[:], 0.0), ["stgb"], ["stgb"])
        for g in range(32):
            ps_ = pA if g % 2 == 0 else pB
            nm = "pA" if g % 2 == 0 else "pB"
            P.op("pe", lambda e, Et=Et, g=g, ps_=ps_: e.transpose(out=ps_[:, 0:64], in_=Et[0:64, g, :, :].rearrange("p j n -> p (j n)"),
                                                                  identity=ident_f[0:64, 0:64]), reads=["E", "ident_f"], writes=[nm])
            c0 = g * 128 + 64 * (g % 2)
            el("act" if g % 2 == 0 else "dve",
               (lambda e, ps_=ps_, c0=c0: e.copy(out=stgb[:, c0:c0 + 64], in_=ps_[:, 0:64])) if g % 2 == 0 else
               (lambda e, ps_=ps_, c0=c0: e.tensor_copy(out=stgb[:, c0:c0 + 64], in_=ps_[:, 0:64])), [nm], ["stgb"])
        P.dma(lambda e, blk=blk: e.dma_start(out=WS[blk], in_=stgb[:]), reads=["stgb"], writes=["WS%d" % blk])
    for ri, (Ft, blk, sg) in enumerate(((Fr, BLK_S5 + 3, 1.0), (Fi, BLK_S5 + 4, -1.0))):
        for g in range(32):
            P.op("dve",
                 lambda e, Ft=Ft, g=g, sg=sg: e.tensor_scalar(out=stgb[:, g * 128:(g + 1) * 128], in0=Ft[:, g, :, :].rearrange("p j n -> p (j n)"),
                                                              scalar1=halfm[:, (g % 2):(g % 2) + 1], scalar2=sg, op0=ALU.mult, op1=ALU.mult),
                 reads=["F", "halfm"], writes=["stgb"])
        P.dma(lambda e, blk=blk: e.dma_start(out=WS[blk], in_=stgb[:]), reads=["stgb"], writes=["WS%d" % blk])
    Gr = sbt([128, 32, 8, 16], F32); Gni = sbt([128, 32, 8, 16], F32)
    i8r = small(32); i8i = small(32)
    tt(u0[:], pnr[:, 7, :], pnr[:, 1, :], ALU.mult, ["pn"], ["u0"]); tt(u1[:], pni[:, 7, :], pni[:, 1, :], ALU.mult, ["pn"], ["u1"])
    tt(i8r[:], u0[:], u1[:], ALU.subtract, ["u0", "u1"], ["i8"])
    tt(u0[:], pnr[:, 7, :], pni[:, 1, :], ALU.mult, ["pn"], ["u0"]); tt(u1[:], pni[:, 7, :], pnr[:, 1, :], ALU.mult, ["pn"], ["u1"])
    tt(i8i[:], u0[:], u1[:], ALU.add, ["u0", "u1"], ["i8"])
    for j in range(8):
        cmul3(Gr[:, :, j, :], Gni[:, :, j, :], i8r[:], i8i[:], "i8", Fr[:, :, j, :], Fi[:, :, j, :], "F", "G")
    ts(Gni[:], Gni[:], -1.0, None, ALU.mult, None, ["G"], ["G"])
    bmask = sbt([128, 8, 16], F32); Dcol = sbt([128, 32], F32)
    el("pool", lambda e: e.memset(bmask[:], 1.0), [], ["bmask"])
    el("pool", lambda e: e.affine_select(out=bmask[:], in_=bmask[:], pattern=[[16, 8], [0, 16]], compare_op=ALU.is_ge, fill=0.0,
                                         base=15, channel_multiplier=-1), ["bmask"], ["bmask"])
    for j in range(8):
        P.dma(lambda e, j=j: e.dma_start(out=Dcol[16 * j:16 * j + 16, :], in_=D_skip.rearrange("g n -> n g"), **nonc), writes=["Dcol"])
    wtmp = sbt([128, 128], F32)
    for g in range(32):
        ps_ = pA if g % 2 == 0 else pB
        nm = "pA" if g % 2 == 0 else "pB"
        P.op("pe", lambda e, g=g, ps_=ps_: e.matmul(ps_[:, 0:128], lhsT=Er[0:64, g, :, :].rearrange("p j n -> p (j n)"),
                                                    rhs=Gr[0:64, g, :, :].rearrange("p j n -> p (j n)"), start=True, stop=False),
             reads=["E", "G"], writes=[nm], sig=False)
        P.op("pe", lambda e, g=g, ps_=ps_: e.matmul(ps_[:, 0:128], lhsT=Ei[0:64, g, :, :].rearrange("p j n -> p (j n)"),
                                                    rhs=Gni[0:64, g, :, :].rearrange("p j n -> p (j n)"), start=False, stop=True),
             reads=["E", "G"], writes=[nm])
        tt(wtmp[:], ps_[:, 0:128], bmask[:].rearrange("p j n -> p (j n)"), ALU.mult, [nm, "bmask"], ["wtmp"])
        el("dve", lambda e, g=g: e.scalar_tensor_tensor(out=stgb[:, g * 128:(g + 1) * 128], in0=ident_f[:], scalar=Dcol[:, g:g + 1],
                                                        in1=wtmp[:], op0=ALU.mult, op1=ALU.add), ["ident_f", "Dcol", "wtmp", "stgb"], ["stgb"])
    P.dma(lambda e: e.dma_start(out=WS[BLK_S5 + 2], in_=stgb[:]), reads=["stgb"], writes=["WS%d" % (BLK_S5 + 2)])
    for g2 in range(2):
        sl = slice(64 * g2, 64 * g2 + 64)
        src_r = pwr[sl, 8, :].rearrange("p (q t) -> p q t", t=2)[:, :, g2]
        src_i = pwi[sl, 8, :].rearrange("p (q t) -> p q t", t=2)[:, :, g2]
        el("dve", lambda e, sl=sl, src_r=src_r: e.tensor_copy(out=AR32[sl, 0, :], in_=src_r), ["pw"], ["AR32"])
        el("dve", lambda e, sl=sl, src_r=src_r: e.tensor_copy(out=AR32[sl, 1, :], in_=src_r), ["pw"], ["AR32"])
        el("dve", lambda e, sl=sl, src_i=src_i: e.tensor_copy(out=API[sl, :], in_=src_i), ["pw"], ["API"])
        ts(ANI[sl, :], src_i, -1.0, None, ALU.mult, None, ["pw"], ["ANI"])

    stage_some(100)
    if dbg_stop == 1:
        tk = P.dma(lambda e: e.dma_start(out=out[0:128, :], in_=gpost[:]), reads=["gpost"])
        P.final_wait("sp", [tk])
        P.emit(); P.temp.close(); P.close()
        return nc
    P.emit()
    P.temp.close()
    P.temp = None
    NSLOT = 3
    wslot = [sbt([128, 4096], BF16, "wslot%d" % i) for i in range(NSLOT)]
    slot_rr = [0]

    def load_block(blk):
        s = slot_rr[0] % NSLOT
        slot_rr[0] += 1
        P.dma(lambda e: e.dma_start(out=wslot[s][:], in_=WS[blk]), reads=["WS%d" % blk], writes=["wslot%d" % s])
        return wslot[s], "wslot%d" % s

    xs = [sbt([128, 1024], F32, "xs%d" % i) for i in range(2)]
    xr = [sbt([128, 1024], F32, "xr%d" % i) for i in range(2)]
    junk = sbt([128, 1024], BF16); xnb = sbt([128, 1024], BF16)
    ss = small(1); rstd = small(1)
    xnT = sbt([128, 8, T], BF16, "xnT"); uaT = sbt([128, 8, T + 4], BF16, "uaT"); cT = sbt([128, 8, T], BF16, "cT")
    zaT = sbt([128, 8, T], BF16, "zaT"); oaT = sbt([128, 8, T], BF16, "oaT"); zbT = sbt([128, 4, T], BF16, "zbT")
    qT = sbt([128, 4, 2, T], BF16, "qT"); kT = sbt([128, 4, 2, T], BF16, "kT")
    ktm = sbt([128, NCH, 4, 256], BF16, "ktm"); vw = sbt([128, NCH, 4, 256], BF16, "vw")
    gates = sbt([128, NCH, 8], F32, "gates")
    hfT = sbt([128, 8, T], BF16, "hfT"); mrgT = zaT; hbT = sbt([128, 4, T], BF16, "hbT")
    Cst = sbt([128, 4, 2, 256], F32, "Cst"); Cbf = sbt([128, 4, 2, 256], BF16, "Cbf")
    nst = sbt([128, 4, 2], F32, "nst"); nrep = sbt([128, 4, 2, 128], BF16, "nrep")
    Utm = sbt([128, 32, 8, 16], BF16, "Utm"); U2 = sbt([128, 32, NC8], BF16, "U2")
    Xall = sbt([128, NC8, 2, 16], F32, "Xall"); Sall = sbt([128, NC8 + 1, 2, 16], F32, "Sall"); Sbf = sbt([128, 2, 16, NC8], BF16, "Sbf")
    Yg = sbt([128, 32, NC8], BF16, "Yg"); Ytm = Utm[:].rearrange("p g j n -> p (g j n)").rearrange("p (j c) -> p j c", j=8); yT = sbt([128, 4, T], BF16, "yT")
    for (t_, nm) in ((Cst, "Cst"), (Cbf, "Cbf"), (nst, "nst"), (nrep, "nrep"), (Sall, "Sall"), (uaT, "uaT")):
        flat = t_[:]
        el("pool", lambda e, flat=flat: e.memset(flat, 0.0), [], [nm])

    e1a = sbt([128, NCH, 4], F32); lfa = sbt([128, NCH, 4], F32); emb4 = sbt([128, NCH, 4, 128], F32); dec4 = sbt([128, NCH, 4], F32)
    wcol4 = sbt([128, NCH, 4], F32); wcolb4 = sbt([128, NCH, 4, 2], BF16); wrep4 = sbt([128, NCH, 4, 128], BF16)
    lfrep = sbt([128, 4, 128], F32); lf = small(4, "lf"); ig = small(4); bcol = small(4); wcol = small(4, "wcol"); wcolb = sbt([128, 4, 2], BF16)
    wrep = sbt([128, 4, 128], BF16); emb = sbt([128, 4, 128], F32, "emb"); dec = small(4, "dec"); SmT = sbt([128, 4, 128], BF16, "SmT")
    aden = sbt([128, 4, 128], F32); rden = sbt([128, 4, 128], F32, "rden"); hT = sbt([128, 8, 128], F32, "hT"); sq = sbt([128, 8, 128], BF16)
    rsh = sbt([128, 4, 128], F32); hn = sbt([128, 8, 128], F32, "hn"); e1 = small(4)
    gAll = oaT; m1 = sbt([128, T], F32); m2 = sbt([128, T], F32)
    ot = [sbt([128, 1024], F32, "ot%d" % i) for i in range(2)]
    ss2 = small(1); rstd2 = small(1); sgl = sbt([128, T], BF16); xg = sbt([128, T], F32)

    def rsqrt_col(dst, src, scale, nm_src, nm_dst):
        ts(dst, src, scale, 1e-6, ALU.mult, ALU.add, [nm_src], [nm_dst])
        el("act", lambda e: e.activation(out=dst, in_=dst, func=AF.Ln), [nm_dst], [nm_dst])
        el("act", lambda e: e.activation(out=dst, in_=dst, func=AF.Exp, scale=-0.5), [nm_dst], [nm_dst])

    def mm(outp, lhsT, rhs, start, stop, r, w, sig=None):
        P.op("pe", lambda e: e.matmul(outp, lhsT=lhsT, rhs=rhs, start=start, stop=stop), reads=r, writes=w,
             sig=(stop if sig is None else sig))

    acc_rr = [0]

    def acc_bank():
        acc_rr[0] += 1
        return (pA, "pA") if acc_rr[0] % 2 else (pB, "pB")

    out_toks = []

    class StopBuild(Exception):
        pass

    def chk(level):
        if dbg_stop == level:
            raise StopBuild()

    prefetched = [False]

    def do_tile(xsrc, row0, main, ti, nxt=None):
        def issue_x(src_, r0, sub):
            xb2, xn2 = xs[sub % 2], "xs%d" % (sub % 2)
            P.dma(lambda e: e.dma_start(out=xb2[:], in_=src_[r0 + sub * 128: r0 + (sub + 1) * 128, :]), writes=[xn2])

        for sub in range(NCH):
            xb_, xnm = xs[sub % 2], "xs%d" % (sub % 2)
            if not (prefetched[0] and sub < 2):
                issue_x(xsrc, row0, sub)
            el("act", lambda e, xb_=xb_: e.activation(out=junk[:], in_=xb_[:], func=AF.Square, accum_out=ss[:]), [xnm], ["junk", "ss"])
            rsqrt_col(rstd[:], ss[:], 1.0 / 1024.0, "ss", "rstd")
            ts(xnb[:], xb_[:], rstd[:, 0:1], None, ALU.mult, None, [xnm, "rstd"], ["xnb"])
            for kt in range(8):
                P.op("pe", lambda e, kt=kt: e.transpose(out=pT[:, kt * 128:(kt + 1) * 128], in_=xnb[:, kt * 128:(kt + 1) * 128], identity=ident_b[:]),
                     reads=["xnb", "ident_b"], writes=["pT"], sig=(kt == 7))
            el("act", lambda e, sub=sub: e.copy(out=xnT[:, :, sub * 128:(sub + 1) * 128], in_=pT[:].rearrange("p (k t) -> p k t", k=8)),
               ["pT"], ["xnT"])

        prefetched[0] = False
        if nxt is not None and NCH <= 2:
            for sub in range(min(2, NCH)):
                issue_x(nxt[0], nxt[1], sub)
            prefetched[0] = True
        chk(2)

        def proj_fm(blk, ncolt, evac):
            W, wn = load_block(blk)
            Wv_ = W[:].rearrange("p (k n) -> p k n", k=8)
            for ct in range(ncolt):
                ps_, pn = acc_bank()
                for kt in range(8):
                    mm(ps_[:, 0:T], Wv_[:, kt, ct * 128:(ct + 1) * 128], xnT[:, kt, :], kt == 0, kt == 7, [wn, "xnT"], [pn])
                evac(ct, ps_, pn)

        el("dve", lambda e: e.tensor_copy(out=uaT[:, :, 1:4], in_=uaT[:, :, T + 1:T + 4]), ["uaT"], ["uaT"])
        for half in range(2):
            proj_fm(BLK_UA + half, 4, lambda ct, ps_, pn, half=half: el(
                "act", lambda e: e.copy(out=uaT[:, half * 4 + ct, 4:T + 4], in_=ps_[:, 0:T]), [pn], ["uaT"]))
        chk(3)
        for ch in range(NCH):
            for kt in range(8):
                mm(pM[:, 0:8], xnT[:, kt, ch * 128:(ch + 1) * 128], Wif[:, kt, :], kt == 0, kt == 7, ["xnT", "Wif"], ["pM"])
            tt(gates[:, ch, :], pM[:, 0:8], bif[:], ALU.add, ["pM", "bif"], ["gates"])
        chk(4)
        W, wn = load_block(BLK_UB)
        Wv_ = W[:].rearrange("p (k n) -> p k n", k=8)
        for j in range(8):
            ps_, pn = acc_bank()
            for kt in range(8):
                lhs = xnT[:, kt, :].rearrange("p (c j) -> p j c", j=8)[:, j, :]
                mm(ps_[0:NC8, :], lhs, Wv_[:, kt, :], kt == 0, kt == 7, [wn, "xnT"], [pn])
            el("act" if j % 2 else "dve",
               (lambda e, ps_=ps_, j=j: e.copy(out=Utm[0:NC8, :, j, :], in_=ps_[0:NC8, :].rearrange("c (g n) -> c g n", g=32))) if j % 2 else
               (lambda e, ps_=ps_, j=j: e.tensor_copy(out=Utm[0:NC8, :, j, :], in_=ps_[0:NC8, :].rearrange("c (g n) -> c g n", g=32))), [pn], ["Utm"])
        chk(5)
        for g in range(32):
            P.op("pe", lambda e, g=g: e.transpose(out=pT[:, g * NC8:(g + 1) * NC8], in_=Utm[0:NC8, g, :, :].rearrange("c j n -> c (j n)"),
                                                  identity=ident_b[0:NC8, 0:NC8]), reads=["Utm", "ident_b"], writes=["pT"], sig=(g == 31))
        el("act", lambda e: e.copy(out=U2[:].rearrange("p g c -> p (g c)"), in_=pT[:, 0:32 * NC8]), ["pT"], ["U2"])
        W1r, w1rn = load_block(BLK_S5)
        W1i, w1in = load_block(BLK_S5 + 1)
        for ri, (Wt, wn_) in enumerate(((W1r, w1rn), (W1i, w1in))):
            Wg = Wt[:].rearrange("p (g n) -> p g n", g=32)
            for q in range(16):
                for g2 in range(2):
                    g = 2 * q + g2
                    mm(pM[:, (q * NC8):(q + 1) * NC8], Wg[:, g, :], U2[:, g, :], g2 == 0, g2 == 1, [wn_, "U2"], ["pM"], sig=(q == 15 and g2 == 1))
            el("act" if ri == 0 else "dve",
               (lambda e, ri=ri: e.copy(out=Xall[:, :, ri, :].rearrange("p c q -> p q c"), in_=pM[:, 0:16 * NC8].rearrange("p (q c) -> p q c", q=16))) if ri == 0 else
               (lambda e, ri=ri: e.tensor_copy(out=Xall[:, :, ri, :].rearrange("p c q -> p q c"), in_=pM[:, 0:16 * NC8].rearrange("p (q c) -> p q c", q=16))),
               ["pM"], ["Xall"])
        chk(6)
        T1 = sbt([128, 2, 16], F32, "scT1_%d" % ti) if False else None
        for c in range(NC8):
            sp_ = Sall[:, c, :, :]
            sn_ = Sall[:, c + 1, :, :]
            P.op("pool", lambda e, sp_=sp_: e.tensor_tensor(out=sc_t1[:], in0=sp_, in1=AR32[:], op=ALU.mult), reads=["Sall"], writes=["sc_t1"])
            P.op("pool", lambda e, c=c: e.tensor_tensor(out=sc_t2[:, 0, :], in0=Sall[:, c, 1, :], in1=ANI[:], op=ALU.mult), reads=["Sall"], writes=["sc_t2"])
            P.op("pool", lambda e, c=c: e.tensor_tensor(out=sc_t2[:, 1, :], in0=Sall[:, c, 0, :], in1=API[:], op=ALU.mult), reads=["Sall"], writes=["sc_t2"])
            P.op("pool", lambda e, c=c: e.tensor_tensor(out=sc_t1[:], in0=sc_t1[:], in1=Xall[:, c, :, :], op=ALU.add), reads=["sc_t1", "Xall"], writes=["sc_t1"])
            P.op("pool", lambda e, sn_=sn_: e.tensor_tensor(out=sn_, in0=sc_t1[:], in1=sc_t2[:], op=ALU.add), reads=["sc_t1", "sc_t2"], writes=["Sall"])
        if main:
            for ri in range(2):
                el("pool", lambda e, ri=ri: e.tensor_copy(out=Sbf[:, ri, :, :], in_=Sall[:, 0:NC8, ri, :].rearrange("p c q -> p q c")),
                   ["Sall"], ["Sbf"])
        el("pool", lambda e: e.tensor_copy(out=Sall[:, 0, :, :], in_=Sall[:, NC8, :, :]), ["Sall"], ["Sall"])
        def s5_out():
            proj_fm(BLK_ZB, 4, lambda ct, ps_, pn: el("act", lambda e: e.activation(out=zbT[:, ct, :], in_=ps_[:, 0:T], func=AF.Silu), [pn], ["zbT"]))
            Wi_, win_ = load_block(BLK_S5 + 2)
            Wr_, wrn_ = load_block(BLK_S5 + 3)
            Wm_, wmn_ = load_block(BLK_S5 + 4)
            Wi_g = Wi_[:].rearrange("p (g n) -> p g n", g=32); Wr_g = Wr_[:].rearrange("p (g n) -> p g n", g=32)
            Wm_g = Wm_[:].rearrange("p (g n) -> p g n", g=32)
            GP = 512 // NC8
            for g0 in range(0, 32, GP):
                ng = min(GP, 32 - g0)
                for gi in range(ng):
                    g = g0 + gi
                    o_ = pM[:, gi * NC8:(gi + 1) * NC8]
                    mm(o_, Wi_g[:, g, :], U2[:, g, :], True, False, [win_, "U2"], ["pM"], sig=False)
                    mm(o_, Wr_g[:, g, :], Sbf[:, 0, g // 2, :], False, False, [wrn_, "Sbf"], ["pM"], sig=False)
                    mm(o_, Wm_g[:, g, :], Sbf[:, 1, g // 2, :], False, True, [wmn_, "Sbf"], ["pM"], sig=(gi == ng - 1))
                el("act", lambda e, g0=g0, ng=ng: e.activation(out=Yg[:, g0:g0 + ng, :].rearrange("p g c -> p (g c)"), in_=pM[:, 0:ng * NC8],
                                                               func=AF.Gelu_apprx_tanh), ["pM"], ["Yg"])
            for g0 in range(0, 32, 8):
                for gi in range(8):
                    g = g0 + gi
                    P.op("pe", lambda e, g=g, gi=gi: e.transpose(out=pT[0:NC8, gi * 128:(gi + 1) * 128], in_=Yg[:, g, :], identity=ident_b[:]),
                         reads=["Yg", "ident_b"], writes=["pT"], sig=(gi == 7))
                el("act", lambda e, g0=g0: e.copy(out=Ytm[0:NC8, :, 16 * g0:16 * g0 + 128].rearrange("c j (g n) -> c g j n", g=8),
                                                  in_=pT[0:NC8, :].rearrange("c (g j n) -> c g j n", g=8, j=8)), ["pT"], ["Utm"])
            for ct in range(4):
                for j in range(8):
                    P.op("pe", lambda e, ct=ct, j=j: e.transpose(out=pT[:, j * NC8:(j + 1) * NC8], in_=Ytm[0:NC8, j, ct * 128:(ct + 1) * 128],
                                                                 identity=ident_b[0:NC8, 0:NC8]), reads=["Utm", "ident_b"], writes=["pT"], sig=(j == 7))
                el("act", lambda e, ct=ct: e.copy(out=yT[:, ct, :], in_=pT[:, 0:T]), ["pT"], ["yT"])
            for ot_ in range(4):
                ps_, pn = acc_bank()
                for ct in range(4):
                    mm(ps_[:, 0:T], Wglu[:, ct, ot_ * 128:(ot_ + 1) * 128], yT[:, ct, :], ct == 0, ct == 3, ["Wglu", "yT"], [pn])
                el("act", lambda e, ps_=ps_, ot_=ot_: e.activation(out=sgl[:], in_=ps_[:, 0:T], func=AF.Sigmoid, bias=bglu[:, ot_:ot_ + 1]),
                   [pn, "bglu"], ["sgl"])
                tt(xg[:], sgl[:], yT[:, ot_, :], ALU.mult, ["sgl", "yT"], ["xg"])
                tt(hbT[:, ot_, :].rearrange("p (j c) -> p j c", j=8), xg[:].rearrange("p (j c) -> p j c", j=8),
                   zbT[:, ot_, :].rearrange("p (c j) -> p j c", j=8), ALU.mult, ["xg", "zbT"], ["hbT"])
        chk(7)
        for mt in range(8):
            ps_, pn = acc_bank()
            for k in range(4):
                mm(ps_[:, 0:T], cdiag[:, mt, k, :], uaT[:, mt, k + 1:k + 1 + T], k == 0, k == 3, ["cdiag", "uaT"], [pn])
            el("act", lambda e, ps_=ps_, mt=mt: e.activation(out=cT[:, mt, :], in_=ps_[:, 0:T], func=AF.Silu, bias=cb[:, mt:mt + 1]),
               [pn, "cb"], ["cT"])
        if main:
            for half in range(2):
                proj_fm(BLK_ZA + half, 4, lambda ct, ps_, pn, half=half: el(
                    "act", lambda e: e.activation(out=zaT[:, half * 4 + ct, :], in_=ps_[:, 0:T], func=AF.Silu), [pn], ["zaT"]))
            for half in range(2):
                proj_fm(BLK_OA + half, 4, lambda ct, ps_, pn, half=half: el(
                    "act", lambda e: e.activation(out=oaT[:, half * 4 + ct, :], in_=ps_[:, 0:T], func=AF.Sigmoid), [pn], ["oaT"]))
        chk(8)
        for h in range(4):
            if main:
                for (Wx, wxn, dst, dn) in ((Wq, "Wqkv0", qT, "qT"), (Wk, "Wqkv1", kT, "kT")):
                    for et in range(2):
                        ps_, pn = acc_bank()
                        for d in range(2):
                            mm(ps_[:, 0:T], Wx[:, h, d, et * 128:(et + 1) * 128], cT[:, 2 * h + d, :], d == 0, d == 1, [wxn, "cT"], [pn])
                        el("act" if et else "dve",
                           (lambda e, ps_=ps_, dst=dst, h=h, et=et: e.copy(out=dst[:, h, et, :], in_=ps_[:, 0:T])) if et else
                           (lambda e, ps_=ps_, dst=dst, h=h, et=et: e.tensor_copy(out=dst[:, h, et, :], in_=ps_[:, 0:T])), [pn], [dn])
        el("act", lambda e: e.activation(out=e1a[:], in_=gates[:, :, 4:8], func=AF.Exp, scale=-1.0), ["gates"], ["e1a"])
        el("act", lambda e: e.activation(out=lfa[:], in_=e1a[:], func=AF.Ln, bias=1.0), ["e1a"], ["lfa"])
        ts(lfa[:], lfa[:], -1.0, None, ALU.mult, None, ["lfa"], ["lfa"])
        for ch in range(NCH):
            tsl = slice(ch * 128, (ch + 1) * 128)
            for h in range(4):
                el("dve", lambda e, h=h, ch=ch: e.tensor_copy(out=lfrep[:, h, :], in_=lfa[:, ch, h:h + 1].to_broadcast([128, 128])), ["lfa"], ["lfrep"])
            for h in range(4):
                mm(pGD[:, h * 128:(h + 1) * 128], lfrep[:, h, :], maskT[:], True, True, ["lfrep", "maskT"], ["pGD"], sig=(h == 3))
            for h in range(4):
                mm(pM[:, h * 128:(h + 1) * 128], maskT[:], lfrep[:, h, :], True, True, ["maskT", "lfrep"], ["pM"], sig=(h == 3))
            en, dn, wn_, wbn, wrn = "emb%d" % ch, "dec%d" % ch, "wcol%d" % ch, "wcolb%d" % ch, "wrep%d" % ch
            el("act", lambda e, ch=ch: e.activation(out=emb4[:, ch, :, :].rearrange("p h t -> p (h t)"), in_=pGD[:], func=AF.Exp, scale=-1.0), ["pGD"], [en])
            el("dve", lambda e, ch=ch: e.reciprocal(out=dec4[:, ch, :], in_=emb4[:, ch, :, 127]), [en], [dn])
            tt(bcol[:], gates[:, ch, 0:4], pM[:].rearrange("p (h t) -> p h t", h=4)[:, :, 0], ALU.subtract, ["gates", "pM"], ["bcol"])
            el("act", lambda e, ch=ch: e.activation(out=wcol4[:, ch, :], in_=bcol[:], func=AF.Exp), ["bcol"], [wn_])
            el("dve", lambda e, ch=ch: e.tensor_copy(out=wcolb4[:, ch, :, :], in_=wcol4[:, ch, :].unsqueeze(2).to_broadcast([128, 4, 2])), [wn_], [wbn])
            if main:
                for h in range(4):
                    el("dve", lambda e, h=h, ch=ch: e.tensor_copy(out=wrep4[:, ch, h, :], in_=wcol4[:, ch, h:h + 1].to_broadcast([128, 128])), [wn_], [wrn])
            for h in range(4):
                ps_, pn = acc_bank()
                for d in range(2):
                    mm(ps_[:, 0:256], cT[:, 2 * h + d, tsl], Wk[:, h, d, :], d == 0, d == 1, ["cT", "Wqkv1"], [pn], sig=False)
                for d in range(2):
                    mm(ps_[:, 256:512], uaT[:, 2 * h + d, 4 + ch * 128:4 + (ch + 1) * 128], Wv[:, h, d, :], d == 0, d == 1, ["uaT", "Wqkv2"], [pn])
                el("act", lambda e, ps_=ps_, ch=ch, h=h: e.copy(out=ktm[:, ch, h, :], in_=ps_[:, 0:256]), [pn], ["ktm"])
                el("act", lambda e, ps_=ps_, ch=ch, h=h: e.activation(out=vw[:, ch, h, :], in_=ps_[:, 256:512], func=AF.Copy, scale=wcol4[:, ch, h:h + 1]),
                   [pn, wn_], ["vw"])
        for ch in range(NCH):
            tsl = slice(ch * 128, (ch + 1) * 128)
            en, dn, wn_, wbn, wrn = "emb%d" % ch, "dec%d" % ch, "wcol%d" % ch, "wcolb%d" % ch, "wrep%d" % ch
            if main:
                for h in range(4):
                    for et in range(2):
                        mm(pS[:, h * 128:(h + 1) * 128], kT[:, h, et, tsl], qT[:, h, et, tsl], et == 0, et == 1, ["kT", "qT"], ["pS"], sig=(h == 3 and et == 1))
                tt(SmT[:].rearrange("p h t -> p (h t)"), pS[:], mask4[:].rearrange("p h t -> p (h t)"), ALU.mult, ["pS", "mask4"], ["SmT"])
                for h in range(4):
                    for d2 in range(2):
                        pn_, pnn = (pN0, "pN0") if h < 2 else (pN1, "pN1")
                        o_ = pn_[:, ((h % 2) * 2 + d2) * 128:((h % 2) * 2 + d2 + 1) * 128]
                        mm(o_, vw[:, ch, h, d2 * 128:(d2 + 1) * 128], SmT[:, h, :], True, False, ["vw", "SmT"], [pnn], sig=False)
                        mm(o_, Cbf[:, h, 0, d2 * 128:(d2 + 1) * 128], qT[:, h, 0, tsl], False, False, ["Cbf", "qT"], [pnn], sig=False)
                        mm(o_, Cbf[:, h, 1, d2 * 128:(d2 + 1) * 128], qT[:, h, 1, tsl], False, True, ["Cbf", "qT"], [pnn], sig=(h % 2 == 1 and d2 == 1))
                    o_ = pGD[:, h * 128:(h + 1) * 128]
                    mm(o_, wrep4[:, ch, h, :], SmT[:, h, :], True, False, [wrn, "SmT"], ["pGD"], sig=False)
                    mm(o_, nrep[:, h, 0, :], qT[:, h, 0, tsl], False, False, ["nrep", "qT"], ["pGD"], sig=False)
                    mm(o_, nrep[:, h, 1, :], qT[:, h, 1, tsl], False, True, ["nrep", "qT"], ["pGD"], sig=(h == 3))
            for h in range(4):
                for et in range(2):
                    ps_, pn = acc_bank()
                    mm(ps_[:, 0:256], ktm[:, ch, h, et * 128:(et + 1) * 128], vw[:, ch, h, :], True, True, ["ktm", "vw"], [pn], sig=False)
                    mm(ps_[:, 256:258], ktm[:, ch, h, et * 128:(et + 1) * 128], wcolb4[:, ch, h, :], True, True, ["ktm", wbn], [pn])
                    tt(Cst[:, h, et, :], Cst[:, h, et, :], ps_[:, 0:256], ALU.add, ["Cst", pn, "Cbf"], ["Cst"])
                    tt(nst[:, h, et:et + 1], nst[:, h, et:et + 1], ps_[:, 256:257], ALU.add, ["nst", pn, "nrep"], ["nst"])
                el("act", lambda e, h=h, ch=ch: e.activation(out=Cst[:, h, :, :], in_=Cst[:, h, :, :], func=AF.Copy, scale=dec4[:, ch, h:h + 1]), ["Cst", dn], ["Cst"])
                ts(nst[:, h, :], nst[:, h, :], dec4[:, ch, h:h + 1], None, ALU.mult, None, ["nst", dn], ["nst"])
                el("act", lambda e, h=h: e.copy(out=Cbf[:, h, :, :], in_=Cst[:, h, :, :]), ["Cst"], ["Cbf"])
                for et in range(2):
                    el("dve", lambda e, h=h, et=et: e.tensor_copy(out=nrep[:, h, et, :], in_=nst[:, h, et:et + 1].to_broadcast([128, 128])),
                       ["nst"], ["nrep"])
            if main:
                el("act", lambda e: e.activation(out=aden[:].rearrange("p h t -> p (h t)"), in_=pGD[:], func=AF.Abs), ["pGD"], ["aden"])
                tt(aden[:], aden[:], emb4[:, ch, :, :], ALU.max, ["aden", en], ["aden"])
                el("dve", lambda e: e.reciprocal(out=rden[:], in_=aden[:]), ["aden"], ["rden"])
                for hp, (pn_, pnn) in enumerate(((pN0, "pN0"), (pN1, "pN1"))):
                    tt(hT[:, 4 * hp:4 * hp + 4, :].rearrange("p (h d) t -> p h d t", h=2), pn_[:].rearrange("p (h d t) -> p h d t", h=2, d=2),
                       rden[:, 2 * hp:2 * hp + 2, :].unsqueeze(2).to_broadcast([128, 2, 2, 128]), ALU.mult, [pnn, "rden"], ["hT"])
                tt(hT[:], hT[:], oaT[:, :, tsl], ALU.mult, ["hT", "oaT"], ["hT"])
                el("act", lambda e: e.activation(out=sq[:], in_=hT[:], func=AF.Square), ["hT"], ["sq"])
                for h in range(4):
                    for d2 in range(2):
                        mm(pS[:, h * 128:(h + 1) * 128], ones_b[:], sq[:, 2 * h + d2, :], d2 == 0, d2 == 1, ["ones_b", "sq", "SmT"], ["pS"], sig=(h == 3 and d2 == 1))
                ts(rsh[:].rearrange("p h t -> p (h t)"), pS[:], 1.0 / 256.0, 1e-6, ALU.mult, ALU.add, ["pS"], ["rsh"])
                el("act", lambda e: e.activation(out=rsh[:], in_=rsh[:], func=AF.Ln), ["rsh"], ["rsh"])
                el("act", lambda e: e.activation(out=rsh[:], in_=rsh[:], func=AF.Exp, scale=-0.5), ["rsh"], ["rsh"])
                for mt in range(8):
                    el("dve", lambda e, mt=mt: e.scalar_tensor_tensor(out=hn[:, mt, :], in0=hT[:, mt, :], scalar=hg[:, mt:mt + 1],
                                                                      in1=rsh[:, mt // 2, :], op0=ALU.mult, op1=ALU.mult), ["hT", "hg", "rsh"], ["hn"])
                    el("dve", lambda e, mt=mt, tsl=tsl: e.scalar_tensor_tensor(out=hn[:, mt, :], in0=cT[:, mt, tsl], scalar=skp[:, mt:mt + 1],
                                                                               in1=hn[:, mt, :], op0=ALU.mult, op1=ALU.add), ["cT", "skp", "hn"], ["hn"])
                tt(hfT[:, :, tsl], hn[:], zaT[:, :, tsl], ALU.mult, ["hn", "zaT"], ["hfT"])
        chk(9)
        if not main:
            return
        s5_out()
        def gate_blocks(br):
            for bi_ in range(2):
                Wg_, wgn = load_block(BLK_G + 2 * br + bi_)
                Wgv = Wg_[:].rearrange("p (k n) -> p k n", k=8)
                for c4 in range(4):
                    ft = bi_ * 4 + c4
                    psg, png = acc_bank()
                    for kt in range(8):
                        mm(psg[:, 0:T], Wgv[:, kt, c4 * 128:(c4 + 1) * 128], xnT[:, kt, :], kt == 0, kt == 7, [wgn, "xnT"], [png])
                    el("act", lambda e, psg=psg, ft=ft: e.activation(out=gAll[:, ft, :], in_=psg[:, 0:T], func=AF.Sigmoid), [png], ["oaT"])

        gate_blocks(0)
        for hf_ in range(2):
            Wa, wan = load_block(BLK_AO + hf_)
            Wav = Wa[:].rearrange("p (k n) -> p k n", k=8)
            for c4 in range(4):
                ft = hf_ * 4 + c4
                psa, pna = acc_bank()
                for kt in range(8):
                    mm(psa[:, 0:T], Wav[:, kt, c4 * 128:(c4 + 1) * 128], hfT[:, kt, :], kt == 0, kt == 7, [wan, "hfT"], [pna])
                tt(mrgT[:, ft, :], psa[:, 0:T], gAll[:, ft, :], ALU.mult, [pna, "oaT"], ["zaT"])
        gate_blocks(1)
        Wbo, wbon = load_block(BLK_BO)
        Wbv = Wbo[:].rearrange("p (k n) -> p k n", k=4)
        for ft in range(8):
            psb, pnb = acc_bank()
            for kt in range(4):
                mm(psb[:, 0:T], Wbv[:, kt, ft * 128:(ft + 1) * 128], hbT[:, kt, :], kt == 0, kt == 3, [wbon, "hbT"], [pnb])
            el("act", lambda e, psb=psb: e.copy(out=m2[:].rearrange("p (c j) -> p j c", j=8), in_=psb[:, 0:T].rearrange("p (j c) -> p j c", j=8)),
               [pnb], ["m2"])
            tt(m2[:], m2[:], gAll[:, ft, :], ALU.mult, ["m2", "oaT"], ["m2"])
            tt(mrgT[:, ft, :], m2[:], mrgT[:, ft, :], ALU.add, ["m2", "zaT"], ["zaT"])
        Wo0, wo0n = load_block(BLK_WO); Wo1, wo1n = load_block(BLK_WO + 1)
        for ch in range(NCH):
            tsl = slice(ch * 128, (ch + 1) * 128)
            for hf_, (Wo_, won, pn_, pnn) in enumerate(((Wo0, wo0n, pN0, "pN0"), (Wo1, wo1n, pN1, "pN1"))):
                Wov = Wo_[:].rearrange("p (k n) -> p k n", k=8)
                for kt in range(8):
                    mm(pn_[:, :], mrgT[:, kt, tsl], Wov[:, kt, :], kt == 0, kt == 7, ["zaT", won], [pnn])
            el("act", lambda e: e.activation(out=junk[:, 0:512], in_=pN0[:], func=AF.Square, accum_out=ss2[:]), ["pN0"], ["junk", "ss2"])
            el("act", lambda e: e.activation(out=junk[:, 512:1024], in_=pN1[:], func=AF.Square, accum_out=rstd2[:]), ["pN1"], ["junk", "rstd2"])
            tt(ss2[:], ss2[:], rstd2[:], ALU.add, ["ss2", "rstd2"], ["ss2"])
            rsqrt_col(rstd2[:], ss2[:], 1.0 / 1024.0, "ss2", "rstd2")
            ob, obn = ot[ch % 2], "ot%d" % (ch % 2)
            xb_, xnm = xr[ch % 2], "xr%d" % (ch % 2)
            P.dma(lambda e, xb_=xb_, ch=ch: e.dma_start(out=xb_[:], in_=xsrc[row0 + ch * 128: row0 + (ch + 1) * 128, :]), writes=[xnm])
            for hf_, (pn_, pnn) in enumerate(((pN0, "pN0"), (pN1, "pN1"))):
                cs = slice(hf_ * 512, hf_ * 512 + 512)
                el("dve", lambda e, ob=ob, pn_=pn_, cs=cs: e.scalar_tensor_tensor(out=ob[:, cs], in0=pn_[:], scalar=rstd2[:, 0:1], in1=gpost[:, cs],
                                                                                 op0=ALU.mult, op1=ALU.mult), [pnn, "rstd2", "gpost"], [obn])
            tt(ob[:], ob[:], xb_[:], ALU.add, [obn, xnm], [obn])
            tok = P.dma(lambda e, ob=ob, ch=ch: e.dma_start(out=out[row0 + ch * 128: row0 + (ch + 1) * 128, :], in_=ob[:]), reads=[obn])
            out_toks.append(tok)

    sc_t1 = sbt([128, 2, 16], F32, "sc_t1"); sc_t2 = sbt([128, 2, 16], F32, "sc_t2")
    try:
        for ti in range(NPRE):
            nxt = (x_pre, (ti + 1) * T) if ti + 1 < NPRE else (x_main, 0)
            do_tile(x_pre, ti * T, False, ti, nxt)
    except StopBuild:
        tk = P.dma(lambda e: e.dma_start(out=out[0:128, :], in_=gpost[:]), reads=["gpost"])
        P.final_wait("sp", [tk])
        P.emit(); P.close()
        return nc
    if NPRE > 0:
        ts(Cst[:].rearrange("p h e d -> p (h e d)"), Cst[:].rearrange("p h e d -> p (h e d)"), flg[:, 0:1], None, ALU.mult, None, ["Cst", "flg"], ["Cst"])
        ts(nst[:].rearrange("p h e -> p (h e)"), nst[:].rearrange("p h e -> p (h e)"), flg[:, 0:1], None, ALU.mult, None, ["nst", "flg"], ["nst"])
        el("act", lambda e: e.copy(out=Cbf[:].rearrange("p h e d -> p (h e d)"), in_=Cst[:].rearrange("p h e d -> p (h e d)")), ["Cst"], ["Cbf"])
        for h in range(4):
            for et in range(2):
                el("dve", lambda e, h=h, et=et: e.tensor_copy(out=nrep[:, h, et, :], in_=nst[:, h, et:et + 1].to_broadcast([128, 128])),
                   ["nst"], ["nrep"])
    for ti in range(NMAIN):
        nxt = (x_main, (ti + 1) * T) if ti + 1 < NMAIN else None
        do_tile(x_main, ti * T, True, NPRE + ti, nxt)
    P.final_wait("sp", out_toks)
    P.emit()
    P.close()
    return nc


T_TILE = 256
_cache = {}


def kernel(**inputs):
    x = np.ascontiguousarray(inputs["x"], dtype=np.float32)
    Bsz, L, Dm = x.shape
    half = L // 2
    npre = half // T_TILE
    nmain = half // T_TILE
    key = (T_TILE, npre, nmain)
    if key not in _cache:
        _cache[key] = build_program(T_TILE, npre, nmain)
    nc = _cache[key]
    shared = {}
    for k, v in inputs.items():
        if k == "x":
            continue
        a = np.ascontiguousarray(np.asarray(v, dtype=np.float32)[0])
        if k in ("b_i", "b_f", "log_dt", "norm_post_g"):
            a = a.reshape(1, -1)
        shared[k] = a
    in_maps = []
    zeros = np.zeros((half, Dm), np.float32)
    for core in range(8):
        b, hf = core // 2, core % 2
        m = dict(shared)
        m["x_main"] = np.ascontiguousarray(x[b, hf * half:(hf + 1) * half])
        m["x_pre"] = zeros if hf == 0 else np.ascontiguousarray(x[b, 0:half])
        m["flag"] = np.full((128, 1), float(hf), np.float32)
        in_maps.append(m)
    res = run_bass_kernel_spmd(nc, in_maps, core_ids=list(range(8)))
    outp = np.empty((Bsz, L, Dm), np.float32)
    for core in range(8):
        b, hf = core // 2, core % 2
        outp[b, hf * half:(hf + 1) * half] = res.results[core]["out"]
    return outp
```

```python
import contextlib
import numpy as np
import concourse.bass as bass
import concourse.mybir as mybir
from concourse.bass_utils import run_bass_kernel_spmd

F32 = mybir.dt.float32
BF16 = mybir.dt.bfloat16
AF = mybir.ActivationFunctionType
ALU = mybir.AluOpType
COMPUTE = ("pe", "act", "dve", "pool")
NDMA_SLOTS = 8
DEBUG_TAGS = False
PI = float(np.pi)


class Prog:
    def __init__(self, nc):
        self.nc = nc
        self.stack = contextlib.ExitStack()
        self.engs = ("pe", "act", "dve", "pool", "sp")
        self.ops = {e: [] for e in self.engs}
        self.waited = {e: {} for e in self.engs}
        self.res = {}
        self.dma_use = {}
        self.dma_rr = {e: 0 for e in self.engs}
        self.sems = {}
        self.base = {e: 0 for e in COMPUTE}
        self.temp = None

    def sb(self, name, shape, dt):
        st = self.temp if self.temp is not None else self.stack
        return st.enter_context(self.nc.sbuf_tensor(name, list(shape), dt))

    def ps(self, name, shape, dt):
        return self.stack.enter_context(self.nc.psum_tensor(name, list(shape), dt))

    def _sem(self, key):
        if key not in self.sems:
            nm = "s_" + "_".join(str(k) for k in (key if isinstance(key, tuple) else (key,)))
            self.sems[key] = self.stack.enter_context(self.nc.semaphore(nm))
        return self.sems[key]

    def _deps(self, eng, reads, writes):
        deps = {}

        def add(tok):
            if tok is None:
                return
            k, v = tok
            if k == "pe" and eng == "pe":
                return
            if deps.get(k, -1) < v:
                deps[k] = v

        for r in reads:
            st = self.res.get(r)
            if st:
                for k, v in st[0].items():
                    add((k, v))
        for w in writes:
            st = self.res.get(w)
            if st:
                for k, v in st[0].items():
                    add((k, v))
                for k, v in st[1].items():
                    add((k, v))
        out = []
        wd = self.waited[eng]
        for k, v in deps.items():
            if wd.get(k, -1) >= v:
                continue
            wd[k] = v
            out.append((k, v))
        return out

    def _commit(self, tok, reads, writes):
        k, v = tok
        for r in reads:
            st = self.res.setdefault(r, [{}, {}])
            if st[1].get(k, -1) < v:
                st[1][k] = v
        for w in writes:
            old = self.res.get(w)
            wr = {}
            if old is not None and k not in COMPUTE:
                wr = {k2: v2 for k2, v2 in old[0].items() if k2 not in COMPUTE}
            wr[k] = v
            self.res[w] = [wr, {}]

    def _tag(self):
        if not DEBUG_TAGS:
            return None
        import sys as _sys
        f = _sys._getframe(2)
        while f is not None and f.f_code.co_name not in ("do_tile", "build_program", "s5_out", "gate_blocks", "proj_fm"):
            f = f.f_back
        return str(f.f_lineno) if f is not None else None

    def op(self, eng, fn, reads=(), writes=(), sig=True):
        waits = self._deps(eng, reads, writes)
        idx = len(self.ops[eng])
        self.ops[eng].append(dict(fn=fn, waits=waits, sig=sig, dma=None, tag=self._tag()))
        tok = (eng, idx)
        self._commit(tok, reads, writes)
        return tok

    def dma(self, fn, reads=(), writes=(), q="sp"):
        waits = self._deps(q, reads, writes)
        slot = self.dma_rr[q] % NDMA_SLOTS
        self.dma_rr[q] += 1
        key = ("d", q, slot)
        n = self.dma_use.get(key, 0)
        if n > 0:
            prev = n * 16
            if self.waited[q].get(key, -1) < prev:
                self.waited[q][key] = prev
                waits.append((key, prev))
        self.dma_use[key] = n + 1
        tok = (key, (n + 1) * 16)
        self.ops[q].append(dict(fn=fn, waits=waits, sig=False, dma=key))
        self._commit(tok, reads, writes)
        return tok

    def final_wait(self, eng, toks):
        self.ops[eng].append(dict(fn=None, waits=list(toks), sig=False, dma=None))

    def emit(self):
        nc = self.nc
        sigcount = {}
        totals = {}
        for e in COMPUTE:
            c = self.base[e]
            arr = []
            for o in self.ops[e]:
                if o["sig"]:
                    c += 1
                arr.append(c)
            need = [None] * len(arr)
            nxt = None
            for i in range(len(arr) - 1, -1, -1):
                if self.ops[e][i]["sig"]:
                    nxt = arr[i]
                need[i] = nxt
            sigcount[e] = need
            totals[e] = c
            self._sem(e)
        for k in self.dma_use:
            self._sem(k)

        def resolve(k, v):
            if k in COMPUTE:
                val = sigcount[k][v]
                assert val is not None, (k, v)
                return self.sems[k], val
            return self.sems[k], v

        with nc.Block() as block:

            def run(eng_name, eng):
                for o in self.ops[eng_name]:
                    for k, v in o["waits"]:
                        s, val = resolve(k, v)
                        eng.wait_ge(s, val)
                    if o["fn"] is None:
                        continue
                    ins = o["fn"](eng)
                    if o.get("tag"):
                        ins.annotate(o["tag"])
                    if o["dma"] is not None:
                        ins.then_inc(self.sems[o["dma"]], 16)
                    elif o["sig"]:
                        ins.then_inc(self.sems[eng_name], 1)
                for o2 in COMPUTE:
                    if o2 != eng_name and totals[o2] > 0:
                        eng.wait_ge(self.sems[o2], totals[o2])
                for k, n in self.dma_use.items():
                    eng.wait_ge(self.sems[k], n * 16)

            @block.tensor
            def _(e):
                run("pe", e)

            @block.scalar
            def _(e):
                run("act", e)

            @block.vector
            def _(e):
                run("dve", e)

            @block.gpsimd
            def _(e):
                run("pool", e)

            @block.sync
            def _(e):
                run("sp", e)

        self.base = totals
        self.ops = {e: [] for e in self.engs}
        self.waited = {e: {} for e in self.engs}
        self.res = {}

    def close(self):
        self.stack.close()


NBLK = 22
BLK_UA, BLK_UB, BLK_ZB, BLK_ZA, BLK_OA, BLK_G, BLK_AO, BLK_BO, BLK_WO, BLK_S5 = 0, 2, 3, 4, 6, 8, 12, 14, 15, 17
COL_UA, COL_ZA, COL_OA, COL_I, COL_UB, COL_ZB, COL_G = 0, 1024, 2048, 3072, 3080, 3592, 4104


def build_program(T, NPRE, NMAIN, dbg_stop=0):
    NCH = T // 128
    NC8 = T // 8
    nc = bass.Bass("TRN2", target_bir_lowering=False)
    dram = {}

    def din(name, shape):
        dram[name] = nc.dram_tensor(name, list(shape), F32, kind="ExternalInput").ap()
        return dram[name]

    x_pre = din("x_pre", [max(NPRE, 1) * T, 1024])
    x_main = din("x_main", [NMAIN * T, 1024])
    flag = din("flag", [128, 1])
    norm_pre_g = din("norm_pre_g", [1024]); w_in = din("w_in", [1024, 6152])
    conv_w = din("conv_w", [4, 1024]); conv_b = din("conv_b", [1024])
    w_q = din("w_q", [4, 256, 256]); w_k = din("w_k", [4, 256, 256]); w_v = din("w_v", [4, 256, 256])
    b_i = din("b_i", [1, 4]); b_f = din("b_f", [1, 4]); head_g = din("head_g", [1024]); skip_a = din("skip_a", [1024])
    w_a_out = din("w_a_out", [1024, 1024])
    lam_re = din("lam_re", [32, 64]); lam_im = din("lam_im", [32, 64]); log_dt = din("log_dt", [1, 32])
    B_re = din("B_re", [32, 64, 16]); B_im = din("B_im", [32, 64, 16])
    C_re = din("C_re", [32, 16, 64]); C_im = din("C_im", [32, 16, 64]); D_skip = din("D_skip", [32, 16])
    w_glu = din("w_glu", [512, 512]); b_glu = din("b_glu", [512]); w_b_out = din("w_b_out", [512, 1024])
    w_o = din("w_o", [1024, 1024]); norm_post_g = din("norm_post_g", [1, 1024])
    out = nc.dram_tensor("out", [NMAIN * T, 1024], F32, kind="ExternalOutput").ap()
    WS = nc.dram_tensor("wscratch", [NBLK, 128, 4096], BF16, kind="Internal").ap()

    P = Prog(nc)
    uid = [0]

    def sbt(shape, dt, name=None):
        uid[0] += 1
        return P.sb(name or ("t%d" % uid[0]), shape, dt)

    ident_f = sbt([128, 128], F32); ident_b = sbt([128, 128], BF16)
    maskT = sbt([128, 128], F32); mask4 = sbt([128, 4, 128], F32); ones_b = sbt([128, 128], BF16)
    P.op("pool", lambda e: e.memset(ident_f[:], 1.0), writes=["ident_f"])
    P.op("pool", lambda e: e.affine_select(out=ident_f[:], in_=ident_f[:], pattern=[[-1, 128]], compare_op=ALU.is_equal,
                                           fill=0.0, base=0, channel_multiplier=1), reads=["ident_f"], writes=["ident_f"])
    P.op("dve", lambda e: e.tensor_copy(out=ident_b[:], in_=ident_f[:]), reads=["ident_f"], writes=["ident_b"])
    P.op("pool", lambda e: e.memset(maskT[:], 1.0), writes=["maskT"])
    P.op("pool", lambda e: e.affine_select(out=maskT[:], in_=maskT[:], pattern=[[1, 128]], compare_op=ALU.is_ge,
                                           fill=0.0, base=0, channel_multiplier=-1), reads=["maskT"], writes=["maskT"])
    for h in range(4):
        P.op("pool", lambda e, h=h: e.tensor_copy(out=mask4[:, h, :], in_=maskT[:]), reads=["maskT"], writes=["mask4"])
    P.op("pool", lambda e: e.memset(ones_b[:], 1.0), writes=["ones_b"])

    gpre = sbt([128, 8], F32); cb = sbt([128, 8], F32); hg = sbt([128, 8], F32); skp = sbt([128, 8], F32)
    cw = sbt([128, 8, 4], F32); bglu = sbt([128, 4], F32); gpost = sbt([128, 1024], F32); bif = sbt([128, 8], F32)
    flg = sbt([128, 1], F32)
    nonc = dict(allow_slow_non_contiguous=True)
    P.dma(lambda e: e.dma_start(out=gpre[:], in_=norm_pre_g.rearrange("(k p) -> p k", p=128), **nonc), writes=["gpre"])
    P.dma(lambda e: e.dma_start(out=cb[:], in_=conv_b.rearrange("(k p) -> p k", p=128), **nonc), writes=["cb"])
    P.dma(lambda e: e.dma_start(out=hg[:], in_=head_g.rearrange("(k p) -> p k", p=128), **nonc), writes=["hg"])
    P.dma(lambda e: e.dma_start(out=skp[:], in_=skip_a.rearrange("(k p) -> p k", p=128), **nonc), writes=["skp"])
    for k in range(4):
        P.dma(lambda e, k=k: e.dma_start(out=cw[:, :, k], in_=conv_w[k].rearrange("(m p) -> p m", p=128), **nonc), writes=["cw"])
    P.dma(lambda e: e.dma_start(out=bglu[:], in_=b_glu.rearrange("(k p) -> p k", p=128), **nonc), writes=["bglu"])
    P.dma(lambda e: e.dma_start(out=gpost[:], in_=norm_post_g.partition_broadcast(128)), writes=["gpost"])
    P.dma(lambda e: e.dma_start(out=bif[:, 0:4], in_=b_i.partition_broadcast(128)), writes=["bif"])
    P.dma(lambda e: e.dma_start(out=bif[:, 4:8], in_=b_f.partition_broadcast(128)), writes=["bif"])
    P.dma(lambda e: e.dma_start(out=flg[:], in_=flag), writes=["flg"])

    Wqkv = [sbt([128, 4, 2, 256], BF16) for _ in range(3)]
    Wif = sbt([128, 8, 8], BF16); Wglu = sbt([128, 4, 512], BF16); cdiag = sbt([128, 8, 4, 128], BF16)
    AR32 = sbt([128, 2, 16], F32); ANI = sbt([128, 16], F32); API = sbt([128, 16], F32)
    pA = P.ps("pA", [128, 512], F32); pB = P.ps("pB", [128, 512], F32)
    pT = P.ps("pT", [128, 1024], BF16); pGD = P.ps("pGD", [128, 512], F32)
    pS = P.ps("pS", [128, 512], F32); pN0 = P.ps("pN0", [128, 512], F32); pN1 = P.ps("pN1", [128, 512], F32)
    pM = P.ps("pM", [128, 512], F32)
    P.temp = contextlib.ExitStack()
    stg = sbt([128, 4096], F32, "stg")
    stgb = sbt([128, 4096], BF16, "stgb")
    for wi, (wsrc, scl) in enumerate(((w_q, 1.0), (w_k, 1.0 / 16.0), (w_v, 1.0))):
        P.dma(lambda e, wsrc=wsrc: e.dma_start(out=stg[:, 0:2048].rearrange("p (h d n) -> p h d n", h=4, d=2),
                                               in_=wsrc.rearrange("h (d p) n -> p h d n", p=128)), writes=["stg"])
        P.op("dve", lambda e, wi=wi, scl=scl: e.tensor_scalar(out=Wqkv[wi][:].rearrange("p h d n -> p (h d n)"), in0=stg[:, 0:2048],
                                                              scalar1=scl, scalar2=None, op0=ALU.mult), reads=["stg"], writes=["Wqkv%d" % wi])
    Wq, Wk, Wv = Wqkv
    P.dma(lambda e: e.dma_start(out=stg[:, 0:64].rearrange("p (k n) -> p k n", k=8),
                                in_=w_in[:, COL_I:COL_I + 8].rearrange("(k p) n -> p k n", p=128), **nonc), writes=["stg"])
    for kt in range(8):
        P.op("dve", lambda e, kt=kt: e.tensor_scalar(out=Wif[:, kt, :], in0=stg[:, kt * 8:(kt + 1) * 8], scalar1=gpre[:, kt:kt + 1],
                                                     scalar2=None, op0=ALU.mult), reads=["stg", "gpre"], writes=["Wif"])
    P.dma(lambda e: e.dma_start(out=stg[:, 0:2048].rearrange("p (k n) -> p k n", k=4),
                                in_=w_glu.rearrange("(k p) n -> p k n", p=128)), writes=["stg"])
    P.op("dve", lambda e: e.tensor_copy(out=Wglu[:].rearrange("p k n -> p (k n)"), in_=stg[:, 0:2048]), reads=["stg"], writes=["Wglu"])
    for mt in range(8):
        for k in range(4):
            P.op("dve", lambda e, mt=mt, k=k: e.tensor_scalar(out=cdiag[:, mt, k, :], in0=ident_f[:], scalar1=cw[:, mt, k:k + 1],
                                                               scalar2=None, op0=ALU.mult), reads=["ident_f", "cw"], writes=["cdiag"])

    def stage_block(blk, src_ap_f, scale_gpre, nk):
        ncol = 4096 // nk
        P.dma(lambda e: e.dma_start(out=stg[:].rearrange("p (k n) -> p k n", k=nk), in_=src_ap_f), writes=["stg"])
        if scale_gpre:
            for kt in range(nk):
                P.op("dve",
                     lambda e, kt=kt: e.tensor_scalar(out=stgb[:, kt * ncol:(kt + 1) * ncol], in0=stg[:, kt * ncol:(kt + 1) * ncol],
                                                      scalar1=gpre[:, kt:kt + 1], scalar2=None, op0=ALU.mult),
                     reads=["stg", "gpre"], writes=["stgb"])
        else:
            P.op("dve", lambda e: e.tensor_copy(out=stgb[:, 0:2048], in_=stg[:, 0:2048]), reads=["stg"], writes=["stgb"])
            P.op("act", lambda e: e.copy(out=stgb[:, 2048:4096], in_=stg[:, 2048:4096]), reads=["stg"], writes=["stgb"])
        P.dma(lambda e: e.dma_start(out=WS[blk], in_=stgb[:]), reads=["stgb"], writes=["WS%d" % blk])

    def win_cols(c0):
        return w_in[:, c0:c0 + 512].rearrange("(k p) n -> p k n", p=128)

    win_blocks = [(BLK_UA, COL_UA), (BLK_UA + 1, COL_UA + 512), (BLK_UB, COL_UB), (BLK_ZB, COL_ZB), (BLK_ZA, COL_ZA),
                  (BLK_ZA + 1, COL_ZA + 512), (BLK_OA, COL_OA), (BLK_OA + 1, COL_OA + 512)] + [(BLK_G + i, COL_G + 512 * i) for i in range(4)]
    pending = []
    for blk, c0 in win_blocks:
        pending.append((blk, win_cols(c0), True, 8))
    for i in range(2):
        pending.append((BLK_AO + i, w_a_out[:, 512 * i:512 * i + 512].rearrange("(k p) n -> p k n", p=128), False, 8))
        pending.append((BLK_WO + i, w_o[:, 512 * i:512 * i + 512].rearrange("(k p) n -> p k n", p=128), False, 8))
    pending.append((BLK_BO, w_b_out.rearrange("(k p) n -> p k n", p=128), False, 4))

    def stage_some(n=1):
        for _ in range(n):
            if pending:
                stage_block(*pending.pop(0))

    stage_some(2)

    def small(n, name=None):
        return sbt([128, n], F32, name)

    cnt = [0]

    def el(eng, fn, r, w):
        P.op(eng, fn, reads=r, writes=w)

    def tt(outp, a, b, op, r, w, eng="dve"):
        el(eng, lambda e: e.tensor_tensor(out=outp, in0=a, in1=b, op=op), r, w)

    def ts(outp, a, s1, s2, op0, op1, r, w, eng="dve"):
        if op1 is None:
            el(eng, lambda e: e.tensor_scalar(out=outp, in0=a, scalar1=s1, scalar2=None, op0=op0), r, w)
        else:
            el(eng, lambda e: e.tensor_scalar(out=outp, in0=a, scalar1=s1, scalar2=s2, op0=op0, op1=op1), r, w)

    LR = small(32); LI = small(32); DT = small(32)
    for hf in range(2):
        sl = slice(64 * hf, 64 * hf + 64)
        P.dma(lambda e, sl=sl: e.dma_start(out=LR[sl, :], in_=lam_re.rearrange("g p -> p g"), **nonc), writes=["LR"])
        P.dma(lambda e, sl=sl: e.dma_start(out=LI[sl, :], in_=lam_im.rearrange("g p -> p g"), **nonc), writes=["LI"])
    P.dma(lambda e: e.dma_start(out=DT[:], in_=log_dt.partition_broadcast(128)), writes=["DT"])
    el("act", lambda e: e.activation(out=DT[:], in_=DT[:], func=AF.Exp), ["DT"], ["DT"])
    TH = small(32); MAG = small(32); t0 = small(32); t1 = small(32); t2 = small(32); kk = small(32)
    tt(TH[:], LI[:], DT[:], ALU.mult, ["LI", "DT"], ["TH"])
    tt(t0[:], LR[:], DT[:], ALU.mult, ["LR", "DT"], ["t0"])
    el("act", lambda e: e.activation(out=MAG[:], in_=t0[:], func=AF.Exp), ["t0"], ["MAG"])
    IMAG2 = small(32)
    el("act", lambda e: e.activation(out=IMAG2[:], in_=t0[:], func=AF.Exp, scale=-2.0), ["t0"], ["IMAG2"])

    def sin_of(dst, src, shift, nm):
        ts(t1[:], src, shift, None, ALU.add, None, [nm, "t1"], ["t1"])
        el("pool", lambda e: e.memset(kk[:], 0.0), [], ["kk"])
        for m in range(7):
            ts(t2[:], t1[:], (2 * m + 1) * PI, None, ALU.is_gt, None, ["t1"], ["t2"])
            tt(kk[:], kk[:], t2[:], ALU.add, ["kk", "t2"], ["kk"])
        ts(kk[:], kk[:], -2.0 * PI, None, ALU.mult, None, ["kk"], ["kk"])
        tt(t1[:], t1[:], kk[:], ALU.add, ["t1", "kk"], ["t1"])
        el("act", lambda e: e.activation(out=dst, in_=t1[:], func=AF.Sin), ["t1"], [nm + "_s"])

    SN = small(32); CS = small(32)
    sin_of(SN[:], TH[:], 0.0, "TH")
    sin_of(CS[:], TH[:], PI / 2.0, "TH")
    pwr = sbt([128, 9, 32], F32); pwi = sbt([128, 9, 32], F32); pnr = sbt([128, 8, 32], F32); pni = sbt([128, 8, 32], F32)
    el("pool", lambda e: e.memset(pwr[:, 0, :], 1.0), [], ["pw"]); el("pool", lambda e: e.memset(pwi[:, 0, :], 0.0), [], ["pw"])
    el("pool", lambda e: e.memset(pnr[:, 0, :], 1.0), [], ["pn"]); el("pool", lambda e: e.memset(pni[:, 0, :], 0.0), [], ["pn"])
    tt(pwr[:, 1, :], MAG[:], CS[:], ALU.mult, ["MAG", "TH_s"], ["pw"])
    tt(pwi[:, 1, :], MAG[:], SN[:], ALU.mult, ["MAG", "TH_s"], ["pw"])
    tt(pnr[:, 1, :], pwr[:, 1, :], IMAG2[:], ALU.mult, ["pw", "IMAG2"], ["pn"])
    tt(t0[:], pwi[:, 1, :], IMAG2[:], ALU.mult, ["pw", "IMAG2"], ["t0"])
    ts(pni[:, 1, :], t0[:], -1.0, None, ALU.mult, None, ["t0"], ["pn"])

    def cmul(or_, oi_, ar, ai, br, bi, r, w):
        raise NotImplementedError

    u0 = small(32); u1 = small(32)
    for k in range(1, 8):
        for (xr, xi, nm, lim) in ((pwr, pwi, "pw", 9), (pnr, pni, "pn", 8)):
            if k + 1 >= lim:
                continue
            tt(u0[:], xr[:, k, :], xr[:, 1, :], ALU.mult, [nm], ["u0"])
            tt(u1[:], xi[:, k, :], xi[:, 1, :], ALU.mult, [nm], ["u1"])
            tt(xr[:, k + 1, :], u0[:], u1[:], ALU.subtract, ["u0", "u1"], [nm])
            tt(u0[:], xr[:, k, :], xi[:, 1, :], ALU.mult, [nm], ["u0"])
            tt(u1[:], xi[:, k, :], xr[:, 1, :], ALU.mult, [nm], ["u1"])
            tt(xi[:, k + 1, :], u0[:], u1[:], ALU.add, ["u0", "u1"], [nm])
    den = small(32); qr = small(32); qi = small(32); nr = small(32)
    tt(u0[:], LR[:], LR[:], ALU.mult, ["LR"], ["u0"]); tt(u1[:], LI[:], LI[:], ALU.mult, ["LI"], ["u1"])
    tt(den[:], u0[:], u1[:], ALU.add, ["u0", "u1"], ["den"])
    el("dve", lambda e: e.reciprocal(out=den[:], in_=den[:]), ["den"], ["den"])
    ts(nr[:], pwr[:, 1, :], -1.0, None, ALU.add, None, ["pw"], ["nr"])
    tt(u0[:], nr[:], LR[:], ALU.mult, ["nr", "LR"], ["u0"]); tt(u1[:], pwi[:, 1, :], LI[:], ALU.mult, ["pw", "LI"], ["u1"])
    tt(qr[:], u0[:], u1[:], ALU.add, ["u0", "u1"], ["qr"]); tt(qr[:], qr[:], den[:], ALU.mult, ["qr", "den"], ["qr"])
    tt(u0[:], pwi[:, 1, :], LR[:], ALU.mult, ["pw", "LR"], ["u0"]); tt(u1[:], nr[:], LI[:], ALU.mult, ["nr", "LI"], ["u1"])
    tt(qi[:], u0[:], u1[:], ALU.subtract, ["u0", "u1"], ["qi"]); tt(qi[:], qi[:], den[:], ALU.mult, ["qi", "den"], ["qi"])
    Br = sbt([128, 32, 16], F32); Bi = sbt([128, 32, 16], F32); bbr = sbt([128, 32, 16], F32); bbi = sbt([128, 32, 16], F32)
    v0 = sbt([128, 32, 16], F32); v1 = sbt([128, 32, 16], F32)
    for hf in range(2):
        sl = slice(64 * hf, 64 * hf + 64)
        P.dma(lambda e, sl=sl: e.dma_start(out=Br[sl], in_=B_re.rearrange("g p n -> p g n"), **nonc), writes=["Br"])
        P.dma(lambda e, sl=sl: e.dma_start(out=Bi[sl], in_=B_im.rearrange("g p n -> p g n"), **nonc), writes=["Bi"])

    def bc(s):
        return s.unsqueeze(2).to_broadcast([128, 32, 16])

    def cmul3(orr, oii, sr, si, sn, xr, xi, xn, on):
        xn = [xn] if isinstance(xn, str) else list(xn)
        sn = [sn] if isinstance(sn, str) else list(sn)
        stage_some(1)
        tt(v0[:], xr, bc(sr), ALU.mult, xn + sn, ["v0"]); tt(v1[:], xi, bc(si), ALU.mult, xn + sn, ["v1"])
        tt(orr, v0[:], v1[:], ALU.subtract, ["v0", "v1"], [on])
        tt(v0[:], xi, bc(sr), ALU.mult, xn + sn, ["v0"]); tt(v1[:], xr, bc(si), ALU.mult, xn + sn, ["v1"])
        tt(oii, v0[:], v1[:], ALU.add, ["v0", "v1"], [on])

    el("dve", lambda e: e.tensor_copy(out=u0[:], in_=qr[:]), ["qr"], ["qq"])
    cmul3(bbr[:], bbi[:], qr[:], qi[:], ["qr", "qi"], Br[:], Bi[:], ["Br", "Bi"], "bb")
    CTr = sbt([128, 32, 16], F32); CTi = sbt([128, 32, 16], F32)
    Cdup = sbt([128, 4, 2, 64], F32)
    for (Csrc, CTt, nm) in ((C_re, CTr, "CTr"), (C_im, CTi, "CTi")):
        for d in range(2):
            P.dma(lambda e, Csrc=Csrc, d=d: e.dma_start(out=Cdup[:, :, d, :], in_=Csrc.rearrange("(t g) n p -> (g n) t p", t=4)),
                  writes=["Cdup"])
        for t in range(4):
            P.op("pe", lambda e, t=t: e.transpose(out=pA[:, t * 128:(t + 1) * 128], in_=Cdup[:, t, :, :].rearrange("q d p -> q (d p)"),
                                                  identity=ident_f[:]), reads=["Cdup", "ident_f"], writes=["pA"])
        el("act", lambda e, CTt=CTt: e.copy(out=CTt[:].rearrange("p g n -> p (g n)"), in_=pA[:]), ["pA"], [nm])
    Er = sbt([128, 32, 8, 16], F32); Ei = sbt([128, 32, 8, 16], F32)
    Fr = sbt([128, 32, 8, 16], F32); Fi = sbt([128, 32, 8, 16], F32)
    for j in range(8):
        cmul3(Er[:, :, j, :], Ei[:, :, j, :], pwr[:, 7 - j, :], pwi[:, 7 - j, :], "pw", bbr[:], bbi[:], "bb", "E")
        cmul3(Fr[:, :, j, :], Fi[:, :, j, :], pwr[:, j + 1, :], pwi[:, j + 1, :], "pw", CTr[:], CTi[:], ["CTr", "CTi"], "F")
    halfm = sbt([128, 2], F32)
    el("pool", lambda e: e.memset(halfm[:], 0.0), [], ["halfm"])
    el("pool", lambda e: e.memset(halfm[0:64, 0:1], 1.0), ["halfm"], ["halfm"])
    el("pool", lambda e: e.memset(halfm[64:128, 1:2], 1.0), ["halfm"], ["halfm"])
    for ri, (Et, blk) in enumerate(((Er, BLK_S5), (Ei, BLK_S5 + 1))):
        el("pool", lambda e: e.memset(stgb[:], 0.0), ["stgb"], ["stgb"])
        for g in range(32):
            ps_ = pA if g % 2 == 0 else pB
            nm = "pA" if g % 2 == 0 else "pB"
            P.op("pe", lambda e, Et=Et, g=g, ps_=ps_: e.transpose(out=ps_[:, 0:64], in_=Et[0:64, g, :, :].rearrange("p j n -> p (j n)"),
                                                                  identity=ident_f[0:64, 0:64]), reads=["E", "ident_f"], writes=[nm])
            c0 = g * 128 + 64 * (g % 2)
            el("act" if g % 2 == 0 else "dve",
               (lambda e, ps_=ps_, c0=c0: e.copy(out=stgb[:, c0:c0 + 64], in_=ps_[:, 0:64])) if g % 2 == 0 else
               (lambda e, ps_=ps_, c0=c0: e.tensor_copy(out=stgb[:, c0:c0 + 64], in_=ps_[:, 0:64])), [nm], ["stgb"])
        P.dma(lambda e, blk=blk: e.dma_start(out=WS[blk], in_=stgb[:]), reads=["stgb"], writes=["WS%d" % blk])
    for ri, (Ft, blk, sg) in enumerate(((Fr, BLK_S5 + 3, 1.0), (Fi, BLK_S5 + 4, -1.0))):
        for g in range(32):
            P.op("dve",
                 lambda e, Ft=Ft, g=g, sg=sg: e.tensor_scalar(out=stgb[:, g * 128:(g + 1) * 128], in0=Ft[:, g, :, :].rearrange("p j n -> p (j n)"),
                                                              scalar1=halfm[:, (g % 2):(g % 2) + 1], scalar2=sg, op0=ALU.mult, op1=ALU.mult),
                 reads=["F", "halfm"], writes=["stgb"])
        P.dma(lambda e, blk=blk: e.dma_start(out=WS[blk], in_=stgb[:]), reads=["stgb"], writes=["WS%d" % blk])
    Gr = sbt([128, 32, 8, 16], F32); Gni = sbt([128, 32, 8, 16], F32)
    i8r = small(32); i8i = small(32)
    tt(u0[:], pnr[:, 7, :], pnr[:, 1, :], ALU.mult, ["pn"], ["u0"]); tt(u1[:], pni[:, 7, :], pni[:, 1, :], ALU.mult, ["pn"], ["u1"])
    tt(i8r[:], u0[:], u1[:], ALU.subtract, ["u0", "u1"], ["i8"])
    tt(u0[:], pnr[:, 7, :], pni[:, 1, :], ALU.mult, ["pn"], ["u0"]); tt(u1[:], pni[:, 7, :], pnr[:, 1, :], ALU.mult, ["pn"], ["u1"])
    tt(i8i[:], u0[:], u1[:], ALU.add, ["u0", "u1"], ["i8"])
    for j in range(8):
        cmul3(Gr[:, :, j, :], Gni[:, :, j, :], i8r[:], i8i[:], "i8", Fr[:, :, j, :], Fi[:, :, j, :], "F", "G")
    ts(Gni[:], Gni[:], -1.0, None, ALU.mult, None, ["G"], ["G"])
    bmask = sbt([128, 8, 16], F32); Dcol = sbt([128, 32], F32)
    el("pool", lambda e: e.memset(bmask[:], 1.0), [], ["bmask"])
    el("pool", lambda e: e.affine_select(out=bmask[:], in_=bmask[:], pattern=[[16, 8], [0, 16]], compare_op=ALU.is_ge, fill=0.0,
                                         base=15, channel_multiplier=-1), ["bmask"], ["bmask"])
    for j in range(8):
        P.dma(lambda e, j=j: e.dma_start(out=Dcol[16 * j:16 * j + 16, :], in_=D_skip.rearrange("g n -> n g"), **nonc), writes=["Dcol"])
    wtmp = sbt([128, 128], F32)
    for g in range(32):
        ps_ = pA if g % 2 == 0 else pB
        nm = "pA" if g % 2 == 0 else "pB"
        P.op("pe", lambda e, g=g, ps_=ps_: e.matmul(ps_[:, 0:128], lhsT=Er[0:64, g, :, :].rearrange("p j n -> p (j n)"),
                                                    rhs=Gr[0:64, g, :, :].rearrange("p j n -> p (j n)"), start=True, stop=False),
             reads=["E", "G"], writes=[nm], sig=False)
        P.op("pe", lambda e, g=g, ps_=ps_: e.matmul(ps_[:, 0:128], lhsT=Ei[0:64, g, :, :].rearrange("p j n -> p (j n)"),
                                                    rhs=Gni[0:64, g, :, :].rearrange("p j n -> p (j n)"), start=False, stop=True),
             reads=["E", "G"], writes=[nm])
        tt(wtmp[:], ps_[:, 0:128], bmask[:].rearrange("p j n -> p (j n)"), ALU.mult, [nm, "bmask"], ["wtmp"])
        el("dve", lambda e, g=g: e.scalar_tensor_tensor(out=stgb[:, g * 128:(g + 1) * 128], in0=ident_f[:], scalar=Dcol[:, g:g + 1],
                                                        in1=wtmp[:], op0=ALU.mult, op1=ALU.add), ["ident_f", "Dcol", "wtmp", "stgb"], ["stgb"])
    P.dma(lambda e: e.dma_start(out=WS[BLK_S5 + 2], in_=stgb[:]), reads=["stgb"], writes=["WS%d" % (BLK_S5 + 2)])
    for g2 in range(2):
        sl = slice(64 * g2, 64 * g2 + 64)
        src_r = pwr[sl, 8, :].rearrange("p (q t) -> p q t", t=2)[:, :, g2]
        src_i = pwi[sl, 8, :].rearrange("p (q t) -> p q t", t=2)[:, :, g2]
        el("dve", lambda e, sl=sl, src_r=src_r: e.tensor_copy(out=AR32[sl, 0, :], in_=src_r), ["pw"], ["AR32"])
        el("dve", lambda e, sl=sl, src_r=src_r: e.tensor_copy(out=AR32[sl, 1, :], in_=src_r), ["pw"], ["AR32"])
        el("dve", lambda e, sl=sl, src_i=src_i: e.tensor_copy(out=API[sl, :], in_=src_i), ["pw"], ["API"])
        ts(ANI[sl, :], src_i, -1.0, None, ALU.mult, None, ["pw"], ["ANI"])

    stage_some(100)
    if dbg_stop == 1:
        tk = P.dma(lambda e: e.dma_start(out=out[0:128, :], in_=gpost[:]), reads=["gpost"])
        P.final_wait("sp", [tk])
        P.emit(); P.temp.close(); P.close()
        return nc
    P.emit()
    P.temp.close()
    P.temp = None
    NSLOT = 3
    wslot = [sbt([128, 4096], BF16, "wslot%d" % i) for i in range(NSLOT)]
    slot_rr = [0]

    def load_block(blk):
        s = slot_rr[0] % NSLOT
        slot_rr[0] += 1
        P.dma(lambda e: e.dma_start(out=wslot[s][:], in_=WS[blk]), reads=["WS%d" % blk], writes=["wslot%d" % s])
        return wslot[s], "wslot%d" % s

    xs = [sbt([128, 1024], F32, "xs%d" % i) for i in range(2)]
    xr = [sbt([128, 1024], F32, "xr%d" % i) for i in range(2)]
    junk = sbt([128, 1024], BF16); xnb = sbt([128, 1024], BF16)
    ss = small(1); rstd = small(1)
    xnT = sbt([128, 8, T], BF16, "xnT"); uaT = sbt([128, 8, T + 4], BF16, "uaT"); cT = sbt([128, 8, T], BF16, "cT")
    zaT = sbt([128, 8, T], BF16, "zaT"); oaT = sbt([128, 8, T], BF16, "oaT"); zbT = sbt([128, 4, T], BF16, "zbT")
    qT = sbt([128, 4, 2, T], BF16, "qT"); kT = sbt([128, 4, 2, T], BF16, "kT")
    ktm = sbt([128, NCH, 4, 256], BF16, "ktm"); vw = sbt([128, NCH, 4, 264], BF16, "vw")
    gates = sbt([128, NCH, 8], F32, "gates")
    hfT = sbt([128, 8, T], BF16, "hfT"); csz = sbt([128, 8, T], BF16, "csz"); mrgT = zaT; hbT = sbt([128, 4, T], BF16, "hbT")
    Cst = sbt([128, 4, 2, 264], F32, "Cst"); Cbf = sbt([128, 4, 2, 264], BF16, "Cbf")
    nst = sbt([128, 4, 2], F32, "nst"); nrep = sbt([128, 4, 2, 128], BF16, "nrep")
    Utm = sbt([128, 32, 8, 16], BF16, "Utm"); U2 = sbt([128, 32, NC8], BF16, "U2")
    Xall = sbt([128, NC8, 2, 16], F32, "Xall"); Sall = sbt([128, NC8 + 1, 2, 16], F32, "Sall"); Sbf = sbt([128, 2, 16, NC8], BF16, "Sbf")
    Yg = sbt([128, 32, NC8], BF16, "Yg"); Ytm = Utm[:].rearrange("p g j n -> p (g j n)").rearrange("p (j c) -> p j c", j=8); yT = sbt([128, 4, T], BF16, "yT")
    for (t_, nm) in ((Cst, "Cst"), (Cbf, "Cbf"), (nst, "nst"), (nrep, "nrep"), (Sall, "Sall"), (uaT, "uaT"), (vw, "vw")):
        flat = t_[:]
        el("pool", lambda e, flat=flat: e.memset(flat, 0.0), [], [nm])

    e1a = sbt([128, NCH, 4], F32); lfa = sbt([128, NCH, 4], F32); emb4 = sbt([128, NCH, 4, 128], F32); dec4 = sbt([128, NCH, 4], F32)
    wcol4 = sbt([128, NCH, 4], F32); wcolb4 = sbt([128, NCH, 4, 2], BF16); wrep4 = sbt([128, NCH, 4, 128], BF16)
    lfrep = sbt([128, 4, 128], F32); lf = small(4, "lf"); ig = small(4); bcol = small(4); wcol = small(4, "wcol"); wcolb = sbt([128, 4, 2], BF16)
    wrep = sbt([128, 4, 128], BF16); emb = sbt([128, 4, 128], F32, "emb"); dec = small(4, "dec"); SmT = sbt([128, 4, 128], BF16, "SmT")
    aden = sbt([128, 4, 128], F32); rden = sbt([128, 4, 128], F32, "rden"); hT = sbt([128, 8, 128], F32, "hT"); sq = sbt([128, 8, 128], BF16)
    rsh = sbt([128, 4, 128], F32); hn = sbt([128, 8, 128], F32, "hn"); e1 = small(4)
    gAll = oaT; m1 = sbt([128, T], F32); m2 = sbt([128, T], F32)
    ot = [sbt([128, 1024], F32, "ot%d" % i) for i in range(2)]
    ss2 = small(1); rstd2 = small(1); sgl = sbt([128, T], BF16); xg = sbt([128, T], F32)

    def rsqrt_col(dst, src, scale, nm_src, nm_dst):
        ts(dst, src, scale, 1e-6, ALU.mult, ALU.add, [nm_src], [nm_dst])
        el("act", lambda e: e.activation(out=dst, in_=dst, func=AF.Ln), [nm_dst], [nm_dst])
        el("act", lambda e: e.activation(out=dst, in_=dst, func=AF.Exp, scale=-0.5), [nm_dst], [nm_dst])

    def mm(outp, lhsT, rhs, start, stop, r, w, sig=None):
        P.op("pe", lambda e: e.matmul(outp, lhsT=lhsT, rhs=rhs, start=start, stop=stop), reads=r, writes=w,
             sig=(stop if sig is None else sig))

    acc_rr = [0]

    def acc_bank():
        acc_rr[0] += 1
        return (pA, "pA") if acc_rr[0] % 2 else (pB, "pB")

    out_toks = []

    class StopBuild(Exception):
        pass

    def chk(level):
        if dbg_stop == level:
            raise StopBuild()

    prefetched = [False]

    def do_tile(xsrc, row0, main, ti, nxt=None):
        def issue_x(src_, r0, sub):
            xb2, xn2 = xs[sub % 2], "xs%d" % (sub % 2)
            P.dma(lambda e: e.dma_start(out=xb2[:], in_=src_[r0 + sub * 128: r0 + (sub + 1) * 128, :]), writes=[xn2])

        for sub in range(NCH):
            xb_, xnm = xs[sub % 2], "xs%d" % (sub % 2)
            if not (prefetched[0] and sub < 2):
                issue_x(xsrc, row0, sub)
            el("act", lambda e, xb_=xb_: e.activation(out=junk[:], in_=xb_[:], func=AF.Square, accum_out=ss[:]), [xnm], ["junk", "ss"])
            rsqrt_col(rstd[:], ss[:], 1.0 / 1024.0, "ss", "rstd")
            ts(xnb[:], xb_[:], rstd[:, 0:1], None, ALU.mult, None, [xnm, "rstd"], ["xnb"])
            for kt in range(8):
                P.op("pe", lambda e, kt=kt: e.transpose(out=pT[:, kt * 128:(kt + 1) * 128], in_=xnb[:, kt * 128:(kt + 1) * 128], identity=ident_b[:]),
                     reads=["xnb", "ident_b"], writes=["pT"], sig=(kt == 7))
            el("act", lambda e, sub=sub: e.copy(out=xnT[:, :, sub * 128:(sub + 1) * 128], in_=pT[:].rearrange("p (k t) -> p k t", k=8)),
               ["pT"], ["xnT"])

        prefetched[0] = False
        if nxt is not None and NCH <= 2:
            for sub in range(min(2, NCH)):
                issue_x(nxt[0], nxt[1], sub)
            prefetched[0] = True
        chk(2)

        def proj_fm(blk, ncolt, evac):
            W, wn = load_block(blk)
            Wv_ = W[:].rearrange("p (k n) -> p k n", k=8)
            for ct in range(ncolt):
                ps_, pn = acc_bank()
                for kt in range(8):
                    mm(ps_[:, 0:T], Wv_[:, kt, ct * 128:(ct + 1) * 128], xnT[:, kt, :], kt == 0, kt == 7, [wn, "xnT"], [pn])
                evac(ct, ps_, pn)

        el("dve", lambda e: e.tensor_copy(out=uaT[:, :, 1:4], in_=uaT[:, :, T + 1:T + 4]), ["uaT"], ["uaT"])
        for half in range(2):
            proj_fm(BLK_UA + half, 4, lambda ct, ps_, pn, half=half: el(
                "act", lambda e: e.copy(out=uaT[:, half * 4 + ct, 4:T + 4], in_=ps_[:, 0:T]), [pn], ["uaT"]))
        chk(3)
        for ch in range(NCH):
            for kt in range(8):
                mm(pM[:, 0:8], xnT[:, kt, ch * 128:(ch + 1) * 128], Wif[:, kt, :], kt == 0, kt == 7, ["xnT", "Wif"], ["pM"])
            tt(gates[:, ch, :], pM[:, 0:8], bif[:], ALU.add, ["pM", "bif"], ["gates"])
        chk(4)
        W, wn = load_block(BLK_UB)
        Wv_ = W[:].rearrange("p (k n) -> p k n", k=8)
        for j in range(8):
            ps_, pn = acc_bank()
            for kt in range(8):
                lhs = xnT[:, kt, :].rearrange("p (c j) -> p j c", j=8)[:, j, :]
                mm(ps_[0:NC8, :], lhs, Wv_[:, kt, :], kt == 0, kt == 7, [wn, "xnT"], [pn])
            el("act" if j % 2 else "dve",
               (lambda e, ps_=ps_, j=j: e.copy(out=Utm[0:NC8, :, j, :], in_=ps_[0:NC8, :].rearrange("c (g n) -> c g n", g=32))) if j % 2 else
               (lambda e, ps_=ps_, j=j: e.tensor_copy(out=Utm[0:NC8, :, j, :], in_=ps_[0:NC8, :].rearrange("c (g n) -> c g n", g=32))), [pn], ["Utm"])
        chk(5)
        for g in range(32):
            P.op("pe", lambda e, g=g: e.transpose(out=pT[:, g * NC8:(g + 1) * NC8], in_=Utm[0:NC8, g, :, :].rearrange("c j n -> c (j n)"),
                                                  identity=ident_b[0:NC8, 0:NC8]), reads=["Utm", "ident_b"], writes=["pT"], sig=(g == 31))
        el("act", lambda e: e.copy(out=U2[:].rearrange("p g c -> p (g c)"), in_=pT[:, 0:32 * NC8]), ["pT"], ["U2"])
        W1r, w1rn = load_block(BLK_S5)
        W1i, w1in = load_block(BLK_S5 + 1)
        for ri, (Wt, wn_) in enumerate(((W1r, w1rn), (W1i, w1in))):
            Wg = Wt[:].rearrange("p (g n) -> p g n", g=32)
            for q in range(16):
                for g2 in range(2):
                    g = 2 * q + g2
                    mm(pM[:, (q * NC8):(q + 1) * NC8], Wg[:, g, :], U2[:, g, :], g2 == 0, g2 == 1, [wn_, "U2"], ["pM"], sig=(q == 15 and g2 == 1))
            el("act" if ri == 0 else "dve",
               (lambda e, ri=ri: e.copy(out=Xall[:, :, ri, :].rearrange("p c q -> p q c"), in_=pM[:, 0:16 * NC8].rearrange("p (q c) -> p q c", q=16))) if ri == 0 else
               (lambda e, ri=ri: e.tensor_copy(out=Xall[:, :, ri, :].rearrange("p c q -> p q c"), in_=pM[:, 0:16 * NC8].rearrange("p (q c) -> p q c", q=16))),
               ["pM"], ["Xall"])
        chk(6)
        T1 = sbt([128, 2, 16], F32, "scT1_%d" % ti) if False else None
        for c in range(NC8):
            sp_ = Sall[:, c, :, :]
            sn_ = Sall[:, c + 1, :, :]
            P.op("pool", lambda e, sp_=sp_: e.tensor_tensor(out=sc_t1[:], in0=sp_, in1=AR32[:], op=ALU.mult), reads=["Sall"], writes=["sc_t1"])
            P.op("pool", lambda e, c=c: e.tensor_tensor(out=sc_t2[:, 0, :], in0=Sall[:, c, 1, :], in1=ANI[:], op=ALU.mult), reads=["Sall"], writes=["sc_t2"])
            P.op("pool", lambda e, c=c: e.tensor_tensor(out=sc_t2[:, 1, :], in0=Sall[:, c, 0, :], in1=API[:], op=ALU.mult), reads=["Sall"], writes=["sc_t2"])
            P.op("pool", lambda e, c=c: e.tensor_tensor(out=sc_t1[:], in0=sc_t1[:], in1=Xall[:, c, :, :], op=ALU.add), reads=["sc_t1", "Xall"], writes=["sc_t1"])
            P.op("pool", lambda e, sn_=sn_: e.tensor_tensor(out=sn_, in0=sc_t1[:], in1=sc_t2[:], op=ALU.add), reads=["sc_t1", "sc_t2"], writes=["Sall"])
        if main:
            for ri in range(2):
                el("pool", lambda e, ri=ri: e.tensor_copy(out=Sbf[:, ri, :, :], in_=Sall[:, 0:NC8, ri, :].rearrange("p c q -> p q c")),
                   ["Sall"], ["Sbf"])
        el("pool", lambda e: e.tensor_copy(out=Sall[:, 0, :, :], in_=Sall[:, NC8, :, :]), ["Sall"], ["Sall"])
        def s5_out():
            proj_fm(BLK_ZB, 4, lambda ct, ps_, pn: el("act", lambda e: e.activation(out=zbT[:, ct, :], in_=ps_[:, 0:T], func=AF.Silu), [pn], ["zbT"]))
            Wi_, win_ = load_block(BLK_S5 + 2)
            Wr_, wrn_ = load_block(BLK_S5 + 3)
            Wm_, wmn_ = load_block(BLK_S5 + 4)
            Wi_g = Wi_[:].rearrange("p (g n) -> p g n", g=32); Wr_g = Wr_[:].rearrange("p (g n) -> p g n", g=32)
            Wm_g = Wm_[:].rearrange("p (g n) -> p g n", g=32)
            GP = 512 // NC8
            for g0 in range(0, 32, GP):
                ng = min(GP, 32 - g0)
                for gi in range(ng):
                    g = g0 + gi
                    o_ = pM[:, gi * NC8:(gi + 1) * NC8]
                    mm(o_, Wi_g[:, g, :], U2[:, g, :], True, False, [win_, "U2"], ["pM"], sig=False)
                    mm(o_, Wr_g[:, g, :], Sbf[:, 0, g // 2, :], False, False, [wrn_, "Sbf"], ["pM"], sig=False)
                    mm(o_, Wm_g[:, g, :], Sbf[:, 1, g // 2, :], False, True, [wmn_, "Sbf"], ["pM"], sig=(gi == ng - 1))
                el("act", lambda e, g0=g0, ng=ng: e.activation(out=Yg[:, g0:g0 + ng, :].rearrange("p g c -> p (g c)"), in_=pM[:, 0:ng * NC8],
                                                               func=AF.Gelu_apprx_tanh), ["pM"], ["Yg"])
            for g0 in range(0, 32, 8):
                for gi in range(8):
                    g = g0 + gi
                    P.op("pe", lambda e, g=g, gi=gi: e.transpose(out=pT[0:NC8, gi * 128:(gi + 1) * 128], in_=Yg[:, g, :], identity=ident_b[:]),
                         reads=["Yg", "ident_b"], writes=["pT"], sig=(gi == 7))
                el("act", lambda e, g0=g0: e.copy(out=Ytm[0:NC8, :, 16 * g0:16 * g0 + 128].rearrange("c j (g n) -> c g j n", g=8),
                                                  in_=pT[0:NC8, :].rearrange("c (g j n) -> c g j n", g=8, j=8)), ["pT"], ["Utm"])
            for ct in range(4):
                for j in range(8):
                    P.op("pe", lambda e, ct=ct, j=j: e.transpose(out=pT[:, j * NC8:(j + 1) * NC8], in_=Ytm[0:NC8, j, ct * 128:(ct + 1) * 128],
                                                                 identity=ident_b[0:NC8, 0:NC8]), reads=["Utm", "ident_b"], writes=["pT"], sig=(j == 7))
                el("act", lambda e, ct=ct: e.copy(out=yT[:, ct, :], in_=pT[:, 0:T]), ["pT"], ["yT"])
            for ot_ in range(4):
                ps_, pn = acc_bank()
                for ct in range(4):
                    mm(ps_[:, 0:T], Wglu[:, ct, ot_ * 128:(ot_ + 1) * 128], yT[:, ct, :], ct == 0, ct == 3, ["Wglu", "yT"], [pn])
                el("act", lambda e, ps_=ps_, ot_=ot_: e.activation(out=sgl[:], in_=ps_[:, 0:T], func=AF.Sigmoid, bias=bglu[:, ot_:ot_ + 1]),
                   [pn, "bglu"], ["sgl"])
                tt(xg[:], sgl[:], yT[:, ot_, :], ALU.mult, ["sgl", "yT"], ["xg"])
                tt(hbT[:, ot_, :].rearrange("p (j c) -> p j c", j=8), xg[:].rearrange("p (j c) -> p j c", j=8),
                   zbT[:, ot_, :].rearrange("p (c j) -> p j c", j=8), ALU.mult, ["xg", "zbT"], ["hbT"])
        chk(7)
        for mt in range(8):
            ps_, pn = acc_bank()
            for k in range(4):
                mm(ps_[:, 0:T], cdiag[:, mt, k, :], uaT[:, mt, k + 1:k + 1 + T], k == 0, k == 3, ["cdiag", "uaT"], [pn])
            el("act", lambda e, ps_=ps_, mt=mt: e.activation(out=cT[:, mt, :], in_=ps_[:, 0:T], func=AF.Silu, bias=cb[:, mt:mt + 1]),
               [pn, "cb"], ["cT"])
        if main:
            for half in range(2):
                proj_fm(BLK_ZA + half, 4, lambda ct, ps_, pn, half=half: el(
                    "act", lambda e: e.activation(out=zaT[:, half * 4 + ct, :], in_=ps_[:, 0:T], func=AF.Silu), [pn], ["zaT"]))
            for half in range(2):
                proj_fm(BLK_OA + half, 4, lambda ct, ps_, pn, half=half: el(
                    "act", lambda e: e.activation(out=oaT[:, half * 4 + ct, :], in_=ps_[:, 0:T], func=AF.Sigmoid), [pn], ["oaT"]))
        if main:
            for mt in range(8):
                el("dve", lambda e, mt=mt: e.scalar_tensor_tensor(out=csz[:, mt, :], in0=cT[:, mt, :], scalar=skp[:, mt:mt + 1], in1=zaT[:, mt, :],
                                                                  op0=ALU.mult, op1=ALU.mult), ["cT", "skp", "zaT"], ["csz"])
            for mt in range(8):
                ts(zaT[:, mt, :], zaT[:, mt, :], hg[:, mt:mt + 1], None, ALU.mult, None, ["zaT", "hg", "csz"], ["zaT"])
        chk(8)
        for h in range(4):
            if main:
                for (Wx, wxn, dst, dn) in ((Wq, "Wqkv0", qT, "qT"), (Wk, "Wqkv1", kT, "kT")):
                    for et in range(2):
                        ps_, pn = acc_bank()
                        for d in range(2):
                            mm(ps_[:, 0:T], Wx[:, h, d, et * 128:(et + 1) * 128], cT[:, 2 * h + d, :], d == 0, d == 1, [wxn, "cT"], [pn])
                        el("act" if et else "dve",
                           (lambda e, ps_=ps_, dst=dst, h=h, et=et: e.copy(out=dst[:, h, et, :], in_=ps_[:, 0:T])) if et else
                           (lambda e, ps_=ps_, dst=dst, h=h, et=et: e.tensor_copy(out=dst[:, h, et, :], in_=ps_[:, 0:T])), [pn], [dn])
        el("act", lambda e: e.activation(out=e1a[:], in_=gates[:, :, 4:8], func=AF.Exp, scale=-1.0), ["gates"], ["e1a"])
        el("act", lambda e: e.activation(out=lfa[:], in_=e1a[:], func=AF.Ln, bias=1.0), ["e1a"], ["lfa"])
        ts(lfa[:], lfa[:], -1.0, None, ALU.mult, None, ["lfa"], ["lfa"])
        for ch in range(NCH):
            tsl = slice(ch * 128, (ch + 1) * 128)
            for h in range(4):
                el("dve", lambda e, h=h, ch=ch: e.tensor_copy(out=lfrep[:, h, :], in_=lfa[:, ch, h:h + 1].to_broadcast([128, 128])), ["lfa"], ["lfrep"])
            for h in range(4):
                mm(pGD[:, h * 128:(h + 1) * 128], lfrep[:, h, :], maskT[:], True, True, ["lfrep", "maskT"], ["pGD"], sig=(h == 3))
            for h in range(4):
                mm(pM[:, h * 128:(h + 1) * 128], maskT[:], lfrep[:, h, :], True, True, ["maskT", "lfrep"], ["pM"], sig=(h == 3))
            en, dn, wn_, wbn, wrn = "emb%d" % ch, "dec%d" % ch, "wcol%d" % ch, "wcolb%d" % ch, "wrep%d" % ch
            el("act", lambda e, ch=ch: e.activation(out=emb4[:, ch, :, :].rearrange("p h t -> p (h t)"), in_=pGD[:], func=AF.Exp, scale=-1.0), ["pGD"], [en])
            el("dve", lambda e, ch=ch: e.reciprocal(out=dec4[:, ch, :], in_=emb4[:, ch, :, 127]), [en], [dn])
            tt(bcol[:], gates[:, ch, 0:4], pM[:].rearrange("p (h t) -> p h t", h=4)[:, :, 0], ALU.subtract, ["gates", "pM"], ["bcol"])
            el("act", lambda e, ch=ch: e.activation(out=wcol4[:, ch, :], in_=bcol[:], func=AF.Exp), ["bcol"], [wn_])
            el("dve", lambda e, ch=ch: e.tensor_copy(out=vw[:, ch, :, 256], in_=wcol4[:, ch, :]), [wn_], ["vw"])
            if main:
                for h in range(4):
                    el("dve", lambda e, h=h, ch=ch: e.tensor_copy(out=wrep4[:, ch, h, :], in_=wcol4[:, ch, h:h + 1].to_broadcast([128, 128])), [wn_], [wrn])
            for h in range(4):
                ps_, pn = acc_bank()
                for d in range(2):
                    mm(ps_[:, 0:256], cT[:, 2 * h + d, tsl], Wk[:, h, d, :], d == 0, d == 1, ["cT", "Wqkv1"], [pn], sig=False)
                for d in range(2):
                    mm(ps_[:, 256:512], uaT[:, 2 * h + d, 4 + ch * 128:4 + (ch + 1) * 128], Wv[:, h, d, :], d == 0, d == 1, ["uaT", "Wqkv2"], [pn])
                el("act", lambda e, ps_=ps_, ch=ch, h=h: e.copy(out=ktm[:, ch, h, :], in_=ps_[:, 0:256]), [pn], ["ktm"])
                el("act", lambda e, ps_=ps_, ch=ch, h=h: e.activation(out=vw[:, ch, h, 0:256], in_=ps_[:, 256:512], func=AF.Copy, scale=wcol4[:, ch, h:h + 1]),
                   [pn, wn_], ["vw"])
        for ch in range(NCH):
            tsl = slice(ch * 128, (ch + 1) * 128)
            en, dn, wn_, wbn, wrn = "emb%d" % ch, "dec%d" % ch, "wcol%d" % ch, "wcolb%d" % ch, "wrep%d" % ch
            if main:
                for h in range(4):
                    for et in range(2):
                        mm(pS[:, h * 128:(h + 1) * 128], kT[:, h, et, tsl], qT[:, h, et, tsl], et == 0, et == 1, ["kT", "qT"], ["pS"], sig=(h == 3 and et == 1))
                tt(SmT[:].rearrange("p h t -> p (h t)"), pS[:], mask4[:].rearrange("p h t -> p (h t)"), ALU.mult, ["pS", "mask4"], ["SmT"])
                for h in range(4):
                    for d2 in range(2):
                        pn_, pnn = (pN0, "pN0") if h < 2 else (pN1, "pN1")
                        o_ = pn_[:, ((h % 2) * 2 + d2) * 128:((h % 2) * 2 + d2 + 1) * 128]
                        mm(o_, vw[:, ch, h, d2 * 128:(d2 + 1) * 128], SmT[:, h, :], True, False, ["vw", "SmT"], [pnn], sig=False)
                        mm(o_, Cbf[:, h, 0, d2 * 128:(d2 + 1) * 128], qT[:, h, 0, tsl], False, False, ["Cbf", "qT"], [pnn], sig=False)
                        mm(o_, Cbf[:, h, 1, d2 * 128:(d2 + 1) * 128], qT[:, h, 1, tsl], False, True, ["Cbf", "qT"], [pnn], sig=(h % 2 == 1 and d2 == 1))
                    o_ = pGD[:, h * 128:(h + 1) * 128]
                    mm(o_, wrep4[:, ch, h, :], SmT[:, h, :], True, False, [wrn, "SmT"], ["pGD"], sig=False)
                    mm(o_, nrep[:, h, 0, :], qT[:, h, 0, tsl], False, False, ["nrep", "qT"], ["pGD"], sig=False)
                    mm(o_, nrep[:, h, 1, :], qT[:, h, 1, tsl], False, True, ["nrep", "qT"], ["pGD"], sig=(h == 3))
            for h in range(4):
                for et in range(2):
                    ps_, pn = acc_bank()
                    mm(ps_[:, 0:258], ktm[:, ch, h, et * 128:(et + 1) * 128], vw[:, ch, h, 0:258], True, True, ["ktm", "vw"], [pn])
                    tt(Cst[:, h, et, 0:258], Cst[:, h, et, 0:258], ps_[:, 0:258], ALU.add, ["Cst", pn, "Cbf", "nrep"], ["Cst"])
                el("act", lambda e, h=h, ch=ch: e.activation(out=Cst[:, h, :, :], in_=Cst[:, h, :, :], func=AF.Copy, scale=dec4[:, ch, h:h + 1]), ["Cst", dn], ["Cst"])
                el("act", lambda e, h=h: e.copy(out=Cbf[:, h, :, :], in_=Cst[:, h, :, :]), ["Cst"], ["Cbf"])
                for et in range(2):
                    el("dve", lambda e, h=h, et=et: e.tensor_copy(out=nrep[:, h, et, :], in_=Cst[:, h, et, 256:257].to_broadcast([128, 128])),
                       ["Cst"], ["nrep"])
            if main:
                el("act", lambda e: e.activation(out=aden[:].rearrange("p h t -> p (h t)"), in_=pGD[:], func=AF.Abs), ["pGD"], ["aden"])
                tt(aden[:], aden[:], emb4[:, ch, :, :], ALU.max, ["aden", en], ["aden"])
                el("dve", lambda e: e.reciprocal(out=rden[:], in_=aden[:]), ["aden"], ["rden"])
                for hp, (pn_, pnn) in enumerate(((pN0, "pN0"), (pN1, "pN1"))):
                    tt(hT[:, 4 * hp:4 * hp + 4, :].rearrange("p (h d) t -> p h d t", h=2), pn_[:].rearrange("p (h d t) -> p h d t", h=2, d=2),
                       rden[:, 2 * hp:2 * hp + 2, :].unsqueeze(2).to_broadcast([128, 2, 2, 128]), ALU.mult, [pnn, "rden"], ["hT"])
                tt(hT[:], hT[:], oaT[:, :, tsl], ALU.mult, ["hT", "oaT"], ["hT"])
                el("act", lambda e: e.activation(out=sq[:], in_=hT[:], func=AF.Square), ["hT"], ["sq"])
                for h in range(4):
                    for d2 in range(2):
                        mm(pS[:, h * 128:(h + 1) * 128], ones_b[:], sq[:, 2 * h + d2, :], d2 == 0, d2 == 1, ["ones_b", "sq", "SmT"], ["pS"], sig=(h == 3 and d2 == 1))
                ts(rsh[:].rearrange("p h t -> p (h t)"), pS[:], 1.0 / 256.0, 1e-6, ALU.mult, ALU.add, ["pS"], ["rsh"])
                el("act", lambda e: e.activation(out=rsh[:], in_=rsh[:], func=AF.Ln), ["rsh"], ["rsh"])
                el("act", lambda e: e.activation(out=rsh[:], in_=rsh[:], func=AF.Exp, scale=-0.5), ["rsh"], ["rsh"])
                tt(hn[:].rearrange("p (h d) t -> p h d t", h=4), hT[:].rearrange("p (h d) t -> p h d t", h=4),
                   rsh[:].unsqueeze(2).to_broadcast([128, 4, 2, 128]), ALU.mult, ["hT", "rsh"], ["hn"])
                tt(hn[:], hn[:], zaT[:, :, tsl], ALU.mult, ["hn", "zaT"], ["hn"])
                tt(hfT[:, :, tsl], hn[:], csz[:, :, tsl], ALU.add, ["hn", "csz"], ["hfT"])
        chk(9)
        if not main:
            return
        s5_out()
        def gate_blocks(br):
            for bi_ in range(2):
                Wg_, wgn = load_block(BLK_G + 2 * br + bi_)
                Wgv = Wg_[:].rearrange("p (k n) -> p k n", k=8)
                for c4 in range(4):
                    ft = bi_ * 4 + c4
                    psg, png = acc_bank()
                    for kt in range(8):
                        mm(psg[:, 0:T], Wgv[:, kt, c4 * 128:(c4 + 1) * 128], xnT[:, kt, :], kt == 0, kt == 7, [wgn, "xnT"], [png])
                    el("act", lambda e, psg=psg, ft=ft: e.activation(out=gAll[:, ft, :], in_=psg[:, 0:T], func=AF.Sigmoid), [png], ["oaT"])

        gate_blocks(0)
        for hf_ in range(2):
            Wa, wan = load_block(BLK_AO + hf_)
            Wav = Wa[:].rearrange("p (k n) -> p k n", k=8)
            for c4 in range(4):
                ft = hf_ * 4 + c4
                psa, pna = acc_bank()
                for kt in range(8):
                    mm(psa[:, 0:T], Wav[:, kt, c4 * 128:(c4 + 1) * 128], hfT[:, kt, :], kt == 0, kt == 7, [wan, "hfT"], [pna])
                tt(mrgT[:, ft, :], psa[:, 0:T], gAll[:, ft, :], ALU.mult, [pna, "oaT"], ["zaT"])
        gate_blocks(1)
        Wbo, wbon = load_block(BLK_BO)
        Wbv = Wbo[:].rearrange("p (k n) -> p k n", k=4)
        for ft in range(8):
            psb, pnb = acc_bank()
            for kt in range(4):
                mm(psb[:, 0:T], Wbv[:, kt, ft * 128:(ft + 1) * 128], hbT[:, kt, :], kt == 0, kt == 3, [wbon, "hbT"], [pnb])
            el("act", lambda e, psb=psb: e.copy(out=m2[:].rearrange("p (c j) -> p j c", j=8), in_=psb[:, 0:T].rearrange("p (j c) -> p j c", j=8)),
               [pnb], ["m2"])
            tt(m2[:], m2[:], gAll[:, ft, :], ALU.mult, ["m2", "oaT"], ["m2"])
            tt(mrgT[:, ft, :], m2[:], mrgT[:, ft, :], ALU.add, ["m2", "zaT"], ["zaT"])
        Wo0, wo0n = load_block(BLK_WO); Wo1, wo1n = load_block(BLK_WO + 1)
        for ch in range(NCH):
            tsl = slice(ch * 128, (ch + 1) * 128)
            for hf_, (Wo_, won, pn_, pnn) in enumerate(((Wo0, wo0n, pN0, "pN0"), (Wo1, wo1n, pN1, "pN1"))):
                Wov = Wo_[:].rearrange("p (k n) -> p k n", k=8)
                for kt in range(8):
                    mm(pn_[:, :], mrgT[:, kt, tsl], Wov[:, kt, :], kt == 0, kt == 7, ["zaT", won], [pnn])
            el("act", lambda e: e.activation(out=junk[:, 0:512], in_=pN0[:], func=AF.Square, accum_out=ss2[:]), ["pN0"], ["junk", "ss2"])
            el("act", lambda e: e.activation(out=junk[:, 512:1024], in_=pN1[:], func=AF.Square, accum_out=rstd2[:]), ["pN1"], ["junk", "rstd2"])
            tt(ss2[:], ss2[:], rstd2[:], ALU.add, ["ss2", "rstd2"], ["ss2"])
            rsqrt_col(rstd2[:], ss2[:], 1.0 / 1024.0, "ss2", "rstd2")
            ob, obn = ot[ch % 2], "ot%d" % (ch % 2)
            xb_, xnm = xr[ch % 2], "xr%d" % (ch % 2)
            P.dma(lambda e, xb_=xb_, ch=ch: e.dma_start(out=xb_[:], in_=xsrc[row0 + ch * 128: row0 + (ch + 1) * 128, :]), writes=[xnm])
            for hf_, (pn_, pnn) in enumerate(((pN0, "pN0"), (pN1, "pN1"))):
                cs = slice(hf_ * 512, hf_ * 512 + 512)
                el("dve", lambda e, ob=ob, pn_=pn_, cs=cs: e.scalar_tensor_tensor(out=ob[:, cs], in0=pn_[:], scalar=rstd2[:, 0:1], in1=gpost[:, cs],
                                                                                 op0=ALU.mult, op1=ALU.mult), [pnn, "rstd2", "gpost"], [obn])
            tt(ob[:], ob[:], xb_[:], ALU.add, [obn, xnm], [obn])
            tok = P.dma(lambda e, ob=ob, ch=ch: e.dma_start(out=out[row0 + ch * 128: row0 + (ch + 1) * 128, :], in_=ob[:]), reads=[obn])
            out_toks.append(tok)

    sc_t1 = sbt([128, 2, 16], F32, "sc_t1"); sc_t2 = sbt([128, 2, 16], F32, "sc_t2")
    try:
        for ti in range(NPRE):
            nxt = (x_pre, (ti + 1) * T) if ti + 1 < NPRE else (x_main, 0)
            do_tile(x_pre, ti * T, False, ti, nxt)
    except StopBuild:
        tk = P.dma(lambda e: e.dma_start(out=out[0:128, :], in_=gpost[:]), reads=["gpost"])
        P.final_wait("sp", [tk])
        P.emit(); P.close()
        return nc
    if NPRE > 0:
        ts(Cst[:].rearrange("p h e d -> p (h e d)"), Cst[:].rearrange("p h e d -> p (h e d)"), flg[:, 0:1], None, ALU.mult, None, ["Cst", "flg"], ["Cst"])
        el("act", lambda e: e.copy(out=Cbf[:].rearrange("p h e d -> p (h e d)"), in_=Cst[:].rearrange("p h e d -> p (h e d)")), ["Cst"], ["Cbf"])
        for h in range(4):
            for et in range(2):
                el("dve", lambda e, h=h, et=et: e.tensor_copy(out=nrep[:, h, et, :], in_=Cst[:, h, et, 256:257].to_broadcast([128, 128])),
                   ["Cst"], ["nrep"])
    for ti in range(NMAIN):
        nxt = (x_main, (ti + 1) * T) if ti + 1 < NMAIN else None
        do_tile(x_main, ti * T, True, NPRE + ti, nxt)
    P.final_wait("sp", out_toks)
    P.emit()
    P.close()
    return nc


T_TILE = 256
_cache = {}


def kernel(**inputs):
    x = np.ascontiguousarray(inputs["x"], dtype=np.float32)
    Bsz, L, Dm = x.shape
    half = L // 2
    npre = half // T_TILE
    nmain = half // T_TILE
    key = (T_TILE, npre, nmain)
    if key not in _cache:
        _cache[key] = build_program(T_TILE, npre, nmain)
    nc = _cache[key]
    shared = {}
    for k, v in inputs.items():
        if k == "x":
            continue
        a = np.ascontiguousarray(np.asarray(v, dtype=np.float32)[0])
        if k in ("b_i", "b_f", "log_dt", "norm_post_g"):
            a = a.reshape(1, -1)
        shared[k] = a
    in_maps = []
    zeros = np.zeros((half, Dm), np.float32)
    for core in range(8):
        b, hf = core // 2, core % 2
        m = dict(shared)
        m["x_main"] = np.ascontiguousarray(x[b, hf * half:(hf + 1) * half])
        m["x_pre"] = zeros if hf == 0 else np.ascontiguousarray(x[b, 0:half])
        m["flag"] = np.full((128, 1), float(hf), np.float32)
        in_maps.append(m)
    res = run_bass_kernel_spmd(nc, in_maps, core_ids=list(range(8)))
    outp = np.empty((Bsz, L, Dm), np.float32)
    for core in range(8):
        b, hf = core // 2, core % 2
        outp[b, hf * half:(hf + 1) * half] = res.results[core]["out"]
    return outp
```

```python
import contextlib
import numpy as np
import concourse.bass as bass
import concourse.mybir as mybir
from concourse.bass_utils import run_bass_kernel_spmd

F32 = mybir.dt.float32
BF16 = mybir.dt.bfloat16
AF = mybir.ActivationFunctionType
ALU = mybir.AluOpType
COMPUTE = ("pe", "act", "dve", "pool")
NDMA_SLOTS = 8
DEBUG_TAGS = False
PI = float(np.pi)


class Prog:
    def __init__(self, nc):
        self.nc = nc
        self.stack = contextlib.ExitStack()
        self.engs = ("pe", "act", "dve", "pool", "sp")
        self.ops = {e: [] for e in self.engs}
        self.waited = {e: {} for e in self.engs}
        self.res = {}
        self.dma_use = {}
        self.dma_rr = {e: 0 for e in self.engs}
        self.sems = {}
        self.base = {e: 0 for e in COMPUTE}
        self.temp = None

    def sb(self, name, shape, dt):
        st = self.temp if self.temp is not None else self.stack
        return st.enter_context(self.nc.sbuf_tensor(name, list(shape), dt))

    def ps(self, name, shape, dt):
        return self.stack.enter_context(self.nc.psum_tensor(name, list(shape), dt))

    def _sem(self, key):
        if key not in self.sems:
            nm = "s_" + "_".join(str(k) for k in (key if isinstance(key, tuple) else (key,)))
            self.sems[key] = self.stack.enter_context(self.nc.semaphore(nm))
        return self.sems[key]

    def _deps(self, eng, reads, writes):
        deps = {}

        def add(tok):
            if tok is None:
                return
            k, v = tok
            if k == "pe" and eng == "pe":
                return
            if deps.get(k, -1) < v:
                deps[k] = v

        for r in reads:
            st = self.res.get(r)
            if st:
                for k, v in st[0].items():
                    add((k, v))
        for w in writes:
            st = self.res.get(w)
            if st:
                for k, v in st[0].items():
                    add((k, v))
                for k, v in st[1].items():
                    add((k, v))
        out = []
        wd = self.waited[eng]
        for k, v in deps.items():
            if wd.get(k, -1) >= v:
                continue
            wd[k] = v
            out.append((k, v))
        return out

    def _commit(self, tok, reads, writes):
        k, v = tok
        for r in reads:
            st = self.res.setdefault(r, [{}, {}])
            if st[1].get(k, -1) < v:
                st[1][k] = v
        for w in writes:
            old = self.res.get(w)
            wr = {}
            if old is not None and k not in COMPUTE:
                wr = {k2: v2 for k2, v2 in old[0].items() if k2 not in COMPUTE}
            wr[k] = v
            self.res[w] = [wr, {}]

    def _tag(self):
        if not DEBUG_TAGS:
            return None
        import sys as _sys
        f = _sys._getframe(2)
        while f is not None and f.f_code.co_name not in ("do_tile", "build_program", "s5_out", "gate_blocks", "proj_fm"):
            f = f.f_back
        return str(f.f_lineno) if f is not None else None

    def op(self, eng, fn, reads=(), writes=(), sig=True):
        waits = self._deps(eng, reads, writes)
        idx = len(self.ops[eng])
        self.ops[eng].append(dict(fn=fn, waits=waits, sig=sig, dma=None, tag=self._tag()))
        tok = (eng, idx)
        self._commit(tok, reads, writes)
        return tok

    def dma(self, fn, reads=(), writes=(), q="sp"):
        waits = self._deps(q, reads, writes)
        slot = self.dma_rr[q] % NDMA_SLOTS
        self.dma_rr[q] += 1
        key = ("d", q, slot)
        n = self.dma_use.get(key, 0)
        if n > 0:
            prev = n * 16
            if self.waited[q].get(key, -1) < prev:
                self.waited[q][key] = prev
                waits.append((key, prev))
        self.dma_use[key] = n + 1
        tok = (key, (n + 1) * 16)
        self.ops[q].append(dict(fn=fn, waits=waits, sig=False, dma=key))
        self._commit(tok, reads, writes)
        return tok

    def final_wait(self, eng, toks):
        self.ops[eng].append(dict(fn=None, waits=list(toks), sig=False, dma=None))

    def emit(self):
        nc = self.nc
        sigcount = {}
        totals = {}
        for e in COMPUTE:
            c = self.base[e]
            arr = []
            for o in self.ops[e]:
                if o["sig"]:
                    c += 1
                arr.append(c)
            need = [None] * len(arr)
            nxt = None
            for i in range(len(arr) - 1, -1, -1):
                if self.ops[e][i]["sig"]:
                    nxt = arr[i]
                need[i] = nxt
            sigcount[e] = need
            totals[e] = c
            self._sem(e)
        for k in self.dma_use:
            self._sem(k)

        def resolve(k, v):
            if k in COMPUTE:
                val = sigcount[k][v]
                assert val is not None, (k, v)
                return self.sems[k], val
            return self.sems[k], v

        with nc.Block() as block:

            def run(eng_name, eng):
                for o in self.ops[eng_name]:
                    for k, v in o["waits"]:
                        s, val = resolve(k, v)
                        eng.wait_ge(s, val)
                    if o["fn"] is None:
                        continue
                    ins = o["fn"](eng)
                    if o.get("tag"):
                        ins.annotate(o["tag"])
                    if o["dma"] is not None:
                        ins.then_inc(self.sems[o["dma"]], 16)
                    elif o["sig"]:
                        ins.then_inc(self.sems[eng_name], 1)
                for o2 in COMPUTE:
                    if o2 != eng_name and totals[o2] > 0:
                        eng.wait_ge(self.sems[o2], totals[o2])
                for k, n in self.dma_use.items():
                    eng.wait_ge(self.sems[k], n * 16)

            @block.tensor
            def _(e):
                run("pe", e)

            @block.scalar
            def _(e):
                run("act", e)

            @block.vector
            def _(e):
                run("dve", e)

            @block.gpsimd
            def _(e):
                run("pool", e)

            @block.sync
            def _(e):
                run("sp", e)

        self.base = totals
        self.ops = {e: [] for e in self.engs}
        self.waited = {e: {} for e in self.engs}
        self.res = {}

    def close(self):
        self.stack.close()


NBLK = 22
BLK_UA, BLK_UB, BLK_ZB, BLK_ZA, BLK_OA, BLK_G, BLK_AO, BLK_BO, BLK_WO, BLK_S5 = 0, 2, 3, 4, 6, 8, 12, 14, 15, 17
COL_UA, COL_ZA, COL_OA, COL_I, COL_UB, COL_ZB, COL_G = 0, 1024, 2048, 3072, 3080, 3592, 4104


def build_program(T, NPRE, NMAIN, dbg_stop=0):
    NCH = T // 128
    NC8 = T // 8
    nc = bass.Bass("TRN2", target_bir_lowering=False)
    dram = {}

    def din(name, shape):
        dram[name] = nc.dram_tensor(name, list(shape), F32, kind="ExternalInput").ap()
        return dram[name]

    x_pre = din("x_pre", [max(NPRE, 1) * T, 1024])
    x_main = din("x_main", [NMAIN * T, 1024])
    flag = din("flag", [128, 1])
    norm_pre_g = din("norm_pre_g", [1024]); w_in = din("w_in", [1024, 6152])
    conv_w = din("conv_w", [4, 1024]); conv_b = din("conv_b", [1024])
    w_q = din("w_q", [4, 256, 256]); w_k = din("w_k", [4, 256, 256]); w_v = din("w_v", [4, 256, 256])
    b_i = din("b_i", [1, 4]); b_f = din("b_f", [1, 4]); head_g = din("head_g", [1024]); skip_a = din("skip_a", [1024])
    w_a_out = din("w_a_out", [1024, 1024])
    lam_re = din("lam_re", [32, 64]); lam_im = din("lam_im", [32, 64]); log_dt = din("log_dt", [1, 32])
    B_re = din("B_re", [32, 64, 16]); B_im = din("B_im", [32, 64, 16])
    C_re = din("C_re", [32, 16, 64]); C_im = din("C_im", [32, 16, 64]); D_skip = din("D_skip", [32, 16])
    w_glu = din("w_glu", [512, 512]); b_glu = din("b_glu", [512]); w_b_out = din("w_b_out", [512, 1024])
    w_o = din("w_o", [1024, 1024]); norm_post_g = din("norm_post_g", [1, 1024])
    out = nc.dram_tensor("out", [NMAIN * T, 1024], F32, kind="ExternalOutput").ap()
    WS = nc.dram_tensor("wscratch", [NBLK, 128, 4096], BF16, kind="Internal").ap()

    P = Prog(nc)
    uid = [0]

    def sbt(shape, dt, name=None):
        uid[0] += 1
        return P.sb(name or ("t%d" % uid[0]), shape, dt)

    ident_f = sbt([128, 128], F32); ident_b = sbt([128, 128], BF16)
    maskT = sbt([128, 128], F32); mask4 = sbt([128, 4, 128], F32); ones_b = sbt([128, 128], BF16)
    P.op("pool", lambda e: e.memset(ident_f[:], 1.0), writes=["ident_f"])
    P.op("pool", lambda e: e.affine_select(out=ident_f[:], in_=ident_f[:], pattern=[[-1, 128]], compare_op=ALU.is_equal,
                                           fill=0.0, base=0, channel_multiplier=1), reads=["ident_f"], writes=["ident_f"])
    P.op("dve", lambda e: e.tensor_copy(out=ident_b[:], in_=ident_f[:]), reads=["ident_f"], writes=["ident_b"])
    P.op("pool", lambda e: e.memset(maskT[:], 1.0), writes=["maskT"])
    P.op("pool", lambda e: e.affine_select(out=maskT[:], in_=maskT[:], pattern=[[1, 128]], compare_op=ALU.is_ge,
                                           fill=0.0, base=0, channel_multiplier=-1), reads=["maskT"], writes=["maskT"])
    for h in range(4):
        P.op("pool", lambda e, h=h: e.tensor_copy(out=mask4[:, h, :], in_=maskT[:]), reads=["maskT"], writes=["mask4"])
    P.op("pool", lambda e: e.memset(ones_b[:], 1.0), writes=["ones_b"])

    gpre = sbt([128, 8], F32); cb = sbt([128, 8], F32); hg = sbt([128, 8], F32); skp = sbt([128, 8], F32)
    cw = sbt([128, 8, 4], F32); bglu = sbt([128, 4], F32); gpost = sbt([128, 1024], F32); bif = sbt([128, 8], F32)
    flg = sbt([128, 1], F32)
    nonc = dict(allow_slow_non_contiguous=True)
    P.dma(lambda e: e.dma_start(out=gpre[:], in_=norm_pre_g.rearrange("(k p) -> p k", p=128), **nonc), writes=["gpre"])
    P.dma(lambda e: e.dma_start(out=cb[:], in_=conv_b.rearrange("(k p) -> p k", p=128), **nonc), writes=["cb"])
    P.dma(lambda e: e.dma_start(out=hg[:], in_=head_g.rearrange("(k p) -> p k", p=128), **nonc), writes=["hg"])
    P.dma(lambda e: e.dma_start(out=skp[:], in_=skip_a.rearrange("(k p) -> p k", p=128), **nonc), writes=["skp"])
    for k in range(4):
        P.dma(lambda e, k=k: e.dma_start(out=cw[:, :, k], in_=conv_w[k].rearrange("(m p) -> p m", p=128), **nonc), writes=["cw"])
    P.dma(lambda e: e.dma_start(out=bglu[:], in_=b_glu.rearrange("(k p) -> p k", p=128), **nonc), writes=["bglu"])
    P.dma(lambda e: e.dma_start(out=gpost[:], in_=norm_post_g.partition_broadcast(128)), writes=["gpost"])
    P.dma(lambda e: e.dma_start(out=bif[:, 0:4], in_=b_i.partition_broadcast(128)), writes=["bif"])
    P.dma(lambda e: e.dma_start(out=bif[:, 4:8], in_=b_f.partition_broadcast(128)), writes=["bif"])
    P.dma(lambda e: e.dma_start(out=flg[:], in_=flag), writes=["flg"])

    Wqkv = [sbt([128, 4, 2, 256], BF16) for _ in range(3)]
    Wif = sbt([128, 8, 8], BF16); Wglu = sbt([128, 4, 512], BF16); cdiag = sbt([128, 8, 4, 128], BF16)
    AR32 = sbt([128, 2, 16], F32); ANI = sbt([128, 16], F32); API = sbt([128, 16], F32)
    pA = P.ps("pA", [128, 512], F32); pB = P.ps("pB", [128, 512], F32)
    pT = P.ps("pT", [128, 1024], BF16); pGD = P.ps("pGD", [128, 512], F32)
    pS = P.ps("pS", [128, 512], F32); pN0 = P.ps("pN0", [128, 512], F32); pN1 = P.ps("pN1", [128, 512], F32)
    pM = P.ps("pM", [128, 512], F32)
    P.temp = contextlib.ExitStack()
    stg = sbt([128, 4096], F32, "stg")
    stgb = sbt([128, 4096], BF16, "stgb")
    for wi, (wsrc, scl) in enumerate(((w_q, 1.0), (w_k, 1.0 / 16.0), (w_v, 1.0))):
        P.dma(lambda e, wsrc=wsrc: e.dma_start(out=stg[:, 0:2048].rearrange("p (h d n) -> p h d n", h=4, d=2),
                                               in_=wsrc.rearrange("h (d p) n -> p h d n", p=128)), writes=["stg"])
        P.op("dve", lambda e, wi=wi, scl=scl: e.tensor_scalar(out=Wqkv[wi][:].rearrange("p h d n -> p (h d n)"), in0=stg[:, 0:2048],
                                                              scalar1=scl, scalar2=None, op0=ALU.mult), reads=["stg"], writes=["Wqkv%d" % wi])
    Wq, Wk, Wv = Wqkv
    P.dma(lambda e: e.dma_start(out=stg[:, 0:64].rearrange("p (k n) -> p k n", k=8),
                                in_=w_in[:, COL_I:COL_I + 8].rearrange("(k p) n -> p k n", p=128), **nonc), writes=["stg"])
    for kt in range(8):
        P.op("dve", lambda e, kt=kt: e.tensor_scalar(out=Wif[:, kt, :], in0=stg[:, kt * 8:(kt + 1) * 8], scalar1=gpre[:, kt:kt + 1],
                                                     scalar2=None, op0=ALU.mult), reads=["stg", "gpre"], writes=["Wif"])
    P.dma(lambda e: e.dma_start(out=stg[:, 0:2048].rearrange("p (k n) -> p k n", k=4),
                                in_=w_glu.rearrange("(k p) n -> p k n", p=128)), writes=["stg"])
    P.op("dve", lambda e: e.tensor_copy(out=Wglu[:].rearrange("p k n -> p (k n)"), in_=stg[:, 0:2048]), reads=["stg"], writes=["Wglu"])
    for mt in range(8):
        for k in range(4):
            P.op("dve", lambda e, mt=mt, k=k: e.tensor_scalar(out=cdiag[:, mt, k, :], in0=ident_f[:], scalar1=cw[:, mt, k:k + 1],
                                                               scalar2=None, op0=ALU.mult), reads=["ident_f", "cw"], writes=["cdiag"])

    def stage_block(blk, src_ap_f, scale_gpre, nk):
        ncol = 4096 // nk
        P.dma(lambda e: e.dma_start(out=stg[:].rearrange("p (k n) -> p k n", k=nk), in_=src_ap_f), writes=["stg"])
        if scale_gpre:
            for kt in range(nk):
                P.op("dve",
                     lambda e, kt=kt: e.tensor_scalar(out=stgb[:, kt * ncol:(kt + 1) * ncol], in0=stg[:, kt * ncol:(kt + 1) * ncol],
                                                      scalar1=gpre[:, kt:kt + 1], scalar2=None, op0=ALU.mult),
                     reads=["stg", "gpre"], writes=["stgb"])
        else:
            P.op("dve", lambda e: e.tensor_copy(out=stgb[:, 0:2048], in_=stg[:, 0:2048]), reads=["stg"], writes=["stgb"])
            P.op("act", lambda e: e.copy(out=stgb[:, 2048:4096], in_=stg[:, 2048:4096]), reads=["stg"], writes=["stgb"])
        P.dma(lambda e: e.dma_start(out=WS[blk], in_=stgb[:]), reads=["stgb"], writes=["WS%d" % blk])

    def win_cols(c0):
        return w_in[:, c0:c0 + 512].rearrange("(k p) n -> p k n", p=128)

    win_blocks = [(BLK_UA, COL_UA), (BLK_UA + 1, COL_UA + 512), (BLK_UB, COL_UB), (BLK_ZB, COL_ZB), (BLK_ZA, COL_ZA),
                  (BLK_ZA + 1, COL_ZA + 512), (BLK_OA, COL_OA), (BLK_OA + 1, COL_OA + 512)] + [(BLK_G + i, COL_G + 512 * i) for i in range(4)]
    pending = []
    for blk, c0 in win_blocks:
        pending.append((blk, win_cols(c0), True, 8))
    for i in range(2):
        pending.append((BLK_AO + i, w_a_out[:, 512 * i:512 * i + 512].rearrange("(k p) n -> p k n", p=128), False, 8))
        pending.append((BLK_WO + i, w_o[:, 512 * i:512 * i + 512].rearrange("(k p) n -> p k n", p=128), False, 8))
    pending.append((BLK_BO, w_b_out.rearrange("(k p) n -> p k n", p=128), False, 4))

    def stage_some(n=1):
        for _ in range(n):
            if pending:
                stage_block(*pending.pop(0))

    stage_some(2)

    def small(n, name=None):
        return sbt([128, n], F32, name)

    cnt = [0]

    def el(eng, fn, r, w):
        P.op(eng, fn, reads=r, writes=w)

    def tt(outp, a, b, op, r, w, eng="dve"):
        el(eng, lambda e: e.tensor_tensor(out=outp, in0=a, in1=b, op=op), r, w)

    def ts(outp, a, s1, s2, op0, op1, r, w, eng="dve"):
        if op1 is None:
            el(eng, lambda e: e.tensor_scalar(out=outp, in0=a, scalar1=s1, scalar2=None, op0=op0), r, w)
        else:
            el(eng, lambda e: e.tensor_scalar(out=outp, in0=a, scalar1=s1, scalar2=s2, op0=op0, op1=op1), r, w)

    LR = small(32); LI = small(32); DT = small(32)
    for hf in range(2):
        sl = slice(64 * hf, 64 * hf + 64)
        P.dma(lambda e, sl=sl: e.dma_start(out=LR[sl, :], in_=lam_re.rearrange("g p -> p g"), **nonc), writes=["LR"])
        P.dma(lambda e, sl=sl: e.dma_start(out=LI[sl, :], in_=lam_im.rearrange("g p -> p g"), **nonc), writes=["LI"])
    P.dma(lambda e: e.dma_start(out=DT[:], in_=log_dt.partition_broadcast(128)), writes=["DT"])
    el("act", lambda e: e.activation(out=DT[:], in_=DT[:], func=AF.Exp), ["DT"], ["DT"])
    TH = small(32); MAG = small(32); t0 = small(32); t1 = small(32); t2 = small(32); kk = small(32)
    tt(TH[:], LI[:], DT[:], ALU.mult, ["LI", "DT"], ["TH"])
    tt(t0[:], LR[:], DT[:], ALU.mult, ["LR", "DT"], ["t0"])
    el("act", lambda e: e.activation(out=MAG[:], in_=t0[:], func=AF.Exp), ["t0"], ["MAG"])
    IMAG2 = small(32)
    el("act", lambda e: e.activation(out=IMAG2[:], in_=t0[:], func=AF.Exp, scale=-2.0), ["t0"], ["IMAG2"])

    def sin_of(dst, src, shift, nm):
        ts(t1[:], src, shift, None, ALU.add, None, [nm, "t1"], ["t1"])
        el("pool", lambda e: e.memset(kk[:], 0.0), [], ["kk"])
        for m in range(7):
            ts(t2[:], t1[:], (2 * m + 1) * PI, None, ALU.is_gt, None, ["t1"], ["t2"])
            tt(kk[:], kk[:], t2[:], ALU.add, ["kk", "t2"], ["kk"])
        ts(kk[:], kk[:], -2.0 * PI, None, ALU.mult, None, ["kk"], ["kk"])
        tt(t1[:], t1[:], kk[:], ALU.add, ["t1", "kk"], ["t1"])
        el("act", lambda e: e.activation(out=dst, in_=t1[:], func=AF.Sin), ["t1"], [nm + "_s"])

    SN = small(32); CS = small(32)
    sin_of(SN[:], TH[:], 0.0, "TH")
    sin_of(CS[:], TH[:], PI / 2.0, "TH")
    pwr = sbt([128, 9, 32], F32); pwi = sbt([128, 9, 32], F32); pnr = sbt([128, 8, 32], F32); pni = sbt([128, 8, 32], F32)
    el("pool", lambda e: e.memset(pwr[:, 0, :], 1.0), [], ["pw"]); el("pool", lambda e: e.memset(pwi[:, 0, :], 0.0), [], ["pw"])
    el("pool", lambda e: e.memset(pnr[:, 0, :], 1.0), [], ["pn"]); el("pool", lambda e: e.memset(pni[:, 0, :], 0.0), [], ["pn"])
    tt(pwr[:, 1, :], MAG[:], CS[:], ALU.mult, ["MAG", "TH_s"], ["pw"])
    tt(pwi[:, 1, :], MAG[:], SN[:], ALU.mult, ["MAG", "TH_s"], ["pw"])
    tt(pnr[:, 1, :], pwr[:, 1, :], IMAG2[:], ALU.mult, ["pw", "IMAG2"], ["pn"])
    tt(t0[:], pwi[:, 1, :], IMAG2[:], ALU.mult, ["pw", "IMAG2"], ["t0"])
    ts(pni[:, 1, :], t0[:], -1.0, None, ALU.mult, None, ["t0"], ["pn"])

    def cmul(or_, oi_, ar, ai, br, bi, r, w):
        raise NotImplementedError

    u0 = small(32); u1 = small(32)
    for k in range(1, 8):
        for (xr, xi, nm, lim) in ((pwr, pwi, "pw", 9), (pnr, pni, "pn", 8)):
            if k + 1 >= lim:
                continue
            tt(u0[:], xr[:, k, :], xr[:, 1, :], ALU.mult, [nm], ["u0"])
            tt(u1[:], xi[:, k, :], xi[:, 1, :], ALU.mult, [nm], ["u1"])
            tt(xr[:, k + 1, :], u0[:], u1[:], ALU.subtract, ["u0", "u1"], [nm])
            tt(u0[:], xr[:, k, :], xi[:, 1, :], ALU.mult, [nm], ["u0"])
            tt(u1[:], xi[:, k, :], xr[:, 1, :], ALU.mult, [nm], ["u1"])
            tt(xi[:, k + 1, :], u0[:], u1[:], ALU.add, ["u0", "u1"], [nm])
    den = small(32); qr = small(32); qi = small(32); nr = small(32)
    tt(u0[:], LR[:], LR[:], ALU.mult, ["LR"], ["u0"]); tt(u1[:], LI[:], LI[:], ALU.mult, ["LI"], ["u1"])
    tt(den[:], u0[:], u1[:], ALU.add, ["u0", "u1"], ["den"])
    el("dve", lambda e: e.reciprocal(out=den[:], in_=den[:]), ["den"], ["den"])
    ts(nr[:], pwr[:, 1, :], -1.0, None, ALU.add, None, ["pw"], ["nr"])
    tt(u0[:], nr[:], LR[:], ALU.mult, ["nr", "LR"], ["u0"]); tt(u1[:], pwi[:, 1, :], LI[:], ALU.mult, ["pw", "LI"], ["u1"])
    tt(qr[:], u0[:], u1[:], ALU.add, ["u0", "u1"], ["qr"]); tt(qr[:], qr[:], den[:], ALU.mult, ["qr", "den"], ["qr"])
    tt(u0[:], pwi[:, 1, :], LR[:], ALU.mult, ["pw", "LR"], ["u0"]); tt(u1[:], nr[:], LI[:], ALU.mult, ["nr", "LI"], ["u1"])
    tt(qi[:], u0[:], u1[:], ALU.subtract, ["u0", "u1"], ["qi"]); tt(qi[:], qi[:], den[:], ALU.mult, ["qi", "den"], ["qi"])
    Br = sbt([128, 32, 16], F32); Bi = sbt([128, 32, 16], F32); bbr = sbt([128, 32, 16], F32); bbi = sbt([128, 32, 16], F32)
    v0 = sbt([128, 32, 16], F32); v1 = sbt([128, 32, 16], F32)
    for hf in range(2):
        sl = slice(64 * hf, 64 * hf + 64)
        P.dma(lambda e, sl=sl: e.dma_start(out=Br[sl], in_=B_re.rearrange("g p n -> p g n"), **nonc), writes=["Br"])
        P.dma(lambda e, sl=sl: e.dma_start(out=Bi[sl], in_=B_im.rearrange("g p n -> p g n"), **nonc), writes=["Bi"])

    def bc(s):
        return s.unsqueeze(2).to_broadcast([128, 32, 16])

    def cmul3(orr, oii, sr, si, sn, xr, xi, xn, on):
        xn = [xn] if isinstance(xn, str) else list(xn)
        sn = [sn] if isinstance(sn, str) else list(sn)
        stage_some(1)
        tt(v0[:], xr, bc(sr), ALU.mult, xn + sn, ["v0"]); tt(v1[:], xi, bc(si), ALU.mult, xn + sn, ["v1"])
        tt(orr, v0[:], v1[:], ALU.subtract, ["v0", "v1"], [on])
        tt(v0[:], xi, bc(sr), ALU.mult, xn + sn, ["v0"]); tt(v1[:], xr, bc(si), ALU.mult, xn + sn, ["v1"])
        tt(oii, v0[:], v1[:], ALU.add, ["v0", "v1"], [on])

    el("dve", lambda e: e.tensor_copy(out=u0[:], in_=qr[:]), ["qr"], ["qq"])
    cmul3(bbr[:], bbi[:], qr[:], qi[:], ["qr", "qi"], Br[:], Bi[:], ["Br", "Bi"], "bb")
    CTr = sbt([128, 32, 16], F32); CTi = sbt([128, 32, 16], F32)
    Cdup = sbt([128, 4, 2, 64], F32)
    for (Csrc, CTt, nm) in ((C_re, CTr, "CTr"), (C_im, CTi, "CTi")):
        for d in range(2):
            P.dma(lambda e, Csrc=Csrc, d=d: e.dma_start(out=Cdup[:, :, d, :], in_=Csrc.rearrange("(t g) n p -> (g n) t p", t=4)),
                  writes=["Cdup"])
        for t in range(4):
            P.op("pe", lambda e, t=t: e.transpose(out=pA[:, t * 128:(t + 1) * 128], in_=Cdup[:, t, :, :].rearrange("q d p -> q (d p)"),
                                                  identity=ident_f[:]), reads=["Cdup", "ident_f"], writes=["pA"])
        el("act", lambda e, CTt=CTt: e.copy(out=CTt[:].rearrange("p g n -> p (g n)"), in_=pA[:]), ["pA"], [nm])
    Er = sbt([128, 32, 8, 16], F32); Ei = sbt([128, 32, 8, 16], F32)
    Fr = sbt([128, 32, 8, 16], F32); Fi = sbt([128, 32, 8, 16], F32)
    for j in range(8):
        cmul3(Er[:, :, j, :], Ei[:, :, j, :], pwr[:, 7 - j, :], pwi[:, 7 - j, :], "pw", bbr[:], bbi[:], "bb", "E")
        cmul3(Fr[:, :, j, :], Fi[:, :, j, :], pwr[:, j + 1, :], pwi[:, j + 1, :], "pw", CTr[:], CTi[:], ["CTr", "CTi"], "F")
    halfm = sbt([128, 2], F32)
    el("pool", lambda e: e.memset(halfm[:], 0.0), [], ["halfm"])
    el("pool", lambda e: e.memset(halfm[0:64, 0:1], 1.0), ["halfm"], ["halfm"])
    el("pool", lambda e: e.memset(halfm[64:128, 1:2], 1.0), ["halfm"], ["halfm"])
    for ri, (Et, blk) in enumerate(((Er, BLK_S5), (Ei, BLK_S5 + 1))):
        el("pool", lambda e: e.memset(stgb[:], 0.0), ["stgb"], ["stgb"])
        for g in range(32):
            ps_ = pA if g % 2 == 0 else pB
            nm = "pA" if g % 2 == 0 else "pB"
            P.op("pe", lambda e, Et=Et, g=g, ps_=ps_: e.transpose(out=ps_[:, 0:64], in_=Et[0:64, g, :, :].rearrange("p j n -> p (j n)"),
                                                                  identity=ident_f[0:64, 0:64]), reads=["E", "ident_f"], writes=[nm])
            c0 = g * 128 + 64 * (g % 2)
            el("act" if g % 2 == 0 else "dve",
               (lambda e, ps_=ps_, c0=c0: e.copy(out=stgb[:, c0:c0 + 64], in_=ps_[:, 0:64])) if g % 2 == 0 else
               (lambda e, ps_=ps_, c0=c0: e.tensor_copy(out=stgb[:, c0:c0 + 64], in_=ps_[:, 0:64])), [nm], ["stgb"])
        P.dma(lambda e, blk=blk: e.dma_start(out=WS[blk], in_=stgb[:]), reads=["stgb"], writes=["WS%d" % blk])
    for ri, (Ft, blk, sg) in enumerate(((Fr, BLK_S5 + 3, 1.0), (Fi, BLK_S5 + 4, -1.0))):
        for g in range(32):
            P.op("dve",
                 lambda e, Ft=Ft, g=g, sg=sg: e.tensor_scalar(out=stgb[:, g * 128:(g + 1) * 128], in0=Ft[:, g, :, :].rearrange("p j n -> p (j n)"),
                                                              scalar1=halfm[:, (g % 2):(g % 2) + 1], scalar2=sg, op0=ALU.mult, op1=ALU.mult),
                 reads=["F", "halfm"], writes=["stgb"])
        P.dma(lambda e, blk=blk: e.dma_start(out=WS[blk], in_=stgb[:]), reads=["stgb"], writes=["WS%d" % blk])
    Gr = sbt([128, 32, 8, 16], F32); Gni = sbt([128, 32, 8, 16], F32)
    i8r = small(32); i8i = small(32)
    tt(u0[:], pnr[:, 7, :], pnr[:, 1, :], ALU.mult, ["pn"], ["u0"]); tt(u1[:], pni[:, 7, :], pni[:, 1, :], ALU.mult, ["pn"], ["u1"])
    tt(i8r[:], u0[:], u1[:], ALU.subtract, ["u0", "u1"], ["i8"])
    tt(u0[:], pnr[:, 7, :], pni[:, 1, :], ALU.mult, ["pn"], ["u0"]); tt(u1[:], pni[:, 7, :], pnr[:, 1, :], ALU.mult, ["pn"], ["u1"])
    tt(i8i[:], u0[:], u1[:], ALU.add, ["u0", "u1"], ["i8"])
    for j in range(8):
        cmul3(Gr[:, :, j, :], Gni[:, :, j, :], i8r[:], i8i[:], "i8", Fr[:, :, j, :], Fi[:, :, j, :], "F", "G")
    ts(Gni[:], Gni[:], -1.0, None, ALU.mult, None, ["G"], ["G"])
    bmask = sbt([128, 8, 16], F32); Dcol = sbt([128, 32], F32)
    el("pool", lambda e: e.memset(bmask[:], 1.0), [], ["bmask"])
    el("pool", lambda e: e.affine_select(out=bmask[:], in_=bmask[:], pattern=[[16, 8], [0, 16]], compare_op=ALU.is_ge, fill=0.0,
                                         base=15, channel_multiplier=-1), ["bmask"], ["bmask"])
    for j in range(8):
        P.dma(lambda e, j=j: e.dma_start(out=Dcol[16 * j:16 * j + 16, :], in_=D_skip.rearrange("g n -> n g"), **nonc), writes=["Dcol"])
    wtmp = sbt([128, 128], F32)
    for g in range(32):
        ps_ = pA if g % 2 == 0 else pB
        nm = "pA" if g % 2 == 0 else "pB"
        P.op("pe", lambda e, g=g, ps_=ps_: e.matmul(ps_[:, 0:128], lhsT=Er[0:64, g, :, :].rearrange("p j n -> p (j n)"),
                                                    rhs=Gr[0:64, g, :, :].rearrange("p j n -> p (j n)"), start=True, stop=False),
             reads=["E", "G"], writes=[nm], sig=False)
        P.op("pe", lambda e, g=g, ps_=ps_: e.matmul(ps_[:, 0:128], lhsT=Ei[0:64, g, :, :].rearrange("p j n -> p (j n)"),
                                                    rhs=Gni[0:64, g, :, :].rearrange("p j n -> p (j n)"), start=False, stop=True),
             reads=["E", "G"], writes=[nm])
        tt(wtmp[:], ps_[:, 0:128], bmask[:].rearrange("p j n -> p (j n)"), ALU.mult, [nm, "bmask"], ["wtmp"])
        el("dve", lambda e, g=g: e.scalar_tensor_tensor(out=stgb[:, g * 128:(g + 1) * 128], in0=ident_f[:], scalar=Dcol[:, g:g + 1],
                                                        in1=wtmp[:], op0=ALU.mult, op1=ALU.add), ["ident_f", "Dcol", "wtmp", "stgb"], ["stgb"])
    P.dma(lambda e: e.dma_start(out=WS[BLK_S5 + 2], in_=stgb[:]), reads=["stgb"], writes=["WS%d" % (BLK_S5 + 2)])
    for g2 in range(2):
        sl = slice(64 * g2, 64 * g2 + 64)
        src_r = pwr[sl, 8, :].rearrange("p (q t) -> p q t", t=2)[:, :, g2]
        src_i = pwi[sl, 8, :].rearrange("p (q t) -> p q t", t=2)[:, :, g2]
        el("dve", lambda e, sl=sl, src_r=src_r: e.tensor_copy(out=AR32[sl, 0, :], in_=src_r), ["pw"], ["AR32"])
        el("dve", lambda e, sl=sl, src_r=src_r: e.tensor_copy(out=AR32[sl, 1, :], in_=src_r), ["pw"], ["AR32"])
        el("dve", lambda e, sl=sl, src_i=src_i: e.tensor_copy(out=API[sl, :], in_=src_i), ["pw"], ["API"])
        ts(ANI[sl, :], src_i, -1.0, None, ALU.mult, None, ["pw"], ["ANI"])

    stage_some(100)
    if dbg_stop == 1:
        tk = P.dma(lambda e: e.dma_start(out=out[0:128, :], in_=gpost[:]), reads=["gpost"])
        P.final_wait("sp", [tk])
        P.emit(); P.temp.close(); P.close()
        return nc
    P.emit()
    P.temp.close()
    P.temp = None
    NSLOT = 3
    wslot = [sbt([128, 4096], BF16, "wslot%d" % i) for i in range(NSLOT)]
    slot_rr = [0]

    def load_block(blk):
        s = slot_rr[0] % NSLOT
        slot_rr[0] += 1
        P.dma(lambda e: e.dma_start(out=wslot[s][:], in_=WS[blk]), reads=["WS%d" % blk], writes=["wslot%d" % s])
        return wslot[s], "wslot%d" % s

    xs = [sbt([128, 1024], F32, "xs%d" % i) for i in range(2)]
    xr = [sbt([128, 1024], F32, "xr%d" % i) for i in range(2)]
    junk = sbt([128, 1024], BF16); xnb = sbt([128, 1024], BF16)
    ss = small(1); rstd = small(1)
    xnT = sbt([128, 8, T], BF16, "xnT"); uaT = sbt([128, 8, T + 4], BF16, "uaT"); cT = sbt([128, 8, T], BF16, "cT")
    zaT = sbt([128, 8, T], BF16, "zaT"); oaT = sbt([128, 8, T], BF16, "oaT"); zbT = sbt([128, 4, T], BF16, "zbT")
    qT = sbt([128, 4, 2, T], BF16, "qT"); kT = sbt([128, 4, 2, T], BF16, "kT")
    ktm = sbt([128, NCH, 4, 256], BF16, "ktm"); vw = sbt([128, NCH, 4, 264], BF16, "vw")
    gates = sbt([128, NCH, 8], F32, "gates")
    hfT = sbt([128, 8, T], BF16, "hfT"); csz = sbt([128, 8, T], BF16, "csz"); mrgT = zaT; hbT = sbt([128, 4, T], BF16, "hbT")
    Cst = sbt([128, 4, 2, 264], F32, "Cst"); Cbf = sbt([128, 4, 2, 264], BF16, "Cbf")
    nst = sbt([128, 4, 2], F32, "nst"); nrep = sbt([128, 4, 2, 128], BF16, "nrep")
    Utm = sbt([128, 32, 8, 16], BF16, "Utm"); U2 = sbt([128, 32, NC8], BF16, "U2")
    Xall = sbt([128, NC8, 2, 16], F32, "Xall"); Sall = sbt([128, NC8 + 1, 2, 16], F32, "Sall"); Sbf = sbt([128, 2, 16, NC8], BF16, "Sbf")
    Yg = sbt([128, 32, NC8], BF16, "Yg"); Ytm = Utm[:].rearrange("p g j n -> p (g j n)").rearrange("p (j c) -> p j c", j=8); yT = sbt([128, 4, T], BF16, "yT")
    for (t_, nm) in ((Cst, "Cst"), (Cbf, "Cbf"), (nst, "nst"), (nrep, "nrep"), (Sall, "Sall"), (uaT, "uaT"), (vw, "vw")):
        flat = t_[:]
        wnames = [nm] + (["%s%d" % (nm, h_) for h_ in range(4)] if nm in ("Cst", "Cbf", "nrep") else [])
        el("pool", lambda e, flat=flat: e.memset(flat, 0.0), [], wnames)

    e1a = sbt([128, NCH, 4], F32); lfa = sbt([128, NCH, 4], F32); emb4 = sbt([128, NCH, 4, 128], F32); dec4 = sbt([128, NCH, 4], F32)
    wcol4 = sbt([128, NCH, 4], F32); wcolb4 = sbt([128, NCH, 4, 2], BF16); wrep4 = sbt([128, NCH, 4, 128], BF16)
    lfrep = sbt([128, 4, 128], F32); lf = small(4, "lf"); ig = small(4); bcol = small(4); wcol = small(4, "wcol"); wcolb = sbt([128, 4, 2], BF16)
    wrep = sbt([128, 4, 128], BF16); emb = sbt([128, 4, 128], F32, "emb"); dec = small(4, "dec"); SmT = sbt([128, 4, 128], BF16, "SmT")
    aden = sbt([128, 4, 128], F32); rden = sbt([128, 4, 128], F32, "rden"); hT = sbt([128, 8, 128], F32, "hT"); sq = sbt([128, 8, 128], BF16)
    rsh = sbt([128, 4, 128], F32); hn = sbt([128, 8, 128], F32, "hn"); e1 = small(4)
    gAll = oaT; m1 = sbt([128, T], F32); m2 = sbt([128, T], F32)
    ot = [sbt([128, 1024], F32, "ot%d" % i) for i in range(2)]
    ss2 = small(1); rstd2 = small(1); sgl = sbt([128, T], BF16); xg = sbt([128, T], F32)

    def rsqrt_col(dst, src, scale, nm_src, nm_dst):
        ts(dst, src, scale, 1e-6, ALU.mult, ALU.add, [nm_src], [nm_dst])
        el("act", lambda e: e.activation(out=dst, in_=dst, func=AF.Ln), [nm_dst], [nm_dst])
        el("act", lambda e: e.activation(out=dst, in_=dst, func=AF.Exp, scale=-0.5), [nm_dst], [nm_dst])

    def mm(outp, lhsT, rhs, start, stop, r, w, sig=None):
        P.op("pe", lambda e: e.matmul(outp, lhsT=lhsT, rhs=rhs, start=start, stop=stop), reads=r, writes=w,
             sig=(stop if sig is None else sig))

    acc_rr = [0]

    def acc_bank():
        acc_rr[0] += 1
        return (pA, "pA") if acc_rr[0] % 2 else (pB, "pB")

    out_toks = []

    class StopBuild(Exception):
        pass

    def chk(level):
        if dbg_stop == level:
            raise StopBuild()

    prefetched = [False]

    def do_tile(xsrc, row0, main, ti, nxt=None):
        def issue_x(src_, r0, sub):
            xb2, xn2 = xs[sub % 2], "xs%d" % (sub % 2)
            P.dma(lambda e: e.dma_start(out=xb2[:], in_=src_[r0 + sub * 128: r0 + (sub + 1) * 128, :]), writes=[xn2])

        for sub in range(NCH):
            xb_, xnm = xs[sub % 2], "xs%d" % (sub % 2)
            if not (prefetched[0] and sub < 2):
                issue_x(xsrc, row0, sub)
            el("act", lambda e, xb_=xb_: e.activation(out=junk[:], in_=xb_[:], func=AF.Square, accum_out=ss[:]), [xnm], ["junk", "ss"])
            rsqrt_col(rstd[:], ss[:], 1.0 / 1024.0, "ss", "rstd")
            ts(xnb[:], xb_[:], rstd[:, 0:1], None, ALU.mult, None, [xnm, "rstd"], ["xnb"])
            for kt in range(8):
                P.op("pe", lambda e, kt=kt: e.transpose(out=pT[:, kt * 128:(kt + 1) * 128], in_=xnb[:, kt * 128:(kt + 1) * 128], identity=ident_b[:]),
                     reads=["xnb", "ident_b"], writes=["pT"], sig=(kt == 7))
            el("act", lambda e, sub=sub: e.copy(out=xnT[:, :, sub * 128:(sub + 1) * 128], in_=pT[:].rearrange("p (k t) -> p k t", k=8)),
               ["pT"], ["xnT"])

        prefetched[0] = False
        if nxt is not None and NCH <= 2:
            for sub in range(min(2, NCH)):
                issue_x(nxt[0], nxt[1], sub)
            prefetched[0] = True
        chk(2)

        def proj_fm(blk, ncolt, evac):
            W, wn = load_block(blk)
            Wv_ = W[:].rearrange("p (k n) -> p k n", k=8)
            for ct in range(ncolt):
                ps_, pn = acc_bank()
                for kt in range(8):
                    mm(ps_[:, 0:T], Wv_[:, kt, ct * 128:(ct + 1) * 128], xnT[:, kt, :], kt == 0, kt == 7, [wn, "xnT"], [pn])
                evac(ct, ps_, pn)

        el("dve", lambda e: e.tensor_copy(out=uaT[:, :, 1:4], in_=uaT[:, :, T + 1:T + 4]), ["uaT"], ["uaT"])
        for half in range(2):
            proj_fm(BLK_UA + half, 4, lambda ct, ps_, pn, half=half: el(
                "act", lambda e: e.copy(out=uaT[:, half * 4 + ct, 4:T + 4], in_=ps_[:, 0:T]), [pn], ["uaT"]))
        chk(3)
        for ch in range(NCH):
            for kt in range(8):
                mm(pM[:, 0:8], xnT[:, kt, ch * 128:(ch + 1) * 128], Wif[:, kt, :], kt == 0, kt == 7, ["xnT", "Wif"], ["pM"])
            tt(gates[:, ch, :], pM[:, 0:8], bif[:], ALU.add, ["pM", "bif"], ["gates"])
        chk(4)
        W, wn = load_block(BLK_UB)
        Wv_ = W[:].rearrange("p (k n) -> p k n", k=8)
        for j in range(8):
            ps_, pn = acc_bank()
            for kt in range(8):
                lhs = xnT[:, kt, :].rearrange("p (c j) -> p j c", j=8)[:, j, :]
                mm(ps_[0:NC8, :], lhs, Wv_[:, kt, :], kt == 0, kt == 7, [wn, "xnT"], [pn])
            el("act" if j % 2 else "dve",
               (lambda e, ps_=ps_, j=j: e.copy(out=Utm[0:NC8, :, j, :], in_=ps_[0:NC8, :].rearrange("c (g n) -> c g n", g=32))) if j % 2 else
               (lambda e, ps_=ps_, j=j: e.tensor_copy(out=Utm[0:NC8, :, j, :], in_=ps_[0:NC8, :].rearrange("c (g n) -> c g n", g=32))), [pn], ["Utm"])
        chk(5)
        for g in range(32):
            P.op("pe", lambda e, g=g: e.transpose(out=pT[:, g * NC8:(g + 1) * NC8], in_=Utm[0:NC8, g, :, :].rearrange("c j n -> c (j n)"),
                                                  identity=ident_b[0:NC8, 0:NC8]), reads=["Utm", "ident_b"], writes=["pT"], sig=(g == 31))
        el("act", lambda e: e.copy(out=U2[:].rearrange("p g c -> p (g c)"), in_=pT[:, 0:32 * NC8]), ["pT"], ["U2"])
        W1r, w1rn = load_block(BLK_S5)
        W1i, w1in = load_block(BLK_S5 + 1)
        for ri, (Wt, wn_) in enumerate(((W1r, w1rn), (W1i, w1in))):
            Wg = Wt[:].rearrange("p (g n) -> p g n", g=32)
            for q in range(16):
                for g2 in range(2):
                    g = 2 * q + g2
                    mm(pM[:, (q * NC8):(q + 1) * NC8], Wg[:, g, :], U2[:, g, :], g2 == 0, g2 == 1, [wn_, "U2"], ["pM"], sig=(q == 15 and g2 == 1))
            el("act" if ri == 0 else "dve",
               (lambda e, ri=ri: e.copy(out=Xall[:, :, ri, :].rearrange("p c q -> p q c"), in_=pM[:, 0:16 * NC8].rearrange("p (q c) -> p q c", q=16))) if ri == 0 else
               (lambda e, ri=ri: e.tensor_copy(out=Xall[:, :, ri, :].rearrange("p c q -> p q c"), in_=pM[:, 0:16 * NC8].rearrange("p (q c) -> p q c", q=16))),
               ["pM"], ["Xall"])
        chk(6)
        T1 = sbt([128, 2, 16], F32, "scT1_%d" % ti) if False else None
        for c in range(NC8):
            sp_ = Sall[:, c, :, :]
            sn_ = Sall[:, c + 1, :, :]
            P.op("pool", lambda e, sp_=sp_: e.tensor_tensor(out=sc_t1[:], in0=sp_, in1=AR32[:], op=ALU.mult), reads=["Sall"], writes=["sc_t1"])
            P.op("pool", lambda e, c=c: e.tensor_tensor(out=sc_t2[:, 0, :], in0=Sall[:, c, 1, :], in1=ANI[:], op=ALU.mult), reads=["Sall"], writes=["sc_t2"])
            P.op("pool", lambda e, c=c: e.tensor_tensor(out=sc_t2[:, 1, :], in0=Sall[:, c, 0, :], in1=API[:], op=ALU.mult), reads=["Sall"], writes=["sc_t2"])
            P.op("pool", lambda e, c=c: e.tensor_tensor(out=sc_t1[:], in0=sc_t1[:], in1=Xall[:, c, :, :], op=ALU.add), reads=["sc_t1", "Xall"], writes=["sc_t1"])
            P.op("pool", lambda e, sn_=sn_: e.tensor_tensor(out=sn_, in0=sc_t1[:], in1=sc_t2[:], op=ALU.add), reads=["sc_t1", "sc_t2"], writes=["Sall"])
        if main:
            for ri in range(2):
                el("pool", lambda e, ri=ri: e.tensor_copy(out=Sbf[:, ri, :, :], in_=Sall[:, 0:NC8, ri, :].rearrange("p c q -> p q c")),
                   ["Sall"], ["Sbf"])
        el("pool", lambda e: e.tensor_copy(out=Sall[:, 0, :, :], in_=Sall[:, NC8, :, :]), ["Sall"], ["Sall"])
        def s5_out():
            proj_fm(BLK_ZB, 4, lambda ct, ps_, pn: el("act", lambda e: e.activation(out=zbT[:, ct, :], in_=ps_[:, 0:T], func=AF.Silu), [pn], ["zbT"]))
            Wi_, win_ = load_block(BLK_S5 + 2)
            Wr_, wrn_ = load_block(BLK_S5 + 3)
            Wm_, wmn_ = load_block(BLK_S5 + 4)
            Wi_g = Wi_[:].rearrange("p (g n) -> p g n", g=32); Wr_g = Wr_[:].rearrange("p (g n) -> p g n", g=32)
            Wm_g = Wm_[:].rearrange("p (g n) -> p g n", g=32)
            GP = 512 // NC8
            for g0 in range(0, 32, GP):
                ng = min(GP, 32 - g0)
                for gi in range(ng):
                    g = g0 + gi
                    o_ = pM[:, gi * NC8:(gi + 1) * NC8]
                    mm(o_, Wi_g[:, g, :], U2[:, g, :], True, False, [win_, "U2"], ["pM"], sig=False)
                    mm(o_, Wr_g[:, g, :], Sbf[:, 0, g // 2, :], False, False, [wrn_, "Sbf"], ["pM"], sig=False)
                    mm(o_, Wm_g[:, g, :], Sbf[:, 1, g // 2, :], False, True, [wmn_, "Sbf"], ["pM"], sig=(gi == ng - 1))
                el("act", lambda e, g0=g0, ng=ng: e.activation(out=Yg[:, g0:g0 + ng, :].rearrange("p g c -> p (g c)"), in_=pM[:, 0:ng * NC8],
                                                               func=AF.Gelu_apprx_tanh), ["pM"], ["Yg"])
            for g0 in range(0, 32, 8):
                for gi in range(8):
                    g = g0 + gi
                    P.op("pe", lambda e, g=g, gi=gi: e.transpose(out=pT[0:NC8, gi * 128:(gi + 1) * 128], in_=Yg[:, g, :], identity=ident_b[:]),
                         reads=["Yg", "ident_b"], writes=["pT"], sig=(gi == 7))
                el("act", lambda e, g0=g0: e.copy(out=Ytm[0:NC8, :, 16 * g0:16 * g0 + 128].rearrange("c j (g n) -> c g j n", g=8),
                                                  in_=pT[0:NC8, :].rearrange("c (g j n) -> c g j n", g=8, j=8)), ["pT"], ["Utm"])
            for ct in range(4):
                for j in range(8):
                    P.op("pe", lambda e, ct=ct, j=j: e.transpose(out=pT[:, j * NC8:(j + 1) * NC8], in_=Ytm[0:NC8, j, ct * 128:(ct + 1) * 128],
                                                                 identity=ident_b[0:NC8, 0:NC8]), reads=["Utm", "ident_b"], writes=["pT"], sig=(j == 7))
                el("act", lambda e, ct=ct: e.copy(out=yT[:, ct, :], in_=pT[:, 0:T]), ["pT"], ["yT"])
            for ot_ in range(4):
                ps_, pn = acc_bank()
                for ct in range(4):
                    mm(ps_[:, 0:T], Wglu[:, ct, ot_ * 128:(ot_ + 1) * 128], yT[:, ct, :], ct == 0, ct == 3, ["Wglu", "yT"], [pn])
                el("act", lambda e, ps_=ps_, ot_=ot_: e.activation(out=sgl[:], in_=ps_[:, 0:T], func=AF.Sigmoid, bias=bglu[:, ot_:ot_ + 1]),
                   [pn, "bglu"], ["sgl"])
                tt(xg[:], sgl[:], yT[:, ot_, :], ALU.mult, ["sgl", "yT"], ["xg"])
                tt(hbT[:, ot_, :].rearrange("p (j c) -> p j c", j=8), xg[:].rearrange("p (j c) -> p j c", j=8),
                   zbT[:, ot_, :].rearrange("p (c j) -> p j c", j=8), ALU.mult, ["xg", "zbT"], ["hbT"])
        chk(7)
        for mt in range(8):
            ps_, pn = acc_bank()
            for k in range(4):
                mm(ps_[:, 0:T], cdiag[:, mt, k, :], uaT[:, mt, k + 1:k + 1 + T], k == 0, k == 3, ["cdiag", "uaT"], [pn])
            el("act", lambda e, ps_=ps_, mt=mt: e.activation(out=cT[:, mt, :], in_=ps_[:, 0:T], func=AF.Silu, bias=cb[:, mt:mt + 1]),
               [pn, "cb"], ["cT"])
        if main:
            for half in range(2):
                proj_fm(BLK_ZA + half, 4, lambda ct, ps_, pn, half=half: el(
                    "act", lambda e: e.activation(out=zaT[:, half * 4 + ct, :], in_=ps_[:, 0:T], func=AF.Silu), [pn], ["zaT"]))
            for half in range(2):
                proj_fm(BLK_OA + half, 4, lambda ct, ps_, pn, half=half: el(
                    "act", lambda e: e.activation(out=oaT[:, half * 4 + ct, :], in_=ps_[:, 0:T], func=AF.Sigmoid), [pn], ["oaT"]))
        if main:
            for mt in range(8):
                el("dve", lambda e, mt=mt: e.scalar_tensor_tensor(out=csz[:, mt, :], in0=cT[:, mt, :], scalar=skp[:, mt:mt + 1], in1=zaT[:, mt, :],
                                                                  op0=ALU.mult, op1=ALU.mult), ["cT", "skp", "zaT"], ["csz"])
            for mt in range(8):
                ts(zaT[:, mt, :], zaT[:, mt, :], hg[:, mt:mt + 1], None, ALU.mult, None, ["zaT", "hg", "csz"], ["zaT"])
        chk(8)
        for h in range(4):
            if main:
                for (Wx, wxn, dst, dn) in ((Wq, "Wqkv0", qT, "qT"), (Wk, "Wqkv1", kT, "kT")):
                    for et in range(2):
                        ps_, pn = acc_bank()
                        for d in range(2):
                            mm(ps_[:, 0:T], Wx[:, h, d, et * 128:(et + 1) * 128], cT[:, 2 * h + d, :], d == 0, d == 1, [wxn, "cT"], [pn])
                        el("act" if et else "dve",
                           (lambda e, ps_=ps_, dst=dst, h=h, et=et: e.copy(out=dst[:, h, et, :], in_=ps_[:, 0:T])) if et else
                           (lambda e, ps_=ps_, dst=dst, h=h, et=et: e.tensor_copy(out=dst[:, h, et, :], in_=ps_[:, 0:T])), [pn], [dn])
        el("act", lambda e: e.activation(out=e1a[:], in_=gates[:, :, 4:8], func=AF.Exp, scale=-1.0), ["gates"], ["e1a"])
        el("act", lambda e: e.activation(out=lfa[:], in_=e1a[:], func=AF.Ln, bias=1.0), ["e1a"], ["lfa"])
        ts(lfa[:], lfa[:], -1.0, None, ALU.mult, None, ["lfa"], ["lfa"])
        for ch in range(NCH):
            tsl = slice(ch * 128, (ch + 1) * 128)
            for h in range(4):
                el("dve", lambda e, h=h, ch=ch: e.tensor_copy(out=lfrep[:, h, :], in_=lfa[:, ch, h:h + 1].to_broadcast([128, 128])), ["lfa"], ["lfrep"])
            for h in range(4):
                mm(pGD[:, h * 128:(h + 1) * 128], lfrep[:, h, :], maskT[:], True, True, ["lfrep", "maskT"], ["pGD"], sig=(h == 3))
            for h in range(4):
                mm(pM[:, h * 128:(h + 1) * 128], maskT[:], lfrep[:, h, :], True, True, ["maskT", "lfrep"], ["pM"], sig=(h == 3))
            en, dn, wn_, wbn, wrn = "emb%d" % ch, "dec%d" % ch, "wcol%d" % ch, "wcolb%d" % ch, "wrep%d" % ch
            el("act", lambda e, ch=ch: e.activation(out=emb4[:, ch, :, :].rearrange("p h t -> p (h t)"), in_=pGD[:], func=AF.Exp, scale=-1.0), ["pGD"], [en])
            el("dve", lambda e, ch=ch: e.reciprocal(out=dec4[:, ch, :], in_=emb4[:, ch, :, 127]), [en], [dn])
            tt(bcol[:], gates[:, ch, 0:4], pM[:].rearrange("p (h t) -> p h t", h=4)[:, :, 0], ALU.subtract, ["gates", "pM"], ["bcol"])
            el("act", lambda e, ch=ch: e.activation(out=wcol4[:, ch, :], in_=bcol[:], func=AF.Exp), ["bcol"], [wn_])
            el("dve", lambda e, ch=ch: e.tensor_copy(out=vw[:, ch, :, 256], in_=wcol4[:, ch, :]), [wn_], ["vw"])
            if main:
                for h in range(4):
                    el("dve", lambda e, h=h, ch=ch: e.tensor_copy(out=wrep4[:, ch, h, :], in_=wcol4[:, ch, h:h + 1].to_broadcast([128, 128])), [wn_], [wrn])
            for h in range(4):
                ps_, pn = acc_bank()
                for d in range(2):
                    mm(ps_[:, 0:256], cT[:, 2 * h + d, tsl], Wk[:, h, d, :], d == 0, d == 1, ["cT", "Wqkv1"], [pn], sig=False)
                for d in range(2):
                    mm(ps_[:, 256:512], uaT[:, 2 * h + d, 4 + ch * 128:4 + (ch + 1) * 128], Wv[:, h, d, :], d == 0, d == 1, ["uaT", "Wqkv2"], [pn])
                el("act", lambda e, ps_=ps_, ch=ch, h=h: e.copy(out=ktm[:, ch, h, :], in_=ps_[:, 0:256]), [pn], ["ktm"])
                el("act", lambda e, ps_=ps_, ch=ch, h=h: e.activation(out=vw[:, ch, h, 0:256], in_=ps_[:, 256:512], func=AF.Copy, scale=wcol4[:, ch, h:h + 1]),
                   [pn, wn_], ["vw"])
        for ch in range(NCH):
            tsl = slice(ch * 128, (ch + 1) * 128)
            en, dn, wn_, wbn, wrn = "emb%d" % ch, "dec%d" % ch, "wcol%d" % ch, "wcolb%d" % ch, "wrep%d" % ch
            if main:
                for h in range(4):
                    for et in range(2):
                        mm(pS[:, h * 128:(h + 1) * 128], kT[:, h, et, tsl], qT[:, h, et, tsl], et == 0, et == 1, ["kT", "qT"], ["pS"], sig=(h == 3 and et == 1))
                tt(SmT[:].rearrange("p h t -> p (h t)"), pS[:], mask4[:].rearrange("p h t -> p (h t)"), ALU.mult, ["pS", "mask4"], ["SmT"])
                for h in range(4):
                    for d2 in range(2):
                        pn_, pnn = (pN0, "pN0") if h < 2 else (pN1, "pN1")
                        o_ = pn_[:, ((h % 2) * 2 + d2) * 128:((h % 2) * 2 + d2 + 1) * 128]
                        mm(o_, vw[:, ch, h, d2 * 128:(d2 + 1) * 128], SmT[:, h, :], True, False, ["vw", "SmT"], [pnn], sig=False)
                        mm(o_, Cbf[:, h, 0, d2 * 128:(d2 + 1) * 128], qT[:, h, 0, tsl], False, False, ["Cbf%d" % h, "qT"], [pnn], sig=False)
                        mm(o_, Cbf[:, h, 1, d2 * 128:(d2 + 1) * 128], qT[:, h, 1, tsl], False, True, ["Cbf%d" % h, "qT"], [pnn], sig=(h % 2 == 1 and d2 == 1))
                    o_ = pGD[:, h * 128:(h + 1) * 128]
                    mm(o_, wrep4[:, ch, h, :], SmT[:, h, :], True, False, [wrn, "SmT"], ["pGD"], sig=False)
                    mm(o_, nrep[:, h, 0, :], qT[:, h, 0, tsl], False, False, ["nrep%d" % h, "qT"], ["pGD"], sig=False)
                    mm(o_, nrep[:, h, 1, :], qT[:, h, 1, tsl], False, True, ["nrep%d" % h, "qT"], ["pGD"], sig=(h == 3))
            if main:
                el("act", lambda e: e.activation(out=aden[:].rearrange("p h t -> p (h t)"), in_=pGD[:], func=AF.Abs), ["pGD"], ["aden"])
                tt(aden[:], aden[:], emb4[:, ch, :, :], ALU.max, ["aden", en], ["aden"])
                el("act", lambda e: e.activation(out=aden[:], in_=aden[:], func=AF.Ln), ["aden"], ["aden"])
                el("act", lambda e: e.activation(out=rden[:], in_=aden[:], func=AF.Exp, scale=-1.0), ["aden"], ["rden"])
            for h in range(4):
                cn, bn, nn = "Cst%d" % h, "Cbf%d" % h, "nrep%d" % h
                for et in range(2):
                    ps_, pn = acc_bank()
                    mm(ps_[:, 0:258], ktm[:, ch, h, et * 128:(et + 1) * 128], vw[:, ch, h, 0:258], True, True, ["ktm", "vw"], [pn])
                    tt(Cst[:, h, et, 0:258], Cst[:, h, et, 0:258], ps_[:, 0:258], ALU.add, [cn, pn, bn, nn], [cn])
                el("act", lambda e, h=h, ch=ch: e.activation(out=Cst[:, h, :, :], in_=Cst[:, h, :, :], func=AF.Copy, scale=dec4[:, ch, h:h + 1]), [cn, dn], [cn])
                el("act", lambda e, h=h: e.copy(out=Cbf[:, h, :, :], in_=Cst[:, h, :, :]), [cn], [bn])
                for et in range(2):
                    el("dve", lambda e, h=h, et=et: e.tensor_copy(out=nrep[:, h, et, :], in_=Cst[:, h, et, 256:257].to_broadcast([128, 128])),
                       [cn], [nn])
            if main:
                for hp, (pn_, pnn) in enumerate(((pN0, "pN0"), (pN1, "pN1"))):
                    tt(hT[:, 4 * hp:4 * hp + 4, :].rearrange("p (h d) t -> p h d t", h=2), pn_[:].rearrange("p (h d t) -> p h d t", h=2, d=2),
                       rden[:, 2 * hp:2 * hp + 2, :].unsqueeze(2).to_broadcast([128, 2, 2, 128]), ALU.mult, [pnn, "rden"], ["hT"])
                tt(hT[:], hT[:], oaT[:, :, tsl], ALU.mult, ["hT", "oaT"], ["hT"])
                el("act", lambda e: e.activation(out=sq[:], in_=hT[:], func=AF.Square), ["hT"], ["sq"])
                for h in range(4):
                    for d2 in range(2):
                        mm(pS[:, h * 128:(h + 1) * 128], ones_b[:], sq[:, 2 * h + d2, :], d2 == 0, d2 == 1, ["ones_b", "sq", "SmT"], ["pS"], sig=(h == 3 and d2 == 1))
                ts(rsh[:].rearrange("p h t -> p (h t)"), pS[:], 1.0 / 256.0, 1e-6, ALU.mult, ALU.add, ["pS"], ["rsh"])
                el("act", lambda e: e.activation(out=rsh[:], in_=rsh[:], func=AF.Ln), ["rsh"], ["rsh"])
                el("act", lambda e: e.activation(out=rsh[:], in_=rsh[:], func=AF.Exp, scale=-0.5), ["rsh"], ["rsh"])
                tt(hn[:].rearrange("p (h d) t -> p h d t", h=4), hT[:].rearrange("p (h d) t -> p h d t", h=4),
                   rsh[:].unsqueeze(2).to_broadcast([128, 4, 2, 128]), ALU.mult, ["hT", "rsh"], ["hn"])
                tt(hn[:], hn[:], zaT[:, :, tsl], ALU.mult, ["hn", "zaT"], ["hn"])
                tt(hfT[:, :, tsl], hn[:], csz[:, :, tsl], ALU.add, ["hn", "csz"], ["hfT"])
        chk(9)
        if not main:
            return
        s5_out()
        def gate_blocks(br):
            for bi_ in range(2):
                Wg_, wgn = load_block(BLK_G + 2 * br + bi_)
                Wgv = Wg_[:].rearrange("p (k n) -> p k n", k=8)
                for c4 in range(4):
                    ft = bi_ * 4 + c4
                    psg, png = acc_bank()
                    for kt in range(8):
                        mm(psg[:, 0:T], Wgv[:, kt, c4 * 128:(c4 + 1) * 128], xnT[:, kt, :], kt == 0, kt == 7, [wgn, "xnT"], [png])
                    el("act", lambda e, psg=psg, ft=ft: e.activation(out=gAll[:, ft, :], in_=psg[:, 0:T], func=AF.Sigmoid), [png], ["oaT"])

        gate_blocks(0)
        for hf_ in range(2):
            Wa, wan = load_block(BLK_AO + hf_)
            Wav = Wa[:].rearrange("p (k n) -> p k n", k=8)
            for c4 in range(4):
                ft = hf_ * 4 + c4
                psa, pna = acc_bank()
                for kt in range(8):
                    mm(psa[:, 0:T], Wav[:, kt, c4 * 128:(c4 + 1) * 128], hfT[:, kt, :], kt == 0, kt == 7, [wan, "hfT"], [pna])
                tt(mrgT[:, ft, :], psa[:, 0:T], gAll[:, ft, :], ALU.mult, [pna, "oaT"], ["zaT"])
        gate_blocks(1)
        Wbo, wbon = load_block(BLK_BO)
        Wbv = Wbo[:].rearrange("p (k n) -> p k n", k=4)
        for ft in range(8):
            psb, pnb = acc_bank()
            for kt in range(4):
                mm(psb[:, 0:T], Wbv[:, kt, ft * 128:(ft + 1) * 128], hbT[:, kt, :], kt == 0, kt == 3, [wbon, "hbT"], [pnb])
            el("act", lambda e, psb=psb: e.copy(out=m2[:].rearrange("p (c j) -> p j c", j=8), in_=psb[:, 0:T].rearrange("p (j c) -> p j c", j=8)),
               [pnb], ["m2"])
            tt(m2[:], m2[:], gAll[:, ft, :], ALU.mult, ["m2", "oaT"], ["m2"])
            tt(mrgT[:, ft, :], m2[:], mrgT[:, ft, :], ALU.add, ["m2", "zaT"], ["zaT"])
        Wo0, wo0n = load_block(BLK_WO); Wo1, wo1n = load_block(BLK_WO + 1)
        for ch in range(NCH):
            tsl = slice(ch * 128, (ch + 1) * 128)
            for hf_, (Wo_, won, pn_, pnn) in enumerate(((Wo0, wo0n, pN0, "pN0"), (Wo1, wo1n, pN1, "pN1"))):
                Wov = Wo_[:].rearrange("p (k n) -> p k n", k=8)
                for kt in range(8):
                    mm(pn_[:, :], mrgT[:, kt, tsl], Wov[:, kt, :], kt == 0, kt == 7, ["zaT", won], [pnn])
            el("act", lambda e: e.activation(out=junk[:, 0:512], in_=pN0[:], func=AF.Square, accum_out=ss2[:]), ["pN0"], ["junk", "ss2"])
            el("act", lambda e: e.activation(out=junk[:, 512:1024], in_=pN1[:], func=AF.Square, accum_out=rstd2[:]), ["pN1"], ["junk", "rstd2"])
            tt(ss2[:], ss2[:], rstd2[:], ALU.add, ["ss2", "rstd2"], ["ss2"])
            rsqrt_col(rstd2[:], ss2[:], 1.0 / 1024.0, "ss2", "rstd2")
            ob, obn = ot[ch % 2], "ot%d" % (ch % 2)
            xb_, xnm = xr[ch % 2], "xr%d" % (ch % 2)
            P.dma(lambda e, xb_=xb_, ch=ch: e.dma_start(out=xb_[:], in_=xsrc[row0 + ch * 128: row0 + (ch + 1) * 128, :]), writes=[xnm])
            for hf_, (pn_, pnn) in enumerate(((pN0, "pN0"), (pN1, "pN1"))):
                cs = slice(hf_ * 512, hf_ * 512 + 512)
                el("dve", lambda e, ob=ob, pn_=pn_, cs=cs: e.scalar_tensor_tensor(out=ob[:, cs], in0=pn_[:], scalar=rstd2[:, 0:1], in1=gpost[:, cs],
                                                                                 op0=ALU.mult, op1=ALU.mult), [pnn, "rstd2", "gpost"], [obn])
            tt(ob[:], ob[:], xb_[:], ALU.add, [obn, xnm], [obn])
            tok = P.dma(lambda e, ob=ob, ch=ch: e.dma_start(out=out[row0 + ch * 128: row0 + (ch + 1) * 128, :], in_=ob[:]), reads=[obn])
            out_toks.append(tok)

    sc_t1 = sbt([128, 2, 16], F32, "sc_t1"); sc_t2 = sbt([128, 2, 16], F32, "sc_t2")
    try:
        for ti in range(NPRE):
            nxt = (x_pre, (ti + 1) * T) if ti + 1 < NPRE else (x_main, 0)
            do_tile(x_pre, ti * T, False, ti, nxt)
    except StopBuild:
        tk = P.dma(lambda e: e.dma_start(out=out[0:128, :], in_=gpost[:]), reads=["gpost"])
        P.final_wait("sp", [tk])
        P.emit(); P.close()
        return nc
    if NPRE > 0:
        for h in range(4):
            cn, bn, nn = "Cst%d" % h, "Cbf%d" % h, "nrep%d" % h
            ts(Cst[:, h, :, :], Cst[:, h, :, :], flg[:, 0:1], None, ALU.mult, None, [cn, "flg"], [cn])
            el("act", lambda e, h=h: e.copy(out=Cbf[:, h, :, :], in_=Cst[:, h, :, :]), [cn], [bn])
            for et in range(2):
                el("dve", lambda e, h=h, et=et: e.tensor_copy(out=nrep[:, h, et, :], in_=Cst[:, h, et, 256:257].to_broadcast([128, 128])),
                   [cn], [nn])
    for ti in range(NMAIN):
        nxt = (x_main, (ti + 1) * T) if ti + 1 < NMAIN else None
        do_tile(x_main, ti * T, True, NPRE + ti, nxt)
    P.final_wait("sp", out_toks)
    P.emit()
    P.close()
    return nc


T_TILE = 256
_cache = {}


def kernel(**inputs):
    x = np.ascontiguousarray(inputs["x"], dtype=np.float32)
    Bsz, L, Dm = x.shape
    half = L // 2
    npre = half // T_TILE
    nmain = half // T_TILE
    key = (T_TILE, npre, nmain)
    if key not in _cache:
        _cache[key] = build_program(T_TILE, npre, nmain)
    nc = _cache[key]
    shared = {}
    for k, v in inputs.items():
        if k == "x":
            continue
        a = np.ascontiguousarray(np.asarray(v, dtype=np.float32)[0])
        if k in ("b_i", "b_f", "log_dt", "norm_post_g"):
            a = a.reshape(1, -1)
        shared[k] = a
    in_maps = []
    zeros = np.zeros((half, Dm), np.float32)
    for core in range(8):
        b, hf = core // 2, core % 2
        m = dict(shared)
        m["x_main"] = np.ascontiguousarray(x[b, hf * half:(hf + 1) * half])
        m["x_pre"] = zeros if hf == 0 else np.ascontiguousarray(x[b, 0:half])
        m["flag"] = np.full((128, 1), float(hf), np.float32)
        in_maps.append(m)
    res = run_bass_kernel_spmd(nc, in_maps, core_ids=list(range(8)))
    outp = np.empty((Bsz, L, Dm), np.float32)
    for core in range(8):
        b, hf = core // 2, core % 2
        outp[b, hf * half:(hf + 1) * half] = res.results[core]["out"]
    return outp
```

```python
import contextlib
import numpy as np
import concourse.bass as bass
import concourse.mybir as mybir
from concourse.bass_utils import run_bass_kernel_spmd

F32 = mybir.dt.float32
BF16 = mybir.dt.bfloat16
AF = mybir.ActivationFunctionType
ALU = mybir.AluOpType
COMPUTE = ("pe", "act", "dve", "pool")
NDMA_SLOTS = 8
DEBUG_TAGS = False
PI = float(np.pi)


class Prog:
    def __init__(self, nc):
        self.nc = nc
        self.stack = contextlib.ExitStack()
        self.engs = ("pe", "act", "dve", "pool", "sp")
        self.ops = {e: [] for e in self.engs}
        self.waited = {e: {} for e in self.engs}
        self.res = {}
        self.dma_use = {}
        self.dma_rr = {e: 0 for e in self.engs}
        self.sems = {}
        self.base = {e: 0 for e in COMPUTE}
        self.temp = None

    def sb(self, name, shape, dt):
        st = self.temp if self.temp is not None else self.stack
        return st.enter_context(self.nc.sbuf_tensor(name, list(shape), dt))

    def ps(self, name, shape, dt):
        return self.stack.enter_context(self.nc.psum_tensor(name, list(shape), dt))

    def _sem(self, key):
        if key not in self.sems:
            nm = "s_" + "_".join(str(k) for k in (key if isinstance(key, tuple) else (key,)))
            self.sems[key] = self.stack.enter_context(self.nc.semaphore(nm))
        return self.sems[key]

    def _deps(self, eng, reads, writes):
        deps = {}

        def add(tok):
            if tok is None:
                return
            k, v = tok
            if k == "pe" and eng == "pe":
                return
            if deps.get(k, -1) < v:
                deps[k] = v

        for r in reads:
            st = self.res.get(r)
            if st:
                for k, v in st[0].items():
                    add((k, v))
        for w in writes:
            st = self.res.get(w)
            if st:
                for k, v in st[0].items():
                    add((k, v))
                for k, v in st[1].items():
                    add((k, v))
        out = []
        wd = self.waited[eng]
        for k, v in deps.items():
            if wd.get(k, -1) >= v:
                continue
            wd[k] = v
            out.append((k, v))
        return out

    def _commit(self, tok, reads, writes):
        k, v = tok
        for r in reads:
            st = self.res.setdefault(r, [{}, {}])
            if st[1].get(k, -1) < v:
                st[1][k] = v
        for w in writes:
            old = self.res.get(w)
            wr = {}
            if old is not None and k not in COMPUTE:
                wr = {k2: v2 for k2, v2 in old[0].items() if k2 not in COMPUTE}
            wr[k] = v
            self.res[w] = [wr, {}]

    def _tag(self):
        if not DEBUG_TAGS:
            return None
        import sys as _sys
        f = _sys._getframe(2)
        while f is not None and f.f_code.co_name not in ("do_tile", "build_program", "s5_out", "gate_blocks", "proj_fm"):
            f = f.f_back
        return str(f.f_lineno) if f is not None else None

    def op(self, eng, fn, reads=(), writes=(), sig=True):
        waits = self._deps(eng, reads, writes)
        idx = len(self.ops[eng])
        self.ops[eng].append(dict(fn=fn, waits=waits, sig=sig, dma=None, tag=self._tag()))
        tok = (eng, idx)
        self._commit(tok, reads, writes)
        return tok

    def dma(self, fn, reads=(), writes=(), q="sp"):
        waits = self._deps(q, reads, writes)
        slot = self.dma_rr[q] % NDMA_SLOTS
        self.dma_rr[q] += 1
        key = ("d", q, slot)
        n = self.dma_use.get(key, 0)
        if n > 0:
            prev = n * 16
            if self.waited[q].get(key, -1) < prev:
                self.waited[q][key] = prev
                waits.append((key, prev))
        self.dma_use[key] = n + 1
        tok = (key, (n + 1) * 16)
        self.ops[q].append(dict(fn=fn, waits=waits, sig=False, dma=key))
        self._commit(tok, reads, writes)
        return tok

    def final_wait(self, eng, toks):
        self.ops[eng].append(dict(fn=None, waits=list(toks), sig=False, dma=None))

    def emit(self):
        nc = self.nc
        sigcount = {}
        totals = {}
        for e in COMPUTE:
            c = self.base[e]
            arr = []
            for o in self.ops[e]:
                if o["sig"]:
                    c += 1
                arr.append(c)
            need = [None] * len(arr)
            nxt = None
            for i in range(len(arr) - 1, -1, -1):
                if self.ops[e][i]["sig"]:
                    nxt = arr[i]
                need[i] = nxt
            sigcount[e] = need
            totals[e] = c
            self._sem(e)
        for k in self.dma_use:
            self._sem(k)

        def resolve(k, v):
            if k in COMPUTE:
                val = sigcount[k][v]
                assert val is not None, (k, v)
                return self.sems[k], val
            return self.sems[k], v

        with nc.Block() as block:

            def run(eng_name, eng):
                for o in self.ops[eng_name]:
                    for k, v in o["waits"]:
                        s, val = resolve(k, v)
                        eng.wait_ge(s, val)
                    if o["fn"] is None:
                        continue
                    ins = o["fn"](eng)
                    if o.get("tag"):
                        ins.annotate(o["tag"])
                    if o["dma"] is not None:
                        ins.then_inc(self.sems[o["dma"]], 16)
                    elif o["sig"]:
                        ins.then_inc(self.sems[eng_name], 1)
                for o2 in COMPUTE:
                    if o2 != eng_name and totals[o2] > 0:
                        eng.wait_ge(self.sems[o2], totals[o2])
                for k, n in self.dma_use.items():
                    eng.wait_ge(self.sems[k], n * 16)

            @block.tensor
            def _(e):
                run("pe", e)

            @block.scalar
            def _(e):
                run("act", e)

            @block.vector
            def _(e):
                run("dve", e)

            @block.gpsimd
            def _(e):
                run("pool", e)

            @block.sync
            def _(e):
                run("sp", e)

        self.base = totals
        self.ops = {e: [] for e in self.engs}
        self.waited = {e: {} for e in self.engs}
        self.res = {}

    def close(self):
        self.stack.close()


NBLK = 22
BLK_UA, BLK_UB, BLK_ZB, BLK_ZA, BLK_OA, BLK_G, BLK_AO, BLK_BO, BLK_WO, BLK_S5 = 0, 2, 3, 4, 6, 8, 12, 14, 15, 17
COL_UA, COL_ZA, COL_OA, COL_I, COL_UB, COL_ZB, COL_G = 0, 1024, 2048, 3072, 3080, 3592, 4104


def build_program(T, NPRE, NMAIN, dbg_stop=0):
    NCH = T // 128
    NC8 = T // 8
    nc = bass.Bass("TRN2", target_bir_lowering=False)
    dram = {}

    def din(name, shape):
        dram[name] = nc.dram_tensor(name, list(shape), F32, kind="ExternalInput").ap()
        return dram[name]

    x_pre = din("x_pre", [max(NPRE, 1) * T, 1024])
    x_main = din("x_main", [NMAIN * T, 1024])
    flag = din("flag", [128, 1])
    norm_pre_g = din("norm_pre_g", [1024]); w_in = din("w_in", [1024, 6152])
    conv_w = din("conv_w", [4, 1024]); conv_b = din("conv_b", [1024])
    w_q = din("w_q", [4, 256, 256]); w_k = din("w_k", [4, 256, 256]); w_v = din("w_v", [4, 256, 256])
    b_i = din("b_i", [1, 4]); b_f = din("b_f", [1, 4]); head_g = din("head_g", [1024]); skip_a = din("skip_a", [1024])
    w_a_out = din("w_a_out", [1024, 1024])
    lam_re = din("lam_re", [32, 64]); lam_im = din("lam_im", [32, 64]); log_dt = din("log_dt", [1, 32])
    B_re = din("B_re", [32, 64, 16]); B_im = din("B_im", [32, 64, 16])
    C_re = din("C_re", [32, 16, 64]); C_im = din("C_im", [32, 16, 64]); D_skip = din("D_skip", [32, 16])
    w_glu = din("w_glu", [512, 512]); b_glu = din("b_glu", [512]); w_b_out = din("w_b_out", [512, 1024])
    w_o = din("w_o", [1024, 1024]); norm_post_g = din("norm_post_g", [1, 1024])
    out = nc.dram_tensor("out", [NMAIN * T, 1024], F32, kind="ExternalOutput").ap()
    WS = nc.dram_tensor("wscratch", [NBLK, 128, 4096], BF16, kind="Internal").ap()

    P = Prog(nc)
    uid = [0]

    def sbt(shape, dt, name=None):
        uid[0] += 1
        return P.sb(name or ("t%d" % uid[0]), shape, dt)

    ident_f = sbt([128, 128], F32); ident_b = sbt([128, 128], BF16)
    maskT = sbt([128, 128], F32); mask4 = sbt([128, 4, 128], F32); ones_b = sbt([128, 128], BF16)
    P.op("pool", lambda e: e.memset(ident_f[:], 1.0), writes=["ident_f"])
    P.op("pool", lambda e: e.affine_select(out=ident_f[:], in_=ident_f[:], pattern=[[-1, 128]], compare_op=ALU.is_equal,
                                           fill=0.0, base=0, channel_multiplier=1), reads=["ident_f"], writes=["ident_f"])
    P.op("dve", lambda e: e.tensor_copy(out=ident_b[:], in_=ident_f[:]), reads=["ident_f"], writes=["ident_b"])
    P.op("pool", lambda e: e.memset(maskT[:], 1.0), writes=["maskT"])
    P.op("pool", lambda e: e.affine_select(out=maskT[:], in_=maskT[:], pattern=[[1, 128]], compare_op=ALU.is_ge,
                                           fill=0.0, base=0, channel_multiplier=-1), reads=["maskT"], writes=["maskT"])
    for h in range(4):
        P.op("pool", lambda e, h=h: e.tensor_copy(out=mask4[:, h, :], in_=maskT[:]), reads=["maskT"], writes=["mask4"])
    P.op("pool", lambda e: e.memset(ones_b[:], 1.0), writes=["ones_b"])

    gpre = sbt([128, 8], F32); cb = sbt([128, 8], F32); hg = sbt([128, 8], F32); skp = sbt([128, 8], F32)
    cw = sbt([128, 8, 4], F32); bglu = sbt([128, 4], F32); gpost = sbt([128, 1024], F32); bif = sbt([128, 8], F32)
    flg = sbt([128, 1], F32)
    nonc = dict(allow_slow_non_contiguous=True)
    P.dma(lambda e: e.dma_start(out=gpre[:], in_=norm_pre_g.rearrange("(k p) -> p k", p=128), **nonc), writes=["gpre"])
    P.dma(lambda e: e.dma_start(out=cb[:], in_=conv_b.rearrange("(k p) -> p k", p=128), **nonc), writes=["cb"])
    P.dma(lambda e: e.dma_start(out=hg[:], in_=head_g.rearrange("(k p) -> p k", p=128), **nonc), writes=["hg"])
    P.dma(lambda e: e.dma_start(out=skp[:], in_=skip_a.rearrange("(k p) -> p k", p=128), **nonc), writes=["skp"])
    for k in range(4):
        P.dma(lambda e, k=k: e.dma_start(out=cw[:, :, k], in_=conv_w[k].rearrange("(m p) -> p m", p=128), **nonc), writes=["cw"])
    P.dma(lambda e: e.dma_start(out=bglu[:], in_=b_glu.rearrange("(k p) -> p k", p=128), **nonc), writes=["bglu"])
    P.dma(lambda e: e.dma_start(out=gpost[:], in_=norm_post_g.partition_broadcast(128)), writes=["gpost"])
    P.dma(lambda e: e.dma_start(out=bif[:, 0:4], in_=b_i.partition_broadcast(128)), writes=["bif"])
    P.dma(lambda e: e.dma_start(out=bif[:, 4:8], in_=b_f.partition_broadcast(128)), writes=["bif"])
    P.dma(lambda e: e.dma_start(out=flg[:], in_=flag), writes=["flg"])

    Wqkv = [sbt([128, 4, 2, 256], BF16) for _ in range(3)]
    Wif = sbt([128, 8, 8], BF16); Wglu = sbt([128, 4, 512], BF16); cdiag = sbt([128, 8, 4, 128], BF16)
    AR32 = sbt([128, 2, 16], F32); ANI = sbt([128, 16], F32); API = sbt([128, 16], F32)
    pA = P.ps("pA", [128, 512], F32); pB = P.ps("pB", [128, 512], F32)
    pT = P.ps("pT", [128, 1024], BF16); pGD = P.ps("pGD", [128, 512], F32)
    pS = P.ps("pS", [128, 512], F32); pN0 = P.ps("pN0", [128, 512], F32); pN1 = P.ps("pN1", [128, 512], F32)
    pM = P.ps("pM", [128, 512], F32)
    P.temp = contextlib.ExitStack()
    stg = sbt([128, 4096], F32, "stg")
    stgb = sbt([128, 4096], BF16, "stgb")
    for wi, (wsrc, scl) in enumerate(((w_q, 1.0), (w_k, 1.0 / 16.0), (w_v, 1.0))):
        P.dma(lambda e, wsrc=wsrc: e.dma_start(out=stg[:, 0:2048].rearrange("p (h d n) -> p h d n", h=4, d=2),
                                               in_=wsrc.rearrange("h (d p) n -> p h d n", p=128)), writes=["stg"])
        P.op("dve", lambda e, wi=wi, scl=scl: e.tensor_scalar(out=Wqkv[wi][:].rearrange("p h d n -> p (h d n)"), in0=stg[:, 0:2048],
                                                              scalar1=scl, scalar2=None, op0=ALU.mult), reads=["stg"], writes=["Wqkv%d" % wi])
    Wq, Wk, Wv = Wqkv
    P.dma(lambda e: e.dma_start(out=stg[:, 0:64].rearrange("p (k n) -> p k n", k=8),
                                in_=w_in[:, COL_I:COL_I + 8].rearrange("(k p) n -> p k n", p=128), **nonc), writes=["stg"])
    for kt in range(8):
        P.op("dve", lambda e, kt=kt: e.tensor_scalar(out=Wif[:, kt, :], in0=stg[:, kt * 8:(kt + 1) * 8], scalar1=gpre[:, kt:kt + 1],
                                                     scalar2=None, op0=ALU.mult), reads=["stg", "gpre"], writes=["Wif"])
    P.dma(lambda e: e.dma_start(out=stg[:, 0:2048].rearrange("p (k n) -> p k n", k=4),
                                in_=w_glu.rearrange("(k p) n -> p k n", p=128)), writes=["stg"])
    P.op("dve", lambda e: e.tensor_copy(out=Wglu[:].rearrange("p k n -> p (k n)"), in_=stg[:, 0:2048]), reads=["stg"], writes=["Wglu"])
    for mt in range(8):
        for k in range(4):
            P.op("dve", lambda e, mt=mt, k=k: e.tensor_scalar(out=cdiag[:, mt, k, :], in0=ident_f[:], scalar1=cw[:, mt, k:k + 1],
                                                               scalar2=None, op0=ALU.mult), reads=["ident_f", "cw"], writes=["cdiag"])

    def stage_block(blk, src_ap_f, scale_gpre, nk):
        ncol = 4096 // nk
        P.dma(lambda e: e.dma_start(out=stg[:].rearrange("p (k n) -> p k n", k=nk), in_=src_ap_f), writes=["stg"])
        if scale_gpre:
            for kt in range(nk):
                P.op("dve",
                     lambda e, kt=kt: e.tensor_scalar(out=stgb[:, kt * ncol:(kt + 1) * ncol], in0=stg[:, kt * ncol:(kt + 1) * ncol],
                                                      scalar1=gpre[:, kt:kt + 1], scalar2=None, op0=ALU.mult),
                     reads=["stg", "gpre"], writes=["stgb"])
        else:
            P.op("dve", lambda e: e.tensor_copy(out=stgb[:, 0:2048], in_=stg[:, 0:2048]), reads=["stg"], writes=["stgb"])
            P.op("act", lambda e: e.copy(out=stgb[:, 2048:4096], in_=stg[:, 2048:4096]), reads=["stg"], writes=["stgb"])
        P.dma(lambda e: e.dma_start(out=WS[blk], in_=stgb[:]), reads=["stgb"], writes=["WS%d" % blk])

    def win_cols(c0):
        return w_in[:, c0:c0 + 512].rearrange("(k p) n -> p k n", p=128)

    win_blocks = [(BLK_UA, COL_UA), (BLK_UA + 1, COL_UA + 512), (BLK_UB, COL_UB), (BLK_ZB, COL_ZB), (BLK_ZA, COL_ZA),
                  (BLK_ZA + 1, COL_ZA + 512), (BLK_OA, COL_OA), (BLK_OA + 1, COL_OA + 512)] + [(BLK_G + i, COL_G + 512 * i) for i in range(4)]
    pending = []
    for blk, c0 in win_blocks:
        pending.append((blk, win_cols(c0), True, 8))
    for i in range(2):
        pending.append((BLK_AO + i, w_a_out[:, 512 * i:512 * i + 512].rearrange("(k p) n -> p k n", p=128), False, 8))
        pending.append((BLK_WO + i, w_o[:, 512 * i:512 * i + 512].rearrange("(k p) n -> p k n", p=128), False, 8))
    pending.append((BLK_BO, w_b_out.rearrange("(k p) n -> p k n", p=128), False, 4))

    def stage_some(n=1):
        for _ in range(n):
            if pending:
                stage_block(*pending.pop(0))

    stage_some(2)

    def small(n, name=None):
        return sbt([128, n], F32, name)

    cnt = [0]

    def el(eng, fn, r, w):
        P.op(eng, fn, reads=r, writes=w)

    def tt(outp, a, b, op, r, w, eng="dve"):
        el(eng, lambda e: e.tensor_tensor(out=outp, in0=a, in1=b, op=op), r, w)

    def ts(outp, a, s1, s2, op0, op1, r, w, eng="dve"):
        if op1 is None:
            el(eng, lambda e: e.tensor_scalar(out=outp, in0=a, scalar1=s1, scalar2=None, op0=op0), r, w)
        else:
            el(eng, lambda e: e.tensor_scalar(out=outp, in0=a, scalar1=s1, scalar2=s2, op0=op0, op1=op1), r, w)

    LR = small(32); LI = small(32); DT = small(32)
    for hf in range(2):
        sl = slice(64 * hf, 64 * hf + 64)
        P.dma(lambda e, sl=sl: e.dma_start(out=LR[sl, :], in_=lam_re.rearrange("g p -> p g"), **nonc), writes=["LR"])
        P.dma(lambda e, sl=sl: e.dma_start(out=LI[sl, :], in_=lam_im.rearrange("g p -> p g"), **nonc), writes=["LI"])
    P.dma(lambda e: e.dma_start(out=DT[:], in_=log_dt.partition_broadcast(128)), writes=["DT"])
    el("act", lambda e: e.activation(out=DT[:], in_=DT[:], func=AF.Exp), ["DT"], ["DT"])
    TH = small(32); MAG = small(32); t0 = small(32); t1 = small(32); t2 = small(32); kk = small(32)
    tt(TH[:], LI[:], DT[:], ALU.mult, ["LI", "DT"], ["TH"])
    tt(t0[:], LR[:], DT[:], ALU.mult, ["LR", "DT"], ["t0"])
    el("act", lambda e: e.activation(out=MAG[:], in_=t0[:], func=AF.Exp), ["t0"], ["MAG"])
    IMAG2 = small(32)
    el("act", lambda e: e.activation(out=IMAG2[:], in_=t0[:], func=AF.Exp, scale=-2.0), ["t0"], ["IMAG2"])

    def sin_of(dst, src, shift, nm):
        ts(t1[:], src, shift, None, ALU.add, None, [nm, "t1"], ["t1"])
        el("pool", lambda e: e.memset(kk[:], 0.0), [], ["kk"])
        for m in range(7):
            ts(t2[:], t1[:], (2 * m + 1) * PI, None, ALU.is_gt, None, ["t1"], ["t2"])
            tt(kk[:], kk[:], t2[:], ALU.add, ["kk", "t2"], ["kk"])
        ts(kk[:], kk[:], -2.0 * PI, None, ALU.mult, None, ["kk"], ["kk"])
        tt(t1[:], t1[:], kk[:], ALU.add, ["t1", "kk"], ["t1"])
        el("act", lambda e: e.activation(out=dst, in_=t1[:], func=AF.Sin), ["t1"], [nm + "_s"])

    SN = small(32); CS = small(32)
    sin_of(SN[:], TH[:], 0.0, "TH")
    sin_of(CS[:], TH[:], PI / 2.0, "TH")
    pwr = sbt([128, 9, 32], F32); pwi = sbt([128, 9, 32], F32); pnr = sbt([128, 8, 32], F32); pni = sbt([128, 8, 32], F32)
    el("pool", lambda e: e.memset(pwr[:, 0, :], 1.0), [], ["pw"]); el("pool", lambda e: e.memset(pwi[:, 0, :], 0.0), [], ["pw"])
    el("pool", lambda e: e.memset(pnr[:, 0, :], 1.0), [], ["pn"]); el("pool", lambda e: e.memset(pni[:, 0, :], 0.0), [], ["pn"])
    tt(pwr[:, 1, :], MAG[:], CS[:], ALU.mult, ["MAG", "TH_s"], ["pw"])
    tt(pwi[:, 1, :], MAG[:], SN[:], ALU.mult, ["MAG", "TH_s"], ["pw"])
    tt(pnr[:, 1, :], pwr[:, 1, :], IMAG2[:], ALU.mult, ["pw", "IMAG2"], ["pn"])
    tt(t0[:], pwi[:, 1, :], IMAG2[:], ALU.mult, ["pw", "IMAG2"], ["t0"])
    ts(pni[:, 1, :], t0[:], -1.0, None, ALU.mult, None, ["t0"], ["pn"])

    def cmul(or_, oi_, ar, ai, br, bi, r, w):
        raise NotImplementedError

    u0 = small(32); u1 = small(32)
    for k in range(1, 8):
        for (xr, xi, nm, lim) in ((pwr, pwi, "pw", 9), (pnr, pni, "pn", 8)):
            if k + 1 >= lim:
                continue
            tt(u0[:], xr[:, k, :], xr[:, 1, :], ALU.mult, [nm], ["u0"])
            tt(u1[:], xi[:, k, :], xi[:, 1, :], ALU.mult, [nm], ["u1"])
            tt(xr[:, k + 1, :], u0[:], u1[:], ALU.subtract, ["u0", "u1"], [nm])
            tt(u0[:], xr[:, k, :], xi[:, 1, :], ALU.mult, [nm], ["u0"])
            tt(u1[:], xi[:, k, :], xr[:, 1, :], ALU.mult, [nm], ["u1"])
            tt(xi[:, k + 1, :], u0[:], u1[:], ALU.add, ["u0", "u1"], [nm])
    den = small(32); qr = small(32); qi = small(32); nr = small(32)
    tt(u0[:], LR[:], LR[:], ALU.mult, ["LR"], ["u0"]); tt(u1[:], LI[:], LI[:], ALU.mult, ["LI"], ["u1"])
    tt(den[:], u0[:], u1[:], ALU.add, ["u0", "u1"], ["den"])
    el("dve", lambda e: e.reciprocal(out=den[:], in_=den[:]), ["den"], ["den"])
    ts(nr[:], pwr[:, 1, :], -1.0, None, ALU.add, None, ["pw"], ["nr"])
    tt(u0[:], nr[:], LR[:], ALU.mult, ["nr", "LR"], ["u0"]); tt(u1[:], pwi[:, 1, :], LI[:], ALU.mult, ["pw", "LI"], ["u1"])
    tt(qr[:], u0[:], u1[:], ALU.add, ["u0", "u1"], ["qr"]); tt(qr[:], qr[:], den[:], ALU.mult, ["qr", "den"], ["qr"])
    tt(u0[:], pwi[:, 1, :], LR[:], ALU.mult, ["pw", "LR"], ["u0"]); tt(u1[:], nr[:], LI[:], ALU.mult, ["nr", "LI"], ["u1"])
    tt(qi[:], u0[:], u1[:], ALU.subtract, ["u0", "u1"], ["qi"]); tt(qi[:], qi[:], den[:], ALU.mult, ["qi", "den"], ["qi"])
    Br = sbt([128, 32, 16], F32); Bi = sbt([128, 32, 16], F32); bbr = sbt([128, 32, 16], F32); bbi = sbt([128, 32, 16], F32)
    v0 = sbt([128, 32, 16], F32); v1 = sbt([128, 32, 16], F32)
    for hf in range(2):
        sl = slice(64 * hf, 64 * hf + 64)
        P.dma(lambda e, sl=sl: e.dma_start(out=Br[sl], in_=B_re.rearrange("g p n -> p g n"), **nonc), writes=["Br"])
        P.dma(lambda e, sl=sl: e.dma_start(out=Bi[sl], in_=B_im.rearrange("g p n -> p g n"), **nonc), writes=["Bi"])

    def bc(s):
        return s.unsqueeze(2).to_broadcast([128, 32, 16])

    def cmul3(orr, oii, sr, si, sn, xr, xi, xn, on):
        xn = [xn] if isinstance(xn, str) else list(xn)
        sn = [sn] if isinstance(sn, str) else list(sn)
        stage_some(1)
        tt(v0[:], xr, bc(sr), ALU.mult, xn + sn, ["v0"]); tt(v1[:], xi, bc(si), ALU.mult, xn + sn, ["v1"])
        tt(orr, v0[:], v1[:], ALU.subtract, ["v0", "v1"], [on])
        tt(v0[:], xi, bc(sr), ALU.mult, xn + sn, ["v0"]); tt(v1[:], xr, bc(si), ALU.mult, xn + sn, ["v1"])
        tt(oii, v0[:], v1[:], ALU.add, ["v0", "v1"], [on])

    el("dve", lambda e: e.tensor_copy(out=u0[:], in_=qr[:]), ["qr"], ["qq"])
    cmul3(bbr[:], bbi[:], qr[:], qi[:], ["qr", "qi"], Br[:], Bi[:], ["Br", "Bi"], "bb")
    CTr = sbt([128, 32, 16], F32); CTi = sbt([128, 32, 16], F32)
    Cdup = sbt([128, 4, 2, 64], F32)
    for (Csrc, CTt, nm) in ((C_re, CTr, "CTr"), (C_im, CTi, "CTi")):
        for d in range(2):
            P.dma(lambda e, Csrc=Csrc, d=d: e.dma_start(out=Cdup[:, :, d, :], in_=Csrc.rearrange("(t g) n p -> (g n) t p", t=4)),
                  writes=["Cdup"])
        for t in range(4):
            P.op("pe", lambda e, t=t: e.transpose(out=pA[:, t * 128:(t + 1) * 128], in_=Cdup[:, t, :, :].rearrange("q d p -> q (d p)"),
                                                  identity=ident_f[:]), reads=["Cdup", "ident_f"], writes=["pA"])
        el("act", lambda e, CTt=CTt: e.copy(out=CTt[:].rearrange("p g n -> p (g n)"), in_=pA[:]), ["pA"], [nm])
    Er = sbt([128, 32, 8, 16], F32); Ei = sbt([128, 32, 8, 16], F32)
    Fr = sbt([128, 32, 8, 16], F32); Fi = sbt([128, 32, 8, 16], F32)
    for j in range(8):
        cmul3(Er[:, :, j, :], Ei[:, :, j, :], pwr[:, 7 - j, :], pwi[:, 7 - j, :], "pw", bbr[:], bbi[:], "bb", "E")
        cmul3(Fr[:, :, j, :], Fi[:, :, j, :], pwr[:, j + 1, :], pwi[:, j + 1, :], "pw", CTr[:], CTi[:], ["CTr", "CTi"], "F")
    halfm = sbt([128, 2], F32)
    el("pool", lambda e: e.memset(halfm[:], 0.0), [], ["halfm"])
    el("pool", lambda e: e.memset(halfm[0:64, 0:1], 1.0), ["halfm"], ["halfm"])
    el("pool", lambda e: e.memset(halfm[64:128, 1:2], 1.0), ["halfm"], ["halfm"])
    for ri, (Et, blk) in enumerate(((Er, BLK_S5), (Ei, BLK_S5 + 1))):
        el("pool", lambda e: e.memset(stgb[:], 0.0), ["stgb"], ["stgb"])
        for g in range(32):
            ps_ = pA if g % 2 == 0 else pB
            nm = "pA" if g % 2 == 0 else "pB"
            P.op("pe", lambda e, Et=Et, g=g, ps_=ps_: e.transpose(out=ps_[:, 0:64], in_=Et[0:64, g, :, :].rearrange("p j n -> p (j n)"),
                                                                  identity=ident_f[0:64, 0:64]), reads=["E", "ident_f"], writes=[nm])
            c0 = g * 128 + 64 * (g % 2)
            el("act" if g % 2 == 0 else "dve",
               (lambda e, ps_=ps_, c0=c0: e.copy(out=stgb[:, c0:c0 + 64], in_=ps_[:, 0:64])) if g % 2 == 0 else
               (lambda e, ps_=ps_, c0=c0: e.tensor_copy(out=stgb[:, c0:c0 + 64], in_=ps_[:, 0:64])), [nm], ["stgb"])
        P.dma(lambda e, blk=blk: e.dma_start(out=WS[blk], in_=stgb[:]), reads=["stgb"], writes=["WS%d" % blk])
    for ri, (Ft, blk, sg) in enumerate(((Fr, BLK_S5 + 3, 1.0), (Fi, BLK_S5 + 4, -1.0))):
        for g in range(32):
            P.op("dve",
                 lambda e, Ft=Ft, g=g, sg=sg: e.tensor_scalar(out=stgb[:, g * 128:(g + 1) * 128], in0=Ft[:, g, :, :].rearrange("p j n -> p (j n)"),
                                                              scalar1=halfm[:, (g % 2):(g % 2) + 1], scalar2=sg, op0=ALU.mult, op1=ALU.mult),
                 reads=["F", "halfm"], writes=["stgb"])
        P.dma(lambda e, blk=blk: e.dma_start(out=WS[blk], in_=stgb[:]), reads=["stgb"], writes=["WS%d" % blk])
    Gr = sbt([128, 32, 8, 16], F32); Gni = sbt([128, 32, 8, 16], F32)
    i8r = small(32); i8i = small(32)
    tt(u0[:], pnr[:, 7, :], pnr[:, 1, :], ALU.mult, ["pn"], ["u0"]); tt(u1[:], pni[:, 7, :], pni[:, 1, :], ALU.mult, ["pn"], ["u1"])
    tt(i8r[:], u0[:], u1[:], ALU.subtract, ["u0", "u1"], ["i8"])
    tt(u0[:], pnr[:, 7, :], pni[:, 1, :], ALU.mult, ["pn"], ["u0"]); tt(u1[:], pni[:, 7, :], pnr[:, 1, :], ALU.mult, ["pn"], ["u1"])
    tt(i8i[:], u0[:], u1[:], ALU.add, ["u0", "u1"], ["i8"])
    for j in range(8):
        cmul3(Gr[:, :, j, :], Gni[:, :, j, :], i8r[:], i8i[:], "i8", Fr[:, :, j, :], Fi[:, :, j, :], "F", "G")
    ts(Gni[:], Gni[:], -1.0, None, ALU.mult, None, ["G"], ["G"])
    bmask = sbt([128, 8, 16], F32); Dcol = sbt([128, 32], F32)
    el("pool", lambda e: e.memset(bmask[:], 1.0), [], ["bmask"])
    el("pool", lambda e: e.affine_select(out=bmask[:], in_=bmask[:], pattern=[[16, 8], [0, 16]], compare_op=ALU.is_ge, fill=0.0,
                                         base=15, channel_multiplier=-1), ["bmask"], ["bmask"])
    for j in range(8):
        P.dma(lambda e, j=j: e.dma_start(out=Dcol[16 * j:16 * j + 16, :], in_=D_skip.rearrange("g n -> n g"), **nonc), writes=["Dcol"])
    wtmp = sbt([128, 128], F32)
    for g in range(32):
        ps_ = pA if g % 2 == 0 else pB
        nm = "pA" if g % 2 == 0 else "pB"
        P.op("pe", lambda e, g=g, ps_=ps_: e.matmul(ps_[:, 0:128], lhsT=Er[0:64, g, :, :].rearrange("p j n -> p (j n)"),
                                                    rhs=Gr[0:64, g, :, :].rearrange("p j n -> p (j n)"), start=True, stop=False),
             reads=["E", "G"], writes=[nm], sig=False)
        P.op("pe", lambda e, g=g, ps_=ps_: e.matmul(ps_[:, 0:128], lhsT=Ei[0:64, g, :, :].rearrange("p j n -> p (j n)"),
                                                    rhs=Gni[0:64, g, :, :].rearrange("p j n -> p (j n)"), start=False, stop=True),
             reads=["E", "G"], writes=[nm])
        tt(wtmp[:], ps_[:, 0:128], bmask[:].rearrange("p j n -> p (j n)"), ALU.mult, [nm, "bmask"], ["wtmp"])
        el("dve", lambda e, g=g: e.scalar_tensor_tensor(out=stgb[:, g * 128:(g + 1) * 128], in0=ident_f[:], scalar=Dcol[:, g:g + 1],
                                                        in1=wtmp[:], op0=ALU.mult, op1=ALU.add), ["ident_f", "Dcol", "wtmp", "stgb"], ["stgb"])
    P.dma(lambda e: e.dma_start(out=WS[BLK_S5 + 2], in_=stgb[:]), reads=["stgb"], writes=["WS%d" % (BLK_S5 + 2)])
    for g2 in range(2):
        sl = slice(64 * g2, 64 * g2 + 64)
        src_r = pwr[sl, 8, :].rearrange("p (q t) -> p q t", t=2)[:, :, g2]
        src_i = pwi[sl, 8, :].rearrange("p (q t) -> p q t", t=2)[:, :, g2]
        el("dve", lambda e, sl=sl, src_r=src_r: e.tensor_copy(out=AR32[sl, 0, :], in_=src_r), ["pw"], ["AR32"])
        el("dve", lambda e, sl=sl, src_r=src_r: e.tensor_copy(out=AR32[sl, 1, :], in_=src_r), ["pw"], ["AR32"])
        el("dve", lambda e, sl=sl, src_i=src_i: e.tensor_copy(out=API[sl, :], in_=src_i), ["pw"], ["API"])
        ts(ANI[sl, :], src_i, -1.0, None, ALU.mult, None, ["pw"], ["ANI"])

    stage_some(100)
    if dbg_stop == 1:
        tk = P.dma(lambda e: e.dma_start(out=out[0:128, :], in_=gpost[:]), reads=["gpost"])
        P.final_wait("sp", [tk])
        P.emit(); P.temp.close(); P.close()
        return nc
    P.emit()
    P.temp.close()
    P.temp = None
    NSLOT = 3
    wslot = [sbt([128, 4096], BF16, "wslot%d" % i) for i in range(NSLOT)]
    slot_rr = [0]

    def load_block(blk):
        s = slot_rr[0] % NSLOT
        slot_rr[0] += 1
        P.dma(lambda e: e.dma_start(out=wslot[s][:], in_=WS[blk]), reads=["WS%d" % blk], writes=["wslot%d" % s])
        return wslot[s], "wslot%d" % s

    xs = [sbt([128, 1024], F32, "xs%d" % i) for i in range(2)]
    xr = [sbt([128, 1024], F32, "xr%d" % i) for i in range(2)]
    junk = sbt([128, 1024], BF16); xnb = sbt([128, 1024], BF16)
    ss = small(1); rstd = small(1)
    xnT = sbt([128, 8, T], BF16, "xnT"); uaT = sbt([128, 8, T + 4], BF16, "uaT"); cT = sbt([128, 8, T], BF16, "cT")
    zaT = sbt([128, 8, T], BF16, "zaT"); oaT = sbt([128, 8, T], BF16, "oaT"); zbT = sbt([128, 4, T], BF16, "zbT")
    qT = sbt([128, 4, 2, T], BF16, "qT"); kT = sbt([128, 4, 2, T], BF16, "kT")
    ktm = sbt([128, NCH, 4, 256], BF16, "ktm"); vw = sbt([128, NCH, 4, 264], BF16, "vw")
    gates = sbt([128, NCH, 8], F32, "gates")
    hfT = sbt([128, 8, T], BF16, "hfT"); csz = sbt([128, 8, T], BF16, "csz"); mrgT = zaT; hbT = sbt([128, 4, T], BF16, "hbT")
    Cst = sbt([128, 4, 2, 264], F32, "Cst"); Cbf = sbt([128, 4, 2, 264], BF16, "Cbf")
    nst = sbt([128, 4, 2], F32, "nst"); nrep = sbt([128, 4, 2, 128], BF16, "nrep")
    Utm = sbt([128, 32, 8, 16], BF16, "Utm"); U2 = sbt([128, 32, NC8], BF16, "U2")
    Xall = sbt([128, NC8, 2, 16], F32, "Xall"); Sall = sbt([128, NC8 + 1, 2, 16], F32, "Sall"); Sbf = sbt([128, 2, 16, NC8], BF16, "Sbf")
    Yg = sbt([128, 32, NC8], BF16, "Yg"); Ytm = Utm[:].rearrange("p g j n -> p (g j n)").rearrange("p (j c) -> p j c", j=8); yT = sbt([128, 4, T], BF16, "yT")
    for (t_, nm) in ((Cst, "Cst"), (Cbf, "Cbf"), (nst, "nst"), (nrep, "nrep"), (Sall, "Sall"), (uaT, "uaT"), (vw, "vw")):
        flat = t_[:]
        wnames = [nm] + (["%s%d" % (nm, h_) for h_ in range(4)] if nm in ("Cst", "Cbf", "nrep") else [])
        el("pool", lambda e, flat=flat: e.memset(flat, 0.0), [], wnames)

    e1a = sbt([128, NCH, 4], F32); lfa = sbt([128, NCH, 4], F32); emb4 = sbt([128, NCH, 4, 128], F32); dec4 = sbt([128, NCH, 4], F32)
    wcol4 = sbt([128, NCH, 4], F32); wcolb4 = sbt([128, NCH, 4, 2], BF16); wrep4 = sbt([128, NCH, 4, 128], BF16)
    lfrep = sbt([128, 4, 128], F32); lf = small(4, "lf"); ig = small(4); bcol = small(4); wcol = small(4, "wcol"); wcolb = sbt([128, 4, 2], BF16)
    wrep = sbt([128, 4, 128], BF16); emb = sbt([128, 4, 128], F32, "emb"); dec = small(4, "dec"); SmT = sbt([128, 4, 128], BF16, "SmT")
    aden = sbt([128, 4, 128], F32); rden = sbt([128, 4, 128], F32, "rden"); hT = sbt([128, 8, 128], F32, "hT"); sq = sbt([128, 8, 128], BF16)
    rsh = sbt([128, 4, 128], F32); hn = sbt([128, 8, 128], F32, "hn"); e1 = small(4)
    gAll = oaT; m1 = sbt([128, T], F32); m2 = sbt([128, T], F32)
    ot = [sbt([128, 1024], F32, "ot%d" % i) for i in range(2)]
    ss2 = small(1); rstd2 = small(1); sgl = sbt([128, T], BF16); xg = sbt([128, T], F32)

    def rsqrt_col(dst, src, scale, nm_src, nm_dst):
        ts(dst, src, scale, 1e-6, ALU.mult, ALU.add, [nm_src], [nm_dst])
        el("act", lambda e: e.activation(out=dst, in_=dst, func=AF.Ln), [nm_dst], [nm_dst])
        el("act", lambda e: e.activation(out=dst, in_=dst, func=AF.Exp, scale=-0.5), [nm_dst], [nm_dst])

    def mm(outp, lhsT, rhs, start, stop, r, w, sig=None):
        P.op("pe", lambda e: e.matmul(outp, lhsT=lhsT, rhs=rhs, start=start, stop=stop), reads=r, writes=w,
             sig=(stop if sig is None else sig))

    acc_rr = [0]

    def acc_bank():
        acc_rr[0] += 1
        return (pA, "pA") if acc_rr[0] % 2 else (pB, "pB")

    out_toks = []

    class StopBuild(Exception):
        pass

    def chk(level):
        if dbg_stop == level:
            raise StopBuild()

    prefetched = [False]

    def do_tile(xsrc, row0, main, ti, nxt=None):
        def issue_x(src_, r0, sub):
            xb2, xn2 = xs[sub % 2], "xs%d" % (sub % 2)
            P.dma(lambda e: e.dma_start(out=xb2[:], in_=src_[r0 + sub * 128: r0 + (sub + 1) * 128, :]), writes=[xn2])

        for sub in range(NCH):
            xb_, xnm = xs[sub % 2], "xs%d" % (sub % 2)
            if not (prefetched[0] and sub < 2):
                issue_x(xsrc, row0, sub)
            el("act", lambda e, xb_=xb_: e.activation(out=junk[:], in_=xb_[:], func=AF.Square, accum_out=ss[:]), [xnm], ["junk", "ss"])
            rsqrt_col(rstd[:], ss[:], 1.0 / 1024.0, "ss", "rstd")
            ts(xnb[:], xb_[:], rstd[:, 0:1], None, ALU.mult, None, [xnm, "rstd"], ["xnb"])
            for kt in range(8):
                P.op("pe", lambda e, kt=kt: e.transpose(out=pT[:, kt * 128:(kt + 1) * 128], in_=xnb[:, kt * 128:(kt + 1) * 128], identity=ident_b[:]),
                     reads=["xnb", "ident_b"], writes=["pT"], sig=(kt == 7))
            el("act", lambda e, sub=sub: e.copy(out=xnT[:, :, sub * 128:(sub + 1) * 128], in_=pT[:].rearrange("p (k t) -> p k t", k=8)),
               ["pT"], ["xnT"])

        prefetched[0] = False
        if nxt is not None and NCH <= 2:
            for sub in range(min(2, NCH)):
                issue_x(nxt[0], nxt[1], sub)
            prefetched[0] = True
        chk(2)

        def proj_fm(blk, ncolt, evac):
            W, wn = load_block(blk)
            Wv_ = W[:].rearrange("p (k n) -> p k n", k=8)
            for ct in range(ncolt):
                ps_, pn = acc_bank()
                for kt in range(8):
                    mm(ps_[:, 0:T], Wv_[:, kt, ct * 128:(ct + 1) * 128], xnT[:, kt, :], kt == 0, kt == 7, [wn, "xnT"], [pn])
                evac(ct, ps_, pn)

        el("dve", lambda e: e.tensor_copy(out=uaT[:, :, 1:4], in_=uaT[:, :, T + 1:T + 4]), ["uaT"], ["uaT"])
        for half in range(2):
            proj_fm(BLK_UA + half, 4, lambda ct, ps_, pn, half=half: el(
                "act", lambda e: e.copy(out=uaT[:, half * 4 + ct, 4:T + 4], in_=ps_[:, 0:T]), [pn], ["uaT"]))
        chk(3)
        for ch in range(NCH):
            for kt in range(8):
                mm(pM[:, 0:8], xnT[:, kt, ch * 128:(ch + 1) * 128], Wif[:, kt, :], kt == 0, kt == 7, ["xnT", "Wif"], ["pM"])
            tt(gates[:, ch, :], pM[:, 0:8], bif[:], ALU.add, ["pM", "bif"], ["gates"])
        chk(4)
        W, wn = load_block(BLK_UB)
        Wv_ = W[:].rearrange("p (k n) -> p k n", k=8)
        for j in range(8):
            ps_, pn = acc_bank()
            for kt in range(8):
                lhs = xnT[:, kt, :].rearrange("p (c j) -> p j c", j=8)[:, j, :]
                mm(ps_[0:NC8, :], lhs, Wv_[:, kt, :], kt == 0, kt == 7, [wn, "xnT"], [pn])
            el("act" if j % 2 else "dve",
               (lambda e, ps_=ps_, j=j: e.copy(out=Utm[0:NC8, :, j, :], in_=ps_[0:NC8, :].rearrange("c (g n) -> c g n", g=32))) if j % 2 else
               (lambda e, ps_=ps_, j=j: e.tensor_copy(out=Utm[0:NC8, :, j, :], in_=ps_[0:NC8, :].rearrange("c (g n) -> c g n", g=32))), [pn], ["Utm"])
        chk(5)
        for g in range(32):
            P.op("pe", lambda e, g=g: e.transpose(out=pT[:, g * NC8:(g + 1) * NC8], in_=Utm[0:NC8, g, :, :].rearrange("c j n -> c (j n)"),
                                                  identity=ident_b[0:NC8, 0:NC8]), reads=["Utm", "ident_b"], writes=["pT"], sig=(g == 31))
        el("act", lambda e: e.copy(out=U2[:].rearrange("p g c -> p (g c)"), in_=pT[:, 0:32 * NC8]), ["pT"], ["U2"])
        W1r, w1rn = load_block(BLK_S5)
        W1i, w1in = load_block(BLK_S5 + 1)
        for ri, (Wt, wn_) in enumerate(((W1r, w1rn), (W1i, w1in))):
            Wg = Wt[:].rearrange("p (g n) -> p g n", g=32)
            for q in range(16):
                for g2 in range(2):
                    g = 2 * q + g2
                    mm(pM[:, (q * NC8):(q + 1) * NC8], Wg[:, g, :], U2[:, g, :], g2 == 0, g2 == 1, [wn_, "U2"], ["pM"], sig=(q == 15 and g2 == 1))
            el("act" if ri == 0 else "dve",
               (lambda e, ri=ri: e.copy(out=Xall[:, :, ri, :].rearrange("p c q -> p q c"), in_=pM[:, 0:16 * NC8].rearrange("p (q c) -> p q c", q=16))) if ri == 0 else
               (lambda e, ri=ri: e.tensor_copy(out=Xall[:, :, ri, :].rearrange("p c q -> p q c"), in_=pM[:, 0:16 * NC8].rearrange("p (q c) -> p q c", q=16))),
               ["pM"], ["Xall"])
        chk(6)
        T1 = sbt([128, 2, 16], F32, "scT1_%d" % ti) if False else None
        for c in range(NC8):
            sp_ = Sall[:, c, :, :]
            sn_ = Sall[:, c + 1, :, :]
            P.op("pool", lambda e, sp_=sp_: e.tensor_tensor(out=sc_t1[:], in0=sp_, in1=AR32[:], op=ALU.mult), reads=["Sall"], writes=["sc_t1"])
            P.op("pool", lambda e, c=c: e.tensor_tensor(out=sc_t2[:, 0, :], in0=Sall[:, c, 1, :], in1=ANI[:], op=ALU.mult), reads=["Sall"], writes=["sc_t2"])
            P.op("pool", lambda e, c=c: e.tensor_tensor(out=sc_t2[:, 1, :], in0=Sall[:, c, 0, :], in1=API[:], op=ALU.mult), reads=["Sall"], writes=["sc_t2"])
            P.op("pool", lambda e, c=c: e.tensor_tensor(out=sc_t1[:], in0=sc_t1[:], in1=Xall[:, c, :, :], op=ALU.add), reads=["sc_t1", "Xall"], writes=["sc_t1"])
            P.op("pool", lambda e, sn_=sn_: e.tensor_tensor(out=sn_, in0=sc_t1[:], in1=sc_t2[:], op=ALU.add), reads=["sc_t1", "sc_t2"], writes=["Sall"])
        if main:
            for ri in range(2):
                el("pool", lambda e, ri=ri: e.tensor_copy(out=Sbf[:, ri, :, :], in_=Sall[:, 0:NC8, ri, :].rearrange("p c q -> p q c")),
                   ["Sall"], ["Sbf"])
        el("pool", lambda e: e.tensor_copy(out=Sall[:, 0, :, :], in_=Sall[:, NC8, :, :]), ["Sall"], ["Sall"])
        def s5_out():
            proj_fm(BLK_ZB, 4, lambda ct, ps_, pn: el("act", lambda e: e.activation(out=zbT[:, ct, :], in_=ps_[:, 0:T], func=AF.Silu), [pn], ["zbT"]))
            Wi_, win_ = load_block(BLK_S5 + 2)
            Wr_, wrn_ = load_block(BLK_S5 + 3)
            Wm_, wmn_ = load_block(BLK_S5 + 4)
            Wi_g = Wi_[:].rearrange("p (g n) -> p g n", g=32); Wr_g = Wr_[:].rearrange("p (g n) -> p g n", g=32)
            Wm_g = Wm_[:].rearrange("p (g n) -> p g n", g=32)
            GP = 512 // NC8
            for g0 in range(0, 32, GP):
                ng = min(GP, 32 - g0)
                for gi in range(ng):
                    g = g0 + gi
                    o_ = pM[:, gi * NC8:(gi + 1) * NC8]
                    mm(o_, Wi_g[:, g, :], U2[:, g, :], True, False, [win_, "U2"], ["pM"], sig=False)
                    mm(o_, Wr_g[:, g, :], Sbf[:, 0, g // 2, :], False, False, [wrn_, "Sbf"], ["pM"], sig=False)
                    mm(o_, Wm_g[:, g, :], Sbf[:, 1, g // 2, :], False, True, [wmn_, "Sbf"], ["pM"], sig=(gi == ng - 1))
                el("act", lambda e, g0=g0, ng=ng: e.activation(out=Yg[:, g0:g0 + ng, :].rearrange("p g c -> p (g c)"), in_=pM[:, 0:ng * NC8],
                                                               func=AF.Gelu_apprx_tanh), ["pM"], ["Yg"])
            for g0 in range(0, 32, 8):
                for gi in range(8):
                    g = g0 + gi
                    P.op("pe", lambda e, g=g, gi=gi: e.transpose(out=pT[0:NC8, gi * 128:(gi + 1) * 128], in_=Yg[:, g, :], identity=ident_b[:]),
                         reads=["Yg", "ident_b"], writes=["pT"], sig=(gi == 7))
                el("act", lambda e, g0=g0: e.copy(out=Ytm[0:NC8, :, 16 * g0:16 * g0 + 128].rearrange("c j (g n) -> c g j n", g=8),
                                                  in_=pT[0:NC8, :].rearrange("c (g j n) -> c g j n", g=8, j=8)), ["pT"], ["Utm"])
            for ct in range(4):
                for j in range(8):
                    P.op("pe", lambda e, ct=ct, j=j: e.transpose(out=pT[:, j * NC8:(j + 1) * NC8], in_=Ytm[0:NC8, j, ct * 128:(ct + 1) * 128],
                                                                 identity=ident_b[0:NC8, 0:NC8]), reads=["Utm", "ident_b"], writes=["pT"], sig=(j == 7))
                el("act", lambda e, ct=ct: e.copy(out=yT[:, ct, :], in_=pT[:, 0:T]), ["pT"], ["yT"])
            for ot_ in range(4):
                ps_, pn = acc_bank()
                for ct in range(4):
                    mm(ps_[:, 0:T], Wglu[:, ct, ot_ * 128:(ot_ + 1) * 128], yT[:, ct, :], ct == 0, ct == 3, ["Wglu", "yT"], [pn])
                el("act", lambda e, ps_=ps_, ot_=ot_: e.activation(out=sgl[:], in_=ps_[:, 0:T], func=AF.Sigmoid, bias=bglu[:, ot_:ot_ + 1]),
                   [pn, "bglu"], ["sgl"])
                tt(xg[:], sgl[:], yT[:, ot_, :], ALU.mult, ["sgl", "yT"], ["xg"])
                tt(hbT[:, ot_, :].rearrange("p (j c) -> p j c", j=8), xg[:].rearrange("p (j c) -> p j c", j=8),
                   zbT[:, ot_, :].rearrange("p (c j) -> p j c", j=8), ALU.mult, ["xg", "zbT"], ["hbT"])
        chk(7)
        for mt in range(8):
            ps_, pn = acc_bank()
            for k in range(4):
                mm(ps_[:, 0:T], cdiag[:, mt, k, :], uaT[:, mt, k + 1:k + 1 + T], k == 0, k == 3, ["cdiag", "uaT"], [pn])
            el("act", lambda e, ps_=ps_, mt=mt: e.activation(out=cT[:, mt, :], in_=ps_[:, 0:T], func=AF.Silu, bias=cb[:, mt:mt + 1]),
               [pn, "cb"], ["cT"])
        if main:
            for half in range(2):
                proj_fm(BLK_ZA + half, 4, lambda ct, ps_, pn, half=half: el(
                    "act", lambda e: e.activation(out=zaT[:, half * 4 + ct, :], in_=ps_[:, 0:T], func=AF.Silu), [pn], ["zaT"]))
            for half in range(2):
                proj_fm(BLK_OA + half, 4, lambda ct, ps_, pn, half=half: el(
                    "act", lambda e: e.activation(out=oaT[:, half * 4 + ct, :], in_=ps_[:, 0:T], func=AF.Sigmoid), [pn], ["oaT"]))
        if main:
            for mt in range(8):
                el("dve", lambda e, mt=mt: e.scalar_tensor_tensor(out=csz[:, mt, :], in0=cT[:, mt, :], scalar=skp[:, mt:mt + 1], in1=zaT[:, mt, :],
                                                                  op0=ALU.mult, op1=ALU.mult), ["cT", "skp", "zaT"], ["csz"])
            for mt in range(8):
                ts(zaT[:, mt, :], zaT[:, mt, :], hg[:, mt:mt + 1], None, ALU.mult, None, ["zaT", "hg", "csz"], ["zaT"])
        chk(8)
        for h in range(4):
            if main:
                for (Wx, wxn, dst, dn) in ((Wq, "Wqkv0", qT, "qT"), (Wk, "Wqkv1", kT, "kT")):
                    for et in range(2):
                        ps_, pn = acc_bank()
                        for d in range(2):
                            mm(ps_[:, 0:T], Wx[:, h, d, et * 128:(et + 1) * 128], cT[:, 2 * h + d, :], d == 0, d == 1, [wxn, "cT"], [pn])
                        el("act" if et else "dve",
                           (lambda e, ps_=ps_, dst=dst, h=h, et=et: e.copy(out=dst[:, h, et, :], in_=ps_[:, 0:T])) if et else
                           (lambda e, ps_=ps_, dst=dst, h=h, et=et: e.tensor_copy(out=dst[:, h, et, :], in_=ps_[:, 0:T])), [pn], [dn])
        el("act", lambda e: e.activation(out=e1a[:], in_=gates[:, :, 4:8], func=AF.Exp, scale=-1.0), ["gates"], ["e1a"])
        el("act", lambda e: e.activation(out=lfa[:], in_=e1a[:], func=AF.Ln, bias=1.0), ["e1a"], ["lfa"])
        ts(lfa[:], lfa[:], -1.0, None, ALU.mult, None, ["lfa"], ["lfa"])
        for ch in range(NCH):
            tsl = slice(ch * 128, (ch + 1) * 128)
            for h in range(4):
                el("dve", lambda e, h=h, ch=ch: e.tensor_copy(out=lfrep[:, h, :], in_=lfa[:, ch, h:h + 1].to_broadcast([128, 128])), ["lfa"], ["lfrep"])
            for h in range(4):
                mm(pGD[:, h * 128:(h + 1) * 128], lfrep[:, h, :], maskT[:], True, True, ["lfrep", "maskT"], ["pGD"], sig=(h == 3))
            for h in range(4):
                mm(pM[:, h * 128:(h + 1) * 128], maskT[:], lfrep[:, h, :], True, True, ["maskT", "lfrep"], ["pM"], sig=(h == 3))
            en, dn, wn_, wbn, wrn = "emb%d" % ch, "dec%d" % ch, "wcol%d" % ch, "wcolb%d" % ch, "wrep%d" % ch
            el("act", lambda e, ch=ch: e.activation(out=emb4[:, ch, :, :].rearrange("p h t -> p (h t)"), in_=pGD[:], func=AF.Exp, scale=-1.0), ["pGD"], [en])
            el("dve", lambda e, ch=ch: e.reciprocal(out=dec4[:, ch, :], in_=emb4[:, ch, :, 127]), [en], [dn])
            tt(bcol[:], gates[:, ch, 0:4], pM[:].rearrange("p (h t) -> p h t", h=4)[:, :, 0], ALU.subtract, ["gates", "pM"], ["bcol"])
            el("act", lambda e, ch=ch: e.activation(out=wcol4[:, ch, :], in_=bcol[:], func=AF.Exp), ["bcol"], [wn_])
            el("dve", lambda e, ch=ch: e.tensor_copy(out=vw[:, ch, :, 256], in_=wcol4[:, ch, :]), [wn_], ["vw"])
            if main:
                for h in range(4):
                    el("dve", lambda e, h=h, ch=ch: e.tensor_copy(out=wrep4[:, ch, h, :], in_=wcol4[:, ch, h:h + 1].to_broadcast([128, 128])), [wn_], [wrn])
            for h in range(4):
                ps_, pn = acc_bank()
                for d in range(2):
                    mm(ps_[:, 0:256], cT[:, 2 * h + d, tsl], Wk[:, h, d, :], d == 0, d == 1, ["cT", "Wqkv1"], [pn], sig=False)
                for d in range(2):
                    mm(ps_[:, 256:512], uaT[:, 2 * h + d, 4 + ch * 128:4 + (ch + 1) * 128], Wv[:, h, d, :], d == 0, d == 1, ["uaT", "Wqkv2"], [pn])
                el("act", lambda e, ps_=ps_, ch=ch, h=h: e.copy(out=ktm[:, ch, h, :], in_=ps_[:, 0:256]), [pn], ["ktm"])
                el("act", lambda e, ps_=ps_, ch=ch, h=h: e.activation(out=vw[:, ch, h, 0:256], in_=ps_[:, 256:512], func=AF.Copy, scale=wcol4[:, ch, h:h + 1]),
                   [pn, wn_], ["vw"])
        for ch in range(NCH):
            tsl = slice(ch * 128, (ch + 1) * 128)
            en, dn, wn_, wbn, wrn = "emb%d" % ch, "dec%d" % ch, "wcol%d" % ch, "wcolb%d" % ch, "wrep%d" % ch
            if main:
                for h in range(4):
                    for et in range(2):
                        mm(pS[:, h * 128:(h + 1) * 128], kT[:, h, et, tsl], qT[:, h, et, tsl], et == 0, et == 1, ["kT", "qT"], ["pS"], sig=(h == 3 and et == 1))
                tt(SmT[:].rearrange("p h t -> p (h t)"), pS[:], mask4[:].rearrange("p h t -> p (h t)"), ALU.mult, ["pS", "mask4"], ["SmT"])
                for h in range(4):
                    for d2 in range(2):
                        pn_, pnn = (pN0, "pN0") if h < 2 else (pN1, "pN1")
                        o_ = pn_[:, ((h % 2) * 2 + d2) * 128:((h % 2) * 2 + d2 + 1) * 128]
                        mm(o_, vw[:, ch, h, d2 * 128:(d2 + 1) * 128], SmT[:, h, :], True, False, ["vw", "SmT"], [pnn], sig=False)
                        mm(o_, Cbf[:, h, 0, d2 * 128:(d2 + 1) * 128], qT[:, h, 0, tsl], False, False, ["Cbf%d" % h, "qT"], [pnn], sig=False)
                        mm(o_, Cbf[:, h, 1, d2 * 128:(d2 + 1) * 128], qT[:, h, 1, tsl], False, True, ["Cbf%d" % h, "qT"], [pnn], sig=(h % 2 == 1 and d2 == 1))
                    o_ = pGD[:, h * 128:(h + 1) * 128]
                    mm(o_, wrep4[:, ch, h, :], SmT[:, h, :], True, False, [wrn, "SmT"], ["pGD"], sig=False)
                    mm(o_, nrep[:, h, 0, :], qT[:, h, 0, tsl], False, False, ["nrep%d" % h, "qT"], ["pGD"], sig=False)
                    mm(o_, nrep[:, h, 1, :], qT[:, h, 1, tsl], False, True, ["nrep%d" % h, "qT"], ["pGD"], sig=(h == 3))
            if main:
                el("act", lambda e: e.activation(out=aden[:].rearrange("p h t -> p (h t)"), in_=pGD[:], func=AF.Abs), ["pGD"], ["aden"])
                tt(aden[:], aden[:], emb4[:, ch, :, :], ALU.max, ["aden", en], ["aden"])
                el("act", lambda e: e.activation(out=aden[:], in_=aden[:], func=AF.Ln), ["aden"], ["aden"])
                el("act", lambda e: e.activation(out=rden[:], in_=aden[:], func=AF.Exp, scale=-1.0), ["aden"], ["rden"])
            for h in range(4):
                cn, bn, nn = "Cst%d" % h, "Cbf%d" % h, "nrep%d" % h
                for et in range(2):
                    ps_, pn = acc_bank()
                    mm(ps_[:, 0:258], ktm[:, ch, h, et * 128:(et + 1) * 128], vw[:, ch, h, 0:258], True, True, ["ktm", "vw"], [pn])
                    tt(Cst[:, h, et, 0:258], Cst[:, h, et, 0:258], ps_[:, 0:258], ALU.add, [cn, pn, bn, nn], [cn])
            for h in range(4):
                cn, bn, nn = "Cst%d" % h, "Cbf%d" % h, "nrep%d" % h
                el("act", lambda e, h=h, ch=ch: e.activation(out=Cst[:, h, :, :], in_=Cst[:, h, :, :], func=AF.Copy, scale=dec4[:, ch, h:h + 1]), [cn, dn], [cn])
                el("act", lambda e, h=h: e.copy(out=Cbf[:, h, :, :], in_=Cst[:, h, :, :]), [cn], [bn])
            for h in range(4):
                cn, bn, nn = "Cst%d" % h, "Cbf%d" % h, "nrep%d" % h
                for et in range(2):
                    el("dve", lambda e, h=h, et=et: e.tensor_copy(out=nrep[:, h, et, :], in_=Cst[:, h, et, 256:257].to_broadcast([128, 128])),
                       [cn], [nn])
            if main:
                for hp, (pn_, pnn) in enumerate(((pN0, "pN0"), (pN1, "pN1"))):
                    tt(hT[:, 4 * hp:4 * hp + 4, :].rearrange("p (h d) t -> p h d t", h=2), pn_[:].rearrange("p (h d t) -> p h d t", h=2, d=2),
                       rden[:, 2 * hp:2 * hp + 2, :].unsqueeze(2).to_broadcast([128, 2, 2, 128]), ALU.mult, [pnn, "rden"], ["hT"])
                tt(hT[:], hT[:], oaT[:, :, tsl], ALU.mult, ["hT", "oaT"], ["hT"])
                el("act", lambda e: e.activation(out=sq[:], in_=hT[:], func=AF.Square), ["hT"], ["sq"])
                for h in range(4):
                    for d2 in range(2):
                        mm(pS[:, h * 128:(h + 1) * 128], ones_b[:], sq[:, 2 * h + d2, :], d2 == 0, d2 == 1, ["ones_b", "sq", "SmT"], ["pS"], sig=(h == 3 and d2 == 1))
                ts(rsh[:].rearrange("p h t -> p (h t)"), pS[:], 1.0 / 256.0, 1e-6, ALU.mult, ALU.add, ["pS"], ["rsh"])
                el("act", lambda e: e.activation(out=rsh[:], in_=rsh[:], func=AF.Ln), ["rsh"], ["rsh"])
                el("act", lambda e: e.activation(out=rsh[:], in_=rsh[:], func=AF.Exp, scale=-0.5), ["rsh"], ["rsh"])
                tt(hn[:].rearrange("p (h d) t -> p h d t", h=4), hT[:].rearrange("p (h d) t -> p h d t", h=4),
                   rsh[:].unsqueeze(2).to_broadcast([128, 4, 2, 128]), ALU.mult, ["hT", "rsh"], ["hn"])
                tt(hn[:], hn[:], zaT[:, :, tsl], ALU.mult, ["hn", "zaT"], ["hn"])
                tt(hfT[:, :, tsl], hn[:], csz[:, :, tsl], ALU.add, ["hn", "csz"], ["hfT"])
        chk(9)
        if not main:
            return
        s5_out()
        def gate_blocks(br):
            for bi_ in range(2):
                Wg_, wgn = load_block(BLK_G + 2 * br + bi_)
                Wgv = Wg_[:].rearrange("p (k n) -> p k n", k=8)
                for c4 in range(4):
                    ft = bi_ * 4 + c4
                    psg, png = acc_bank()
                    for kt in range(8):
                        mm(psg[:, 0:T], Wgv[:, kt, c4 * 128:(c4 + 1) * 128], xnT[:, kt, :], kt == 0, kt == 7, [wgn, "xnT"], [png])
                    el("act", lambda e, psg=psg, ft=ft: e.activation(out=gAll[:, ft, :], in_=psg[:, 0:T], func=AF.Sigmoid), [png], ["oaT"])

        gate_blocks(0)
        for hf_ in range(2):
            Wa, wan = load_block(BLK_AO + hf_)
            Wav = Wa[:].rearrange("p (k n) -> p k n", k=8)
            for c4 in range(4):
                ft = hf_ * 4 + c4
                psa, pna = acc_bank()
                for kt in range(8):
                    mm(psa[:, 0:T], Wav[:, kt, c4 * 128:(c4 + 1) * 128], hfT[:, kt, :], kt == 0, kt == 7, [wan, "hfT"], [pna])
                tt(mrgT[:, ft, :], psa[:, 0:T], gAll[:, ft, :], ALU.mult, [pna, "oaT"], ["zaT"])
        gate_blocks(1)
        Wbo, wbon = load_block(BLK_BO)
        Wbv = Wbo[:].rearrange("p (k n) -> p k n", k=4)
        for ft in range(8):
            psb, pnb = acc_bank()
            for kt in range(4):
                mm(psb[:, 0:T], Wbv[:, kt, ft * 128:(ft + 1) * 128], hbT[:, kt, :], kt == 0, kt == 3, [wbon, "hbT"], [pnb])
            el("act", lambda e, psb=psb: e.copy(out=m2[:].rearrange("p (c j) -> p j c", j=8), in_=psb[:, 0:T].rearrange("p (j c) -> p j c", j=8)),
               [pnb], ["m2"])
            tt(m2[:], m2[:], gAll[:, ft, :], ALU.mult, ["m2", "oaT"], ["m2"])
            tt(mrgT[:, ft, :], m2[:], mrgT[:, ft, :], ALU.add, ["m2", "zaT"], ["zaT"])
        Wo0, wo0n = load_block(BLK_WO); Wo1, wo1n = load_block(BLK_WO + 1)
        for ch in range(NCH):
            tsl = slice(ch * 128, (ch + 1) * 128)
            for hf_, (Wo_, won, pn_, pnn) in enumerate(((Wo0, wo0n, pN0, "pN0"), (Wo1, wo1n, pN1, "pN1"))):
                Wov = Wo_[:].rearrange("p (k n) -> p k n", k=8)
                for kt in range(8):
                    mm(pn_[:, :], mrgT[:, kt, tsl], Wov[:, kt, :], kt == 0, kt == 7, ["zaT", won], [pnn])
            el("act", lambda e: e.activation(out=junk[:, 0:512], in_=pN0[:], func=AF.Square, accum_out=ss2[:]), ["pN0"], ["junk", "ss2"])
            el("act", lambda e: e.activation(out=junk[:, 512:1024], in_=pN1[:], func=AF.Square, accum_out=rstd2[:]), ["pN1"], ["junk", "rstd2"])
            tt(ss2[:], ss2[:], rstd2[:], ALU.add, ["ss2", "rstd2"], ["ss2"])
            rsqrt_col(rstd2[:], ss2[:], 1.0 / 1024.0, "ss2", "rstd2")
            ob, obn = ot[ch % 2], "ot%d" % (ch % 2)
            xb_, xnm = xr[ch % 2], "xr%d" % (ch % 2)
            P.dma(lambda e, xb_=xb_, ch=ch: e.dma_start(out=xb_[:], in_=xsrc[row0 + ch * 128: row0 + (ch + 1) * 128, :]), writes=[xnm])
            for hf_, (pn_, pnn) in enumerate(((pN0, "pN0"), (pN1, "pN1"))):
                cs = slice(hf_ * 512, hf_ * 512 + 512)
                el("dve", lambda e, ob=ob, pn_=pn_, cs=cs: e.scalar_tensor_tensor(out=ob[:, cs], in0=pn_[:], scalar=rstd2[:, 0:1], in1=gpost[:, cs],
                                                                                 op0=ALU.mult, op1=ALU.mult), [pnn, "rstd2", "gpost"], [obn])
            tt(ob[:], ob[:], xb_[:], ALU.add, [obn, xnm], [obn])
            tok = P.dma(lambda e, ob=ob, ch=ch: e.dma_start(out=out[row0 + ch * 128: row0 + (ch + 1) * 128, :], in_=ob[:]), reads=[obn])
            out_toks.append(tok)

    sc_t1 = sbt([128, 2, 16], F32, "sc_t1"); sc_t2 = sbt([128, 2, 16], F32, "sc_t2")
    try:
        for ti in range(NPRE):
            nxt = (x_pre, (ti + 1) * T) if ti + 1 < NPRE else (x_main, 0)
            do_tile(x_pre, ti * T, False, ti, nxt)
    except StopBuild:
        tk = P.dma(lambda e: e.dma_start(out=out[0:128, :], in_=gpost[:]), reads=["gpost"])
        P.final_wait("sp", [tk])
        P.emit(); P.close()
        return nc
    if NPRE > 0:
        for h in range(4):
            cn, bn, nn = "Cst%d" % h, "Cbf%d" % h, "nrep%d" % h
            ts(Cst[:, h, :, :], Cst[:, h, :, :], flg[:, 0:1], None, ALU.mult, None, [cn, "flg"], [cn])
            el("act", lambda e, h=h: e.copy(out=Cbf[:, h, :, :], in_=Cst[:, h, :, :]), [cn], [bn])
            for et in range(2):
                el("dve", lambda e, h=h, et=et: e.tensor_copy(out=nrep[:, h, et, :], in_=Cst[:, h, et, 256:257].to_broadcast([128, 128])),
                   [cn], [nn])
    for ti in range(NMAIN):
        nxt = (x_main, (ti + 1) * T) if ti + 1 < NMAIN else None
        do_tile(x_main, ti * T, True, NPRE + ti, nxt)
    P.final_wait("sp", out_toks)
    P.emit()
    P.close()
    return nc


T_TILE = 256
_cache = {}


def kernel(**inputs):
    x = np.ascontiguousarray(inputs["x"], dtype=np.float32)
    Bsz, L, Dm = x.shape
    half = L // 2
    npre = half // T_TILE
    nmain = half // T_TILE
    key = (T_TILE, npre, nmain)
    if key not in _cache:
        _cache[key] = build_program(T_TILE, npre, nmain)
    nc = _cache[key]
    shared = {}
    for k, v in inputs.items():
        if k == "x":
            continue
        a = np.ascontiguousarray(np.asarray(v, dtype=np.float32)[0])
        if k in ("b_i", "b_f", "log_dt", "norm_post_g"):
            a = a.reshape(1, -1)
        shared[k] = a
    in_maps = []
    zeros = np.zeros((half, Dm), np.float32)
    for core in range(8):
        b, hf = core // 2, core % 2
        m = dict(shared)
        m["x_main"] = np.ascontiguousarray(x[b, hf * half:(hf + 1) * half])
        m["x_pre"] = zeros if hf == 0 else np.ascontiguousarray(x[b, 0:half])
        m["flag"] = np.full((128, 1), float(hf), np.float32)
        in_maps.append(m)
    res = run_bass_kernel_spmd(nc, in_maps, core_ids=list(range(8)))
    outp = np.empty((Bsz, L, Dm), np.float32)
    for core in range(8):
        b, hf = core // 2, core % 2
        outp[b, hf * half:(hf + 1) * half] = res.results[core]["out"]
    return outp
```

```python
import contextlib
import numpy as np
import concourse.bass as bass
import concourse.mybir as mybir
from concourse.bass_utils import run_bass_kernel_spmd

F32 = mybir.dt.float32
BF16 = mybir.dt.bfloat16
AF = mybir.ActivationFunctionType
ALU = mybir.AluOpType
COMPUTE = ("pe", "act", "dve", "pool")
NDMA_SLOTS = 8
DEBUG_TAGS = False
PI = float(np.pi)


class Prog:
    def __init__(self, nc):
        self.nc = nc
        self.stack = contextlib.ExitStack()
        self.engs = ("pe", "act", "dve", "pool", "sp")
        self.ops = {e: [] for e in self.engs}
        self.waited = {e: {} for e in self.engs}
        self.res = {}
        self.dma_use = {}
        self.dma_rr = {e: 0 for e in self.engs}
        self.sems = {}
        self.base = {e: 0 for e in COMPUTE}
        self.temp = None

    def sb(self, name, shape, dt):
        st = self.temp if self.temp is not None else self.stack
        return st.enter_context(self.nc.sbuf_tensor(name, list(shape), dt))

    def ps(self, name, shape, dt):
        return self.stack.enter_context(self.nc.psum_tensor(name, list(shape), dt))

    def _sem(self, key):
        if key not in self.sems:
            nm = "s_" + "_".join(str(k) for k in (key if isinstance(key, tuple) else (key,)))
            self.sems[key] = self.stack.enter_context(self.nc.semaphore(nm))
        return self.sems[key]

    def _deps(self, eng, reads, writes):
        deps = {}

        def add(tok):
            if tok is None:
                return
            k, v = tok
            if k == "pe" and eng == "pe":
                return
            if deps.get(k, -1) < v:
                deps[k] = v

        for r in reads:
            st = self.res.get(r)
            if st:
                for k, v in st[0].items():
                    add((k, v))
        for w in writes:
            st = self.res.get(w)
            if st:
                for k, v in st[0].items():
                    add((k, v))
                for k, v in st[1].items():
                    add((k, v))
        out = []
        wd = self.waited[eng]
        for k, v in deps.items():
            if wd.get(k, -1) >= v:
                continue
            wd[k] = v
            out.append((k, v))
        return out

    def _commit(self, tok, reads, writes):
        k, v = tok
        for r in reads:
            st = self.res.setdefault(r, [{}, {}])
            if st[1].get(k, -1) < v:
                st[1][k] = v
        for w in writes:
            old = self.res.get(w)
            wr = {}
            if old is not None and k not in COMPUTE:
                wr = {k2: v2 for k2, v2 in old[0].items() if k2 not in COMPUTE}
            wr[k] = v
            self.res[w] = [wr, {}]

    def _tag(self):
        if not DEBUG_TAGS:
            return None
        import sys as _sys
        f = _sys._getframe(2)
        while f is not None and f.f_code.co_name not in ("do_tile", "build_program", "s5_out", "gate_blocks", "proj_fm"):
            f = f.f_back
        return str(f.f_lineno) if f is not None else None

    def op(self, eng, fn, reads=(), writes=(), sig=True):
        waits = self._deps(eng, reads, writes)
        idx = len(self.ops[eng])
        self.ops[eng].append(dict(fn=fn, waits=waits, sig=sig, dma=None, tag=self._tag()))
        tok = (eng, idx)
        self._commit(tok, reads, writes)
        return tok

    def dma(self, fn, reads=(), writes=(), q="sp"):
        waits = self._deps(q, reads, writes)
        slot = self.dma_rr[q] % NDMA_SLOTS
        self.dma_rr[q] += 1
        key = ("d", q, slot)
        n = self.dma_use.get(key, 0)
        if n > 0:
            prev = n * 16
            if self.waited[q].get(key, -1) < prev:
                self.waited[q][key] = prev
                waits.append((key, prev))
        self.dma_use[key] = n + 1
        tok = (key, (n + 1) * 16)
        self.ops[q].append(dict(fn=fn, waits=waits, sig=False, dma=key))
        self._commit(tok, reads, writes)
        return tok

    def final_wait(self, eng, toks):
        self.ops[eng].append(dict(fn=None, waits=list(toks), sig=False, dma=None))

    def emit(self):
        nc = self.nc
        sigcount = {}
        totals = {}
        for e in COMPUTE:
            c = self.base[e]
            arr = []
            for o in self.ops[e]:
                if o["sig"]:
                    c += 1
                arr.append(c)
            need = [None] * len(arr)
            nxt = None
            for i in range(len(arr) - 1, -1, -1):
                if self.ops[e][i]["sig"]:
                    nxt = arr[i]
                need[i] = nxt
            sigcount[e] = need
            totals[e] = c
            self._sem(e)
        for k in self.dma_use:
            self._sem(k)

        def resolve(k, v):
            if k in COMPUTE:
                val = sigcount[k][v]
                assert val is not None, (k, v)
                return self.sems[k], val
            return self.sems[k], v

        with nc.Block() as block:

            def run(eng_name, eng):
                for o in self.ops[eng_name]:
                    for k, v in o["waits"]:
                        s, val = resolve(k, v)
                        eng.wait_ge(s, val)
                    if o["fn"] is None:
                        continue
                    ins = o["fn"](eng)
                    if o.get("tag"):
                        ins.annotate(o["tag"])
                    if o["dma"] is not None:
                        ins.then_inc(self.sems[o["dma"]], 16)
                    elif o["sig"]:
                        ins.then_inc(self.sems[eng_name], 1)
                for o2 in COMPUTE:
                    if o2 != eng_name and totals[o2] > 0:
                        eng.wait_ge(self.sems[o2], totals[o2])
                for k, n in self.dma_use.items():
                    eng.wait_ge(self.sems[k], n * 16)

            @block.tensor
            def _(e):
                run("pe", e)

            @block.scalar
            def _(e):
                run("act", e)

            @block.vector
            def _(e):
                run("dve", e)

            @block.gpsimd
            def _(e):
                run("pool", e)

            @block.sync
            def _(e):
                run("sp", e)

        self.base = totals
        self.ops = {e: [] for e in self.engs}
        self.waited = {e: {} for e in self.engs}
        self.res = {}

    def close(self):
        self.stack.close()


NBLK = 22
BLK_UA, BLK_UB, BLK_ZB, BLK_ZA, BLK_OA, BLK_G, BLK_AO, BLK_BO, BLK_WO, BLK_S5 = 0, 2, 3, 4, 6, 8, 12, 14, 15, 17
COL_UA, COL_ZA, COL_OA, COL_I, COL_UB, COL_ZB, COL_G = 0, 1024, 2048, 3072, 3080, 3592, 4104


def build_program(T, NPRE, NMAIN, dbg_stop=0):
    NCH = T // 128
    NC8 = T // 8
    nc = bass.Bass("TRN2", target_bir_lowering=False)
    dram = {}

    def din(name, shape):
        dram[name] = nc.dram_tensor(name, list(shape), F32, kind="ExternalInput").ap()
        return dram[name]

    x_pre = din("x_pre", [max(NPRE, 1) * T, 1024])
    x_main = din("x_main", [NMAIN * T, 1024])
    flag = din("flag", [128, 1])
    norm_pre_g = din("norm_pre_g", [1024]); w_in = din("w_in", [1024, 6152])
    conv_w = din("conv_w", [4, 1024]); conv_b = din("conv_b", [1024])
    w_q = din("w_q", [4, 256, 256]); w_k = din("w_k", [4, 256, 256]); w_v = din("w_v", [4, 256, 256])
    b_i = din("b_i", [1, 4]); b_f = din("b_f", [1, 4]); head_g = din("head_g", [1024]); skip_a = din("skip_a", [1024])
    w_a_out = din("w_a_out", [1024, 1024])
    lam_re = din("lam_re", [32, 64]); lam_im = din("lam_im", [32, 64]); log_dt = din("log_dt", [1, 32])
    B_re = din("B_re", [32, 64, 16]); B_im = din("B_im", [32, 64, 16])
    C_re = din("C_re", [32, 16, 64]); C_im = din("C_im", [32, 16, 64]); D_skip = din("D_skip", [32, 16])
    w_glu = din("w_glu", [512, 512]); b_glu = din("b_glu", [512]); w_b_out = din("w_b_out", [512, 1024])
    w_o = din("w_o", [1024, 1024]); norm_post_g = din("norm_post_g", [1, 1024])
    out = nc.dram_tensor("out", [NMAIN * T, 1024], F32, kind="ExternalOutput").ap()
    WS = nc.dram_tensor("wscratch", [NBLK, 128, 4096], BF16, kind="Internal").ap()

    P = Prog(nc)
    uid = [0]

    def sbt(shape, dt, name=None):
        uid[0] += 1
        return P.sb(name or ("t%d" % uid[0]), shape, dt)

    ident_f = sbt([128, 128], F32); ident_b = sbt([128, 128], BF16)
    maskT = sbt([128, 128], F32); mask4 = sbt([128, 4, 128], F32); ones_b = sbt([128, 128], BF16)
    P.op("pool", lambda e: e.memset(ident_f[:], 1.0), writes=["ident_f"])
    P.op("pool", lambda e: e.affine_select(out=ident_f[:], in_=ident_f[:], pattern=[[-1, 128]], compare_op=ALU.is_equal,
                                           fill=0.0, base=0, channel_multiplier=1), reads=["ident_f"], writes=["ident_f"])
    P.op("dve", lambda e: e.tensor_copy(out=ident_b[:], in_=ident_f[:]), reads=["ident_f"], writes=["ident_b"])
    P.op("pool", lambda e: e.memset(maskT[:], 1.0), writes=["maskT"])
    P.op("pool", lambda e: e.affine_select(out=maskT[:], in_=maskT[:], pattern=[[1, 128]], compare_op=ALU.is_ge,
                                           fill=0.0, base=0, channel_multiplier=-1), reads=["maskT"], writes=["maskT"])
    for h in range(4):
        P.op("pool", lambda e, h=h: e.tensor_copy(out=mask4[:, h, :], in_=maskT[:]), reads=["maskT"], writes=["mask4"])
    P.op("pool", lambda e: e.memset(ones_b[:], 1.0), writes=["ones_b"])

    gpre = sbt([128, 8], F32); cb = sbt([128, 8], F32); hg = sbt([128, 8], F32); skp = sbt([128, 8], F32)
    cw = sbt([128, 8, 4], F32); bglu = sbt([128, 4], F32); gpost = sbt([128, 1024], F32); bif = sbt([128, 8], F32)
    flg = sbt([128, 1], F32)
    nonc = dict(allow_slow_non_contiguous=True)
    P.dma(lambda e: e.dma_start(out=gpre[:], in_=norm_pre_g.rearrange("(k p) -> p k", p=128), **nonc), writes=["gpre"])
    P.dma(lambda e: e.dma_start(out=cb[:], in_=conv_b.rearrange("(k p) -> p k", p=128), **nonc), writes=["cb"])
    P.dma(lambda e: e.dma_start(out=hg[:], in_=head_g.rearrange("(k p) -> p k", p=128), **nonc), writes=["hg"])
    P.dma(lambda e: e.dma_start(out=skp[:], in_=skip_a.rearrange("(k p) -> p k", p=128), **nonc), writes=["skp"])
    for k in range(4):
        P.dma(lambda e, k=k: e.dma_start(out=cw[:, :, k], in_=conv_w[k].rearrange("(m p) -> p m", p=128), **nonc), writes=["cw"])
    P.dma(lambda e: e.dma_start(out=bglu[:], in_=b_glu.rearrange("(k p) -> p k", p=128), **nonc), writes=["bglu"])
    P.dma(lambda e: e.dma_start(out=gpost[:], in_=norm_post_g.partition_broadcast(128)), writes=["gpost"])
    P.dma(lambda e: e.dma_start(out=bif[:, 0:4], in_=b_i.partition_broadcast(128)), writes=["bif"])
    P.dma(lambda e: e.dma_start(out=bif[:, 4:8], in_=b_f.partition_broadcast(128)), writes=["bif"])
    P.dma(lambda e: e.dma_start(out=flg[:], in_=flag), writes=["flg"])

    Wqkv = [sbt([128, 4, 2, 256], BF16) for _ in range(3)]
    Wif = sbt([128, 8, 8], BF16); Wglu = sbt([128, 4, 512], BF16); cdiag = sbt([128, 8, 4, 128], BF16)
    AR32 = sbt([128, 2, 16], F32); ANI = sbt([128, 16], F32); API = sbt([128, 16], F32)
    pA = P.ps("pA", [128, 512], F32); pB = P.ps("pB", [128, 512], F32)
    pT = P.ps("pT", [128, 1024], BF16); pGD = P.ps("pGD", [128, 512], F32)
    pS = P.ps("pS", [128, 512], F32); pN0 = P.ps("pN0", [128, 512], F32); pN1 = P.ps("pN1", [128, 512], F32)
    pM = P.ps("pM", [128, 512], F32)
    P.temp = contextlib.ExitStack()
    stg = sbt([128, 4096], F32, "stg")
    stgb = sbt([128, 4096], BF16, "stgb")
    stgb2 = sbt([128, 4096], BF16, "stgb2")
    for wi, (wsrc, scl) in enumerate(((w_q, 1.0), (w_k, 1.0 / 16.0), (w_v, 1.0))):
        P.dma(lambda e, wsrc=wsrc: e.dma_start(out=stg[:, 0:2048].rearrange("p (h d n) -> p h d n", h=4, d=2),
                                               in_=wsrc.rearrange("h (d p) n -> p h d n", p=128)), writes=["stg"])
        P.op("dve", lambda e, wi=wi, scl=scl: e.tensor_scalar(out=Wqkv[wi][:].rearrange("p h d n -> p (h d n)"), in0=stg[:, 0:2048],
                                                              scalar1=scl, scalar2=None, op0=ALU.mult), reads=["stg"], writes=["Wqkv%d" % wi])
    Wq, Wk, Wv = Wqkv
    P.dma(lambda e: e.dma_start(out=stg[:, 0:64].rearrange("p (k n) -> p k n", k=8),
                                in_=w_in[:, COL_I:COL_I + 8].rearrange("(k p) n -> p k n", p=128), **nonc), writes=["stg"])
    for kt in range(8):
        P.op("dve", lambda e, kt=kt: e.tensor_scalar(out=Wif[:, kt, :], in0=stg[:, kt * 8:(kt + 1) * 8], scalar1=gpre[:, kt:kt + 1],
                                                     scalar2=None, op0=ALU.mult), reads=["stg", "gpre"], writes=["Wif"])
    P.dma(lambda e: e.dma_start(out=stg[:, 0:2048].rearrange("p (k n) -> p k n", k=4),
                                in_=w_glu.rearrange("(k p) n -> p k n", p=128)), writes=["stg"])
    P.op("dve", lambda e: e.tensor_copy(out=Wglu[:].rearrange("p k n -> p (k n)"), in_=stg[:, 0:2048]), reads=["stg"], writes=["Wglu"])
    for mt in range(8):
        for k in range(4):
            P.op("dve", lambda e, mt=mt, k=k: e.tensor_scalar(out=cdiag[:, mt, k, :], in0=ident_f[:], scalar1=cw[:, mt, k:k + 1],
                                                               scalar2=None, op0=ALU.mult), reads=["ident_f", "cw"], writes=["cdiag"])

    def stage_block(blk, src_ap_f, scale_gpre, nk):
        ncol = 4096 // nk
        P.dma(lambda e: e.dma_start(out=stg[:].rearrange("p (k n) -> p k n", k=nk), in_=src_ap_f), writes=["stg"])
        if scale_gpre:
            for kt in range(nk):
                P.op("act",
                     lambda e, kt=kt: e.activation(out=stgb[:, kt * ncol:(kt + 1) * ncol], in_=stg[:, kt * ncol:(kt + 1) * ncol],
                                                   func=AF.Copy, scale=gpre[:, kt:kt + 1]),
                     reads=["stg", "gpre"], writes=["stgb"])
        else:
            P.op("act", lambda e: e.copy(out=stgb[:, 0:2048], in_=stg[:, 0:2048]), reads=["stg"], writes=["stgb"])
            P.op("act", lambda e: e.copy(out=stgb[:, 2048:4096], in_=stg[:, 2048:4096]), reads=["stg"], writes=["stgb"])
        P.dma(lambda e: e.dma_start(out=WS[blk], in_=stgb[:]), reads=["stgb"], writes=["WS%d" % blk])

    def win_cols(c0):
        return w_in[:, c0:c0 + 512].rearrange("(k p) n -> p k n", p=128)

    win_blocks = [(BLK_UA, COL_UA), (BLK_UA + 1, COL_UA + 512), (BLK_UB, COL_UB), (BLK_ZB, COL_ZB), (BLK_ZA, COL_ZA),
                  (BLK_ZA + 1, COL_ZA + 512), (BLK_OA, COL_OA), (BLK_OA + 1, COL_OA + 512)] + [(BLK_G + i, COL_G + 512 * i) for i in range(4)]
    pending = []
    for blk, c0 in win_blocks:
        pending.append((blk, win_cols(c0), True, 8))
    for i in range(2):
        pending.append((BLK_AO + i, w_a_out[:, 512 * i:512 * i + 512].rearrange("(k p) n -> p k n", p=128), False, 8))
        pending.append((BLK_WO + i, w_o[:, 512 * i:512 * i + 512].rearrange("(k p) n -> p k n", p=128), False, 8))
    pending.append((BLK_BO, w_b_out.rearrange("(k p) n -> p k n", p=128), False, 4))

    def stage_some(n=1):
        for _ in range(n):
            if pending:
                stage_block(*pending.pop(0))


    def small(n, name=None):
        return sbt([128, n], F32, name)

    cnt = [0]

    def el(eng, fn, r, w):
        P.op(eng, fn, reads=r, writes=w)

    def tt(outp, a, b, op, r, w, eng="dve"):
        el(eng, lambda e: e.tensor_tensor(out=outp, in0=a, in1=b, op=op), r, w)

    def ts(outp, a, s1, s2, op0, op1, r, w, eng="dve"):
        if op1 is None:
            el(eng, lambda e: e.tensor_scalar(out=outp, in0=a, scalar1=s1, scalar2=None, op0=op0), r, w)
        else:
            el(eng, lambda e: e.tensor_scalar(out=outp, in0=a, scalar1=s1, scalar2=s2, op0=op0, op1=op1), r, w)

    LR = small(32); LI = small(32); DT = small(32)
    for hf in range(2):
        sl = slice(64 * hf, 64 * hf + 64)
        P.dma(lambda e, sl=sl: e.dma_start(out=LR[sl, :], in_=lam_re.rearrange("g p -> p g"), **nonc), writes=["LR"])
        P.dma(lambda e, sl=sl: e.dma_start(out=LI[sl, :], in_=lam_im.rearrange("g p -> p g"), **nonc), writes=["LI"])
    P.dma(lambda e: e.dma_start(out=DT[:], in_=log_dt.partition_broadcast(128)), writes=["DT"])
    el("act", lambda e: e.activation(out=DT[:], in_=DT[:], func=AF.Exp), ["DT"], ["DT"])
    TH = small(32); MAG = small(32); t0 = small(32); t1 = small(32); t2 = small(32); kk = small(32)
    tt(TH[:], LI[:], DT[:], ALU.mult, ["LI", "DT"], ["TH"])
    tt(t0[:], LR[:], DT[:], ALU.mult, ["LR", "DT"], ["t0"])
    el("act", lambda e: e.activation(out=MAG[:], in_=t0[:], func=AF.Exp), ["t0"], ["MAG"])
    IMAG2 = small(32)
    el("act", lambda e: e.activation(out=IMAG2[:], in_=t0[:], func=AF.Exp, scale=-2.0), ["t0"], ["IMAG2"])

    def sin_of(dst, src, shift, nm):
        ts(t1[:], src, shift, None, ALU.add, None, [nm, "t1"], ["t1"])
        el("pool", lambda e: e.memset(kk[:], 0.0), [], ["kk"])
        for m in range(7):
            ts(t2[:], t1[:], (2 * m + 1) * PI, None, ALU.is_gt, None, ["t1"], ["t2"])
            tt(kk[:], kk[:], t2[:], ALU.add, ["kk", "t2"], ["kk"])
        ts(kk[:], kk[:], -2.0 * PI, None, ALU.mult, None, ["kk"], ["kk"])
        tt(t1[:], t1[:], kk[:], ALU.add, ["t1", "kk"], ["t1"])
        el("act", lambda e: e.activation(out=dst, in_=t1[:], func=AF.Sin), ["t1"], [nm + "_s"])

    SN = small(32); CS = small(32)
    sin_of(SN[:], TH[:], 0.0, "TH")
    sin_of(CS[:], TH[:], PI / 2.0, "TH")
    pwr = sbt([128, 9, 32], F32); pwi = sbt([128, 9, 32], F32); pnr = sbt([128, 8, 32], F32); pni = sbt([128, 8, 32], F32)
    el("pool", lambda e: e.memset(pwr[:, 0, :], 1.0), [], ["pw"]); el("pool", lambda e: e.memset(pwi[:, 0, :], 0.0), [], ["pw"])
    el("pool", lambda e: e.memset(pnr[:, 0, :], 1.0), [], ["pn"]); el("pool", lambda e: e.memset(pni[:, 0, :], 0.0), [], ["pn"])
    tt(pwr[:, 1, :], MAG[:], CS[:], ALU.mult, ["MAG", "TH_s"], ["pw"])
    tt(pwi[:, 1, :], MAG[:], SN[:], ALU.mult, ["MAG", "TH_s"], ["pw"])
    tt(pnr[:, 1, :], pwr[:, 1, :], IMAG2[:], ALU.mult, ["pw", "IMAG2"], ["pn"])
    tt(t0[:], pwi[:, 1, :], IMAG2[:], ALU.mult, ["pw", "IMAG2"], ["t0"])
    ts(pni[:, 1, :], t0[:], -1.0, None, ALU.mult, None, ["t0"], ["pn"])

    def cmul(or_, oi_, ar, ai, br, bi, r, w):
        raise NotImplementedError

    u0 = small(32); u1 = small(32)
    for k in range(1, 8):
        for (xr, xi, nm, lim) in ((pwr, pwi, "pw", 9), (pnr, pni, "pn", 8)):
            if k + 1 >= lim:
                continue
            tt(u0[:], xr[:, k, :], xr[:, 1, :], ALU.mult, [nm], ["u0"])
            tt(u1[:], xi[:, k, :], xi[:, 1, :], ALU.mult, [nm], ["u1"])
            tt(xr[:, k + 1, :], u0[:], u1[:], ALU.subtract, ["u0", "u1"], [nm])
            tt(u0[:], xr[:, k, :], xi[:, 1, :], ALU.mult, [nm], ["u0"])
            tt(u1[:], xi[:, k, :], xr[:, 1, :], ALU.mult, [nm], ["u1"])
            tt(xi[:, k + 1, :], u0[:], u1[:], ALU.add, ["u0", "u1"], [nm])
    den = small(32); qr = small(32); qi = small(32); nr = small(32)
    tt(u0[:], LR[:], LR[:], ALU.mult, ["LR"], ["u0"]); tt(u1[:], LI[:], LI[:], ALU.mult, ["LI"], ["u1"])
    tt(den[:], u0[:], u1[:], ALU.add, ["u0", "u1"], ["den"])
    el("dve", lambda e: e.reciprocal(out=den[:], in_=den[:]), ["den"], ["den"])
    ts(nr[:], pwr[:, 1, :], -1.0, None, ALU.add, None, ["pw"], ["nr"])
    tt(u0[:], nr[:], LR[:], ALU.mult, ["nr", "LR"], ["u0"]); tt(u1[:], pwi[:, 1, :], LI[:], ALU.mult, ["pw", "LI"], ["u1"])
    tt(qr[:], u0[:], u1[:], ALU.add, ["u0", "u1"], ["qr"]); tt(qr[:], qr[:], den[:], ALU.mult, ["qr", "den"], ["qr"])
    tt(u0[:], pwi[:, 1, :], LR[:], ALU.mult, ["pw", "LR"], ["u0"]); tt(u1[:], nr[:], LI[:], ALU.mult, ["nr", "LI"], ["u1"])
    tt(qi[:], u0[:], u1[:], ALU.subtract, ["u0", "u1"], ["qi"]); tt(qi[:], qi[:], den[:], ALU.mult, ["qi", "den"], ["qi"])
    Br = sbt([128, 32, 16], F32); Bi = sbt([128, 32, 16], F32); bbr = sbt([128, 32, 16], F32); bbi = sbt([128, 32, 16], F32)
    v0 = sbt([128, 32, 16], F32); v1 = sbt([128, 32, 16], F32)
    for hf in range(2):
        sl = slice(64 * hf, 64 * hf + 64)
        P.dma(lambda e, sl=sl: e.dma_start(out=Br[sl], in_=B_re.rearrange("g p n -> p g n"), **nonc), writes=["Br"])
        P.dma(lambda e, sl=sl: e.dma_start(out=Bi[sl], in_=B_im.rearrange("g p n -> p g n"), **nonc), writes=["Bi"])

    def bc(s):
        return s.unsqueeze(2).to_broadcast([128, 32, 16])

    def cmul3(orr, oii, sr, si, sn, xr, xi, xn, on):
        xn = [xn] if isinstance(xn, str) else list(xn)
        sn = [sn] if isinstance(sn, str) else list(sn)
        tt(v0[:], xr, bc(sr), ALU.mult, xn + sn, ["v0"]); tt(v1[:], xi, bc(si), ALU.mult, xn + sn, ["v1"])
        tt(orr, v0[:], v1[:], ALU.subtract, ["v0", "v1"], [on])
        tt(v0[:], xi, bc(sr), ALU.mult, xn + sn, ["v0"]); tt(v1[:], xr, bc(si), ALU.mult, xn + sn, ["v1"])
        tt(oii, v0[:], v1[:], ALU.add, ["v0", "v1"], [on])

    el("dve", lambda e: e.tensor_copy(out=u0[:], in_=qr[:]), ["qr"], ["qq"])
    cmul3(bbr[:], bbi[:], qr[:], qi[:], ["qr", "qi"], Br[:], Bi[:], ["Br", "Bi"], "bb")
    CTr = sbt([128, 32, 16], F32); CTi = sbt([128, 32, 16], F32)
    Cdup = sbt([128, 4, 2, 64], F32)
    for (Csrc, CTt, nm) in ((C_re, CTr, "CTr"), (C_im, CTi, "CTi")):
        for d in range(2):
            P.dma(lambda e, Csrc=Csrc, d=d: e.dma_start(out=Cdup[:, :, d, :], in_=Csrc.rearrange("(t g) n p -> (g n) t p", t=4)),
                  writes=["Cdup"])
        for t in range(4):
            P.op("pe", lambda e, t=t: e.transpose(out=pA[:, t * 128:(t + 1) * 128], in_=Cdup[:, t, :, :].rearrange("q d p -> q (d p)"),
                                                  identity=ident_f[:]), reads=["Cdup", "ident_f"], writes=["pA"])
        el("act", lambda e, CTt=CTt: e.copy(out=CTt[:].rearrange("p g n -> p (g n)"), in_=pA[:]), ["pA"], [nm])
    stage_some(100)
    Er = sbt([128, 32, 8, 16], F32); Ei = sbt([128, 32, 8, 16], F32)
    Fr = sbt([128, 32, 8, 16], F32); Fi = sbt([128, 32, 8, 16], F32)
    for j in range(8):
        cmul3(Er[:, :, j, :], Ei[:, :, j, :], pwr[:, 7 - j, :], pwi[:, 7 - j, :], "pw", bbr[:], bbi[:], "bb", "E")
        cmul3(Fr[:, :, j, :], Fi[:, :, j, :], pwr[:, j + 1, :], pwi[:, j + 1, :], "pw", CTr[:], CTi[:], ["CTr", "CTi"], "F")
    halfm = sbt([128, 2], F32)
    el("pool", lambda e: e.memset(halfm[:], 0.0), [], ["halfm"])
    el("pool", lambda e: e.memset(halfm[0:64, 0:1], 1.0), ["halfm"], ["halfm"])
    el("pool", lambda e: e.memset(halfm[64:128, 1:2], 1.0), ["halfm"], ["halfm"])
    for ri, (Et, blk) in enumerate(((Er, BLK_S5), (Ei, BLK_S5 + 1))):
        el("pool", lambda e: e.memset(stgb2[:], 0.0), ["stgb2"], ["stgb2"])
        for g in range(32):
            ps_ = pA if g % 2 == 0 else pB
            nm = "pA" if g % 2 == 0 else "pB"
            P.op("pe", lambda e, Et=Et, g=g, ps_=ps_: e.transpose(out=ps_[:, 0:64], in_=Et[0:64, g, :, :].rearrange("p j n -> p (j n)"),
                                                                  identity=ident_f[0:64, 0:64]), reads=["E", "ident_f"], writes=[nm])
            c0 = g * 128 + 64 * (g % 2)
            el("dve", lambda e, ps_=ps_, c0=c0: e.tensor_copy(out=stgb2[:, c0:c0 + 64], in_=ps_[:, 0:64]), [nm], ["stgb2"])
        P.dma(lambda e, blk=blk: e.dma_start(out=WS[blk], in_=stgb2[:]), reads=["stgb2"], writes=["WS%d" % blk])
    for ri, (Ft, blk, sg) in enumerate(((Fr, BLK_S5 + 3, 1.0), (Fi, BLK_S5 + 4, -1.0))):
        for g in range(32):
            P.op("dve",
                 lambda e, Ft=Ft, g=g, sg=sg: e.tensor_scalar(out=stgb2[:, g * 128:(g + 1) * 128], in0=Ft[:, g, :, :].rearrange("p j n -> p (j n)"),
                                                              scalar1=halfm[:, (g % 2):(g % 2) + 1], scalar2=sg, op0=ALU.mult, op1=ALU.mult),
                 reads=["F", "halfm"], writes=["stgb2"])
        P.dma(lambda e, blk=blk: e.dma_start(out=WS[blk], in_=stgb2[:]), reads=["stgb2"], writes=["WS%d" % blk])
    Gr = sbt([128, 32, 8, 16], F32); Gni = sbt([128, 32, 8, 16], F32)
    i8r = small(32); i8i = small(32)
    tt(u0[:], pnr[:, 7, :], pnr[:, 1, :], ALU.mult, ["pn"], ["u0"]); tt(u1[:], pni[:, 7, :], pni[:, 1, :], ALU.mult, ["pn"], ["u1"])
    tt(i8r[:], u0[:], u1[:], ALU.subtract, ["u0", "u1"], ["i8"])
    tt(u0[:], pnr[:, 7, :], pni[:, 1, :], ALU.mult, ["pn"], ["u0"]); tt(u1[:], pni[:, 7, :], pnr[:, 1, :], ALU.mult, ["pn"], ["u1"])
    tt(i8i[:], u0[:], u1[:], ALU.add, ["u0", "u1"], ["i8"])
    for j in range(8):
        cmul3(Gr[:, :, j, :], Gni[:, :, j, :], i8r[:], i8i[:], "i8", Fr[:, :, j, :], Fi[:, :, j, :], "F", "G")
    ts(Gni[:], Gni[:], -1.0, None, ALU.mult, None, ["G"], ["G"])
    bmask = sbt([128, 8, 16], F32); Dcol = sbt([128, 32], F32)
    el("pool", lambda e: e.memset(bmask[:], 1.0), [], ["bmask"])
    el("pool", lambda e: e.affine_select(out=bmask[:], in_=bmask[:], pattern=[[16, 8], [0, 16]], compare_op=ALU.is_ge, fill=0.0,
                                         base=15, channel_multiplier=-1), ["bmask"], ["bmask"])
    for j in range(8):
        P.dma(lambda e, j=j: e.dma_start(out=Dcol[16 * j:16 * j + 16, :], in_=D_skip.rearrange("g n -> n g"), **nonc), writes=["Dcol"])
    wtmp = sbt([128, 128], F32)
    for g in range(32):
        ps_ = pA if g % 2 == 0 else pB
        nm = "pA" if g % 2 == 0 else "pB"
        P.op("pe", lambda e, g=g, ps_=ps_: e.matmul(ps_[:, 0:128], lhsT=Er[0:64, g, :, :].rearrange("p j n -> p (j n)"),
                                                    rhs=Gr[0:64, g, :, :].rearrange("p j n -> p (j n)"), start=True, stop=False),
             reads=["E", "G"], writes=[nm], sig=False)
        P.op("pe", lambda e, g=g, ps_=ps_: e.matmul(ps_[:, 0:128], lhsT=Ei[0:64, g, :, :].rearrange("p j n -> p (j n)"),
                                                    rhs=Gni[0:64, g, :, :].rearrange("p j n -> p (j n)"), start=False, stop=True),
             reads=["E", "G"], writes=[nm])
        tt(wtmp[:], ps_[:, 0:128], bmask[:].rearrange("p j n -> p (j n)"), ALU.mult, [nm, "bmask"], ["wtmp"])
        el("dve", lambda e, g=g: e.scalar_tensor_tensor(out=stgb2[:, g * 128:(g + 1) * 128], in0=ident_f[:], scalar=Dcol[:, g:g + 1],
                                                        in1=wtmp[:], op0=ALU.mult, op1=ALU.add), ["ident_f", "Dcol", "wtmp", "stgb2"], ["stgb2"])
    P.dma(lambda e: e.dma_start(out=WS[BLK_S5 + 2], in_=stgb2[:]), reads=["stgb2"], writes=["WS%d" % (BLK_S5 + 2)])
    for g2 in range(2):
        sl = slice(64 * g2, 64 * g2 + 64)
        src_r = pwr[sl, 8, :].rearrange("p (q t) -> p q t", t=2)[:, :, g2]
        src_i = pwi[sl, 8, :].rearrange("p (q t) -> p q t", t=2)[:, :, g2]
        el("dve", lambda e, sl=sl, src_r=src_r: e.tensor_copy(out=AR32[sl, 0, :], in_=src_r), ["pw"], ["AR32"])
        el("dve", lambda e, sl=sl, src_r=src_r: e.tensor_copy(out=AR32[sl, 1, :], in_=src_r), ["pw"], ["AR32"])
        el("dve", lambda e, sl=sl, src_i=src_i: e.tensor_copy(out=API[sl, :], in_=src_i), ["pw"], ["API"])
        ts(ANI[sl, :], src_i, -1.0, None, ALU.mult, None, ["pw"], ["ANI"])

    stage_some(100)
    if dbg_stop == 1:
        tk = P.dma(lambda e: e.dma_start(out=out[0:128, :], in_=gpost[:]), reads=["gpost"])
        P.final_wait("sp", [tk])
        P.emit(); P.temp.close(); P.close()
        return nc
    P.emit()
    P.temp.close()
    P.temp = None
    NSLOT = 3
    wslot = [sbt([128, 4096], BF16, "wslot%d" % i) for i in range(NSLOT)]
    slot_rr = [0]

    def load_block(blk):
        s = slot_rr[0] % NSLOT
        slot_rr[0] += 1
        P.dma(lambda e: e.dma_start(out=wslot[s][:], in_=WS[blk]), reads=["WS%d" % blk], writes=["wslot%d" % s])
        return wslot[s], "wslot%d" % s

    xs = [sbt([128, 1024], F32, "xs%d" % i) for i in range(2)]
    xr = [sbt([128, 1024], F32, "xr%d" % i) for i in range(2)]
    junk = sbt([128, 1024], BF16); xnb = sbt([128, 1024], BF16)
    ss = small(1); rstd = small(1)
    xnT = sbt([128, 8, T], BF16, "xnT"); uaT = sbt([128, 8, T + 4], BF16, "uaT"); cT = sbt([128, 8, T], BF16, "cT")
    zaT = sbt([128, 8, T], BF16, "zaT"); oaT = sbt([128, 8, T], BF16, "oaT"); zbT = sbt([128, 4, T], BF16, "zbT")
    qT = sbt([128, 4, 2, T], BF16, "qT"); kT = sbt([128, 4, 2, T], BF16, "kT")
    ktm = sbt([128, NCH, 4, 256], BF16, "ktm"); vw = sbt([128, NCH, 4, 264], BF16, "vw")
    gates = sbt([128, NCH, 8], F32, "gates")
    hfT = sbt([128, 8, T], BF16, "hfT"); csz = sbt([128, 8, T], BF16, "csz"); mrgT = zaT; hbT = sbt([128, 4, T], BF16, "hbT")
    Cst = sbt([128, 4, 2, 264], F32, "Cst"); Cbf = sbt([128, 4, 2, 264], BF16, "Cbf")
    nst = sbt([128, 4, 2], F32, "nst"); nrep = sbt([128, 4, 2, 128], BF16, "nrep")
    Utm = sbt([128, 32, 8, 16], BF16, "Utm"); U2 = sbt([128, 32, NC8], BF16, "U2")
    Xall = sbt([128, NC8, 2, 16], F32, "Xall"); Sall = sbt([128, NC8 + 1, 2, 16], F32, "Sall"); Sbf = sbt([128, 2, 16, NC8], BF16, "Sbf")
    Yg = sbt([128, 32, NC8], BF16, "Yg"); Ytm = Utm[:].rearrange("p g j n -> p (g j n)").rearrange("p (j c) -> p j c", j=8); yT = sbt([128, 4, T], BF16, "yT")
    for (t_, nm) in ((Cst, "Cst"), (Cbf, "Cbf"), (nst, "nst"), (nrep, "nrep"), (Sall, "Sall"), (uaT, "uaT"), (vw, "vw")):
        flat = t_[:]
        wnames = [nm] + (["%s%d" % (nm, h_) for h_ in range(4)] if nm in ("Cst", "Cbf", "nrep") else [])
        el("pool", lambda e, flat=flat: e.memset(flat, 0.0), [], wnames)

    e1a = sbt([128, NCH, 4], F32); lfa = sbt([128, NCH, 4], F32); emb4 = sbt([128, NCH, 4, 128], F32); dec4 = sbt([128, NCH, 4], F32)
    wcol4 = sbt([128, NCH, 4], F32); wcolb4 = sbt([128, NCH, 4, 2], BF16); wrep4 = sbt([128, NCH, 4, 128], BF16)
    lfrep = sbt([128, 4, 128], F32); lf = small(4, "lf"); ig = small(4); bcol = small(4); wcol = small(4, "wcol"); wcolb = sbt([128, 4, 2], BF16)
    wrep = sbt([128, 4, 128], BF16); emb = sbt([128, 4, 128], F32, "emb"); dec = small(4, "dec"); SmT = sbt([128, 4, 128], BF16, "SmT")
    aden = sbt([128, 4, 128], F32); rden = sbt([128, 4, 128], F32, "rden"); hT = sbt([128, 8, 128], F32, "hT"); sq = sbt([128, 8, 128], BF16)
    rsh = sbt([128, 4, 128], F32); hn = sbt([128, 8, 128], F32, "hn"); e1 = small(4)
    gAll = oaT; m1 = sbt([128, T], F32); m2 = sbt([128, T], F32)
    ot = [sbt([128, 1024], F32, "ot%d" % i) for i in range(2)]
    ss2 = small(1); rstd2 = small(1); sgl = sbt([128, T], BF16); xg = sbt([128, T], F32)

    def rsqrt_col(dst, src, scale, nm_src, nm_dst):
        ts(dst, src, scale, 1e-6, ALU.mult, ALU.add, [nm_src], [nm_dst])
        el("act", lambda e: e.activation(out=dst, in_=dst, func=AF.Ln), [nm_dst], [nm_dst])
        el("act", lambda e: e.activation(out=dst, in_=dst, func=AF.Exp, scale=-0.5), [nm_dst], [nm_dst])

    def mm(outp, lhsT, rhs, start, stop, r, w, sig=None):
        P.op("pe", lambda e: e.matmul(outp, lhsT=lhsT, rhs=rhs, start=start, stop=stop), reads=r, writes=w,
             sig=(stop if sig is None else sig))

    acc_rr = [0]

    def acc_bank():
        acc_rr[0] += 1
        return (pA, "pA") if acc_rr[0] % 2 else (pB, "pB")

    out_toks = []

    class StopBuild(Exception):
        pass

    def chk(level):
        if dbg_stop == level:
            raise StopBuild()

    prefetched = [False]

    def do_tile(xsrc, row0, main, ti, nxt=None):
        def issue_x(src_, r0, sub):
            xb2, xn2 = xs[sub % 2], "xs%d" % (sub % 2)
            P.dma(lambda e: e.dma_start(out=xb2[:], in_=src_[r0 + sub * 128: r0 + (sub + 1) * 128, :]), writes=[xn2])

        for sub in range(NCH):
            xb_, xnm = xs[sub % 2], "xs%d" % (sub % 2)
            if not (prefetched[0] and sub < 2):
                issue_x(xsrc, row0, sub)
            el("act", lambda e, xb_=xb_: e.activation(out=junk[:], in_=xb_[:], func=AF.Square, accum_out=ss[:]), [xnm], ["junk", "ss"])
            rsqrt_col(rstd[:], ss[:], 1.0 / 1024.0, "ss", "rstd")
            ts(xnb[:], xb_[:], rstd[:, 0:1], None, ALU.mult, None, [xnm, "rstd"], ["xnb"])
            for kt in range(8):
                P.op("pe", lambda e, kt=kt: e.transpose(out=pT[:, kt * 128:(kt + 1) * 128], in_=xnb[:, kt * 128:(kt + 1) * 128], identity=ident_b[:]),
                     reads=["xnb", "ident_b"], writes=["pT"], sig=(kt == 7))
            el("act", lambda e, sub=sub: e.copy(out=xnT[:, :, sub * 128:(sub + 1) * 128], in_=pT[:].rearrange("p (k t) -> p k t", k=8)),
               ["pT"], ["xnT"])

        prefetched[0] = False
        if nxt is not None and NCH <= 2:
            for sub in range(min(2, NCH)):
                issue_x(nxt[0], nxt[1], sub)
            prefetched[0] = True
        chk(2)

        def proj_fm(blk, ncolt, evac):
            W, wn = load_block(blk)
            Wv_ = W[:].rearrange("p (k n) -> p k n", k=8)
            for ct in range(ncolt):
                ps_, pn = acc_bank()
                for kt in range(8):
                    mm(ps_[:, 0:T], Wv_[:, kt, ct * 128:(ct + 1) * 128], xnT[:, kt, :], kt == 0, kt == 7, [wn, "xnT"], [pn])
                evac(ct, ps_, pn)

        el("dve", lambda e: e.tensor_copy(out=uaT[:, :, 1:4], in_=uaT[:, :, T + 1:T + 4]), ["uaT"], ["uaT"])
        for half in range(2):
            proj_fm(BLK_UA + half, 4, lambda ct, ps_, pn, half=half: el(
                "act", lambda e: e.copy(out=uaT[:, half * 4 + ct, 4:T + 4], in_=ps_[:, 0:T]), [pn], ["uaT"]))
        chk(3)
        for ch in range(NCH):
            for kt in range(8):
                mm(pM[:, 0:8], xnT[:, kt, ch * 128:(ch + 1) * 128], Wif[:, kt, :], kt == 0, kt == 7, ["xnT", "Wif"], ["pM"])
            tt(gates[:, ch, :], pM[:, 0:8], bif[:], ALU.add, ["pM", "bif"], ["gates"])
        chk(4)
        W, wn = load_block(BLK_UB)
        Wv_ = W[:].rearrange("p (k n) -> p k n", k=8)
        for j in range(8):
            ps_, pn = acc_bank()
            for kt in range(8):
                lhs = xnT[:, kt, :].rearrange("p (c j) -> p j c", j=8)[:, j, :]
                mm(ps_[0:NC8, :], lhs, Wv_[:, kt, :], kt == 0, kt == 7, [wn, "xnT"], [pn])
            el("act" if j % 2 else "dve",
               (lambda e, ps_=ps_, j=j: e.copy(out=Utm[0:NC8, :, j, :], in_=ps_[0:NC8, :].rearrange("c (g n) -> c g n", g=32))) if j % 2 else
               (lambda e, ps_=ps_, j=j: e.tensor_copy(out=Utm[0:NC8, :, j, :], in_=ps_[0:NC8, :].rearrange("c (g n) -> c g n", g=32))), [pn], ["Utm"])
        chk(5)
        for g in range(32):
            P.op("pe", lambda e, g=g: e.transpose(out=pT[:, g * NC8:(g + 1) * NC8], in_=Utm[0:NC8, g, :, :].rearrange("c j n -> c (j n)"),
                                                  identity=ident_b[0:NC8, 0:NC8]), reads=["Utm", "ident_b"], writes=["pT"], sig=(g == 31))
        el("act", lambda e: e.copy(out=U2[:].rearrange("p g c -> p (g c)"), in_=pT[:, 0:32 * NC8]), ["pT"], ["U2"])
        W1r, w1rn = load_block(BLK_S5)
        W1i, w1in = load_block(BLK_S5 + 1)
        for ri, (Wt, wn_) in enumerate(((W1r, w1rn), (W1i, w1in))):
            Wg = Wt[:].rearrange("p (g n) -> p g n", g=32)
            for q in range(16):
                for g2 in range(2):
                    g = 2 * q + g2
                    mm(pM[:, (q * NC8):(q + 1) * NC8], Wg[:, g, :], U2[:, g, :], g2 == 0, g2 == 1, [wn_, "U2"], ["pM"], sig=(q == 15 and g2 == 1))
            el("act" if ri == 0 else "dve",
               (lambda e, ri=ri: e.copy(out=Xall[:, :, ri, :].rearrange("p c q -> p q c"), in_=pM[:, 0:16 * NC8].rearrange("p (q c) -> p q c", q=16))) if ri == 0 else
               (lambda e, ri=ri: e.tensor_copy(out=Xall[:, :, ri, :].rearrange("p c q -> p q c"), in_=pM[:, 0:16 * NC8].rearrange("p (q c) -> p q c", q=16))),
               ["pM"], ["Xall"])
        chk(6)
        T1 = sbt([128, 2, 16], F32, "scT1_%d" % ti) if False else None
        for c in range(NC8):
            sp_ = Sall[:, c, :, :]
            sn_ = Sall[:, c + 1, :, :]
            P.op("pool", lambda e, sp_=sp_: e.tensor_tensor(out=sc_t1[:], in0=sp_, in1=AR32[:], op=ALU.mult), reads=["Sall"], writes=["sc_t1"])
            P.op("pool", lambda e, c=c: e.tensor_tensor(out=sc_t2[:, 0, :], in0=Sall[:, c, 1, :], in1=ANI[:], op=ALU.mult), reads=["Sall"], writes=["sc_t2"])
            P.op("pool", lambda e, c=c: e.tensor_tensor(out=sc_t2[:, 1, :], in0=Sall[:, c, 0, :], in1=API[:], op=ALU.mult), reads=["Sall"], writes=["sc_t2"])
            P.op("pool", lambda e, c=c: e.tensor_tensor(out=sc_t1[:], in0=sc_t1[:], in1=Xall[:, c, :, :], op=ALU.add), reads=["sc_t1", "Xall"], writes=["sc_t1"])
            P.op("pool", lambda e, sn_=sn_: e.tensor_tensor(out=sn_, in0=sc_t1[:], in1=sc_t2[:], op=ALU.add), reads=["sc_t1", "sc_t2"], writes=["Sall"])
        if main:
            for ri in range(2):
                el("pool", lambda e, ri=ri: e.tensor_copy(out=Sbf[:, ri, :, :], in_=Sall[:, 0:NC8, ri, :].rearrange("p c q -> p q c")),
                   ["Sall"], ["Sbf"])
        el("pool", lambda e: e.tensor_copy(out=Sall[:, 0, :, :], in_=Sall[:, NC8, :, :]), ["Sall"], ["Sall"])
        def s5_out():
            proj_fm(BLK_ZB, 4, lambda ct, ps_, pn: el("act", lambda e: e.activation(out=zbT[:, ct, :], in_=ps_[:, 0:T], func=AF.Silu), [pn], ["zbT"]))
            Wi_, win_ = load_block(BLK_S5 + 2)
            Wr_, wrn_ = load_block(BLK_S5 + 3)
            Wm_, wmn_ = load_block(BLK_S5 + 4)
            Wi_g = Wi_[:].rearrange("p (g n) -> p g n", g=32); Wr_g = Wr_[:].rearrange("p (g n) -> p g n", g=32)
            Wm_g = Wm_[:].rearrange("p (g n) -> p g n", g=32)
            GP = 512 // NC8
            for g0 in range(0, 32, GP):
                ng = min(GP, 32 - g0)
                for gi in range(ng):
                    g = g0 + gi
                    o_ = pM[:, gi * NC8:(gi + 1) * NC8]
                    mm(o_, Wi_g[:, g, :], U2[:, g, :], True, False, [win_, "U2"], ["pM"], sig=False)
                    mm(o_, Wr_g[:, g, :], Sbf[:, 0, g // 2, :], False, False, [wrn_, "Sbf"], ["pM"], sig=False)
                    mm(o_, Wm_g[:, g, :], Sbf[:, 1, g // 2, :], False, True, [wmn_, "Sbf"], ["pM"], sig=(gi == ng - 1))
                el("act", lambda e, g0=g0, ng=ng: e.activation(out=Yg[:, g0:g0 + ng, :].rearrange("p g c -> p (g c)"), in_=pM[:, 0:ng * NC8],
                                                               func=AF.Gelu_apprx_tanh), ["pM"], ["Yg"])
            for g0 in range(0, 32, 8):
                for gi in range(8):
                    g = g0 + gi
                    P.op("pe", lambda e, g=g, gi=gi: e.transpose(out=pT[0:NC8, gi * 128:(gi + 1) * 128], in_=Yg[:, g, :], identity=ident_b[:]),
                         reads=["Yg", "ident_b"], writes=["pT"], sig=(gi == 7))
                el("act", lambda e, g0=g0: e.copy(out=Ytm[0:NC8, :, 16 * g0:16 * g0 + 128].rearrange("c j (g n) -> c g j n", g=8),
                                                  in_=pT[0:NC8, :].rearrange("c (g j n) -> c g j n", g=8, j=8)), ["pT"], ["Utm"])
            for ct in range(4):
                for j in range(8):
                    P.op("pe", lambda e, ct=ct, j=j: e.transpose(out=pT[:, j * NC8:(j + 1) * NC8], in_=Ytm[0:NC8, j, ct * 128:(ct + 1) * 128],
                                                                 identity=ident_b[0:NC8, 0:NC8]), reads=["Utm", "ident_b"], writes=["pT"], sig=(j == 7))
                el("act", lambda e, ct=ct: e.copy(out=yT[:, ct, :], in_=pT[:, 0:T]), ["pT"], ["yT"])
            for ot_ in range(4):
                ps_, pn = acc_bank()
                for ct in range(4):
                    mm(ps_[:, 0:T], Wglu[:, ct, ot_ * 128:(ot_ + 1) * 128], yT[:, ct, :], ct == 0, ct == 3, ["Wglu", "yT"], [pn])
                el("act", lambda e, ps_=ps_, ot_=ot_: e.activation(out=sgl[:], in_=ps_[:, 0:T], func=AF.Sigmoid, bias=bglu[:, ot_:ot_ + 1]),
                   [pn, "bglu"], ["sgl"])
                tt(xg[:], sgl[:], yT[:, ot_, :], ALU.mult, ["sgl", "yT"], ["xg"])
                tt(hbT[:, ot_, :].rearrange("p (j c) -> p j c", j=8), xg[:].rearrange("p (j c) -> p j c", j=8),
                   zbT[:, ot_, :].rearrange("p (c j) -> p j c", j=8), ALU.mult, ["xg", "zbT"], ["hbT"])
        chk(7)
        for mt in range(8):
            ps_, pn = acc_bank()
            for k in range(4):
                mm(ps_[:, 0:T], cdiag[:, mt, k, :], uaT[:, mt, k + 1:k + 1 + T], k == 0, k == 3, ["cdiag", "uaT"], [pn])
            el("act", lambda e, ps_=ps_, mt=mt: e.activation(out=cT[:, mt, :], in_=ps_[:, 0:T], func=AF.Silu, bias=cb[:, mt:mt + 1]),
               [pn, "cb"], ["cT"])
        if main:
            for half in range(2):
                proj_fm(BLK_ZA + half, 4, lambda ct, ps_, pn, half=half: el(
                    "act", lambda e: e.activation(out=zaT[:, half * 4 + ct, :], in_=ps_[:, 0:T], func=AF.Silu), [pn], ["zaT"]))
            for half in range(2):
                proj_fm(BLK_OA + half, 4, lambda ct, ps_, pn, half=half: el(
                    "act", lambda e: e.activation(out=oaT[:, half * 4 + ct, :], in_=ps_[:, 0:T], func=AF.Sigmoid), [pn], ["oaT"]))
        if main:
            for mt in range(8):
                el("dve", lambda e, mt=mt: e.scalar_tensor_tensor(out=csz[:, mt, :], in0=cT[:, mt, :], scalar=skp[:, mt:mt + 1], in1=zaT[:, mt, :],
                                                                  op0=ALU.mult, op1=ALU.mult), ["cT", "skp", "zaT"], ["csz"])
            for mt in range(8):
                ts(zaT[:, mt, :], zaT[:, mt, :], hg[:, mt:mt + 1], None, ALU.mult, None, ["zaT", "hg", "csz"], ["zaT"])
        chk(8)
        for h in range(4):
            if main:
                for (Wx, wxn, dst, dn) in ((Wq, "Wqkv0", qT, "qT"), (Wk, "Wqkv1", kT, "kT")):
                    for et in range(2):
                        ps_, pn = acc_bank()
                        for d in range(2):
                            mm(ps_[:, 0:T], Wx[:, h, d, et * 128:(et + 1) * 128], cT[:, 2 * h + d, :], d == 0, d == 1, [wxn, "cT"], [pn])
                        el("act" if et else "dve",
                           (lambda e, ps_=ps_, dst=dst, h=h, et=et: e.copy(out=dst[:, h, et, :], in_=ps_[:, 0:T])) if et else
                           (lambda e, ps_=ps_, dst=dst, h=h, et=et: e.tensor_copy(out=dst[:, h, et, :], in_=ps_[:, 0:T])), [pn], [dn])
        el("act", lambda e: e.activation(out=e1a[:], in_=gates[:, :, 4:8], func=AF.Exp, scale=-1.0), ["gates"], ["e1a"])
        el("act", lambda e: e.activation(out=lfa[:], in_=e1a[:], func=AF.Ln, bias=1.0), ["e1a"], ["lfa"])
        ts(lfa[:], lfa[:], -1.0, None, ALU.mult, None, ["lfa"], ["lfa"])
        for ch in range(NCH):
            tsl = slice(ch * 128, (ch + 1) * 128)
            for h in range(4):
                el("dve", lambda e, h=h, ch=ch: e.tensor_copy(out=lfrep[:, h, :], in_=lfa[:, ch, h:h + 1].to_broadcast([128, 128])), ["lfa"], ["lfrep"])
            for h in range(4):
                mm(pGD[:, h * 128:(h + 1) * 128], lfrep[:, h, :], maskT[:], True, True, ["lfrep", "maskT"], ["pGD"], sig=(h == 3))
            for h in range(4):
                mm(pM[:, h * 128:(h + 1) * 128], maskT[:], lfrep[:, h, :], True, True, ["maskT", "lfrep"], ["pM"], sig=(h == 3))
            en, dn, wn_, wbn, wrn = "emb%d" % ch, "dec%d" % ch, "wcol%d" % ch, "wcolb%d" % ch, "wrep%d" % ch
            el("act", lambda e, ch=ch: e.activation(out=emb4[:, ch, :, :].rearrange("p h t -> p (h t)"), in_=pGD[:], func=AF.Exp, scale=-1.0), ["pGD"], [en])
            el("dve", lambda e, ch=ch: e.reciprocal(out=dec4[:, ch, :], in_=emb4[:, ch, :, 127]), [en], [dn])
            tt(bcol[:], gates[:, ch, 0:4], pM[:].rearrange("p (h t) -> p h t", h=4)[:, :, 0], ALU.subtract, ["gates", "pM"], ["bcol"])
            el("act", lambda e, ch=ch: e.activation(out=wcol4[:, ch, :], in_=bcol[:], func=AF.Exp), ["bcol"], [wn_])
            el("dve", lambda e, ch=ch: e.tensor_copy(out=vw[:, ch, :, 256], in_=wcol4[:, ch, :]), [wn_], ["vw"])
            if main:
                for h in range(4):
                    el("dve", lambda e, h=h, ch=ch: e.tensor_copy(out=wrep4[:, ch, h, :], in_=wcol4[:, ch, h:h + 1].to_broadcast([128, 128])), [wn_], [wrn])
            for h in range(4):
                ps_, pn = acc_bank()
                for d in range(2):
                    mm(ps_[:, 0:256], cT[:, 2 * h + d, tsl], Wk[:, h, d, :], d == 0, d == 1, ["cT", "Wqkv1"], [pn], sig=False)
                for d in range(2):
                    mm(ps_[:, 256:512], uaT[:, 2 * h + d, 4 + ch * 128:4 + (ch + 1) * 128], Wv[:, h, d, :], d == 0, d == 1, ["uaT", "Wqkv2"], [pn])
                el("act", lambda e, ps_=ps_, ch=ch, h=h: e.copy(out=ktm[:, ch, h, :], in_=ps_[:, 0:256]), [pn], ["ktm"])
                el("act", lambda e, ps_=ps_, ch=ch, h=h: e.activation(out=vw[:, ch, h, 0:256], in_=ps_[:, 256:512], func=AF.Copy, scale=wcol4[:, ch, h:h + 1]),
                   [pn, wn_], ["vw"])
        for ch in range(NCH):
            tsl = slice(ch * 128, (ch + 1) * 128)
            en, dn, wn_, wbn, wrn = "emb%d" % ch, "dec%d" % ch, "wcol%d" % ch, "wcolb%d" % ch, "wrep%d" % ch
            if main:
                for h in range(4):
                    for et in range(2):
                        mm(pS[:, h * 128:(h + 1) * 128], kT[:, h, et, tsl], qT[:, h, et, tsl], et == 0, et == 1, ["kT", "qT"], ["pS"], sig=(h == 3 and et == 1))
                tt(SmT[:].rearrange("p h t -> p (h t)"), pS[:], mask4[:].rearrange("p h t -> p (h t)"), ALU.mult, ["pS", "mask4"], ["SmT"])
                for h in range(4):
                    for d2 in range(2):
                        pn_, pnn = (pN0, "pN0") if h < 2 else (pN1, "pN1")
                        o_ = pn_[:, ((h % 2) * 2 + d2) * 128:((h % 2) * 2 + d2 + 1) * 128]
                        mm(o_, vw[:, ch, h, d2 * 128:(d2 + 1) * 128], SmT[:, h, :], True, False, ["vw", "SmT"], [pnn], sig=False)
                        mm(o_, Cbf[:, h, 0, d2 * 128:(d2 + 1) * 128], qT[:, h, 0, tsl], False, False, ["Cbf%d" % h, "qT"], [pnn], sig=False)
                        mm(o_, Cbf[:, h, 1, d2 * 128:(d2 + 1) * 128], qT[:, h, 1, tsl], False, True, ["Cbf%d" % h, "qT"], [pnn], sig=(h % 2 == 1 and d2 == 1))
                    o_ = pGD[:, h * 128:(h + 1) * 128]
                    mm(o_, wrep4[:, ch, h, :], SmT[:, h, :], True, False, [wrn, "SmT"], ["pGD"], sig=False)
                    mm(o_, nrep[:, h, 0, :], qT[:, h, 0, tsl], False, False, ["nrep%d" % h, "qT"], ["pGD"], sig=False)
                    mm(o_, nrep[:, h, 1, :], qT[:, h, 1, tsl], False, True, ["nrep%d" % h, "qT"], ["pGD"], sig=(h == 3))
            if main:
                el("act", lambda e: e.activation(out=aden[:].rearrange("p h t -> p (h t)"), in_=pGD[:], func=AF.Abs), ["pGD"], ["aden"])
                tt(aden[:], aden[:], emb4[:, ch, :, :], ALU.max, ["aden", en], ["aden"])
                el("act", lambda e: e.activation(out=aden[:], in_=aden[:], func=AF.Ln), ["aden"], ["aden"])
                el("act", lambda e: e.activation(out=rden[:], in_=aden[:], func=AF.Exp, scale=-1.0), ["aden"], ["rden"])
            for h in range(4):
                cn, bn, nn = "Cst%d" % h, "Cbf%d" % h, "nrep%d" % h
                for et in range(2):
                    ps_, pn = acc_bank()
                    mm(ps_[:, 0:258], ktm[:, ch, h, et * 128:(et + 1) * 128], vw[:, ch, h, 0:258], True, True, ["ktm", "vw"], [pn])
                    tt(Cst[:, h, et, 0:258], Cst[:, h, et, 0:258], ps_[:, 0:258], ALU.add, [cn, pn, bn, nn], [cn])
            for h in range(4):
                cn, bn, nn = "Cst%d" % h, "Cbf%d" % h, "nrep%d" % h
                el("act", lambda e, h=h, ch=ch: e.activation(out=Cst[:, h, :, :], in_=Cst[:, h, :, :], func=AF.Copy, scale=dec4[:, ch, h:h + 1]), [cn, dn], [cn])
                el("act", lambda e, h=h: e.copy(out=Cbf[:, h, :, :], in_=Cst[:, h, :, :]), [cn], [bn])
            for h in range(4):
                cn, bn, nn = "Cst%d" % h, "Cbf%d" % h, "nrep%d" % h
                for et in range(2):
                    el("dve", lambda e, h=h, et=et: e.tensor_copy(out=nrep[:, h, et, :], in_=Cst[:, h, et, 256:257].to_broadcast([128, 128])),
                       [cn], [nn])
            if main:
                for hp, (pn_, pnn) in enumerate(((pN0, "pN0"), (pN1, "pN1"))):
                    tt(hT[:, 4 * hp:4 * hp + 4, :].rearrange("p (h d) t -> p h d t", h=2), pn_[:].rearrange("p (h d t) -> p h d t", h=2, d=2),
                       rden[:, 2 * hp:2 * hp + 2, :].unsqueeze(2).to_broadcast([128, 2, 2, 128]), ALU.mult, [pnn, "rden"], ["hT"])
                tt(hT[:], hT[:], oaT[:, :, tsl], ALU.mult, ["hT", "oaT"], ["hT"])
                el("act", lambda e: e.activation(out=sq[:], in_=hT[:], func=AF.Square), ["hT"], ["sq"])
                for h in range(4):
                    for d2 in range(2):
                        mm(pS[:, h * 128:(h + 1) * 128], ones_b[:], sq[:, 2 * h + d2, :], d2 == 0, d2 == 1, ["ones_b", "sq", "SmT"], ["pS"], sig=(h == 3 and d2 == 1))
                ts(rsh[:].rearrange("p h t -> p (h t)"), pS[:], 1.0 / 256.0, 1e-6, ALU.mult, ALU.add, ["pS"], ["rsh"])
                el("act", lambda e: e.activation(out=rsh[:], in_=rsh[:], func=AF.Ln), ["rsh"], ["rsh"])
                el("act", lambda e: e.activation(out=rsh[:], in_=rsh[:], func=AF.Exp, scale=-0.5), ["rsh"], ["rsh"])
                tt(hn[:].rearrange("p (h d) t -> p h d t", h=4), hT[:].rearrange("p (h d) t -> p h d t", h=4),
                   rsh[:].unsqueeze(2).to_broadcast([128, 4, 2, 128]), ALU.mult, ["hT", "rsh"], ["hn"])
                tt(hn[:], hn[:], zaT[:, :, tsl], ALU.mult, ["hn", "zaT"], ["hn"])
                tt(hfT[:, :, tsl], hn[:], csz[:, :, tsl], ALU.add, ["hn", "csz"], ["hfT"])
        chk(9)
        if not main:
            return
        s5_out()
        def gate_blocks(br):
            for bi_ in range(2):
                Wg_, wgn = load_block(BLK_G + 2 * br + bi_)
                Wgv = Wg_[:].rearrange("p (k n) -> p k n", k=8)
                for c4 in range(4):
                    ft = bi_ * 4 + c4
                    psg, png = acc_bank()
                    for kt in range(8):
                        mm(psg[:, 0:T], Wgv[:, kt, c4 * 128:(c4 + 1) * 128], xnT[:, kt, :], kt == 0, kt == 7, [wgn, "xnT"], [png])
                    el("act", lambda e, psg=psg, ft=ft: e.activation(out=gAll[:, ft, :], in_=psg[:, 0:T], func=AF.Sigmoid), [png], ["oaT"])

        gate_blocks(0)
        for hf_ in range(2):
            Wa, wan = load_block(BLK_AO + hf_)
            Wav = Wa[:].rearrange("p (k n) -> p k n", k=8)
            for c4 in range(4):
                ft = hf_ * 4 + c4
                psa, pna = acc_bank()
                for kt in range(8):
                    mm(psa[:, 0:T], Wav[:, kt, c4 * 128:(c4 + 1) * 128], hfT[:, kt, :], kt == 0, kt == 7, [wan, "hfT"], [pna])
                tt(mrgT[:, ft, :], psa[:, 0:T], gAll[:, ft, :], ALU.mult, [pna, "oaT"], ["zaT"])
        gate_blocks(1)
        Wbo, wbon = load_block(BLK_BO)
        Wbv = Wbo[:].rearrange("p (k n) -> p k n", k=4)
        for ft in range(8):
            psb, pnb = acc_bank()
            for kt in range(4):
                mm(psb[:, 0:T], Wbv[:, kt, ft * 128:(ft + 1) * 128], hbT[:, kt, :], kt == 0, kt == 3, [wbon, "hbT"], [pnb])
            el("act", lambda e, psb=psb: e.copy(out=m2[:].rearrange("p (c j) -> p j c", j=8), in_=psb[:, 0:T].rearrange("p (j c) -> p j c", j=8)),
               [pnb], ["m2"])
            tt(m2[:], m2[:], gAll[:, ft, :], ALU.mult, ["m2", "oaT"], ["m2"])
            tt(mrgT[:, ft, :], m2[:], mrgT[:, ft, :], ALU.add, ["m2", "zaT"], ["zaT"])
        Wo0, wo0n = load_block(BLK_WO); Wo1, wo1n = load_block(BLK_WO + 1)
        for ch in range(NCH):
            tsl = slice(ch * 128, (ch + 1) * 128)
            for hf_, (Wo_, won, pn_, pnn) in enumerate(((Wo0, wo0n, pN0, "pN0"), (Wo1, wo1n, pN1, "pN1"))):
                Wov = Wo_[:].rearrange("p (k n) -> p k n", k=8)
                for kt in range(8):
                    mm(pn_[:, :], mrgT[:, kt, tsl], Wov[:, kt, :], kt == 0, kt == 7, ["zaT", won], [pnn])
            el("act", lambda e: e.activation(out=junk[:, 0:512], in_=pN0[:], func=AF.Square, accum_out=ss2[:]), ["pN0"], ["junk", "ss2"])
            el("act", lambda e: e.activation(out=junk[:, 512:1024], in_=pN1[:], func=AF.Square, accum_out=rstd2[:]), ["pN1"], ["junk", "rstd2"])
            tt(ss2[:], ss2[:], rstd2[:], ALU.add, ["ss2", "rstd2"], ["ss2"])
            rsqrt_col(rstd2[:], ss2[:], 1.0 / 1024.0, "ss2", "rstd2")
            ob, obn = ot[ch % 2], "ot%d" % (ch % 2)
            xb_, xnm = xr[ch % 2], "xr%d" % (ch % 2)
            P.dma(lambda e, xb_=xb_, ch=ch: e.dma_start(out=xb_[:], in_=xsrc[row0 + ch * 128: row0 + (ch + 1) * 128, :]), writes=[xnm])
            for hf_, (pn_, pnn) in enumerate(((pN0, "pN0"), (pN1, "pN1"))):
                cs = slice(hf_ * 512, hf_ * 512 + 512)
                el("dve", lambda e, ob=ob, pn_=pn_, cs=cs: e.scalar_tensor_tensor(out=ob[:, cs], in0=pn_[:], scalar=rstd2[:, 0:1], in1=gpost[:, cs],
                                                                                 op0=ALU.mult, op1=ALU.mult), [pnn, "rstd2", "gpost"], [obn])
            tt(ob[:], ob[:], xb_[:], ALU.add, [obn, xnm], [obn])
            tok = P.dma(lambda e, ob=ob, ch=ch: e.dma_start(out=out[row0 + ch * 128: row0 + (ch + 1) * 128, :], in_=ob[:]), reads=[obn])
            out_toks.append(tok)

    sc_t1 = sbt([128, 2, 16], F32, "sc_t1"); sc_t2 = sbt([128, 2, 16], F32, "sc_t2")
    try:
        for ti in range(NPRE):
            nxt = (x_pre, (ti + 1) * T) if ti + 1 < NPRE else (x_main, 0)
            do_tile(x_pre, ti * T, False, ti, nxt)
    except StopBuild:
        tk = P.dma(lambda e: e.dma_start(out=out[0:128, :], in_=gpost[:]), reads=["gpost"])
        P.final_wait("sp", [tk])
        P.emit(); P.close()
        return nc
    if NPRE > 0:
        for h in range(4):
            cn, bn, nn = "Cst%d" % h, "Cbf%d" % h, "nrep%d" % h
            ts(Cst[:, h, :, :], Cst[:, h, :, :], flg[:, 0:1], None, ALU.mult, None, [cn, "flg"], [cn])
            el("act", lambda e, h=h: e.copy(out=Cbf[:, h, :, :], in_=Cst[:, h, :, :]), [cn], [bn])
            for et in range(2):
                el("dve", lambda e, h=h, et=et: e.tensor_copy(out=nrep[:, h, et, :], in_=Cst[:, h, et, 256:257].to_broadcast([128, 128])),
                   [cn], [nn])
    for ti in range(NMAIN):
        nxt = (x_main, (ti + 1) * T) if ti + 1 < NMAIN else None
        do_tile(x_main, ti * T, True, NPRE + ti, nxt)
    P.final_wait("sp", out_toks)
    P.emit()
    P.close()
    return nc


T_TILE = 256
_cache = {}


def kernel(**inputs):
    x = np.ascontiguousarray(inputs["x"], dtype=np.float32)
    Bsz, L, Dm = x.shape
    half = L // 2
    npre = half // T_TILE
    nmain = half // T_TILE
    key = (T_TILE, npre, nmain)
    if key not in _cache:
        _cache[key] = build_program(T_TILE, npre, nmain)
    nc = _cache[key]
    shared = {}
    for k, v in inputs.items():
        if k == "x":
            continue
        a = np.ascontiguousarray(np.asarray(v, dtype=np.float32)[0])
        if k in ("b_i", "b_f", "log_dt", "norm_post_g"):
            a = a.reshape(1, -1)
        shared[k] = a
    in_maps = []
    zeros = np.zeros((half, Dm), np.float32)
    for core in range(8):
        b, hf = core // 2, core % 2
        m = dict(shared)
        m["x_main"] = np.ascontiguousarray(x[b, hf * half:(hf + 1) * half])
        m["x_pre"] = zeros if hf == 0 else np.ascontiguousarray(x[b, 0:half])
        m["flag"] = np.full((128, 1), float(hf), np.float32)
        in_maps.append(m)
    res = run_bass_kernel_spmd(nc, in_maps, core_ids=list(range(8)))
    outp = np.empty((Bsz, L, Dm), np.float32)
    for core in range(8):
        b, hf = core // 2, core % 2
        outp[b, hf * half:(hf + 1) * half] = res.results[core]["out"]
    return outp
```

```python
import contextlib
import numpy as np
import concourse.bass as bass
import concourse.mybir as mybir
from concourse.bass_utils import run_bass_kernel_spmd

F32 = mybir.dt.float32
BF16 = mybir.dt.bfloat16
AF = mybir.ActivationFunctionType
ALU = mybir.AluOpType
COMPUTE = ("pe", "act", "dve", "pool")
NDMA_SLOTS = 8
DEBUG_TAGS = False
PI = float(np.pi)


class Prog:
    def __init__(self, nc):
        self.nc = nc
        self.stack = contextlib.ExitStack()
        self.engs = ("pe", "act", "dve", "pool", "sp")
        self.ops = {e: [] for e in self.engs}
        self.waited = {e: {} for e in self.engs}
        self.res = {}
        self.dma_use = {}
        self.dma_rr = {e: 0 for e in self.engs}
        self.sems = {}
        self.base = {e: 0 for e in COMPUTE}
        self.temp = None

    def sb(self, name, shape, dt):
        st = self.temp if self.temp is not None else self.stack
        return st.enter_context(self.nc.sbuf_tensor(name, list(shape), dt))

    def ps(self, name, shape, dt):
        return self.stack.enter_context(self.nc.psum_tensor(name, list(shape), dt))

    def _sem(self, key):
        if key not in self.sems:
            nm = "s_" + "_".join(str(k) for k in (key if isinstance(key, tuple) else (key,)))
            self.sems[key] = self.stack.enter_context(self.nc.semaphore(nm))
        return self.sems[key]

    def _deps(self, eng, reads, writes):
        deps = {}

        def add(tok):
            if tok is None:
                return
            k, v = tok
            if k == "pe" and eng == "pe":
                return
            if deps.get(k, -1) < v:
                deps[k] = v

        for r in reads:
            st = self.res.get(r)
            if st:
                for k, v in st[0].items():
                    add((k, v))
        for w in writes:
            st = self.res.get(w)
            if st:
                for k, v in st[0].items():
                    add((k, v))
                for k, v in st[1].items():
                    add((k, v))
        out = []
        wd = self.waited[eng]
        for k, v in deps.items():
            if wd.get(k, -1) >= v:
                continue
            wd[k] = v
            out.append((k, v))
        return out

    def _commit(self, tok, reads, writes):
        k, v = tok
        for r in reads:
            st = self.res.setdefault(r, [{}, {}])
            if st[1].get(k, -1) < v:
                st[1][k] = v
        for w in writes:
            old = self.res.get(w)
            wr = {}
            if old is not None and k not in COMPUTE:
                wr = {k2: v2 for k2, v2 in old[0].items() if k2 not in COMPUTE}
            wr[k] = v
            self.res[w] = [wr, {}]

    def _tag(self):
        if not DEBUG_TAGS:
            return None
        import sys as _sys
        f = _sys._getframe(2)
        while f is not None and f.f_code.co_name not in ("do_tile", "build_program", "s5_out", "gate_blocks", "proj_fm"):
            f = f.f_back
        return str(f.f_lineno) if f is not None else None

    def op(self, eng, fn, reads=(), writes=(), sig=True):
        waits = self._deps(eng, reads, writes)
        idx = len(self.ops[eng])
        self.ops[eng].append(dict(fn=fn, waits=waits, sig=sig, dma=None, tag=self._tag()))
        tok = (eng, idx)
        self._commit(tok, reads, writes)
        return tok

    def dma(self, fn, reads=(), writes=(), q="sp"):
        waits = self._deps(q, reads, writes)
        slot = self.dma_rr[q] % NDMA_SLOTS
        self.dma_rr[q] += 1
        key = ("d", q, slot)
        n = self.dma_use.get(key, 0)
        if n > 0:
            prev = n * 16
            if self.waited[q].get(key, -1) < prev:
                self.waited[q][key] = prev
                waits.append((key, prev))
        self.dma_use[key] = n + 1
        tok = (key, (n + 1) * 16)
        self.ops[q].append(dict(fn=fn, waits=waits, sig=False, dma=key))
        self._commit(tok, reads, writes)
        return tok

    def final_wait(self, eng, toks):
        self.ops[eng].append(dict(fn=None, waits=list(toks), sig=False, dma=None))

    def emit(self):
        nc = self.nc
        sigcount = {}
        totals = {}
        for e in COMPUTE:
            c = self.base[e]
            arr = []
            for o in self.ops[e]:
                if o["sig"]:
                    c += 1
                arr.append(c)
            need = [None] * len(arr)
            nxt = None
            for i in range(len(arr) - 1, -1, -1):
                if self.ops[e][i]["sig"]:
                    nxt = arr[i]
                need[i] = nxt
            sigcount[e] = need
            totals[e] = c
            self._sem(e)
        for k in self.dma_use:
            self._sem(k)

        def resolve(k, v):
            if k in COMPUTE:
                val = sigcount[k][v]
                assert val is not None, (k, v)
                return self.sems[k], val
            return self.sems[k], v

        with nc.Block() as block:

            def run(eng_name, eng):
                for o in self.ops[eng_name]:
                    for k, v in o["waits"]:
                        s, val = resolve(k, v)
                        eng.wait_ge(s, val)
                    if o["fn"] is None:
                        continue
                    ins = o["fn"](eng)
                    if o.get("tag"):
                        ins.annotate(o["tag"])
                    if o["dma"] is not None:
                        ins.then_inc(self.sems[o["dma"]], 16)
                    elif o["sig"]:
                        ins.then_inc(self.sems[eng_name], 1)
                for o2 in COMPUTE:
                    if o2 != eng_name and totals[o2] > 0:
                        eng.wait_ge(self.sems[o2], totals[o2])
                for k, n in self.dma_use.items():
                    eng.wait_ge(self.sems[k], n * 16)

            @block.tensor
            def _(e):
                run("pe", e)

            @block.scalar
            def _(e):
                run("act", e)

            @block.vector
            def _(e):
                run("dve", e)

            @block.gpsimd
            def _(e):
                run("pool", e)

            @block.sync
            def _(e):
                run("sp", e)

        self.base = totals
        self.ops = {e: [] for e in self.engs}
        self.waited = {e: {} for e in self.engs}
        self.res = {}

    def close(self):
        self.stack.close()


NBLK = 22
BLK_UA, BLK_UB, BLK_ZB, BLK_ZA, BLK_OA, BLK_G, BLK_AO, BLK_BO, BLK_WO, BLK_S5 = 0, 2, 3, 4, 6, 8, 12, 14, 15, 17
COL_UA, COL_ZA, COL_OA, COL_I, COL_UB, COL_ZB, COL_G = 0, 1024, 2048, 3072, 3080, 3592, 4104


def build_program(T, NPRE, NMAIN, dbg_stop=0):
    NCH = T // 128
    NC8 = T // 8
    nc = bass.Bass("TRN2", target_bir_lowering=False)
    dram = {}

    def din(name, shape):
        dram[name] = nc.dram_tensor(name, list(shape), F32, kind="ExternalInput").ap()
        return dram[name]

    x_pre = din("x_pre", [max(NPRE, 1) * T, 1024])
    x_main = din("x_main", [NMAIN * T, 1024])
    flag = din("flag", [128, 1])
    norm_pre_g = din("norm_pre_g", [1024]); w_in = din("w_in", [1024, 6152])
    conv_w = din("conv_w", [4, 1024]); conv_b = din("conv_b", [1024])
    w_q = din("w_q", [4, 256, 256]); w_k = din("w_k", [4, 256, 256]); w_v = din("w_v", [4, 256, 256])
    b_i = din("b_i", [1, 4]); b_f = din("b_f", [1, 4]); head_g = din("head_g", [1024]); skip_a = din("skip_a", [1024])
    w_a_out = din("w_a_out", [1024, 1024])
    lam_re = din("lam_re", [32, 64]); lam_im = din("lam_im", [32, 64]); log_dt = din("log_dt", [1, 32])
    B_re = din("B_re", [32, 64, 16]); B_im = din("B_im", [32, 64, 16])
    C_re = din("C_re", [32, 16, 64]); C_im = din("C_im", [32, 16, 64]); D_skip = din("D_skip", [32, 16])
    w_glu = din("w_glu", [512, 512]); b_glu = din("b_glu", [512]); w_b_out = din("w_b_out", [512, 1024])
    w_o = din("w_o", [1024, 1024]); norm_post_g = din("norm_post_g", [1, 1024])
    out = nc.dram_tensor("out", [NMAIN * T, 1024], F32, kind="ExternalOutput").ap()
    WS = nc.dram_tensor("wscratch", [NBLK, 128, 4096], BF16, kind="Internal").ap()

    P = Prog(nc)
    uid = [0]

    def sbt(shape, dt, name=None):
        uid[0] += 1
        return P.sb(name or ("t%d" % uid[0]), shape, dt)

    ident_f = sbt([128, 128], F32); ident_b = sbt([128, 128], BF16)
    maskT = sbt([128, 128], F32); mask4 = sbt([128, 4, 128], F32); ones_b = sbt([128, 128], BF16)
    P.op("pool", lambda e: e.memset(ident_f[:], 1.0), writes=["ident_f"])
    P.op("pool", lambda e: e.affine_select(out=ident_f[:], in_=ident_f[:], pattern=[[-1, 128]], compare_op=ALU.is_equal,
                                           fill=0.0, base=0, channel_multiplier=1), reads=["ident_f"], writes=["ident_f"])
    P.op("dve", lambda e: e.tensor_copy(out=ident_b[:], in_=ident_f[:]), reads=["ident_f"], writes=["ident_b"])
    P.op("pool", lambda e: e.memset(maskT[:], 1.0), writes=["maskT"])
    P.op("pool", lambda e: e.affine_select(out=maskT[:], in_=maskT[:], pattern=[[1, 128]], compare_op=ALU.is_ge,
                                           fill=0.0, base=0, channel_multiplier=-1), reads=["maskT"], writes=["maskT"])
    for h in range(4):
        P.op("pool", lambda e, h=h: e.tensor_copy(out=mask4[:, h, :], in_=maskT[:]), reads=["maskT"], writes=["mask4"])
    P.op("pool", lambda e: e.memset(ones_b[:], 1.0), writes=["ones_b"])

    gpre = sbt([128, 8], F32); cb = sbt([128, 8], F32); hg = sbt([128, 8], F32); skp = sbt([128, 8], F32)
    cw = sbt([128, 8, 4], F32); bglu = sbt([128, 4], F32); gpost = sbt([128, 1024], F32); bif = sbt([128, 8], F32)
    flg = sbt([128, 1], F32)
    nonc = dict(allow_slow_non_contiguous=True)
    P.dma(lambda e: e.dma_start(out=gpre[:], in_=norm_pre_g.rearrange("(k p) -> p k", p=128), **nonc), writes=["gpre"])
    P.dma(lambda e: e.dma_start(out=cb[:], in_=conv_b.rearrange("(k p) -> p k", p=128), **nonc), writes=["cb"])
    P.dma(lambda e: e.dma_start(out=hg[:], in_=head_g.rearrange("(k p) -> p k", p=128), **nonc), writes=["hg"])
    P.dma(lambda e: e.dma_start(out=skp[:], in_=skip_a.rearrange("(k p) -> p k", p=128), **nonc), writes=["skp"])
    for k in range(4):
        P.dma(lambda e, k=k: e.dma_start(out=cw[:, :, k], in_=conv_w[k].rearrange("(m p) -> p m", p=128), **nonc), writes=["cw"])
    P.dma(lambda e: e.dma_start(out=bglu[:], in_=b_glu.rearrange("(k p) -> p k", p=128), **nonc), writes=["bglu"])
    P.dma(lambda e: e.dma_start(out=gpost[:], in_=norm_post_g.partition_broadcast(128)), writes=["gpost"])
    P.dma(lambda e: e.dma_start(out=bif[:, 0:4], in_=b_i.partition_broadcast(128)), writes=["bif"])
    P.dma(lambda e: e.dma_start(out=bif[:, 4:8], in_=b_f.partition_broadcast(128)), writes=["bif"])
    P.dma(lambda e: e.dma_start(out=flg[:], in_=flag), writes=["flg"])

    Wqkv = [sbt([128, 4, 2, 256], BF16) for _ in range(3)]
    Wif = sbt([128, 8, 8], BF16); Wglu = sbt([128, 4, 512], BF16); cdiag = sbt([128, 8, 4, 128], BF16)
    AR32 = sbt([128, 2, 16], F32); ANI = sbt([128, 16], F32); API = sbt([128, 16], F32)
    pA = P.ps("pA", [128, 512], F32); pB = P.ps("pB", [128, 512], F32)
    pT = P.ps("pT", [128, 1024], BF16); pGD = P.ps("pGD", [128, 512], F32)
    pS = P.ps("pS", [128, 512], F32); pN0 = P.ps("pN0", [128, 512], F32); pN1 = P.ps("pN1", [128, 512], F32)
    pM = P.ps("pM", [128, 512], F32)
    P.temp = contextlib.ExitStack()
    stg = sbt([128, 4096], F32, "stg")
    stgb = sbt([128, 4096], BF16, "stgb")
    stgb2 = sbt([128, 4096], BF16, "stgb2")
    for wi, (wsrc, scl) in enumerate(((w_q, 1.0), (w_k, 1.0 / 16.0), (w_v, 1.0))):
        P.dma(lambda e, wsrc=wsrc: e.dma_start(out=stg[:, 0:2048].rearrange("p (h d n) -> p h d n", h=4, d=2),
                                               in_=wsrc.rearrange("h (d p) n -> p h d n", p=128)), writes=["stg"])
        P.op("dve", lambda e, wi=wi, scl=scl: e.tensor_scalar(out=Wqkv[wi][:].rearrange("p h d n -> p (h d n)"), in0=stg[:, 0:2048],
                                                              scalar1=scl, scalar2=None, op0=ALU.mult), reads=["stg"], writes=["Wqkv%d" % wi])
    Wq, Wk, Wv = Wqkv
    P.dma(lambda e: e.dma_start(out=stg[:, 0:64].rearrange("p (k n) -> p k n", k=8),
                                in_=w_in[:, COL_I:COL_I + 8].rearrange("(k p) n -> p k n", p=128), **nonc), writes=["stg"])
    for kt in range(8):
        P.op("dve", lambda e, kt=kt: e.tensor_scalar(out=Wif[:, kt, :], in0=stg[:, kt * 8:(kt + 1) * 8], scalar1=gpre[:, kt:kt + 1],
                                                     scalar2=None, op0=ALU.mult), reads=["stg", "gpre"], writes=["Wif"])
    P.dma(lambda e: e.dma_start(out=stg[:, 0:2048].rearrange("p (k n) -> p k n", k=4),
                                in_=w_glu.rearrange("(k p) n -> p k n", p=128)), writes=["stg"])
    P.op("dve", lambda e: e.tensor_copy(out=Wglu[:].rearrange("p k n -> p (k n)"), in_=stg[:, 0:2048]), reads=["stg"], writes=["Wglu"])
    for mt in range(8):
        for k in range(4):
            P.op("dve", lambda e, mt=mt, k=k: e.tensor_scalar(out=cdiag[:, mt, k, :], in0=ident_f[:], scalar1=cw[:, mt, k:k + 1],
                                                               scalar2=None, op0=ALU.mult), reads=["ident_f", "cw"], writes=["cdiag"])

    def stage_block(blk, src_ap_f, scale_gpre, nk):
        ncol = 4096 // nk
        P.dma(lambda e: e.dma_start(out=stg[:].rearrange("p (k n) -> p k n", k=nk), in_=src_ap_f), writes=["stg"])
        if scale_gpre:
            for kt in range(nk):
                P.op("act",
                     lambda e, kt=kt: e.activation(out=stgb[:, kt * ncol:(kt + 1) * ncol], in_=stg[:, kt * ncol:(kt + 1) * ncol],
                                                   func=AF.Copy, scale=gpre[:, kt:kt + 1]),
                     reads=["stg", "gpre"], writes=["stgb"])
        else:
            P.op("act", lambda e: e.copy(out=stgb[:, 0:2048], in_=stg[:, 0:2048]), reads=["stg"], writes=["stgb"])
            P.op("act", lambda e: e.copy(out=stgb[:, 2048:4096], in_=stg[:, 2048:4096]), reads=["stg"], writes=["stgb"])
        P.dma(lambda e: e.dma_start(out=WS[blk], in_=stgb[:]), reads=["stgb"], writes=["WS%d" % blk])

    def win_cols(c0):
        return w_in[:, c0:c0 + 512].rearrange("(k p) n -> p k n", p=128)

    win_blocks = [(BLK_UA, COL_UA), (BLK_UA + 1, COL_UA + 512), (BLK_UB, COL_UB), (BLK_ZB, COL_ZB), (BLK_ZA, COL_ZA),
                  (BLK_ZA + 1, COL_ZA + 512), (BLK_OA, COL_OA), (BLK_OA + 1, COL_OA + 512)] + [(BLK_G + i, COL_G + 512 * i) for i in range(4)]
    pending = []
    for blk, c0 in win_blocks:
        pending.append((blk, win_cols(c0), True, 8))
    for i in range(2):
        pending.append((BLK_AO + i, w_a_out[:, 512 * i:512 * i + 512].rearrange("(k p) n -> p k n", p=128), False, 8))
        pending.append((BLK_WO + i, w_o[:, 512 * i:512 * i + 512].rearrange("(k p) n -> p k n", p=128), False, 8))
    pending.append((BLK_BO, w_b_out.rearrange("(k p) n -> p k n", p=128), False, 4))

    def stage_some(n=1):
        for _ in range(n):
            if pending:
                stage_block(*pending.pop(0))


    def small(n, name=None):
        return sbt([128, n], F32, name)

    cnt = [0]

    def el(eng, fn, r, w):
        P.op(eng, fn, reads=r, writes=w)

    def tt(outp, a, b, op, r, w, eng="dve"):
        el(eng, lambda e: e.tensor_tensor(out=outp, in0=a, in1=b, op=op), r, w)

    def ts(outp, a, s1, s2, op0, op1, r, w, eng="dve"):
        if op1 is None:
            el(eng, lambda e: e.tensor_scalar(out=outp, in0=a, scalar1=s1, scalar2=None, op0=op0), r, w)
        else:
            el(eng, lambda e: e.tensor_scalar(out=outp, in0=a, scalar1=s1, scalar2=s2, op0=op0, op1=op1), r, w)

    LR = small(32); LI = small(32); DT = small(32)
    for hf in range(2):
        sl = slice(64 * hf, 64 * hf + 64)
        P.dma(lambda e, sl=sl: e.dma_start(out=LR[sl, :], in_=lam_re.rearrange("g p -> p g"), **nonc), writes=["LR"])
        P.dma(lambda e, sl=sl: e.dma_start(out=LI[sl, :], in_=lam_im.rearrange("g p -> p g"), **nonc), writes=["LI"])
    P.dma(lambda e: e.dma_start(out=DT[:], in_=log_dt.partition_broadcast(128)), writes=["DT"])
    el("act", lambda e: e.activation(out=DT[:], in_=DT[:], func=AF.Exp), ["DT"], ["DT"])
    TH = small(32); MAG = small(32); t0 = small(32); t1 = small(32); t2 = small(32); kk = small(32)
    tt(TH[:], LI[:], DT[:], ALU.mult, ["LI", "DT"], ["TH"])
    tt(t0[:], LR[:], DT[:], ALU.mult, ["LR", "DT"], ["t0"])
    el("act", lambda e: e.activation(out=MAG[:], in_=t0[:], func=AF.Exp), ["t0"], ["MAG"])
    IMAG2 = small(32)
    el("act", lambda e: e.activation(out=IMAG2[:], in_=t0[:], func=AF.Exp, scale=-2.0), ["t0"], ["IMAG2"])

    def sin_of(dst, src, shift, nm):
        ts(t1[:], src, shift, None, ALU.add, None, [nm, "t1"], ["t1"])
        el("pool", lambda e: e.memset(kk[:], 0.0), [], ["kk"])
        for m in range(7):
            ts(t2[:], t1[:], (2 * m + 1) * PI, None, ALU.is_gt, None, ["t1"], ["t2"])
            tt(kk[:], kk[:], t2[:], ALU.add, ["kk", "t2"], ["kk"])
        ts(kk[:], kk[:], -2.0 * PI, None, ALU.mult, None, ["kk"], ["kk"])
        tt(t1[:], t1[:], kk[:], ALU.add, ["t1", "kk"], ["t1"])
        el("act", lambda e: e.activation(out=dst, in_=t1[:], func=AF.Sin), ["t1"], [nm + "_s"])

    SN = small(32); CS = small(32)
    sin_of(SN[:], TH[:], 0.0, "TH")
    sin_of(CS[:], TH[:], PI / 2.0, "TH")
    pwr = sbt([128, 9, 32], F32); pwi = sbt([128, 9, 32], F32); pnr = sbt([128, 8, 32], F32); pni = sbt([128, 8, 32], F32)
    el("pool", lambda e: e.memset(pwr[:, 0, :], 1.0), [], ["pw"]); el("pool", lambda e: e.memset(pwi[:, 0, :], 0.0), [], ["pw"])
    el("pool", lambda e: e.memset(pnr[:, 0, :], 1.0), [], ["pn"]); el("pool", lambda e: e.memset(pni[:, 0, :], 0.0), [], ["pn"])
    tt(pwr[:, 1, :], MAG[:], CS[:], ALU.mult, ["MAG", "TH_s"], ["pw"])
    tt(pwi[:, 1, :], MAG[:], SN[:], ALU.mult, ["MAG", "TH_s"], ["pw"])
    tt(pnr[:, 1, :], pwr[:, 1, :], IMAG2[:], ALU.mult, ["pw", "IMAG2"], ["pn"])
    tt(t0[:], pwi[:, 1, :], IMAG2[:], ALU.mult, ["pw", "IMAG2"], ["t0"])
    ts(pni[:, 1, :], t0[:], -1.0, None, ALU.mult, None, ["t0"], ["pn"])

    def cmul(or_, oi_, ar, ai, br, bi, r, w):
        raise NotImplementedError

    u0 = small(32); u1 = small(32)
    for k in range(1, 8):
        for (xr, xi, nm, lim) in ((pwr, pwi, "pw", 9), (pnr, pni, "pn", 8)):
            if k + 1 >= lim:
                continue
            tt(u0[:], xr[:, k, :], xr[:, 1, :], ALU.mult, [nm], ["u0"])
            tt(u1[:], xi[:, k, :], xi[:, 1, :], ALU.mult, [nm], ["u1"])
            tt(xr[:, k + 1, :], u0[:], u1[:], ALU.subtract, ["u0", "u1"], [nm])
            tt(u0[:], xr[:, k, :], xi[:, 1, :], ALU.mult, [nm], ["u0"])
            tt(u1[:], xi[:, k, :], xr[:, 1, :], ALU.mult, [nm], ["u1"])
            tt(xi[:, k + 1, :], u0[:], u1[:], ALU.add, ["u0", "u1"], [nm])
    den = small(32); qr = small(32); qi = small(32); nr = small(32)
    tt(u0[:], LR[:], LR[:], ALU.mult, ["LR"], ["u0"]); tt(u1[:], LI[:], LI[:], ALU.mult, ["LI"], ["u1"])
    tt(den[:], u0[:], u1[:], ALU.add, ["u0", "u1"], ["den"])
    el("dve", lambda e: e.reciprocal(out=den[:], in_=den[:]), ["den"], ["den"])
    ts(nr[:], pwr[:, 1, :], -1.0, None, ALU.add, None, ["pw"], ["nr"])
    tt(u0[:], nr[:], LR[:], ALU.mult, ["nr", "LR"], ["u0"]); tt(u1[:], pwi[:, 1, :], LI[:], ALU.mult, ["pw", "LI"], ["u1"])
    tt(qr[:], u0[:], u1[:], ALU.add, ["u0", "u1"], ["qr"]); tt(qr[:], qr[:], den[:], ALU.mult, ["qr", "den"], ["qr"])
    tt(u0[:], pwi[:, 1, :], LR[:], ALU.mult, ["pw", "LR"], ["u0"]); tt(u1[:], nr[:], LI[:], ALU.mult, ["nr", "LI"], ["u1"])
    tt(qi[:], u0[:], u1[:], ALU.subtract, ["u0", "u1"], ["qi"]); tt(qi[:], qi[:], den[:], ALU.mult, ["qi", "den"], ["qi"])
    Br = sbt([128, 32, 16], F32); Bi = sbt([128, 32, 16], F32); bbr = sbt([128, 32, 16], F32); bbi = sbt([128, 32, 16], F32)
    v0 = sbt([128, 32, 16], F32); v1 = sbt([128, 32, 16], F32)
    for hf in range(2):
        sl = slice(64 * hf, 64 * hf + 64)
        P.dma(lambda e, sl=sl: e.dma_start(out=Br[sl], in_=B_re.rearrange("g p n -> p g n"), **nonc), writes=["Br"])
        P.dma(lambda e, sl=sl: e.dma_start(out=Bi[sl], in_=B_im.rearrange("g p n -> p g n"), **nonc), writes=["Bi"])

    def bc(s):
        return s.unsqueeze(2).to_broadcast([128, 32, 16])

    def cmul3(orr, oii, sr, si, sn, xr, xi, xn, on):
        xn = [xn] if isinstance(xn, str) else list(xn)
        sn = [sn] if isinstance(sn, str) else list(sn)
        tt(v0[:], xr, bc(sr), ALU.mult, xn + sn, ["v0"]); tt(v1[:], xi, bc(si), ALU.mult, xn + sn, ["v1"])
        tt(orr, v0[:], v1[:], ALU.subtract, ["v0", "v1"], [on])
        tt(v0[:], xi, bc(sr), ALU.mult, xn + sn, ["v0"]); tt(v1[:], xr, bc(si), ALU.mult, xn + sn, ["v1"])
        tt(oii, v0[:], v1[:], ALU.add, ["v0", "v1"], [on])

    el("dve", lambda e: e.tensor_copy(out=u0[:], in_=qr[:]), ["qr"], ["qq"])
    cmul3(bbr[:], bbi[:], qr[:], qi[:], ["qr", "qi"], Br[:], Bi[:], ["Br", "Bi"], "bb")
    CTr = sbt([128, 32, 16], F32); CTi = sbt([128, 32, 16], F32)
    Cdup = sbt([128, 4, 2, 64], F32)
    for (Csrc, CTt, nm) in ((C_re, CTr, "CTr"), (C_im, CTi, "CTi")):
        for d in range(2):
            P.dma(lambda e, Csrc=Csrc, d=d: e.dma_start(out=Cdup[:, :, d, :], in_=Csrc.rearrange("(t g) n p -> (g n) t p", t=4)),
                  writes=["Cdup"])
        for t in range(4):
            P.op("pe", lambda e, t=t: e.transpose(out=pA[:, t * 128:(t + 1) * 128], in_=Cdup[:, t, :, :].rearrange("q d p -> q (d p)"),
                                                  identity=ident_f[:]), reads=["Cdup", "ident_f"], writes=["pA"])
        el("act", lambda e, CTt=CTt: e.copy(out=CTt[:].rearrange("p g n -> p (g n)"), in_=pA[:]), ["pA"], [nm])
    stage_some(100)
    Er = sbt([128, 32, 8, 16], F32); Ei = sbt([128, 32, 8, 16], F32)
    Fr = sbt([128, 32, 8, 16], F32); Fi = sbt([128, 32, 8, 16], F32)
    for j in range(8):
        cmul3(Er[:, :, j, :], Ei[:, :, j, :], pwr[:, 7 - j, :], pwi[:, 7 - j, :], "pw", bbr[:], bbi[:], "bb", "E")
        cmul3(Fr[:, :, j, :], Fi[:, :, j, :], pwr[:, j + 1, :], pwi[:, j + 1, :], "pw", CTr[:], CTi[:], ["CTr", "CTi"], "F")
    halfm = sbt([128, 2], F32)
    el("pool", lambda e: e.memset(halfm[:], 0.0), [], ["halfm"])
    el("pool", lambda e: e.memset(halfm[0:64, 0:1], 1.0), ["halfm"], ["halfm"])
    el("pool", lambda e: e.memset(halfm[64:128, 1:2], 1.0), ["halfm"], ["halfm"])
    for ri, (Et, blk) in enumerate(((Er, BLK_S5), (Ei, BLK_S5 + 1))):
        el("pool", lambda e: e.memset(stgb2[:], 0.0), ["stgb2"], ["stgb2"])
        for g in range(32):
            ps_ = pA if g % 2 == 0 else pB
            nm = "pA" if g % 2 == 0 else "pB"
            P.op("pe", lambda e, Et=Et, g=g, ps_=ps_: e.transpose(out=ps_[:, 0:64], in_=Et[0:64, g, :, :].rearrange("p j n -> p (j n)"),
                                                                  identity=ident_f[0:64, 0:64]), reads=["E", "ident_f"], writes=[nm])
            c0 = g * 128 + 64 * (g % 2)
            el("dve", lambda e, ps_=ps_, c0=c0: e.tensor_copy(out=stgb2[:, c0:c0 + 64], in_=ps_[:, 0:64]), [nm], ["stgb2"])
        P.dma(lambda e, blk=blk: e.dma_start(out=WS[blk], in_=stgb2[:]), reads=["stgb2"], writes=["WS%d" % blk], q="pool")
    for ri, (Ft, blk, sg) in enumerate(((Fr, BLK_S5 + 3, 1.0), (Fi, BLK_S5 + 4, -1.0))):
        for g in range(32):
            P.op("dve",
                 lambda e, Ft=Ft, g=g, sg=sg: e.tensor_scalar(out=stgb2[:, g * 128:(g + 1) * 128], in0=Ft[:, g, :, :].rearrange("p j n -> p (j n)"),
                                                              scalar1=halfm[:, (g % 2):(g % 2) + 1], scalar2=sg, op0=ALU.mult, op1=ALU.mult),
                 reads=["F", "halfm"], writes=["stgb2"])
        P.dma(lambda e, blk=blk: e.dma_start(out=WS[blk], in_=stgb2[:]), reads=["stgb2"], writes=["WS%d" % blk], q="pool")
    Gr = sbt([128, 32, 8, 16], F32); Gni = sbt([128, 32, 8, 16], F32)
    i8r = small(32); i8i = small(32)
    tt(u0[:], pnr[:, 7, :], pnr[:, 1, :], ALU.mult, ["pn"], ["u0"]); tt(u1[:], pni[:, 7, :], pni[:, 1, :], ALU.mult, ["pn"], ["u1"])
    tt(i8r[:], u0[:], u1[:], ALU.subtract, ["u0", "u1"], ["i8"])
    tt(u0[:], pnr[:, 7, :], pni[:, 1, :], ALU.mult, ["pn"], ["u0"]); tt(u1[:], pni[:, 7, :], pnr[:, 1, :], ALU.mult, ["pn"], ["u1"])
    tt(i8i[:], u0[:], u1[:], ALU.add, ["u0", "u1"], ["i8"])
    for j in range(8):
        cmul3(Gr[:, :, j, :], Gni[:, :, j, :], i8r[:], i8i[:], "i8", Fr[:, :, j, :], Fi[:, :, j, :], "F", "G")
    ts(Gni[:], Gni[:], -1.0, None, ALU.mult, None, ["G"], ["G"])
    bmask = sbt([128, 8, 16], F32); Dcol = sbt([128, 32], F32)
    el("pool", lambda e: e.memset(bmask[:], 1.0), [], ["bmask"])
    el("pool", lambda e: e.affine_select(out=bmask[:], in_=bmask[:], pattern=[[16, 8], [0, 16]], compare_op=ALU.is_ge, fill=0.0,
                                         base=15, channel_multiplier=-1), ["bmask"], ["bmask"])
    for j in range(8):
        P.dma(lambda e, j=j: e.dma_start(out=Dcol[16 * j:16 * j + 16, :], in_=D_skip.rearrange("g n -> n g"), **nonc), writes=["Dcol"], q="pool")
    wtmp = sbt([128, 128], F32)
    for g in range(32):
        ps_ = pA if g % 2 == 0 else pB
        nm = "pA" if g % 2 == 0 else "pB"
        P.op("pe", lambda e, g=g, ps_=ps_: e.matmul(ps_[:, 0:128], lhsT=Er[0:64, g, :, :].rearrange("p j n -> p (j n)"),
                                                    rhs=Gr[0:64, g, :, :].rearrange("p j n -> p (j n)"), start=True, stop=False),
             reads=["E", "G"], writes=[nm], sig=False)
        P.op("pe", lambda e, g=g, ps_=ps_: e.matmul(ps_[:, 0:128], lhsT=Ei[0:64, g, :, :].rearrange("p j n -> p (j n)"),
                                                    rhs=Gni[0:64, g, :, :].rearrange("p j n -> p (j n)"), start=False, stop=True),
             reads=["E", "G"], writes=[nm])
        tt(wtmp[:], ps_[:, 0:128], bmask[:].rearrange("p j n -> p (j n)"), ALU.mult, [nm, "bmask"], ["wtmp"])
        el("dve", lambda e, g=g: e.scalar_tensor_tensor(out=stgb2[:, g * 128:(g + 1) * 128], in0=ident_f[:], scalar=Dcol[:, g:g + 1],
                                                        in1=wtmp[:], op0=ALU.mult, op1=ALU.add), ["ident_f", "Dcol", "wtmp", "stgb2"], ["stgb2"])
    P.dma(lambda e: e.dma_start(out=WS[BLK_S5 + 2], in_=stgb2[:]), reads=["stgb2"], writes=["WS%d" % (BLK_S5 + 2)], q="pool")
    for g2 in range(2):
        sl = slice(64 * g2, 64 * g2 + 64)
        src_r = pwr[sl, 8, :].rearrange("p (q t) -> p q t", t=2)[:, :, g2]
        src_i = pwi[sl, 8, :].rearrange("p (q t) -> p q t", t=2)[:, :, g2]
        el("dve", lambda e, sl=sl, src_r=src_r: e.tensor_copy(out=AR32[sl, 0, :], in_=src_r), ["pw"], ["AR32"])
        el("dve", lambda e, sl=sl, src_r=src_r: e.tensor_copy(out=AR32[sl, 1, :], in_=src_r), ["pw"], ["AR32"])
        el("dve", lambda e, sl=sl, src_i=src_i: e.tensor_copy(out=API[sl, :], in_=src_i), ["pw"], ["API"])
        ts(ANI[sl, :], src_i, -1.0, None, ALU.mult, None, ["pw"], ["ANI"])

    stage_some(100)
    if dbg_stop == 1:
        tk = P.dma(lambda e: e.dma_start(out=out[0:128, :], in_=gpost[:]), reads=["gpost"])
        P.final_wait("sp", [tk])
        P.emit(); P.temp.close(); P.close()
        return nc
    P.emit()
    P.temp.close()
    P.temp = None
    NSLOT = 3
    wslot = [sbt([128, 4096], BF16, "wslot%d" % i) for i in range(NSLOT)]
    slot_rr = [0]

    def load_block(blk):
        s = slot_rr[0] % NSLOT
        slot_rr[0] += 1
        P.dma(lambda e: e.dma_start(out=wslot[s][:], in_=WS[blk]), reads=["WS%d" % blk], writes=["wslot%d" % s])
        return wslot[s], "wslot%d" % s

    xs = [sbt([128, 1024], F32, "xs%d" % i) for i in range(2)]
    xr = [sbt([128, 1024], F32, "xr%d" % i) for i in range(2)]
    junk = sbt([128, 1024], BF16); xnb = sbt([128, 1024], BF16)
    ss = small(1); rstd = small(1)
    xnT = sbt([128, 8, T], BF16, "xnT"); uaT = sbt([128, 8, T + 4], BF16, "uaT"); cT = sbt([128, 8, T], BF16, "cT")
    zaT = sbt([128, 8, T], BF16, "zaT"); oaT = sbt([128, 8, T], BF16, "oaT"); zbT = sbt([128, 4, T], BF16, "zbT")
    qT = sbt([128, 4, 2, T], BF16, "qT"); kT = sbt([128, 4, 2, T], BF16, "kT")
    ktm = sbt([128, NCH, 4, 256], BF16, "ktm"); vw = sbt([128, NCH, 4, 264], BF16, "vw")
    gates = sbt([128, NCH, 8], F32, "gates")
    hfT = sbt([128, 8, T], BF16, "hfT"); csz = sbt([128, 8, T], BF16, "csz"); mrgT = zaT; hbT = sbt([128, 4, T], BF16, "hbT")
    Cst = sbt([128, 4, 2, 264], F32, "Cst"); Cbf = sbt([128, 4, 2, 264], BF16, "Cbf")
    nst = sbt([128, 4, 2], F32, "nst"); nrep = sbt([128, 4, 2, 128], BF16, "nrep")
    Utm = sbt([128, 32, 8, 16], BF16, "Utm"); U2 = sbt([128, 32, NC8], BF16, "U2")
    Xall = sbt([128, NC8, 2, 16], F32, "Xall"); Sall = sbt([128, NC8 + 1, 2, 16], F32, "Sall"); Sbf = sbt([128, 2, 16, NC8], BF16, "Sbf")
    Yg = sbt([128, 32, NC8], BF16, "Yg"); Ytm = Utm[:].rearrange("p g j n -> p (g j n)").rearrange("p (j c) -> p j c", j=8); yT = sbt([128, 4, T], BF16, "yT")
    for (t_, nm) in ((Cst, "Cst"), (Cbf, "Cbf"), (nst, "nst"), (nrep, "nrep"), (Sall, "Sall"), (uaT, "uaT"), (vw, "vw")):
        flat = t_[:]
        wnames = [nm] + (["%s%d" % (nm, h_) for h_ in range(4)] if nm in ("Cst", "Cbf", "nrep") else [])
        el("pool", lambda e, flat=flat: e.memset(flat, 0.0), [], wnames)

    e1a = sbt([128, NCH, 4], F32); lfa = sbt([128, NCH, 4], F32); emb4 = sbt([128, NCH, 4, 128], F32); dec4 = sbt([128, NCH, 4], F32)
    wcol4 = sbt([128, NCH, 4], F32); wcolb4 = sbt([128, NCH, 4, 2], BF16); wrep4 = sbt([128, NCH, 4, 128], BF16)
    lfrep = sbt([128, 4, 128], F32); lf = small(4, "lf"); ig = small(4); bcol = small(4); wcol = small(4, "wcol"); wcolb = sbt([128, 4, 2], BF16)
    wrep = sbt([128, 4, 128], BF16); emb = sbt([128, 4, 128], F32, "emb"); dec = small(4, "dec"); SmT = sbt([128, 4, 128], BF16, "SmT")
    aden = sbt([128, 4, 128], F32); rden = sbt([128, 4, 128], F32, "rden"); hT = sbt([128, 8, 128], F32, "hT"); sq = sbt([128, 8, 128], BF16)
    rsh = sbt([128, 4, 128], F32); hn = sbt([128, 8, 128], F32, "hn"); e1 = small(4)
    gAll = oaT; m1 = sbt([128, T], F32); m2 = sbt([128, T], F32)
    ot = [sbt([128, 1024], F32, "ot%d" % i) for i in range(2)]
    ss2 = small(1); rstd2 = small(1); sgl = sbt([128, T], BF16); xg = sbt([128, T], F32)

    def rsqrt_col(dst, src, scale, nm_src, nm_dst):
        ts(dst, src, scale, 1e-6, ALU.mult, ALU.add, [nm_src], [nm_dst])
        el("act", lambda e: e.activation(out=dst, in_=dst, func=AF.Ln), [nm_dst], [nm_dst])
        el("act", lambda e: e.activation(out=dst, in_=dst, func=AF.Exp, scale=-0.5), [nm_dst], [nm_dst])

    def mm(outp, lhsT, rhs, start, stop, r, w, sig=None):
        P.op("pe", lambda e: e.matmul(outp, lhsT=lhsT, rhs=rhs, start=start, stop=stop), reads=r, writes=w,
             sig=(stop if sig is None else sig))

    acc_rr = [0]

    def acc_bank():
        acc_rr[0] += 1
        return (pA, "pA") if acc_rr[0] % 2 else (pB, "pB")

    out_toks = []

    class StopBuild(Exception):
        pass

    def chk(level):
        if dbg_stop == level:
            raise StopBuild()

    prefetched = [False]

    def do_tile(xsrc, row0, main, ti, nxt=None):
        def issue_x(src_, r0, sub):
            xb2, xn2 = xs[sub % 2], "xs%d" % (sub % 2)
            P.dma(lambda e: e.dma_start(out=xb2[:], in_=src_[r0 + sub * 128: r0 + (sub + 1) * 128, :]), writes=[xn2])

        for sub in range(NCH):
            xb_, xnm = xs[sub % 2], "xs%d" % (sub % 2)
            if not (prefetched[0] and sub < 2):
                issue_x(xsrc, row0, sub)
            el("act", lambda e, xb_=xb_: e.activation(out=junk[:], in_=xb_[:], func=AF.Square, accum_out=ss[:]), [xnm], ["junk", "ss"])
            rsqrt_col(rstd[:], ss[:], 1.0 / 1024.0, "ss", "rstd")
            ts(xnb[:], xb_[:], rstd[:, 0:1], None, ALU.mult, None, [xnm, "rstd"], ["xnb"])
            for kt in range(8):
                P.op("pe", lambda e, kt=kt: e.transpose(out=pT[:, kt * 128:(kt + 1) * 128], in_=xnb[:, kt * 128:(kt + 1) * 128], identity=ident_b[:]),
                     reads=["xnb", "ident_b"], writes=["pT"], sig=(kt == 7))
            el("act", lambda e, sub=sub: e.copy(out=xnT[:, :, sub * 128:(sub + 1) * 128], in_=pT[:].rearrange("p (k t) -> p k t", k=8)),
               ["pT"], ["xnT"])

        prefetched[0] = False
        if nxt is not None and NCH <= 2:
            for sub in range(min(2, NCH)):
                issue_x(nxt[0], nxt[1], sub)
            prefetched[0] = True
        chk(2)

        def proj_fm(blk, ncolt, evac):
            W, wn = load_block(blk)
            Wv_ = W[:].rearrange("p (k n) -> p k n", k=8)
            for ct in range(ncolt):
                ps_, pn = acc_bank()
                for kt in range(8):
                    mm(ps_[:, 0:T], Wv_[:, kt, ct * 128:(ct + 1) * 128], xnT[:, kt, :], kt == 0, kt == 7, [wn, "xnT"], [pn])
                evac(ct, ps_, pn)

        el("dve", lambda e: e.tensor_copy(out=uaT[:, :, 1:4], in_=uaT[:, :, T + 1:T + 4]), ["uaT"], ["uaT"])
        for half in range(2):
            proj_fm(BLK_UA + half, 4, lambda ct, ps_, pn, half=half: el(
                "act", lambda e: e.copy(out=uaT[:, half * 4 + ct, 4:T + 4], in_=ps_[:, 0:T]), [pn], ["uaT"]))
        chk(3)
        for ch in range(NCH):
            for kt in range(8):
                mm(pM[:, 0:8], xnT[:, kt, ch * 128:(ch + 1) * 128], Wif[:, kt, :], kt == 0, kt == 7, ["xnT", "Wif"], ["pM"])
            tt(gates[:, ch, :], pM[:, 0:8], bif[:], ALU.add, ["pM", "bif"], ["gates"])
        chk(4)
        W, wn = load_block(BLK_UB)
        Wv_ = W[:].rearrange("p (k n) -> p k n", k=8)
        for j in range(8):
            ps_, pn = acc_bank()
            for kt in range(8):
                lhs = xnT[:, kt, :].rearrange("p (c j) -> p j c", j=8)[:, j, :]
                mm(ps_[0:NC8, :], lhs, Wv_[:, kt, :], kt == 0, kt == 7, [wn, "xnT"], [pn])
            el("act" if j % 2 else "dve",
               (lambda e, ps_=ps_, j=j: e.copy(out=Utm[0:NC8, :, j, :], in_=ps_[0:NC8, :].rearrange("c (g n) -> c g n", g=32))) if j % 2 else
               (lambda e, ps_=ps_, j=j: e.tensor_copy(out=Utm[0:NC8, :, j, :], in_=ps_[0:NC8, :].rearrange("c (g n) -> c g n", g=32))), [pn], ["Utm"])
        chk(5)
        for g in range(32):
            P.op("pe", lambda e, g=g: e.transpose(out=pT[:, g * NC8:(g + 1) * NC8], in_=Utm[0:NC8, g, :, :].rearrange("c j n -> c (j n)"),
                                                  identity=ident_b[0:NC8, 0:NC8]), reads=["Utm", "ident_b"], writes=["pT"], sig=(g == 31))
        el("act", lambda e: e.copy(out=U2[:].rearrange("p g c -> p (g c)"), in_=pT[:, 0:32 * NC8]), ["pT"], ["U2"])
        W1r, w1rn = load_block(BLK_S5)
        W1i, w1in = load_block(BLK_S5 + 1)
        for ri, (Wt, wn_) in enumerate(((W1r, w1rn), (W1i, w1in))):
            Wg = Wt[:].rearrange("p (g n) -> p g n", g=32)
            for q in range(16):
                for g2 in range(2):
                    g = 2 * q + g2
                    mm(pM[:, (q * NC8):(q + 1) * NC8], Wg[:, g, :], U2[:, g, :], g2 == 0, g2 == 1, [wn_, "U2"], ["pM"], sig=(q == 15 and g2 == 1))
            el("act" if ri == 0 else "dve",
               (lambda e, ri=ri: e.copy(out=Xall[:, :, ri, :].rearrange("p c q -> p q c"), in_=pM[:, 0:16 * NC8].rearrange("p (q c) -> p q c", q=16))) if ri == 0 else
               (lambda e, ri=ri: e.tensor_copy(out=Xall[:, :, ri, :].rearrange("p c q -> p q c"), in_=pM[:, 0:16 * NC8].rearrange("p (q c) -> p q c", q=16))),
               ["pM"], ["Xall"])
        chk(6)
        T1 = sbt([128, 2, 16], F32, "scT1_%d" % ti) if False else None
        for c in range(NC8):
            sp_ = Sall[:, c, :, :]
            sn_ = Sall[:, c + 1, :, :]
            P.op("pool", lambda e, sp_=sp_: e.tensor_tensor(out=sc_t1[:], in0=sp_, in1=AR32[:], op=ALU.mult), reads=["Sall"], writes=["sc_t1"])
            P.op("pool", lambda e, c=c: e.tensor_tensor(out=sc_t2[:, 0, :], in0=Sall[:, c, 1, :], in1=ANI[:], op=ALU.mult), reads=["Sall"], writes=["sc_t2"])
            P.op("pool", lambda e, c=c: e.tensor_tensor(out=sc_t2[:, 1, :], in0=Sall[:, c, 0, :], in1=API[:], op=ALU.mult), reads=["Sall"], writes=["sc_t2"])
            P.op("pool", lambda e, c=c: e.tensor_tensor(out=sc_t1[:], in0=sc_t1[:], in1=Xall[:, c, :, :], op=ALU.add), reads=["sc_t1", "Xall"], writes=["sc_t1"])
            P.op("pool", lambda e, sn_=sn_: e.tensor_tensor(out=sn_, in0=sc_t1[:], in1=sc_t2[:], op=ALU.add), reads=["sc_t1", "sc_t2"], writes=["Sall"])
        if main:
            for ri in range(2):
                el("pool", lambda e, ri=ri: e.tensor_copy(out=Sbf[:, ri, :, :], in_=Sall[:, 0:NC8, ri, :].rearrange("p c q -> p q c")),
                   ["Sall"], ["Sbf"])
        el("pool", lambda e: e.tensor_copy(out=Sall[:, 0, :, :], in_=Sall[:, NC8, :, :]), ["Sall"], ["Sall"])
        def s5_out():
            proj_fm(BLK_ZB, 4, lambda ct, ps_, pn: el("act", lambda e: e.activation(out=zbT[:, ct, :], in_=ps_[:, 0:T], func=AF.Silu), [pn], ["zbT"]))
            Wi_, win_ = load_block(BLK_S5 + 2)
            Wr_, wrn_ = load_block(BLK_S5 + 3)
            Wm_, wmn_ = load_block(BLK_S5 + 4)
            Wi_g = Wi_[:].rearrange("p (g n) -> p g n", g=32); Wr_g = Wr_[:].rearrange("p (g n) -> p g n", g=32)
            Wm_g = Wm_[:].rearrange("p (g n) -> p g n", g=32)
            GP = 512 // NC8
            for g0 in range(0, 32, GP):
                ng = min(GP, 32 - g0)
                for gi in range(ng):
                    g = g0 + gi
                    o_ = pM[:, gi * NC8:(gi + 1) * NC8]
                    mm(o_, Wi_g[:, g, :], U2[:, g, :], True, False, [win_, "U2"], ["pM"], sig=False)
                    mm(o_, Wr_g[:, g, :], Sbf[:, 0, g // 2, :], False, False, [wrn_, "Sbf"], ["pM"], sig=False)
                    mm(o_, Wm_g[:, g, :], Sbf[:, 1, g // 2, :], False, True, [wmn_, "Sbf"], ["pM"], sig=(gi == ng - 1))
                el("act", lambda e, g0=g0, ng=ng: e.activation(out=Yg[:, g0:g0 + ng, :].rearrange("p g c -> p (g c)"), in_=pM[:, 0:ng * NC8],
                                                               func=AF.Gelu_apprx_tanh), ["pM"], ["Yg"])
            for g0 in range(0, 32, 8):
                for gi in range(8):
                    g = g0 + gi
                    P.op("pe", lambda e, g=g, gi=gi: e.transpose(out=pT[0:NC8, gi * 128:(gi + 1) * 128], in_=Yg[:, g, :], identity=ident_b[:]),
                         reads=["Yg", "ident_b"], writes=["pT"], sig=(gi == 7))
                el("act", lambda e, g0=g0: e.copy(out=Ytm[0:NC8, :, 16 * g0:16 * g0 + 128].rearrange("c j (g n) -> c g j n", g=8),
                                                  in_=pT[0:NC8, :].rearrange("c (g j n) -> c g j n", g=8, j=8)), ["pT"], ["Utm"])
            for ct in range(4):
                for j in range(8):
                    P.op("pe", lambda e, ct=ct, j=j: e.transpose(out=pT[:, j * NC8:(j + 1) * NC8], in_=Ytm[0:NC8, j, ct * 128:(ct + 1) * 128],
                                                                 identity=ident_b[0:NC8, 0:NC8]), reads=["Utm", "ident_b"], writes=["pT"], sig=(j == 7))
                el("act", lambda e, ct=ct: e.copy(out=yT[:, ct, :], in_=pT[:, 0:T]), ["pT"], ["yT"])
            for ot_ in range(4):
                ps_, pn = acc_bank()
                for ct in range(4):
                    mm(ps_[:, 0:T], Wglu[:, ct, ot_ * 128:(ot_ + 1) * 128], yT[:, ct, :], ct == 0, ct == 3, ["Wglu", "yT"], [pn])
                el("act", lambda e, ps_=ps_, ot_=ot_: e.activation(out=sgl[:], in_=ps_[:, 0:T], func=AF.Sigmoid, bias=bglu[:, ot_:ot_ + 1]),
                   [pn, "bglu"], ["sgl"])
                tt(xg[:], sgl[:], yT[:, ot_, :], ALU.mult, ["sgl", "yT"], ["xg"])
                tt(hbT[:, ot_, :].rearrange("p (j c) -> p j c", j=8), xg[:].rearrange("p (j c) -> p j c", j=8),
                   zbT[:, ot_, :].rearrange("p (c j) -> p j c", j=8), ALU.mult, ["xg", "zbT"], ["hbT"])
        chk(7)
        for mt in range(8):
            ps_, pn = acc_bank()
            for k in range(4):
                mm(ps_[:, 0:T], cdiag[:, mt, k, :], uaT[:, mt, k + 1:k + 1 + T], k == 0, k == 3, ["cdiag", "uaT"], [pn])
            el("act", lambda e, ps_=ps_, mt=mt: e.activation(out=cT[:, mt, :], in_=ps_[:, 0:T], func=AF.Silu, bias=cb[:, mt:mt + 1]),
               [pn, "cb"], ["cT"])
        if main:
            for half in range(2):
                proj_fm(BLK_ZA + half, 4, lambda ct, ps_, pn, half=half: el(
                    "act", lambda e: e.activation(out=zaT[:, half * 4 + ct, :], in_=ps_[:, 0:T], func=AF.Silu), [pn], ["zaT"]))
            for half in range(2):
                proj_fm(BLK_OA + half, 4, lambda ct, ps_, pn, half=half: el(
                    "act", lambda e: e.activation(out=oaT[:, half * 4 + ct, :], in_=ps_[:, 0:T], func=AF.Sigmoid), [pn], ["oaT"]))
        if main:
            for mt in range(8):
                el("dve", lambda e, mt=mt: e.scalar_tensor_tensor(out=csz[:, mt, :], in0=cT[:, mt, :], scalar=skp[:, mt:mt + 1], in1=zaT[:, mt, :],
                                                                  op0=ALU.mult, op1=ALU.mult), ["cT", "skp", "zaT"], ["csz"])
            for mt in range(8):
                ts(zaT[:, mt, :], zaT[:, mt, :], hg[:, mt:mt + 1], None, ALU.mult, None, ["zaT", "hg", "csz"], ["zaT"])
        chk(8)
        for h in range(4):
            if main:
                for (Wx, wxn, dst, dn) in ((Wq, "Wqkv0", qT, "qT"), (Wk, "Wqkv1", kT, "kT")):
                    for et in range(2):
                        ps_, pn = acc_bank()
                        for d in range(2):
                            mm(ps_[:, 0:T], Wx[:, h, d, et * 128:(et + 1) * 128], cT[:, 2 * h + d, :], d == 0, d == 1, [wxn, "cT"], [pn])
                        el("act" if et else "dve",
                           (lambda e, ps_=ps_, dst=dst, h=h, et=et: e.copy(out=dst[:, h, et, :], in_=ps_[:, 0:T])) if et else
                           (lambda e, ps_=ps_, dst=dst, h=h, et=et: e.tensor_copy(out=dst[:, h, et, :], in_=ps_[:, 0:T])), [pn], [dn])
        el("act", lambda e: e.activation(out=e1a[:], in_=gates[:, :, 4:8], func=AF.Exp, scale=-1.0), ["gates"], ["e1a"])
        el("act", lambda e: e.activation(out=lfa[:], in_=e1a[:], func=AF.Ln, bias=1.0), ["e1a"], ["lfa"])
        ts(lfa[:], lfa[:], -1.0, None, ALU.mult, None, ["lfa"], ["lfa"])
        for ch in range(NCH):
            tsl = slice(ch * 128, (ch + 1) * 128)
            for h in range(4):
                el("dve", lambda e, h=h, ch=ch: e.tensor_copy(out=lfrep[:, h, :], in_=lfa[:, ch, h:h + 1].to_broadcast([128, 128])), ["lfa"], ["lfrep"])
            for h in range(4):
                mm(pGD[:, h * 128:(h + 1) * 128], lfrep[:, h, :], maskT[:], True, True, ["lfrep", "maskT"], ["pGD"], sig=(h == 3))
            for h in range(4):
                mm(pM[:, h * 128:(h + 1) * 128], maskT[:], lfrep[:, h, :], True, True, ["maskT", "lfrep"], ["pM"], sig=(h == 3))
            en, dn, wn_, wbn, wrn = "emb%d" % ch, "dec%d" % ch, "wcol%d" % ch, "wcolb%d" % ch, "wrep%d" % ch
            el("act", lambda e, ch=ch: e.activation(out=emb4[:, ch, :, :].rearrange("p h t -> p (h t)"), in_=pGD[:], func=AF.Exp, scale=-1.0), ["pGD"], [en])
            el("dve", lambda e, ch=ch: e.reciprocal(out=dec4[:, ch, :], in_=emb4[:, ch, :, 127]), [en], [dn])
            tt(bcol[:], gates[:, ch, 0:4], pM[:].rearrange("p (h t) -> p h t", h=4)[:, :, 0], ALU.subtract, ["gates", "pM"], ["bcol"])
            el("act", lambda e, ch=ch: e.activation(out=wcol4[:, ch, :], in_=bcol[:], func=AF.Exp), ["bcol"], [wn_])
            el("dve", lambda e, ch=ch: e.tensor_copy(out=vw[:, ch, :, 256], in_=wcol4[:, ch, :]), [wn_], ["vw"])
            if main:
                for h in range(4):
                    el("dve", lambda e, h=h, ch=ch: e.tensor_copy(out=wrep4[:, ch, h, :], in_=wcol4[:, ch, h:h + 1].to_broadcast([128, 128])), [wn_], [wrn])
            for h in range(4):
                ps_, pn = acc_bank()
                for d in range(2):
                    mm(ps_[:, 0:256], cT[:, 2 * h + d, tsl], Wk[:, h, d, :], d == 0, d == 1, ["cT", "Wqkv1"], [pn], sig=False)
                for d in range(2):
                    mm(ps_[:, 256:512], uaT[:, 2 * h + d, 4 + ch * 128:4 + (ch + 1) * 128], Wv[:, h, d, :], d == 0, d == 1, ["uaT", "Wqkv2"], [pn])
                el("act", lambda e, ps_=ps_, ch=ch, h=h: e.copy(out=ktm[:, ch, h, :], in_=ps_[:, 0:256]), [pn], ["ktm"])
                el("act", lambda e, ps_=ps_, ch=ch, h=h: e.activation(out=vw[:, ch, h, 0:256], in_=ps_[:, 256:512], func=AF.Copy, scale=wcol4[:, ch, h:h + 1]),
                   [pn, wn_], ["vw"])
        for ch in range(NCH):
            tsl = slice(ch * 128, (ch + 1) * 128)
            en, dn, wn_, wbn, wrn = "emb%d" % ch, "dec%d" % ch, "wcol%d" % ch, "wcolb%d" % ch, "wrep%d" % ch
            if main:
                for h in range(4):
                    for et in range(2):
                        mm(pS[:, h * 128:(h + 1) * 128], kT[:, h, et, tsl], qT[:, h, et, tsl], et == 0, et == 1, ["kT", "qT"], ["pS"], sig=(h == 3 and et == 1))
                tt(SmT[:].rearrange("p h t -> p (h t)"), pS[:], mask4[:].rearrange("p h t -> p (h t)"), ALU.mult, ["pS", "mask4"], ["SmT"])
                for h in range(4):
                    for d2 in range(2):
                        pn_, pnn = (pN0, "pN0") if h < 2 else (pN1, "pN1")
                        o_ = pn_[:, ((h % 2) * 2 + d2) * 128:((h % 2) * 2 + d2 + 1) * 128]
                        mm(o_, vw[:, ch, h, d2 * 128:(d2 + 1) * 128], SmT[:, h, :], True, False, ["vw", "SmT"], [pnn], sig=False)
                        mm(o_, Cbf[:, h, 0, d2 * 128:(d2 + 1) * 128], qT[:, h, 0, tsl], False, False, ["Cbf%d" % h, "qT"], [pnn], sig=False)
                        mm(o_, Cbf[:, h, 1, d2 * 128:(d2 + 1) * 128], qT[:, h, 1, tsl], False, True, ["Cbf%d" % h, "qT"], [pnn], sig=(h % 2 == 1 and d2 == 1))
                    o_ = pGD[:, h * 128:(h + 1) * 128]
                    mm(o_, wrep4[:, ch, h, :], SmT[:, h, :], True, False, [wrn, "SmT"], ["pGD"], sig=False)
                    mm(o_, nrep[:, h, 0, :], qT[:, h, 0, tsl], False, False, ["nrep%d" % h, "qT"], ["pGD"], sig=False)
                    mm(o_, nrep[:, h, 1, :], qT[:, h, 1, tsl], False, True, ["nrep%d" % h, "qT"], ["pGD"], sig=(h == 3))
            if main:
                el("act", lambda e: e.activation(out=aden[:].rearrange("p h t -> p (h t)"), in_=pGD[:], func=AF.Abs), ["pGD"], ["aden"])
                tt(aden[:], aden[:], emb4[:, ch, :, :], ALU.max, ["aden", en], ["aden"])
                el("act", lambda e: e.activation(out=aden[:], in_=aden[:], func=AF.Ln), ["aden"], ["aden"])
                el("act", lambda e: e.activation(out=rden[:], in_=aden[:], func=AF.Exp, scale=-1.0), ["aden"], ["rden"])
            for h in range(4):
                cn, bn, nn = "Cst%d" % h, "Cbf%d" % h, "nrep%d" % h
                for et in range(2):
                    ps_, pn = acc_bank()
                    mm(ps_[:, 0:258], ktm[:, ch, h, et * 128:(et + 1) * 128], vw[:, ch, h, 0:258], True, True, ["ktm", "vw"], [pn])
                    tt(Cst[:, h, et, 0:258], Cst[:, h, et, 0:258], ps_[:, 0:258], ALU.add, [cn, pn, bn, nn], [cn])
            for h in range(4):
                cn, bn, nn = "Cst%d" % h, "Cbf%d" % h, "nrep%d" % h
                el("act", lambda e, h=h, ch=ch: e.activation(out=Cst[:, h, :, :], in_=Cst[:, h, :, :], func=AF.Copy, scale=dec4[:, ch, h:h + 1]), [cn, dn], [cn])
                el("act", lambda e, h=h: e.copy(out=Cbf[:, h, :, :], in_=Cst[:, h, :, :]), [cn], [bn])
            for h in range(4):
                cn, bn, nn = "Cst%d" % h, "Cbf%d" % h, "nrep%d" % h
                for et in range(2):
                    el("dve", lambda e, h=h, et=et: e.tensor_copy(out=nrep[:, h, et, :], in_=Cst[:, h, et, 256:257].to_broadcast([128, 128])),
                       [cn], [nn])
            if main:
                for hp, (pn_, pnn) in enumerate(((pN0, "pN0"), (pN1, "pN1"))):
                    tt(hT[:, 4 * hp:4 * hp + 4, :].rearrange("p (h d) t -> p h d t", h=2), pn_[:].rearrange("p (h d t) -> p h d t", h=2, d=2),
                       rden[:, 2 * hp:2 * hp + 2, :].unsqueeze(2).to_broadcast([128, 2, 2, 128]), ALU.mult, [pnn, "rden"], ["hT"])
                tt(hT[:], hT[:], oaT[:, :, tsl], ALU.mult, ["hT", "oaT"], ["hT"])
                el("act", lambda e: e.activation(out=sq[:], in_=hT[:], func=AF.Square), ["hT"], ["sq"])
                for h in range(4):
                    for d2 in range(2):
                        mm(pS[:, h * 128:(h + 1) * 128], ones_b[:], sq[:, 2 * h + d2, :], d2 == 0, d2 == 1, ["ones_b", "sq", "SmT"], ["pS"], sig=(h == 3 and d2 == 1))
                ts(rsh[:].rearrange("p h t -> p (h t)"), pS[:], 1.0 / 256.0, 1e-6, ALU.mult, ALU.add, ["pS"], ["rsh"])
                el("act", lambda e: e.activation(out=rsh[:], in_=rsh[:], func=AF.Ln), ["rsh"], ["rsh"])
                el("act", lambda e: e.activation(out=rsh[:], in_=rsh[:], func=AF.Exp, scale=-0.5), ["rsh"], ["rsh"])
                tt(hn[:].rearrange("p (h d) t -> p h d t", h=4), hT[:].rearrange("p (h d) t -> p h d t", h=4),
                   rsh[:].unsqueeze(2).to_broadcast([128, 4, 2, 128]), ALU.mult, ["hT", "rsh"], ["hn"])
                tt(hn[:], hn[:], zaT[:, :, tsl], ALU.mult, ["hn", "zaT"], ["hn"])
                tt(hfT[:, :, tsl], hn[:], csz[:, :, tsl], ALU.add, ["hn", "csz"], ["hfT"])
        chk(9)
        if not main:
            return
        s5_out()
        def gate_blocks(br):
            for bi_ in range(2):
                Wg_, wgn = load_block(BLK_G + 2 * br + bi_)
                Wgv = Wg_[:].rearrange("p (k n) -> p k n", k=8)
                for c4 in range(4):
                    ft = bi_ * 4 + c4
                    psg, png = acc_bank()
                    for kt in range(8):
                        mm(psg[:, 0:T], Wgv[:, kt, c4 * 128:(c4 + 1) * 128], xnT[:, kt, :], kt == 0, kt == 7, [wgn, "xnT"], [png])
                    el("act", lambda e, psg=psg, ft=ft: e.activation(out=gAll[:, ft, :], in_=psg[:, 0:T], func=AF.Sigmoid), [png], ["oaT"])

        gate_blocks(0)
        for hf_ in range(2):
            Wa, wan = load_block(BLK_AO + hf_)
            Wav = Wa[:].rearrange("p (k n) -> p k n", k=8)
            for c4 in range(4):
                ft = hf_ * 4 + c4
                psa, pna = acc_bank()
                for kt in range(8):
                    mm(psa[:, 0:T], Wav[:, kt, c4 * 128:(c4 + 1) * 128], hfT[:, kt, :], kt == 0, kt == 7, [wan, "hfT"], [pna])
                tt(mrgT[:, ft, :], psa[:, 0:T], gAll[:, ft, :], ALU.mult, [pna, "oaT"], ["zaT"])
        gate_blocks(1)
        Wbo, wbon = load_block(BLK_BO)
        Wbv = Wbo[:].rearrange("p (k n) -> p k n", k=4)
        for ft in range(8):
            psb, pnb = acc_bank()
            for kt in range(4):
                mm(psb[:, 0:T], Wbv[:, kt, ft * 128:(ft + 1) * 128], hbT[:, kt, :], kt == 0, kt == 3, [wbon, "hbT"], [pnb])
            el("act", lambda e, psb=psb: e.copy(out=m2[:].rearrange("p (c j) -> p j c", j=8), in_=psb[:, 0:T].rearrange("p (j c) -> p j c", j=8)),
               [pnb], ["m2"])
            tt(m2[:], m2[:], gAll[:, ft, :], ALU.mult, ["m2", "oaT"], ["m2"])
            tt(mrgT[:, ft, :], m2[:], mrgT[:, ft, :], ALU.add, ["m2", "zaT"], ["zaT"])
        Wo0, wo0n = load_block(BLK_WO); Wo1, wo1n = load_block(BLK_WO + 1)
        for ch in range(NCH):
            tsl = slice(ch * 128, (ch + 1) * 128)
            for hf_, (Wo_, won, pn_, pnn) in enumerate(((Wo0, wo0n, pN0, "pN0"), (Wo1, wo1n, pN1, "pN1"))):
                Wov = Wo_[:].rearrange("p (k n) -> p k n", k=8)
                for kt in range(8):
                    mm(pn_[:, :], mrgT[:, kt, tsl], Wov[:, kt, :], kt == 0, kt == 7, ["zaT", won], [pnn])
            el("act", lambda e: e.activation(out=junk[:, 0:512], in_=pN0[:], func=AF.Square, accum_out=ss2[:]), ["pN0"], ["junk", "ss2"])
            el("act", lambda e: e.activation(out=junk[:, 512:1024], in_=pN1[:], func=AF.Square, accum_out=rstd2[:]), ["pN1"], ["junk", "rstd2"])
            tt(ss2[:], ss2[:], rstd2[:], ALU.add, ["ss2", "rstd2"], ["ss2"])
            rsqrt_col(rstd2[:], ss2[:], 1.0 / 1024.0, "ss2", "rstd2")
            ob, obn = ot[ch % 2], "ot%d" % (ch % 2)
            xb_, xnm = xr[ch % 2], "xr%d" % (ch % 2)
            P.dma(lambda e, xb_=xb_, ch=ch: e.dma_start(out=xb_[:], in_=xsrc[row0 + ch * 128: row0 + (ch + 1) * 128, :]), writes=[xnm])
            for hf_, (pn_, pnn) in enumerate(((pN0, "pN0"), (pN1, "pN1"))):
                cs = slice(hf_ * 512, hf_ * 512 + 512)
                el("dve", lambda e, ob=ob, pn_=pn_, cs=cs: e.scalar_tensor_tensor(out=ob[:, cs], in0=pn_[:], scalar=rstd2[:, 0:1], in1=gpost[:, cs],
                                                                                 op0=ALU.mult, op1=ALU.mult), [pnn, "rstd2", "gpost"], [obn])
            tt(ob[:], ob[:], xb_[:], ALU.add, [obn, xnm], [obn])
            tok = P.dma(lambda e, ob=ob, ch=ch: e.dma_start(out=out[row0 + ch * 128: row0 + (ch + 1) * 128, :], in_=ob[:]), reads=[obn])
            out_toks.append(tok)

    sc_t1 = sbt([128, 2, 16], F32, "sc_t1"); sc_t2 = sbt([128, 2, 16], F32, "sc_t2")
    try:
        for ti in range(NPRE):
            nxt = (x_pre, (ti + 1) * T) if ti + 1 < NPRE else (x_main, 0)
            do_tile(x_pre, ti * T, False, ti, nxt)
    except StopBuild:
        tk = P.dma(lambda e: e.dma_start(out=out[0:128, :], in_=gpost[:]), reads=["gpost"])
        P.final_wait("sp", [tk])
        P.emit(); P.close()
        return nc
    if NPRE > 0:
        for h in range(4):
            cn, bn, nn = "Cst%d" % h, "Cbf%d" % h, "nrep%d" % h
            ts(Cst[:, h, :, :], Cst[:, h, :, :], flg[:, 0:1], None, ALU.mult, None, [cn, "flg"], [cn])
            el("act", lambda e, h=h: e.copy(out=Cbf[:, h, :, :], in_=Cst[:, h, :, :]), [cn], [bn])
            for et in range(2):
                el("dve", lambda e, h=h, et=et: e.tensor_copy(out=nrep[:, h, et, :], in_=Cst[:, h, et, 256:257].to_broadcast([128, 128])),
                   [cn], [nn])
    for ti in range(NMAIN):
        nxt = (x_main, (ti + 1) * T) if ti + 1 < NMAIN else None
        do_tile(x_main, ti * T, True, NPRE + ti, nxt)
    P.final_wait("sp", out_toks)
    P.emit()
    P.close()
    return nc


T_TILE = 256
_cache = {}


def kernel(**inputs):
    x = np.ascontiguousarray(inputs["x"], dtype=np.float32)
    Bsz, L, Dm = x.shape
    half = L // 2
    npre = half // T_TILE
    nmain = half // T_TILE
    key = (T_TILE, npre, nmain)
    if key not in _cache:
        _cache[key] = build_program(T_TILE, npre, nmain)
    nc = _cache[key]
    shared = {}
    for k, v in inputs.items():
        if k == "x":
            continue
        a = np.ascontiguousarray(np.asarray(v, dtype=np.float32)[0])
        if k in ("b_i", "b_f", "log_dt", "norm_post_g"):
            a = a.reshape(1, -1)
        shared[k] = a
    in_maps = []
    zeros = np.zeros((half, Dm), np.float32)
    for core in range(8):
        b, hf = core // 2, core % 2
        m = dict(shared)
        m["x_main"] = np.ascontiguousarray(x[b, hf * half:(hf + 1) * half])
        m["x_pre"] = zeros if hf == 0 else np.ascontiguousarray(x[b, 0:half])
        m["flag"] = np.full((128, 1), float(hf), np.float32)
        in_maps.append(m)
    res = run_bass_kernel_spmd(nc, in_maps, core_ids=list(range(8)))
    outp = np.empty((Bsz, L, Dm), np.float32)
    for core in range(8):
        b, hf = core // 2, core % 2
        outp[b, hf * half:(hf + 1) * half] = res.results[core]["out"]
    return outp
```

```python
import contextlib
import numpy as np
import concourse.bass as bass
import concourse.mybir as mybir
from concourse.bass_utils import run_bass_kernel_spmd

F32 = mybir.dt.float32
BF16 = mybir.dt.bfloat16
AF = mybir.ActivationFunctionType
ALU = mybir.AluOpType
COMPUTE = ("pe", "act", "dve", "pool")
NDMA_SLOTS = 8
DEBUG_TAGS = False
PI = float(np.pi)


class Prog:
    def __init__(self, nc):
        self.nc = nc
        self.stack = contextlib.ExitStack()
        self.engs = ("pe", "act", "dve", "pool", "sp")
        self.ops = {e: [] for e in self.engs}
        self.waited = {e: {} for e in self.engs}
        self.res = {}
        self.dma_use = {}
        self.dma_rr = {e: 0 for e in self.engs}
        self.sems = {}
        self.base = {e: 0 for e in COMPUTE}
        self.temp = None

    def sb(self, name, shape, dt):
        st = self.temp if self.temp is not None else self.stack
        return st.enter_context(self.nc.sbuf_tensor(name, list(shape), dt))

    def ps(self, name, shape, dt):
        return self.stack.enter_context(self.nc.psum_tensor(name, list(shape), dt))

    def _sem(self, key):
        if key not in self.sems:
            nm = "s_" + "_".join(str(k) for k in (key if isinstance(key, tuple) else (key,)))
            self.sems[key] = self.stack.enter_context(self.nc.semaphore(nm))
        return self.sems[key]

    def _deps(self, eng, reads, writes):
        deps = {}

        def add(tok):
            if tok is None:
                return
            k, v = tok
            if k == "pe" and eng == "pe":
                return
            if deps.get(k, -1) < v:
                deps[k] = v

        for r in reads:
            st = self.res.get(r)
            if st:
                for k, v in st[0].items():
                    add((k, v))
        for w in writes:
            st = self.res.get(w)
            if st:
                for k, v in st[0].items():
                    add((k, v))
                for k, v in st[1].items():
                    add((k, v))
        out = []
        wd = self.waited[eng]
        for k, v in deps.items():
            if wd.get(k, -1) >= v:
                continue
            wd[k] = v
            out.append((k, v))
        return out

    def _commit(self, tok, reads, writes):
        k, v = tok
        for r in reads:
            st = self.res.setdefault(r, [{}, {}])
            if st[1].get(k, -1) < v:
                st[1][k] = v
        for w in writes:
            old = self.res.get(w)
            wr = {}
            if old is not None and k not in COMPUTE:
                wr = {k2: v2 for k2, v2 in old[0].items() if k2 not in COMPUTE}
            wr[k] = v
            self.res[w] = [wr, {}]

    def _tag(self):
        if not DEBUG_TAGS:
            return None
        import sys as _sys
        f = _sys._getframe(2)
        while f is not None and f.f_code.co_name not in ("do_tile", "build_program", "s5_out", "gate_blocks", "proj_fm"):
            f = f.f_back
        return str(f.f_lineno) if f is not None else None

    def op(self, eng, fn, reads=(), writes=(), sig=True):
        waits = self._deps(eng, reads, writes)
        idx = len(self.ops[eng])
        self.ops[eng].append(dict(fn=fn, waits=waits, sig=sig, dma=None, tag=self._tag()))
        tok = (eng, idx)
        self._commit(tok, reads, writes)
        return tok

    def dma(self, fn, reads=(), writes=(), q="sp"):
        waits = self._deps(q, reads, writes)
        slot = self.dma_rr[q] % NDMA_SLOTS
        self.dma_rr[q] += 1
        key = ("d", q, slot)
        n = self.dma_use.get(key, 0)
        if n > 0:
            prev = n * 16
            if self.waited[q].get(key, -1) < prev:
                self.waited[q][key] = prev
                waits.append((key, prev))
        self.dma_use[key] = n + 1
        tok = (key, (n + 1) * 16)
        self.ops[q].append(dict(fn=fn, waits=waits, sig=False, dma=key))
        self._commit(tok, reads, writes)
        return tok

    def final_wait(self, eng, toks):
        self.ops[eng].append(dict(fn=None, waits=list(toks), sig=False, dma=None))

    def emit(self):
        nc = self.nc
        sigcount = {}
        totals = {}
        for e in COMPUTE:
            c = self.base[e]
            arr = []
            for o in self.ops[e]:
                if o["sig"]:
                    c += 1
                arr.append(c)
            need = [None] * len(arr)
            nxt = None
            for i in range(len(arr) - 1, -1, -1):
                if self.ops[e][i]["sig"]:
                    nxt = arr[i]
                need[i] = nxt
            sigcount[e] = need
            totals[e] = c
            self._sem(e)
        for k in self.dma_use:
            self._sem(k)

        def resolve(k, v):
            if k in COMPUTE:
                val = sigcount[k][v]
                assert val is not None, (k, v)
                return self.sems[k], val
            return self.sems[k], v

        with nc.Block() as block:

            def run(eng_name, eng):
                for o in self.ops[eng_name]:
                    for k, v in o["waits"]:
                        s, val = resolve(k, v)
                        eng.wait_ge(s, val)
                    if o["fn"] is None:
                        continue
                    ins = o["fn"](eng)
                    if o.get("tag"):
                        ins.annotate(o["tag"])
                    if o["dma"] is not None:
                        ins.then_inc(self.sems[o["dma"]], 16)
                    elif o["sig"]:
                        ins.then_inc(self.sems[eng_name], 1)
                for o2 in COMPUTE:
                    if o2 != eng_name and totals[o2] > 0:
                        eng.wait_ge(self.sems[o2], totals[o2])
                for k, n in self.dma_use.items():
                    eng.wait_ge(self.sems[k], n * 16)

            @block.tensor
            def _(e):
                run("pe", e)

            @block.scalar
            def _(e):
                run("act", e)

            @block.vector
            def _(e):
                run("dve", e)

            @block.gpsimd
            def _(e):
                run("pool", e)

            @block.sync
            def _(e):
                run("sp", e)

        self.base = totals
        self.ops = {e: [] for e in self.engs}
        self.waited = {e: {} for e in self.engs}
        self.res = {}

    def close(self):
        self.stack.close()


NBLK = 22
BLK_UA, BLK_UB, BLK_ZB, BLK_ZA, BLK_OA, BLK_G, BLK_AO, BLK_BO, BLK_WO, BLK_S5 = 0, 2, 3, 4, 6, 8, 12, 14, 15, 17
COL_UA, COL_ZA, COL_OA, COL_I, COL_UB, COL_ZB, COL_G = 0, 1024, 2048, 3072, 3080, 3592, 4104


def build_program(T, NPRE, NMAIN, dbg_stop=0):
    NCH = T // 128
    NC8 = T // 8
    nc = bass.Bass("TRN2", target_bir_lowering=False)
    dram = {}

    def din(name, shape):
        dram[name] = nc.dram_tensor(name, list(shape), F32, kind="ExternalInput").ap()
        return dram[name]

    x_pre = din("x_pre", [max(NPRE, 1) * T, 1024])
    x_main = din("x_main", [NMAIN * T, 1024])
    flag = din("flag", [128, 1])
    norm_pre_g = din("norm_pre_g", [1024]); w_in = din("w_in", [1024, 6152])
    conv_w = din("conv_w", [4, 1024]); conv_b = din("conv_b", [1024])
    w_q = din("w_q", [4, 256, 256]); w_k = din("w_k", [4, 256, 256]); w_v = din("w_v", [4, 256, 256])
    b_i = din("b_i", [1, 4]); b_f = din("b_f", [1, 4]); head_g = din("head_g", [1024]); skip_a = din("skip_a", [1024])
    w_a_out = din("w_a_out", [1024, 1024])
    lam_re = din("lam_re", [32, 64]); lam_im = din("lam_im", [32, 64]); log_dt = din("log_dt", [1, 32])
    B_re = din("B_re", [32, 64, 16]); B_im = din("B_im", [32, 64, 16])
    C_re = din("C_re", [32, 16, 64]); C_im = din("C_im", [32, 16, 64]); D_skip = din("D_skip", [32, 16])
    w_glu = din("w_glu", [512, 512]); b_glu = din("b_glu", [512]); w_b_out = din("w_b_out", [512, 1024])
    w_o = din("w_o", [1024, 1024]); norm_post_g = din("norm_post_g", [1, 1024])
    out = nc.dram_tensor("out", [NMAIN * T, 1024], F32, kind="ExternalOutput").ap()
    WS = nc.dram_tensor("wscratch", [NBLK, 128, 4096], BF16, kind="Internal").ap()

    P = Prog(nc)
    uid = [0]

    def sbt(shape, dt, name=None):
        uid[0] += 1
        return P.sb(name or ("t%d" % uid[0]), shape, dt)

    ident_f = sbt([128, 128], F32); ident_b = sbt([128, 128], BF16)
    maskT = sbt([128, 128], F32); mask4 = sbt([128, 4, 128], F32); ones_b = sbt([128, 128], BF16)
    P.op("pool", lambda e: e.memset(ident_f[:], 1.0), writes=["ident_f"])
    P.op("pool", lambda e: e.affine_select(out=ident_f[:], in_=ident_f[:], pattern=[[-1, 128]], compare_op=ALU.is_equal,
                                           fill=0.0, base=0, channel_multiplier=1), reads=["ident_f"], writes=["ident_f"])
    P.op("dve", lambda e: e.tensor_copy(out=ident_b[:], in_=ident_f[:]), reads=["ident_f"], writes=["ident_b"])
    P.op("pool", lambda e: e.memset(maskT[:], 1.0), writes=["maskT"])
    P.op("pool", lambda e: e.affine_select(out=maskT[:], in_=maskT[:], pattern=[[1, 128]], compare_op=ALU.is_ge,
                                           fill=0.0, base=0, channel_multiplier=-1), reads=["maskT"], writes=["maskT"])
    for h in range(4):
        P.op("pool", lambda e, h=h: e.tensor_copy(out=mask4[:, h, :], in_=maskT[:]), reads=["maskT"], writes=["mask4"])
    P.op("pool", lambda e: e.memset(ones_b[:], 1.0), writes=["ones_b"])

    gpre = sbt([128, 8], F32); cb = sbt([128, 8], F32); hg = sbt([128, 8], F32); skp = sbt([128, 8], F32)
    cw = sbt([128, 8, 4], F32); bglu = sbt([128, 4], F32); gpost = sbt([128, 1024], F32); bif = sbt([128, 8], F32)
    flg = sbt([128, 1], F32)
    nonc = dict(allow_slow_non_contiguous=True)
    P.dma(lambda e: e.dma_start(out=gpre[:], in_=norm_pre_g.rearrange("(k p) -> p k", p=128), **nonc), writes=["gpre"])
    P.dma(lambda e: e.dma_start(out=cb[:], in_=conv_b.rearrange("(k p) -> p k", p=128), **nonc), writes=["cb"])
    P.dma(lambda e: e.dma_start(out=hg[:], in_=head_g.rearrange("(k p) -> p k", p=128), **nonc), writes=["hg"])
    P.dma(lambda e: e.dma_start(out=skp[:], in_=skip_a.rearrange("(k p) -> p k", p=128), **nonc), writes=["skp"])
    for k in range(4):
        P.dma(lambda e, k=k: e.dma_start(out=cw[:, :, k], in_=conv_w[k].rearrange("(m p) -> p m", p=128), **nonc), writes=["cw"])
    P.dma(lambda e: e.dma_start(out=bglu[:], in_=b_glu.rearrange("(k p) -> p k", p=128), **nonc), writes=["bglu"])
    P.dma(lambda e: e.dma_start(out=gpost[:], in_=norm_post_g.partition_broadcast(128)), writes=["gpost"])
    P.dma(lambda e: e.dma_start(out=bif[:, 0:4], in_=b_i.partition_broadcast(128)), writes=["bif"])
    P.dma(lambda e: e.dma_start(out=bif[:, 4:8], in_=b_f.partition_broadcast(128)), writes=["bif"])
    P.dma(lambda e: e.dma_start(out=flg[:], in_=flag), writes=["flg"])

    Wqkv = [sbt([128, 4, 2, 256], BF16) for _ in range(3)]
    Wif = sbt([128, 8, 8], BF16); Wglu = sbt([128, 4, 512], BF16); cdiag = sbt([128, 8, 4, 128], BF16)
    AR32 = sbt([128, 2, 16], F32); ANI = sbt([128, 16], F32); API = sbt([128, 16], F32)
    pA = P.ps("pA", [128, 512], F32); pB = P.ps("pB", [128, 512], F32)
    pT = P.ps("pT", [128, 1024], BF16); pGD = P.ps("pGD", [128, 512], F32)
    pS = P.ps("pS", [128, 512], F32); pN0 = P.ps("pN0", [128, 512], F32); pN1 = P.ps("pN1", [128, 512], F32)
    pM = P.ps("pM", [128, 512], F32)
    P.temp = contextlib.ExitStack()
    stg = sbt([128, 4096], F32, "stg")
    stgb = sbt([128, 4096], BF16, "stgb")
    stgb2 = sbt([128, 4096], BF16, "stgb2")
    for wi, (wsrc, scl) in enumerate(((w_q, 1.0), (w_k, 1.0 / 16.0), (w_v, 1.0))):
        P.dma(lambda e, wsrc=wsrc: e.dma_start(out=stg[:, 0:2048].rearrange("p (h d n) -> p h d n", h=4, d=2),
                                               in_=wsrc.rearrange("h (d p) n -> p h d n", p=128)), writes=["stg"])
        P.op("dve", lambda e, wi=wi, scl=scl: e.tensor_scalar(out=Wqkv[wi][:].rearrange("p h d n -> p (h d n)"), in0=stg[:, 0:2048],
                                                              scalar1=scl, scalar2=None, op0=ALU.mult), reads=["stg"], writes=["Wqkv%d" % wi])
    Wq, Wk, Wv = Wqkv
    P.dma(lambda e: e.dma_start(out=stg[:, 0:64].rearrange("p (k n) -> p k n", k=8),
                                in_=w_in[:, COL_I:COL_I + 8].rearrange("(k p) n -> p k n", p=128), **nonc), writes=["stg"])
    for kt in range(8):
        P.op("dve", lambda e, kt=kt: e.tensor_scalar(out=Wif[:, kt, :], in0=stg[:, kt * 8:(kt + 1) * 8], scalar1=gpre[:, kt:kt + 1],
                                                     scalar2=None, op0=ALU.mult), reads=["stg", "gpre"], writes=["Wif"])
    P.dma(lambda e: e.dma_start(out=stg[:, 0:2048].rearrange("p (k n) -> p k n", k=4),
                                in_=w_glu.rearrange("(k p) n -> p k n", p=128)), writes=["stg"])
    P.op("dve", lambda e: e.tensor_copy(out=Wglu[:].rearrange("p k n -> p (k n)"), in_=stg[:, 0:2048]), reads=["stg"], writes=["Wglu"])
    for mt in range(8):
        for k in range(4):
            P.op("dve", lambda e, mt=mt, k=k: e.tensor_scalar(out=cdiag[:, mt, k, :], in0=ident_f[:], scalar1=cw[:, mt, k:k + 1],
                                                               scalar2=None, op0=ALU.mult), reads=["ident_f", "cw"], writes=["cdiag"])

    def stage_block(blk, src_ap_f, scale_gpre, nk):
        ncol = 4096 // nk
        P.dma(lambda e: e.dma_start(out=stg[:].rearrange("p (k n) -> p k n", k=nk), in_=src_ap_f), writes=["stg"])
        if scale_gpre:
            for kt in range(nk):
                P.op("act",
                     lambda e, kt=kt: e.activation(out=stgb[:, kt * ncol:(kt + 1) * ncol], in_=stg[:, kt * ncol:(kt + 1) * ncol],
                                                   func=AF.Copy, scale=gpre[:, kt:kt + 1]),
                     reads=["stg", "gpre"], writes=["stgb"])
        else:
            P.op("act", lambda e: e.copy(out=stgb[:, 0:2048], in_=stg[:, 0:2048]), reads=["stg"], writes=["stgb"])
            P.op("act", lambda e: e.copy(out=stgb[:, 2048:4096], in_=stg[:, 2048:4096]), reads=["stg"], writes=["stgb"])
        P.dma(lambda e: e.dma_start(out=WS[blk], in_=stgb[:]), reads=["stgb"], writes=["WS%d" % blk])

    def win_cols(c0):
        return w_in[:, c0:c0 + 512].rearrange("(k p) n -> p k n", p=128)

    win_blocks = [(BLK_UA, COL_UA), (BLK_UA + 1, COL_UA + 512), (BLK_UB, COL_UB), (BLK_ZB, COL_ZB), (BLK_ZA, COL_ZA),
                  (BLK_ZA + 1, COL_ZA + 512), (BLK_OA, COL_OA), (BLK_OA + 1, COL_OA + 512)] + [(BLK_G + i, COL_G + 512 * i) for i in range(4)]
    pending = []
    for blk, c0 in win_blocks:
        pending.append((blk, win_cols(c0), True, 8))
    for i in range(2):
        pending.append((BLK_AO + i, w_a_out[:, 512 * i:512 * i + 512].rearrange("(k p) n -> p k n", p=128), False, 8))
        pending.append((BLK_WO + i, w_o[:, 512 * i:512 * i + 512].rearrange("(k p) n -> p k n", p=128), False, 8))
    pending.append((BLK_BO, w_b_out.rearrange("(k p) n -> p k n", p=128), False, 4))

    def stage_some(n=1):
        for _ in range(n):
            if pending:
                stage_block(*pending.pop(0))


    def small(n, name=None):
        return sbt([128, n], F32, name)

    cnt = [0]

    def el(eng, fn, r, w):
        P.op(eng, fn, reads=r, writes=w)

    def tt(outp, a, b, op, r, w, eng="dve"):
        el(eng, lambda e: e.tensor_tensor(out=outp, in0=a, in1=b, op=op), r, w)

    def ts(outp, a, s1, s2, op0, op1, r, w, eng="dve"):
        if op1 is None:
            el(eng, lambda e: e.tensor_scalar(out=outp, in0=a, scalar1=s1, scalar2=None, op0=op0), r, w)
        else:
            el(eng, lambda e: e.tensor_scalar(out=outp, in0=a, scalar1=s1, scalar2=s2, op0=op0, op1=op1), r, w)

    LR = small(32); LI = small(32); DT = small(32)
    for hf in range(2):
        sl = slice(64 * hf, 64 * hf + 64)
        P.dma(lambda e, sl=sl: e.dma_start(out=LR[sl, :], in_=lam_re.rearrange("g p -> p g"), **nonc), writes=["LR"])
        P.dma(lambda e, sl=sl: e.dma_start(out=LI[sl, :], in_=lam_im.rearrange("g p -> p g"), **nonc), writes=["LI"])
    P.dma(lambda e: e.dma_start(out=DT[:], in_=log_dt.partition_broadcast(128)), writes=["DT"])
    el("act", lambda e: e.activation(out=DT[:], in_=DT[:], func=AF.Exp), ["DT"], ["DT"])
    TH = small(32); MAG = small(32); t0 = small(32); t1 = small(32); t2 = small(32); kk = small(32)
    tt(TH[:], LI[:], DT[:], ALU.mult, ["LI", "DT"], ["TH"])
    tt(t0[:], LR[:], DT[:], ALU.mult, ["LR", "DT"], ["t0"])
    el("act", lambda e: e.activation(out=MAG[:], in_=t0[:], func=AF.Exp), ["t0"], ["MAG"])
    IMAG2 = small(32)
    el("act", lambda e: e.activation(out=IMAG2[:], in_=t0[:], func=AF.Exp, scale=-2.0), ["t0"], ["IMAG2"])

    def sin_of(dst, src, shift, nm):
        ts(t1[:], src, shift, None, ALU.add, None, [nm, "t1"], ["t1"])
        el("pool", lambda e: e.memset(kk[:], 0.0), [], ["kk"])
        for m in range(7):
            ts(t2[:], t1[:], (2 * m + 1) * PI, None, ALU.is_gt, None, ["t1"], ["t2"])
            tt(kk[:], kk[:], t2[:], ALU.add, ["kk", "t2"], ["kk"])
        ts(kk[:], kk[:], -2.0 * PI, None, ALU.mult, None, ["kk"], ["kk"])
        tt(t1[:], t1[:], kk[:], ALU.add, ["t1", "kk"], ["t1"])
        el("act", lambda e: e.activation(out=dst, in_=t1[:], func=AF.Sin), ["t1"], [nm + "_s"])

    SN = small(32); CS = small(32)
    sin_of(SN[:], TH[:], 0.0, "TH")
    sin_of(CS[:], TH[:], PI / 2.0, "TH")
    pwr = sbt([128, 9, 32], F32); pwi = sbt([128, 9, 32], F32); pnr = sbt([128, 8, 32], F32); pni = sbt([128, 8, 32], F32)
    el("pool", lambda e: e.memset(pwr[:, 0, :], 1.0), [], ["pw"]); el("pool", lambda e: e.memset(pwi[:, 0, :], 0.0), [], ["pw"])
    el("pool", lambda e: e.memset(pnr[:, 0, :], 1.0), [], ["pn"]); el("pool", lambda e: e.memset(pni[:, 0, :], 0.0), [], ["pn"])
    tt(pwr[:, 1, :], MAG[:], CS[:], ALU.mult, ["MAG", "TH_s"], ["pw"])
    tt(pwi[:, 1, :], MAG[:], SN[:], ALU.mult, ["MAG", "TH_s"], ["pw"])
    tt(pnr[:, 1, :], pwr[:, 1, :], IMAG2[:], ALU.mult, ["pw", "IMAG2"], ["pn"])
    tt(t0[:], pwi[:, 1, :], IMAG2[:], ALU.mult, ["pw", "IMAG2"], ["t0"])
    ts(pni[:, 1, :], t0[:], -1.0, None, ALU.mult, None, ["t0"], ["pn"])

    def cmul(or_, oi_, ar, ai, br, bi, r, w):
        raise NotImplementedError

    u0 = small(32); u1 = small(32)
    for k in range(1, 8):
        for (xr, xi, nm, lim) in ((pwr, pwi, "pw", 9), (pnr, pni, "pn", 8)):
            if k + 1 >= lim:
                continue
            tt(u0[:], xr[:, k, :], xr[:, 1, :], ALU.mult, [nm], ["u0"])
            tt(u1[:], xi[:, k, :], xi[:, 1, :], ALU.mult, [nm], ["u1"])
            tt(xr[:, k + 1, :], u0[:], u1[:], ALU.subtract, ["u0", "u1"], [nm])
            tt(u0[:], xr[:, k, :], xi[:, 1, :], ALU.mult, [nm], ["u0"])
            tt(u1[:], xi[:, k, :], xr[:, 1, :], ALU.mult, [nm], ["u1"])
            tt(xi[:, k + 1, :], u0[:], u1[:], ALU.add, ["u0", "u1"], [nm])
    den = small(32); qr = small(32); qi = small(32); nr = small(32)
    tt(u0[:], LR[:], LR[:], ALU.mult, ["LR"], ["u0"]); tt(u1[:], LI[:], LI[:], ALU.mult, ["LI"], ["u1"])
    tt(den[:], u0[:], u1[:], ALU.add, ["u0", "u1"], ["den"])
    el("dve", lambda e: e.reciprocal(out=den[:], in_=den[:]), ["den"], ["den"])
    ts(nr[:], pwr[:, 1, :], -1.0, None, ALU.add, None, ["pw"], ["nr"])
    tt(u0[:], nr[:], LR[:], ALU.mult, ["nr", "LR"], ["u0"]); tt(u1[:], pwi[:, 1, :], LI[:], ALU.mult, ["pw", "LI"], ["u1"])
    tt(qr[:], u0[:], u1[:], ALU.add, ["u0", "u1"], ["qr"]); tt(qr[:], qr[:], den[:], ALU.mult, ["qr", "den"], ["qr"])
    tt(u0[:], pwi[:, 1, :], LR[:], ALU.mult, ["pw", "LR"], ["u0"]); tt(u1[:], nr[:], LI[:], ALU.mult, ["nr", "LI"], ["u1"])
    tt(qi[:], u0[:], u1[:], ALU.subtract, ["u0", "u1"], ["qi"]); tt(qi[:], qi[:], den[:], ALU.mult, ["qi", "den"], ["qi"])
    Br = sbt([128, 32, 16], F32); Bi = sbt([128, 32, 16], F32); bbr = sbt([128, 32, 16], F32); bbi = sbt([128, 32, 16], F32)
    v0 = sbt([128, 32, 16], F32); v1 = sbt([128, 32, 16], F32)
    for hf in range(2):
        sl = slice(64 * hf, 64 * hf + 64)
        P.dma(lambda e, sl=sl: e.dma_start(out=Br[sl], in_=B_re.rearrange("g p n -> p g n"), **nonc), writes=["Br"])
        P.dma(lambda e, sl=sl: e.dma_start(out=Bi[sl], in_=B_im.rearrange("g p n -> p g n"), **nonc), writes=["Bi"])

    def bc(s):
        return s.unsqueeze(2).to_broadcast([128, 32, 16])

    def cmul3(orr, oii, sr, si, sn, xr, xi, xn, on):
        xn = [xn] if isinstance(xn, str) else list(xn)
        sn = [sn] if isinstance(sn, str) else list(sn)
        tt(v0[:], xr, bc(sr), ALU.mult, xn + sn, ["v0"]); tt(v1[:], xi, bc(si), ALU.mult, xn + sn, ["v1"])
        tt(orr, v0[:], v1[:], ALU.subtract, ["v0", "v1"], [on])
        tt(v0[:], xi, bc(sr), ALU.mult, xn + sn, ["v0"]); tt(v1[:], xr, bc(si), ALU.mult, xn + sn, ["v1"])
        tt(oii, v0[:], v1[:], ALU.add, ["v0", "v1"], [on])

    el("dve", lambda e: e.tensor_copy(out=u0[:], in_=qr[:]), ["qr"], ["qq"])
    cmul3(bbr[:], bbi[:], qr[:], qi[:], ["qr", "qi"], Br[:], Bi[:], ["Br", "Bi"], "bb")
    CTr = sbt([128, 32, 16], F32); CTi = sbt([128, 32, 16], F32)
    Cdup = sbt([128, 4, 2, 64], F32)
    for (Csrc, CTt, nm) in ((C_re, CTr, "CTr"), (C_im, CTi, "CTi")):
        for d in range(2):
            P.dma(lambda e, Csrc=Csrc, d=d: e.dma_start(out=Cdup[:, :, d, :], in_=Csrc.rearrange("(t g) n p -> (g n) t p", t=4)),
                  writes=["Cdup"])
        for t in range(4):
            P.op("pe", lambda e, t=t: e.transpose(out=pA[:, t * 128:(t + 1) * 128], in_=Cdup[:, t, :, :].rearrange("q d p -> q (d p)"),
                                                  identity=ident_f[:]), reads=["Cdup", "ident_f"], writes=["pA"])
        el("act", lambda e, CTt=CTt: e.copy(out=CTt[:].rearrange("p g n -> p (g n)"), in_=pA[:]), ["pA"], [nm])
    stage_some(100)
    Er = sbt([128, 32, 8, 16], F32); Ei = sbt([128, 32, 8, 16], F32)
    Fr = sbt([128, 32, 8, 16], F32); Fi = sbt([128, 32, 8, 16], F32)
    for j in range(8):
        cmul3(Er[:, :, j, :], Ei[:, :, j, :], pwr[:, 7 - j, :], pwi[:, 7 - j, :], "pw", bbr[:], bbi[:], "bb", "E")
        cmul3(Fr[:, :, j, :], Fi[:, :, j, :], pwr[:, j + 1, :], pwi[:, j + 1, :], "pw", CTr[:], CTi[:], ["CTr", "CTi"], "F")
    halfm = sbt([128, 2], F32)
    el("pool", lambda e: e.memset(halfm[:], 0.0), [], ["halfm"])
    el("pool", lambda e: e.memset(halfm[0:64, 0:1], 1.0), ["halfm"], ["halfm"])
    el("pool", lambda e: e.memset(halfm[64:128, 1:2], 1.0), ["halfm"], ["halfm"])
    for ri, (Et, blk) in enumerate(((Er, BLK_S5), (Ei, BLK_S5 + 1))):
        el("pool", lambda e: e.memset(stgb2[:], 0.0), ["stgb2"], ["stgb2"])
        for g in range(32):
            ps_ = pA if g % 2 == 0 else pB
            nm = "pA" if g % 2 == 0 else "pB"
            P.op("pe", lambda e, Et=Et, g=g, ps_=ps_: e.transpose(out=ps_[:, 0:64], in_=Et[0:64, g, :, :].rearrange("p j n -> p (j n)"),
                                                                  identity=ident_f[0:64, 0:64]), reads=["E", "ident_f"], writes=[nm])
            c0 = g * 128 + 64 * (g % 2)
            el("dve", lambda e, ps_=ps_, c0=c0: e.tensor_copy(out=stgb2[:, c0:c0 + 64], in_=ps_[:, 0:64]), [nm], ["stgb2"])
        P.dma(lambda e, blk=blk: e.dma_start(out=WS[blk], in_=stgb2[:]), reads=["stgb2"], writes=["WS%d" % blk], q="pool")
    for ri, (Ft, blk, sg) in enumerate(((Fr, BLK_S5 + 3, 1.0), (Fi, BLK_S5 + 4, -1.0))):
        for g in range(32):
            P.op("dve",
                 lambda e, Ft=Ft, g=g, sg=sg: e.tensor_scalar(out=stgb2[:, g * 128:(g + 1) * 128], in0=Ft[:, g, :, :].rearrange("p j n -> p (j n)"),
                                                              scalar1=halfm[:, (g % 2):(g % 2) + 1], scalar2=sg, op0=ALU.mult, op1=ALU.mult),
                 reads=["F", "halfm"], writes=["stgb2"])
        P.dma(lambda e, blk=blk: e.dma_start(out=WS[blk], in_=stgb2[:]), reads=["stgb2"], writes=["WS%d" % blk], q="pool")
    Gr = sbt([128, 32, 8, 16], F32); Gni = sbt([128, 32, 8, 16], F32)
    i8r = small(32); i8i = small(32)
    tt(u0[:], pnr[:, 7, :], pnr[:, 1, :], ALU.mult, ["pn"], ["u0"]); tt(u1[:], pni[:, 7, :], pni[:, 1, :], ALU.mult, ["pn"], ["u1"])
    tt(i8r[:], u0[:], u1[:], ALU.subtract, ["u0", "u1"], ["i8"])
    tt(u0[:], pnr[:, 7, :], pni[:, 1, :], ALU.mult, ["pn"], ["u0"]); tt(u1[:], pni[:, 7, :], pnr[:, 1, :], ALU.mult, ["pn"], ["u1"])
    tt(i8i[:], u0[:], u1[:], ALU.add, ["u0", "u1"], ["i8"])
    for j in range(8):
        cmul3(Gr[:, :, j, :], Gni[:, :, j, :], i8r[:], i8i[:], "i8", Fr[:, :, j, :], Fi[:, :, j, :], "F", "G")
    ts(Gni[:], Gni[:], -1.0, None, ALU.mult, None, ["G"], ["G"])
    bmask = sbt([128, 8, 16], F32); Dcol = sbt([128, 32], F32)
    el("pool", lambda e: e.memset(bmask[:], 1.0), [], ["bmask"])
    el("pool", lambda e: e.affine_select(out=bmask[:], in_=bmask[:], pattern=[[16, 8], [0, 16]], compare_op=ALU.is_ge, fill=0.0,
                                         base=15, channel_multiplier=-1), ["bmask"], ["bmask"])
    for j in range(8):
        P.dma(lambda e, j=j: e.dma_start(out=Dcol[16 * j:16 * j + 16, :], in_=D_skip.rearrange("g n -> n g"), **nonc), writes=["Dcol"], q="pool")
    wtmp = sbt([128, 128], F32)
    for g in range(32):
        ps_ = pA if g % 2 == 0 else pB
        nm = "pA" if g % 2 == 0 else "pB"
        P.op("pe", lambda e, g=g, ps_=ps_: e.matmul(ps_[:, 0:128], lhsT=Er[0:64, g, :, :].rearrange("p j n -> p (j n)"),
                                                    rhs=Gr[0:64, g, :, :].rearrange("p j n -> p (j n)"), start=True, stop=False),
             reads=["E", "G"], writes=[nm], sig=False)
        P.op("pe", lambda e, g=g, ps_=ps_: e.matmul(ps_[:, 0:128], lhsT=Ei[0:64, g, :, :].rearrange("p j n -> p (j n)"),
                                                    rhs=Gni[0:64, g, :, :].rearrange("p j n -> p (j n)"), start=False, stop=True),
             reads=["E", "G"], writes=[nm])
        tt(wtmp[:], ps_[:, 0:128], bmask[:].rearrange("p j n -> p (j n)"), ALU.mult, [nm, "bmask"], ["wtmp"])
        el("dve", lambda e, g=g: e.scalar_tensor_tensor(out=stgb2[:, g * 128:(g + 1) * 128], in0=ident_f[:], scalar=Dcol[:, g:g + 1],
                                                        in1=wtmp[:], op0=ALU.mult, op1=ALU.add), ["ident_f", "Dcol", "wtmp", "stgb2"], ["stgb2"])
    P.dma(lambda e: e.dma_start(out=WS[BLK_S5 + 2], in_=stgb2[:]), reads=["stgb2"], writes=["WS%d" % (BLK_S5 + 2)], q="pool")
    for g2 in range(2):
        sl = slice(64 * g2, 64 * g2 + 64)
        src_r = pwr[sl, 8, :].rearrange("p (q t) -> p q t", t=2)[:, :, g2]
        src_i = pwi[sl, 8, :].rearrange("p (q t) -> p q t", t=2)[:, :, g2]
        el("dve", lambda e, sl=sl, src_r=src_r: e.tensor_copy(out=AR32[sl, 0, :], in_=src_r), ["pw"], ["AR32"])
        el("dve", lambda e, sl=sl, src_r=src_r: e.tensor_copy(out=AR32[sl, 1, :], in_=src_r), ["pw"], ["AR32"])
        el("dve", lambda e, sl=sl, src_i=src_i: e.tensor_copy(out=API[sl, :], in_=src_i), ["pw"], ["API"])
        ts(ANI[sl, :], src_i, -1.0, None, ALU.mult, None, ["pw"], ["ANI"])

    stage_some(100)
    if dbg_stop == 1:
        tk = P.dma(lambda e: e.dma_start(out=out[0:128, :], in_=gpost[:]), reads=["gpost"])
        P.final_wait("sp", [tk])
        P.emit(); P.temp.close(); P.close()
        return nc
    P.emit()
    P.temp.close()
    P.temp = None
    NSLOT = 3
    wslot = [sbt([128, 4096], BF16, "wslot%d" % i) for i in range(NSLOT)]
    slot_rr = [0]

    def load_block(blk):
        s = slot_rr[0] % NSLOT
        slot_rr[0] += 1
        P.dma(lambda e: e.dma_start(out=wslot[s][:], in_=WS[blk]), reads=["WS%d" % blk], writes=["wslot%d" % s])
        return wslot[s], "wslot%d" % s

    xs = [sbt([128, 1024], F32, "xs%d" % i) for i in range(2)]
    xr = [sbt([128, 1024], F32, "xr%d" % i) for i in range(2)]
    junk = sbt([128, 1024], BF16); xnb = sbt([128, 1024], BF16)
    ss = small(1); rstd = small(1)
    xnT = sbt([128, 8, T], BF16, "xnT"); uaT = sbt([128, 8, T + 4], BF16, "uaT"); cT = sbt([128, 8, T], BF16, "cT")
    zaT = sbt([128, 8, T], BF16, "zaT"); oaT = sbt([128, 8, T], BF16, "oaT"); zbT = sbt([128, 4, T], BF16, "zbT")
    qT = sbt([128, 4, 2, T], BF16, "qT"); kT = sbt([128, 4, 2, T], BF16, "kT")
    ktm = sbt([128, NCH, 4, 256], BF16, "ktm"); vw = sbt([128, NCH, 4, 264], BF16, "vw")
    gates = sbt([128, NCH, 8], F32, "gates")
    hfT = sbt([128, 8, T], BF16, "hfT"); csz = sbt([128, 8, T], BF16, "csz"); mrgT = zaT; hbT = sbt([128, 4, T], BF16, "hbT")
    Cst = sbt([128, 4, 2, 264], F32, "Cst"); Cbf = sbt([128, 4, 2, 264], BF16, "Cbf")
    nst = sbt([128, 4, 2], F32, "nst"); nrep = sbt([128, 4, 2, 128], BF16, "nrep")
    Utm = sbt([128, 32, 8, 16], BF16, "Utm"); U2 = sbt([128, 32, NC8], BF16, "U2")
    Xall = sbt([128, NC8, 2, 16], F32, "Xall"); Sall = sbt([128, NC8 + 1, 2, 16], F32, "Sall"); Sbf = sbt([128, 2, 16, NC8], BF16, "Sbf")
    Yg = sbt([128, 32, NC8], BF16, "Yg"); Ytm = Utm[:].rearrange("p g j n -> p (g j n)").rearrange("p (j c) -> p j c", j=8); yT = sbt([128, 4, T], BF16, "yT")
    for (t_, nm) in ((Cst, "Cst"), (Cbf, "Cbf"), (nst, "nst"), (nrep, "nrep"), (Sall, "Sall"), (uaT, "uaT"), (vw, "vw")):
        flat = t_[:]
        wnames = [nm] + (["%s%d" % (nm, h_) for h_ in range(4)] if nm in ("Cst", "Cbf", "nrep") else [])
        el("pool", lambda e, flat=flat: e.memset(flat, 0.0), [], wnames)

    e1a = sbt([128, NCH, 4], F32); lfa = sbt([128, NCH, 4], F32); emb4 = sbt([128, NCH, 4, 128], F32); dec4 = sbt([128, NCH, 4], F32)
    wcol4 = sbt([128, NCH, 4], F32); wcolb4 = sbt([128, NCH, 4, 2], BF16); wrep4 = sbt([128, NCH, 4, 128], BF16)
    lfrep = sbt([128, 4, 128], F32); lf = small(4, "lf"); ig = small(4); bcol = small(4); wcol = small(4, "wcol"); wcolb = sbt([128, 4, 2], BF16)
    wrep = sbt([128, 4, 128], BF16); emb = sbt([128, 4, 128], F32, "emb"); dec = small(4, "dec"); SmT = sbt([128, 4, 128], BF16, "SmT")
    aden = sbt([128, 4, 128], F32); rden = sbt([128, 4, 128], F32, "rden"); hT = sbt([128, 8, 128], F32, "hT"); sq = sbt([128, 8, 128], BF16)
    rsh = sbt([128, 4, 128], F32); hn = sbt([128, 8, 128], F32, "hn"); e1 = small(4)
    gAll = oaT; m1 = sbt([128, T], F32); m2 = sbt([128, T], F32)
    ot = [sbt([128, 1024], F32, "ot%d" % i) for i in range(2)]
    ss2 = small(1); rstd2 = small(1); ss3 = small(1); rstd3 = small(1); sgl = sbt([128, T], BF16); xg = sbt([128, T], F32)

    def rsqrt_col(dst, src, scale, nm_src, nm_dst):
        ts(dst, src, scale, 1e-6, ALU.mult, ALU.add, [nm_src], [nm_dst])
        el("act", lambda e: e.activation(out=dst, in_=dst, func=AF.Ln), [nm_dst], [nm_dst])
        el("act", lambda e: e.activation(out=dst, in_=dst, func=AF.Exp, scale=-0.5), [nm_dst], [nm_dst])

    def mm(outp, lhsT, rhs, start, stop, r, w, sig=None):
        P.op("pe", lambda e: e.matmul(outp, lhsT=lhsT, rhs=rhs, start=start, stop=stop), reads=r, writes=w,
             sig=(stop if sig is None else sig))

    acc_rr = [0]

    def acc_bank():
        acc_rr[0] += 1
        return (pA, "pA") if acc_rr[0] % 2 else (pB, "pB")

    out_toks = []

    class StopBuild(Exception):
        pass

    def chk(level):
        if dbg_stop == level:
            raise StopBuild()

    prefetched = [False]

    def do_tile(xsrc, row0, main, ti, nxt=None):
        def issue_x(src_, r0, sub):
            xb2, xn2 = xs[sub % 2], "xs%d" % (sub % 2)
            P.dma(lambda e: e.dma_start(out=xb2[:], in_=src_[r0 + sub * 128: r0 + (sub + 1) * 128, :]), writes=[xn2])

        for sub in range(NCH):
            xb_, xnm = xs[sub % 2], "xs%d" % (sub % 2)
            if not (prefetched[0] and sub < 2):
                issue_x(xsrc, row0, sub)
            el("act", lambda e, xb_=xb_: e.activation(out=junk[:], in_=xb_[:], func=AF.Square, accum_out=ss[:]), [xnm], ["junk", "ss"])
            rsqrt_col(rstd[:], ss[:], 1.0 / 1024.0, "ss", "rstd")
            ts(xnb[:], xb_[:], rstd[:, 0:1], None, ALU.mult, None, [xnm, "rstd"], ["xnb"])
            for kt in range(8):
                P.op("pe", lambda e, kt=kt: e.transpose(out=pT[:, kt * 128:(kt + 1) * 128], in_=xnb[:, kt * 128:(kt + 1) * 128], identity=ident_b[:]),
                     reads=["xnb", "ident_b"], writes=["pT"], sig=(kt == 7))
            el("act", lambda e, sub=sub: e.copy(out=xnT[:, :, sub * 128:(sub + 1) * 128], in_=pT[:].rearrange("p (k t) -> p k t", k=8)),
               ["pT"], ["xnT"])

        prefetched[0] = False
        if nxt is not None and NCH <= 2:
            for sub in range(min(2, NCH)):
                issue_x(nxt[0], nxt[1], sub)
            prefetched[0] = True
        chk(2)

        def proj_fm(blk, ncolt, evac):
            W, wn = load_block(blk)
            Wv_ = W[:].rearrange("p (k n) -> p k n", k=8)
            for ct in range(ncolt):
                ps_, pn = acc_bank()
                for kt in range(8):
                    mm(ps_[:, 0:T], Wv_[:, kt, ct * 128:(ct + 1) * 128], xnT[:, kt, :], kt == 0, kt == 7, [wn, "xnT"], [pn])
                evac(ct, ps_, pn)

        el("dve", lambda e: e.tensor_copy(out=uaT[:, :, 1:4], in_=uaT[:, :, T + 1:T + 4]), ["uaT"], ["uaT"])
        for half in range(2):
            proj_fm(BLK_UA + half, 4, lambda ct, ps_, pn, half=half: el(
                "act", lambda e: e.copy(out=uaT[:, half * 4 + ct, 4:T + 4], in_=ps_[:, 0:T]), [pn], ["uaT"]))
        chk(3)
        for ch in range(NCH):
            for kt in range(8):
                mm(pM[:, 0:8], xnT[:, kt, ch * 128:(ch + 1) * 128], Wif[:, kt, :], kt == 0, kt == 7, ["xnT", "Wif"], ["pM"])
            tt(gates[:, ch, :], pM[:, 0:8], bif[:], ALU.add, ["pM", "bif"], ["gates"])
        chk(4)
        W, wn = load_block(BLK_UB)
        Wv_ = W[:].rearrange("p (k n) -> p k n", k=8)
        for j in range(8):
            ps_, pn = acc_bank()
            for kt in range(8):
                lhs = xnT[:, kt, :].rearrange("p (c j) -> p j c", j=8)[:, j, :]
                mm(ps_[0:NC8, :], lhs, Wv_[:, kt, :], kt == 0, kt == 7, [wn, "xnT"], [pn])
            el("act" if j % 2 else "dve",
               (lambda e, ps_=ps_, j=j: e.copy(out=Utm[0:NC8, :, j, :], in_=ps_[0:NC8, :].rearrange("c (g n) -> c g n", g=32))) if j % 2 else
               (lambda e, ps_=ps_, j=j: e.tensor_copy(out=Utm[0:NC8, :, j, :], in_=ps_[0:NC8, :].rearrange("c (g n) -> c g n", g=32))), [pn], ["Utm"])
        chk(5)
        for g in range(32):
            P.op("pe", lambda e, g=g: e.transpose(out=pT[:, g * NC8:(g + 1) * NC8], in_=Utm[0:NC8, g, :, :].rearrange("c j n -> c (j n)"),
                                                  identity=ident_b[0:NC8, 0:NC8]), reads=["Utm", "ident_b"], writes=["pT"], sig=(g == 31))
        el("act", lambda e: e.copy(out=U2[:].rearrange("p g c -> p (g c)"), in_=pT[:, 0:32 * NC8]), ["pT"], ["U2"])
        W1r, w1rn = load_block(BLK_S5)
        W1i, w1in = load_block(BLK_S5 + 1)
        for ri, (Wt, wn_) in enumerate(((W1r, w1rn), (W1i, w1in))):
            Wg = Wt[:].rearrange("p (g n) -> p g n", g=32)
            for q in range(16):
                for g2 in range(2):
                    g = 2 * q + g2
                    mm(pM[:, (q * NC8):(q + 1) * NC8], Wg[:, g, :], U2[:, g, :], g2 == 0, g2 == 1, [wn_, "U2"], ["pM"], sig=(q == 15 and g2 == 1))
            el("act" if ri == 0 else "dve",
               (lambda e, ri=ri: e.copy(out=Xall[:, :, ri, :].rearrange("p c q -> p q c"), in_=pM[:, 0:16 * NC8].rearrange("p (q c) -> p q c", q=16))) if ri == 0 else
               (lambda e, ri=ri: e.tensor_copy(out=Xall[:, :, ri, :].rearrange("p c q -> p q c"), in_=pM[:, 0:16 * NC8].rearrange("p (q c) -> p q c", q=16))),
               ["pM"], ["Xall"])
        chk(6)
        T1 = sbt([128, 2, 16], F32, "scT1_%d" % ti) if False else None
        for c in range(NC8):
            sp_ = Sall[:, c, :, :]
            sn_ = Sall[:, c + 1, :, :]
            P.op("pool", lambda e, sp_=sp_: e.tensor_tensor(out=sc_t1[:], in0=sp_, in1=AR32[:], op=ALU.mult), reads=["Sall"], writes=["sc_t1"])
            P.op("pool", lambda e, c=c: e.tensor_tensor(out=sc_t2[:, 0, :], in0=Sall[:, c, 1, :], in1=ANI[:], op=ALU.mult), reads=["Sall"], writes=["sc_t2"])
            P.op("pool", lambda e, c=c: e.tensor_tensor(out=sc_t2[:, 1, :], in0=Sall[:, c, 0, :], in1=API[:], op=ALU.mult), reads=["Sall"], writes=["sc_t2"])
            P.op("pool", lambda e, c=c: e.tensor_tensor(out=sc_t1[:], in0=sc_t1[:], in1=Xall[:, c, :, :], op=ALU.add), reads=["sc_t1", "Xall"], writes=["sc_t1"])
            P.op("pool", lambda e, sn_=sn_: e.tensor_tensor(out=sn_, in0=sc_t1[:], in1=sc_t2[:], op=ALU.add), reads=["sc_t1", "sc_t2"], writes=["Sall"])
        if main:
            for ri in range(2):
                el("pool", lambda e, ri=ri: e.tensor_copy(out=Sbf[:, ri, :, :], in_=Sall[:, 0:NC8, ri, :].rearrange("p c q -> p q c")),
                   ["Sall"], ["Sbf"])
        el("pool", lambda e: e.tensor_copy(out=Sall[:, 0, :, :], in_=Sall[:, NC8, :, :]), ["Sall"], ["Sall"])
        def s5_out():
            proj_fm(BLK_ZB, 4, lambda ct, ps_, pn: el("act", lambda e: e.activation(out=zbT[:, ct, :], in_=ps_[:, 0:T], func=AF.Silu), [pn], ["zbT"]))
            Wi_, win_ = load_block(BLK_S5 + 2)
            Wr_, wrn_ = load_block(BLK_S5 + 3)
            Wm_, wmn_ = load_block(BLK_S5 + 4)
            Wi_g = Wi_[:].rearrange("p (g n) -> p g n", g=32); Wr_g = Wr_[:].rearrange("p (g n) -> p g n", g=32)
            Wm_g = Wm_[:].rearrange("p (g n) -> p g n", g=32)
            GP = 512 // NC8
            for g0 in range(0, 32, GP):
                ng = min(GP, 32 - g0)
                for gi in range(ng):
                    g = g0 + gi
                    o_ = pM[:, gi * NC8:(gi + 1) * NC8]
                    mm(o_, Wi_g[:, g, :], U2[:, g, :], True, False, [win_, "U2"], ["pM"], sig=False)
                    mm(o_, Wr_g[:, g, :], Sbf[:, 0, g // 2, :], False, False, [wrn_, "Sbf"], ["pM"], sig=False)
                    mm(o_, Wm_g[:, g, :], Sbf[:, 1, g // 2, :], False, True, [wmn_, "Sbf"], ["pM"], sig=(gi == ng - 1))
                el("act", lambda e, g0=g0, ng=ng: e.activation(out=Yg[:, g0:g0 + ng, :].rearrange("p g c -> p (g c)"), in_=pM[:, 0:ng * NC8],
                                                               func=AF.Gelu_apprx_tanh), ["pM"], ["Yg"])
            for g0 in range(0, 32, 8):
                for gi in range(8):
                    g = g0 + gi
                    P.op("pe", lambda e, g=g, gi=gi: e.transpose(out=pT[0:NC8, gi * 128:(gi + 1) * 128], in_=Yg[:, g, :], identity=ident_b[:]),
                         reads=["Yg", "ident_b"], writes=["pT"], sig=(gi == 7))
                el("act", lambda e, g0=g0: e.copy(out=Ytm[0:NC8, :, 16 * g0:16 * g0 + 128].rearrange("c j (g n) -> c g j n", g=8),
                                                  in_=pT[0:NC8, :].rearrange("c (g j n) -> c g j n", g=8, j=8)), ["pT"], ["Utm"])
            for ct in range(4):
                for j in range(8):
                    P.op("pe", lambda e, ct=ct, j=j: e.transpose(out=pT[:, j * NC8:(j + 1) * NC8], in_=Ytm[0:NC8, j, ct * 128:(ct + 1) * 128],
                                                                 identity=ident_b[0:NC8, 0:NC8]), reads=["Utm", "ident_b"], writes=["pT"], sig=(j == 7))
                el("act", lambda e, ct=ct: e.copy(out=yT[:, ct, :], in_=pT[:, 0:T]), ["pT"], ["yT"])
            for ot_ in range(4):
                ps_, pn = acc_bank()
                for ct in range(4):
                    mm(ps_[:, 0:T], Wglu[:, ct, ot_ * 128:(ot_ + 1) * 128], yT[:, ct, :], ct == 0, ct == 3, ["Wglu", "yT"], [pn])
                el("act", lambda e, ps_=ps_, ot_=ot_: e.activation(out=sgl[:], in_=ps_[:, 0:T], func=AF.Sigmoid, bias=bglu[:, ot_:ot_ + 1]),
                   [pn, "bglu"], ["sgl"])
                tt(xg[:], sgl[:], yT[:, ot_, :], ALU.mult, ["sgl", "yT"], ["xg"])
                tt(hbT[:, ot_, :].rearrange("p (j c) -> p j c", j=8), xg[:].rearrange("p (j c) -> p j c", j=8),
                   zbT[:, ot_, :].rearrange("p (c j) -> p j c", j=8), ALU.mult, ["xg", "zbT"], ["hbT"])
        chk(7)
        for mt in range(8):
            ps_, pn = acc_bank()
            for k in range(4):
                mm(ps_[:, 0:T], cdiag[:, mt, k, :], uaT[:, mt, k + 1:k + 1 + T], k == 0, k == 3, ["cdiag", "uaT"], [pn])
            el("act", lambda e, ps_=ps_, mt=mt: e.activation(out=cT[:, mt, :], in_=ps_[:, 0:T], func=AF.Silu, bias=cb[:, mt:mt + 1]),
               [pn, "cb"], ["cT"])
        if main:
            for half in range(2):
                proj_fm(BLK_ZA + half, 4, lambda ct, ps_, pn, half=half: el(
                    "act", lambda e: e.activation(out=zaT[:, half * 4 + ct, :], in_=ps_[:, 0:T], func=AF.Silu), [pn], ["zaT"]))
            for half in range(2):
                proj_fm(BLK_OA + half, 4, lambda ct, ps_, pn, half=half: el(
                    "act", lambda e: e.activation(out=oaT[:, half * 4 + ct, :], in_=ps_[:, 0:T], func=AF.Sigmoid), [pn], ["oaT"]))
        if main:
            for mt in range(8):
                el("dve", lambda e, mt=mt: e.scalar_tensor_tensor(out=csz[:, mt, :], in0=cT[:, mt, :], scalar=skp[:, mt:mt + 1], in1=zaT[:, mt, :],
                                                                  op0=ALU.mult, op1=ALU.mult), ["cT", "skp", "zaT"], ["csz"])
            for mt in range(8):
                ts(zaT[:, mt, :], zaT[:, mt, :], hg[:, mt:mt + 1], None, ALU.mult, None, ["zaT", "hg", "csz"], ["zaT"])
        chk(8)
        for h in range(4):
            if main:
                for (Wx, wxn, dst, dn) in ((Wq, "Wqkv0", qT, "qT"), (Wk, "Wqkv1", kT, "kT")):
                    for et in range(2):
                        ps_, pn = acc_bank()
                        for d in range(2):
                            mm(ps_[:, 0:T], Wx[:, h, d, et * 128:(et + 1) * 128], cT[:, 2 * h + d, :], d == 0, d == 1, [wxn, "cT"], [pn])
                        el("act" if et else "dve",
                           (lambda e, ps_=ps_, dst=dst, h=h, et=et: e.copy(out=dst[:, h, et, :], in_=ps_[:, 0:T])) if et else
                           (lambda e, ps_=ps_, dst=dst, h=h, et=et: e.tensor_copy(out=dst[:, h, et, :], in_=ps_[:, 0:T])), [pn], [dn])
        el("act", lambda e: e.activation(out=e1a[:], in_=gates[:, :, 4:8], func=AF.Exp, scale=-1.0), ["gates"], ["e1a"])
        el("act", lambda e: e.activation(out=lfa[:], in_=e1a[:], func=AF.Ln, bias=1.0), ["e1a"], ["lfa"])
        ts(lfa[:], lfa[:], -1.0, None, ALU.mult, None, ["lfa"], ["lfa"])
        for ch in range(NCH):
            tsl = slice(ch * 128, (ch + 1) * 128)
            for h in range(4):
                el("dve", lambda e, h=h, ch=ch: e.tensor_copy(out=lfrep[:, h, :], in_=lfa[:, ch, h:h + 1].to_broadcast([128, 128])), ["lfa"], ["lfrep"])
            for h in range(4):
                mm(pGD[:, h * 128:(h + 1) * 128], lfrep[:, h, :], maskT[:], True, True, ["lfrep", "maskT"], ["pGD"], sig=(h == 3))
            for h in range(4):
                mm(pM[:, h * 128:(h + 1) * 128], maskT[:], lfrep[:, h, :], True, True, ["maskT", "lfrep"], ["pM"], sig=(h == 3))
            en, dn, wn_, wbn, wrn = "emb%d" % ch, "dec%d" % ch, "wcol%d" % ch, "wcolb%d" % ch, "wrep%d" % ch
            el("act", lambda e, ch=ch: e.activation(out=emb4[:, ch, :, :].rearrange("p h t -> p (h t)"), in_=pGD[:], func=AF.Exp, scale=-1.0), ["pGD"], [en])
            el("dve", lambda e, ch=ch: e.reciprocal(out=dec4[:, ch, :], in_=emb4[:, ch, :, 127]), [en], [dn])
            tt(bcol[:], gates[:, ch, 0:4], pM[:].rearrange("p (h t) -> p h t", h=4)[:, :, 0], ALU.subtract, ["gates", "pM"], ["bcol"])
            el("act", lambda e, ch=ch: e.activation(out=wcol4[:, ch, :], in_=bcol[:], func=AF.Exp), ["bcol"], [wn_])
            el("dve", lambda e, ch=ch: e.tensor_copy(out=vw[:, ch, :, 256], in_=wcol4[:, ch, :]), [wn_], ["vw"])
            if main:
                for h in range(4):
                    el("dve", lambda e, h=h, ch=ch: e.tensor_copy(out=wrep4[:, ch, h, :], in_=wcol4[:, ch, h:h + 1].to_broadcast([128, 128])), [wn_], [wrn])
            for h in range(4):
                ps_, pn = acc_bank()
                for d in range(2):
                    mm(ps_[:, 0:256], cT[:, 2 * h + d, tsl], Wk[:, h, d, :], d == 0, d == 1, ["cT", "Wqkv1"], [pn], sig=False)
                for d in range(2):
                    mm(ps_[:, 256:512], uaT[:, 2 * h + d, 4 + ch * 128:4 + (ch + 1) * 128], Wv[:, h, d, :], d == 0, d == 1, ["uaT", "Wqkv2"], [pn])
                el("act", lambda e, ps_=ps_, ch=ch, h=h: e.copy(out=ktm[:, ch, h, :], in_=ps_[:, 0:256]), [pn], ["ktm"])
                el("act", lambda e, ps_=ps_, ch=ch, h=h: e.activation(out=vw[:, ch, h, 0:256], in_=ps_[:, 256:512], func=AF.Copy, scale=wcol4[:, ch, h:h + 1]),
                   [pn, wn_], ["vw"])
        for ch in range(NCH):
            tsl = slice(ch * 128, (ch + 1) * 128)
            en, dn, wn_, wbn, wrn = "emb%d" % ch, "dec%d" % ch, "wcol%d" % ch, "wcolb%d" % ch, "wrep%d" % ch
            if main:
                for h in range(4):
                    for et in range(2):
                        mm(pS[:, h * 128:(h + 1) * 128], kT[:, h, et, tsl], qT[:, h, et, tsl], et == 0, et == 1, ["kT", "qT"], ["pS"], sig=(h == 3 and et == 1))
                tt(SmT[:].rearrange("p h t -> p (h t)"), pS[:], mask4[:].rearrange("p h t -> p (h t)"), ALU.mult, ["pS", "mask4"], ["SmT"])
                for h in range(4):
                    for d2 in range(2):
                        pn_, pnn = (pN0, "pN0") if h < 2 else (pN1, "pN1")
                        o_ = pn_[:, ((h % 2) * 2 + d2) * 128:((h % 2) * 2 + d2 + 1) * 128]
                        mm(o_, vw[:, ch, h, d2 * 128:(d2 + 1) * 128], SmT[:, h, :], True, False, ["vw", "SmT"], [pnn], sig=False)
                        mm(o_, Cbf[:, h, 0, d2 * 128:(d2 + 1) * 128], qT[:, h, 0, tsl], False, False, ["Cbf%d" % h, "qT"], [pnn], sig=False)
                        mm(o_, Cbf[:, h, 1, d2 * 128:(d2 + 1) * 128], qT[:, h, 1, tsl], False, True, ["Cbf%d" % h, "qT"], [pnn], sig=(h % 2 == 1 and d2 == 1))
                    o_ = pGD[:, h * 128:(h + 1) * 128]
                    mm(o_, wrep4[:, ch, h, :], SmT[:, h, :], True, False, [wrn, "SmT"], ["pGD"], sig=False)
                    mm(o_, nrep[:, h, 0, :], qT[:, h, 0, tsl], False, False, ["nrep%d" % h, "qT"], ["pGD"], sig=False)
                    mm(o_, nrep[:, h, 1, :], qT[:, h, 1, tsl], False, True, ["nrep%d" % h, "qT"], ["pGD"], sig=(h == 3))
            if main:
                el("act", lambda e: e.activation(out=aden[:].rearrange("p h t -> p (h t)"), in_=pGD[:], func=AF.Abs), ["pGD"], ["aden"])
                tt(aden[:], aden[:], emb4[:, ch, :, :], ALU.max, ["aden", en], ["aden"])
                el("act", lambda e: e.activation(out=aden[:], in_=aden[:], func=AF.Ln), ["aden"], ["aden"])
                el("act", lambda e: e.activation(out=rden[:], in_=aden[:], func=AF.Exp, scale=-1.0), ["aden"], ["rden"])
            for h in range(4):
                cn, bn, nn = "Cst%d" % h, "Cbf%d" % h, "nrep%d" % h
                for et in range(2):
                    ps_, pn = acc_bank()
                    mm(ps_[:, 0:258], ktm[:, ch, h, et * 128:(et + 1) * 128], vw[:, ch, h, 0:258], True, True, ["ktm", "vw"], [pn])
                    tt(Cst[:, h, et, 0:258], Cst[:, h, et, 0:258], ps_[:, 0:258], ALU.add, [cn, pn, bn, nn], [cn])
            for h in range(4):
                cn, bn, nn = "Cst%d" % h, "Cbf%d" % h, "nrep%d" % h
                el("act", lambda e, h=h, ch=ch: e.activation(out=Cst[:, h, :, :], in_=Cst[:, h, :, :], func=AF.Copy, scale=dec4[:, ch, h:h + 1]), [cn, dn], [cn])
                el("act", lambda e, h=h: e.copy(out=Cbf[:, h, :, :], in_=Cst[:, h, :, :]), [cn], [bn])
            for h in range(4):
                cn, bn, nn = "Cst%d" % h, "Cbf%d" % h, "nrep%d" % h
                for et in range(2):
                    el("dve", lambda e, h=h, et=et: e.tensor_copy(out=nrep[:, h, et, :], in_=Cst[:, h, et, 256:257].to_broadcast([128, 128])),
                       [cn], [nn])
            if main:
                for hp, (pn_, pnn) in enumerate(((pN0, "pN0"), (pN1, "pN1"))):
                    tt(hT[:, 4 * hp:4 * hp + 4, :].rearrange("p (h d) t -> p h d t", h=2), pn_[:].rearrange("p (h d t) -> p h d t", h=2, d=2),
                       rden[:, 2 * hp:2 * hp + 2, :].unsqueeze(2).to_broadcast([128, 2, 2, 128]), ALU.mult, [pnn, "rden"], ["hT"])
                tt(hT[:], hT[:], oaT[:, :, tsl], ALU.mult, ["hT", "oaT"], ["hT"])
                el("act", lambda e: e.activation(out=sq[:], in_=hT[:], func=AF.Square), ["hT"], ["sq"])
                for h in range(4):
                    for d2 in range(2):
                        mm(pS[:, h * 128:(h + 1) * 128], ones_b[:], sq[:, 2 * h + d2, :], d2 == 0, d2 == 1, ["ones_b", "sq", "SmT"], ["pS"], sig=(h == 3 and d2 == 1))
                ts(rsh[:].rearrange("p h t -> p (h t)"), pS[:], 1.0 / 256.0, 1e-6, ALU.mult, ALU.add, ["pS"], ["rsh"])
                el("act", lambda e: e.activation(out=rsh[:], in_=rsh[:], func=AF.Ln), ["rsh"], ["rsh"])
                el("act", lambda e: e.activation(out=rsh[:], in_=rsh[:], func=AF.Exp, scale=-0.5), ["rsh"], ["rsh"])
                tt(hn[:].rearrange("p (h d) t -> p h d t", h=4), hT[:].rearrange("p (h d) t -> p h d t", h=4),
                   rsh[:].unsqueeze(2).to_broadcast([128, 4, 2, 128]), ALU.mult, ["hT", "rsh"], ["hn"])
                tt(hn[:], hn[:], zaT[:, :, tsl], ALU.mult, ["hn", "zaT"], ["hn"])
                tt(hfT[:, :, tsl], hn[:], csz[:, :, tsl], ALU.add, ["hn", "csz"], ["hfT"])
        chk(9)
        if not main:
            return
        s5_out()
        def gate_blocks(br):
            for bi_ in range(2):
                Wg_, wgn = load_block(BLK_G + 2 * br + bi_)
                Wgv = Wg_[:].rearrange("p (k n) -> p k n", k=8)
                for c4 in range(4):
                    ft = bi_ * 4 + c4
                    psg, png = acc_bank()
                    for kt in range(8):
                        mm(psg[:, 0:T], Wgv[:, kt, c4 * 128:(c4 + 1) * 128], xnT[:, kt, :], kt == 0, kt == 7, [wgn, "xnT"], [png])
                    el("act", lambda e, psg=psg, ft=ft: e.activation(out=gAll[:, ft, :], in_=psg[:, 0:T], func=AF.Sigmoid), [png], ["oaT"])

        gate_blocks(0)
        for hf_ in range(2):
            Wa, wan = load_block(BLK_AO + hf_)
            Wav = Wa[:].rearrange("p (k n) -> p k n", k=8)
            for c4 in range(4):
                ft = hf_ * 4 + c4
                psa, pna = acc_bank()
                for kt in range(8):
                    mm(psa[:, 0:T], Wav[:, kt, c4 * 128:(c4 + 1) * 128], hfT[:, kt, :], kt == 0, kt == 7, [wan, "hfT"], [pna])
                tt(mrgT[:, ft, :], psa[:, 0:T], gAll[:, ft, :], ALU.mult, [pna, "oaT"], ["zaT"])
        gate_blocks(1)
        Wbo, wbon = load_block(BLK_BO)
        Wbv = Wbo[:].rearrange("p (k n) -> p k n", k=4)
        for ft in range(8):
            psb, pnb = acc_bank()
            for kt in range(4):
                mm(psb[:, 0:T], Wbv[:, kt, ft * 128:(ft + 1) * 128], hbT[:, kt, :], kt == 0, kt == 3, [wbon, "hbT"], [pnb])
            el("act", lambda e, psb=psb: e.copy(out=m2[:].rearrange("p (c j) -> p j c", j=8), in_=psb[:, 0:T].rearrange("p (j c) -> p j c", j=8)),
               [pnb], ["m2"])
            tt(m2[:], m2[:], gAll[:, ft, :], ALU.mult, ["m2", "oaT"], ["m2"])
            tt(mrgT[:, ft, :], m2[:], mrgT[:, ft, :], ALU.add, ["m2", "zaT"], ["zaT"])
        Wo0, wo0n = load_block(BLK_WO); Wo1, wo1n = load_block(BLK_WO + 1)
        for ch in range(NCH):
            xb_, xnm = xr[ch % 2], "xr%d" % (ch % 2)
            if ch < 2:
                P.dma(lambda e, xb_=xb_, ch=ch: e.dma_start(out=xb_[:], in_=xsrc[row0 + ch * 128: row0 + (ch + 1) * 128, :]), writes=[xnm])
        for ch in range(NCH):
            tsl = slice(ch * 128, (ch + 1) * 128)
            bk = ((pN0, "pN0"), (pN1, "pN1")) if ch % 2 == 0 else ((pS, "pS"), (pGD, "pGD"))
            ssA, ssAn = (ss2, "ss2") if ch % 2 == 0 else (ss3, "ss3")
            ssB, ssBn = (rstd2, "rstd2") if ch % 2 == 0 else (rstd3, "rstd3")
            for hf_, (Wo_, won) in enumerate(((Wo0, wo0n), (Wo1, wo1n))):
                pn_, pnn = bk[hf_]
                Wov = Wo_[:].rearrange("p (k n) -> p k n", k=8)
                for kt in range(8):
                    mm(pn_[:, :], mrgT[:, kt, tsl], Wov[:, kt, :], kt == 0, kt == 7, ["zaT", won], [pnn])
            jk, jkn = (junk, "junk") if ch % 2 == 0 else (xnb, "xnb")
            el("act", lambda e, bk=bk, ssA=ssA, jk=jk: e.activation(out=jk[:, 0:512], in_=bk[0][0][:], func=AF.Square, accum_out=ssA[:]), [bk[0][1]], [jkn, ssAn])
            el("act", lambda e, bk=bk, ssB=ssB, jk=jk: e.activation(out=jk[:, 512:1024], in_=bk[1][0][:], func=AF.Square, accum_out=ssB[:]), [bk[1][1]], [jkn, ssBn])
            tt(ssA[:], ssA[:], ssB[:], ALU.add, [ssAn, ssBn], [ssAn])
            rsqrt_col(ssB[:], ssA[:], 1.0 / 1024.0, ssAn, ssBn)
            ob, obn = ot[ch % 2], "ot%d" % (ch % 2)
            xb_, xnm = xr[ch % 2], "xr%d" % (ch % 2)
            if ch >= 2:
                P.dma(lambda e, xb_=xb_, ch=ch: e.dma_start(out=xb_[:], in_=xsrc[row0 + ch * 128: row0 + (ch + 1) * 128, :]), writes=[xnm])
            for hf_ in range(2):
                pn_, pnn = bk[hf_]
                cs = slice(hf_ * 512, hf_ * 512 + 512)
                el("dve", lambda e, ob=ob, pn_=pn_, cs=cs, ssB=ssB: e.scalar_tensor_tensor(out=ob[:, cs], in0=pn_[:], scalar=ssB[:, 0:1], in1=gpost[:, cs],
                                                                                          op0=ALU.mult, op1=ALU.mult), [pnn, ssBn, "gpost"], [obn])
            tt(ob[:], ob[:], xb_[:], ALU.add, [obn, xnm], [obn])
            tok = P.dma(lambda e, ob=ob, ch=ch: e.dma_start(out=out[row0 + ch * 128: row0 + (ch + 1) * 128, :], in_=ob[:]), reads=[obn])
            out_toks.append(tok)

    sc_t1 = sbt([128, 2, 16], F32, "sc_t1"); sc_t2 = sbt([128, 2, 16], F32, "sc_t2")
    try:
        for ti in range(NPRE):
            nxt = (x_pre, (ti + 1) * T) if ti + 1 < NPRE else (x_main, 0)
            do_tile(x_pre, ti * T, False, ti, nxt)
    except StopBuild:
        tk = P.dma(lambda e: e.dma_start(out=out[0:128, :], in_=gpost[:]), reads=["gpost"])
        P.final_wait("sp", [tk])
        P.emit(); P.close()
        return nc
    if NPRE > 0:
        for h in range(4):
            cn, bn, nn = "Cst%d" % h, "Cbf%d" % h, "nrep%d" % h
            ts(Cst[:, h, :, :], Cst[:, h, :, :], flg[:, 0:1], None, ALU.mult, None, [cn, "flg"], [cn])
            el("act", lambda e, h=h: e.copy(out=Cbf[:, h, :, :], in_=Cst[:, h, :, :]), [cn], [bn])
            for et in range(2):
                el("dve", lambda e, h=h, et=et: e.tensor_copy(out=nrep[:, h, et, :], in_=Cst[:, h, et, 256:257].to_broadcast([128, 128])),
                   [cn], [nn])
    for ti in range(NMAIN):
        nxt = (x_main, (ti + 1) * T) if ti + 1 < NMAIN else None
        do_tile(x_main, ti * T, True, NPRE + ti, nxt)
    P.final_wait("sp", out_toks)
    P.emit()
    P.close()
    return nc


T_TILE = 256
_cache = {}


def kernel(**inputs):
    x = np.ascontiguousarray(inputs["x"], dtype=np.float32)
    Bsz, L, Dm = x.shape
    half = L // 2
    npre = half // T_TILE
    nmain = half // T_TILE
    key = (T_TILE, npre, nmain)
    if key not in _cache:
        _cache[key] = build_program(T_TILE, npre, nmain)
    nc = _cache[key]
    shared = {}
    for k, v in inputs.items():
        if k == "x":
            continue
        a = np.ascontiguousarray(np.asarray(v, dtype=np.float32)[0])
        if k in ("b_i", "b_f", "log_dt", "norm_post_g"):
            a = a.reshape(1, -1)
        shared[k] = a
    in_maps = []
    zeros = np.zeros((half, Dm), np.float32)
    for core in range(8):
        b, hf = core // 2, core % 2
        m = dict(shared)
        m["x_main"] = np.ascontiguousarray(x[b, hf * half:(hf + 1) * half])
        m["x_pre"] = zeros if hf == 0 else np.ascontiguousarray(x[b, 0:half])
        m["flag"] = np.full((128, 1), float(hf), np.float32)
        in_maps.append(m)
    res = run_bass_kernel_spmd(nc, in_maps, core_ids=list(range(8)))
    outp = np.empty((Bsz, L, Dm), np.float32)
    for core in range(8):
        b, hf = core // 2, core % 2
        outp[b, hf * half:(hf + 1) * half] = res.results[core]["out"]
    return outp
```

```python
import contextlib
import numpy as np
import concourse.bass as bass
import concourse.mybir as mybir
from concourse.bass_utils import run_bass_kernel_spmd

F32 = mybir.dt.float32
BF16 = mybir.dt.bfloat16
AF = mybir.ActivationFunctionType
ALU = mybir.AluOpType
COMPUTE = ("pe", "act", "dve", "pool")
NDMA_SLOTS = 8
DEBUG_TAGS = False
PI = float(np.pi)


class Prog:
    def __init__(self, nc):
        self.nc = nc
        self.stack = contextlib.ExitStack()
        self.engs = ("pe", "act", "dve", "pool", "sp")
        self.ops = {e: [] for e in self.engs}
        self.waited = {e: {} for e in self.engs}
        self.res = {}
        self.dma_use = {}
        self.dma_rr = {e: 0 for e in self.engs}
        self.sems = {}
        self.base = {e: 0 for e in COMPUTE}
        self.temp = None

    def sb(self, name, shape, dt):
        st = self.temp if self.temp is not None else self.stack
        return st.enter_context(self.nc.sbuf_tensor(name, list(shape), dt))

    def ps(self, name, shape, dt):
        return self.stack.enter_context(self.nc.psum_tensor(name, list(shape), dt))

    def _sem(self, key):
        if key not in self.sems:
            nm = "s_" + "_".join(str(k) for k in (key if isinstance(key, tuple) else (key,)))
            self.sems[key] = self.stack.enter_context(self.nc.semaphore(nm))
        return self.sems[key]

    def _deps(self, eng, reads, writes):
        deps = {}

        def add(tok):
            if tok is None:
                return
            k, v = tok
            if k == "pe" and eng == "pe":
                return
            if deps.get(k, -1) < v:
                deps[k] = v

        for r in reads:
            st = self.res.get(r)
            if st:
                for k, v in st[0].items():
                    add((k, v))
        for w in writes:
            st = self.res.get(w)
            if st:
                for k, v in st[0].items():
                    add((k, v))
                for k, v in st[1].items():
                    add((k, v))
        out = []
        wd = self.waited[eng]
        for k, v in deps.items():
            if wd.get(k, -1) >= v:
                continue
            wd[k] = v
            out.append((k, v))
        return out

    def _commit(self, tok, reads, writes):
        k, v = tok
        for r in reads:
            st = self.res.setdefault(r, [{}, {}])
            if st[1].get(k, -1) < v:
                st[1][k] = v
        for w in writes:
            old = self.res.get(w)
            wr = {}
            if old is not None and k not in COMPUTE:
                wr = {k2: v2 for k2, v2 in old[0].items() if k2 not in COMPUTE}
            wr[k] = v
            self.res[w] = [wr, {}]

    def _tag(self):
        if not DEBUG_TAGS:
            return None
        import sys as _sys
        f = _sys._getframe(2)
        while f is not None and f.f_code.co_name not in ("do_tile", "build_program", "s5_out", "gate_blocks", "proj_fm"):
            f = f.f_back
        return str(f.f_lineno) if f is not None else None

    def op(self, eng, fn, reads=(), writes=(), sig=True):
        waits = self._deps(eng, reads, writes)
        idx = len(self.ops[eng])
        self.ops[eng].append(dict(fn=fn, waits=waits, sig=sig, dma=None, tag=self._tag()))
        tok = (eng, idx)
        self._commit(tok, reads, writes)
        return tok

    def dma(self, fn, reads=(), writes=(), q="sp"):
        waits = self._deps(q, reads, writes)
        slot = self.dma_rr[q] % NDMA_SLOTS
        self.dma_rr[q] += 1
        key = ("d", q, slot)
        n = self.dma_use.get(key, 0)
        if n > 0:
            prev = n * 16
            if self.waited[q].get(key, -1) < prev:
                self.waited[q][key] = prev
                waits.append((key, prev))
        self.dma_use[key] = n + 1
        tok = (key, (n + 1) * 16)
        self.ops[q].append(dict(fn=fn, waits=waits, sig=False, dma=key))
        self._commit(tok, reads, writes)
        return tok

    def final_wait(self, eng, toks):
        self.ops[eng].append(dict(fn=None, waits=list(toks), sig=False, dma=None))

    def emit(self):
        nc = self.nc
        sigcount = {}
        totals = {}
        for e in COMPUTE:
            c = self.base[e]
            arr = []
            for o in self.ops[e]:
                if o["sig"]:
                    c += 1
                arr.append(c)
            need = [None] * len(arr)
            nxt = None
            for i in range(len(arr) - 1, -1, -1):
                if self.ops[e][i]["sig"]:
                    nxt = arr[i]
                need[i] = nxt
            sigcount[e] = need
            totals[e] = c
            self._sem(e)
        for k in self.dma_use:
            self._sem(k)

        def resolve(k, v):
            if k in COMPUTE:
                val = sigcount[k][v]
                assert val is not None, (k, v)
                return self.sems[k], val
            return self.sems[k], v

        with nc.Block() as block:

            def run(eng_name, eng):
                for o in self.ops[eng_name]:
                    for k, v in o["waits"]:
                        s, val = resolve(k, v)
                        eng.wait_ge(s, val)
                    if o["fn"] is None:
                        continue
                    ins = o["fn"](eng)
                    if o.get("tag"):
                        ins.annotate(o["tag"])
                    if o["dma"] is not None:
                        ins.then_inc(self.sems[o["dma"]], 16)
                    elif o["sig"]:
                        ins.then_inc(self.sems[eng_name], 1)
                for o2 in COMPUTE:
                    if o2 != eng_name and totals[o2] > 0:
                        eng.wait_ge(self.sems[o2], totals[o2])
                for k, n in self.dma_use.items():
                    eng.wait_ge(self.sems[k], n * 16)

            @block.tensor
            def _(e):
                run("pe", e)

            @block.scalar
            def _(e):
                run("act", e)

            @block.vector
            def _(e):
                run("dve", e)

            @block.gpsimd
            def _(e):
                run("pool", e)

            @block.sync
            def _(e):
                run("sp", e)

        self.base = totals
        self.ops = {e: [] for e in self.engs}
        self.waited = {e: {} for e in self.engs}
        self.res = {}

    def close(self):
        self.stack.close()


NBLK = 22
BLK_UA, BLK_UB, BLK_ZB, BLK_ZA, BLK_OA, BLK_G, BLK_AO, BLK_BO, BLK_WO, BLK_S5 = 0, 2, 3, 4, 6, 8, 12, 14, 15, 17
COL_UA, COL_ZA, COL_OA, COL_I, COL_UB, COL_ZB, COL_G = 0, 1024, 2048, 3072, 3080, 3592, 4104


def build_program(T, NPRE, NMAIN, dbg_stop=0):
    NCH = T // 128
    NC8 = T // 8
    nc = bass.Bass("TRN2", target_bir_lowering=False)
    dram = {}

    def din(name, shape):
        dram[name] = nc.dram_tensor(name, list(shape), F32, kind="ExternalInput").ap()
        return dram[name]

    x_pre = din("x_pre", [max(NPRE, 1) * T, 1024])
    x_main = din("x_main", [NMAIN * T, 1024])
    flag = din("flag", [128, 1])
    norm_pre_g = din("norm_pre_g", [1024]); w_in = din("w_in", [1024, 6152])
    conv_w = din("conv_w", [4, 1024]); conv_b = din("conv_b", [1024])
    w_q = din("w_q", [4, 256, 256]); w_k = din("w_k", [4, 256, 256]); w_v = din("w_v", [4, 256, 256])
    b_i = din("b_i", [1, 4]); b_f = din("b_f", [1, 4]); head_g = din("head_g", [1024]); skip_a = din("skip_a", [1024])
    w_a_out = din("w_a_out", [1024, 1024])
    lam_re = din("lam_re", [32, 64]); lam_im = din("lam_im", [32, 64]); log_dt = din("log_dt", [1, 32])
    B_re = din("B_re", [32, 64, 16]); B_im = din("B_im", [32, 64, 16])
    C_re = din("C_re", [32, 16, 64]); C_im = din("C_im", [32, 16, 64]); D_skip = din("D_skip", [32, 16])
    w_glu = din("w_glu", [512, 512]); b_glu = din("b_glu", [512]); w_b_out = din("w_b_out", [512, 1024])
    w_o = din("w_o", [1024, 1024]); norm_post_g = din("norm_post_g", [1, 1024])
    out = nc.dram_tensor("out", [NMAIN * T, 1024], F32, kind="ExternalOutput").ap()
    WS = nc.dram_tensor("wscratch", [NBLK, 128, 4096], BF16, kind="Internal").ap()

    P = Prog(nc)
    uid = [0]

    def sbt(shape, dt, name=None):
        uid[0] += 1
        return P.sb(name or ("t%d" % uid[0]), shape, dt)

    ident_f = sbt([128, 128], F32); ident_b = sbt([128, 128], BF16)
    maskT = sbt([128, 128], F32); mask4 = sbt([128, 4, 128], F32); ones_b = sbt([128, 128], BF16)
    P.op("pool", lambda e: e.memset(ident_f[:], 1.0), writes=["ident_f"])
    P.op("pool", lambda e: e.affine_select(out=ident_f[:], in_=ident_f[:], pattern=[[-1, 128]], compare_op=ALU.is_equal,
                                           fill=0.0, base=0, channel_multiplier=1), reads=["ident_f"], writes=["ident_f"])
    P.op("dve", lambda e: e.tensor_copy(out=ident_b[:], in_=ident_f[:]), reads=["ident_f"], writes=["ident_b"])
    P.op("pool", lambda e: e.memset(maskT[:], 1.0), writes=["maskT"])
    P.op("pool", lambda e: e.affine_select(out=maskT[:], in_=maskT[:], pattern=[[1, 128]], compare_op=ALU.is_ge,
                                           fill=0.0, base=0, channel_multiplier=-1), reads=["maskT"], writes=["maskT"])
    for h in range(4):
        P.op("pool", lambda e, h=h: e.tensor_copy(out=mask4[:, h, :], in_=maskT[:]), reads=["maskT"], writes=["mask4"])
    P.op("pool", lambda e: e.memset(ones_b[:], 1.0), writes=["ones_b"])

    gpre = sbt([128, 8], F32); cb = sbt([128, 8], F32); hg = sbt([128, 8], F32); skp = sbt([128, 8], F32)
    cw = sbt([128, 8, 4], F32); bglu = sbt([128, 4], F32); gpost = sbt([128, 1024], F32); bif = sbt([128, 8], F32)
    flg = sbt([128, 1], F32)
    nonc = dict(allow_slow_non_contiguous=True)
    P.dma(lambda e: e.dma_start(out=gpre[:], in_=norm_pre_g.rearrange("(k p) -> p k", p=128), **nonc), writes=["gpre"])
    P.dma(lambda e: e.dma_start(out=cb[:], in_=conv_b.rearrange("(k p) -> p k", p=128), **nonc), writes=["cb"])
    P.dma(lambda e: e.dma_start(out=hg[:], in_=head_g.rearrange("(k p) -> p k", p=128), **nonc), writes=["hg"])
    P.dma(lambda e: e.dma_start(out=skp[:], in_=skip_a.rearrange("(k p) -> p k", p=128), **nonc), writes=["skp"])
    for k in range(4):
        P.dma(lambda e, k=k: e.dma_start(out=cw[:, :, k], in_=conv_w[k].rearrange("(m p) -> p m", p=128), **nonc), writes=["cw"])
    P.dma(lambda e: e.dma_start(out=bglu[:], in_=b_glu.rearrange("(k p) -> p k", p=128), **nonc), writes=["bglu"])
    P.dma(lambda e: e.dma_start(out=gpost[:], in_=norm_post_g.partition_broadcast(128)), writes=["gpost"])
    P.dma(lambda e: e.dma_start(out=bif[:, 0:4], in_=b_i.partition_broadcast(128)), writes=["bif"])
    P.dma(lambda e: e.dma_start(out=bif[:, 4:8], in_=b_f.partition_broadcast(128)), writes=["bif"])
    P.dma(lambda e: e.dma_start(out=flg[:], in_=flag), writes=["flg"])

    Wqkv = [sbt([128, 4, 2, 256], BF16) for _ in range(3)]
    Wif = sbt([128, 8, 8], BF16); Wglu = sbt([128, 4, 512], BF16); cdiag = sbt([128, 8, 4, 128], BF16)
    AR32 = sbt([128, 2, 16], F32); ANI = sbt([128, 16], F32); API = sbt([128, 16], F32)
    pA = P.ps("pA", [128, 512], F32); pB = P.ps("pB", [128, 512], F32)
    pT = P.ps("pT", [128, 1024], BF16); pGD = P.ps("pGD", [128, 512], F32)
    pS = P.ps("pS", [128, 512], F32); pN0 = P.ps("pN0", [128, 512], F32); pN1 = P.ps("pN1", [128, 512], F32)
    pM = P.ps("pM", [128, 512], F32)
    P.temp = contextlib.ExitStack()
    stg = sbt([128, 4096], F32, "stg")
    stgb = sbt([128, 4096], BF16, "stgb")
    stgb2 = sbt([128, 4096], BF16, "stgb2")
    for wi, (wsrc, scl) in enumerate(((w_q, 1.0), (w_k, 1.0 / 16.0), (w_v, 1.0))):
        P.dma(lambda e, wsrc=wsrc: e.dma_start(out=stg[:, 0:2048].rearrange("p (h d n) -> p h d n", h=4, d=2),
                                               in_=wsrc.rearrange("h (d p) n -> p h d n", p=128)), writes=["stg"])
        P.op("dve", lambda e, wi=wi, scl=scl: e.tensor_scalar(out=Wqkv[wi][:].rearrange("p h d n -> p (h d n)"), in0=stg[:, 0:2048],
                                                              scalar1=scl, scalar2=None, op0=ALU.mult), reads=["stg"], writes=["Wqkv%d" % wi])
    Wq, Wk, Wv = Wqkv
    P.dma(lambda e: e.dma_start(out=stg[:, 0:64].rearrange("p (k n) -> p k n", k=8),
                                in_=w_in[:, COL_I:COL_I + 8].rearrange("(k p) n -> p k n", p=128), **nonc), writes=["stg"])
    for kt in range(8):
        P.op("dve", lambda e, kt=kt: e.tensor_scalar(out=Wif[:, kt, :], in0=stg[:, kt * 8:(kt + 1) * 8], scalar1=gpre[:, kt:kt + 1],
                                                     scalar2=None, op0=ALU.mult), reads=["stg", "gpre"], writes=["Wif"])
    P.dma(lambda e: e.dma_start(out=stg[:, 0:2048].rearrange("p (k n) -> p k n", k=4),
                                in_=w_glu.rearrange("(k p) n -> p k n", p=128)), writes=["stg"])
    P.op("dve", lambda e: e.tensor_copy(out=Wglu[:].rearrange("p k n -> p (k n)"), in_=stg[:, 0:2048]), reads=["stg"], writes=["Wglu"])
    for mt in range(8):
        for k in range(4):
            P.op("dve", lambda e, mt=mt, k=k: e.tensor_scalar(out=cdiag[:, mt, k, :], in0=ident_f[:], scalar1=cw[:, mt, k:k + 1],
                                                               scalar2=None, op0=ALU.mult), reads=["ident_f", "cw"], writes=["cdiag"])

    def stage_block(blk, src_ap_f, scale_gpre, nk):
        ncol = 4096 // nk
        P.dma(lambda e: e.dma_start(out=stg[:].rearrange("p (k n) -> p k n", k=nk), in_=src_ap_f), writes=["stg"])
        if scale_gpre:
            for kt in range(nk):
                P.op("act",
                     lambda e, kt=kt: e.activation(out=stgb[:, kt * ncol:(kt + 1) * ncol], in_=stg[:, kt * ncol:(kt + 1) * ncol],
                                                   func=AF.Copy, scale=gpre[:, kt:kt + 1]),
                     reads=["stg", "gpre"], writes=["stgb"])
        else:
            P.op("act", lambda e: e.copy(out=stgb[:, 0:2048], in_=stg[:, 0:2048]), reads=["stg"], writes=["stgb"])
            P.op("act", lambda e: e.copy(out=stgb[:, 2048:4096], in_=stg[:, 2048:4096]), reads=["stg"], writes=["stgb"])
        P.dma(lambda e: e.dma_start(out=WS[blk], in_=stgb[:]), reads=["stgb"], writes=["WS%d" % blk])

    def win_cols(c0):
        return w_in[:, c0:c0 + 512].rearrange("(k p) n -> p k n", p=128)

    win_blocks = [(BLK_UA, COL_UA), (BLK_UA + 1, COL_UA + 512), (BLK_UB, COL_UB), (BLK_ZB, COL_ZB), (BLK_ZA, COL_ZA),
                  (BLK_ZA + 1, COL_ZA + 512), (BLK_OA, COL_OA), (BLK_OA + 1, COL_OA + 512)] + [(BLK_G + i, COL_G + 512 * i) for i in range(4)]
    pending = []
    for blk, c0 in win_blocks:
        pending.append((blk, win_cols(c0), True, 8))
    for i in range(2):
        pending.append((BLK_AO + i, w_a_out[:, 512 * i:512 * i + 512].rearrange("(k p) n -> p k n", p=128), False, 8))
        pending.append((BLK_WO + i, w_o[:, 512 * i:512 * i + 512].rearrange("(k p) n -> p k n", p=128), False, 8))
    pending.append((BLK_BO, w_b_out.rearrange("(k p) n -> p k n", p=128), False, 4))

    def stage_some(n=1):
        for _ in range(n):
            if pending:
                stage_block(*pending.pop(0))


    def small(n, name=None):
        return sbt([128, n], F32, name)

    cnt = [0]

    def el(eng, fn, r, w):
        P.op(eng, fn, reads=r, writes=w)

    def tt(outp, a, b, op, r, w, eng="dve"):
        el(eng, lambda e: e.tensor_tensor(out=outp, in0=a, in1=b, op=op), r, w)

    def ts(outp, a, s1, s2, op0, op1, r, w, eng="dve"):
        if op1 is None:
            el(eng, lambda e: e.tensor_scalar(out=outp, in0=a, scalar1=s1, scalar2=None, op0=op0), r, w)
        else:
            el(eng, lambda e: e.tensor_scalar(out=outp, in0=a, scalar1=s1, scalar2=s2, op0=op0, op1=op1), r, w)

    LR = small(32); LI = small(32); DT = small(32)
    for hf in range(2):
        sl = slice(64 * hf, 64 * hf + 64)
        P.dma(lambda e, sl=sl: e.dma_start(out=LR[sl, :], in_=lam_re.rearrange("g p -> p g"), **nonc), writes=["LR"])
        P.dma(lambda e, sl=sl: e.dma_start(out=LI[sl, :], in_=lam_im.rearrange("g p -> p g"), **nonc), writes=["LI"])
    P.dma(lambda e: e.dma_start(out=DT[:], in_=log_dt.partition_broadcast(128)), writes=["DT"])
    el("act", lambda e: e.activation(out=DT[:], in_=DT[:], func=AF.Exp), ["DT"], ["DT"])
    TH = small(32); MAG = small(32); t0 = small(32); t1 = small(32); t2 = small(32); kk = small(32)
    tt(TH[:], LI[:], DT[:], ALU.mult, ["LI", "DT"], ["TH"])
    tt(t0[:], LR[:], DT[:], ALU.mult, ["LR", "DT"], ["t0"])
    el("act", lambda e: e.activation(out=MAG[:], in_=t0[:], func=AF.Exp), ["t0"], ["MAG"])
    IMAG2 = small(32)
    el("act", lambda e: e.activation(out=IMAG2[:], in_=t0[:], func=AF.Exp, scale=-2.0), ["t0"], ["IMAG2"])

    def sin_of(dst, src, shift, nm):
        ts(t1[:], src, shift, None, ALU.add, None, [nm, "t1"], ["t1"])
        el("pool", lambda e: e.memset(kk[:], 0.0), [], ["kk"])
        for m in range(7):
            ts(t2[:], t1[:], (2 * m + 1) * PI, None, ALU.is_gt, None, ["t1"], ["t2"])
            tt(kk[:], kk[:], t2[:], ALU.add, ["kk", "t2"], ["kk"])
        ts(kk[:], kk[:], -2.0 * PI, None, ALU.mult, None, ["kk"], ["kk"])
        tt(t1[:], t1[:], kk[:], ALU.add, ["t1", "kk"], ["t1"])
        el("act", lambda e: e.activation(out=dst, in_=t1[:], func=AF.Sin), ["t1"], [nm + "_s"])

    SN = small(32); CS = small(32)
    sin_of(SN[:], TH[:], 0.0, "TH")
    sin_of(CS[:], TH[:], PI / 2.0, "TH")
    pwr = sbt([128, 9, 32], F32); pwi = sbt([128, 9, 32], F32); pnr = sbt([128, 8, 32], F32); pni = sbt([128, 8, 32], F32)
    el("pool", lambda e: e.memset(pwr[:, 0, :], 1.0), [], ["pw"]); el("pool", lambda e: e.memset(pwi[:, 0, :], 0.0), [], ["pw"])
    el("pool", lambda e: e.memset(pnr[:, 0, :], 1.0), [], ["pn"]); el("pool", lambda e: e.memset(pni[:, 0, :], 0.0), [], ["pn"])
    tt(pwr[:, 1, :], MAG[:], CS[:], ALU.mult, ["MAG", "TH_s"], ["pw"])
    tt(pwi[:, 1, :], MAG[:], SN[:], ALU.mult, ["MAG", "TH_s"], ["pw"])
    tt(pnr[:, 1, :], pwr[:, 1, :], IMAG2[:], ALU.mult, ["pw", "IMAG2"], ["pn"])
    tt(t0[:], pwi[:, 1, :], IMAG2[:], ALU.mult, ["pw", "IMAG2"], ["t0"])
    ts(pni[:, 1, :], t0[:], -1.0, None, ALU.mult, None, ["t0"], ["pn"])

    def cmul(or_, oi_, ar, ai, br, bi, r, w):
        raise NotImplementedError

    u0 = small(32); u1 = small(32)
    for k in range(1, 8):
        for (xr, xi, nm, lim) in ((pwr, pwi, "pw", 9), (pnr, pni, "pn", 8)):
            if k + 1 >= lim:
                continue
            tt(u0[:], xr[:, k, :], xr[:, 1, :], ALU.mult, [nm], ["u0"])
            tt(u1[:], xi[:, k, :], xi[:, 1, :], ALU.mult, [nm], ["u1"])
            tt(xr[:, k + 1, :], u0[:], u1[:], ALU.subtract, ["u0", "u1"], [nm])
            tt(u0[:], xr[:, k, :], xi[:, 1, :], ALU.mult, [nm], ["u0"])
            tt(u1[:], xi[:, k, :], xr[:, 1, :], ALU.mult, [nm], ["u1"])
            tt(xi[:, k + 1, :], u0[:], u1[:], ALU.add, ["u0", "u1"], [nm])
    den = small(32); qr = small(32); qi = small(32); nr = small(32)
    tt(u0[:], LR[:], LR[:], ALU.mult, ["LR"], ["u0"]); tt(u1[:], LI[:], LI[:], ALU.mult, ["LI"], ["u1"])
    tt(den[:], u0[:], u1[:], ALU.add, ["u0", "u1"], ["den"])
    el("dve", lambda e: e.reciprocal(out=den[:], in_=den[:]), ["den"], ["den"])
    ts(nr[:], pwr[:, 1, :], -1.0, None, ALU.add, None, ["pw"], ["nr"])
    tt(u0[:], nr[:], LR[:], ALU.mult, ["nr", "LR"], ["u0"]); tt(u1[:], pwi[:, 1, :], LI[:], ALU.mult, ["pw", "LI"], ["u1"])
    tt(qr[:], u0[:], u1[:], ALU.add, ["u0", "u1"], ["qr"]); tt(qr[:], qr[:], den[:], ALU.mult, ["qr", "den"], ["qr"])
    tt(u0[:], pwi[:, 1, :], LR[:], ALU.mult, ["pw", "LR"], ["u0"]); tt(u1[:], nr[:], LI[:], ALU.mult, ["nr", "LI"], ["u1"])
    tt(qi[:], u0[:], u1[:], ALU.subtract, ["u0", "u1"], ["qi"]); tt(qi[:], qi[:], den[:], ALU.mult, ["qi", "den"], ["qi"])
    Br = sbt([128, 32, 16], F32); Bi = sbt([128, 32, 16], F32); bbr = sbt([128, 32, 16], F32); bbi = sbt([128, 32, 16], F32)
    v0 = sbt([128, 32, 16], F32); v1 = sbt([128, 32, 16], F32)
    for hf in range(2):
        sl = slice(64 * hf, 64 * hf + 64)
        P.dma(lambda e, sl=sl: e.dma_start(out=Br[sl], in_=B_re.rearrange("g p n -> p g n"), **nonc), writes=["Br"])
        P.dma(lambda e, sl=sl: e.dma_start(out=Bi[sl], in_=B_im.rearrange("g p n -> p g n"), **nonc), writes=["Bi"])

    def bc(s):
        return s.unsqueeze(2).to_broadcast([128, 32, 16])

    def cmul3(orr, oii, sr, si, sn, xr, xi, xn, on):
        xn = [xn] if isinstance(xn, str) else list(xn)
        sn = [sn] if isinstance(sn, str) else list(sn)
        tt(v0[:], xr, bc(sr), ALU.mult, xn + sn, ["v0"]); tt(v1[:], xi, bc(si), ALU.mult, xn + sn, ["v1"])
        tt(orr, v0[:], v1[:], ALU.subtract, ["v0", "v1"], [on])
        tt(v0[:], xi, bc(sr), ALU.mult, xn + sn, ["v0"]); tt(v1[:], xr, bc(si), ALU.mult, xn + sn, ["v1"])
        tt(oii, v0[:], v1[:], ALU.add, ["v0", "v1"], [on])

    el("dve", lambda e: e.tensor_copy(out=u0[:], in_=qr[:]), ["qr"], ["qq"])
    cmul3(bbr[:], bbi[:], qr[:], qi[:], ["qr", "qi"], Br[:], Bi[:], ["Br", "Bi"], "bb")
    CTr = sbt([128, 32, 16], F32); CTi = sbt([128, 32, 16], F32)
    Cdup = sbt([128, 4, 2, 64], F32)
    for (Csrc, CTt, nm) in ((C_re, CTr, "CTr"), (C_im, CTi, "CTi")):
        for d in range(2):
            P.dma(lambda e, Csrc=Csrc, d=d: e.dma_start(out=Cdup[:, :, d, :], in_=Csrc.rearrange("(t g) n p -> (g n) t p", t=4)),
                  writes=["Cdup"])
        for t in range(4):
            P.op("pe", lambda e, t=t: e.transpose(out=pA[:, t * 128:(t + 1) * 128], in_=Cdup[:, t, :, :].rearrange("q d p -> q (d p)"),
                                                  identity=ident_f[:]), reads=["Cdup", "ident_f"], writes=["pA"])
        el("act", lambda e, CTt=CTt: e.copy(out=CTt[:].rearrange("p g n -> p (g n)"), in_=pA[:]), ["pA"], [nm])
    stage_some(100)
    Er = sbt([128, 32, 8, 16], F32); Ei = sbt([128, 32, 8, 16], F32)
    Fr = sbt([128, 32, 8, 16], F32); Fi = sbt([128, 32, 8, 16], F32)
    for j in range(8):
        cmul3(Er[:, :, j, :], Ei[:, :, j, :], pwr[:, 7 - j, :], pwi[:, 7 - j, :], "pw", bbr[:], bbi[:], "bb", "E")
        cmul3(Fr[:, :, j, :], Fi[:, :, j, :], pwr[:, j + 1, :], pwi[:, j + 1, :], "pw", CTr[:], CTi[:], ["CTr", "CTi"], "F")
    halfm = sbt([128, 2], F32)
    el("pool", lambda e: e.memset(halfm[:], 0.0), [], ["halfm"])
    el("pool", lambda e: e.memset(halfm[0:64, 0:1], 1.0), ["halfm"], ["halfm"])
    el("pool", lambda e: e.memset(halfm[64:128, 1:2], 1.0), ["halfm"], ["halfm"])
    for ri, (Et, blk) in enumerate(((Er, BLK_S5), (Ei, BLK_S5 + 1))):
        el("pool", lambda e: e.memset(stgb2[:], 0.0), ["stgb2"], ["stgb2"])
        for g in range(32):
            ps_ = pA if g % 2 == 0 else pB
            nm = "pA" if g % 2 == 0 else "pB"
            P.op("pe", lambda e, Et=Et, g=g, ps_=ps_: e.transpose(out=ps_[:, 0:64], in_=Et[0:64, g, :, :].rearrange("p j n -> p (j n)"),
                                                                  identity=ident_f[0:64, 0:64]), reads=["E", "ident_f"], writes=[nm])
            c0 = g * 128 + 64 * (g % 2)
            el("dve", lambda e, ps_=ps_, c0=c0: e.tensor_copy(out=stgb2[:, c0:c0 + 64], in_=ps_[:, 0:64]), [nm], ["stgb2"])
        P.dma(lambda e, blk=blk: e.dma_start(out=WS[blk], in_=stgb2[:]), reads=["stgb2"], writes=["WS%d" % blk], q="pool")
    for ri, (Ft, blk, sg) in enumerate(((Fr, BLK_S5 + 3, 1.0), (Fi, BLK_S5 + 4, -1.0))):
        for g in range(32):
            P.op("dve",
                 lambda e, Ft=Ft, g=g, sg=sg: e.tensor_scalar(out=stgb2[:, g * 128:(g + 1) * 128], in0=Ft[:, g, :, :].rearrange("p j n -> p (j n)"),
                                                              scalar1=halfm[:, (g % 2):(g % 2) + 1], scalar2=sg, op0=ALU.mult, op1=ALU.mult),
                 reads=["F", "halfm"], writes=["stgb2"])
        P.dma(lambda e, blk=blk: e.dma_start(out=WS[blk], in_=stgb2[:]), reads=["stgb2"], writes=["WS%d" % blk], q="pool")
    Gr = sbt([128, 32, 8, 16], F32); Gni = sbt([128, 32, 8, 16], F32)
    i8r = small(32); i8i = small(32)
    tt(u0[:], pnr[:, 7, :], pnr[:, 1, :], ALU.mult, ["pn"], ["u0"]); tt(u1[:], pni[:, 7, :], pni[:, 1, :], ALU.mult, ["pn"], ["u1"])
    tt(i8r[:], u0[:], u1[:], ALU.subtract, ["u0", "u1"], ["i8"])
    tt(u0[:], pnr[:, 7, :], pni[:, 1, :], ALU.mult, ["pn"], ["u0"]); tt(u1[:], pni[:, 7, :], pnr[:, 1, :], ALU.mult, ["pn"], ["u1"])
    tt(i8i[:], u0[:], u1[:], ALU.add, ["u0", "u1"], ["i8"])
    for j in range(8):
        cmul3(Gr[:, :, j, :], Gni[:, :, j, :], i8r[:], i8i[:], "i8", Fr[:, :, j, :], Fi[:, :, j, :], "F", "G")
    ts(Gni[:], Gni[:], -1.0, None, ALU.mult, None, ["G"], ["G"])
    bmask = sbt([128, 8, 16], F32); Dcol = sbt([128, 32], F32)
    el("pool", lambda e: e.memset(bmask[:], 1.0), [], ["bmask"])
    el("pool", lambda e: e.affine_select(out=bmask[:], in_=bmask[:], pattern=[[16, 8], [0, 16]], compare_op=ALU.is_ge, fill=0.0,
                                         base=15, channel_multiplier=-1), ["bmask"], ["bmask"])
    for j in range(8):
        P.dma(lambda e, j=j: e.dma_start(out=Dcol[16 * j:16 * j + 16, :], in_=D_skip.rearrange("g n -> n g"), **nonc), writes=["Dcol"], q="pool")
    wtmp = sbt([128, 128], F32)
    for g in range(32):
        ps_ = pA if g % 2 == 0 else pB
        nm = "pA" if g % 2 == 0 else "pB"
        P.op("pe", lambda e, g=g, ps_=ps_: e.matmul(ps_[:, 0:128], lhsT=Er[0:64, g, :, :].rearrange("p j n -> p (j n)"),
                                                    rhs=Gr[0:64, g, :, :].rearrange("p j n -> p (j n)"), start=True, stop=False),
             reads=["E", "G"], writes=[nm], sig=False)
        P.op("pe", lambda e, g=g, ps_=ps_: e.matmul(ps_[:, 0:128], lhsT=Ei[0:64, g, :, :].rearrange("p j n -> p (j n)"),
                                                    rhs=Gni[0:64, g, :, :].rearrange("p j n -> p (j n)"), start=False, stop=True),
             reads=["E", "G"], writes=[nm])
        tt(wtmp[:], ps_[:, 0:128], bmask[:].rearrange("p j n -> p (j n)"), ALU.mult, [nm, "bmask"], ["wtmp"])
        el("dve", lambda e, g=g: e.scalar_tensor_tensor(out=stgb2[:, g * 128:(g + 1) * 128], in0=ident_f[:], scalar=Dcol[:, g:g + 1],
                                                        in1=wtmp[:], op0=ALU.mult, op1=ALU.add), ["ident_f", "Dcol", "wtmp", "stgb2"], ["stgb2"])
    P.dma(lambda e: e.dma_start(out=WS[BLK_S5 + 2], in_=stgb2[:]), reads=["stgb2"], writes=["WS%d" % (BLK_S5 + 2)], q="pool")
    for g2 in range(2):
        sl = slice(64 * g2, 64 * g2 + 64)
        src_r = pwr[sl, 8, :].rearrange("p (q t) -> p q t", t=2)[:, :, g2]
        src_i = pwi[sl, 8, :].rearrange("p (q t) -> p q t", t=2)[:, :, g2]
        el("dve", lambda e, sl=sl, src_r=src_r: e.tensor_copy(out=AR32[sl, 0, :], in_=src_r), ["pw"], ["AR32"])
        el("dve", lambda e, sl=sl, src_r=src_r: e.tensor_copy(out=AR32[sl, 1, :], in_=src_r), ["pw"], ["AR32"])
        el("dve", lambda e, sl=sl, src_i=src_i: e.tensor_copy(out=API[sl, :], in_=src_i), ["pw"], ["API"])
        ts(ANI[sl, :], src_i, -1.0, None, ALU.mult, None, ["pw"], ["ANI"])

    stage_some(100)
    if dbg_stop == 1:
        tk = P.dma(lambda e: e.dma_start(out=out[0:128, :], in_=gpost[:]), reads=["gpost"])
        P.final_wait("sp", [tk])
        P.emit(); P.temp.close(); P.close()
        return nc
    P.emit()
    P.temp.close()
    P.temp = None
    NSLOT = 3
    wslot = [sbt([128, 4096], BF16, "wslot%d" % i) for i in range(NSLOT)]
    slot_rr = [0]

    pre = {}

    def prefetch(blk):
        pre[blk] = load_block(blk, force=True)

    def load_block(blk, force=False):
        if not force and blk in pre:
            return pre.pop(blk)
        s = slot_rr[0] % NSLOT
        slot_rr[0] += 1
        P.dma(lambda e: e.dma_start(out=wslot[s][:], in_=WS[blk]), reads=["WS%d" % blk], writes=["wslot%d" % s])
        return wslot[s], "wslot%d" % s

    xs = [sbt([128, 1024], F32, "xs%d" % i) for i in range(2)]
    xr = [sbt([128, 1024], F32, "xr%d" % i) for i in range(2)]
    junk = sbt([128, 1024], BF16); xnb = sbt([128, 1024], BF16)
    ss = small(1); rstd = small(1)
    xnT = sbt([128, 8, T], BF16, "xnT"); uaT = sbt([128, 8, T + 4], BF16, "uaT"); cT = sbt([128, 8, T], BF16, "cT")
    zaT = sbt([128, 8, T], BF16, "zaT"); oaT = sbt([128, 8, T], BF16, "oaT"); zbT = sbt([128, 4, T], BF16, "zbT")
    qT = sbt([128, 4, 2, T], BF16, "qT"); kT = sbt([128, 4, 2, T], BF16, "kT")
    ktm = sbt([128, NCH, 4, 256], BF16, "ktm"); vw = sbt([128, NCH, 4, 264], BF16, "vw")
    gates = sbt([128, NCH, 8], F32, "gates")
    hfT = sbt([128, 8, T], BF16, "hfT"); csz = sbt([128, 8, T], BF16, "csz"); mrgT = zaT; hbT = sbt([128, 4, T], BF16, "hbT")
    Cst = sbt([128, 4, 2, 264], F32, "Cst"); Cbf = sbt([128, 4, 2, 264], BF16, "Cbf")
    nst = sbt([128, 4, 2], F32, "nst"); nrep = sbt([128, 4, 2, 128], BF16, "nrep")
    Utm = sbt([128, 32, 8, 16], BF16, "Utm"); U2 = sbt([128, 32, NC8], BF16, "U2")
    Xall = sbt([128, NC8, 2, 16], F32, "Xall"); Sall = sbt([128, NC8 + 1, 2, 16], F32, "Sall"); Sbf = sbt([128, 2, 16, NC8], BF16, "Sbf")
    Yg = sbt([128, 32, NC8], BF16, "Yg"); Ytm = Utm[:].rearrange("p g j n -> p (g j n)").rearrange("p (j c) -> p j c", j=8); yT = sbt([128, 4, T], BF16, "yT")
    for (t_, nm) in ((Cst, "Cst"), (Cbf, "Cbf"), (nst, "nst"), (nrep, "nrep"), (Sall, "Sall"), (uaT, "uaT"), (vw, "vw")):
        flat = t_[:]
        wnames = [nm] + (["%s%d" % (nm, h_) for h_ in range(4)] if nm in ("Cst", "Cbf", "nrep") else [])
        el("pool", lambda e, flat=flat: e.memset(flat, 0.0), [], wnames)

    e1a = sbt([128, NCH, 4], F32); lfa = sbt([128, NCH, 4], F32); emb4 = sbt([128, NCH, 4, 128], F32); dec4 = sbt([128, NCH, 4], F32)
    wcol4 = sbt([128, NCH, 4], F32); wcolb4 = sbt([128, NCH, 4, 2], BF16); wrep4 = sbt([128, NCH, 4, 128], BF16)
    lfrep = sbt([128, 4, 128], F32); lf = small(4, "lf"); ig = small(4); bcol = small(4); wcol = small(4, "wcol"); wcolb = sbt([128, 4, 2], BF16)
    wrep = sbt([128, 4, 128], BF16); emb = sbt([128, 4, 128], F32, "emb"); dec = small(4, "dec"); SmT = sbt([128, 4, 128], BF16, "SmT")
    aden = sbt([128, 4, 128], F32); rden = sbt([128, 4, 128], F32, "rden"); hT = sbt([128, 8, 128], F32, "hT"); sq = sbt([128, 8, 128], BF16)
    rsh = sbt([128, 4, 128], F32); hn = sbt([128, 8, 128], F32, "hn"); e1 = small(4)
    gAll = oaT; m1 = sbt([128, T], F32); m2 = sbt([128, T], F32)
    ot = [sbt([128, 1024], F32, "ot%d" % i) for i in range(2)]
    ss2 = small(1); rstd2 = small(1); ss3 = small(1); rstd3 = small(1); sgl = sbt([128, T], BF16); xg = sbt([128, T], F32)

    def rsqrt_col(dst, src, scale, nm_src, nm_dst):
        ts(dst, src, scale, 1e-6, ALU.mult, ALU.add, [nm_src], [nm_dst])
        el("act", lambda e: e.activation(out=dst, in_=dst, func=AF.Ln), [nm_dst], [nm_dst])
        el("act", lambda e: e.activation(out=dst, in_=dst, func=AF.Exp, scale=-0.5), [nm_dst], [nm_dst])

    def mm(outp, lhsT, rhs, start, stop, r, w, sig=None):
        P.op("pe", lambda e: e.matmul(outp, lhsT=lhsT, rhs=rhs, start=start, stop=stop), reads=r, writes=w,
             sig=(stop if sig is None else sig))

    acc_rr = [0]

    def acc_bank():
        acc_rr[0] += 1
        return (pA, "pA") if acc_rr[0] % 2 else (pB, "pB")

    out_toks = []

    class StopBuild(Exception):
        pass

    def chk(level):
        if dbg_stop == level:
            raise StopBuild()

    prefetched = [False]

    def do_tile(xsrc, row0, main, ti, nxt=None):
        def issue_x(src_, r0, sub):
            xb2, xn2 = xs[sub % 2], "xs%d" % (sub % 2)
            P.dma(lambda e: e.dma_start(out=xb2[:], in_=src_[r0 + sub * 128: r0 + (sub + 1) * 128, :]), writes=[xn2])

        for sub in range(NCH):
            xb_, xnm = xs[sub % 2], "xs%d" % (sub % 2)
            if not (prefetched[0] and sub < 2):
                issue_x(xsrc, row0, sub)
            el("act", lambda e, xb_=xb_: e.activation(out=junk[:], in_=xb_[:], func=AF.Square, accum_out=ss[:]), [xnm], ["junk", "ss"])
            rsqrt_col(rstd[:], ss[:], 1.0 / 1024.0, "ss", "rstd")
            ts(xnb[:], xb_[:], rstd[:, 0:1], None, ALU.mult, None, [xnm, "rstd"], ["xnb"])
            for kt in range(8):
                P.op("pe", lambda e, kt=kt: e.transpose(out=pT[:, kt * 128:(kt + 1) * 128], in_=xnb[:, kt * 128:(kt + 1) * 128], identity=ident_b[:]),
                     reads=["xnb", "ident_b"], writes=["pT"], sig=(kt == 7))
            el("act", lambda e, sub=sub: e.copy(out=xnT[:, :, sub * 128:(sub + 1) * 128], in_=pT[:].rearrange("p (k t) -> p k t", k=8)),
               ["pT"], ["xnT"])

        prefetched[0] = False
        if nxt is not None and NCH <= 2:
            for sub in range(min(2, NCH)):
                issue_x(nxt[0], nxt[1], sub)
            prefetched[0] = True
        chk(2)

        def proj_fm(blk, ncolt, evac):
            W, wn = load_block(blk)
            Wv_ = W[:].rearrange("p (k n) -> p k n", k=8)
            for ct in range(ncolt):
                ps_, pn = acc_bank()
                for kt in range(8):
                    mm(ps_[:, 0:T], Wv_[:, kt, ct * 128:(ct + 1) * 128], xnT[:, kt, :], kt == 0, kt == 7, [wn, "xnT"], [pn])
                evac(ct, ps_, pn)

        el("dve", lambda e: e.tensor_copy(out=uaT[:, :, 1:4], in_=uaT[:, :, T + 1:T + 4]), ["uaT"], ["uaT"])
        for half in range(2):
            proj_fm(BLK_UA + half, 4, lambda ct, ps_, pn, half=half: el(
                "act", lambda e: e.copy(out=uaT[:, half * 4 + ct, 4:T + 4], in_=ps_[:, 0:T]), [pn], ["uaT"]))
        chk(3)
        for ch in range(NCH):
            for kt in range(8):
                mm(pM[:, 0:8], xnT[:, kt, ch * 128:(ch + 1) * 128], Wif[:, kt, :], kt == 0, kt == 7, ["xnT", "Wif"], ["pM"])
            tt(gates[:, ch, :], pM[:, 0:8], bif[:], ALU.add, ["pM", "bif"], ["gates"])
        chk(4)
        W, wn = load_block(BLK_UB)
        Wv_ = W[:].rearrange("p (k n) -> p k n", k=8)
        for j in range(8):
            ps_, pn = acc_bank()
            for kt in range(8):
                lhs = xnT[:, kt, :].rearrange("p (c j) -> p j c", j=8)[:, j, :]
                mm(ps_[0:NC8, :], lhs, Wv_[:, kt, :], kt == 0, kt == 7, [wn, "xnT"], [pn])
            el("act" if j % 2 else "dve",
               (lambda e, ps_=ps_, j=j: e.copy(out=Utm[0:NC8, :, j, :], in_=ps_[0:NC8, :].rearrange("c (g n) -> c g n", g=32))) if j % 2 else
               (lambda e, ps_=ps_, j=j: e.tensor_copy(out=Utm[0:NC8, :, j, :], in_=ps_[0:NC8, :].rearrange("c (g n) -> c g n", g=32))), [pn], ["Utm"])
        chk(5)
        for g in range(32):
            P.op("pe", lambda e, g=g: e.transpose(out=pT[:, g * NC8:(g + 1) * NC8], in_=Utm[0:NC8, g, :, :].rearrange("c j n -> c (j n)"),
                                                  identity=ident_b[0:NC8, 0:NC8]), reads=["Utm", "ident_b"], writes=["pT"], sig=(g == 31))
        el("act", lambda e: e.copy(out=U2[:].rearrange("p g c -> p (g c)"), in_=pT[:, 0:32 * NC8]), ["pT"], ["U2"])
        W1r, w1rn = load_block(BLK_S5)
        W1i, w1in = load_block(BLK_S5 + 1)
        for ri, (Wt, wn_) in enumerate(((W1r, w1rn), (W1i, w1in))):
            Wg = Wt[:].rearrange("p (g n) -> p g n", g=32)
            for q in range(16):
                for g2 in range(2):
                    g = 2 * q + g2
                    mm(pM[:, (q * NC8):(q + 1) * NC8], Wg[:, g, :], U2[:, g, :], g2 == 0, g2 == 1, [wn_, "U2"], ["pM"], sig=(q == 15 and g2 == 1))
            el("act" if ri == 0 else "dve",
               (lambda e, ri=ri: e.copy(out=Xall[:, :, ri, :].rearrange("p c q -> p q c"), in_=pM[:, 0:16 * NC8].rearrange("p (q c) -> p q c", q=16))) if ri == 0 else
               (lambda e, ri=ri: e.tensor_copy(out=Xall[:, :, ri, :].rearrange("p c q -> p q c"), in_=pM[:, 0:16 * NC8].rearrange("p (q c) -> p q c", q=16))),
               ["pM"], ["Xall"])
        chk(6)
        T1 = sbt([128, 2, 16], F32, "scT1_%d" % ti) if False else None
        for c in range(NC8):
            sp_ = Sall[:, c, :, :]
            sn_ = Sall[:, c + 1, :, :]
            P.op("pool", lambda e, sp_=sp_: e.tensor_tensor(out=sc_t1[:], in0=sp_, in1=AR32[:], op=ALU.mult), reads=["Sall"], writes=["sc_t1"])
            P.op("pool", lambda e, c=c: e.tensor_tensor(out=sc_t2[:, 0, :], in0=Sall[:, c, 1, :], in1=ANI[:], op=ALU.mult), reads=["Sall"], writes=["sc_t2"])
            P.op("pool", lambda e, c=c: e.tensor_tensor(out=sc_t2[:, 1, :], in0=Sall[:, c, 0, :], in1=API[:], op=ALU.mult), reads=["Sall"], writes=["sc_t2"])
            P.op("pool", lambda e, c=c: e.tensor_tensor(out=sc_t1[:], in0=sc_t1[:], in1=Xall[:, c, :, :], op=ALU.add), reads=["sc_t1", "Xall"], writes=["sc_t1"])
            P.op("pool", lambda e, sn_=sn_: e.tensor_tensor(out=sn_, in0=sc_t1[:], in1=sc_t2[:], op=ALU.add), reads=["sc_t1", "sc_t2"], writes=["Sall"])
        if main:
            for ri in range(2):
                el("pool", lambda e, ri=ri: e.tensor_copy(out=Sbf[:, ri, :, :], in_=Sall[:, 0:NC8, ri, :].rearrange("p c q -> p q c")),
                   ["Sall"], ["Sbf"])
        el("pool", lambda e: e.tensor_copy(out=Sall[:, 0, :, :], in_=Sall[:, NC8, :, :]), ["Sall"], ["Sall"])
        def s5_out():
            proj_fm(BLK_ZB, 4, lambda ct, ps_, pn: el("act", lambda e: e.activation(out=zbT[:, ct, :], in_=ps_[:, 0:T], func=AF.Silu), [pn], ["zbT"]))
            Wi_, win_ = load_block(BLK_S5 + 2)
            Wr_, wrn_ = load_block(BLK_S5 + 3)
            Wm_, wmn_ = load_block(BLK_S5 + 4)
            Wi_g = Wi_[:].rearrange("p (g n) -> p g n", g=32); Wr_g = Wr_[:].rearrange("p (g n) -> p g n", g=32)
            Wm_g = Wm_[:].rearrange("p (g n) -> p g n", g=32)
            GP = 512 // NC8
            for g0 in range(0, 32, GP):
                ng = min(GP, 32 - g0)
                for gi in range(ng):
                    g = g0 + gi
                    o_ = pM[:, gi * NC8:(gi + 1) * NC8]
                    mm(o_, Wi_g[:, g, :], U2[:, g, :], True, False, [win_, "U2"], ["pM"], sig=False)
                    mm(o_, Wr_g[:, g, :], Sbf[:, 0, g // 2, :], False, False, [wrn_, "Sbf"], ["pM"], sig=False)
                    mm(o_, Wm_g[:, g, :], Sbf[:, 1, g // 2, :], False, True, [wmn_, "Sbf"], ["pM"], sig=(gi == ng - 1))
                el("act", lambda e, g0=g0, ng=ng: e.activation(out=Yg[:, g0:g0 + ng, :].rearrange("p g c -> p (g c)"), in_=pM[:, 0:ng * NC8],
                                                               func=AF.Gelu_apprx_tanh), ["pM"], ["Yg"])
            prefetch(BLK_G + 0); prefetch(BLK_G + 1)
            for g0 in range(0, 32, 8):
                for gi in range(8):
                    g = g0 + gi
                    P.op("pe", lambda e, g=g, gi=gi: e.transpose(out=pT[0:NC8, gi * 128:(gi + 1) * 128], in_=Yg[:, g, :], identity=ident_b[:]),
                         reads=["Yg", "ident_b"], writes=["pT"], sig=(gi == 7))
                el("act", lambda e, g0=g0: e.copy(out=Ytm[0:NC8, :, 16 * g0:16 * g0 + 128].rearrange("c j (g n) -> c g j n", g=8),
                                                  in_=pT[0:NC8, :].rearrange("c (g j n) -> c g j n", g=8, j=8)), ["pT"], ["Utm"])
            for ct in range(4):
                for j in range(8):
                    P.op("pe", lambda e, ct=ct, j=j: e.transpose(out=pT[:, j * NC8:(j + 1) * NC8], in_=Ytm[0:NC8, j, ct * 128:(ct + 1) * 128],
                                                                 identity=ident_b[0:NC8, 0:NC8]), reads=["Utm", "ident_b"], writes=["pT"], sig=(j == 7))
                el("act", lambda e, ct=ct: e.copy(out=yT[:, ct, :], in_=pT[:, 0:T]), ["pT"], ["yT"])
            for ot_ in range(4):
                ps_, pn = acc_bank()
                for ct in range(4):
                    mm(ps_[:, 0:T], Wglu[:, ct, ot_ * 128:(ot_ + 1) * 128], yT[:, ct, :], ct == 0, ct == 3, ["Wglu", "yT"], [pn])
                el("act", lambda e, ps_=ps_, ot_=ot_: e.activation(out=sgl[:], in_=ps_[:, 0:T], func=AF.Sigmoid, bias=bglu[:, ot_:ot_ + 1]),
                   [pn, "bglu"], ["sgl"])
                tt(xg[:], sgl[:], yT[:, ot_, :], ALU.mult, ["sgl", "yT"], ["xg"])
                tt(hbT[:, ot_, :].rearrange("p (j c) -> p j c", j=8), xg[:].rearrange("p (j c) -> p j c", j=8),
                   zbT[:, ot_, :].rearrange("p (c j) -> p j c", j=8), ALU.mult, ["xg", "zbT"], ["hbT"])
        chk(7)
        for mt in range(8):
            ps_, pn = acc_bank()
            for k in range(4):
                mm(ps_[:, 0:T], cdiag[:, mt, k, :], uaT[:, mt, k + 1:k + 1 + T], k == 0, k == 3, ["cdiag", "uaT"], [pn])
            el("act", lambda e, ps_=ps_, mt=mt: e.activation(out=cT[:, mt, :], in_=ps_[:, 0:T], func=AF.Silu, bias=cb[:, mt:mt + 1]),
               [pn, "cb"], ["cT"])
        if main:
            for half in range(2):
                proj_fm(BLK_ZA + half, 4, lambda ct, ps_, pn, half=half: el(
                    "act", lambda e: e.activation(out=zaT[:, half * 4 + ct, :], in_=ps_[:, 0:T], func=AF.Silu), [pn], ["zaT"]))
            for half in range(2):
                proj_fm(BLK_OA + half, 4, lambda ct, ps_, pn, half=half: el(
                    "act", lambda e: e.activation(out=oaT[:, half * 4 + ct, :], in_=ps_[:, 0:T], func=AF.Sigmoid), [pn], ["oaT"]))
        if main:
            for mt in range(8):
                el("dve", lambda e, mt=mt: e.scalar_tensor_tensor(out=csz[:, mt, :], in0=cT[:, mt, :], scalar=skp[:, mt:mt + 1], in1=zaT[:, mt, :],
                                                                  op0=ALU.mult, op1=ALU.mult), ["cT", "skp", "zaT"], ["csz"])
            for mt in range(8):
                ts(zaT[:, mt, :], zaT[:, mt, :], hg[:, mt:mt + 1], None, ALU.mult, None, ["zaT", "hg", "csz"], ["zaT"])
        chk(8)
        for h in range(4):
            if main:
                for (Wx, wxn, dst, dn) in ((Wq, "Wqkv0", qT, "qT"), (Wk, "Wqkv1", kT, "kT")):
                    for et in range(2):
                        ps_, pn = acc_bank()
                        for d in range(2):
                            mm(ps_[:, 0:T], Wx[:, h, d, et * 128:(et + 1) * 128], cT[:, 2 * h + d, :], d == 0, d == 1, [wxn, "cT"], [pn])
                        el("act" if et else "dve",
                           (lambda e, ps_=ps_, dst=dst, h=h, et=et: e.copy(out=dst[:, h, et, :], in_=ps_[:, 0:T])) if et else
                           (lambda e, ps_=ps_, dst=dst, h=h, et=et: e.tensor_copy(out=dst[:, h, et, :], in_=ps_[:, 0:T])), [pn], [dn])
        if main:
            prefetch(BLK_ZB); prefetch(BLK_S5 + 2); prefetch(BLK_S5 + 3)
        el("act", lambda e: e.activation(out=e1a[:], in_=gates[:, :, 4:8], func=AF.Exp, scale=-1.0), ["gates"], ["e1a"])
        el("act", lambda e: e.activation(out=lfa[:], in_=e1a[:], func=AF.Ln, bias=1.0), ["e1a"], ["lfa"])
        ts(lfa[:], lfa[:], -1.0, None, ALU.mult, None, ["lfa"], ["lfa"])
        for ch in range(NCH):
            tsl = slice(ch * 128, (ch + 1) * 128)
            for h in range(4):
                el("dve", lambda e, h=h, ch=ch: e.tensor_copy(out=lfrep[:, h, :], in_=lfa[:, ch, h:h + 1].to_broadcast([128, 128])), ["lfa"], ["lfrep"])
            for h in range(4):
                mm(pGD[:, h * 128:(h + 1) * 128], lfrep[:, h, :], maskT[:], True, True, ["lfrep", "maskT"], ["pGD"], sig=(h == 3))
            for h in range(4):
                mm(pM[:, h * 128:(h + 1) * 128], maskT[:], lfrep[:, h, :], True, True, ["maskT", "lfrep"], ["pM"], sig=(h == 3))
            en, dn, wn_, wbn, wrn = "emb%d" % ch, "dec%d" % ch, "wcol%d" % ch, "wcolb%d" % ch, "wrep%d" % ch
            el("act", lambda e, ch=ch: e.activation(out=emb4[:, ch, :, :].rearrange("p h t -> p (h t)"), in_=pGD[:], func=AF.Exp, scale=-1.0), ["pGD"], [en])
            el("dve", lambda e, ch=ch: e.reciprocal(out=dec4[:, ch, :], in_=emb4[:, ch, :, 127]), [en], [dn])
            tt(bcol[:], gates[:, ch, 0:4], pM[:].rearrange("p (h t) -> p h t", h=4)[:, :, 0], ALU.subtract, ["gates", "pM"], ["bcol"])
            el("act", lambda e, ch=ch: e.activation(out=wcol4[:, ch, :], in_=bcol[:], func=AF.Exp), ["bcol"], [wn_])
            el("dve", lambda e, ch=ch: e.tensor_copy(out=vw[:, ch, :, 256], in_=wcol4[:, ch, :]), [wn_], ["vw"])
            if main:
                for h in range(4):
                    el("dve", lambda e, h=h, ch=ch: e.tensor_copy(out=wrep4[:, ch, h, :], in_=wcol4[:, ch, h:h + 1].to_broadcast([128, 128])), [wn_], [wrn])
            for h in range(4):
                ps_, pn = acc_bank()
                for d in range(2):
                    mm(ps_[:, 0:256], cT[:, 2 * h + d, tsl], Wk[:, h, d, :], d == 0, d == 1, ["cT", "Wqkv1"], [pn], sig=False)
                for d in range(2):
                    mm(ps_[:, 256:512], uaT[:, 2 * h + d, 4 + ch * 128:4 + (ch + 1) * 128], Wv[:, h, d, :], d == 0, d == 1, ["uaT", "Wqkv2"], [pn])
                el("act", lambda e, ps_=ps_, ch=ch, h=h: e.copy(out=ktm[:, ch, h, :], in_=ps_[:, 0:256]), [pn], ["ktm"])
                el("act", lambda e, ps_=ps_, ch=ch, h=h: e.activation(out=vw[:, ch, h, 0:256], in_=ps_[:, 256:512], func=AF.Copy, scale=wcol4[:, ch, h:h + 1]),
                   [pn, wn_], ["vw"])
        for ch in range(NCH):
            tsl = slice(ch * 128, (ch + 1) * 128)
            en, dn, wn_, wbn, wrn = "emb%d" % ch, "dec%d" % ch, "wcol%d" % ch, "wcolb%d" % ch, "wrep%d" % ch
            if main:
                for h in range(4):
                    for et in range(2):
                        mm(pS[:, h * 128:(h + 1) * 128], kT[:, h, et, tsl], qT[:, h, et, tsl], et == 0, et == 1, ["kT", "qT"], ["pS"], sig=(h == 3 and et == 1))
                tt(SmT[:].rearrange("p h t -> p (h t)"), pS[:], mask4[:].rearrange("p h t -> p (h t)"), ALU.mult, ["pS", "mask4"], ["SmT"])
                for h in range(4):
                    for d2 in range(2):
                        pn_, pnn = (pN0, "pN0") if h < 2 else (pN1, "pN1")
                        o_ = pn_[:, ((h % 2) * 2 + d2) * 128:((h % 2) * 2 + d2 + 1) * 128]
                        mm(o_, vw[:, ch, h, d2 * 128:(d2 + 1) * 128], SmT[:, h, :], True, False, ["vw", "SmT"], [pnn], sig=False)
                        mm(o_, Cbf[:, h, 0, d2 * 128:(d2 + 1) * 128], qT[:, h, 0, tsl], False, False, ["Cbf%d" % h, "qT"], [pnn], sig=False)
                        mm(o_, Cbf[:, h, 1, d2 * 128:(d2 + 1) * 128], qT[:, h, 1, tsl], False, True, ["Cbf%d" % h, "qT"], [pnn], sig=(h % 2 == 1 and d2 == 1))
                    o_ = pGD[:, h * 128:(h + 1) * 128]
                    mm(o_, wrep4[:, ch, h, :], SmT[:, h, :], True, False, [wrn, "SmT"], ["pGD"], sig=False)
                    mm(o_, nrep[:, h, 0, :], qT[:, h, 0, tsl], False, False, ["nrep%d" % h, "qT"], ["pGD"], sig=False)
                    mm(o_, nrep[:, h, 1, :], qT[:, h, 1, tsl], False, True, ["nrep%d" % h, "qT"], ["pGD"], sig=(h == 3))
            if main:
                el("act", lambda e: e.activation(out=aden[:].rearrange("p h t -> p (h t)"), in_=pGD[:], func=AF.Abs), ["pGD"], ["aden"])
                tt(aden[:], aden[:], emb4[:, ch, :, :], ALU.max, ["aden", en], ["aden"])
                el("act", lambda e: e.activation(out=aden[:], in_=aden[:], func=AF.Ln), ["aden"], ["aden"])
                el("act", lambda e: e.activation(out=rden[:], in_=aden[:], func=AF.Exp, scale=-1.0), ["aden"], ["rden"])
            for h in range(4):
                cn, bn, nn = "Cst%d" % h, "Cbf%d" % h, "nrep%d" % h
                for et in range(2):
                    ps_, pn = acc_bank()
                    mm(ps_[:, 0:258], ktm[:, ch, h, et * 128:(et + 1) * 128], vw[:, ch, h, 0:258], True, True, ["ktm", "vw"], [pn])
                    tt(Cst[:, h, et, 0:258], Cst[:, h, et, 0:258], ps_[:, 0:258], ALU.add, [cn, pn, bn, nn], [cn])
            for h in range(4):
                cn, bn, nn = "Cst%d" % h, "Cbf%d" % h, "nrep%d" % h
                el("act", lambda e, h=h, ch=ch: e.activation(out=Cst[:, h, :, :], in_=Cst[:, h, :, :], func=AF.Copy, scale=dec4[:, ch, h:h + 1]), [cn, dn], [cn])
                el("act", lambda e, h=h: e.copy(out=Cbf[:, h, :, :], in_=Cst[:, h, :, :]), [cn], [bn])
            for h in range(4):
                cn, bn, nn = "Cst%d" % h, "Cbf%d" % h, "nrep%d" % h
                for et in range(2):
                    el("dve", lambda e, h=h, et=et: e.tensor_copy(out=nrep[:, h, et, :], in_=Cst[:, h, et, 256:257].to_broadcast([128, 128])),
                       [cn], [nn])
            if main:
                for hp, (pn_, pnn) in enumerate(((pN0, "pN0"), (pN1, "pN1"))):
                    tt(hT[:, 4 * hp:4 * hp + 4, :].rearrange("p (h d) t -> p h d t", h=2), pn_[:].rearrange("p (h d t) -> p h d t", h=2, d=2),
                       rden[:, 2 * hp:2 * hp + 2, :].unsqueeze(2).to_broadcast([128, 2, 2, 128]), ALU.mult, [pnn, "rden"], ["hT"])
                tt(hT[:], hT[:], oaT[:, :, tsl], ALU.mult, ["hT", "oaT"], ["hT"])
                el("act", lambda e: e.activation(out=sq[:], in_=hT[:], func=AF.Square), ["hT"], ["sq"])
                for h in range(4):
                    for d2 in range(2):
                        mm(pS[:, h * 128:(h + 1) * 128], ones_b[:], sq[:, 2 * h + d2, :], d2 == 0, d2 == 1, ["ones_b", "sq", "SmT"], ["pS"], sig=(h == 3 and d2 == 1))
                ts(rsh[:].rearrange("p h t -> p (h t)"), pS[:], 1.0 / 256.0, 1e-6, ALU.mult, ALU.add, ["pS"], ["rsh"])
                el("act", lambda e: e.activation(out=rsh[:], in_=rsh[:], func=AF.Ln), ["rsh"], ["rsh"])
                el("act", lambda e: e.activation(out=rsh[:], in_=rsh[:], func=AF.Exp, scale=-0.5), ["rsh"], ["rsh"])
                tt(hn[:].rearrange("p (h d) t -> p h d t", h=4), hT[:].rearrange("p (h d) t -> p h d t", h=4),
                   rsh[:].unsqueeze(2).to_broadcast([128, 4, 2, 128]), ALU.mult, ["hT", "rsh"], ["hn"])
                tt(hn[:], hn[:], zaT[:, :, tsl], ALU.mult, ["hn", "zaT"], ["hn"])
                tt(hfT[:, :, tsl], hn[:], csz[:, :, tsl], ALU.add, ["hn", "csz"], ["hfT"])
        chk(9)
        if not main:
            return
        s5_out()
        def gate_blocks(br):
            for bi_ in range(2):
                Wg_, wgn = load_block(BLK_G + 2 * br + bi_)
                Wgv = Wg_[:].rearrange("p (k n) -> p k n", k=8)
                for c4 in range(4):
                    ft = bi_ * 4 + c4
                    psg, png = acc_bank()
                    for kt in range(8):
                        mm(psg[:, 0:T], Wgv[:, kt, c4 * 128:(c4 + 1) * 128], xnT[:, kt, :], kt == 0, kt == 7, [wgn, "xnT"], [png])
                    el("act", lambda e, psg=psg, ft=ft: e.activation(out=gAll[:, ft, :], in_=psg[:, 0:T], func=AF.Sigmoid), [png], ["oaT"])

        gate_blocks(0)
        for hf_ in range(2):
            Wa, wan = load_block(BLK_AO + hf_)
            Wav = Wa[:].rearrange("p (k n) -> p k n", k=8)
            for c4 in range(4):
                ft = hf_ * 4 + c4
                psa, pna = acc_bank()
                for kt in range(8):
                    mm(psa[:, 0:T], Wav[:, kt, c4 * 128:(c4 + 1) * 128], hfT[:, kt, :], kt == 0, kt == 7, [wan, "hfT"], [pna])
                tt(mrgT[:, ft, :], psa[:, 0:T], gAll[:, ft, :], ALU.mult, [pna, "oaT"], ["zaT"])
        gate_blocks(1)
        Wbo, wbon = load_block(BLK_BO)
        Wbv = Wbo[:].rearrange("p (k n) -> p k n", k=4)
        for ft in range(8):
            psb, pnb = acc_bank()
            for kt in range(4):
                mm(psb[:, 0:T], Wbv[:, kt, ft * 128:(ft + 1) * 128], hbT[:, kt, :], kt == 0, kt == 3, [wbon, "hbT"], [pnb])
            el("act", lambda e, psb=psb: e.copy(out=m2[:].rearrange("p (c j) -> p j c", j=8), in_=psb[:, 0:T].rearrange("p (j c) -> p j c", j=8)),
               [pnb], ["m2"])
            tt(m2[:], m2[:], gAll[:, ft, :], ALU.mult, ["m2", "oaT"], ["m2"])
            tt(mrgT[:, ft, :], m2[:], mrgT[:, ft, :], ALU.add, ["m2", "zaT"], ["zaT"])
        Wo0, wo0n = load_block(BLK_WO); Wo1, wo1n = load_block(BLK_WO + 1)
        for ch in range(NCH):
            xb_, xnm = xr[ch % 2], "xr%d" % (ch % 2)
            if ch < 2:
                P.dma(lambda e, xb_=xb_, ch=ch: e.dma_start(out=xb_[:], in_=xsrc[row0 + ch * 128: row0 + (ch + 1) * 128, :]), writes=[xnm])
        for ch in range(NCH):
            tsl = slice(ch * 128, (ch + 1) * 128)
            bk = ((pN0, "pN0"), (pN1, "pN1")) if ch % 2 == 0 else ((pS, "pS"), (pGD, "pGD"))
            ssA, ssAn = (ss2, "ss2") if ch % 2 == 0 else (ss3, "ss3")
            ssB, ssBn = (rstd2, "rstd2") if ch % 2 == 0 else (rstd3, "rstd3")
            for hf_, (Wo_, won) in enumerate(((Wo0, wo0n), (Wo1, wo1n))):
                pn_, pnn = bk[hf_]
                Wov = Wo_[:].rearrange("p (k n) -> p k n", k=8)
                for kt in range(8):
                    mm(pn_[:, :], mrgT[:, kt, tsl], Wov[:, kt, :], kt == 0, kt == 7, ["zaT", won], [pnn])
            jk, jkn = (junk, "junk") if ch % 2 == 0 else (xnb, "xnb")
            el("act", lambda e, bk=bk, ssA=ssA, jk=jk: e.activation(out=jk[:, 0:512], in_=bk[0][0][:], func=AF.Square, accum_out=ssA[:]), [bk[0][1]], [jkn, ssAn])
            el("act", lambda e, bk=bk, ssB=ssB, jk=jk: e.activation(out=jk[:, 512:1024], in_=bk[1][0][:], func=AF.Square, accum_out=ssB[:]), [bk[1][1]], [jkn, ssBn])
            tt(ssA[:], ssA[:], ssB[:], ALU.add, [ssAn, ssBn], [ssAn])
            rsqrt_col(ssB[:], ssA[:], 1.0 / 1024.0, ssAn, ssBn)
            ob, obn = ot[ch % 2], "ot%d" % (ch % 2)
            xb_, xnm = xr[ch % 2], "xr%d" % (ch % 2)
            if ch >= 2:
                P.dma(lambda e, xb_=xb_, ch=ch: e.dma_start(out=xb_[:], in_=xsrc[row0 + ch * 128: row0 + (ch + 1) * 128, :]), writes=[xnm])
            for hf_ in range(2):
                pn_, pnn = bk[hf_]
                cs = slice(hf_ * 512, hf_ * 512 + 512)
                el("dve", lambda e, ob=ob, pn_=pn_, cs=cs, ssB=ssB: e.scalar_tensor_tensor(out=ob[:, cs], in0=pn_[:], scalar=ssB[:, 0:1], in1=gpost[:, cs],
                                                                                          op0=ALU.mult, op1=ALU.mult), [pnn, ssBn, "gpost"], [obn])
            tt(ob[:], ob[:], xb_[:], ALU.add, [obn, xnm], [obn])
            tok = P.dma(lambda e, ob=ob, ch=ch: e.dma_start(out=out[row0 + ch * 128: row0 + (ch + 1) * 128, :], in_=ob[:]), reads=[obn])
            out_toks.append(tok)

    sc_t1 = sbt([128, 2, 16], F32, "sc_t1"); sc_t2 = sbt([128, 2, 16], F32, "sc_t2")
    try:
        for ti in range(NPRE):
            nxt = (x_pre, (ti + 1) * T) if ti + 1 < NPRE else (x_main, 0)
            do_tile(x_pre, ti * T, False, ti, nxt)
    except StopBuild:
        tk = P.dma(lambda e: e.dma_start(out=out[0:128, :], in_=gpost[:]), reads=["gpost"])
        P.final_wait("sp", [tk])
        P.emit(); P.close()
        return nc
    if NPRE > 0:
        for h in range(4):
            cn, bn, nn = "Cst%d" % h, "Cbf%d" % h, "nrep%d" % h
            ts(Cst[:, h, :, :], Cst[:, h, :, :], flg[:, 0:1], None, ALU.mult, None, [cn, "flg"], [cn])
            el("act", lambda e, h=h: e.copy(out=Cbf[:, h, :, :], in_=Cst[:, h, :, :]), [cn], [bn])
            for et in range(2):
                el("dve", lambda e, h=h, et=et: e.tensor_copy(out=nrep[:, h, et, :], in_=Cst[:, h, et, 256:257].to_broadcast([128, 128])),
                   [cn], [nn])
    for ti in range(NMAIN):
        nxt = (x_main, (ti + 1) * T) if ti + 1 < NMAIN else None
        do_tile(x_main, ti * T, True, NPRE + ti, nxt)
    P.final_wait("sp", out_toks)
    P.emit()
    P.close()
    return nc


T_TILE = 256
_cache = {}


def kernel(**inputs):
    x = np.ascontiguousarray(inputs["x"], dtype=np.float32)
    Bsz, L, Dm = x.shape
    half = L // 2
    npre = half // T_TILE
    nmain = half // T_TILE
    key = (T_TILE, npre, nmain)
    if key not in _cache:
        _cache[key] = build_program(T_TILE, npre, nmain)
    nc = _cache[key]
    shared = {}
    for k, v in inputs.items():
        if k == "x":
            continue
        a = np.ascontiguousarray(np.asarray(v, dtype=np.float32)[0])
        if k in ("b_i", "b_f", "log_dt", "norm_post_g"):
            a = a.reshape(1, -1)
        shared[k] = a
    in_maps = []
    zeros = np.zeros((half, Dm), np.float32)
    for core in range(8):
        b, hf = core // 2, core % 2
        m = dict(shared)
        m["x_main"] = np.ascontiguousarray(x[b, hf * half:(hf + 1) * half])
        m["x_pre"] = zeros if hf == 0 else np.ascontiguousarray(x[b, 0:half])
        m["flag"] = np.full((128, 1), float(hf), np.float32)
        in_maps.append(m)
    res = run_bass_kernel_spmd(nc, in_maps, core_ids=list(range(8)))
    outp = np.empty((Bsz, L, Dm), np.float32)
    for core in range(8):
        b, hf = core // 2, core % 2
        outp[b, hf * half:(hf + 1) * half] = res.results[core]["out"]
    return outp
```

```python
import contextlib
import numpy as np
import concourse.bass as bass
import concourse.mybir as mybir
from concourse.bass_utils import run_bass_kernel_spmd

F32 = mybir.dt.float32
BF16 = mybir.dt.bfloat16
AF = mybir.ActivationFunctionType
ALU = mybir.AluOpType
COMPUTE = ("pe", "act", "dve", "pool")
NDMA_SLOTS = 8
DEBUG_TAGS = False
EARLY_A = True
PI = float(np.pi)


class Prog:
    def __init__(self, nc):
        self.nc = nc
        self.stack = contextlib.ExitStack()
        self.engs = ("pe", "act", "dve", "pool", "sp")
        self.ops = {e: [] for e in self.engs}
        self.waited = {e: {} for e in self.engs}
        self.res = {}
        self.dma_use = {}
        self.dma_rr = {e: 0 for e in self.engs}
        self.sems = {}
        self.base = {e: 0 for e in COMPUTE}
        self.temp = None

    def sb(self, name, shape, dt):
        st = self.temp if self.temp is not None else self.stack
        return st.enter_context(self.nc.sbuf_tensor(name, list(shape), dt))

    def ps(self, name, shape, dt):
        return self.stack.enter_context(self.nc.psum_tensor(name, list(shape), dt))

    def _sem(self, key):
        if key not in self.sems:
            nm = "s_" + "_".join(str(k) for k in (key if isinstance(key, tuple) else (key,)))
            self.sems[key] = self.stack.enter_context(self.nc.semaphore(nm))
        return self.sems[key]

    def _deps(self, eng, reads, writes):
        deps = {}

        def add(tok):
            if tok is None:
                return
            k, v = tok
            if k == "pe" and eng == "pe":
                return
            if deps.get(k, -1) < v:
                deps[k] = v

        for r in reads:
            st = self.res.get(r)
            if st:
                for k, v in st[0].items():
                    add((k, v))
        for w in writes:
            st = self.res.get(w)
            if st:
                for k, v in st[0].items():
                    add((k, v))
                for k, v in st[1].items():
                    add((k, v))
        out = []
        wd = self.waited[eng]
        for k, v in deps.items():
            if wd.get(k, -1) >= v:
                continue
            wd[k] = v
            out.append((k, v))
        return out

    def _commit(self, tok, reads, writes):
        k, v = tok
        for r in reads:
            st = self.res.setdefault(r, [{}, {}])
            if st[1].get(k, -1) < v:
                st[1][k] = v
        for w in writes:
            old = self.res.get(w)
            wr = {}
            if old is not None and k not in COMPUTE:
                wr = {k2: v2 for k2, v2 in old[0].items() if k2 not in COMPUTE}
            wr[k] = v
            self.res[w] = [wr, {}]

    def _tag(self):
        if not DEBUG_TAGS:
            return None
        import sys as _sys
        f = _sys._getframe(2)
        while f is not None and f.f_code.co_name not in ("do_tile", "build_program", "s5_out", "gate_blocks", "proj_fm"):
            f = f.f_back
        return str(f.f_lineno) if f is not None else None

    def op(self, eng, fn, reads=(), writes=(), sig=True):
        waits = self._deps(eng, reads, writes)
        idx = len(self.ops[eng])
        self.ops[eng].append(dict(fn=fn, waits=waits, sig=sig, dma=None, tag=self._tag()))
        tok = (eng, idx)
        self._commit(tok, reads, writes)
        return tok

    def dma(self, fn, reads=(), writes=(), q="sp"):
        waits = self._deps(q, reads, writes)
        slot = self.dma_rr[q] % NDMA_SLOTS
        self.dma_rr[q] += 1
        key = ("d", q, slot)
        n = self.dma_use.get(key, 0)
        if n > 0:
            prev = n * 16
            if self.waited[q].get(key, -1) < prev:
                self.waited[q][key] = prev
                waits.append((key, prev))
        self.dma_use[key] = n + 1
        tok = (key, (n + 1) * 16)
        self.ops[q].append(dict(fn=fn, waits=waits, sig=False, dma=key))
        self._commit(tok, reads, writes)
        return tok

    def final_wait(self, eng, toks):
        self.ops[eng].append(dict(fn=None, waits=list(toks), sig=False, dma=None))

    def emit(self):
        nc = self.nc
        sigcount = {}
        totals = {}
        for e in COMPUTE:
            c = self.base[e]
            arr = []
            for o in self.ops[e]:
                if o["sig"]:
                    c += 1
                arr.append(c)
            need = [None] * len(arr)
            nxt = None
            for i in range(len(arr) - 1, -1, -1):
                if self.ops[e][i]["sig"]:
                    nxt = arr[i]
                need[i] = nxt
            sigcount[e] = need
            totals[e] = c
            self._sem(e)
        for k in self.dma_use:
            self._sem(k)

        def resolve(k, v):
            if k in COMPUTE:
                val = sigcount[k][v]
                assert val is not None, (k, v)
                return self.sems[k], val
            return self.sems[k], v

        with nc.Block() as block:

            def run(eng_name, eng):
                for o in self.ops[eng_name]:
                    for k, v in o["waits"]:
                        s, val = resolve(k, v)
                        eng.wait_ge(s, val)
                    if o["fn"] is None:
                        continue
                    ins = o["fn"](eng)
                    if o.get("tag"):
                        ins.annotate(o["tag"])
                    if o["dma"] is not None:
                        ins.then_inc(self.sems[o["dma"]], 16)
                    elif o["sig"]:
                        ins.then_inc(self.sems[eng_name], 1)
                for o2 in COMPUTE:
                    if o2 != eng_name and totals[o2] > 0:
                        eng.wait_ge(self.sems[o2], totals[o2])
                for k, n in self.dma_use.items():
                    eng.wait_ge(self.sems[k], n * 16)

            @block.tensor
            def _(e):
                run("pe", e)

            @block.scalar
            def _(e):
                run("act", e)

            @block.vector
            def _(e):
                run("dve", e)

            @block.gpsimd
            def _(e):
                run("pool", e)

            @block.sync
            def _(e):
                run("sp", e)

        self.base = totals
        self.ops = {e: [] for e in self.engs}
        self.waited = {e: {} for e in self.engs}
        self.res = {}

    def close(self):
        self.stack.close()


NBLK = 22
BLK_UA, BLK_UB, BLK_ZB, BLK_ZA, BLK_OA, BLK_G, BLK_AO, BLK_BO, BLK_WO, BLK_S5 = 0, 2, 3, 4, 6, 8, 12, 14, 15, 17
COL_UA, COL_ZA, COL_OA, COL_I, COL_UB, COL_ZB, COL_G = 0, 1024, 2048, 3072, 3080, 3592, 4104


def build_program(T, NPRE, NMAIN, dbg_stop=0):
    NCH = T // 128
    NC8 = T // 8
    nc = bass.Bass("TRN2", target_bir_lowering=False)
    dram = {}

    def din(name, shape):
        dram[name] = nc.dram_tensor(name, list(shape), F32, kind="ExternalInput").ap()
        return dram[name]

    x_pre = din("x_pre", [max(NPRE, 1) * T, 1024])
    x_main = din("x_main", [NMAIN * T, 1024])
    flag = din("flag", [128, 1])
    norm_pre_g = din("norm_pre_g", [1024]); w_in = din("w_in", [1024, 6152])
    conv_w = din("conv_w", [4, 1024]); conv_b = din("conv_b", [1024])
    w_q = din("w_q", [4, 256, 256]); w_k = din("w_k", [4, 256, 256]); w_v = din("w_v", [4, 256, 256])
    b_i = din("b_i", [1, 4]); b_f = din("b_f", [1, 4]); head_g = din("head_g", [1024]); skip_a = din("skip_a", [1024])
    w_a_out = din("w_a_out", [1024, 1024])
    lam_re = din("lam_re", [32, 64]); lam_im = din("lam_im", [32, 64]); log_dt = din("log_dt", [1, 32])
    B_re = din("B_re", [32, 64, 16]); B_im = din("B_im", [32, 64, 16])
    C_re = din("C_re", [32, 16, 64]); C_im = din("C_im", [32, 16, 64]); D_skip = din("D_skip", [32, 16])
    w_glu = din("w_glu", [512, 512]); b_glu = din("b_glu", [512]); w_b_out = din("w_b_out", [512, 1024])
    w_o = din("w_o", [1024, 1024]); norm_post_g = din("norm_post_g", [1, 1024])
    out = nc.dram_tensor("out", [NMAIN * T, 1024], F32, kind="ExternalOutput").ap()
    WS = nc.dram_tensor("wscratch", [NBLK, 128, 4096], BF16, kind="Internal").ap()

    P = Prog(nc)
    uid = [0]

    def sbt(shape, dt, name=None):
        uid[0] += 1
        return P.sb(name or ("t%d" % uid[0]), shape, dt)

    ident_f = sbt([128, 128], F32); ident_b = sbt([128, 128], BF16)
    maskT = sbt([128, 128], F32); mask4 = sbt([128, 4, 128], F32); ones_b = sbt([128, 128], BF16)
    P.op("pool", lambda e: e.memset(ident_f[:], 1.0), writes=["ident_f"])
    P.op("pool", lambda e: e.affine_select(out=ident_f[:], in_=ident_f[:], pattern=[[-1, 128]], compare_op=ALU.is_equal,
                                           fill=0.0, base=0, channel_multiplier=1), reads=["ident_f"], writes=["ident_f"])
    P.op("dve", lambda e: e.tensor_copy(out=ident_b[:], in_=ident_f[:]), reads=["ident_f"], writes=["ident_b"])
    P.op("pool", lambda e: e.memset(maskT[:], 1.0), writes=["maskT"])
    P.op("pool", lambda e: e.affine_select(out=maskT[:], in_=maskT[:], pattern=[[1, 128]], compare_op=ALU.is_ge,
                                           fill=0.0, base=0, channel_multiplier=-1), reads=["maskT"], writes=["maskT"])
    for h in range(4):
        P.op("pool", lambda e, h=h: e.tensor_copy(out=mask4[:, h, :], in_=maskT[:]), reads=["maskT"], writes=["mask4"])
    P.op("pool", lambda e: e.memset(ones_b[:], 1.0), writes=["ones_b"])

    gpre = sbt([128, 8], F32); cb = sbt([128, 8], F32); hg = sbt([128, 8], F32); skp = sbt([128, 8], F32)
    cw = sbt([128, 8, 4], F32); bglu = sbt([128, 4], F32); gpost = sbt([128, 1024], F32); bif = sbt([128, 8], F32)
    flg = sbt([128, 1], F32)
    nonc = dict(allow_slow_non_contiguous=True)
    P.dma(lambda e: e.dma_start(out=gpre[:], in_=norm_pre_g.rearrange("(k p) -> p k", p=128), **nonc), writes=["gpre"])
    P.dma(lambda e: e.dma_start(out=cb[:], in_=conv_b.rearrange("(k p) -> p k", p=128), **nonc), writes=["cb"])
    P.dma(lambda e: e.dma_start(out=hg[:], in_=head_g.rearrange("(k p) -> p k", p=128), **nonc), writes=["hg"])
    P.dma(lambda e: e.dma_start(out=skp[:], in_=skip_a.rearrange("(k p) -> p k", p=128), **nonc), writes=["skp"])
    for k in range(4):
        P.dma(lambda e, k=k: e.dma_start(out=cw[:, :, k], in_=conv_w[k].rearrange("(m p) -> p m", p=128), **nonc), writes=["cw"])
    P.dma(lambda e: e.dma_start(out=bglu[:], in_=b_glu.rearrange("(k p) -> p k", p=128), **nonc), writes=["bglu"])
    P.dma(lambda e: e.dma_start(out=gpost[:], in_=norm_post_g.partition_broadcast(128)), writes=["gpost"])
    P.dma(lambda e: e.dma_start(out=bif[:, 0:4], in_=b_i.partition_broadcast(128)), writes=["bif"])
    P.dma(lambda e: e.dma_start(out=bif[:, 4:8], in_=b_f.partition_broadcast(128)), writes=["bif"])
    P.dma(lambda e: e.dma_start(out=flg[:], in_=flag), writes=["flg"])

    Wqkv = [sbt([128, 4, 2, 256], BF16) for _ in range(3)]
    Wif = sbt([128, 8, 8], BF16); Wglu = sbt([128, 4, 512], BF16); cdiag = sbt([128, 8, 4, 128], BF16)
    AR32 = sbt([128, 2, 16], F32); ANI = sbt([128, 16], F32); API = sbt([128, 16], F32)
    pA = P.ps("pA", [128, 512], F32); pB = P.ps("pB", [128, 512], F32)
    pT = P.ps("pT", [128, 1024], BF16); pGD = P.ps("pGD", [128, 512], F32)
    pS = P.ps("pS", [128, 512], F32); pN0 = P.ps("pN0", [128, 512], F32); pN1 = P.ps("pN1", [128, 512], F32)
    pM = P.ps("pM", [128, 512], F32)
    P.temp = contextlib.ExitStack()
    stg = sbt([128, 4096], F32, "stg")
    stgb = sbt([128, 4096], BF16, "stgb")
    stgb2 = sbt([128, 4096], BF16, "stgb2")
    for wi, (wsrc, scl) in enumerate(((w_q, 1.0), (w_k, 1.0 / 16.0), (w_v, 1.0))):
        P.dma(lambda e, wsrc=wsrc: e.dma_start(out=stg[:, 0:2048].rearrange("p (h d n) -> p h d n", h=4, d=2),
                                               in_=wsrc.rearrange("h (d p) n -> p h d n", p=128)), writes=["stg"])
        P.op("dve", lambda e, wi=wi, scl=scl: e.tensor_scalar(out=Wqkv[wi][:].rearrange("p h d n -> p (h d n)"), in0=stg[:, 0:2048],
                                                              scalar1=scl, scalar2=None, op0=ALU.mult), reads=["stg"], writes=["Wqkv%d" % wi])
    Wq, Wk, Wv = Wqkv
    P.dma(lambda e: e.dma_start(out=stg[:, 0:64].rearrange("p (k n) -> p k n", k=8),
                                in_=w_in[:, COL_I:COL_I + 8].rearrange("(k p) n -> p k n", p=128), **nonc), writes=["stg"])
    for kt in range(8):
        P.op("dve", lambda e, kt=kt: e.tensor_scalar(out=Wif[:, kt, :], in0=stg[:, kt * 8:(kt + 1) * 8], scalar1=gpre[:, kt:kt + 1],
                                                     scalar2=None, op0=ALU.mult), reads=["stg", "gpre"], writes=["Wif"])
    P.dma(lambda e: e.dma_start(out=stg[:, 0:2048].rearrange("p (k n) -> p k n", k=4),
                                in_=w_glu.rearrange("(k p) n -> p k n", p=128)), writes=["stg"])
    P.op("dve", lambda e: e.tensor_copy(out=Wglu[:].rearrange("p k n -> p (k n)"), in_=stg[:, 0:2048]), reads=["stg"], writes=["Wglu"])
    for mt in range(8):
        for k in range(4):
            P.op("dve", lambda e, mt=mt, k=k: e.tensor_scalar(out=cdiag[:, mt, k, :], in0=ident_f[:], scalar1=cw[:, mt, k:k + 1],
                                                               scalar2=None, op0=ALU.mult), reads=["ident_f", "cw"], writes=["cdiag"])

    def stage_block(blk, src_ap_f, scale_gpre, nk):
        ncol = 4096 // nk
        P.dma(lambda e: e.dma_start(out=stg[:].rearrange("p (k n) -> p k n", k=nk), in_=src_ap_f), writes=["stg"])
        if scale_gpre:
            for kt in range(nk):
                P.op("act",
                     lambda e, kt=kt: e.activation(out=stgb[:, kt * ncol:(kt + 1) * ncol], in_=stg[:, kt * ncol:(kt + 1) * ncol],
                                                   func=AF.Copy, scale=gpre[:, kt:kt + 1]),
                     reads=["stg", "gpre"], writes=["stgb"])
        else:
            P.op("act", lambda e: e.copy(out=stgb[:, 0:2048], in_=stg[:, 0:2048]), reads=["stg"], writes=["stgb"])
            P.op("act", lambda e: e.copy(out=stgb[:, 2048:4096], in_=stg[:, 2048:4096]), reads=["stg"], writes=["stgb"])
        P.dma(lambda e: e.dma_start(out=WS[blk], in_=stgb[:]), reads=["stgb"], writes=["WS%d" % blk])

    def win_cols(c0):
        return w_in[:, c0:c0 + 512].rearrange("(k p) n -> p k n", p=128)

    win_blocks = [(BLK_UA, COL_UA), (BLK_UA + 1, COL_UA + 512), (BLK_UB, COL_UB), (BLK_ZB, COL_ZB), (BLK_ZA, COL_ZA),
                  (BLK_ZA + 1, COL_ZA + 512), (BLK_OA, COL_OA), (BLK_OA + 1, COL_OA + 512)] + [(BLK_G + i, COL_G + 512 * i) for i in range(4)]
    pending = []
    for blk, c0 in win_blocks:
        pending.append((blk, win_cols(c0), True, 8))
    for i in range(2):
        pending.append((BLK_AO + i, w_a_out[:, 512 * i:512 * i + 512].rearrange("(k p) n -> p k n", p=128), False, 8))
        pending.append((BLK_WO + i, w_o[:, 512 * i:512 * i + 512].rearrange("(k p) n -> p k n", p=128), False, 8))
    pending.append((BLK_BO, w_b_out.rearrange("(k p) n -> p k n", p=128), False, 4))

    def stage_some(n=1):
        for _ in range(n):
            if pending:
                stage_block(*pending.pop(0))


    def small(n, name=None):
        return sbt([128, n], F32, name)

    cnt = [0]

    def el(eng, fn, r, w):
        P.op(eng, fn, reads=r, writes=w)

    def tt(outp, a, b, op, r, w, eng="dve"):
        el(eng, lambda e: e.tensor_tensor(out=outp, in0=a, in1=b, op=op), r, w)

    def ts(outp, a, s1, s2, op0, op1, r, w, eng="dve"):
        if op1 is None:
            el(eng, lambda e: e.tensor_scalar(out=outp, in0=a, scalar1=s1, scalar2=None, op0=op0), r, w)
        else:
            el(eng, lambda e: e.tensor_scalar(out=outp, in0=a, scalar1=s1, scalar2=s2, op0=op0, op1=op1), r, w)

    LR = small(32); LI = small(32); DT = small(32)
    for hf in range(2):
        sl = slice(64 * hf, 64 * hf + 64)
        P.dma(lambda e, sl=sl: e.dma_start(out=LR[sl, :], in_=lam_re.rearrange("g p -> p g"), **nonc), writes=["LR"])
        P.dma(lambda e, sl=sl: e.dma_start(out=LI[sl, :], in_=lam_im.rearrange("g p -> p g"), **nonc), writes=["LI"])
    P.dma(lambda e: e.dma_start(out=DT[:], in_=log_dt.partition_broadcast(128)), writes=["DT"])
    el("act", lambda e: e.activation(out=DT[:], in_=DT[:], func=AF.Exp), ["DT"], ["DT"])
    TH = small(32); MAG = small(32); t0 = small(32); t1 = small(32); t2 = small(32); kk = small(32)
    tt(TH[:], LI[:], DT[:], ALU.mult, ["LI", "DT"], ["TH"])
    tt(t0[:], LR[:], DT[:], ALU.mult, ["LR", "DT"], ["t0"])
    el("act", lambda e: e.activation(out=MAG[:], in_=t0[:], func=AF.Exp), ["t0"], ["MAG"])
    IMAG2 = small(32)
    el("act", lambda e: e.activation(out=IMAG2[:], in_=t0[:], func=AF.Exp, scale=-2.0), ["t0"], ["IMAG2"])

    def sin_of(dst, src, shift, nm):
        ts(t1[:], src, shift, None, ALU.add, None, [nm, "t1"], ["t1"])
        el("pool", lambda e: e.memset(kk[:], 0.0), [], ["kk"])
        for m in range(7):
            ts(t2[:], t1[:], (2 * m + 1) * PI, None, ALU.is_gt, None, ["t1"], ["t2"])
            tt(kk[:], kk[:], t2[:], ALU.add, ["kk", "t2"], ["kk"])
        ts(kk[:], kk[:], -2.0 * PI, None, ALU.mult, None, ["kk"], ["kk"])
        tt(t1[:], t1[:], kk[:], ALU.add, ["t1", "kk"], ["t1"])
        el("act", lambda e: e.activation(out=dst, in_=t1[:], func=AF.Sin), ["t1"], [nm + "_s"])

    SN = small(32); CS = small(32)
    sin_of(SN[:], TH[:], 0.0, "TH")
    sin_of(CS[:], TH[:], PI / 2.0, "TH")
    pwr = sbt([128, 9, 32], F32); pwi = sbt([128, 9, 32], F32); pnr = sbt([128, 8, 32], F32); pni = sbt([128, 8, 32], F32)
    el("pool", lambda e: e.memset(pwr[:, 0, :], 1.0), [], ["pw"]); el("pool", lambda e: e.memset(pwi[:, 0, :], 0.0), [], ["pw"])
    el("pool", lambda e: e.memset(pnr[:, 0, :], 1.0), [], ["pn"]); el("pool", lambda e: e.memset(pni[:, 0, :], 0.0), [], ["pn"])
    tt(pwr[:, 1, :], MAG[:], CS[:], ALU.mult, ["MAG", "TH_s"], ["pw"])
    tt(pwi[:, 1, :], MAG[:], SN[:], ALU.mult, ["MAG", "TH_s"], ["pw"])
    tt(pnr[:, 1, :], pwr[:, 1, :], IMAG2[:], ALU.mult, ["pw", "IMAG2"], ["pn"])
    tt(t0[:], pwi[:, 1, :], IMAG2[:], ALU.mult, ["pw", "IMAG2"], ["t0"])
    ts(pni[:, 1, :], t0[:], -1.0, None, ALU.mult, None, ["t0"], ["pn"])

    def cmul(or_, oi_, ar, ai, br, bi, r, w):
        raise NotImplementedError

    u0 = small(32); u1 = small(32)
    for k in range(1, 8):
        for (xr, xi, nm, lim) in ((pwr, pwi, "pw", 9), (pnr, pni, "pn", 8)):
            if k + 1 >= lim:
                continue
            tt(u0[:], xr[:, k, :], xr[:, 1, :], ALU.mult, [nm], ["u0"])
            tt(u1[:], xi[:, k, :], xi[:, 1, :], ALU.mult, [nm], ["u1"])
            tt(xr[:, k + 1, :], u0[:], u1[:], ALU.subtract, ["u0", "u1"], [nm])
            tt(u0[:], xr[:, k, :], xi[:, 1, :], ALU.mult, [nm], ["u0"])
            tt(u1[:], xi[:, k, :], xr[:, 1, :], ALU.mult, [nm], ["u1"])
            tt(xi[:, k + 1, :], u0[:], u1[:], ALU.add, ["u0", "u1"], [nm])
    den = small(32); qr = small(32); qi = small(32); nr = small(32)
    tt(u0[:], LR[:], LR[:], ALU.mult, ["LR"], ["u0"]); tt(u1[:], LI[:], LI[:], ALU.mult, ["LI"], ["u1"])
    tt(den[:], u0[:], u1[:], ALU.add, ["u0", "u1"], ["den"])
    el("dve", lambda e: e.reciprocal(out=den[:], in_=den[:]), ["den"], ["den"])
    ts(nr[:], pwr[:, 1, :], -1.0, None, ALU.add, None, ["pw"], ["nr"])
    tt(u0[:], nr[:], LR[:], ALU.mult, ["nr", "LR"], ["u0"]); tt(u1[:], pwi[:, 1, :], LI[:], ALU.mult, ["pw", "LI"], ["u1"])
    tt(qr[:], u0[:], u1[:], ALU.add, ["u0", "u1"], ["qr"]); tt(qr[:], qr[:], den[:], ALU.mult, ["qr", "den"], ["qr"])
    tt(u0[:], pwi[:, 1, :], LR[:], ALU.mult, ["pw", "LR"], ["u0"]); tt(u1[:], nr[:], LI[:], ALU.mult, ["nr", "LI"], ["u1"])
    tt(qi[:], u0[:], u1[:], ALU.subtract, ["u0", "u1"], ["qi"]); tt(qi[:], qi[:], den[:], ALU.mult, ["qi", "den"], ["qi"])
    Br = sbt([128, 32, 16], F32); Bi = sbt([128, 32, 16], F32); bbr = sbt([128, 32, 16], F32); bbi = sbt([128, 32, 16], F32)
    v0 = sbt([128, 32, 16], F32); v1 = sbt([128, 32, 16], F32)
    for hf in range(2):
        sl = slice(64 * hf, 64 * hf + 64)
        P.dma(lambda e, sl=sl: e.dma_start(out=Br[sl], in_=B_re.rearrange("g p n -> p g n"), **nonc), writes=["Br"])
        P.dma(lambda e, sl=sl: e.dma_start(out=Bi[sl], in_=B_im.rearrange("g p n -> p g n"), **nonc), writes=["Bi"])

    def bc(s):
        return s.unsqueeze(2).to_broadcast([128, 32, 16])

    def cmul3(orr, oii, sr, si, sn, xr, xi, xn, on):
        xn = [xn] if isinstance(xn, str) else list(xn)
        sn = [sn] if isinstance(sn, str) else list(sn)
        tt(v0[:], xr, bc(sr), ALU.mult, xn + sn, ["v0"]); tt(v1[:], xi, bc(si), ALU.mult, xn + sn, ["v1"])
        tt(orr, v0[:], v1[:], ALU.subtract, ["v0", "v1"], [on])
        tt(v0[:], xi, bc(sr), ALU.mult, xn + sn, ["v0"]); tt(v1[:], xr, bc(si), ALU.mult, xn + sn, ["v1"])
        tt(oii, v0[:], v1[:], ALU.add, ["v0", "v1"], [on])

    el("dve", lambda e: e.tensor_copy(out=u0[:], in_=qr[:]), ["qr"], ["qq"])
    cmul3(bbr[:], bbi[:], qr[:], qi[:], ["qr", "qi"], Br[:], Bi[:], ["Br", "Bi"], "bb")
    CTr = sbt([128, 32, 16], F32); CTi = sbt([128, 32, 16], F32)
    Cdup = sbt([128, 4, 2, 64], F32)
    for (Csrc, CTt, nm) in ((C_re, CTr, "CTr"), (C_im, CTi, "CTi")):
        for d in range(2):
            P.dma(lambda e, Csrc=Csrc, d=d: e.dma_start(out=Cdup[:, :, d, :], in_=Csrc.rearrange("(t g) n p -> (g n) t p", t=4)),
                  writes=["Cdup"])
        for t in range(4):
            P.op("pe", lambda e, t=t: e.transpose(out=pA[:, t * 128:(t + 1) * 128], in_=Cdup[:, t, :, :].rearrange("q d p -> q (d p)"),
                                                  identity=ident_f[:]), reads=["Cdup", "ident_f"], writes=["pA"])
        el("act", lambda e, CTt=CTt: e.copy(out=CTt[:].rearrange("p g n -> p (g n)"), in_=pA[:]), ["pA"], [nm])
    stage_some(100)
    Er = sbt([128, 32, 8, 16], F32); Ei = sbt([128, 32, 8, 16], F32)
    Fr = sbt([128, 32, 8, 16], F32); Fi = sbt([128, 32, 8, 16], F32)
    for j in range(8):
        cmul3(Er[:, :, j, :], Ei[:, :, j, :], pwr[:, 7 - j, :], pwi[:, 7 - j, :], "pw", bbr[:], bbi[:], "bb", "E")
        cmul3(Fr[:, :, j, :], Fi[:, :, j, :], pwr[:, j + 1, :], pwi[:, j + 1, :], "pw", CTr[:], CTi[:], ["CTr", "CTi"], "F")
    halfm = sbt([128, 2], F32)
    el("pool", lambda e: e.memset(halfm[:], 0.0), [], ["halfm"])
    el("pool", lambda e: e.memset(halfm[0:64, 0:1], 1.0), ["halfm"], ["halfm"])
    el("pool", lambda e: e.memset(halfm[64:128, 1:2], 1.0), ["halfm"], ["halfm"])
    for ri, (Et, blk) in enumerate(((Er, BLK_S5), (Ei, BLK_S5 + 1))):
        el("pool", lambda e: e.memset(stgb2[:], 0.0), ["stgb2"], ["stgb2"])
        for g in range(32):
            ps_ = pA if g % 2 == 0 else pB
            nm = "pA" if g % 2 == 0 else "pB"
            P.op("pe", lambda e, Et=Et, g=g, ps_=ps_: e.transpose(out=ps_[:, 0:64], in_=Et[0:64, g, :, :].rearrange("p j n -> p (j n)"),
                                                                  identity=ident_f[0:64, 0:64]), reads=["E", "ident_f"], writes=[nm])
            c0 = g * 128 + 64 * (g % 2)
            el("dve", lambda e, ps_=ps_, c0=c0: e.tensor_copy(out=stgb2[:, c0:c0 + 64], in_=ps_[:, 0:64]), [nm], ["stgb2"])
        P.dma(lambda e, blk=blk: e.dma_start(out=WS[blk], in_=stgb2[:]), reads=["stgb2"], writes=["WS%d" % blk], q="pool")
    for ri, (Ft, blk, sg) in enumerate(((Fr, BLK_S5 + 3, 1.0), (Fi, BLK_S5 + 4, -1.0))):
        for g in range(32):
            P.op("dve",
                 lambda e, Ft=Ft, g=g, sg=sg: e.tensor_scalar(out=stgb2[:, g * 128:(g + 1) * 128], in0=Ft[:, g, :, :].rearrange("p j n -> p (j n)"),
                                                              scalar1=halfm[:, (g % 2):(g % 2) + 1], scalar2=sg, op0=ALU.mult, op1=ALU.mult),
                 reads=["F", "halfm"], writes=["stgb2"])
        P.dma(lambda e, blk=blk: e.dma_start(out=WS[blk], in_=stgb2[:]), reads=["stgb2"], writes=["WS%d" % blk], q="pool")
    Gr = sbt([128, 32, 8, 16], F32); Gni = sbt([128, 32, 8, 16], F32)
    i8r = small(32); i8i = small(32)
    tt(u0[:], pnr[:, 7, :], pnr[:, 1, :], ALU.mult, ["pn"], ["u0"]); tt(u1[:], pni[:, 7, :], pni[:, 1, :], ALU.mult, ["pn"], ["u1"])
    tt(i8r[:], u0[:], u1[:], ALU.subtract, ["u0", "u1"], ["i8"])
    tt(u0[:], pnr[:, 7, :], pni[:, 1, :], ALU.mult, ["pn"], ["u0"]); tt(u1[:], pni[:, 7, :], pnr[:, 1, :], ALU.mult, ["pn"], ["u1"])
    tt(i8i[:], u0[:], u1[:], ALU.add, ["u0", "u1"], ["i8"])
    for j in range(8):
        cmul3(Gr[:, :, j, :], Gni[:, :, j, :], i8r[:], i8i[:], "i8", Fr[:, :, j, :], Fi[:, :, j, :], "F", "G")
    ts(Gni[:], Gni[:], -1.0, None, ALU.mult, None, ["G"], ["G"])
    bmask = sbt([128, 8, 16], F32); Dcol = sbt([128, 32], F32)
    el("pool", lambda e: e.memset(bmask[:], 1.0), [], ["bmask"])
    el("pool", lambda e: e.affine_select(out=bmask[:], in_=bmask[:], pattern=[[16, 8], [0, 16]], compare_op=ALU.is_ge, fill=0.0,
                                         base=15, channel_multiplier=-1), ["bmask"], ["bmask"])
    for j in range(8):
        P.dma(lambda e, j=j: e.dma_start(out=Dcol[16 * j:16 * j + 16, :], in_=D_skip.rearrange("g n -> n g"), **nonc), writes=["Dcol"], q="pool")
    wtmp = sbt([128, 128], F32)
    for g in range(32):
        ps_ = pA if g % 2 == 0 else pB
        nm = "pA" if g % 2 == 0 else "pB"
        P.op("pe", lambda e, g=g, ps_=ps_: e.matmul(ps_[:, 0:128], lhsT=Er[0:64, g, :, :].rearrange("p j n -> p (j n)"),
                                                    rhs=Gr[0:64, g, :, :].rearrange("p j n -> p (j n)"), start=True, stop=False),
             reads=["E", "G"], writes=[nm], sig=False)
        P.op("pe", lambda e, g=g, ps_=ps_: e.matmul(ps_[:, 0:128], lhsT=Ei[0:64, g, :, :].rearrange("p j n -> p (j n)"),
                                                    rhs=Gni[0:64, g, :, :].rearrange("p j n -> p (j n)"), start=False, stop=True),
             reads=["E", "G"], writes=[nm])
        tt(wtmp[:], ps_[:, 0:128], bmask[:].rearrange("p j n -> p (j n)"), ALU.mult, [nm, "bmask"], ["wtmp"])
        el("dve", lambda e, g=g: e.scalar_tensor_tensor(out=stgb2[:, g * 128:(g + 1) * 128], in0=ident_f[:], scalar=Dcol[:, g:g + 1],
                                                        in1=wtmp[:], op0=ALU.mult, op1=ALU.add), ["ident_f", "Dcol", "wtmp", "stgb2"], ["stgb2"])
    P.dma(lambda e: e.dma_start(out=WS[BLK_S5 + 2], in_=stgb2[:]), reads=["stgb2"], writes=["WS%d" % (BLK_S5 + 2)], q="pool")
    for g2 in range(2):
        sl = slice(64 * g2, 64 * g2 + 64)
        src_r = pwr[sl, 8, :].rearrange("p (q t) -> p q t", t=2)[:, :, g2]
        src_i = pwi[sl, 8, :].rearrange("p (q t) -> p q t", t=2)[:, :, g2]
        el("dve", lambda e, sl=sl, src_r=src_r: e.tensor_copy(out=AR32[sl, 0, :], in_=src_r), ["pw"], ["AR32"])
        el("dve", lambda e, sl=sl, src_r=src_r: e.tensor_copy(out=AR32[sl, 1, :], in_=src_r), ["pw"], ["AR32"])
        el("dve", lambda e, sl=sl, src_i=src_i: e.tensor_copy(out=API[sl, :], in_=src_i), ["pw"], ["API"])
        ts(ANI[sl, :], src_i, -1.0, None, ALU.mult, None, ["pw"], ["ANI"])

    stage_some(100)
    if dbg_stop == 1:
        tk = P.dma(lambda e: e.dma_start(out=out[0:128, :], in_=gpost[:]), reads=["gpost"])
        P.final_wait("sp", [tk])
        P.emit(); P.temp.close(); P.close()
        return nc
    P.emit()
    P.temp.close()
    P.temp = None
    NSLOT = 3
    wslot = [sbt([128, 4096], BF16, "wslot%d" % i) for i in range(NSLOT)]
    slot_rr = [0]

    pre = {}

    def prefetch(blk):
        pre[blk] = load_block(blk, force=True)

    def load_block(blk, force=False):
        if not force and blk in pre:
            return pre.pop(blk)
        s = slot_rr[0] % NSLOT
        slot_rr[0] += 1
        P.dma(lambda e: e.dma_start(out=wslot[s][:], in_=WS[blk]), reads=["WS%d" % blk], writes=["wslot%d" % s])
        return wslot[s], "wslot%d" % s

    xs = [sbt([128, 1024], F32, "xs%d" % i) for i in range(2)]
    xr = [sbt([128, 1024], F32, "xr%d" % i) for i in range(2)]
    junk = sbt([128, 1024], BF16); xnb = sbt([128, 1024], BF16)
    ss = small(1); rstd = small(1)
    xnT = sbt([128, 8, T], BF16, "xnT"); uaT = sbt([128, 8, T + 4], BF16, "uaT"); cT = sbt([128, 8, T], BF16, "cT")
    zaT = sbt([128, 8, T], BF16, "zaT"); oaT = sbt([128, 8, T], BF16, "oaT"); zbT = sbt([128, 4, T], BF16, "zbT")
    qT = sbt([128, 4, 2, T], BF16, "qT"); kT = sbt([128, 4, 2, T], BF16, "kT")
    ktm = sbt([128, NCH, 4, 256], BF16, "ktm"); vw = sbt([128, NCH, 4, 264], BF16, "vw")
    gates = sbt([128, NCH, 8], F32, "gates")
    hfT = sbt([128, 8, T], BF16, "hfT"); csz = sbt([128, 8, T], BF16, "csz"); mrgT = zaT; hbT = sbt([128, 4, T], BF16, "hbT")
    Cst = sbt([128, 4, 2, 264], F32, "Cst"); Cbf = sbt([128, 4, 2, 264], BF16, "Cbf")
    nst = sbt([128, 4, 2], F32, "nst"); nrep = sbt([128, 4, 2, 128], BF16, "nrep")
    Utm = sbt([128, 32, 8, 16], BF16, "Utm"); U2 = sbt([128, 32, NC8], BF16, "U2")
    Xall = sbt([128, NC8, 2, 16], F32, "Xall"); Sall = sbt([128, NC8 + 1, 2, 16], F32, "Sall"); Sbf = sbt([128, 2, 16, NC8], BF16, "Sbf")
    Yg = sbt([128, 32, NC8], BF16, "Yg"); Ytm = Utm[:].rearrange("p g j n -> p (g j n)").rearrange("p (j c) -> p j c", j=8); yT = sbt([128, 4, T], BF16, "yT")
    for (t_, nm) in ((Cst, "Cst"), (Cbf, "Cbf"), (nst, "nst"), (nrep, "nrep"), (Sall, "Sall"), (uaT, "uaT"), (vw, "vw")):
        flat = t_[:]
        wnames = [nm] + (["%s%d" % (nm, h_) for h_ in range(4)] if nm in ("Cst", "Cbf", "nrep") else [])
        el("pool", lambda e, flat=flat: e.memset(flat, 0.0), [], wnames)

    e1a = sbt([128, NCH, 4], F32); lfa = sbt([128, NCH, 4], F32); emb4 = sbt([128, NCH, 4, 128], F32); dec4 = sbt([128, NCH, 4], F32)
    wcol4 = sbt([128, NCH, 4], F32); wcolb4 = sbt([128, NCH, 4, 2], BF16); wrep4 = sbt([128, NCH, 4, 128], BF16)
    lfrep = sbt([128, 4, 128], F32); lf = small(4, "lf"); ig = small(4); bcol = small(4); wcol = small(4, "wcol"); wcolb = sbt([128, 4, 2], BF16)
    wrep = sbt([128, 4, 128], BF16); emb = sbt([128, 4, 128], F32, "emb"); dec = small(4, "dec"); SmT = sbt([128, 4, 128], BF16, "SmT")
    aden = sbt([128, 4, 128], F32); rden = sbt([128, 4, 128], F32, "rden"); hT = sbt([128, 8, 128], F32, "hT"); sq = sbt([128, 8, 128], BF16)
    rsh = sbt([128, 4, 128], F32); hn = sbt([128, 8, 128], F32, "hn"); e1 = small(4)
    gAll = oaT; m1 = sbt([128, T], F32); m2 = sbt([128, T], F32)
    ot = [sbt([128, 1024], F32, "ot%d" % i) for i in range(2)]
    ss2 = small(1); rstd2 = small(1); ss3 = small(1); rstd3 = small(1); sgl = sbt([128, T], BF16); xg = sbt([128, T], F32)

    def rsqrt_col(dst, src, scale, nm_src, nm_dst):
        ts(dst, src, scale, 1e-6, ALU.mult, ALU.add, [nm_src], [nm_dst])
        el("act", lambda e: e.activation(out=dst, in_=dst, func=AF.Ln), [nm_dst], [nm_dst])
        el("act", lambda e: e.activation(out=dst, in_=dst, func=AF.Exp, scale=-0.5), [nm_dst], [nm_dst])

    def mm(outp, lhsT, rhs, start, stop, r, w, sig=None):
        P.op("pe", lambda e: e.matmul(outp, lhsT=lhsT, rhs=rhs, start=start, stop=stop), reads=r, writes=w,
             sig=(stop if sig is None else sig))

    acc_rr = [0]

    def acc_bank():
        acc_rr[0] += 1
        return (pA, "pA") if acc_rr[0] % 2 else (pB, "pB")

    out_toks = []

    class StopBuild(Exception):
        pass

    def chk(level):
        if dbg_stop == level:
            raise StopBuild()

    prefetched = [False]
    a_done = [None]
    nxt2 = {}

    def issue_x(src_, r0, sub):
        xb2, xn2 = xs[sub % 2], "xs%d" % (sub % 2)
        P.dma(lambda e: e.dma_start(out=xb2[:], in_=src_[r0 + sub * 128: r0 + (sub + 1) * 128, :]), writes=[xn2])

    def stage_A(xsrc, row0):
        for sub in range(NCH):
            xb_, xnm = xs[sub % 2], "xs%d" % (sub % 2)
            if not (prefetched[0] and sub < 2):
                issue_x(xsrc, row0, sub)
            el("act", lambda e, xb_=xb_: e.activation(out=junk[:], in_=xb_[:], func=AF.Square, accum_out=ss[:]), [xnm], ["junk", "ss"])
            rsqrt_col(rstd[:], ss[:], 1.0 / 1024.0, "ss", "rstd")
            ts(xnb[:], xb_[:], rstd[:, 0:1], None, ALU.mult, None, [xnm, "rstd"], ["xnb"])
            for kt in range(8):
                P.op("pe", lambda e, kt=kt: e.transpose(out=pT[:, kt * 128:(kt + 1) * 128], in_=xnb[:, kt * 128:(kt + 1) * 128], identity=ident_b[:]),
                     reads=["xnb", "ident_b"], writes=["pT"], sig=(kt == 7))
            el("act", lambda e, sub=sub: e.copy(out=xnT[:, :, sub * 128:(sub + 1) * 128], in_=pT[:].rearrange("p (k t) -> p k t", k=8)),
               ["pT"], ["xnT"])
        a_done[0] = (id(xsrc), row0)
        prefetched[0] = False
        n2 = nxt2.get((id(xsrc), row0))
        if n2 is not None and NCH <= 2:
            for sub in range(min(2, NCH)):
                issue_x(n2[0], n2[1], sub)
            prefetched[0] = True

    def do_tile(xsrc, row0, main, ti, nxt=None):
        if a_done[0] != (id(xsrc), row0):
            stage_A(xsrc, row0)

        def early_A():
            if nxt is not None and EARLY_A:
                stage_A(nxt[0], nxt[1])

        chk(2)

        def proj_fm(blk, ncolt, evac):
            W, wn = load_block(blk)
            Wv_ = W[:].rearrange("p (k n) -> p k n", k=8)
            for ct in range(ncolt):
                ps_, pn = acc_bank()
                for kt in range(8):
                    mm(ps_[:, 0:T], Wv_[:, kt, ct * 128:(ct + 1) * 128], xnT[:, kt, :], kt == 0, kt == 7, [wn, "xnT"], [pn])
                evac(ct, ps_, pn)

        el("dve", lambda e: e.tensor_copy(out=uaT[:, :, 1:4], in_=uaT[:, :, T + 1:T + 4]), ["uaT"], ["uaT"])
        for half in range(2):
            proj_fm(BLK_UA + half, 4, lambda ct, ps_, pn, half=half: el(
                "act", lambda e: e.copy(out=uaT[:, half * 4 + ct, 4:T + 4], in_=ps_[:, 0:T]), [pn], ["uaT"]))
        chk(3)
        for ch in range(NCH):
            for kt in range(8):
                mm(pM[:, 0:8], xnT[:, kt, ch * 128:(ch + 1) * 128], Wif[:, kt, :], kt == 0, kt == 7, ["xnT", "Wif"], ["pM"])
            tt(gates[:, ch, :], pM[:, 0:8], bif[:], ALU.add, ["pM", "bif"], ["gates"])
        chk(4)
        W, wn = load_block(BLK_UB)
        Wv_ = W[:].rearrange("p (k n) -> p k n", k=8)
        for j in range(8):
            ps_, pn = acc_bank()
            for kt in range(8):
                lhs = xnT[:, kt, :].rearrange("p (c j) -> p j c", j=8)[:, j, :]
                mm(ps_[0:NC8, :], lhs, Wv_[:, kt, :], kt == 0, kt == 7, [wn, "xnT"], [pn])
            el("act" if j % 2 else "dve",
               (lambda e, ps_=ps_, j=j: e.copy(out=Utm[0:NC8, :, j, :], in_=ps_[0:NC8, :].rearrange("c (g n) -> c g n", g=32))) if j % 2 else
               (lambda e, ps_=ps_, j=j: e.tensor_copy(out=Utm[0:NC8, :, j, :], in_=ps_[0:NC8, :].rearrange("c (g n) -> c g n", g=32))), [pn], ["Utm"])
        if not main:
            early_A()
        chk(5)
        for g in range(32):
            P.op("pe", lambda e, g=g: e.transpose(out=pT[:, g * NC8:(g + 1) * NC8], in_=Utm[0:NC8, g, :, :].rearrange("c j n -> c (j n)"),
                                                  identity=ident_b[0:NC8, 0:NC8]), reads=["Utm", "ident_b"], writes=["pT"], sig=(g == 31))
        el("act", lambda e: e.copy(out=U2[:].rearrange("p g c -> p (g c)"), in_=pT[:, 0:32 * NC8]), ["pT"], ["U2"])
        W1r, w1rn = load_block(BLK_S5)
        W1i, w1in = load_block(BLK_S5 + 1)
        for ri, (Wt, wn_) in enumerate(((W1r, w1rn), (W1i, w1in))):
            Wg = Wt[:].rearrange("p (g n) -> p g n", g=32)
            for q in range(16):
                for g2 in range(2):
                    g = 2 * q + g2
                    mm(pM[:, (q * NC8):(q + 1) * NC8], Wg[:, g, :], U2[:, g, :], g2 == 0, g2 == 1, [wn_, "U2"], ["pM"], sig=(q == 15 and g2 == 1))
            el("act" if ri == 0 else "dve",
               (lambda e, ri=ri: e.copy(out=Xall[:, :, ri, :].rearrange("p c q -> p q c"), in_=pM[:, 0:16 * NC8].rearrange("p (q c) -> p q c", q=16))) if ri == 0 else
               (lambda e, ri=ri: e.tensor_copy(out=Xall[:, :, ri, :].rearrange("p c q -> p q c"), in_=pM[:, 0:16 * NC8].rearrange("p (q c) -> p q c", q=16))),
               ["pM"], ["Xall"])
        chk(6)
        T1 = sbt([128, 2, 16], F32, "scT1_%d" % ti) if False else None
        for c in range(NC8):
            sp_ = Sall[:, c, :, :]
            sn_ = Sall[:, c + 1, :, :]
            P.op("pool", lambda e, sp_=sp_: e.tensor_tensor(out=sc_t1[:], in0=sp_, in1=AR32[:], op=ALU.mult), reads=["Sall"], writes=["sc_t1"])
            P.op("pool", lambda e, c=c: e.tensor_tensor(out=sc_t2[:, 0, :], in0=Sall[:, c, 1, :], in1=ANI[:], op=ALU.mult), reads=["Sall"], writes=["sc_t2"])
            P.op("pool", lambda e, c=c: e.tensor_tensor(out=sc_t2[:, 1, :], in0=Sall[:, c, 0, :], in1=API[:], op=ALU.mult), reads=["Sall"], writes=["sc_t2"])
            P.op("pool", lambda e, c=c: e.tensor_tensor(out=sc_t1[:], in0=sc_t1[:], in1=Xall[:, c, :, :], op=ALU.add), reads=["sc_t1", "Xall"], writes=["sc_t1"])
            P.op("pool", lambda e, sn_=sn_: e.tensor_tensor(out=sn_, in0=sc_t1[:], in1=sc_t2[:], op=ALU.add), reads=["sc_t1", "sc_t2"], writes=["Sall"])
        if main:
            for ri in range(2):
                el("pool", lambda e, ri=ri: e.tensor_copy(out=Sbf[:, ri, :, :], in_=Sall[:, 0:NC8, ri, :].rearrange("p c q -> p q c")),
                   ["Sall"], ["Sbf"])
        el("pool", lambda e: e.tensor_copy(out=Sall[:, 0, :, :], in_=Sall[:, NC8, :, :]), ["Sall"], ["Sall"])
        def s5_out():
            proj_fm(BLK_ZB, 4, lambda ct, ps_, pn: el("act", lambda e: e.activation(out=zbT[:, ct, :], in_=ps_[:, 0:T], func=AF.Silu), [pn], ["zbT"]))
            Wi_, win_ = load_block(BLK_S5 + 2)
            Wr_, wrn_ = load_block(BLK_S5 + 3)
            Wm_, wmn_ = load_block(BLK_S5 + 4)
            Wi_g = Wi_[:].rearrange("p (g n) -> p g n", g=32); Wr_g = Wr_[:].rearrange("p (g n) -> p g n", g=32)
            Wm_g = Wm_[:].rearrange("p (g n) -> p g n", g=32)
            GP = 512 // NC8
            for g0 in range(0, 32, GP):
                ng = min(GP, 32 - g0)
                for gi in range(ng):
                    g = g0 + gi
                    o_ = pM[:, gi * NC8:(gi + 1) * NC8]
                    mm(o_, Wi_g[:, g, :], U2[:, g, :], True, False, [win_, "U2"], ["pM"], sig=False)
                    mm(o_, Wr_g[:, g, :], Sbf[:, 0, g // 2, :], False, False, [wrn_, "Sbf"], ["pM"], sig=False)
                    mm(o_, Wm_g[:, g, :], Sbf[:, 1, g // 2, :], False, True, [wmn_, "Sbf"], ["pM"], sig=(gi == ng - 1))
                el("act", lambda e, g0=g0, ng=ng: e.activation(out=Yg[:, g0:g0 + ng, :].rearrange("p g c -> p (g c)"), in_=pM[:, 0:ng * NC8],
                                                               func=AF.Gelu_apprx_tanh), ["pM"], ["Yg"])
            prefetch(BLK_G + 0); prefetch(BLK_G + 1)
            for g0 in range(0, 32, 8):
                for gi in range(8):
                    g = g0 + gi
                    P.op("pe", lambda e, g=g, gi=gi: e.transpose(out=pT[0:NC8, gi * 128:(gi + 1) * 128], in_=Yg[:, g, :], identity=ident_b[:]),
                         reads=["Yg", "ident_b"], writes=["pT"], sig=(gi == 7))
                el("act", lambda e, g0=g0: e.copy(out=Ytm[0:NC8, :, 16 * g0:16 * g0 + 128].rearrange("c j (g n) -> c g j n", g=8),
                                                  in_=pT[0:NC8, :].rearrange("c (g j n) -> c g j n", g=8, j=8)), ["pT"], ["Utm"])
            for ct in range(4):
                for j in range(8):
                    P.op("pe", lambda e, ct=ct, j=j: e.transpose(out=pT[:, j * NC8:(j + 1) * NC8], in_=Ytm[0:NC8, j, ct * 128:(ct + 1) * 128],
                                                                 identity=ident_b[0:NC8, 0:NC8]), reads=["Utm", "ident_b"], writes=["pT"], sig=(j == 7))
                el("act", lambda e, ct=ct: e.copy(out=yT[:, ct, :], in_=pT[:, 0:T]), ["pT"], ["yT"])
            for ot_ in range(4):
                ps_, pn = acc_bank()
                for ct in range(4):
                    mm(ps_[:, 0:T], Wglu[:, ct, ot_ * 128:(ot_ + 1) * 128], yT[:, ct, :], ct == 0, ct == 3, ["Wglu", "yT"], [pn])
                el("act", lambda e, ps_=ps_, ot_=ot_: e.activation(out=sgl[:], in_=ps_[:, 0:T], func=AF.Sigmoid, bias=bglu[:, ot_:ot_ + 1]),
                   [pn, "bglu"], ["sgl"])
                tt(xg[:], sgl[:], yT[:, ot_, :], ALU.mult, ["sgl", "yT"], ["xg"])
                tt(hbT[:, ot_, :].rearrange("p (j c) -> p j c", j=8), xg[:].rearrange("p (j c) -> p j c", j=8),
                   zbT[:, ot_, :].rearrange("p (c j) -> p j c", j=8), ALU.mult, ["xg", "zbT"], ["hbT"])
        chk(7)
        for mt in range(8):
            ps_, pn = acc_bank()
            for k in range(4):
                mm(ps_[:, 0:T], cdiag[:, mt, k, :], uaT[:, mt, k + 1:k + 1 + T], k == 0, k == 3, ["cdiag", "uaT"], [pn])
            el("act", lambda e, ps_=ps_, mt=mt: e.activation(out=cT[:, mt, :], in_=ps_[:, 0:T], func=AF.Silu, bias=cb[:, mt:mt + 1]),
               [pn, "cb"], ["cT"])
        if main:
            for half in range(2):
                proj_fm(BLK_ZA + half, 4, lambda ct, ps_, pn, half=half: el(
                    "act", lambda e: e.activation(out=zaT[:, half * 4 + ct, :], in_=ps_[:, 0:T], func=AF.Silu), [pn], ["zaT"]))
            for half in range(2):
                proj_fm(BLK_OA + half, 4, lambda ct, ps_, pn, half=half: el(
                    "act", lambda e: e.activation(out=oaT[:, half * 4 + ct, :], in_=ps_[:, 0:T], func=AF.Sigmoid), [pn], ["oaT"]))
        if main:
            for mt in range(8):
                el("dve", lambda e, mt=mt: e.scalar_tensor_tensor(out=csz[:, mt, :], in0=cT[:, mt, :], scalar=skp[:, mt:mt + 1], in1=zaT[:, mt, :],
                                                                  op0=ALU.mult, op1=ALU.mult), ["cT", "skp", "zaT"], ["csz"])
            for mt in range(8):
                ts(zaT[:, mt, :], zaT[:, mt, :], hg[:, mt:mt + 1], None, ALU.mult, None, ["zaT", "hg", "csz"], ["zaT"])
        chk(8)
        for h in range(4):
            if main:
                for (Wx, wxn, dst, dn) in ((Wq, "Wqkv0", qT, "qT"), (Wk, "Wqkv1", kT, "kT")):
                    for et in range(2):
                        ps_, pn = acc_bank()
                        for d in range(2):
                            mm(ps_[:, 0:T], Wx[:, h, d, et * 128:(et + 1) * 128], cT[:, 2 * h + d, :], d == 0, d == 1, [wxn, "cT"], [pn])
                        el("act" if et else "dve",
                           (lambda e, ps_=ps_, dst=dst, h=h, et=et: e.copy(out=dst[:, h, et, :], in_=ps_[:, 0:T])) if et else
                           (lambda e, ps_=ps_, dst=dst, h=h, et=et: e.tensor_copy(out=dst[:, h, et, :], in_=ps_[:, 0:T])), [pn], [dn])
        if main:
            prefetch(BLK_ZB); prefetch(BLK_S5 + 2); prefetch(BLK_S5 + 3)
        el("act", lambda e: e.activation(out=e1a[:], in_=gates[:, :, 4:8], func=AF.Exp, scale=-1.0), ["gates"], ["e1a"])
        el("act", lambda e: e.activation(out=lfa[:], in_=e1a[:], func=AF.Ln, bias=1.0), ["e1a"], ["lfa"])
        ts(lfa[:], lfa[:], -1.0, None, ALU.mult, None, ["lfa"], ["lfa"])
        for ch in range(NCH):
            tsl = slice(ch * 128, (ch + 1) * 128)
            for h in range(4):
                el("dve", lambda e, h=h, ch=ch: e.tensor_copy(out=lfrep[:, h, :], in_=lfa[:, ch, h:h + 1].to_broadcast([128, 128])), ["lfa"], ["lfrep"])
            for h in range(4):
                mm(pGD[:, h * 128:(h + 1) * 128], lfrep[:, h, :], maskT[:], True, True, ["lfrep", "maskT"], ["pGD"], sig=(h == 3))
            for h in range(4):
                mm(pM[:, h * 128:(h + 1) * 128], maskT[:], lfrep[:, h, :], True, True, ["maskT", "lfrep"], ["pM"], sig=(h == 3))
            en, dn, wn_, wbn, wrn = "emb%d" % ch, "dec%d" % ch, "wcol%d" % ch, "wcolb%d" % ch, "wrep%d" % ch
            el("act", lambda e, ch=ch: e.activation(out=emb4[:, ch, :, :].rearrange("p h t -> p (h t)"), in_=pGD[:], func=AF.Exp, scale=-1.0), ["pGD"], [en])
            el("dve", lambda e, ch=ch: e.reciprocal(out=dec4[:, ch, :], in_=emb4[:, ch, :, 127]), [en], [dn])
            tt(bcol[:], gates[:, ch, 0:4], pM[:].rearrange("p (h t) -> p h t", h=4)[:, :, 0], ALU.subtract, ["gates", "pM"], ["bcol"])
            el("act", lambda e, ch=ch: e.activation(out=wcol4[:, ch, :], in_=bcol[:], func=AF.Exp), ["bcol"], [wn_])
            el("dve", lambda e, ch=ch: e.tensor_copy(out=vw[:, ch, :, 256], in_=wcol4[:, ch, :]), [wn_], ["vw"])
            if main:
                for h in range(4):
                    el("dve", lambda e, h=h, ch=ch: e.tensor_copy(out=wrep4[:, ch, h, :], in_=wcol4[:, ch, h:h + 1].to_broadcast([128, 128])), [wn_], [wrn])
            for h in range(4):
                ps_, pn = acc_bank()
                for d in range(2):
                    mm(ps_[:, 0:256], cT[:, 2 * h + d, tsl], Wk[:, h, d, :], d == 0, d == 1, ["cT", "Wqkv1"], [pn], sig=False)
                for d in range(2):
                    mm(ps_[:, 256:512], uaT[:, 2 * h + d, 4 + ch * 128:4 + (ch + 1) * 128], Wv[:, h, d, :], d == 0, d == 1, ["uaT", "Wqkv2"], [pn])
                el("act", lambda e, ps_=ps_, ch=ch, h=h: e.copy(out=ktm[:, ch, h, :], in_=ps_[:, 0:256]), [pn], ["ktm"])
                el("act", lambda e, ps_=ps_, ch=ch, h=h: e.activation(out=vw[:, ch, h, 0:256], in_=ps_[:, 256:512], func=AF.Copy, scale=wcol4[:, ch, h:h + 1]),
                   [pn, wn_], ["vw"])
        for ch in range(NCH):
            tsl = slice(ch * 128, (ch + 1) * 128)
            en, dn, wn_, wbn, wrn = "emb%d" % ch, "dec%d" % ch, "wcol%d" % ch, "wcolb%d" % ch, "wrep%d" % ch
            if main:
                for h in range(4):
                    for et in range(2):
                        mm(pS[:, h * 128:(h + 1) * 128], kT[:, h, et, tsl], qT[:, h, et, tsl], et == 0, et == 1, ["kT", "qT"], ["pS"], sig=(h == 3 and et == 1))
                tt(SmT[:].rearrange("p h t -> p (h t)"), pS[:], mask4[:].rearrange("p h t -> p (h t)"), ALU.mult, ["pS", "mask4"], ["SmT"])
                for h in range(4):
                    for d2 in range(2):
                        pn_, pnn = (pN0, "pN0") if h < 2 else (pN1, "pN1")
                        o_ = pn_[:, ((h % 2) * 2 + d2) * 128:((h % 2) * 2 + d2 + 1) * 128]
                        mm(o_, vw[:, ch, h, d2 * 128:(d2 + 1) * 128], SmT[:, h, :], True, False, ["vw", "SmT"], [pnn], sig=False)
                        mm(o_, Cbf[:, h, 0, d2 * 128:(d2 + 1) * 128], qT[:, h, 0, tsl], False, False, ["Cbf%d" % h, "qT"], [pnn], sig=False)
                        mm(o_, Cbf[:, h, 1, d2 * 128:(d2 + 1) * 128], qT[:, h, 1, tsl], False, True, ["Cbf%d" % h, "qT"], [pnn], sig=(h % 2 == 1 and d2 == 1))
                    o_ = pGD[:, h * 128:(h + 1) * 128]
                    mm(o_, wrep4[:, ch, h, :], SmT[:, h, :], True, False, [wrn, "SmT"], ["pGD"], sig=False)
                    mm(o_, nrep[:, h, 0, :], qT[:, h, 0, tsl], False, False, ["nrep%d" % h, "qT"], ["pGD"], sig=False)
                    mm(o_, nrep[:, h, 1, :], qT[:, h, 1, tsl], False, True, ["nrep%d" % h, "qT"], ["pGD"], sig=(h == 3))
            if main:
                el("act", lambda e: e.activation(out=aden[:].rearrange("p h t -> p (h t)"), in_=pGD[:], func=AF.Abs), ["pGD"], ["aden"])
                tt(aden[:], aden[:], emb4[:, ch, :, :], ALU.max, ["aden", en], ["aden"])
                el("act", lambda e: e.activation(out=aden[:], in_=aden[:], func=AF.Ln), ["aden"], ["aden"])
                el("act", lambda e: e.activation(out=rden[:], in_=aden[:], func=AF.Exp, scale=-1.0), ["aden"], ["rden"])
            for h in range(4):
                cn, bn, nn = "Cst%d" % h, "Cbf%d" % h, "nrep%d" % h
                for et in range(2):
                    ps_, pn = acc_bank()
                    mm(ps_[:, 0:258], ktm[:, ch, h, et * 128:(et + 1) * 128], vw[:, ch, h, 0:258], True, True, ["ktm", "vw"], [pn])
                    tt(Cst[:, h, et, 0:258], Cst[:, h, et, 0:258], ps_[:, 0:258], ALU.add, [cn, pn, bn, nn], [cn])
            for h in range(4):
                cn, bn, nn = "Cst%d" % h, "Cbf%d" % h, "nrep%d" % h
                el("act", lambda e, h=h, ch=ch: e.activation(out=Cst[:, h, :, :], in_=Cst[:, h, :, :], func=AF.Copy, scale=dec4[:, ch, h:h + 1]), [cn, dn], [cn])
                el("act", lambda e, h=h: e.copy(out=Cbf[:, h, :, :], in_=Cst[:, h, :, :]), [cn], [bn])
            for h in range(4):
                cn, bn, nn = "Cst%d" % h, "Cbf%d" % h, "nrep%d" % h
                for et in range(2):
                    el("dve", lambda e, h=h, et=et: e.tensor_copy(out=nrep[:, h, et, :], in_=Cst[:, h, et, 256:257].to_broadcast([128, 128])),
                       [cn], [nn])
            if main:
                for hp, (pn_, pnn) in enumerate(((pN0, "pN0"), (pN1, "pN1"))):
                    tt(hT[:, 4 * hp:4 * hp + 4, :].rearrange("p (h d) t -> p h d t", h=2), pn_[:].rearrange("p (h d t) -> p h d t", h=2, d=2),
                       rden[:, 2 * hp:2 * hp + 2, :].unsqueeze(2).to_broadcast([128, 2, 2, 128]), ALU.mult, [pnn, "rden"], ["hT"])
                tt(hT[:], hT[:], oaT[:, :, tsl], ALU.mult, ["hT", "oaT"], ["hT"])
                el("act", lambda e: e.activation(out=sq[:], in_=hT[:], func=AF.Square), ["hT"], ["sq"])
                for h in range(4):
                    for d2 in range(2):
                        mm(pS[:, h * 128:(h + 1) * 128], ones_b[:], sq[:, 2 * h + d2, :], d2 == 0, d2 == 1, ["ones_b", "sq", "SmT"], ["pS"], sig=(h == 3 and d2 == 1))
                ts(rsh[:].rearrange("p h t -> p (h t)"), pS[:], 1.0 / 256.0, 1e-6, ALU.mult, ALU.add, ["pS"], ["rsh"])
                el("act", lambda e: e.activation(out=rsh[:], in_=rsh[:], func=AF.Ln), ["rsh"], ["rsh"])
                el("act", lambda e: e.activation(out=rsh[:], in_=rsh[:], func=AF.Exp, scale=-0.5), ["rsh"], ["rsh"])
                tt(hn[:].rearrange("p (h d) t -> p h d t", h=4), hT[:].rearrange("p (h d) t -> p h d t", h=4),
                   rsh[:].unsqueeze(2).to_broadcast([128, 4, 2, 128]), ALU.mult, ["hT", "rsh"], ["hn"])
                tt(hn[:], hn[:], zaT[:, :, tsl], ALU.mult, ["hn", "zaT"], ["hn"])
                tt(hfT[:, :, tsl], hn[:], csz[:, :, tsl], ALU.add, ["hn", "csz"], ["hfT"])
        chk(9)
        if not main:
            return
        s5_out()
        def gate_blocks(br):
            for bi_ in range(2):
                Wg_, wgn = load_block(BLK_G + 2 * br + bi_)
                Wgv = Wg_[:].rearrange("p (k n) -> p k n", k=8)
                for c4 in range(4):
                    ft = bi_ * 4 + c4
                    psg, png = acc_bank()
                    for kt in range(8):
                        mm(psg[:, 0:T], Wgv[:, kt, c4 * 128:(c4 + 1) * 128], xnT[:, kt, :], kt == 0, kt == 7, [wgn, "xnT"], [png])
                    el("act", lambda e, psg=psg, ft=ft: e.activation(out=gAll[:, ft, :], in_=psg[:, 0:T], func=AF.Sigmoid), [png], ["oaT"])

        gate_blocks(0)
        for hf_ in range(2):
            Wa, wan = load_block(BLK_AO + hf_)
            Wav = Wa[:].rearrange("p (k n) -> p k n", k=8)
            for c4 in range(4):
                ft = hf_ * 4 + c4
                psa, pna = acc_bank()
                for kt in range(8):
                    mm(psa[:, 0:T], Wav[:, kt, c4 * 128:(c4 + 1) * 128], hfT[:, kt, :], kt == 0, kt == 7, [wan, "hfT"], [pna])
                tt(mrgT[:, ft, :], psa[:, 0:T], gAll[:, ft, :], ALU.mult, [pna, "oaT"], ["zaT"])
        gate_blocks(1)
        Wbo, wbon = load_block(BLK_BO)
        Wbv = Wbo[:].rearrange("p (k n) -> p k n", k=4)
        for ft in range(8):
            psb, pnb = acc_bank()
            for kt in range(4):
                mm(psb[:, 0:T], Wbv[:, kt, ft * 128:(ft + 1) * 128], hbT[:, kt, :], kt == 0, kt == 3, [wbon, "hbT"], [pnb])
            el("act", lambda e, psb=psb: e.copy(out=m2[:].rearrange("p (c j) -> p j c", j=8), in_=psb[:, 0:T].rearrange("p (j c) -> p j c", j=8)),
               [pnb], ["m2"])
            tt(m2[:], m2[:], gAll[:, ft, :], ALU.mult, ["m2", "oaT"], ["m2"])
            tt(mrgT[:, ft, :], m2[:], mrgT[:, ft, :], ALU.add, ["m2", "zaT"], ["zaT"])
        Wo0, wo0n = load_block(BLK_WO); Wo1, wo1n = load_block(BLK_WO + 1)
        for ch in range(NCH):
            xb_, xnm = xr[ch % 2], "xr%d" % (ch % 2)
            if ch < 2:
                P.dma(lambda e, xb_=xb_, ch=ch: e.dma_start(out=xb_[:], in_=xsrc[row0 + ch * 128: row0 + (ch + 1) * 128, :]), writes=[xnm])
        for ch in range(NCH):
            tsl = slice(ch * 128, (ch + 1) * 128)
            bk = ((pN0, "pN0"), (pN1, "pN1")) if ch % 2 == 0 else ((pS, "pS"), (pGD, "pGD"))
            ssA, ssAn = (ss2, "ss2") if ch % 2 == 0 else (ss3, "ss3")
            ssB, ssBn = (rstd2, "rstd2") if ch % 2 == 0 else (rstd3, "rstd3")
            for hf_, (Wo_, won) in enumerate(((Wo0, wo0n), (Wo1, wo1n))):
                pn_, pnn = bk[hf_]
                Wov = Wo_[:].rearrange("p (k n) -> p k n", k=8)
                for kt in range(8):
                    mm(pn_[:, :], mrgT[:, kt, tsl], Wov[:, kt, :], kt == 0, kt == 7, ["zaT", won], [pnn])
            if ch == 0:
                early_A()
            jk, jkn = (junk, "junk") if ch % 2 == 0 else (xnb, "xnb")
            el("act", lambda e, bk=bk, ssA=ssA, jk=jk: e.activation(out=jk[:, 0:512], in_=bk[0][0][:], func=AF.Square, accum_out=ssA[:]), [bk[0][1]], [jkn, ssAn])
            el("act", lambda e, bk=bk, ssB=ssB, jk=jk: e.activation(out=jk[:, 512:1024], in_=bk[1][0][:], func=AF.Square, accum_out=ssB[:]), [bk[1][1]], [jkn, ssBn])
            tt(ssA[:], ssA[:], ssB[:], ALU.add, [ssAn, ssBn], [ssAn])
            rsqrt_col(ssB[:], ssA[:], 1.0 / 1024.0, ssAn, ssBn)
            ob, obn = ot[ch % 2], "ot%d" % (ch % 2)
            xb_, xnm = xr[ch % 2], "xr%d" % (ch % 2)
            if ch >= 2:
                P.dma(lambda e, xb_=xb_, ch=ch: e.dma_start(out=xb_[:], in_=xsrc[row0 + ch * 128: row0 + (ch + 1) * 128, :]), writes=[xnm])
            for hf_ in range(2):
                pn_, pnn = bk[hf_]
                cs = slice(hf_ * 512, hf_ * 512 + 512)
                el("dve", lambda e, ob=ob, pn_=pn_, cs=cs, ssB=ssB: e.scalar_tensor_tensor(out=ob[:, cs], in0=pn_[:], scalar=ssB[:, 0:1], in1=gpost[:, cs],
                                                                                          op0=ALU.mult, op1=ALU.mult), [pnn, ssBn, "gpost"], [obn])
            tt(ob[:], ob[:], xb_[:], ALU.add, [obn, xnm], [obn])
            tok = P.dma(lambda e, ob=ob, ch=ch: e.dma_start(out=out[row0 + ch * 128: row0 + (ch + 1) * 128, :], in_=ob[:]), reads=[obn])
            out_toks.append(tok)

    sc_t1 = sbt([128, 2, 16], F32, "sc_t1"); sc_t2 = sbt([128, 2, 16], F32, "sc_t2")
    tiles_ = [(x_pre, ti * T) for ti in range(NPRE)] + [(x_main, ti * T) for ti in range(NMAIN)]
    for i_ in range(len(tiles_) - 1):
        nxt2[(id(tiles_[i_][0]), tiles_[i_][1])] = tiles_[i_ + 1]
    try:
        for ti in range(NPRE):
            nxt = (x_pre, (ti + 1) * T) if ti + 1 < NPRE else (x_main, 0)
            do_tile(x_pre, ti * T, False, ti, nxt)
    except StopBuild:
        tk = P.dma(lambda e: e.dma_start(out=out[0:128, :], in_=gpost[:]), reads=["gpost"])
        P.final_wait("sp", [tk])
        P.emit(); P.close()
        return nc
    if NPRE > 0:
        for h in range(4):
            cn, bn, nn = "Cst%d" % h, "Cbf%d" % h, "nrep%d" % h
            ts(Cst[:, h, :, :], Cst[:, h, :, :], flg[:, 0:1], None, ALU.mult, None, [cn, "flg"], [cn])
            el("act", lambda e, h=h: e.copy(out=Cbf[:, h, :, :], in_=Cst[:, h, :, :]), [cn], [bn])
            for et in range(2):
                el("dve", lambda e, h=h, et=et: e.tensor_copy(out=nrep[:, h, et, :], in_=Cst[:, h, et, 256:257].to_broadcast([128, 128])),
                   [cn], [nn])
    for ti in range(NMAIN):
        nxt = (x_main, (ti + 1) * T) if ti + 1 < NMAIN else None
        do_tile(x_main, ti * T, True, NPRE + ti, nxt)
    P.final_wait("sp", out_toks)
    P.emit()
    P.close()
    return nc


T_TILE = 256
_cache = {}


def kernel(**inputs):
    x = np.ascontiguousarray(inputs["x"], dtype=np.float32)
    Bsz, L, Dm = x.shape
    half = L // 2
    npre = half // T_TILE
    nmain = half // T_TILE
    key = (T_TILE, npre, nmain)
    if key not in _cache:
        _cache[key] = build_program(T_TILE, npre, nmain)
    nc = _cache[key]
    shared = {}
    for k, v in inputs.items():
        if k == "x":
            continue
        a = np.ascontiguousarray(np.asarray(v, dtype=np.float32)[0])
        if k in ("b_i", "b_f", "log_dt", "norm_post_g"):
            a = a.reshape(1, -1)
        shared[k] = a
    in_maps = []
    zeros = np.zeros((half, Dm), np.float32)
    for core in range(8):
        b, hf = core // 2, core % 2
        m = dict(shared)
        m["x_main"] = np.ascontiguousarray(x[b, hf * half:(hf + 1) * half])
        m["x_pre"] = zeros if hf == 0 else np.ascontiguousarray(x[b, 0:half])
        m["flag"] = np.full((128, 1), float(hf), np.float32)
        in_maps.append(m)
    res = run_bass_kernel_spmd(nc, in_maps, core_ids=list(range(8)))
    outp = np.empty((Bsz, L, Dm), np.float32)
    for core in range(8):
        b, hf = core // 2, core % 2
        outp[b, hf * half:(hf + 1) * half] = res.results[core]["out"]
    return outp
```

```python
import contextlib
import numpy as np
import concourse.bass as bass
import concourse.mybir as mybir
from concourse.bass_utils import run_bass_kernel_spmd

F32 = mybir.dt.float32
BF16 = mybir.dt.bfloat16
AF = mybir.ActivationFunctionType
ALU = mybir.AluOpType
COMPUTE = ("pe", "act", "dve", "pool")
NDMA_SLOTS = 8
DEBUG_TAGS = False
EARLY_A = True
PI = float(np.pi)


class Prog:
    def __init__(self, nc):
        self.nc = nc
        self.stack = contextlib.ExitStack()
        self.engs = ("pe", "act", "dve", "pool", "sp")
        self.ops = {e: [] for e in self.engs}
        self.waited = {e: {} for e in self.engs}
        self.res = {}
        self.dma_use = {}
        self.dma_rr = {e: 0 for e in self.engs}
        self.sems = {}
        self.base = {e: 0 for e in COMPUTE}
        self.temp = None

    def sb(self, name, shape, dt):
        st = self.temp if self.temp is not None else self.stack
        return st.enter_context(self.nc.sbuf_tensor(name, list(shape), dt))

    def ps(self, name, shape, dt):
        return self.stack.enter_context(self.nc.psum_tensor(name, list(shape), dt))

    def _sem(self, key):
        if key not in self.sems:
            nm = "s_" + "_".join(str(k) for k in (key if isinstance(key, tuple) else (key,)))
            self.sems[key] = self.stack.enter_context(self.nc.semaphore(nm))
        return self.sems[key]

    def _deps(self, eng, reads, writes):
        deps = {}

        def add(tok):
            if tok is None:
                return
            k, v = tok
            if k == "pe" and eng == "pe":
                return
            if deps.get(k, -1) < v:
                deps[k] = v

        for r in reads:
            st = self.res.get(r)
            if st:
                for k, v in st[0].items():
                    add((k, v))
        for w in writes:
            st = self.res.get(w)
            if st:
                for k, v in st[0].items():
                    add((k, v))
                for k, v in st[1].items():
                    add((k, v))
        out = []
        wd = self.waited[eng]
        for k, v in deps.items():
            if wd.get(k, -1) >= v:
                continue
            wd[k] = v
            out.append((k, v))
        return out

    def _commit(self, tok, reads, writes):
        k, v = tok
        for r in reads:
            st = self.res.setdefault(r, [{}, {}])
            if st[1].get(k, -1) < v:
                st[1][k] = v
        for w in writes:
            old = self.res.get(w)
            wr = {}
            if old is not None and k not in COMPUTE:
                wr = {k2: v2 for k2, v2 in old[0].items() if k2 not in COMPUTE}
            wr[k] = v
            self.res[w] = [wr, {}]

    def _tag(self):
        if not DEBUG_TAGS:
            return None
        import sys as _sys
        f = _sys._getframe(2)
        while f is not None and f.f_code.co_name not in ("do_tile", "build_program", "s5_out", "gate_blocks", "proj_fm"):
            f = f.f_back
        return str(f.f_lineno) if f is not None else None

    def op(self, eng, fn, reads=(), writes=(), sig=True):
        waits = self._deps(eng, reads, writes)
        idx = len(self.ops[eng])
        self.ops[eng].append(dict(fn=fn, waits=waits, sig=sig, dma=None, tag=self._tag()))
        tok = (eng, idx)
        self._commit(tok, reads, writes)
        return tok

    def dma(self, fn, reads=(), writes=(), q="sp"):
        waits = self._deps(q, reads, writes)
        slot = self.dma_rr[q] % NDMA_SLOTS
        self.dma_rr[q] += 1
        key = ("d", q, slot)
        n = self.dma_use.get(key, 0)
        if n > 0:
            prev = n * 16
            if self.waited[q].get(key, -1) < prev:
                self.waited[q][key] = prev
                waits.append((key, prev))
        self.dma_use[key] = n + 1
        tok = (key, (n + 1) * 16)
        self.ops[q].append(dict(fn=fn, waits=waits, sig=False, dma=key))
        self._commit(tok, reads, writes)
        return tok

    def final_wait(self, eng, toks):
        self.ops[eng].append(dict(fn=None, waits=list(toks), sig=False, dma=None))

    def emit(self):
        nc = self.nc
        sigcount = {}
        totals = {}
        for e in COMPUTE:
            c = self.base[e]
            arr = []
            for o in self.ops[e]:
                if o["sig"]:
                    c += 1
                arr.append(c)
            need = [None] * len(arr)
            nxt = None
            for i in range(len(arr) - 1, -1, -1):
                if self.ops[e][i]["sig"]:
                    nxt = arr[i]
                need[i] = nxt
            sigcount[e] = need
            totals[e] = c
            self._sem(e)
        for k in self.dma_use:
            self._sem(k)

        def resolve(k, v):
            if k in COMPUTE:
                val = sigcount[k][v]
                assert val is not None, (k, v)
                return self.sems[k], val
            return self.sems[k], v

        with nc.Block() as block:

            def run(eng_name, eng):
                for o in self.ops[eng_name]:
                    for k, v in o["waits"]:
                        s, val = resolve(k, v)
                        eng.wait_ge(s, val)
                    if o["fn"] is None:
                        continue
                    ins = o["fn"](eng)
                    if o.get("tag"):
                        ins.annotate(o["tag"])
                    if o["dma"] is not None:
                        ins.then_inc(self.sems[o["dma"]], 16)
                    elif o["sig"]:
                        ins.then_inc(self.sems[eng_name], 1)
                for o2 in COMPUTE:
                    if o2 != eng_name and totals[o2] > 0:
                        eng.wait_ge(self.sems[o2], totals[o2])
                for k, n in self.dma_use.items():
                    eng.wait_ge(self.sems[k], n * 16)

            @block.tensor
            def _(e):
                run("pe", e)

            @block.scalar
            def _(e):
                run("act", e)

            @block.vector
            def _(e):
                run("dve", e)

            @block.gpsimd
            def _(e):
                run("pool", e)

            @block.sync
            def _(e):
                run("sp", e)

        self.base = totals
        self.ops = {e: [] for e in self.engs}
        self.waited = {e: {} for e in self.engs}
        self.res = {}

    def close(self):
        self.stack.close()


NBLK = 22
BLK_UA, BLK_UB, BLK_ZB, BLK_ZA, BLK_OA, BLK_G, BLK_AO, BLK_BO, BLK_WO, BLK_S5 = 0, 2, 3, 4, 6, 8, 12, 14, 15, 17
COL_UA, COL_ZA, COL_OA, COL_I, COL_UB, COL_ZB, COL_G = 0, 1024, 2048, 3072, 3080, 3592, 4104


def build_program(T, NPRE, NMAIN, dbg_stop=0):
    NCH = T // 128
    NC8 = T // 8
    nc = bass.Bass("TRN2", target_bir_lowering=False)
    dram = {}

    def din(name, shape):
        dram[name] = nc.dram_tensor(name, list(shape), F32, kind="ExternalInput").ap()
        return dram[name]

    x_pre = din("x_pre", [max(NPRE, 1) * T, 1024])
    x_main = din("x_main", [NMAIN * T, 1024])
    flag = din("flag", [128, 1])
    norm_pre_g = din("norm_pre_g", [1024]); w_in = din("w_in", [1024, 6152])
    conv_w = din("conv_w", [4, 1024]); conv_b = din("conv_b", [1024])
    w_q = din("w_q", [4, 256, 256]); w_k = din("w_k", [4, 256, 256]); w_v = din("w_v", [4, 256, 256])
    b_i = din("b_i", [1, 4]); b_f = din("b_f", [1, 4]); head_g = din("head_g", [1024]); skip_a = din("skip_a", [1024])
    w_a_out = din("w_a_out", [1024, 1024])
    lam_re = din("lam_re", [32, 64]); lam_im = din("lam_im", [32, 64]); log_dt = din("log_dt", [1, 32])
    B_re = din("B_re", [32, 64, 16]); B_im = din("B_im", [32, 64, 16])
    C_re = din("C_re", [32, 16, 64]); C_im = din("C_im", [32, 16, 64]); D_skip = din("D_skip", [32, 16])
    w_glu = din("w_glu", [512, 512]); b_glu = din("b_glu", [512]); w_b_out = din("w_b_out", [512, 1024])
    w_o = din("w_o", [1024, 1024]); norm_post_g = din("norm_post_g", [1, 1024])
    out = nc.dram_tensor("out", [NMAIN * T, 1024], F32, kind="ExternalOutput").ap()
    WS = nc.dram_tensor("wscratch", [NBLK, 128, 4096], BF16, kind="Internal").ap()

    P = Prog(nc)
    uid = [0]

    def sbt(shape, dt, name=None):
        uid[0] += 1
        return P.sb(name or ("t%d" % uid[0]), shape, dt)

    ident_f = sbt([128, 128], F32); ident_b = sbt([128, 128], BF16)
    maskT = sbt([128, 128], F32); mask4 = sbt([128, 4, 128], F32); ones_b = sbt([128, 128], BF16)
    P.op("pool", lambda e: e.memset(ident_f[:], 1.0), writes=["ident_f"])
    P.op("pool", lambda e: e.affine_select(out=ident_f[:], in_=ident_f[:], pattern=[[-1, 128]], compare_op=ALU.is_equal,
                                           fill=0.0, base=0, channel_multiplier=1), reads=["ident_f"], writes=["ident_f"])
    P.op("dve", lambda e: e.tensor_copy(out=ident_b[:], in_=ident_f[:]), reads=["ident_f"], writes=["ident_b"])
    P.op("pool", lambda e: e.memset(maskT[:], 1.0), writes=["maskT"])
    P.op("pool", lambda e: e.affine_select(out=maskT[:], in_=maskT[:], pattern=[[1, 128]], compare_op=ALU.is_ge,
                                           fill=0.0, base=0, channel_multiplier=-1), reads=["maskT"], writes=["maskT"])
    for h in range(4):
        P.op("pool", lambda e, h=h: e.tensor_copy(out=mask4[:, h, :], in_=maskT[:]), reads=["maskT"], writes=["mask4"])
    P.op("pool", lambda e: e.memset(ones_b[:], 1.0), writes=["ones_b"])

    gpre = sbt([128, 8], F32); cb = sbt([128, 8], F32); hg = sbt([128, 8], F32); skp = sbt([128, 8], F32)
    cw = sbt([128, 8, 4], F32); bglu = sbt([128, 4], F32); gpost = sbt([128, 1024], F32); bif = sbt([128, 8], F32)
    flg = sbt([128, 1], F32)
    nonc = dict(allow_slow_non_contiguous=True)
    P.dma(lambda e: e.dma_start(out=gpre[:], in_=norm_pre_g.rearrange("(k p) -> p k", p=128), **nonc), writes=["gpre"])
    P.dma(lambda e: e.dma_start(out=cb[:], in_=conv_b.rearrange("(k p) -> p k", p=128), **nonc), writes=["cb"])
    P.dma(lambda e: e.dma_start(out=hg[:], in_=head_g.rearrange("(k p) -> p k", p=128), **nonc), writes=["hg"])
    P.dma(lambda e: e.dma_start(out=skp[:], in_=skip_a.rearrange("(k p) -> p k", p=128), **nonc), writes=["skp"])
    for k in range(4):
        P.dma(lambda e, k=k: e.dma_start(out=cw[:, :, k], in_=conv_w[k].rearrange("(m p) -> p m", p=128), **nonc), writes=["cw"])
    P.dma(lambda e: e.dma_start(out=bglu[:], in_=b_glu.rearrange("(k p) -> p k", p=128), **nonc), writes=["bglu"])
    P.dma(lambda e: e.dma_start(out=gpost[:], in_=norm_post_g.partition_broadcast(128)), writes=["gpost"])
    P.dma(lambda e: e.dma_start(out=bif[:, 0:4], in_=b_i.partition_broadcast(128)), writes=["bif"])
    P.dma(lambda e: e.dma_start(out=bif[:, 4:8], in_=b_f.partition_broadcast(128)), writes=["bif"])
    P.dma(lambda e: e.dma_start(out=flg[:], in_=flag), writes=["flg"])

    Wqkv = [sbt([128, 4, 2, 256], BF16) for _ in range(3)]
    Wif = sbt([128, 8, 8], BF16); Wglu = sbt([128, 4, 512], BF16); cdiag = sbt([128, 8, 4, 128], BF16)
    AR32 = sbt([128, 2, 16], F32); ANI = sbt([128, 16], F32); API = sbt([128, 16], F32)
    pA = P.ps("pA", [128, 512], F32); pB = P.ps("pB", [128, 512], F32)
    pT = P.ps("pT", [128, 1024], BF16); pGD = P.ps("pGD", [128, 512], F32)
    pS = P.ps("pS", [128, 512], F32); pN0 = P.ps("pN0", [128, 512], F32); pN1 = P.ps("pN1", [128, 512], F32)
    pM = P.ps("pM", [128, 512], F32)
    P.temp = contextlib.ExitStack()
    stg = sbt([128, 4096], F32, "stg")
    stgb = sbt([128, 4096], BF16, "stgb")
    stgb2 = sbt([128, 4096], BF16, "stgb2")
    for wi, (wsrc, scl) in enumerate(((w_q, 1.0), (w_k, 1.0 / 16.0), (w_v, 1.0))):
        P.dma(lambda e, wsrc=wsrc: e.dma_start(out=stg[:, 0:2048].rearrange("p (h d n) -> p h d n", h=4, d=2),
                                               in_=wsrc.rearrange("h (d p) n -> p h d n", p=128)), writes=["stg"])
        P.op("dve", lambda e, wi=wi, scl=scl: e.tensor_scalar(out=Wqkv[wi][:].rearrange("p h d n -> p (h d n)"), in0=stg[:, 0:2048],
                                                              scalar1=scl, scalar2=None, op0=ALU.mult), reads=["stg"], writes=["Wqkv%d" % wi])
    Wq, Wk, Wv = Wqkv
    P.dma(lambda e: e.dma_start(out=stg[:, 0:64].rearrange("p (k n) -> p k n", k=8),
                                in_=w_in[:, COL_I:COL_I + 8].rearrange("(k p) n -> p k n", p=128), **nonc), writes=["stg"])
    for kt in range(8):
        P.op("dve", lambda e, kt=kt: e.tensor_scalar(out=Wif[:, kt, :], in0=stg[:, kt * 8:(kt + 1) * 8], scalar1=gpre[:, kt:kt + 1],
                                                     scalar2=None, op0=ALU.mult), reads=["stg", "gpre"], writes=["Wif"])
    P.dma(lambda e: e.dma_start(out=stg[:, 0:2048].rearrange("p (k n) -> p k n", k=4),
                                in_=w_glu.rearrange("(k p) n -> p k n", p=128)), writes=["stg"])
    P.op("dve", lambda e: e.tensor_copy(out=Wglu[:].rearrange("p k n -> p (k n)"), in_=stg[:, 0:2048]), reads=["stg"], writes=["Wglu"])
    for mt in range(8):
        for k in range(4):
            P.op("dve", lambda e, mt=mt, k=k: e.tensor_scalar(out=cdiag[:, mt, k, :], in0=ident_f[:], scalar1=cw[:, mt, k:k + 1],
                                                               scalar2=None, op0=ALU.mult), reads=["ident_f", "cw"], writes=["cdiag"])

    def stage_block(blk, src_ap_f, scale_gpre, nk):
        ncol = 4096 // nk
        P.dma(lambda e: e.dma_start(out=stg[:].rearrange("p (k n) -> p k n", k=nk), in_=src_ap_f), writes=["stg"])
        if scale_gpre:
            for kt in range(nk):
                P.op("act",
                     lambda e, kt=kt: e.activation(out=stgb[:, kt * ncol:(kt + 1) * ncol], in_=stg[:, kt * ncol:(kt + 1) * ncol],
                                                   func=AF.Copy, scale=gpre[:, kt:kt + 1]),
                     reads=["stg", "gpre"], writes=["stgb"])
        else:
            P.op("act", lambda e: e.copy(out=stgb[:, 0:2048], in_=stg[:, 0:2048]), reads=["stg"], writes=["stgb"])
            P.op("act", lambda e: e.copy(out=stgb[:, 2048:4096], in_=stg[:, 2048:4096]), reads=["stg"], writes=["stgb"])
        P.dma(lambda e: e.dma_start(out=WS[blk], in_=stgb[:]), reads=["stgb"], writes=["WS%d" % blk])

    def win_cols(c0):
        return w_in[:, c0:c0 + 512].rearrange("(k p) n -> p k n", p=128)

    win_blocks = [(BLK_UA, COL_UA), (BLK_UA + 1, COL_UA + 512), (BLK_UB, COL_UB), (BLK_ZB, COL_ZB), (BLK_ZA, COL_ZA),
                  (BLK_ZA + 1, COL_ZA + 512), (BLK_OA, COL_OA), (BLK_OA + 1, COL_OA + 512)] + [(BLK_G + i, COL_G + 512 * i) for i in range(4)]
    pending = []
    for blk, c0 in win_blocks:
        pending.append((blk, win_cols(c0), True, 8))
    for i in range(2):
        pending.append((BLK_AO + i, w_a_out[:, 512 * i:512 * i + 512].rearrange("(k p) n -> p k n", p=128), False, 8))
        pending.append((BLK_WO + i, w_o[:, 512 * i:512 * i + 512].rearrange("(k p) n -> p k n", p=128), False, 8))
    pending.append((BLK_BO, w_b_out.rearrange("(k p) n -> p k n", p=128), False, 4))

    def stage_some(n=1):
        for _ in range(n):
            if pending:
                stage_block(*pending.pop(0))


    def small(n, name=None):
        return sbt([128, n], F32, name)

    cnt = [0]

    def el(eng, fn, r, w):
        P.op(eng, fn, reads=r, writes=w)

    def tt(outp, a, b, op, r, w, eng="dve"):
        el(eng, lambda e: e.tensor_tensor(out=outp, in0=a, in1=b, op=op), r, w)

    def ts(outp, a, s1, s2, op0, op1, r, w, eng="dve"):
        if op1 is None:
            el(eng, lambda e: e.tensor_scalar(out=outp, in0=a, scalar1=s1, scalar2=None, op0=op0), r, w)
        else:
            el(eng, lambda e: e.tensor_scalar(out=outp, in0=a, scalar1=s1, scalar2=s2, op0=op0, op1=op1), r, w)

    LR = small(32); LI = small(32); DT = small(32)
    for hf in range(2):
        sl = slice(64 * hf, 64 * hf + 64)
        P.dma(lambda e, sl=sl: e.dma_start(out=LR[sl, :], in_=lam_re.rearrange("g p -> p g"), **nonc), writes=["LR"])
        P.dma(lambda e, sl=sl: e.dma_start(out=LI[sl, :], in_=lam_im.rearrange("g p -> p g"), **nonc), writes=["LI"])
    P.dma(lambda e: e.dma_start(out=DT[:], in_=log_dt.partition_broadcast(128)), writes=["DT"])
    el("act", lambda e: e.activation(out=DT[:], in_=DT[:], func=AF.Exp), ["DT"], ["DT"])
    TH = small(32); MAG = small(32); t0 = small(32); t1 = small(32); t2 = small(32); kk = small(32)
    tt(TH[:], LI[:], DT[:], ALU.mult, ["LI", "DT"], ["TH"])
    tt(t0[:], LR[:], DT[:], ALU.mult, ["LR", "DT"], ["t0"])
    el("act", lambda e: e.activation(out=MAG[:], in_=t0[:], func=AF.Exp), ["t0"], ["MAG"])
    IMAG2 = small(32)
    el("act", lambda e: e.activation(out=IMAG2[:], in_=t0[:], func=AF.Exp, scale=-2.0), ["t0"], ["IMAG2"])

    def sin_of(dst, src, shift, nm):
        ts(t1[:], src, shift, None, ALU.add, None, [nm, "t1"], ["t1"])
        el("pool", lambda e: e.memset(kk[:], 0.0), [], ["kk"])
        for m in range(7):
            ts(t2[:], t1[:], (2 * m + 1) * PI, None, ALU.is_gt, None, ["t1"], ["t2"])
            tt(kk[:], kk[:], t2[:], ALU.add, ["kk", "t2"], ["kk"])
        ts(kk[:], kk[:], -2.0 * PI, None, ALU.mult, None, ["kk"], ["kk"])
        tt(t1[:], t1[:], kk[:], ALU.add, ["t1", "kk"], ["t1"])
        el("act", lambda e: e.activation(out=dst, in_=t1[:], func=AF.Sin), ["t1"], [nm + "_s"])

    SN = small(32); CS = small(32)
    sin_of(SN[:], TH[:], 0.0, "TH")
    sin_of(CS[:], TH[:], PI / 2.0, "TH")
    pwr = sbt([128, 9, 32], F32); pwi = sbt([128, 9, 32], F32); pnr = sbt([128, 8, 32], F32); pni = sbt([128, 8, 32], F32)
    el("pool", lambda e: e.memset(pwr[:, 0, :], 1.0), [], ["pw"]); el("pool", lambda e: e.memset(pwi[:, 0, :], 0.0), [], ["pw"])
    el("pool", lambda e: e.memset(pnr[:, 0, :], 1.0), [], ["pn"]); el("pool", lambda e: e.memset(pni[:, 0, :], 0.0), [], ["pn"])
    tt(pwr[:, 1, :], MAG[:], CS[:], ALU.mult, ["MAG", "TH_s"], ["pw"])
    tt(pwi[:, 1, :], MAG[:], SN[:], ALU.mult, ["MAG", "TH_s"], ["pw"])
    tt(pnr[:, 1, :], pwr[:, 1, :], IMAG2[:], ALU.mult, ["pw", "IMAG2"], ["pn"])
    tt(t0[:], pwi[:, 1, :], IMAG2[:], ALU.mult, ["pw", "IMAG2"], ["t0"])
    ts(pni[:, 1, :], t0[:], -1.0, None, ALU.mult, None, ["t0"], ["pn"])

    def cmul(or_, oi_, ar, ai, br, bi, r, w):
        raise NotImplementedError

    u0 = small(32); u1 = small(32)
    for k in range(1, 8):
        for (xr, xi, nm, lim) in ((pwr, pwi, "pw", 9), (pnr, pni, "pn", 8)):
            if k + 1 >= lim:
                continue
            tt(u0[:], xr[:, k, :], xr[:, 1, :], ALU.mult, [nm], ["u0"])
            tt(u1[:], xi[:, k, :], xi[:, 1, :], ALU.mult, [nm], ["u1"])
            tt(xr[:, k + 1, :], u0[:], u1[:], ALU.subtract, ["u0", "u1"], [nm])
            tt(u0[:], xr[:, k, :], xi[:, 1, :], ALU.mult, [nm], ["u0"])
            tt(u1[:], xi[:, k, :], xr[:, 1, :], ALU.mult, [nm], ["u1"])
            tt(xi[:, k + 1, :], u0[:], u1[:], ALU.add, ["u0", "u1"], [nm])
    den = small(32); qr = small(32); qi = small(32); nr = small(32)
    tt(u0[:], LR[:], LR[:], ALU.mult, ["LR"], ["u0"]); tt(u1[:], LI[:], LI[:], ALU.mult, ["LI"], ["u1"])
    tt(den[:], u0[:], u1[:], ALU.add, ["u0", "u1"], ["den"])
    el("dve", lambda e: e.reciprocal(out=den[:], in_=den[:]), ["den"], ["den"])
    ts(nr[:], pwr[:, 1, :], -1.0, None, ALU.add, None, ["pw"], ["nr"])
    tt(u0[:], nr[:], LR[:], ALU.mult, ["nr", "LR"], ["u0"]); tt(u1[:], pwi[:, 1, :], LI[:], ALU.mult, ["pw", "LI"], ["u1"])
    tt(qr[:], u0[:], u1[:], ALU.add, ["u0", "u1"], ["qr"]); tt(qr[:], qr[:], den[:], ALU.mult, ["qr", "den"], ["qr"])
    tt(u0[:], pwi[:, 1, :], LR[:], ALU.mult, ["pw", "LR"], ["u0"]); tt(u1[:], nr[:], LI[:], ALU.mult, ["nr", "LI"], ["u1"])
    tt(qi[:], u0[:], u1[:], ALU.subtract, ["u0", "u1"], ["qi"]); tt(qi[:], qi[:], den[:], ALU.mult, ["qi", "den"], ["qi"])
    Br = sbt([128, 32, 16], F32); Bi = sbt([128, 32, 16], F32); bbr = sbt([128, 32, 16], F32); bbi = sbt([128, 32, 16], F32)
    v0 = sbt([128, 32, 16], F32); v1 = sbt([128, 32, 16], F32)
    for hf in range(2):
        sl = slice(64 * hf, 64 * hf + 64)
        P.dma(lambda e, sl=sl: e.dma_start(out=Br[sl], in_=B_re.rearrange("g p n -> p g n"), **nonc), writes=["Br"])
        P.dma(lambda e, sl=sl: e.dma_start(out=Bi[sl], in_=B_im.rearrange("g p n -> p g n"), **nonc), writes=["Bi"])

    def bc(s):
        return s.unsqueeze(2).to_broadcast([128, 32, 16])

    def cmul3(orr, oii, sr, si, sn, xr, xi, xn, on):
        xn = [xn] if isinstance(xn, str) else list(xn)
        sn = [sn] if isinstance(sn, str) else list(sn)
        tt(v0[:], xr, bc(sr), ALU.mult, xn + sn, ["v0"]); tt(v1[:], xi, bc(si), ALU.mult, xn + sn, ["v1"])
        tt(orr, v0[:], v1[:], ALU.subtract, ["v0", "v1"], [on])
        tt(v0[:], xi, bc(sr), ALU.mult, xn + sn, ["v0"]); tt(v1[:], xr, bc(si), ALU.mult, xn + sn, ["v1"])
        tt(oii, v0[:], v1[:], ALU.add, ["v0", "v1"], [on])

    el("dve", lambda e: e.tensor_copy(out=u0[:], in_=qr[:]), ["qr"], ["qq"])
    cmul3(bbr[:], bbi[:], qr[:], qi[:], ["qr", "qi"], Br[:], Bi[:], ["Br", "Bi"], "bb")
    CTr = sbt([128, 32, 16], F32); CTi = sbt([128, 32, 16], F32)
    Cdup = sbt([128, 4, 2, 64], F32)
    for (Csrc, CTt, nm) in ((C_re, CTr, "CTr"), (C_im, CTi, "CTi")):
        for d in range(2):
            P.dma(lambda e, Csrc=Csrc, d=d: e.dma_start(out=Cdup[:, :, d, :], in_=Csrc.rearrange("(t g) n p -> (g n) t p", t=4)),
                  writes=["Cdup"])
        for t in range(4):
            P.op("pe", lambda e, t=t: e.transpose(out=pA[:, t * 128:(t + 1) * 128], in_=Cdup[:, t, :, :].rearrange("q d p -> q (d p)"),
                                                  identity=ident_f[:]), reads=["Cdup", "ident_f"], writes=["pA"])
        el("act", lambda e, CTt=CTt: e.copy(out=CTt[:].rearrange("p g n -> p (g n)"), in_=pA[:]), ["pA"], [nm])
    stage_some(100)
    Er = sbt([128, 32, 8, 16], F32); Ei = sbt([128, 32, 8, 16], F32)
    Fr = sbt([128, 32, 8, 16], F32); Fi = sbt([128, 32, 8, 16], F32)
    for j in range(8):
        cmul3(Er[:, :, j, :], Ei[:, :, j, :], pwr[:, 7 - j, :], pwi[:, 7 - j, :], "pw", bbr[:], bbi[:], "bb", "E")
        cmul3(Fr[:, :, j, :], Fi[:, :, j, :], pwr[:, j + 1, :], pwi[:, j + 1, :], "pw", CTr[:], CTi[:], ["CTr", "CTi"], "F")
    halfm = sbt([128, 2], F32)
    el("pool", lambda e: e.memset(halfm[:], 0.0), [], ["halfm"])
    el("pool", lambda e: e.memset(halfm[0:64, 0:1], 1.0), ["halfm"], ["halfm"])
    el("pool", lambda e: e.memset(halfm[64:128, 1:2], 1.0), ["halfm"], ["halfm"])
    for ri, (Et, blk) in enumerate(((Er, BLK_S5), (Ei, BLK_S5 + 1))):
        el("pool", lambda e: e.memset(stgb2[:], 0.0), ["stgb2"], ["stgb2"])
        for g in range(32):
            ps_ = pA if g % 2 == 0 else pB
            nm = "pA" if g % 2 == 0 else "pB"
            P.op("pe", lambda e, Et=Et, g=g, ps_=ps_: e.transpose(out=ps_[:, 0:64], in_=Et[0:64, g, :, :].rearrange("p j n -> p (j n)"),
                                                                  identity=ident_f[0:64, 0:64]), reads=["E", "ident_f"], writes=[nm])
            c0 = g * 128 + 64 * (g % 2)
            el("dve", lambda e, ps_=ps_, c0=c0: e.tensor_copy(out=stgb2[:, c0:c0 + 64], in_=ps_[:, 0:64]), [nm], ["stgb2"])
        P.dma(lambda e, blk=blk: e.dma_start(out=WS[blk], in_=stgb2[:]), reads=["stgb2"], writes=["WS%d" % blk], q="pool")
    for ri, (Ft, blk, sg) in enumerate(((Fr, BLK_S5 + 3, 1.0), (Fi, BLK_S5 + 4, -1.0))):
        for g in range(32):
            P.op("dve",
                 lambda e, Ft=Ft, g=g, sg=sg: e.tensor_scalar(out=stgb2[:, g * 128:(g + 1) * 128], in0=Ft[:, g, :, :].rearrange("p j n -> p (j n)"),
                                                              scalar1=halfm[:, (g % 2):(g % 2) + 1], scalar2=sg, op0=ALU.mult, op1=ALU.mult),
                 reads=["F", "halfm"], writes=["stgb2"])
        P.dma(lambda e, blk=blk: e.dma_start(out=WS[blk], in_=stgb2[:]), reads=["stgb2"], writes=["WS%d" % blk], q="pool")
    Gr = sbt([128, 32, 8, 16], F32); Gni = sbt([128, 32, 8, 16], F32)
    i8r = small(32); i8i = small(32)
    tt(u0[:], pnr[:, 7, :], pnr[:, 1, :], ALU.mult, ["pn"], ["u0"]); tt(u1[:], pni[:, 7, :], pni[:, 1, :], ALU.mult, ["pn"], ["u1"])
    tt(i8r[:], u0[:], u1[:], ALU.subtract, ["u0", "u1"], ["i8"])
    tt(u0[:], pnr[:, 7, :], pni[:, 1, :], ALU.mult, ["pn"], ["u0"]); tt(u1[:], pni[:, 7, :], pnr[:, 1, :], ALU.mult, ["pn"], ["u1"])
    tt(i8i[:], u0[:], u1[:], ALU.add, ["u0", "u1"], ["i8"])
    for j in range(8):
        cmul3(Gr[:, :, j, :], Gni[:, :, j, :], i8r[:], i8i[:], "i8", Fr[:, :, j, :], Fi[:, :, j, :], "F", "G")
    ts(Gni[:], Gni[:], -1.0, None, ALU.mult, None, ["G"], ["G"])
    bmask = sbt([128, 8, 16], F32); Dcol = sbt([128, 32], F32)
    el("pool", lambda e: e.memset(bmask[:], 1.0), [], ["bmask"])
    el("pool", lambda e: e.affine_select(out=bmask[:], in_=bmask[:], pattern=[[16, 8], [0, 16]], compare_op=ALU.is_ge, fill=0.0,
                                         base=15, channel_multiplier=-1), ["bmask"], ["bmask"])
    for j in range(8):
        P.dma(lambda e, j=j: e.dma_start(out=Dcol[16 * j:16 * j + 16, :], in_=D_skip.rearrange("g n -> n g"), **nonc), writes=["Dcol"], q="pool")
    wtmp = sbt([128, 128], F32)
    for g in range(32):
        ps_ = pA if g % 2 == 0 else pB
        nm = "pA" if g % 2 == 0 else "pB"
        P.op("pe", lambda e, g=g, ps_=ps_: e.matmul(ps_[:, 0:128], lhsT=Er[0:64, g, :, :].rearrange("p j n -> p (j n)"),
                                                    rhs=Gr[0:64, g, :, :].rearrange("p j n -> p (j n)"), start=True, stop=False),
             reads=["E", "G"], writes=[nm], sig=False)
        P.op("pe", lambda e, g=g, ps_=ps_: e.matmul(ps_[:, 0:128], lhsT=Ei[0:64, g, :, :].rearrange("p j n -> p (j n)"),
                                                    rhs=Gni[0:64, g, :, :].rearrange("p j n -> p (j n)"), start=False, stop=True),
             reads=["E", "G"], writes=[nm])
        tt(wtmp[:], ps_[:, 0:128], bmask[:].rearrange("p j n -> p (j n)"), ALU.mult, [nm, "bmask"], ["wtmp"])
        el("dve", lambda e, g=g: e.scalar_tensor_tensor(out=stgb2[:, g * 128:(g + 1) * 128], in0=ident_f[:], scalar=Dcol[:, g:g + 1],
                                                        in1=wtmp[:], op0=ALU.mult, op1=ALU.add), ["ident_f", "Dcol", "wtmp", "stgb2"], ["stgb2"])
    P.dma(lambda e: e.dma_start(out=WS[BLK_S5 + 2], in_=stgb2[:]), reads=["stgb2"], writes=["WS%d" % (BLK_S5 + 2)], q="pool")
    for g2 in range(2):
        sl = slice(64 * g2, 64 * g2 + 64)
        src_r = pwr[sl, 8, :].rearrange("p (q t) -> p q t", t=2)[:, :, g2]
        src_i = pwi[sl, 8, :].rearrange("p (q t) -> p q t", t=2)[:, :, g2]
        el("dve", lambda e, sl=sl, src_r=src_r: e.tensor_copy(out=AR32[sl, 0, :], in_=src_r), ["pw"], ["AR32"])
        el("dve", lambda e, sl=sl, src_r=src_r: e.tensor_copy(out=AR32[sl, 1, :], in_=src_r), ["pw"], ["AR32"])
        el("dve", lambda e, sl=sl, src_i=src_i: e.tensor_copy(out=API[sl, :], in_=src_i), ["pw"], ["API"])
        ts(ANI[sl, :], src_i, -1.0, None, ALU.mult, None, ["pw"], ["ANI"])

    stage_some(100)
    if dbg_stop == 1:
        tk = P.dma(lambda e: e.dma_start(out=out[0:128, :], in_=gpost[:]), reads=["gpost"])
        P.final_wait("sp", [tk])
        P.emit(); P.temp.close(); P.close()
        return nc
    P.emit()
    P.temp.close()
    P.temp = None
    NSLOT = 3
    wslot = [sbt([128, 4096], BF16, "wslot%d" % i) for i in range(NSLOT)]
    slot_rr = [0]

    pre = {}

    def prefetch(blk):
        pre[blk] = load_block(blk, force=True)

    def load_block(blk, force=False):
        if not force and blk in pre:
            return pre.pop(blk)
        s = slot_rr[0] % NSLOT
        slot_rr[0] += 1
        P.dma(lambda e: e.dma_start(out=wslot[s][:], in_=WS[blk]), reads=["WS%d" % blk], writes=["wslot%d" % s])
        return wslot[s], "wslot%d" % s

    xs = [sbt([128, 1024], F32, "xs%d" % i) for i in range(2)]
    xr = [sbt([128, 1024], F32, "xr%d" % i) for i in range(2)]
    junk = sbt([128, 1024], BF16); xnb = sbt([128, 1024], BF16)
    ss = small(1); rstd = small(1)
    xnT = sbt([128, 8, T], BF16, "xnT"); uaT = sbt([128, 8, T + 4], BF16, "uaT"); cT = sbt([128, 8, T], BF16, "cT")
    zaT = sbt([128, 8, T], BF16, "zaT"); oaT = sbt([128, 8, T], BF16, "oaT"); zbT = sbt([128, 4, T], BF16, "zbT")
    qT = sbt([128, 4, 2, T], BF16, "qT"); kT = sbt([128, 4, 2, T], BF16, "kT")
    ktm = sbt([128, NCH, 4, 256], BF16, "ktm"); vw = sbt([128, NCH, 4, 264], BF16, "vw")
    gates = sbt([128, NCH, 8], F32, "gates")
    hfT = sbt([128, 8, T], BF16, "hfT"); csz = sbt([128, 8, T], BF16, "csz"); mrgT = zaT; hbT = sbt([128, 4, T], BF16, "hbT")
    Cst = sbt([128, 4, 2, 264], F32, "Cst"); Cbf = sbt([128, 4, 2, 264], BF16, "Cbf")
    nst = sbt([128, 4, 2], F32, "nst"); nrep = sbt([128, 4, 2, 128], BF16, "nrep")
    Utm = sbt([128, 32, 8, 16], BF16, "Utm"); U2 = sbt([128, 32, NC8], BF16, "U2")
    Xall = sbt([128, NC8, 2, 16], F32, "Xall"); Sall = sbt([128, NC8 + 1, 2, 16], F32, "Sall"); Sbf = sbt([128, 2, 16, NC8], BF16, "Sbf")
    Yg = sbt([128, 32, NC8], BF16, "Yg"); Ytm = Utm[:].rearrange("p g j n -> p (g j n)").rearrange("p (j c) -> p j c", j=8); yT = sbt([128, 4, T], BF16, "yT")
    for (t_, nm) in ((Cst, "Cst"), (Cbf, "Cbf"), (nst, "nst"), (nrep, "nrep"), (Sall, "Sall"), (uaT, "uaT"), (vw, "vw")):
        flat = t_[:]
        wnames = [nm] + (["%s%d" % (nm, h_) for h_ in range(4)] if nm in ("Cst", "Cbf", "nrep") else [])
        el("pool", lambda e, flat=flat: e.memset(flat, 0.0), [], wnames)

    e1a = sbt([128, NCH, 4], F32); lfa = sbt([128, NCH, 4], F32); emb4 = sbt([128, NCH, 4, 128], F32); dec4 = sbt([128, NCH, 4], F32)
    wcol4 = sbt([128, NCH, 4], F32); wcolb4 = sbt([128, NCH, 4, 2], BF16); wrep4 = sbt([128, NCH, 4, 128], BF16)
    lfrep = sbt([128, 4, 128], F32); lf = small(4, "lf"); ig = small(4); bcol = small(4); wcol = small(4, "wcol"); wcolb = sbt([128, 4, 2], BF16)
    wrep = sbt([128, 4, 128], BF16); emb = sbt([128, 4, 128], F32, "emb"); dec = small(4, "dec"); SmT = sbt([128, 4, 128], BF16, "SmT")
    aden = sbt([128, 4, 128], F32); rden = sbt([128, 4, 128], F32, "rden"); hT = sbt([128, 8, 128], F32, "hT"); sq = sbt([128, 8, 128], BF16)
    rsh = sbt([128, 4, 128], F32); hn = sbt([128, 8, 128], F32, "hn"); e1 = small(4)
    gAll = oaT; m1 = sbt([128, T], F32); m2 = sbt([128, T], F32)
    ot = [sbt([128, 1024], F32, "ot%d" % i) for i in range(2)]
    ss2 = small(1); rstd2 = small(1); ss3 = small(1); rstd3 = small(1); sgl = sbt([128, T], BF16); xg = sbt([128, T], F32)

    def rsqrt_col(dst, src, scale, nm_src, nm_dst):
        ts(dst, src, scale, 1e-6, ALU.mult, ALU.add, [nm_src], [nm_dst])
        el("act", lambda e: e.activation(out=dst, in_=dst, func=AF.Ln), [nm_dst], [nm_dst])
        el("act", lambda e: e.activation(out=dst, in_=dst, func=AF.Exp, scale=-0.5), [nm_dst], [nm_dst])

    def mm(outp, lhsT, rhs, start, stop, r, w, sig=None):
        P.op("pe", lambda e: e.matmul(outp, lhsT=lhsT, rhs=rhs, start=start, stop=stop), reads=r, writes=w,
             sig=(stop if sig is None else sig))

    acc_rr = [0]

    def acc_bank():
        acc_rr[0] += 1
        return (pA, "pA") if acc_rr[0] % 2 else (pB, "pB")

    out_toks = []

    class StopBuild(Exception):
        pass

    def chk(level):
        if dbg_stop == level:
            raise StopBuild()

    prefetched = [False]
    a_done = [None]
    nxt2 = {}

    def issue_x(src_, r0, sub):
        xb2, xn2 = xs[sub % 2], "xs%d" % (sub % 2)
        P.dma(lambda e: e.dma_start(out=xb2[:], in_=src_[r0 + sub * 128: r0 + (sub + 1) * 128, :]), writes=[xn2])

    def stage_A(xsrc, row0):
        for sub in range(NCH):
            xb_, xnm = xs[sub % 2], "xs%d" % (sub % 2)
            if not (prefetched[0] and sub < 2):
                issue_x(xsrc, row0, sub)
            el("act", lambda e, xb_=xb_: e.activation(out=junk[:], in_=xb_[:], func=AF.Square, accum_out=ss[:]), [xnm], ["junk", "ss"])
            rsqrt_col(rstd[:], ss[:], 1.0 / 1024.0, "ss", "rstd")
            ts(xnb[:], xb_[:], rstd[:, 0:1], None, ALU.mult, None, [xnm, "rstd"], ["xnb"])
            for kt in range(8):
                P.op("pe", lambda e, kt=kt: e.transpose(out=pT[:, kt * 128:(kt + 1) * 128], in_=xnb[:, kt * 128:(kt + 1) * 128], identity=ident_b[:]),
                     reads=["xnb", "ident_b"], writes=["pT"], sig=(kt == 7))
            el("act", lambda e, sub=sub: e.copy(out=xnT[:, :, sub * 128:(sub + 1) * 128], in_=pT[:].rearrange("p (k t) -> p k t", k=8)),
               ["pT"], ["xnT"])
        a_done[0] = (id(xsrc), row0)
        prefetched[0] = False
        n2 = nxt2.get((id(xsrc), row0))
        if n2 is not None and NCH <= 2:
            for sub in range(min(2, NCH)):
                issue_x(n2[0], n2[1], sub)
            prefetched[0] = True

    def do_tile(xsrc, row0, main, ti, nxt=None):
        if a_done[0] != (id(xsrc), row0):
            stage_A(xsrc, row0)

        def early_A():
            if nxt is not None and EARLY_A:
                stage_A(nxt[0], nxt[1])

        chk(2)

        def proj_fm(blk, ncolt, evac):
            W, wn = load_block(blk)
            Wv_ = W[:].rearrange("p (k n) -> p k n", k=8)
            for ct in range(ncolt):
                ps_, pn = acc_bank()
                for kt in range(8):
                    mm(ps_[:, 0:T], Wv_[:, kt, ct * 128:(ct + 1) * 128], xnT[:, kt, :], kt == 0, kt == 7, [wn, "xnT"], [pn])
                evac(ct, ps_, pn)

        el("dve", lambda e: e.tensor_copy(out=uaT[:, :, 1:4], in_=uaT[:, :, T + 1:T + 4]), ["uaT"], ["uaT"])
        for half in range(2):
            proj_fm(BLK_UA + half, 4, lambda ct, ps_, pn, half=half: el(
                "act", lambda e: e.copy(out=uaT[:, half * 4 + ct, 4:T + 4], in_=ps_[:, 0:T]), [pn], ["uaT"]))
        chk(3)
        for ch in range(NCH):
            for kt in range(8):
                mm(pM[:, 0:8], xnT[:, kt, ch * 128:(ch + 1) * 128], Wif[:, kt, :], kt == 0, kt == 7, ["xnT", "Wif"], ["pM"])
            tt(gates[:, ch, :], pM[:, 0:8], bif[:], ALU.add, ["pM", "bif"], ["gates"])
        chk(4)
        W, wn = load_block(BLK_UB)
        Wv_ = W[:].rearrange("p (k n) -> p k n", k=8)
        for j in range(8):
            ps_, pn = acc_bank()
            for kt in range(8):
                lhs = xnT[:, kt, :].rearrange("p (c j) -> p j c", j=8)[:, j, :]
                mm(ps_[0:NC8, :], lhs, Wv_[:, kt, :], kt == 0, kt == 7, [wn, "xnT"], [pn])
            el("act" if j % 2 else "dve",
               (lambda e, ps_=ps_, j=j: e.copy(out=Utm[0:NC8, :, j, :], in_=ps_[0:NC8, :].rearrange("c (g n) -> c g n", g=32))) if j % 2 else
               (lambda e, ps_=ps_, j=j: e.tensor_copy(out=Utm[0:NC8, :, j, :], in_=ps_[0:NC8, :].rearrange("c (g n) -> c g n", g=32))), [pn], ["Utm"])
        if not main:
            early_A()
        chk(5)
        for g in range(32):
            P.op("pe", lambda e, g=g: e.transpose(out=pT[:, g * NC8:(g + 1) * NC8], in_=Utm[0:NC8, g, :, :].rearrange("c j n -> c (j n)"),
                                                  identity=ident_b[0:NC8, 0:NC8]), reads=["Utm", "ident_b"], writes=["pT"], sig=(g == 31))
        el("act", lambda e: e.copy(out=U2[:].rearrange("p g c -> p (g c)"), in_=pT[:, 0:32 * NC8]), ["pT"], ["U2"])
        W1r, w1rn = load_block(BLK_S5)
        W1i, w1in = load_block(BLK_S5 + 1)
        for ri, (Wt, wn_) in enumerate(((W1r, w1rn), (W1i, w1in))):
            Wg = Wt[:].rearrange("p (g n) -> p g n", g=32)
            for q in range(16):
                for g2 in range(2):
                    g = 2 * q + g2
                    mm(pM[:, (q * NC8):(q + 1) * NC8], Wg[:, g, :], U2[:, g, :], g2 == 0, g2 == 1, [wn_, "U2"], ["pM"], sig=(q == 15 and g2 == 1))
            el("act" if ri == 0 else "dve",
               (lambda e, ri=ri: e.copy(out=Xall[:, :, ri, :].rearrange("p c q -> p q c"), in_=pM[:, 0:16 * NC8].rearrange("p (q c) -> p q c", q=16))) if ri == 0 else
               (lambda e, ri=ri: e.tensor_copy(out=Xall[:, :, ri, :].rearrange("p c q -> p q c"), in_=pM[:, 0:16 * NC8].rearrange("p (q c) -> p q c", q=16))),
               ["pM"], ["Xall"])
        chk(6)
        T1 = sbt([128, 2, 16], F32, "scT1_%d" % ti) if False else None
        for c in range(NC8):
            sp_ = Sall[:, c, :, :]
            sn_ = Sall[:, c + 1, :, :]
            P.op("pool", lambda e, sp_=sp_: e.tensor_tensor(out=sc_t1[:], in0=sp_, in1=AR32[:], op=ALU.mult), reads=["Sall"], writes=["sc_t1"])
            P.op("pool", lambda e, c=c: e.tensor_tensor(out=sc_t2[:, 0, :], in0=Sall[:, c, 1, :], in1=ANI[:], op=ALU.mult), reads=["Sall"], writes=["sc_t2"])
            P.op("pool", lambda e, c=c: e.tensor_tensor(out=sc_t2[:, 1, :], in0=Sall[:, c, 0, :], in1=API[:], op=ALU.mult), reads=["Sall"], writes=["sc_t2"])
            P.op("pool", lambda e, c=c: e.tensor_tensor(out=sc_t1[:], in0=sc_t1[:], in1=Xall[:, c, :, :], op=ALU.add), reads=["sc_t1", "Xall"], writes=["sc_t1"])
            P.op("pool", lambda e, sn_=sn_: e.tensor_tensor(out=sn_, in0=sc_t1[:], in1=sc_t2[:], op=ALU.add), reads=["sc_t1", "sc_t2"], writes=["Sall"])
        if main:
            for ri in range(2):
                el("pool", lambda e, ri=ri: e.tensor_copy(out=Sbf[:, ri, :, :], in_=Sall[:, 0:NC8, ri, :].rearrange("p c q -> p q c")),
                   ["Sall"], ["Sbf"])
        el("pool", lambda e: e.tensor_copy(out=Sall[:, 0, :, :], in_=Sall[:, NC8, :, :]), ["Sall"], ["Sall"])
        def s5_out():
            proj_fm(BLK_ZB, 4, lambda ct, ps_, pn: el("act", lambda e: e.activation(out=zbT[:, ct, :], in_=ps_[:, 0:T], func=AF.Silu), [pn], ["zbT"]))
            Wi_, win_ = load_block(BLK_S5 + 2)
            Wr_, wrn_ = load_block(BLK_S5 + 3)
            Wm_, wmn_ = load_block(BLK_S5 + 4)
            Wi_g = Wi_[:].rearrange("p (g n) -> p g n", g=32); Wr_g = Wr_[:].rearrange("p (g n) -> p g n", g=32)
            Wm_g = Wm_[:].rearrange("p (g n) -> p g n", g=32)
            GP = 512 // NC8
            for g0 in range(0, 32, GP):
                ng = min(GP, 32 - g0)
                for gi in range(ng):
                    g = g0 + gi
                    o_ = pM[:, gi * NC8:(gi + 1) * NC8]
                    mm(o_, Wi_g[:, g, :], U2[:, g, :], True, False, [win_, "U2"], ["pM"], sig=False)
                    mm(o_, Wr_g[:, g, :], Sbf[:, 0, g // 2, :], False, False, [wrn_, "Sbf"], ["pM"], sig=False)
                    mm(o_, Wm_g[:, g, :], Sbf[:, 1, g // 2, :], False, True, [wmn_, "Sbf"], ["pM"], sig=(gi == ng - 1))
                el("act", lambda e, g0=g0, ng=ng: e.activation(out=Yg[:, g0:g0 + ng, :].rearrange("p g c -> p (g c)"), in_=pM[:, 0:ng * NC8],
                                                               func=AF.Gelu_apprx_tanh), ["pM"], ["Yg"])
            prefetch(BLK_G + 0); prefetch(BLK_G + 1)
            for g0 in range(0, 32, 8):
                for gi in range(8):
                    g = g0 + gi
                    P.op("pe", lambda e, g=g, gi=gi: e.transpose(out=pT[0:NC8, gi * 128:(gi + 1) * 128], in_=Yg[:, g, :], identity=ident_b[:]),
                         reads=["Yg", "ident_b"], writes=["pT"], sig=(gi == 7))
                el("act", lambda e, g0=g0: e.copy(out=Ytm[0:NC8, :, 16 * g0:16 * g0 + 128].rearrange("c j (g n) -> c g j n", g=8),
                                                  in_=pT[0:NC8, :].rearrange("c (g j n) -> c g j n", g=8, j=8)), ["pT"], ["Utm"])
            for ct in range(4):
                for j in range(8):
                    P.op("pe", lambda e, ct=ct, j=j: e.transpose(out=pT[:, j * NC8:(j + 1) * NC8], in_=Ytm[0:NC8, j, ct * 128:(ct + 1) * 128],
                                                                 identity=ident_b[0:NC8, 0:NC8]), reads=["Utm", "ident_b"], writes=["pT"], sig=(j == 7))
                el("act", lambda e, ct=ct: e.copy(out=yT[:, ct, :], in_=pT[:, 0:T]), ["pT"], ["yT"])
            for ot_ in range(4):
                ps_, pn = acc_bank()
                for ct in range(4):
                    mm(ps_[:, 0:T], Wglu[:, ct, ot_ * 128:(ot_ + 1) * 128], yT[:, ct, :], ct == 0, ct == 3, ["Wglu", "yT"], [pn])
                el("act", lambda e, ps_=ps_, ot_=ot_: e.activation(out=sgl[:], in_=ps_[:, 0:T], func=AF.Sigmoid, bias=bglu[:, ot_:ot_ + 1]),
                   [pn, "bglu"], ["sgl"])
                tt(xg[:], sgl[:], yT[:, ot_, :], ALU.mult, ["sgl", "yT"], ["xg"])
                tt(hbT[:, ot_, :].rearrange("p (j c) -> p j c", j=8), xg[:].rearrange("p (j c) -> p j c", j=8),
                   zbT[:, ot_, :].rearrange("p (c j) -> p j c", j=8), ALU.mult, ["xg", "zbT"], ["hbT"])
        chk(7)
        for mt in range(8):
            ps_, pn = acc_bank()
            for k in range(4):
                mm(ps_[:, 0:T], cdiag[:, mt, k, :], uaT[:, mt, k + 1:k + 1 + T], k == 0, k == 3, ["cdiag", "uaT"], [pn])
            el("act", lambda e, ps_=ps_, mt=mt: e.activation(out=cT[:, mt, :], in_=ps_[:, 0:T], func=AF.Silu, bias=cb[:, mt:mt + 1]),
               [pn, "cb"], ["cT"])
        if main:
            for half in range(2):
                proj_fm(BLK_ZA + half, 4, lambda ct, ps_, pn, half=half: el(
                    "act", lambda e: e.activation(out=zaT[:, half * 4 + ct, :], in_=ps_[:, 0:T], func=AF.Silu), [pn], ["zaT"]))
            for half in range(2):
                proj_fm(BLK_OA + half, 4, lambda ct, ps_, pn, half=half: el(
                    "act", lambda e: e.activation(out=oaT[:, half * 4 + ct, :], in_=ps_[:, 0:T], func=AF.Sigmoid), [pn], ["oaT"]))
        if main:
            for mt in range(8):
                el("dve", lambda e, mt=mt: e.scalar_tensor_tensor(out=csz[:, mt, :], in0=cT[:, mt, :], scalar=skp[:, mt:mt + 1], in1=zaT[:, mt, :],
                                                                  op0=ALU.mult, op1=ALU.mult), ["cT", "skp", "zaT"], ["csz"])
            for mt in range(8):
                ts(zaT[:, mt, :], zaT[:, mt, :], hg[:, mt:mt + 1], None, ALU.mult, None, ["zaT", "hg", "csz"], ["zaT"])
        chk(8)
        for h in range(4):
            if main:
                for (Wx, wxn, dst, dn) in ((Wq, "Wqkv0", qT, "qT"), (Wk, "Wqkv1", kT, "kT")):
                    for et in range(2):
                        ps_, pn = acc_bank()
                        for d in range(2):
                            mm(ps_[:, 0:T], Wx[:, h, d, et * 128:(et + 1) * 128], cT[:, 2 * h + d, :], d == 0, d == 1, [wxn, "cT"], [pn])
                        el("act" if et else "dve",
                           (lambda e, ps_=ps_, dst=dst, h=h, et=et: e.copy(out=dst[:, h, et, :], in_=ps_[:, 0:T])) if et else
                           (lambda e, ps_=ps_, dst=dst, h=h, et=et: e.tensor_copy(out=dst[:, h, et, :], in_=ps_[:, 0:T])), [pn], [dn])
        if main:
            prefetch(BLK_ZB); prefetch(BLK_S5 + 2); prefetch(BLK_S5 + 3)
        el("act", lambda e: e.activation(out=e1a[:], in_=gates[:, :, 4:8], func=AF.Exp, scale=-1.0), ["gates"], ["e1a"])
        el("act", lambda e: e.activation(out=lfa[:], in_=e1a[:], func=AF.Ln, bias=1.0), ["e1a"], ["lfa"])
        ts(lfa[:], lfa[:], -1.0, None, ALU.mult, None, ["lfa"], ["lfa"])
        for ch in range(NCH):
            tsl = slice(ch * 128, (ch + 1) * 128)
            for h in range(4):
                el("dve", lambda e, h=h, ch=ch: e.tensor_copy(out=lfrep[:, h, :], in_=lfa[:, ch, h:h + 1].to_broadcast([128, 128])), ["lfa"], ["lfrep"])
            for h in range(4):
                mm(pGD[:, h * 128:(h + 1) * 128], lfrep[:, h, :], maskT[:], True, True, ["lfrep", "maskT"], ["pGD"], sig=(h == 3))
            for h in range(4):
                mm(pM[:, h * 128:(h + 1) * 128], maskT[:], lfrep[:, h, :], True, True, ["maskT", "lfrep"], ["pM"], sig=(h == 3))
            en, dn, wn_, wbn, wrn = "emb%d" % ch, "dec%d" % ch, "wcol%d" % ch, "wcolb%d" % ch, "wrep%d" % ch
            el("act", lambda e, ch=ch: e.activation(out=emb4[:, ch, :, :].rearrange("p h t -> p (h t)"), in_=pGD[:], func=AF.Exp, scale=-1.0), ["pGD"], [en])
            el("dve", lambda e, ch=ch: e.reciprocal(out=dec4[:, ch, :], in_=emb4[:, ch, :, 127]), [en], [dn])
            tt(bcol[:], gates[:, ch, 0:4], pM[:].rearrange("p (h t) -> p h t", h=4)[:, :, 0], ALU.subtract, ["gates", "pM"], ["bcol"])
            el("act", lambda e, ch=ch: e.activation(out=wcol4[:, ch, :], in_=bcol[:], func=AF.Exp), ["bcol"], [wn_])
            el("dve", lambda e, ch=ch: e.tensor_copy(out=vw[:, ch, :, 256], in_=wcol4[:, ch, :]), [wn_], ["vw"])
            if main:
                for h in range(4):
                    el("dve", lambda e, h=h, ch=ch: e.tensor_copy(out=wrep4[:, ch, h, :], in_=wcol4[:, ch, h:h + 1].to_broadcast([128, 128])), [wn_], [wrn])
            for h in range(4):
                ps_, pn = acc_bank()
                for d in range(2):
                    mm(ps_[:, 0:256], cT[:, 2 * h + d, tsl], Wk[:, h, d, :], d == 0, d == 1, ["cT", "Wqkv1"], [pn], sig=False)
                for d in range(2):
                    mm(ps_[:, 256:512], uaT[:, 2 * h + d, 4 + ch * 128:4 + (ch + 1) * 128], Wv[:, h, d, :], d == 0, d == 1, ["uaT", "Wqkv2"], [pn])
                el("act", lambda e, ps_=ps_, ch=ch, h=h: e.copy(out=ktm[:, ch, h, :], in_=ps_[:, 0:256]), [pn], ["ktm"])
                el("act", lambda e, ps_=ps_, ch=ch, h=h: e.activation(out=vw[:, ch, h, 0:256], in_=ps_[:, 256:512], func=AF.Copy, scale=wcol4[:, ch, h:h + 1]),
                   [pn, wn_], ["vw"])
        for ch in range(NCH):
            tsl = slice(ch * 128, (ch + 1) * 128)
            en, dn, wn_, wbn, wrn = "emb%d" % ch, "dec%d" % ch, "wcol%d" % ch, "wcolb%d" % ch, "wrep%d" % ch
            if main:
                for h in range(4):
                    for et in range(2):
                        mm(pS[:, h * 128:(h + 1) * 128], kT[:, h, et, tsl], qT[:, h, et, tsl], et == 0, et == 1, ["kT", "qT"], ["pS"], sig=(h == 3 and et == 1))
                tt(SmT[:].rearrange("p h t -> p (h t)"), pS[:], mask4[:].rearrange("p h t -> p (h t)"), ALU.mult, ["pS", "mask4"], ["SmT"])
                for h in range(4):
                    for d2 in range(2):
                        pn_, pnn = (pN0, "pN0") if h < 2 else (pN1, "pN1")
                        o_ = pn_[:, ((h % 2) * 2 + d2) * 128:((h % 2) * 2 + d2 + 1) * 128]
                        mm(o_, vw[:, ch, h, d2 * 128:(d2 + 1) * 128], SmT[:, h, :], True, False, ["vw", "SmT"], [pnn], sig=False)
                        mm(o_, Cbf[:, h, 0, d2 * 128:(d2 + 1) * 128], qT[:, h, 0, tsl], False, False, ["Cbf%d" % h, "qT"], [pnn], sig=False)
                        mm(o_, Cbf[:, h, 1, d2 * 128:(d2 + 1) * 128], qT[:, h, 1, tsl], False, True, ["Cbf%d" % h, "qT"], [pnn], sig=(h % 2 == 1 and d2 == 1))
                    o_ = pGD[:, h * 128:(h + 1) * 128]
                    mm(o_, wrep4[:, ch, h, :], SmT[:, h, :], True, False, [wrn, "SmT"], ["pGD"], sig=False)
                    mm(o_, nrep[:, h, 0, :], qT[:, h, 0, tsl], False, False, ["nrep%d" % h, "qT"], ["pGD"], sig=False)
                    mm(o_, nrep[:, h, 1, :], qT[:, h, 1, tsl], False, True, ["nrep%d" % h, "qT"], ["pGD"], sig=(h == 3))
            if main:
                el("act", lambda e: e.activation(out=aden[:].rearrange("p h t -> p (h t)"), in_=pGD[:], func=AF.Abs), ["pGD"], ["aden"])
                tt(aden[:], aden[:], emb4[:, ch, :, :], ALU.max, ["aden", en], ["aden"])
                el("act", lambda e: e.activation(out=aden[:], in_=aden[:], func=AF.Ln), ["aden"], ["aden"])
                el("act", lambda e: e.activation(out=rden[:], in_=aden[:], func=AF.Exp, scale=-1.0), ["aden"], ["rden"])
            for h in range(4):
                cn, bn, nn = "Cst%d" % h, "Cbf%d" % h, "nrep%d" % h
                for et in range(2):
                    ps_, pn = acc_bank()
                    mm(ps_[:, 0:258], ktm[:, ch, h, et * 128:(et + 1) * 128], vw[:, ch, h, 0:258], True, True, ["ktm", "vw"], [pn])
                    tt(Cst[:, h, et, 0:258], Cst[:, h, et, 0:258], ps_[:, 0:258], ALU.add, [cn, pn, bn, nn], [cn])
            for h in range(4):
                cn, bn, nn = "Cst%d" % h, "Cbf%d" % h, "nrep%d" % h
                el("act", lambda e, h=h, ch=ch: e.activation(out=Cst[:, h, :, :], in_=Cst[:, h, :, :], func=AF.Copy, scale=dec4[:, ch, h:h + 1]), [cn, dn], [cn])
                el("act", lambda e, h=h: e.copy(out=Cbf[:, h, :, :], in_=Cst[:, h, :, :]), [cn], [bn])
            for h in range(4):
                cn, bn, nn = "Cst%d" % h, "Cbf%d" % h, "nrep%d" % h
                for et in range(2):
                    el("dve", lambda e, h=h, et=et: e.tensor_copy(out=nrep[:, h, et, :], in_=Cst[:, h, et, 256:257].to_broadcast([128, 128])),
                       [cn], [nn])
            if main:
                for hp, (pn_, pnn) in enumerate(((pN0, "pN0"), (pN1, "pN1"))):
                    tt(hT[:, 4 * hp:4 * hp + 4, :].rearrange("p (h d) t -> p h d t", h=2), pn_[:].rearrange("p (h d t) -> p h d t", h=2, d=2),
                       rden[:, 2 * hp:2 * hp + 2, :].unsqueeze(2).to_broadcast([128, 2, 2, 128]), ALU.mult, [pnn, "rden"], ["hT"])
                tt(hT[:], hT[:], oaT[:, :, tsl], ALU.mult, ["hT", "oaT"], ["hT"])
                el("act", lambda e: e.activation(out=sq[:], in_=hT[:], func=AF.Square), ["hT"], ["sq"])
                for h in range(4):
                    for d2 in range(2):
                        mm(pS[:, h * 128:(h + 1) * 128], ones_b[:], sq[:, 2 * h + d2, :], d2 == 0, d2 == 1, ["ones_b", "sq", "SmT"], ["pS"], sig=(h == 3 and d2 == 1))
                ts(rsh[:].rearrange("p h t -> p (h t)"), pS[:], 1.0 / 256.0, 1e-6, ALU.mult, ALU.add, ["pS"], ["rsh"])
                el("act", lambda e: e.activation(out=rsh[:], in_=rsh[:], func=AF.Ln), ["rsh"], ["rsh"])
                el("act", lambda e: e.activation(out=rsh[:], in_=rsh[:], func=AF.Exp, scale=-0.5), ["rsh"], ["rsh"])
                tt(hn[:].rearrange("p (h d) t -> p h d t", h=4), hT[:].rearrange("p (h d) t -> p h d t", h=4),
                   rsh[:].unsqueeze(2).to_broadcast([128, 4, 2, 128]), ALU.mult, ["hT", "rsh"], ["hn"])
                tt(hn[:], hn[:], zaT[:, :, tsl], ALU.mult, ["hn", "zaT"], ["hn"])
                tt(hfT[:, :, tsl], hn[:], csz[:, :, tsl], ALU.add, ["hn", "csz"], ["hfT"])
        chk(9)
        if not main:
            return
        s5_out()
        def gate_blocks(br):
            for bi_ in range(2):
                Wg_, wgn = load_block(BLK_G + 2 * br + bi_)
                Wgv = Wg_[:].rearrange("p (k n) -> p k n", k=8)
                for c4 in range(4):
                    ft = bi_ * 4 + c4
                    psg, png = acc_bank()
                    for kt in range(8):
                        mm(psg[:, 0:T], Wgv[:, kt, c4 * 128:(c4 + 1) * 128], xnT[:, kt, :], kt == 0, kt == 7, [wgn, "xnT"], [png])
                    el("act", lambda e, psg=psg, ft=ft: e.activation(out=gAll[:, ft, :], in_=psg[:, 0:T], func=AF.Sigmoid), [png], ["oaT"])

        gate_blocks(0)
        for hf_ in range(2):
            Wa, wan = load_block(BLK_AO + hf_)
            Wav = Wa[:].rearrange("p (k n) -> p k n", k=8)
            for c4 in range(4):
                ft = hf_ * 4 + c4
                psa, pna = acc_bank()
                for kt in range(8):
                    mm(psa[:, 0:T], Wav[:, kt, c4 * 128:(c4 + 1) * 128], hfT[:, kt, :], kt == 0, kt == 7, [wan, "hfT"], [pna])
                tt(mrgT[:, ft, :], psa[:, 0:T], gAll[:, ft, :], ALU.mult, [pna, "oaT"], ["zaT"])
        gate_blocks(1)
        Wbo, wbon = load_block(BLK_BO)
        Wbv = Wbo[:].rearrange("p (k n) -> p k n", k=4)
        for ft in range(8):
            psb, pnb = acc_bank()
            for kt in range(4):
                mm(psb[:, 0:T], Wbv[:, kt, ft * 128:(ft + 1) * 128], hbT[:, kt, :], kt == 0, kt == 3, [wbon, "hbT"], [pnb])
            tt(m2[:].rearrange("p (c j) -> p j c", j=8), psb[:, 0:T].rearrange("p (j c) -> p j c", j=8),
               gAll[:, ft, :].rearrange("p (c j) -> p j c", j=8), ALU.mult, [pnb, "oaT"], ["m2"])
            tt(mrgT[:, ft, :], m2[:], mrgT[:, ft, :], ALU.add, ["m2", "zaT"], ["zaT"])
        Wo0, wo0n = load_block(BLK_WO); Wo1, wo1n = load_block(BLK_WO + 1)
        for ch in range(NCH):
            xb_, xnm = xr[ch % 2], "xr%d" % (ch % 2)
            if ch < 2:
                P.dma(lambda e, xb_=xb_, ch=ch: e.dma_start(out=xb_[:], in_=xsrc[row0 + ch * 128: row0 + (ch + 1) * 128, :]), writes=[xnm])
        for ch in range(NCH):
            tsl = slice(ch * 128, (ch + 1) * 128)
            bk = ((pN0, "pN0"), (pN1, "pN1")) if ch % 2 == 0 else ((pS, "pS"), (pGD, "pGD"))
            ssA, ssAn = (ss2, "ss2") if ch % 2 == 0 else (ss3, "ss3")
            ssB, ssBn = (rstd2, "rstd2") if ch % 2 == 0 else (rstd3, "rstd3")
            for hf_, (Wo_, won) in enumerate(((Wo0, wo0n), (Wo1, wo1n))):
                pn_, pnn = bk[hf_]
                Wov = Wo_[:].rearrange("p (k n) -> p k n", k=8)
                for kt in range(8):
                    mm(pn_[:, :], mrgT[:, kt, tsl], Wov[:, kt, :], kt == 0, kt == 7, ["zaT", won], [pnn])
            if ch == 0:
                early_A()
            jk, jkn = (junk, "junk") if ch % 2 == 0 else (xnb, "xnb")
            el("act", lambda e, bk=bk, ssA=ssA, jk=jk: e.activation(out=jk[:, 0:512], in_=bk[0][0][:], func=AF.Square, accum_out=ssA[:]), [bk[0][1]], [jkn, ssAn])
            el("act", lambda e, bk=bk, ssB=ssB, jk=jk: e.activation(out=jk[:, 512:1024], in_=bk[1][0][:], func=AF.Square, accum_out=ssB[:]), [bk[1][1]], [jkn, ssBn])
            tt(ssA[:], ssA[:], ssB[:], ALU.add, [ssAn, ssBn], [ssAn])
            rsqrt_col(ssB[:], ssA[:], 1.0 / 1024.0, ssAn, ssBn)
            ob, obn = ot[ch % 2], "ot%d" % (ch % 2)
            xb_, xnm = xr[ch % 2], "xr%d" % (ch % 2)
            if ch >= 2:
                P.dma(lambda e, xb_=xb_, ch=ch: e.dma_start(out=xb_[:], in_=xsrc[row0 + ch * 128: row0 + (ch + 1) * 128, :]), writes=[xnm])
            for hf_ in range(2):
                pn_, pnn = bk[hf_]
                cs = slice(hf_ * 512, hf_ * 512 + 512)
                el("dve", lambda e, ob=ob, pn_=pn_, cs=cs, ssB=ssB: e.scalar_tensor_tensor(out=ob[:, cs], in0=pn_[:], scalar=ssB[:, 0:1], in1=gpost[:, cs],
                                                                                          op0=ALU.mult, op1=ALU.mult), [pnn, ssBn, "gpost"], [obn])
            tt(ob[:], ob[:], xb_[:], ALU.add, [obn, xnm], [obn])
            tok = P.dma(lambda e, ob=ob, ch=ch: e.dma_start(out=out[row0 + ch * 128: row0 + (ch + 1) * 128, :], in_=ob[:]), reads=[obn])
            out_toks.append(tok)

    sc_t1 = sbt([128, 2, 16], F32, "sc_t1"); sc_t2 = sbt([128, 2, 16], F32, "sc_t2")
    tiles_ = [(x_pre, ti * T) for ti in range(NPRE)] + [(x_main, ti * T) for ti in range(NMAIN)]
    for i_ in range(len(tiles_) - 1):
        nxt2[(id(tiles_[i_][0]), tiles_[i_][1])] = tiles_[i_ + 1]
    try:
        for ti in range(NPRE):
            nxt = (x_pre, (ti + 1) * T) if ti + 1 < NPRE else (x_main, 0)
            do_tile(x_pre, ti * T, False, ti, nxt)
    except StopBuild:
        tk = P.dma(lambda e: e.dma_start(out=out[0:128, :], in_=gpost[:]), reads=["gpost"])
        P.final_wait("sp", [tk])
        P.emit(); P.close()
        return nc
    if NPRE > 0:
        for h in range(4):
            cn, bn, nn = "Cst%d" % h, "Cbf%d" % h, "nrep%d" % h
            ts(Cst[:, h, :, :], Cst[:, h, :, :], flg[:, 0:1], None, ALU.mult, None, [cn, "flg"], [cn])
            el("act", lambda e, h=h: e.copy(out=Cbf[:, h, :, :], in_=Cst[:, h, :, :]), [cn], [bn])
            for et in range(2):
                el("dve", lambda e, h=h, et=et: e.tensor_copy(out=nrep[:, h, et, :], in_=Cst[:, h, et, 256:257].to_broadcast([128, 128])),
                   [cn], [nn])
    for ti in range(NMAIN):
        nxt = (x_main, (ti + 1) * T) if ti + 1 < NMAIN else None
        do_tile(x_main, ti * T, True, NPRE + ti, nxt)
    P.final_wait("sp", out_toks)
    P.emit()
    P.close()
    return nc


T_TILE = 256
_cache = {}


def kernel(**inputs):
    x = np.ascontiguousarray(inputs["x"], dtype=np.float32)
    Bsz, L, Dm = x.shape
    half = L // 2
    npre = half // T_TILE
    nmain = half // T_TILE
    key = (T_TILE, npre, nmain)
    if key not in _cache:
        _cache[key] = build_program(T_TILE, npre, nmain)
    nc = _cache[key]
    shared = {}
    for k, v in inputs.items():
        if k == "x":
            continue
        a = np.ascontiguousarray(np.asarray(v, dtype=np.float32)[0])
        if k in ("b_i", "b_f", "log_dt", "norm_post_g"):
            a = a.reshape(1, -1)
        shared[k] = a
    in_maps = []
    zeros = np.zeros((half, Dm), np.float32)
    for core in range(8):
        b, hf = core // 2, core % 2
        m = dict(shared)
        m["x_main"] = np.ascontiguousarray(x[b, hf * half:(hf + 1) * half])
        m["x_pre"] = zeros if hf == 0 else np.ascontiguousarray(x[b, 0:half])
        m["flag"] = np.full((128, 1), float(hf), np.float32)
        in_maps.append(m)
    res = run_bass_kernel_spmd(nc, in_maps, core_ids=list(range(8)))
    outp = np.empty((Bsz, L, Dm), np.float32)
    for core in range(8):
        b, hf = core // 2, core % 2
        outp[b, hf * half:(hf + 1) * half] = res.results[core]["out"]
    return outp
```
